# Optimizing a Trainium2 kernel written in Bass

```python
import math
import jax, jax.numpy as jnp
from jax import lax
import numpy as np

D_MODEL = 1024
BATCH = 8
SEQ = 2048
DEPTH = 1
DEC_BATCH = 128
DEC_SEQ = 1
PAST_LEN = 16384
PAGE_SIZE = 128

N_MEM = 256
RWKV_HEAD = 64
RWKV_HEADS = D_MODEL // RWKV_HEAD
RWKV_WIDTH = RWKV_HEADS * RWKV_HEAD
DECAY_LORA = 64
AAA_LORA = 64
GATE_LORA = 128
RWKV_PROJ = 3 * RWKV_WIDTH + DECAY_LORA + AAA_LORA + GATE_LORA
GN_EPS = 64e-5
LRU_WIDTH = D_MODEL
LRU_BLOCKS = 16
LRU_BLOCK = LRU_WIDTH // LRU_BLOCKS
CONV_WIDTH = 4
LRU_C = 8.0
N_BRANCH = 2
PROJ_WIDTH = RWKV_PROJ + 2 * LRU_WIDTH + N_BRANCH * D_MODEL
XA_HEADS = 4
XA_HEAD = D_MODEL // XA_HEADS
D_FF = 2816
DN_ALPHA = (2.0 * DEPTH) ** 0.25
DN_BETA = (8.0 * DEPTH) ** -0.25
LN_EPS = 1e-5

kernel_name = "hybrid_rwkv7_rglru_gated_decoder_step"

F32 = jnp.float32


def _layer_norm(x, g, b):
    xf = x.astype(F32)
    mu = jnp.mean(xf, axis=-1, keepdims=True)
    var = jnp.mean(jnp.square(xf - mu), axis=-1, keepdims=True)
    return ((xf - mu) * lax.rsqrt(var + LN_EPS) * g.astype(F32) + b.astype(F32)).astype(x.dtype)


def _swiglu(h, wi, wo):
    u = h @ wi
    gate, up = u[..., :D_FF], u[..., D_FF:]
    return (jax.nn.silu(gate) * up) @ wo


def _rwkv_scan(S0, r, w, k, v, kk, a):
    def step(S, inp):
        r_t, w_t, k_t, v_t, kk_t, a_t = inp
        sa = jnp.einsum('bhvk,bhk->bhv', S, -kk_t)
        S = S * w_t[:, :, None, :] + sa[..., None] * (kk_t * a_t)[:, :, None, :] + v_t[..., None] * k_t[:, :, None, :]
        y = jnp.einsum('bhvk,bhk->bhv', S, r_t)
        return S, y
    xs = tuple(jnp.moveaxis(t, 1, 0) for t in (r, w, k, v, kk, a))
    S, ys = lax.scan(step, S0, xs)
    return S, jnp.moveaxis(ys, 0, 1)


def _lin_combine(c1, c2):
    a1, b1 = c1
    a2, b2 = c2
    return a1 * a2, a2 * b1 + b2


def _mixer(h, S0, shift0, h0, buf0, P):
    B, T, _ = h.shape
    H, N = RWKV_HEADS, RWKV_HEAD
    proj = h @ P['w_in']
    p_rwkv = proj[..., :RWKV_PROJ]
    p_lru = proj[..., RWKV_PROJ:RWKV_PROJ + LRU_WIDTH]
    p_gelu = proj[..., RWKV_PROJ + LRU_WIDTH:RWKV_PROJ + 2 * LRU_WIDTH]
    p_gate = proj[..., RWKV_PROJ + 2 * LRU_WIDTH:]

    prev = jnp.concatenate([shift0[:, None].astype(p_rwkv.dtype), p_rwkv[:, :-1]], axis=1)
    xs = p_rwkv + (prev - p_rwkv) * P['shift_mu']
    W = RWKV_WIDTH
    r = xs[..., :W]
    k = xs[..., W:2 * W]
    v = xs[..., 2 * W:3 * W]
    xw = xs[..., 3 * W:3 * W + DECAY_LORA]
    xa = xs[..., 3 * W + DECAY_LORA:3 * W + DECAY_LORA + AAA_LORA]
    xg = xs[..., 3 * W + DECAY_LORA + AAA_LORA:]
    wlog = -jax.nn.softplus(-(P['decay_w0'] + jnp.tanh(xw) @ P['decay_w2']).astype(F32)) - 0.5
    decay = jnp.exp(-jnp.exp(wlog))
    a = jax.nn.sigmoid((P['aaa_a0'] + xa @ P['aaa_a2']).astype(F32))
    g = jax.nn.sigmoid(xg) @ P['gate_g2']
    rf, kf, vf = r.astype(F32), k.astype(F32), v.astype(F32)
    kk = (kf * P['k_k'].astype(F32)).reshape(B, T, H, N)
    kk = kk / jnp.maximum(jnp.sqrt(jnp.sum(kk * kk, axis=-1, keepdims=True)), 1e-12)
    kf = kf * (1.0 + (a - 1.0) * P['k_a'].astype(F32))
    hd = lambda t: t.reshape(B, T, H, N)
    rh, kh, vh = hd(rf), hd(kf), hd(vf)
    S_new, y = _rwkv_scan(S0.astype(F32), rh, hd(decay), kh, vh, kk, hd(a))
    ym = jnp.mean(y, axis=-1, keepdims=True)
    yv = jnp.mean(jnp.square(y - ym), axis=-1, keepdims=True)
    yn = ((y - ym) * lax.rsqrt(yv + GN_EPS)).reshape(B, T, W) * P['gn_g'].astype(F32) + P['gn_b'].astype(F32)
    bonus = (jnp.sum(rh * kh * P['r_k'].astype(F32), axis=-1, keepdims=True) * vh).reshape(B, T, W)
    rwkv_out = ((yn + bonus) * g.astype(F32)).astype(h.dtype)

    conv_in = jnp.concatenate([buf0.astype(p_lru.dtype), p_lru], axis=1)
    xc = P['conv_b'] + sum(P['conv_w'][j] * conv_in[:, j:j + T] for j in range(CONV_WIDTH))
    new_buf = conv_in[:, -(CONV_WIDTH - 1):]
    xcb = xc.reshape(B, T, LRU_BLOCKS, LRU_BLOCK)
    gr = jax.nn.sigmoid((jnp.einsum('btnc,ncd->btnd', xcb, P['lru_wr']).reshape(B, T, LRU_WIDTH) + P['lru_br']).astype(F32))
    gi = jax.nn.sigmoid((jnp.einsum('btnc,ncd->btnd', xcb, P['lru_wi']).reshape(B, T, LRU_WIDTH) + P['lru_bi']).astype(F32))
    log_a = -LRU_C * gr * jax.nn.softplus(-P['lru_lambda'].astype(F32))
    a_t = jnp.exp(log_a)
    b_t = jnp.sqrt(-jnp.expm1(2.0 * log_a)) * gi * xc.astype(F32)
    b_t = b_t.at[:, 0].add(a_t[:, 0] * h0.astype(F32))
    _, hs = lax.associative_scan(_lin_combine, (a_t, b_t), axis=1)
    lru_out = (hs * jax.nn.gelu(p_gelu.astype(F32))).astype(h.dtype)

    gates = jax.nn.sigmoid(p_gate).reshape(B, T, N_BRANCH, D_MODEL)
    merged = gates[:, :, 0] * rwkv_out + gates[:, :, 1] * lru_out
    out = merged @ P['w_mix_out']
    return out, S_new.astype(h.dtype), p_rwkv[:, -1], hs[:, -1].astype(h.dtype), new_buf


def _xattn(h, mk, mv, wq, wo):
    B, T, _ = h.shape
    q = (h @ wq).reshape(B, T, XA_HEADS, XA_HEAD)
    s = jnp.einsum('bthd,bmhd->bhtm', q, mk).astype(F32) * (XA_HEAD ** -0.5)
    p = jax.nn.softmax(s, axis=-1).astype(h.dtype)
    o = jnp.einsum('bhtm,bmhd->bthd', p, mv).reshape(B, T, D_MODEL)
    return o @ wo


def _layer(h, mk, mv, S0, shift0, h0, buf0, P):
    h = _layer_norm(DN_ALPHA * h + 0.5 * _swiglu(h, P['ffn1_wi'], P['ffn1_wo']), P['ln_g'][0], P['ln_b'][0])
    mix, S_new, shift_new, h_new, buf_new = _mixer(h, S0, shift0, h0, buf0, P)
    h = _layer_norm(DN_ALPHA * h + mix, P['ln_g'][1], P['ln_b'][1])
    h = _layer_norm(DN_ALPHA * h + _xattn(h, mk, mv, P['xa_wq'], P['xa_wo']), P['ln_g'][2], P['ln_b'][2])
    h = _layer_norm(DN_ALPHA * h + 0.5 * _swiglu(h, P['ffn2_wi'], P['ffn2_wo']), P['ln_g'][3], P['ln_b'][3])
    return h, S_new, shift_new, h_new, buf_new


def setup_inputs(seed: int = 0) -> dict:
    key = jax.random.key(seed)
    ks = iter(jax.random.split(key, 64))
    nrm = lambda shape, scale: jax.random.normal(next(ks), shape, F32) * scale
    uni = lambda shape, lo, hi: jax.random.uniform(next(ks), shape, F32, minval=lo, maxval=hi)
    L, D, W = DEPTH, D_MODEL, RWKV_WIDTH
    inv = lambda n: float(n) ** -0.5
    lam_s = uni((L, LRU_WIDTH), 0.9, 0.999) ** (1.0 / LRU_C)
    return {
        'x_prompt': nrm((BATCH, SEQ, D), 1.0),
        'x_sample': nrm((DEC_BATCH, DEC_SEQ, D), 1.0),
        'mem_prompt': nrm((BATCH, N_MEM, D), 1.0),
        'cache_mem_k': nrm((L, DEC_BATCH, N_MEM, XA_HEADS, XA_HEAD), 1.0),
        'cache_mem_v': nrm((L, DEC_BATCH, N_MEM, XA_HEADS, XA_HEAD), 1.0),
        'state_rwkv': nrm((L, DEC_BATCH, RWKV_HEADS, RWKV_HEAD, RWKV_HEAD), 0.5),
        'state_rwkv_shift': nrm((L, DEC_BATCH, RWKV_PROJ), 1.0),
        'state_lru': nrm((L, DEC_BATCH, LRU_WIDTH), 0.5),
        'state_conv': nrm((L, DEC_BATCH, CONV_WIDTH - 1, LRU_WIDTH), 1.0),
        'ln_g': 1.0 + nrm((L, 4, D), 0.02),
        'ln_b': nrm((L, 4, D), 0.02),
        'ffn1_wi': nrm((L, D, 2 * D_FF), inv(D)),
        'ffn1_wo': nrm((L, D_FF, D), inv(D_FF) * DN_BETA),
        'ffn2_wi': nrm((L, D, 2 * D_FF), inv(D)),
        'ffn2_wo': nrm((L, D_FF, D), inv(D_FF) * DN_BETA),
        'w_in': nrm((L, D, PROJ_WIDTH), inv(D)),
        'shift_mu': uni((L, RWKV_PROJ), 0.0, 1.0),
        'decay_w0': uni((L, W), -6.0, -1.0),
        'decay_w2': nrm((L, DECAY_LORA, W), 0.1 * inv(DECAY_LORA)),
        'aaa_a0': nrm((L, W), 0.1),
        'aaa_a2': nrm((L, AAA_LORA, W), 0.1 * inv(AAA_LORA)),
        'gate_g2': nrm((L, GATE_LORA, W), inv(GATE_LORA)),
        'k_k': 0.85 + nrm((L, W), 0.02),
        'k_a': 1.0 + nrm((L, W), 0.02),
        'r_k': nrm((L, RWKV_HEADS, RWKV_HEAD), 0.1),
        'gn_g': 1.0 + nrm((L, W), 0.02),
        'gn_b': nrm((L, W), 0.02),
        'conv_w': nrm((L, CONV_WIDTH, LRU_WIDTH), inv(CONV_WIDTH)),
        'conv_b': nrm((L, LRU_WIDTH), 0.02),
        'lru_wr': nrm((L, LRU_BLOCKS, LRU_BLOCK, LRU_BLOCK), inv(LRU_BLOCK)),
        'lru_br': nrm((L, LRU_WIDTH), 0.02),
        'lru_wi': nrm((L, LRU_BLOCKS, LRU_BLOCK, LRU_BLOCK), inv(LRU_BLOCK)),
        'lru_bi': nrm((L, LRU_WIDTH), 0.02),
        'lru_lambda': jnp.log(lam_s) - jnp.log1p(-lam_s),
        'w_mix_out': nrm((L, D, D), inv(D) * DN_BETA),
        'xa_wq': nrm((L, D, D), inv(D)),
        'xa_wk': nrm((L, D, D), inv(D)),
        'xa_wv': nrm((L, D, D), inv(D)),
        'xa_wo': nrm((L, D, D), inv(D) * DN_BETA),
    }


def reference(x_prompt, x_sample, mem_prompt, cache_mem_k, cache_mem_v, state_rwkv, state_rwkv_shift,
              state_lru, state_conv, ln_g, ln_b, ffn1_wi, ffn1_wo, ffn2_wi, ffn2_wo, w_in, shift_mu,
              decay_w0, decay_w2, aaa_a0, aaa_a2, gate_g2, k_k, k_a, r_k, gn_g, gn_b, conv_w, conv_b,
              lru_wr, lru_br, lru_wi, lru_bi, lru_lambda, w_mix_out, xa_wq, xa_wk, xa_wv, xa_wo):
    dt = x_prompt.dtype
    hp, hs = x_prompt, x_sample
    p_mk, p_mv, p_S, p_sh, p_h, p_cv = [], [], [], [], [], []
    s_S, s_sh, s_h, s_cv = [], [], [], []
    for l in range(DEPTH):
        P = dict(ln_g=ln_g[l], ln_b=ln_b[l], ffn1_wi=ffn1_wi[l], ffn1_wo=ffn1_wo[l], ffn2_wi=ffn2_wi[l],
                 ffn2_wo=ffn2_wo[l], w_in=w_in[l], shift_mu=shift_mu[l], decay_w0=decay_w0[l],
                 decay_w2=decay_w2[l], aaa_a0=aaa_a0[l], aaa_a2=aaa_a2[l], gate_g2=gate_g2[l], k_k=k_k[l],
                 k_a=k_a[l], r_k=r_k[l], gn_g=gn_g[l], gn_b=gn_b[l], conv_w=conv_w[l], conv_b=conv_b[l],
                 lru_wr=lru_wr[l], lru_br=lru_br[l], lru_wi=lru_wi[l], lru_bi=lru_bi[l],
                 lru_lambda=lru_lambda[l], w_mix_out=w_mix_out[l], xa_wq=xa_wq[l], xa_wo=xa_wo[l])
        mk = (mem_prompt @ xa_wk[l]).reshape(BATCH, N_MEM, XA_HEADS, XA_HEAD)
        mv = (mem_prompt @ xa_wv[l]).reshape(BATCH, N_MEM, XA_HEADS, XA_HEAD)
        hp, S1, sh1, h1, b1 = _layer(
            hp, mk, mv,
            jnp.zeros((BATCH, RWKV_HEADS, RWKV_HEAD, RWKV_HEAD), F32),
            jnp.zeros((BATCH, RWKV_PROJ), dt),
            jnp.zeros((BATCH, LRU_WIDTH), F32),
            jnp.zeros((BATCH, CONV_WIDTH - 1, LRU_WIDTH), dt), P)
        p_mk.append(mk); p_mv.append(mv); p_S.append(S1); p_sh.append(sh1); p_h.append(h1); p_cv.append(b1)
        hs, S2, sh2, h2, b2 = _layer(hs, cache_mem_k[l], cache_mem_v[l], state_rwkv[l], state_rwkv_shift[l],
                                     state_lru[l], state_conv[l], P)
        s_S.append(S2); s_sh.append(sh2); s_h.append(h2); s_cv.append(b2)
    return (hp, hs,
            jnp.stack(p_mk), jnp.stack(p_mv), jnp.stack(p_S), jnp.stack(p_sh), jnp.stack(p_h), jnp.stack(p_cv),
            jnp.stack(s_S), jnp.stack(s_sh), jnp.stack(s_h), jnp.stack(s_cv))
```

```python
import math
import os
import numpy as np
from contextlib import ExitStack
import concourse.bass as bass
import concourse.mybir as mybir
from concourse.bass_utils import run_bass_kernel_spmd

F32 = mybir.dt.float32
BF16 = mybir.dt.bfloat16
AF = mybir.ActivationFunctionType
ALU = mybir.AluOpType
AX = mybir.AxisListType

ENGS = ['pe', 'dve', 'act', 'pool', 'sp']

D = 1024
T = 2048
NT = 256
NTILES = T // NT
C = 64
NCH = NT // C
DFF = 2816
NJ = DFF // 128
RP = 3328
PW = 7424
NMEM = 256
NS = 16
ALPHA = 2.0 ** 0.25
LN_EPS = 1e-5
GN_EPS = 64e-5
C0 = math.exp(-0.5)


class DmaSem:
    def __init__(self, sem):
        self.sem = sem
        self.count = 0


class Prog:
    def __init__(self, nc, stack):
        self.nc = nc
        self.stack = stack
        self.ops = {e: [] for e in ENGS}
        self.last_w = {}
        self.readers = {}
        self.seen = {e: {} for e in ENGS}
        self.dsems = []
        self.out_tokens = []

    def dma_sem(self):
        s = DmaSem(self.stack.enter_context(self.nc.semaphore('dsem%d' % len(self.dsems))))
        self.dsems.append(s)
        return s

    def sbuf(self, name, shape, dt):
        return self.stack.enter_context(self.nc.sbuf_tensor(name, list(shape), dt))

    def psum(self, name, shape, dt):
        return self.stack.enter_context(self.nc.psum_tensor(name, list(shape), dt))

    def barrier(self, keys, engines=ENGS):
        if 'B' in os.environ.get('TOG', ''):
            return
        for e in engines:
            self.op(e, None, reads=keys, track=False)

    def op(self, eng, fn, reads=(), writes=(), dsem=None, is_out=False, track=True):
        isps = lambda k: isinstance(k, tuple) and k[0] == 'ps'
        writes = list(writes) + [k for k in reads if isps(k)]
        reads = [k for k in reads if not isps(k)]
        deps = []
        for k in reads:
            t = self.last_w.get(k)
            if t is not None:
                deps.append(t)
        for k in writes:
            t = self.last_w.get(k)
            if t is not None:
                deps.append(t)
            deps.extend(self.readers.get(k, {}).values())
        need = {}
        for t in deps:
            if t[0] == 'eng':
                if t[1] == eng and dsem is None and (eng == 'pe' or os.environ.get('NOSELF')):
                    continue
                key = ('eng', t[1])
            else:
                key = ('dma', id(t[1]))
            if need.get(key, (None, -1))[1] < t[2]:
                need[key] = (t[1], t[2])
        waits = []
        for key, (src, v) in need.items():
            if self.seen[eng].get(key, -1) >= v:
                continue
            self.seen[eng][key] = v
            waits.append((key[0], src, v))
        idx = len(self.ops[eng])
        self.ops[eng].append(dict(fn=fn, waits=waits, dsem=dsem, target=False))
        if dsem is not None:
            dsem.count += 16
            tok = ('dma', dsem, dsem.count)
        else:
            tok = ('eng', eng, idx)
        for k in writes:
            self.last_w[k] = tok
            self.readers[k] = {}
        for k in (reads if track else ()):
            r = self.readers.setdefault(k, {})
            rk = (tok[0], tok[1] if tok[0] == 'eng' else id(tok[1]))
            if rk not in r or r[rk][2] < tok[2]:
                r[rk] = tok
        if is_out:
            self.out_tokens.append(tok)
        return tok

    def emit(self):
        nc = self.nc
        fin = {}
        for t in self.out_tokens:
            fin[id(t[1])] = (t[1], max(fin.get(id(t[1]), (None, 0))[1], t[2]))
        self.ops['sp'].append(dict(fn=None, waits=[('dma', s, v) for s, v in fin.values()], dsem=None, target=False))
        for e in ENGS:
            for o in self.ops[e]:
                for kind, src, v in o['waits']:
                    if kind == 'eng':
                        self.ops[src][v]['target'] = True
        semval = {}
        for e in ENGS:
            c = 0
            vals = []
            for o in self.ops[e]:
                if o['target']:
                    c += 1
                vals.append(c)
            semval[e] = vals
        esem = {e: self.stack.enter_context(nc.semaphore('esem_' + e)) for e in ENGS}
        handles = {'pe': 'tensor', 'dve': 'vector', 'act': 'scalar', 'pool': 'gpsimd', 'sp': 'sync'}
        with nc.Block() as block:
            def make(e):
                def body(eng):
                    for o in self.ops[e]:
                        for kind, src, v in o['waits']:
                            if kind == 'eng':
                                eng.wait_ge(esem[src], semval[src][v])
                            else:
                                eng.wait_ge(src.sem, v)
                        if o['fn'] is None:
                            continue
                        inst = o['fn'](eng)
                        if o['dsem'] is not None:
                            inst.then_inc(o['dsem'].sem, 16)
                        elif o['target']:
                            inst.then_inc(esem[e], 1)
                return body
            for e in ENGS:
                getattr(block, handles[e])(make(e))
        self.stats = {e: len(self.ops[e]) for e in ENGS}


VEC_NAMES = ['decay_w0', 'aaa_a0', 'k_k', 'k_a', 'r_k', 'gn_g', 'gn_b', 'conv_b', 'lru_br', 'lru_bi', 'lru_lambda']
W_NAMES = ['ffn1_wi', 'ffn1_wo', 'ffn2_wi', 'ffn2_wo', 'w_in', 'w_mix_out', 'xa_wq', 'xa_wk', 'xa_wv', 'xa_wo']


def build(dbg=None, do_sample=True, stage=99):
    nc = bass.Bass('TRN2', target_bir_lowering=False)
    di = {}

    DECL = os.environ.get('DECL')

    def din(name, shape):
        if DECL and name not in DECL.split(','):
            return None
        di[name] = nc.dram_tensor(name, list(shape), F32, kind='ExternalInput').ap()
        return di[name]

    def dout(name, shape):
        if DECL and name not in DECL.split(','):
            return None
        di[name] = nc.dram_tensor(name, list(shape), F32, kind='ExternalOutput').ap()
        return di[name]

    din('xp', [T, D]); din('mem', [NMEM, D])
    din('xs', [NS, D]); din('cmk', [NS, NMEM, D]); din('cmv', [NS, NMEM, D])
    din('srw', [NS, D, 64]); din('ssh', [NS, RP]); din('slru', [NS, D]); din('scv', [NS, 3, D])
    din('prm', [210, 128])
    din('ffn1_wi', [D, 2 * DFF]); din('ffn1_wo', [DFF, D]); din('ffn2_wi', [D, 2 * DFF]); din('ffn2_wo', [DFF, D])
    din('w_in', [D, PW])
    din('decay_w2', [64, D]); din('aaa_a2', [64, D]); din('gate_g2', [128, D])
    din('lru_wr', [16, 64, 64]); din('lru_wi', [16, 64, 64])
    for w in ['w_mix_out', 'xa_wq', 'xa_wk', 'xa_wv', 'xa_wo']:
        din(w, [D, D])
    din('c_all', [128, 640 + NT])
    dout('yp', [T, D]); dout('ys', [NS, D]); dout('pmk', [NMEM, D]); dout('pmv', [NMEM, D])
    dout('prw', [D, 64]); dout('psh', [RP]); dout('plru', [D]); dout('pcv', [3, D])
    dout('srw_o', [NS, D, 64]); dout('ssh_o', [NS, RP]); dout('slru_o', [NS, D]); dout('scv_o', [NS, 3, D])
    dbg = dbg or {}
    for k, shp in dbg.items():
        dout('dbg_' + k, shp)

    with ExitStack() as st:
        P = Prog(nc, st)
        n = NT
        ident = P.sbuf('ident', [128, 128], F32)
        identb = P.sbuf('identb', [128, 128], BF16)
        onesb = P.sbuf('onesb', [128, 128], BF16)
        bdb = P.sbuf('bdb', [128, 128], BF16)
        mask = P.sbuf('mask', [128, 256], F32)
        reset = P.sbuf('reset', [128, NT], F32)
        lng = P.sbuf('lng', [128, 4, 8], F32); lnb = P.sbuf('lnb', [128, 4, 8], F32)
        mu = P.sbuf('mu', [128, 26], F32); omu = P.sbuf('omu', [128, 26], F32)
        vec = {v: P.sbuf('v_' + v, [128, 8], F32) for v in VEC_NAMES}
        oka = P.sbuf('oka', [128, 8], F32)
        lsp = P.sbuf('lsp', [128, 8], F32)
        cw = P.sbuf('cw', [128, 4, 8], F32)
        w2a2 = P.sbuf('w2a2', [128, D], BF16)
        g2 = P.sbuf('g2', [128, D], BF16)
        wrbd = P.sbuf('wrbd', [128, 8, 128], BF16); wibd = P.sbuf('wibd', [128, 8, 128], BF16)
        WCAP = 5632
        NWB = 3
        wbuf = [P.sbuf('wbuf%d' % i, [128, WCAP], BF16) for i in range(NWB)]
        wsem = [P.dma_sem() for i in range(NWB)]
        h32 = P.sbuf('h32', [128, 8, n], F32)
        hb = P.sbuf('hb', [128, 8, n], BF16)
        FB_ = [P.sbuf('F%d' % i, [128, 8, n], F32) for i in range(5)]
        HX = P.sbuf('HX', [128, 24, n], BF16)
        HB_ = [P.sbuf('H%d' % i, [128, 8, n], BF16) for i in range(4)]
        memT = HB_[3]
        pl = P.sbuf('pl', [128, 8, n + 3], F32)
        st1 = [P.sbuf('st%d' % i, [128, n], F32) for i in range(5)]
        carry_sh = P.sbuf('carry_sh', [128, 26], F32)
        carry_h = P.sbuf('carry_h', [128, 8], F32)
        A32 = P.sbuf('A32', [128, 8, 64], F32)
        A0b = [P.sbuf('A0b%d' % i, [128, 8, 64], BF16) for i in range(2)]
        RHSb = P.sbuf('RHSb', [128, 8, 64], BF16)
        Ub = P.sbuf('Ub', [128, 8, 64], BF16)
        WCs = P.sbuf('WCs', [128, 8, NCH], F32)
        X32 = [P.sbuf('X32_%d' % i, [128, 8, 64], F32) for i in range(2)]
        Xb = [P.sbuf('Xb_%d' % i, [128, 8, 64], BF16) for i in range(2)]
        PP = [[P.sbuf('PP_%d_%d' % (i, j), [128, 8, 128], BF16) for j in range(2)] for i in range(2)]
        LP = P.sbuf('LP', [128, 8, NCH, 128], BF16)
        PN = P.sbuf('PN', [128, 8, NCH, 128], BF16)
        XT = P.sbuf('XT', [128, 8, NCH, 64], BF16)
        mkT = P.sbuf('mkT', [128, 8, NMEM], BF16)
        mvb = P.sbuf('mvb', [128, 2, D], BF16)
        tok32 = [P.sbuf('tok32_%d' % i, [128, D], F32) for i in range(2)]
        osml = P.sbuf('osml', [128, 8, 8], F32)
        osh = P.sbuf('osh', [128, 26], F32)
        sgb = [P.sbuf('sgb%d' % i, [128, NT], F32) for i in range(2)]
        ps = P.psum('ps', [128, 8, 512], F32)
        dsem_c = [P.dma_sem() for i in range(4)]
        dsem_in = [P.dma_sem() for i in range(2)]
        dsem_out = [P.dma_sem() for i in range(2)]
        dsem_misc = P.dma_sem()

        bank_ctr = [0]
        tokctr = [0]

        def bank():
            b = bank_ctr[0] % 8
            bank_ctr[0] += 1
            return b

        def mm(out, lhsT, rhs, start, stop, reads, writes):
            P.op('pe', lambda e: e.matmul(out, lhsT=lhsT, rhs=rhs, start=start, stop=stop), reads=reads, writes=writes)

        def tr(out, in_, idn, reads, writes):
            P.op('pe', lambda e: e.transpose(out, in_, idn), reads=reads, writes=writes)

        def act(out, in_, func, reads, writes, bias=None, scale=None):
            kw = {}
            if bias is not None:
                kw['bias'] = bias
            if scale is not None:
                kw['scale'] = scale
            P.op('act', lambda e: e.activation(out=out, in_=in_, func=func, **kw), reads=reads, writes=writes)

        def tt(out, in0, in1, op, reads, writes, eng='dve'):
            P.op(eng, lambda e: e.tensor_tensor(out=out, in0=in0, in1=in1, op=op), reads=reads, writes=writes)

        def ts(out, in0, s1, s2, op0, op1, reads, writes, eng='dve'):
            if s2 is None:
                P.op(eng, lambda e: e.tensor_scalar(out=out, in0=in0, scalar1=s1, scalar2=None, op0=op0), reads=reads, writes=writes)
            else:
                P.op(eng, lambda e: e.tensor_scalar(out=out, in0=in0, scalar1=s1, scalar2=s2, op0=op0, op1=op1), reads=reads, writes=writes)

        def stt(out, in0, scalar, in1, op0, op1, reads, writes):
            P.op('dve', lambda e: e.scalar_tensor_tensor(out=out, in0=in0, scalar=scalar, in1=in1, op0=op0, op1=op1), reads=reads, writes=writes)

        def cp(out, in_, reads, writes, eng='dve'):
            if eng == 'act':
                act(out, in_, AF.Copy, reads, writes)
            else:
                P.op(eng, lambda e: e.tensor_copy(out=out, in_=in_), reads=reads, writes=writes)

        def recip(out, in_, reads, writes):
            P.op('dve', lambda e: e.reciprocal(out=out, in_=in_), reads=reads, writes=writes)

        def dma(eng, out, in_, reads, writes, dsem, is_out=False, **kw):
            P.op(eng, lambda e: e.dma_start(out=out, in_=in_, **kw), reads=reads, writes=writes, dsem=dsem, is_out=is_out)

        def dump(name, src_ap, key):
            if name in dbg:
                dma('sp', di['dbg_' + name], src_ap, [key], [], P.dma_sem(), is_out=True)

        PARTS = os.environ.get('PARTS', 'abcdefg')
        dma('sp', ident[:], di['c_all'][:, 0:128], [], ['ident'], dsem_c[0])
        dma('sp', mask[:], di['c_all'][:, 384:640], [], ['mask'], dsem_c[0])
        dma('sp', reset[:], di['c_all'][:, 640:640 + NT], [], ['reset'], dsem_c[0])
        P.barrier(['ident', 'mask', 'reset'])
        if 'b' in PARTS:
            dma('pool', identb[:], di['c_all'][:, 0:128], [], ['identb'], dsem_c[1])
            dma('pool', onesb[:], di['c_all'][:, 128:256], [], ['onesb'], dsem_c[1])
            dma('pool', bdb[:], di['c_all'][:, 256:384], [], ['bdb'], dsem_c[1])
            dma('pool', w2a2[0:64, :], di['decay_w2'], [], ['w2a2'], dsem_c[1])
            dma('pool', w2a2[64:128, :], di['aaa_a2'], [], ['w2a2'], dsem_c[1])
            dma('pool', g2[:], di['gate_g2'], [], ['g2'], dsem_c[1])
        if 'm' in PARTS or PARTS == 'abcdefg':
            P.op('dve', lambda e: e.memset(wrbd[:], 0.0), writes=['wrbd'])
            P.op('dve', lambda e: e.memset(wibd[:], 0.0), writes=['wibd'])
        for (wt, nm, key) in ([(wrbd, 'lru_wr', 'wrbd'), (wibd, 'lru_wi', 'wibd')] if 'c' in PARTS else []):
            src = di[nm].rearrange('(c two) i o -> two i c o', two=2)
            for par in range(2):
                dma('pool', wt[par * 64:(par + 1) * 64, :, par * 64:(par + 1) * 64], src[par], [], [key], dsem_c[1])
        P.barrier(['identb', 'onesb', 'bdb', 'w2a2', 'g2', 'wrbd', 'wibd'])
        prm = [tok32[0], tok32[1]]
        rows = []
        rows.append((lng[:].rearrange('p l c -> p (l c)'), di['prm'][0:32, :], 'lng'))
        rows.append((lnb[:].rearrange('p l c -> p (l c)'), di['prm'][32:64, :], 'lnb'))
        rows.append((mu[:], di['prm'][64:90, :], 'mu'))
        rows.append((cw[:].rearrange('p l c -> p (l c)'), di['prm'][90:122, :], 'cw'))
        for vi_, v in enumerate(VEC_NAMES):
            rows.append((vec[v][:], di['prm'][122 + 8 * vi_:130 + 8 * vi_, :], 'v_' + v))
        groups = [[]]
        cnt = 0
        for r_ in rows:
            k_ = r_[1].shape[0]
            if cnt + k_ > 128:
                groups.append([]); cnt = 0
            groups[-1].append((cnt, k_) + r_)
            cnt += k_
        for gi_, grp in enumerate(groups if 'd' in PARTS else []):
            tk = prm[gi_ % 2]
            tot = 0
            for (o_, k_, dst, src, key) in grp:
                dma('sp', tk[o_:o_ + k_, 0:128], src, [], [('tok32', gi_ % 2)], dsem_c[2 + gi_ % 2])
                tot = o_ + k_
            pb = bank()
            tr(ps[:, pb, 0:tot], tk[0:tot, 0:128], ident[0:tot, 0:tot], [('tok32', gi_ % 2), 'ident'], [('ps', pb)])
            for (o_, k_, dst, src, key) in grp:
                cp(dst, ps[:, pb, o_:o_ + k_], [('ps', pb)], [key])
        if 'e' in PARTS:
            ts(omu[:], mu[:], -1.0, 1.0, ALU.mult, ALU.add, ['mu'], ['omu'])
            ts(oka[:], vec['k_a'][:], -1.0, 1.0, ALU.mult, ALU.add, ['v_k_a'], ['oka'])
            act(lsp[:], vec['lru_lambda'][:], AF.Exp, ['v_lru_lambda'], ['lsp'], scale=-1.0)
            act(lsp[:], lsp[:], AF.Ln, ['lsp'], ['lsp'], bias=1.0)
            ts(lsp[:], lsp[:], -8.0, None, ALU.mult, None, ['lsp'], ['lsp'])

        def wblocks():
            def ffn_blocks(wi, wo):
                for g in range(11):
                    def f(buf, g=g, wi=wi):
                        v = buf[:, 0:4096].rearrange('p (k c) -> p k c', k=8)
                        return [(v[:, :, 0:256], di[wi][:, g * 256:(g + 1) * 256].rearrange('(k p) c -> p k c', p=128)),
                                (v[:, :, 256:512], di[wi][:, DFF + g * 256:DFF + (g + 1) * 256].rearrange('(k p) c -> p k c', p=128))]
                    yield ((wi, g), f)
                for mp in range(4):
                    def f(buf, mp=mp, wo=wo):
                        v = buf[:, 0:NJ * 256].rearrange('p (j c) -> p j c', j=NJ)
                        return [(v, di[wo][:, mp * 256:(mp + 1) * 256].rearrange('(j p) c -> p j c', p=128))]
                    yield ((wo, mp), f)

            def sq_blocks(w, ncols):
                nb = (ncols + 511) // 512
                for b in range(nb):
                    c0 = b * 512
                    cn = min(512, ncols - c0)
                    def f(buf, c0=c0, cn=cn, w=w):
                        v = buf[:, 0:8 * cn].rearrange('p (k c) -> p k c', k=8)
                        return [(v, di[w][:, c0:c0 + cn].rearrange('(k p) c -> p k c', p=128))]
                    yield ((w, b), f)
            yield from sq_blocks('xa_wk', D)
            yield from sq_blocks('xa_wv', D)
            for it in range(int(os.environ.get('NTI', NTILES)) + (1 if do_sample else 0)):
                yield from ffn_blocks('ffn1_wi', 'ffn1_wo')
                yield from sq_blocks('w_in', PW)
                yield from sq_blocks('w_mix_out', D)
                yield from sq_blocks('xa_wq', D)
                yield from sq_blocks('xa_wo', D)
                yield from ffn_blocks('ffn2_wi', 'ffn2_wo')

        wgen = wblocks()
        wstate = dict(issued=0, consumed=0, pending=[])

        def w_issue():
            try:
                tag, f = next(wgen)
            except StopIteration:
                return False
            i = wstate['issued'] % NWB
            for (dst, src) in f(wbuf[i]):
                dma('pool', dst, src, [], [('wbuf', i)], wsem[i])
            wstate['pending'].append((tag, i))
            wstate['issued'] += 1
            return True

        def w_next(tag):
            while wstate['issued'] - wstate['consumed'] < NWB:
                if not w_issue():
                    break
            t, i = wstate['pending'].pop(0)
            assert t == tag, (t, tag)
            wstate['consumed'] += 1
            return wbuf[i], ('wbuf', i)

        def sqview(buf, cn):
            return buf[:, 0:8 * cn].rearrange('p (k c) -> p k c', k=8)

        def load_tokens_fm(src_rows_ap, nrows, dst32, dstb, col0, dkey):
            tokctr[0] += 1
            i = tokctr[0] % 2 if 'A' in os.environ.get('TOG', 'A') else 0
            tk = tok32[i]
            dma('sp', tk[0:nrows, :], src_rows_ap, [], [('tok32', i)], dsem_in[i])
            for half in range(2):
                b = bank()
                for q in range(4):
                    kc = half * 4 + q
                    P.op('pe', lambda e, b=b, q=q, kc=kc: e.transpose(ps[:, b, q * 128:q * 128 + nrows], tk[0:nrows, kc * 128:(kc + 1) * 128], ident[0:nrows, 0:nrows]),
                         reads=[('tok32', i), 'ident'], writes=[('ps', b)], track=('W' not in os.environ.get('TOG', '')))
                src = ps[:, b, :].rearrange('p (q t) -> p q t', q=4)[:, :, 0:nrows]
                if dst32 is not None:
                    cp(dst32[:, half * 4:half * 4 + 4, col0:col0 + nrows], src, [('ps', b), ('tok32', i)], [dkey], eng='act')
                if dstb is not None:
                    cp(dstb[:, half * 4:half * 4 + 4, col0:col0 + nrows], src, [('ps', b)], ['hb' if dkey == 'h32' else 'H3'], eng='dve')

        def store_fm_tokens(src32, skey, col0, nrows, dst_rows_ap, nch=8, feat0=0):
            i = bank_ctr[0] % 2
            tk = tok32[i]
            for g0 in range(0, nch, 4):
                b = bank()
                gn = min(4, nch - g0)
                for q in range(gn):
                    tr(ps[0:nrows, b, q * 128:(q + 1) * 128], src32[:, g0 + q, col0:col0 + nrows], ident[:, :],
                       [skey, 'ident'], [('ps', b)])
                cp(tk[0:nrows, g0 * 128:(g0 + gn) * 128], ps[0:nrows, b, 0:gn * 128], [('ps', b)], [('tok32', i)], eng='act')
            dma('sp', dst_rows_ap, tk[0:nrows, 0:nch * 128], [('tok32', i)], [], dsem_out[i], is_out=True)

        def layernorm(idx, n, eps):
            zsq = HB_[0]
            cp(hb[:, :, 0:n], h32[:, :, 0:n], ['h32'], ['hb'], eng='act')
            act(zsq[:, :, 0:n], h32[:, :, 0:n], AF.Square, ['h32'], ['H0'])
            b1 = bank(); b2 = bank()
            for kc in range(8):
                mm(ps[:, b1, 0:n], onesb[:], hb[:, kc, 0:n], kc == 0, kc == 7, ['onesb', 'hb'], [('ps', b1)])
            for kc in range(8):
                mm(ps[:, b2, 0:n], onesb[:], zsq[:, kc, 0:n], kc == 0, kc == 7, ['onesb', 'H0'], [('ps', b2)])
            mean, msq, var, rstd, nmr = [s[:, 0:n] for s in st1]
            ts(mean, ps[:, b1, 0:n], 1.0 / D, None, ALU.mult, None, [('ps', b1)], ['st0'])
            tt(msq, mean, mean, ALU.mult, ['st0'], ['st1'])
            stt(var, ps[:, b2, 0:n], 1.0 / D, msq, ALU.mult, ALU.subtract, [('ps', b2), 'st1'], ['st2'])
            ts(var, var, 0.0, eps, ALU.max, ALU.add, ['st2'], ['st2'])
            act(var, var, AF.Sqrt, ['st2'], ['st2'])
            recip(rstd, var, ['st2'], ['st3'])
            tt(nmr, mean, rstd, ALU.mult, ['st0', 'st3'], ['st4'])
            tt(h32[:, :, 0:n], h32[:, :, 0:n], rstd.unsqueeze(1).to_broadcast([128, 8, n]), ALU.mult, ['h32', 'st3'], ['h32'])
            tt(h32[:, :, 0:n], h32[:, :, 0:n], nmr.unsqueeze(1).to_broadcast([128, 8, n]), ALU.subtract, ['h32', 'st4'], ['h32'])
            for kc in range(8):
                act(h32[:, kc, 0:n], h32[:, kc, 0:n], AF.Identity, ['h32', 'lng', 'lnb'], ['h32'],
                    bias=lnb[:, idx, kc:kc + 1], scale=lng[:, idx, kc:kc + 1])
            cp(hb[:, :, 0:n], h32[:, :, 0:n], ['h32'], ['hb'], eng='act')

        def ffn(wi, wo, ln_idx, n):
            actb = HX
            sg = [sgb[0][:, 0:n], sgb[1][:, 0:n]]
            for g in range(11):
                wb, wk = w_next((wi, g))
                wv = sqview(wb, 512)
                for jj in range(2):
                    j = 2 * g + jj
                    pg = bank(); pu = bank()
                    for kc in range(8):
                        mm(ps[:, pg, 0:n], wv[:, kc, jj * 128:(jj + 1) * 128], hb[:, kc, 0:n], kc == 0, kc == 7, [wk, 'hb'], [('ps', pg)])
                    for kc in range(8):
                        mm(ps[:, pu, 0:n], wv[:, kc, 256 + jj * 128:256 + (jj + 1) * 128], hb[:, kc, 0:n], kc == 0, kc == 7, [wk, 'hb'], [('ps', pu)])
                    act(sg[jj], ps[:, pg, 0:n], AF.Silu, [('ps', pg)], [('sg', jj)])
                    tt(actb[:, j, 0:n], sg[jj], ps[:, pu, 0:n], ALU.mult, [('sg', jj), ('ps', pu)], ['HX%d' % (j // 8)])
            for mp in range(4):
                wb, wk = w_next((wo, mp))
                wv = wb[:, 0:NJ * 256].rearrange('p (j c) -> p j c', j=NJ)
                for m2 in range(2):
                    m = 2 * mp + m2
                    po = bank()
                    for j in range(NJ):
                        mm(ps[:, po, 0:n], wv[:, j, m2 * 128:(m2 + 1) * 128], actb[:, j, 0:n], j == 0, j == NJ - 1, [wk, 'HX%d' % (j // 8)], [('ps', po)])
                    stt(h32[:, m, 0:n], ps[:, po, 0:n], 0.5 / ALPHA, h32[:, m, 0:n], ALU.mult, ALU.add, [('ps', po), 'h32'], ['h32'])
            layernorm(ln_idx, n, LN_EPS / (ALPHA * ALPHA))

        def proj(w, ncols, xin, xkey, n, evac):
            nb = (ncols + 511) // 512
            for b in range(nb):
                c0 = b * 512
                cn = min(512, ncols - c0)
                wb, wk = w_next((w, b))
                wv = sqview(wb, cn)
                for q in range(cn // 128):
                    m = c0 // 128 + q
                    pb = bank()
                    for kc in range(8):
                        mm(ps[:, pb, 0:n], wv[:, kc, q * 128:(q + 1) * 128], xin[:, kc, 0:n], kc == 0, kc == 7, [wk, xkey], [('ps', pb)])
                    evac(m, pb)

        def mem_kv():
            for r in range(2):
                load_tokens_fm(di['mem'][r * 128:(r + 1) * 128, :], 128, None, memT, r * 128, 'memT')
            for (w, outname, isk) in [('xa_wk', 'pmk', True), ('xa_wv', 'pmv', False)]:
                for b in range(2):
                    wb, wk = w_next((w, b))
                    wv = sqview(wb, 512)
                    for r in range(2):
                        pb = bank()
                        for kc in range(8):
                            mm(ps[:, pb, :], memT[:, kc, r * 128:(r + 1) * 128], wv[:, kc, :], kc == 0, kc == 7, ['H3', wk], [('ps', pb)])
                        i = bank_ctr[0] % 2
                        cp(tok32[i][:, 0:512], ps[:, pb, :], [('ps', pb)], [('tok32', i)], eng='act')
                        if not isk:
                            cp(mvb[:, r, b * 512:(b + 1) * 512], ps[:, pb, :], [('ps', pb)], ['mvb'], eng='dve')
                        dma('sp', di[outname][r * 128:(r + 1) * 128, b * 512:(b + 1) * 512], tok32[i][:, 0:512], [('tok32', i)], [], dsem_out[i], is_out=True)
                    if isk:
                        for q in range(4):
                            m = b * 4 + q
                            pb = bank()
                            for kc in range(8):
                                mm(ps[:, pb, 0:NMEM], wv[:, kc, q * 128:(q + 1) * 128], memT[:, kc, :], kc == 0, kc == 7, [wk, 'H3'], [('ps', pb)])
                            cp(mkT[:, m, :], ps[:, pb, 0:NMEM], [('ps', pb)], ['mkT'], eng='act')

        def mixer_prompt(ti):
            n = NT
            r32, k32, v32, a32, ls32 = FB_
            glb = HB_[1]; geb = HB_[2]; g0b = HB_[3]
            g1b = HX[:, 0:8, :]; xcb = HX[:, 8:16, :]
            first = (ti == 0)
            tmp = st1[0]

            def evac(m, pb):
                psn = ps[:, pb, 0:n]
                if m < 26:
                    act(tmp[:, 1:n], ps[:, pb, 0:n - 1], AF.Copy, [('ps', pb), 'mu'], ['st0'], scale=mu[:, m:m + 1])
                    if first:
                        P.op('dve', lambda e: e.memset(tmp[:, 0:1], 0.0), writes=['st0'])
                    else:
                        tt(tmp[:, 0:1], carry_sh[:, m:m + 1], mu[:, m:m + 1], ALU.mult, ['carry_sh', 'mu'], ['st0'])
                    cp(carry_sh[:, m:m + 1], ps[:, pb, n - 1:n], [('ps', pb)], ['carry_sh'])
                    if m < 24:
                        dst = [r32, k32, v32][m // 8][:, m % 8, :]
                        dkey = ['F0', 'F1', 'F2'][m // 8]
                        stt(dst, psn, omu[:, m:m + 1], tmp[:, 0:n], ALU.mult, ALU.add, [('ps', pb), 'omu', 'st0'], [dkey])
                    else:
                        xs_ = st1[1][:, 0:n]
                        stt(xs_, psn, omu[:, m:m + 1], tmp[:, 0:n], ALU.mult, ALU.add, [('ps', pb), 'omu', 'st0'], ['st1'])
                        lb = st1[2][:, 0:n].bitcast(BF16)[:, 0:n]
                        if m == 24:
                            act(lb[0:64, :], xs_[0:64, :], AF.Tanh, ['st1'], ['st2'])
                            cp(lb[64:128, :], xs_[64:128, :], ['st1'], ['st2'])
                            for (lo, dstt, dk, bvec) in [(0, ls32, 'F4', 'decay_w0'), (64, a32, 'F3', 'aaa_a0')]:
                                for q in range(8):
                                    p2 = bank()
                                    mm(ps[:, p2, 0:n], w2a2[lo:lo + 64, q * 128:(q + 1) * 128], lb[lo:lo + 64, :], True, True, ['w2a2', 'st2'], [('ps', p2)])
                                    act(dstt[:, q, :], ps[:, p2, 0:n], AF.Sigmoid, [('ps', p2), 'v_' + bvec], [dk], bias=vec[bvec][:, q:q + 1])
                        else:
                            act(lb, xs_, AF.Sigmoid, ['st1'], ['st2'])
                            for q in range(8):
                                p2 = bank()
                                mm(ps[:, p2, 0:n], g2[:, q * 128:(q + 1) * 128], lb, True, True, ['g2', 'st2'], [('ps', p2)])
                                cp(glb[:, q, :], ps[:, p2, 0:n], [('ps', p2)], ['H1'], eng='act')
                elif m < 34:
                    cp(pl[:, m - 26, 3:3 + n], psn, [('ps', pb)], ['pl'], eng='act')
                elif m < 42:
                    act(geb[:, m - 34, :], psn, AF.Gelu, [('ps', pb)], ['H2'])
                elif m < 50:
                    act(g0b[:, m - 42, :], psn, AF.Sigmoid, [('ps', pb)], ['H3'])
                else:
                    act(g1b[:, m - 50, :], psn, AF.Sigmoid, [('ps', pb)], ['HX0'])

            if first:
                P.op('dve', lambda e: e.memset(pl[:, :, 0:3], 0.0), writes=['pl'])
            else:
                cp(pl[:, :, 0:3], osml[:, :, 4:7], ['osml'], ['pl'])
            proj('w_in', PW, hb, 'hb', n, evac)
            cp(osml[:, :, 4:7], pl[:, :, n:n + 3], ['pl'], ['osml'])
            dump('r32', r32[:], 'F0'); dump('k32', k32[:], 'F1'); dump('v32', v32[:], 'F2'); dump('a32', a32[:], 'F3'); dump('ls32', ls32[:], 'F4')
            if ti == int(os.environ.get('NTI', NTILES)) - 1:
                cp(osml[:, :, 0:3], pl[:, :, n:n + 3], ['pl'], ['osml'])
                cp(osh[:], carry_sh[:], ['carry_sh'], ['osh'])
            return dict(r32=r32, k32=k32, v32=v32, a32=a32, ls32=ls32, glb=glb, geb=geb, g0b=g0b, g1b=g1b, xcb=xcb)

        def lru_prompt(ti, B):
            n = NT
            geb, g1b, xcb = B['geb'], B['g1b'], B['xcb']
            lmb = HX[:, 16:24, :]
            xc = st1[0][:, 0:n]; gr = st1[1][:, 0:n]; gi = st1[2][:, 0:n]; t3 = st1[3][:, 0:n]; hs = st1[4][:, 0:n]
            for c in range(8):
                act(xc, pl[:, c, 3:3 + n], AF.Identity, ['pl', 'cw', 'v_conv_b'], ['st0'], bias=vec['conv_b'][:, c:c + 1], scale=cw[:, 3, c:c + 1])
                for j in range(3):
                    stt(xc, pl[:, c, j:j + n], cw[:, j, c:c + 1], xc, ALU.mult, ALU.add, ['pl', 'cw', 'st0'], ['st0'])
                cp(xcb[:, c, :], xc, ['st0'], ['HX1'], eng='act')
                p1 = bank(); p2 = bank()
                mm(ps[:, p1, 0:n], wrbd[:, c, :], xcb[:, c, :], True, True, ['wrbd', 'HX1'], [('ps', p1)])
                mm(ps[:, p2, 0:n], wibd[:, c, :], xcb[:, c, :], True, True, ['wibd', 'HX1'], [('ps', p2)])
                act(gr, ps[:, p1, 0:n], AF.Sigmoid, [('ps', p1), 'v_lru_br'], ['st1'], bias=vec['lru_br'][:, c:c + 1])
                act(gi, ps[:, p2, 0:n], AF.Sigmoid, [('ps', p2), 'v_lru_bi'], ['st2'], bias=vec['lru_bi'][:, c:c + 1])
                act(gr, gr, AF.Exp, ['st1', 'lsp'], ['st1'], scale=lsp[:, c:c + 1])
                act(t3, gr, AF.Square, ['st1'], ['st3'])
                ts(t3, t3, -1.0, 1.0, ALU.mult, ALU.add, ['st3'], ['st3'])
                ts(t3, t3, 0.0, None, ALU.max, None, ['st3'], ['st3'])
                act(t3, t3, AF.Sqrt, ['st3'], ['st3'])
                tt(gi, gi, xc, ALU.mult, ['st2', 'st0'], ['st2'])
                tt(gi, gi, t3, ALU.mult, ['st2', 'st3'], ['st2'])
                if ti == 0:
                    P.op('dve', lambda e: e.tensor_tensor_scan(out=hs, data0=gr, data1=gi, initial=0.0, op0=ALU.mult, op1=ALU.add),
                         reads=['st1', 'st2'], writes=['st4'])
                else:
                    P.op('dve', lambda e, c=c: e.tensor_tensor_scan(out=hs, data0=gr, data1=gi, initial=carry_h[:, c:c + 1], op0=ALU.mult, op1=ALU.add),
                         reads=['st1', 'st2', 'carry_h'], writes=['st4'])
                cp(carry_h[:, c:c + 1], hs[:, n - 1:n], ['st4'], ['carry_h'])
                tt(t3, hs, geb[:, c, :], ALU.mult, ['st4', 'H2'], ['st3'])
                tt(lmb[:, c, :], t3, g1b[:, c, :], ALU.mult, ['st3', 'HX0'], ['HX2'])
            if ti == int(os.environ.get('NTI', NTILES)) - 1:
                cp(osml[:, :, 3:4], carry_h[:].unsqueeze(2), ['carry_h'], ['osml'])
            return lmb


        plb = pl[:].rearrange('p a b -> p (a b)').bitcast(BF16)
        plA = plb[:, 0:8 * NT].rearrange('p (a b) -> p a b', a=8)
        plB = plb[:, 8 * NT:16 * NT].rearrange('p (a b) -> p a b', a=8)

        def rwkv_core(B, state_in, state_out):
            n = NT
            r32, k32, v32, a32, ls32 = B['r32'], B['k32'], B['v32'], B['a32'], B['ls32']
            bc8 = lambda v: v[:].unsqueeze(2).to_broadcast([128, 8, n])
            QR = HX[:, 0:16, :].rearrange('p (k two) (c t) -> p k two c t', two=2, t=C)
            KT = HB_[0]; NB = hb
            kk32 = pl[:, :, 0:n]
            tt(kk32, k32[:], bc8(vec['k_k']), ALU.mult, ['F1', 'v_k_k'], ['pl'])
            act(KT[:], kk32, AF.Square, ['pl'], ['H0'])
            for kc in range(8):
                pb = bank()
                mm(ps[:, pb, 0:n], bdb[:], KT[:, kc, :], True, True, ['bdb', 'H0'], [('ps', pb)])
                s_ = st1[kc % 2][:, 0:n]; sk = 'st%d' % (kc % 2)
                act(s_, ps[:, pb, 0:n], AF.Sqrt, [('ps', pb)], [sk])
                ts(s_, s_, 1e-12, None, ALU.max, None, [sk], [sk])
                recip(s_, s_, [sk], [sk])
                tt(kk32[:, kc, :], kk32[:, kc, :], s_, ALU.mult, ['pl', sk], ['pl'])
            for kc in range(8):
                u_ = st1[2 + kc % 2][:, 0:n]; sk = 'st%d' % (2 + kc % 2)
                ts(u_, a32[:, kc, :], vec['k_a'][:, kc:kc + 1], oka[:, kc:kc + 1], ALU.mult, ALU.add, ['F3', 'v_k_a', 'oka'], [sk])
                tt(k32[:, kc, :], k32[:, kc, :], u_, ALU.mult, ['F1', sk], ['F1'])
            tt(a32[:], a32[:], kk32, ALU.mult, ['F3', 'pl'], ['F3'])
            rkb = KT
            for kc in range(8):
                u_ = st1[kc % 2][:, 0:n]; sk = 'st%d' % (kc % 2)
                tt(u_, r32[:, kc, :], k32[:, kc, :], ALU.mult, ['F0', 'F1'], [sk])
                ts(rkb[:, kc, :], u_, vec['r_k'][:, kc:kc + 1], None, ALU.mult, None, [sk, 'v_r_k'], ['H0'])
            for kc in range(8):
                cs = st1[0][:, 0:n]; dd = st1[1][:, 0:n]; Wi = st1[2][:, 0:n]; We = st1[3][:, 0:n]; Wv = st1[4][:, 0:n]
                P.op('dve', lambda e, kc=kc, cs=cs: e.tensor_tensor_scan(out=cs, data0=reset[:, 0:n], data1=ls32[:, kc, :], initial=0.0, op0=ALU.mult, op1=ALU.add),
                     reads=['reset', 'F4'], writes=['st0'])
                tt(dd, cs, ls32[:, kc, :], ALU.subtract, ['st0', 'F4'], ['st1'])
                act(Wi, cs, AF.Exp, ['st0'], ['st2'], scale=-C0)
                act(We, dd, AF.Exp, ['st1'], ['st3'], scale=-C0)
                act(Wv, cs, AF.Exp, ['st0'], ['st4'], scale=C0)
                c4 = lambda a: a.rearrange('p (c t) -> p c t', t=C)
                tt(QR[:, kc, 1, :, :], c4(r32[:, kc, :]), c4(Wi), ALU.mult, ['F0', 'st2'], ['HX0', 'HX1'])
                tt(QR[:, kc, 0, :, :], c4(kk32[:, kc, :]), c4(We), ALU.mult, ['pl', 'st3'], ['HX0', 'HX1'])
                cp(WCs[:, kc, :], c4(Wi)[:, :, C - 1], ['st2'], ['WCs'])
                tt(Wi, k32[:, kc, :], Wv, ALU.mult, ['F1', 'st4', 'st2'], ['st2'])
                stt(We, a32[:, kc, :], -1.0, Wv, ALU.mult, ALU.mult, ['F3', 'st4', 'st3'], ['st3'])
                cp(NB[:, kc, :], We, ['st3'], ['hb'], eng='act')
                cp(FB_[4][:, kc, :], Wi, ['st2'], ['F4'], eng='act')
            vb = plB
            cp(vb[:, :, 0:n], v32[:], ['F2'], ['pl'], eng='act')
            for kc in range(8):
                pb = bank()
                mm(ps[:, pb, 0:n], bdb[:], rkb[:, kc, :], True, True, ['bdb', 'H0'], [('ps', pb)])
                tt(v32[:, kc, :], v32[:, kc, :], ps[:, pb, 0:n], ALU.mult, ['F2', ('ps', pb)], ['F2'])
            cp(KT[:], FB_[4][:], ['F4'], ['H0'], eng='act')
            tmv = lambda a: a.rearrange('p a b -> p (a b)').rearrange('p (c x) -> p c x', c=NCH)
            vT = tmv(HB_[2][:]); kTt = tmv(plA); nbT = tmv(plB)

            def to_tokmajor(src, skey, dst, dkey):
                for c in range(NCH):
                    pb = bank()
                    pbv = ps[:, pb, :].bitcast(BF16)
                    for hp in range(8):
                        for par in range(2):
                            lo = par * 64
                            tr(pbv[lo:lo + 64, hp * 64:(hp + 1) * 64], src[lo:lo + 64, hp, c * C:(c + 1) * C], identb[lo:lo + 64, lo:lo + 64],
                               [skey, 'identb'], [('ps', pb)])
                    cp(dst[:, c, :], pbv[:, 0:512], [('ps', pb)], [dkey], eng=('act' if c % 2 else 'dve'))
            to_tokmajor(vb, 'pl', vT, 'H2')
            to_tokmajor(KT, 'H0', kTt, 'pl')
            to_tokmajor(NB, 'hb', nbT, 'pl')
            vTv = lambda c, hp, lo: vT[lo:lo + 64, c, hp * 64:(hp + 1) * 64]
            kTv = lambda c, hp, lo: kTt[lo:lo + 64, c, hp * 64:(hp + 1) * 64]
            nTv = lambda c, hp, lo: nbT[lo:lo + 64, c, hp * 64:(hp + 1) * 64]
            m_su_ui = mask[:, 0:128].unsqueeze(1).to_broadcast([128, 8, 128])
            m_sl = mask[:, 128:192].unsqueeze(1).to_broadcast([128, 8, 64])
            m_eye = mask[:, 192:256].unsqueeze(1).to_broadcast([128, 8, 64])
            for c0 in range(0, NCH, 2):
                ctx = []
                for s in range(2):
                    c = c0 + s
                    b1a = bank(); b1b = bank()
                    for hp in range(8):
                        for par in range(2):
                            lo = par * 64
                            bsel = b1a if hp < 4 else b1b
                            mm(ps[lo:lo + 64, bsel, (hp % 4) * 128:(hp % 4 + 1) * 128], KT[lo:lo + 64, hp, c * C:(c + 1) * C],
                               QR[lo:lo + 64, hp, :, c, :], True, True, ['H0', 'HX0', 'HX1'], [('ps', bsel)])
                    for (bsel, h0) in [(b1a, 0), (b1b, 4)]:
                        tt(LP[:, h0:h0 + 4, c, :], ps[:, bsel, :].rearrange('p (h x) -> p h x', h=4), m_su_ui[:, 0:4, :], ALU.mult,
                           [('ps', bsel), 'mask'], [('LP', c)])
                    b2a = bank(); b2b = bank()
                    for hp in range(8):
                        for par in range(2):
                            lo = par * 64
                            bsel = b2a if hp < 4 else b2b
                            mm(ps[lo:lo + 64, bsel, (hp % 4) * 128:(hp % 4 + 1) * 128], NB[lo:lo + 64, hp, c * C:(c + 1) * C],
                               QR[lo:lo + 64, hp, :, c, :], True, True, ['hb', 'HX0', 'HX1'], [('ps', bsel)])
                    for (bsel, h0) in [(b2a, 0), (b2b, 4)]:
                        tt(PN[:, h0:h0 + 4, c, :], ps[:, bsel, :].rearrange('p (h x) -> p h x', h=4), m_su_ui[:, 0:4, :], ALU.mult,
                           [('ps', bsel), 'mask'], [('PN', c)])
                    b3 = bank()
                    for hp in range(8):
                        for par in range(2):
                            lo = par * 64
                            mm(ps[lo:lo + 64, b3, hp * 64:(hp + 1) * 64], QR[lo:lo + 64, hp, 0, c, :], NB[lo:lo + 64, hp, c * C:(c + 1) * C],
                               True, True, ['HX0', 'HX1', 'hb'], [('ps', b3)])
                    pp = PP[s][0]
                    cp(pp[:, :, 0:64], PN[:, :, c, 0:64], [('PN', c)], [('PP', s, 0)], eng='act')
                    tt(pp[:, :, 64:128], ps[:, b3, :].rearrange('p (h x) -> p h x', h=8), m_sl, ALU.mult, [('ps', b3), 'mask'], [('PP', s, 0)])
                    tt(X32[s][:], PN[:, :, c, 0:64], m_eye, ALU.add, [('PN', c), 'mask'], [('X32', s)])
                    cp(Xb[s][:], X32[s][:], [('X32', s)], [('Xb', s)], eng='act')
                    ctx.append(c)
                for lvl in range(1, 6):
                    cur = (lvl - 1) % 2; nxt = lvl % 2
                    banks = []
                    for s in range(2):
                        ba = bank(); bb = bank()
                        src = PP[s][cur]
                        for hp in range(8):
                            for par in range(2):
                                lo = par * 64
                                bsel = ba if hp < 4 else bb
                                o0 = (hp % 4) * 128
                                mm(ps[lo:lo + 64, bsel, o0:o0 + 64], src[lo:lo + 64, hp, 64:128], src[lo:lo + 64, hp, 0:64], True, True,
                                   [('PP', s, cur)], [('ps', bsel)])
                                mm(ps[lo:lo + 64, bsel, o0 + 64:o0 + 128], src[lo:lo + 64, hp, 0:64], src[lo:lo + 64, hp, 64:128], True, True,
                                   [('PP', s, cur)], [('ps', bsel)])
                        banks.append((ba, bb))
                    for s in range(2):
                        ba, bb = banks[s]
                        dst = PP[s][nxt]
                        cp(dst[:, 0:4, :], ps[:, ba, :].rearrange('p (h x) -> p h x', h=4), [('ps', ba)], [('PP', s, nxt)], eng='act')
                        cp(dst[:, 4:8, :], ps[:, bb, :].rearrange('p (h x) -> p h x', h=4), [('ps', bb)], [('PP', s, nxt)], eng='dve')
                    xb_ = []
                    for s in range(2):
                        bx = bank()
                        src = PP[s][nxt]
                        for hp in range(8):
                            for par in range(2):
                                lo = par * 64
                                mm(ps[lo:lo + 64, bx, hp * 64:(hp + 1) * 64], src[lo:lo + 64, hp, 64:128], Xb[s][lo:lo + 64, hp, :], True, True,
                                   [('PP', s, nxt), ('Xb', s)], [('ps', bx)])
                        xb_.append(bx)
                    for s in range(2):
                        bx = xb_[s]
                        tt(X32[s][:], X32[s][:], ps[:, bx, :].rearrange('p (h x) -> p h x', h=8), ALU.add, [('X32', s), ('ps', bx)], [('X32', s)])
                        if lvl < 5:
                            cp(Xb[s][:], X32[s][:], [('X32', s)], [('Xb', s)], eng='act')
                        else:
                            cp(XT[:, :, ctx[s], :], X32[s][:], [('X32', s)], [('XT', ctx[s])], eng='act')
            Y32 = FB_[0]
            for c in range(NCH):
                state_in(c)
                cur = c % 2
                a0 = A0b[cur]
                bR = bank()
                for hp in range(8):
                    for par in range(2):
                        lo = par * 64
                        o = ps[lo:lo + 64, bR, hp * 64:(hp + 1) * 64]
                        mm(o, QR[lo:lo + 64, hp, 0, c, :], a0[lo:lo + 64, hp, :], True, False, ['HX0', 'HX1', ('A0b', cur)], [('ps', bR)])
                        mm(o, LP[lo:lo + 64, hp, c, 0:64], vTv(c, hp, lo), False, True, [('LP', c), 'H2'], [('ps', bR)])
                cp(RHSb[:], ps[:, bR, :].rearrange('p (h x) -> p h x', h=8), [('ps', bR)], ['RHSb'], eng='act')
                bU = bank()
                for hp in range(8):
                    for par in range(2):
                        lo = par * 64
                        mm(ps[lo:lo + 64, bU, hp * 64:(hp + 1) * 64], XT[lo:lo + 64, hp, c, :], RHSb[lo:lo + 64, hp, :], True, True,
                           [('XT', c), 'RHSb'], [('ps', bU)])
                cp(Ub[:], ps[:, bU, :].rearrange('p (h x) -> p h x', h=8), [('ps', bU)], ['Ub'], eng='act')
                bD = bank()
                for hp in range(8):
                    for par in range(2):
                        lo = par * 64
                        o = ps[lo:lo + 64, bD, hp * 64:(hp + 1) * 64]
                        mm(o, kTv(c, hp, lo), vTv(c, hp, lo), True, False, ['pl', 'H2'], [('ps', bD)])
                        mm(o, nTv(c, hp, lo), Ub[lo:lo + 64, hp, :], False, True, ['pl', 'Ub'], [('ps', bD)])
                bY = bank()
                for hp in range(8):
                    for par in range(2):
                        lo = par * 64
                        o = ps[lo:lo + 64, bY, hp * 64:(hp + 1) * 64]
                        mm(o, a0[lo:lo + 64, hp, :], QR[lo:lo + 64, hp, 1, c, :], True, False, [('A0b', cur), 'HX0', 'HX1'], [('ps', bY)])
                        mm(o, vTv(c, hp, lo), LP[lo:lo + 64, hp, c, 64:128], False, False, ['H2', ('LP', c)], [('ps', bY)])
                        mm(o, Ub[lo:lo + 64, hp, :], PN[lo:lo + 64, hp, c, 64:128], False, True, ['Ub', ('PN', c)], [('ps', bY)])
                tt(A32[:], A32[:], ps[:, bD, :].rearrange('p (h x) -> p h x', h=8), ALU.add, ['A32', ('ps', bD)], ['A32'])
                tt(A32[:], A32[:], WCs[:, :, c:c + 1].to_broadcast([128, 8, 64]), ALU.mult, ['A32', 'WCs'], ['A32'])
                cp(A0b[1 - cur][:], A32[:], ['A32'], [('A0b', 1 - cur)], eng='act')
                cp(Y32[:, :, c * C:(c + 1) * C], ps[:, bY, :].rearrange('p (h x) -> p h x', h=8), [('ps', bY)], ['F0'], eng='dve')
                state_out(c)
            return Y32, v32

        def rwkv_post(Y, Ykey, bonus, bkey, glb, g0b, lmb, mb, n):
            Yb = HB_[0]; ysq = hb
            cp(Yb[:, :, 0:n], Y[:, :, 0:n], [Ykey], ['H0'], eng='act')
            act(ysq[:, :, 0:n], Y[:, :, 0:n], AF.Square, [Ykey], ['hb'])
            for kc in range(8):
                b1 = bank(); b2 = bank()
                mm(ps[:, b1, 0:n], bdb[:], Yb[:, kc, 0:n], True, True, ['bdb', 'H0'], [('ps', b1)])
                mm(ps[:, b2, 0:n], bdb[:], ysq[:, kc, 0:n], True, True, ['bdb', 'hb'], [('ps', b2)])
                mean = st1[0][:, 0:n]; var = st1[1][:, 0:n]; t_ = st1[2][:, 0:n]
                ts(mean, ps[:, b1, 0:n], 1.0 / 64, None, ALU.mult, None, [('ps', b1)], ['st0'])
                tt(var, mean, mean, ALU.mult, ['st0'], ['st1'])
                stt(var, ps[:, b2, 0:n], 1.0 / 64, var, ALU.mult, ALU.subtract, [('ps', b2), 'st1'], ['st1'])
                ts(var, var, 0.0, GN_EPS, ALU.max, ALU.add, ['st1'], ['st1'])
                act(var, var, AF.Sqrt, ['st1'], ['st1'])
                recip(var, var, ['st1'], ['st1'])
                tt(t_, Y[:, kc, 0:n], mean, ALU.subtract, [Ykey, 'st0'], ['st2'])
                tt(t_, t_, var, ALU.mult, ['st2', 'st1'], ['st2'])
                act(t_, t_, AF.Identity, ['st2', 'v_gn_g', 'v_gn_b'], ['st2'], bias=vec['gn_b'][:, kc:kc + 1], scale=vec['gn_g'][:, kc:kc + 1])
                tt(t_, t_, bonus[:, kc, 0:n], ALU.add, ['st2', bkey], ['st2'])
                tt(t_, t_, glb[:, kc, 0:n], ALU.mult, ['st2', 'H1'], ['st2'])
                tt(t_, t_, g0b[:, kc, 0:n], ALU.mult, ['st2', 'H3'], ['st2'])
                tt(mb[:, kc, 0:n], t_, lmb[:, kc, 0:n], ALU.add, ['st2', 'HX2'], ['H2'])

        def resid_ln(w, xin, xkey, ln_idx, n):
            def evac(m, pb):
                stt(h32[:, m, 0:n], ps[:, pb, 0:n], 1.0 / ALPHA, h32[:, m, 0:n], ALU.mult, ALU.add, [('ps', pb), 'h32'], ['h32'])
            proj(w, D, xin, xkey, n, evac)
            layernorm(ln_idx, n, LN_EPS / (ALPHA * ALPHA))

        def xattn_prompt(n):
            qb = HB_[0]; ob = HB_[1]; pT = HX[:, 0:8, :]
            def evq(m, pb):
                cp(qb[:, m, 0:n], ps[:, pb, 0:n], [('ps', pb)], ['H0'], eng='act')
            proj('xa_wq', D, hb, 'hb', n, evq)
            for h in range(4):
                for mc in range(2):
                    pb = bank()
                    for dc in range(2):
                        mm(ps[:, pb, 0:n], mkT[:, 2 * h + dc, mc * 128:(mc + 1) * 128], qb[:, 2 * h + dc, 0:n], dc == 0, dc == 1, ['mkT', 'H0'], [('ps', pb)])
                    act(pT[:, 2 * h + mc, 0:n], ps[:, pb, 0:n], AF.Exp, [('ps', pb)], ['HX0'], scale=1.0 / 16.0)
                pd = bank()
                for mc in range(2):
                    mm(ps[:, pd, 0:n], onesb[:], pT[:, 2 * h + mc, 0:n], mc == 0, mc == 1, ['onesb', 'HX0'], [('ps', pd)])
                rd = st1[h % 2][:, 0:n]; rk_ = 'st%d' % (h % 2)
                recip(rd, ps[:, pd, 0:n], [('ps', pd)], [rk_])
                for dc in range(2):
                    po = bank()
                    for mc in range(2):
                        mm(ps[:, po, 0:n], mvb[:, mc, (2 * h + dc) * 128:(2 * h + dc + 1) * 128], pT[:, 2 * h + mc, 0:n], mc == 0, mc == 1, ['mvb', 'HX0'], [('ps', po)])
                    tt(ob[:, 2 * h + dc, 0:n], ps[:, po, 0:n], rd, ALU.mult, [('ps', po), rk_], ['H1'])
            resid_ln('xa_wo', ob, 'H1', 2, n)

        if stage >= 1:
            mem_kv()
        if 'm' in PARTS or PARTS == 'abcdefg':
            P.op('dve', lambda e: e.memset(A32[:], 0.0), writes=['A32'])
            P.op('dve', lambda e: e.memset(A0b[0][:], 0.0), writes=[('A0b', 0)])
        for ti in range(int(os.environ.get('NTI', NTILES)) if stage >= 9 else 1):
            TOG = os.environ.get('TOG', '')
            for r in range(1 if '1' in TOG else NT // 128):
                load_tokens_fm(di['xp'][ti * NT + r * 128: ti * NT + (r + 1) * 128, :], 128, h32, None if 'D' in TOG else hb, r * 128, 'h32')
            if stage >= 2:
                ffn('ffn1_wi', 'ffn1_wo', 0, NT)
            if ti == 0:
                dump('h1', h32[:], 'h32')
            if stage < 3:
                break
            B = mixer_prompt(ti)
            if stage < 4:
                break
            lmb = lru_prompt(ti, B)
            if ti == 0:
                dump('lm', lmb, 'HX2')
            if stage < 5:
                break
            Y32, bonus = rwkv_core(B, lambda c: None, lambda c: None)
            if ti == 0:
                dump('Y', Y32[:], 'F0')
            if stage < 6:
                break
            mb = HB_[2]
            rwkv_post(Y32, 'F0', bonus, 'F2', B['glb'], B['g0b'], lmb, mb, NT)
            resid_ln('w_mix_out', mb, 'H2', 1, NT)
            if ti == 0:
                dump('h2', h32[:], 'h32')
            if stage < 7:
                break
            xattn_prompt(NT)
            if ti == 0:
                dump('h3', h32[:], 'h32')
            if stage < 8:
                break
            ffn('ffn2_wi', 'ffn2_wo', 3, NT)
            for r in range(NT // 128):
                store_fm_tokens(h32, 'h32', r * 128, 128, di['yp'][ti * NT + r * 128: ti * NT + (r + 1) * 128, :])
        if stage < 9:
            P.emit()
            return nc, P
        def prompt_outputs():
            pass
            Sout = FB_[1].rearrange('p a b -> p (a b)')[:, 0:1024].rearrange('p (hp par k) -> p hp par k', hp=8, par=2)
            for g in range(2):
                pb = bank()
                for q in range(4):
                    hp = g * 4 + q
                    tr(ps[0:64, pb, q * 128:(q + 1) * 128], A32[:, hp, :], ident[:, :], ['A32', 'ident'], [('ps', pb)])
                cp(Sout[0:64, g * 4:(g + 1) * 4, :, :], ps[0:64, pb, :].rearrange('p (q par k) -> p q par k', q=4, par=2), [('ps', pb)], ['F1'], eng='act')
            dma('sp', di['prw'].rearrange('(hp par v) k -> v hp par k', hp=8, par=2), Sout[0:64, :, :, :], ['F1'], [], dsem_misc, is_out=True)
            osm2 = FB_[2]
            cp(osm2[:, 0:8, 0:3], osml[:, :, 0:3], ['osml'], ['F2'])
            cp(osm2[:, 0:8, 3:4], osml[:, :, 3:4], ['osml'], ['F2'])
            store_fm_tokens(osm2, 'F2', 0, 3, di['pcv'][:, :])
            store_fm_tokens(osm2, 'F2', 3, 1, di['plru'].rearrange('(o d) -> o d', o=1))
            osh3 = FB_[3]
            cp(osh3[:, 0:8, 0:1], osh[:, 0:8].unsqueeze(2), ['osh'], ['F3'])
            cp(osh3[:, 0:8, 1:2], osh[:, 8:16].unsqueeze(2), ['osh'], ['F3'])
            cp(osh3[:, 0:8, 2:3], osh[:, 16:24].unsqueeze(2), ['osh'], ['F3'])
            cp(osh3[:, 0:2, 3:4], osh[:, 24:26].unsqueeze(2), ['osh'], ['F3'])
            pshv = di['psh'].rearrange('(o d) -> o d', o=1)
            for q in range(3):
                store_fm_tokens(osh3, 'F3', q, 1, pshv[:, q * 1024:(q + 1) * 1024])
            store_fm_tokens(osh3, 'F3', 3, 1, pshv[:, 3072:3328], nch=2)


        if os.environ.get('NTI') != '0':
            prompt_outputs()
        def sample_path():
            n = NS
            sm = lambda nm, shp, dt=F32: P.sbuf(nm, shp, dt)
            rc = sm('s_rc', [128, 8, n]); kc_ = sm('s_kc', [128, 8, n]); vc = sm('s_vc', [128, 8, n]); ac = sm('s_ac', [128, 8, n]); lsc = sm('s_lsc', [128, 8, n])
            yc = sm('s_yc', [128, 8, n]); bonc = sm('s_bonc', [128, 8, n]); plc = sm('s_plc', [128, 8, n]); hsc = sm('s_hsc', [128, 8, n])
            prevS = sm('s_prev', [128, 26, n]); praw = sm('s_praw', [128, 26, n]); h0S = sm('s_h0', [128, 8, n]); scvT = sm('s_scvT', [128, 8, 3 * n])
            BD = tok32[1][:].rearrange('p (h x) -> p h x', h=8); ones32 = st1[4][:, 0:128]
            glb = HB_[1]; geb = HB_[2]; g0b = HB_[3]; g1b = HX[:, 0:8, :]; lmb = HX[:, 16:24, :]
            dsS = [P.dma_sem() for _ in range(7)]

            def load_fm(src_rows_ap, nrows, nchunks, dst, dkey, tki, sem):
                tk = tok32[tki]
                dma('sp', tk[0:nrows, 0:nchunks * 128], src_rows_ap, [], [('tok32', tki)], sem)
                for g0 in range(0, nchunks, 4):
                    gn = min(4, nchunks - g0)
                    b = bank()
                    for q in range(gn):
                        tr(ps[:, b, q * 128:q * 128 + nrows], tk[0:nrows, (g0 + q) * 128:(g0 + q + 1) * 128], ident[0:nrows, 0:nrows],
                           [('tok32', tki), 'ident'], [('ps', b)])
                    cp(dst[:, g0:g0 + gn, 0:nrows], ps[:, b, :].rearrange('p (q t) -> p q t', q=4)[:, 0:gn, 0:nrows], [('ps', b)], [dkey], eng='act')

            load_tokens_fm(di['xs'], n, h32, hb, 0, 'h32')
            for q in range(4):
                c0 = q * 8; cn = min(8, 26 - c0)
                tmpd = FB_[0] if q % 2 == 0 else FB_[1]
                load_fm(di['ssh'][:, c0 * 128:(c0 + cn) * 128], n, cn, tmpd, 'F%d' % (q % 2), q % 2, dsS[q % 2])
                cp(prevS[:, c0:c0 + cn, :], tmpd[:, 0:cn, 0:n], ['F%d' % (q % 2)], ['s_prev'])
            load_fm(di['slru'], n, 8, h0S, 's_h0', 0, dsS[0])
            load_fm(di['scv'].rearrange('b j d -> (b j) d'), 3 * n, 8, scvT, 's_scvT', 1, dsS[1])
            dma('sp', di['scv_o'][:, 0:2, :], di['scv'][:, 1:3, :], [], [], dsem_misc, is_out=True)

            ffn('ffn1_wi', 'ffn1_wo', 0, n)

            tmp = st1[0]

            def evac(m, pb):
                psn = ps[:, pb, 0:n]
                if m < 26:
                    cp(praw[:, m, :], psn, [('ps', pb)], ['s_praw'], eng='act')
                    ts(tmp[:, 0:n], prevS[:, m, :], mu[:, m:m + 1], None, ALU.mult, None, ['s_prev', 'mu'], ['st0'])
                    if m < 24:
                        dst = [rc, kc_, vc][m // 8][:, m % 8, :]
                        dkey = ['s_rc', 's_kc', 's_vc'][m // 8]
                        stt(dst, psn, omu[:, m:m + 1], tmp[:, 0:n], ALU.mult, ALU.add, [('ps', pb), 'omu', 'st0'], [dkey])
                    else:
                        xs_ = st1[1][:, 0:n]
                        stt(xs_, psn, omu[:, m:m + 1], tmp[:, 0:n], ALU.mult, ALU.add, [('ps', pb), 'omu', 'st0'], ['st1'])
                        lb = st1[2][:, 0:NT].bitcast(BF16)[:, 0:n]
                        if m == 24:
                            act(lb[0:64, :], xs_[0:64, :], AF.Tanh, ['st1'], ['st2'])
                            cp(lb[64:128, :], xs_[64:128, :], ['st1'], ['st2'])
                            for (lo, dstt, dk, bvec) in [(0, lsc, 's_lsc', 'decay_w0'), (64, ac, 's_ac', 'aaa_a0')]:
                                for q in range(8):
                                    p2 = bank()
                                    mm(ps[:, p2, 0:n], w2a2[lo:lo + 64, q * 128:(q + 1) * 128], lb[lo:lo + 64, :], True, True, ['w2a2', 'st2'], [('ps', p2)])
                                    act(dstt[:, q, :], ps[:, p2, 0:n], AF.Sigmoid, [('ps', p2), 'v_' + bvec], [dk], bias=vec[bvec][:, q:q + 1])
                        else:
                            act(lb, xs_, AF.Sigmoid, ['st1'], ['st2'])
                            for q in range(8):
                                p2 = bank()
                                mm(ps[:, p2, 0:n], g2[:, q * 128:(q + 1) * 128], lb, True, True, ['g2', 'st2'], [('ps', p2)])
                                cp(glb[:, q, 0:n], ps[:, p2, 0:n], [('ps', p2)], ['H1'], eng='act')
                elif m < 34:
                    cp(plc[:, m - 26, :], psn, [('ps', pb)], ['s_plc'], eng='act')
                elif m < 42:
                    act(geb[:, m - 34, 0:n], psn, AF.Gelu, [('ps', pb)], ['H2'])
                elif m < 50:
                    act(g0b[:, m - 42, 0:n], psn, AF.Sigmoid, [('ps', pb)], ['H3'])
                else:
                    act(g1b[:, m - 50, 0:n], psn, AF.Sigmoid, [('ps', pb)], ['HX0'])
            proj('w_in', PW, hb, 'hb', n, evac)
            for q in range(4):
                c0 = q * 8; cn = min(8, 26 - c0)
                store_fm_tokens(praw[:, c0:c0 + cn, :], 's_praw', 0, n, di['ssh_o'][:, c0 * 128:(c0 + cn) * 128], nch=cn)
            store_fm_tokens(plc, 's_plc', 0, n, di['scv_o'][:, 2, :])

            sc3 = scvT[:].rearrange('p c (b j) -> p c b j', j=3)
            xc = st1[0][:, 0:n]; gr = st1[1][:, 0:n]; gi = st1[2][:, 0:n]; t3 = st1[3][:, 0:n]; hs = st1[4][:, 0:n]
            xcb = HX[:, 8:16, :]
            for c in range(8):
                act(xc, plc[:, c, :], AF.Identity, ['s_plc', 'cw', 'v_conv_b'], ['st0'], bias=vec['conv_b'][:, c:c + 1], scale=cw[:, 3, c:c + 1])
                for j in range(3):
                    stt(xc, sc3[:, c, :, j], cw[:, j, c:c + 1], xc, ALU.mult, ALU.add, ['s_scvT', 'cw', 'st0'], ['st0'])
                cp(xcb[:, c, 0:n], xc, ['st0'], ['HX1'], eng='act')
                p1 = bank(); p2 = bank()
                mm(ps[:, p1, 0:n], wrbd[:, c, :], xcb[:, c, 0:n], True, True, ['wrbd', 'HX1'], [('ps', p1)])
                mm(ps[:, p2, 0:n], wibd[:, c, :], xcb[:, c, 0:n], True, True, ['wibd', 'HX1'], [('ps', p2)])
                act(gr, ps[:, p1, 0:n], AF.Sigmoid, [('ps', p1), 'v_lru_br'], ['st1'], bias=vec['lru_br'][:, c:c + 1])
                act(gi, ps[:, p2, 0:n], AF.Sigmoid, [('ps', p2), 'v_lru_bi'], ['st2'], bias=vec['lru_bi'][:, c:c + 1])
                act(gr, gr, AF.Exp, ['st1', 'lsp'], ['st1'], scale=lsp[:, c:c + 1])
                act(t3, gr, AF.Square, ['st1'], ['st3'])
                ts(t3, t3, -1.0, 1.0, ALU.mult, ALU.add, ['st3'], ['st3'])
                ts(t3, t3, 0.0, None, ALU.max, None, ['st3'], ['st3'])
                act(t3, t3, AF.Sqrt, ['st3'], ['st3'])
                tt(gi, gi, xc, ALU.mult, ['st2', 'st0'], ['st2'])
                tt(gi, gi, t3, ALU.mult, ['st2', 'st3'], ['st2'])
                tt(hs, gr, h0S[:, c, :], ALU.mult, ['st1', 's_h0'], ['st4'])
                tt(hsc[:, c, :], hs, gi, ALU.add, ['st4', 'st2'], ['s_hsc'])
                tt(t3, hsc[:, c, :], geb[:, c, 0:n], ALU.mult, ['s_hsc', 'H2'], ['st3'])
                tt(lmb[:, c, 0:n], t3, g1b[:, c, 0:n], ALU.mult, ['st3', 'HX0'], ['HX2'])
            store_fm_tokens(hsc, 's_hsc', 0, n, di['slru_o'])

            Sout = FB_[1].rearrange('p a b -> p (a b)')[:, 0:1024].rearrange('p (hp par k) -> p hp par k', hp=8, par=2)
            P.op('dve', lambda e: e.memset(tok32[1][:], 0.0), writes=[('tok32', 1)])
            for g in range(NS // NCH):
                Bp = dict(r32=FB_[0], k32=FB_[1], v32=FB_[2], a32=FB_[3], ls32=FB_[4])
                for (dstF, fk, srcc, sk) in [(FB_[0], 'F0', rc, 's_rc'), (FB_[1], 'F1', kc_, 's_kc'), (FB_[2], 'F2', vc, 's_vc'), (FB_[3], 'F3', ac, 's_ac'), (FB_[4], 'F4', lsc, 's_lsc')]:
                    P.op('dve', lambda e, dstF=dstF: e.memset(dstF[:], 0.0), writes=[fk])
                    cp(dstF[:].rearrange('p k (c t) -> p k c t', t=C)[:, :, :, 0], srcc[:, :, g * NCH:(g + 1) * NCH], [sk], [fk])

                def state_in(c, g=g):
                    b_ = g * NCH + c
                    src = di['srw'][b_].rearrange('(hp par v) k -> par v hp k', hp=8, par=2)
                    for par in range(2):
                        dma('sp', BD[par * 64:(par + 1) * 64, :, par * 64:(par + 1) * 64], src[par], [], [('tok32', 1)], dsS[3])
                    pb = bank()
                    for hp in range(8):
                        mm(ps[:, pb, hp * 64:(hp + 1) * 64], BD[:, hp, :], mask[:, 192:256], True, True, [('tok32', 1), 'mask'], [('ps', pb)])
                    v3 = ps[:, pb, :].rearrange('p (h x) -> p h x', h=8)
                    cp(A32[:], v3, [('ps', pb)], ['A32'], eng='dve')
                    cp(A0b[c % 2][:], v3, [('ps', pb)], [('A0b', c % 2)], eng='act')

                def state_out(c, g=g):
                    b_ = g * NCH + c
                    for gg in range(2):
                        pb = bank()
                        for q in range(4):
                            hp = gg * 4 + q
                            tr(ps[0:64, pb, q * 128:(q + 1) * 128], A32[:, hp, :], ident[:, :], ['A32', 'ident'], [('ps', pb)])
                        cp(Sout[0:64, gg * 4:(gg + 1) * 4, :, :], ps[0:64, pb, :].rearrange('p (q par k) -> p q par k', q=4, par=2), [('ps', pb)], ['F1'], eng='act')
                    dma('sp', di['srw_o'][b_].rearrange('(hp par v) k -> v hp par k', hp=8, par=2), Sout[0:64, :, :, :], ['F1'], [], dsS[4], is_out=True)
                Y32, bon = rwkv_core(Bp, state_in, state_out)
                cp(yc[:, :, g * NCH:(g + 1) * NCH], Y32[:].rearrange('p k (c t) -> p k c t', t=C)[:, :, :, 0], ['F0'], ['s_yc'])
                cp(bonc[:, :, g * NCH:(g + 1) * NCH], bon[:].rearrange('p k (c t) -> p k c t', t=C)[:, :, :, 0], ['F2'], ['s_bonc'])
            mb = HB_[2]
            rwkv_post(yc, 's_yc', bonc, 's_bonc', glb, g0b, lmb, mb, n)
            resid_ln('w_mix_out', mb, 'H2', 1, n)

            qc = FB_[0]; qT = FB_[1].rearrange('p a b -> p (a b)')[:, 0:1024]; sel = FB_[2].rearrange('p a b -> p (a b)')[:, 0:NS * 128].rearrange('p (b m) -> p b m', b=NS)
            Kb = FB_[3].rearrange('p a b -> p (a b)').rearrange('p (mc f) -> p mc f', mc=2)
            prod = FB_[4].rearrange('p a b -> p (a b)')[:, 0:1024]
            Vb = HB_[0][:].rearrange('p a b -> p (a b)').rearrange('p (mc f) -> p mc f', mc=2)
            ob = HB_[1]
            sc = st1[0][:, 0:128]; ex = st1[1][:, 0:128]; den = st1[2][:, 0:64]; pbf = st1[3][:, 0:NT].bitcast(BF16)[:, 0:128]

            def evq(m, pb):
                cp(qc[:, m, 0:n], ps[:, pb, 0:n], [('ps', pb)], ['F0'], eng='act')
            proj('xa_wq', D, hb, 'hb', n, evq)
            for g0 in range(0, 8, 4):
                pb = bank()
                for q in range(4):
                    tr(ps[0:n, pb, q * 128:(q + 1) * 128], qc[:, g0 + q, 0:n], ident[:, :], ['F0', 'ident'], [('ps', pb)])
                cp(qT[0:n, g0 * 128:(g0 + 4) * 128], ps[0:n, pb, :], [('ps', pb)], ['F1'], eng='act')
            cp(sel[0:n, :, :], ident[0:n, 0:n].unsqueeze(2).to_broadcast([n, n, 128]), ['ident'], ['F2'])
            for b_ in range(NS):
                dma('sp', Kb, di['cmk'][b_].rearrange('(mc p) f -> p mc f', p=128), [], ['F3'], dsS[5])
                pq = [bank(), bank()]
                for hf in range(2):
                    mm(ps[:, pq[hf], :], sel[0:n, b_, :], qT[0:n, hf * 512:(hf + 1) * 512], True, True, ['F2', 'F1'], [('ps', pq[hf])])
                for mc in range(2):
                    for hf in range(2):
                        tt(prod[:, hf * 512:(hf + 1) * 512], Kb[:, mc, hf * 512:(hf + 1) * 512], ps[:, pq[hf], :], ALU.mult, ['F3', ('ps', pq[hf])], ['F4'])
                    P.op('dve', lambda e, b_=b_, mc=mc: e.tensor_reduce(out=sc[:, (b_ * 2 + mc) * 4:(b_ * 2 + mc) * 4 + 4], in_=prod.rearrange('p (h d) -> p h d', h=4), axis=AX.X, op=ALU.add),
                         reads=['F4'], writes=['st0'])
            act(ex, sc, AF.Exp, ['st0'], ['st1'], scale=1.0 / 16.0)
            dma('sp', ones32, di['c_all'][:, 128:256], [], ['st4'], dsS[2])
            pdn = bank()
            mm(ps[:, pdn, 0:128], ones32, ex, True, True, ['st4', 'st1'], [('ps', pdn)])
            d4 = ps[:, pdn, 0:128].rearrange('p (b mc h) -> p b mc h', mc=2, h=4)
            den3 = den.rearrange('p (b h) -> p b h', h=4)
            cp(den3, d4[:, :, 0, :], [('ps', pdn)], ['st2'])
            tt(den3, den3, d4[:, :, 1, :], ALU.add, ['st2', ('ps', pdn)], ['st2'])
            recip(den, den, ['st2'], ['st2'])
            tt(pbf.rearrange('p (b mc h) -> p b mc h', mc=2, h=4), ex.rearrange('p (b mc h) -> p b mc h', mc=2, h=4),
               den3.unsqueeze(2).to_broadcast([128, NS, 2, 4]), ALU.mult, ['st1', 'st2'], ['st3'])
            po = bank()
            for b_ in range(NS):
                dma('pool', Vb, di['cmv'][b_].rearrange('(mc p) f -> p mc f', p=128), [], ['H0'], dsS[6])
                for c in range(8):
                    for mc in range(2):
                        col = (b_ * 2 + mc) * 4 + c // 2
                        mm(ps[:, po, c * NS + b_:c * NS + b_ + 1], Vb[:, mc, c * 128:(c + 1) * 128], pbf[:, col:col + 1], mc == 0, mc == 1, ['H0', 'st3'], [('ps', po)])
            cp(ob[:, :, 0:n], ps[:, po, 0:8 * NS].rearrange('p (c b) -> p c b', c=8), [('ps', po)], ['H1'], eng='act')
            resid_ln('xa_wo', ob, 'H1', 2, n)
            ffn('ffn2_wi', 'ffn2_wo', 3, n)
            store_fm_tokens(h32, 'h32', 0, n, di['ys'])

        if do_sample:
            sample_path()
        P.emit()
    return nc, P


_CACHE = {}


def _consts():
    a = np.arange(128) % 64
    b = np.arange(64)
    su = (a[:, None] < b[None, :]).astype(np.float32)
    ui = (a[:, None] <= b[None, :]).astype(np.float32)
    sl = (a[:, None] > b[None, :]).astype(np.float32)
    ey = (a[:, None] == b[None, :]).astype(np.float32)
    bd = np.zeros((128, 128), np.float32)
    bd[:64, :64] = 1.0
    bd[64:, 64:] = 1.0
    rs = np.ones((128, NT), np.float32)
    rs[:, ::C] = 0.0
    return {'c_all': np.ascontiguousarray(np.concatenate([np.eye(128, dtype=np.float32), np.ones((128, 128), np.float32), bd, su, ui, sl, ey, rs], axis=1))}


def make_in_maps(inputs):
    f = lambda a: np.ascontiguousarray(np.asarray(a, dtype=np.float32))
    shared = {}
    for nm in ['ffn1_wi', 'ffn1_wo', 'ffn2_wi', 'ffn2_wo', 'w_in', 'decay_w2', 'aaa_a2', 'gate_g2',
               'lru_wr', 'lru_wi', 'w_mix_out', 'xa_wq', 'xa_wk', 'xa_wv', 'xa_wo']:
        shared[nm] = np.ascontiguousarray(f(inputs[nm])[0])
    shared['prm'] = np.ascontiguousarray(np.concatenate(
        [f(inputs[nm])[0].reshape(-1, 128) for nm in ['ln_g', 'ln_b', 'shift_mu', 'conv_w'] + VEC_NAMES], axis=0))
    shared.update(_consts())
    maps = []
    for c in range(8):
        m = dict(shared)
        sl = slice(c * NS, (c + 1) * NS)
        m['xp'] = f(inputs['x_prompt'][c])
        m['mem'] = f(inputs['mem_prompt'][c])
        m['xs'] = f(inputs['x_sample'][sl, 0])
        m['cmk'] = f(inputs['cache_mem_k'][0, sl]).reshape(NS, NMEM, D)
        m['cmv'] = f(inputs['cache_mem_v'][0, sl]).reshape(NS, NMEM, D)
        m['srw'] = f(inputs['state_rwkv'][0, sl]).reshape(NS, D, 64)
        m['ssh'] = f(inputs['state_rwkv_shift'][0, sl])
        m['slru'] = f(inputs['state_lru'][0, sl])
        m['scv'] = f(inputs['state_conv'][0, sl])
        maps.append(m)
    return maps


def kernel(**inputs):
    if 'nc' not in _CACHE:
        _CACHE['nc'] = build()[0]
    nc = _CACHE['nc']
    maps = make_in_maps(inputs)
    res = run_bass_kernel_spmd(nc, maps, core_ids=list(range(8)))
    R = res.results
    cat = lambda k: np.stack([np.asarray(r[k], dtype=np.float32) for r in R])
    catc = lambda k: np.concatenate([np.asarray(r[k], dtype=np.float32) for r in R], axis=0)
    yp = cat('yp')
    ys = catc('ys').reshape(8 * NS, 1, D)
    pmk = cat('pmk').reshape(1, 8, NMEM, 4, 256)
    pmv = cat('pmv').reshape(1, 8, NMEM, 4, 256)
    prw = cat('prw').reshape(1, 8, 16, 64, 64)
    psh = cat('psh').reshape(1, 8, RP)
    plru = cat('plru').reshape(1, 8, D)
    pcv = cat('pcv').reshape(1, 8, 3, D)
    srw = catc('srw_o').reshape(1, 8 * NS, 16, 64, 64)
    ssh = catc('ssh_o').reshape(1, 8 * NS, RP)
    slru = catc('slru_o').reshape(1, 8 * NS, D)
    scv = catc('scv_o').reshape(1, 8 * NS, 3, D)
    return (yp, ys, pmk, pmv, prw, psh, plru, pcv, srw, ssh, slru, scv)
```

```python
import math
import os
import numpy as np
from contextlib import ExitStack
import concourse.bass as bass
import concourse.mybir as mybir
from concourse.bass_utils import run_bass_kernel_spmd

F32 = mybir.dt.float32
BF16 = mybir.dt.bfloat16
AF = mybir.ActivationFunctionType
ALU = mybir.AluOpType
AX = mybir.AxisListType

ENGS = ['pe', 'dve', 'act', 'pool', 'sp']

D = 1024
T = 2048
NT = 256
NTILES = T // NT
C = 64
NCH = NT // C
DFF = 2816
NJ = DFF // 128
RP = 3328
PW = 7424
NMEM = 256
NS = 16
ALPHA = 2.0 ** 0.25
LN_EPS = 1e-5
GN_EPS = 64e-5
C0 = math.exp(-0.5)


class DmaSem:
    def __init__(self, sem):
        self.sem = sem
        self.count = 0


class Prog:
    def __init__(self, nc, stack):
        self.nc = nc
        self.stack = stack
        self.ops = {e: [] for e in ENGS}
        self.last_w = {}
        self.readers = {}
        self.seen = {e: {} for e in ENGS}
        self.dsems = []
        self.out_tokens = []

    def dma_sem(self):
        s = DmaSem(self.stack.enter_context(self.nc.semaphore('dsem%d' % len(self.dsems))))
        self.dsems.append(s)
        return s

    def sbuf(self, name, shape, dt):
        return self.stack.enter_context(self.nc.sbuf_tensor(name, list(shape), dt))

    def psum(self, name, shape, dt):
        return self.stack.enter_context(self.nc.psum_tensor(name, list(shape), dt))

    def barrier(self, keys, engines=ENGS):
        if 'B' in os.environ.get('TOG', ''):
            return
        for e in engines:
            self.op(e, None, reads=keys, track=False)

    def op(self, eng, fn, reads=(), writes=(), dsem=None, is_out=False, track=True):
        isps = lambda k: isinstance(k, tuple) and k[0] == 'ps'
        writes = list(writes) + [k for k in reads if isps(k)]
        reads = [k for k in reads if not isps(k)]
        deps = []
        for k in reads:
            t = self.last_w.get(k)
            if t is not None:
                deps.append(t)
        for k in writes:
            t = self.last_w.get(k)
            if t is not None:
                deps.append(t)
            deps.extend(self.readers.get(k, {}).values())
        need = {}
        for t in deps:
            if t[0] == 'eng':
                if t[1] == eng and dsem is None and (eng == 'pe' or os.environ.get('NOSELF')):
                    continue
                key = ('eng', t[1])
            else:
                key = ('dma', id(t[1]))
            if need.get(key, (None, -1))[1] < t[2]:
                need[key] = (t[1], t[2])
        waits = []
        for key, (src, v) in need.items():
            if self.seen[eng].get(key, -1) >= v:
                continue
            self.seen[eng][key] = v
            waits.append((key[0], src, v))
        idx = len(self.ops[eng])
        self.ops[eng].append(dict(fn=fn, waits=waits, dsem=dsem, target=False, phase=getattr(self, 'phase', '')))
        if dsem is not None:
            dsem.count += 16
            tok = ('dma', dsem, dsem.count)
        else:
            tok = ('eng', eng, idx)
        for k in writes:
            self.last_w[k] = tok
            self.readers[k] = {}
        for k in (reads if track else ()):
            r = self.readers.setdefault(k, {})
            rk = (tok[0], tok[1] if tok[0] == 'eng' else id(tok[1]))
            if rk not in r or r[rk][2] < tok[2]:
                r[rk] = tok
        if is_out:
            self.out_tokens.append(tok)
        return tok

    def emit(self):
        nc = self.nc
        fin = {}
        for t in self.out_tokens:
            fin[id(t[1])] = (t[1], max(fin.get(id(t[1]), (None, 0))[1], t[2]))
        self.ops['sp'].append(dict(fn=None, waits=[('dma', s, v) for s, v in fin.values()], dsem=None, target=False))
        for e in ENGS:
            for o in self.ops[e]:
                for kind, src, v in o['waits']:
                    if kind == 'eng':
                        self.ops[src][v]['target'] = True
        semval = {}
        for e in ENGS:
            c = 0
            vals = []
            for o in self.ops[e]:
                if o['target']:
                    c += 1
                vals.append(c)
            semval[e] = vals
        esem = {e: self.stack.enter_context(nc.semaphore('esem_' + e)) for e in ENGS}
        handles = {'pe': 'tensor', 'dve': 'vector', 'act': 'scalar', 'pool': 'gpsimd', 'sp': 'sync'}
        with nc.Block() as block:
            def make(e):
                def body(eng):
                    for o in self.ops[e]:
                        for kind, src, v in o['waits']:
                            if kind == 'eng':
                                eng.wait_ge(esem[src], semval[src][v])
                            else:
                                eng.wait_ge(src.sem, v)
                        if o['fn'] is None:
                            continue
                        inst = o['fn'](eng)
                        if os.environ.get('ANNOT') and o.get('phase'):
                            inst.annotate(o['phase'])
                        if o['dsem'] is not None:
                            inst.then_inc(o['dsem'].sem, 16)
                        elif o['target']:
                            inst.then_inc(esem[e], 1)
                return body
            for e in ENGS:
                getattr(block, handles[e])(make(e))
        self.stats = {e: len(self.ops[e]) for e in ENGS}


VEC_NAMES = ['decay_w0', 'aaa_a0', 'k_k', 'k_a', 'r_k', 'gn_g', 'gn_b', 'conv_b', 'lru_br', 'lru_bi', 'lru_lambda']
W_NAMES = ['ffn1_wi', 'ffn1_wo', 'ffn2_wi', 'ffn2_wo', 'w_in', 'w_mix_out', 'xa_wq', 'xa_wk', 'xa_wv', 'xa_wo']


def build(dbg=None, do_sample=True, stage=99):
    nc = bass.Bass('TRN2', target_bir_lowering=False)
    di = {}

    DECL = os.environ.get('DECL')

    def din(name, shape):
        if DECL and name not in DECL.split(','):
            return None
        di[name] = nc.dram_tensor(name, list(shape), F32, kind='ExternalInput').ap()
        return di[name]

    def dout(name, shape):
        if DECL and name not in DECL.split(','):
            return None
        di[name] = nc.dram_tensor(name, list(shape), F32, kind='ExternalOutput').ap()
        return di[name]

    din('xp', [T, D]); din('mem', [NMEM, D])
    din('xs', [NS, D]); din('cmk', [NS, NMEM, D]); din('cmv', [NS, NMEM, D])
    din('srw', [NS, D, 64]); din('ssh', [NS, RP]); din('slru', [NS, D]); din('scv', [NS, 3, D])
    din('prm', [210, 128])
    din('ffn1_wi', [D, 2 * DFF]); din('ffn1_wo', [DFF, D]); din('ffn2_wi', [D, 2 * DFF]); din('ffn2_wo', [DFF, D])
    din('w_in', [D, PW])
    din('decay_w2', [64, D]); din('aaa_a2', [64, D]); din('gate_g2', [128, D])
    din('lru_wr', [16, 64, 64]); din('lru_wi', [16, 64, 64])
    for w in ['w_mix_out', 'xa_wq', 'xa_wk', 'xa_wv', 'xa_wo']:
        din(w, [D, D])
    din('c_all', [128, 640 + NT])
    dout('yp', [T, D]); dout('ys', [NS, D]); dout('pmk', [NMEM, D]); dout('pmv', [NMEM, D])
    dout('prw', [D, 64]); dout('psh', [RP]); dout('plru', [D]); dout('pcv', [3, D])
    dout('srw_o', [NS, D, 64]); dout('ssh_o', [NS, RP]); dout('slru_o', [NS, D]); dout('scv_o', [NS, 3, D])
    dbg = dbg or {}
    for k, shp in dbg.items():
        dout('dbg_' + k, shp)

    with ExitStack() as st:
        P = Prog(nc, st)
        n = NT
        ident = P.sbuf('ident', [128, 128], F32)
        identb = P.sbuf('identb', [128, 128], BF16)
        onesb = P.sbuf('onesb', [128, 128], BF16)
        bdb = P.sbuf('bdb', [128, 128], BF16)
        mask = P.sbuf('mask', [128, 256], F32)
        reset = P.sbuf('reset', [128, NT], F32)
        lng = P.sbuf('lng', [128, 4, 8], F32); lnb = P.sbuf('lnb', [128, 4, 8], F32)
        mu = P.sbuf('mu', [128, 26], F32); omu = P.sbuf('omu', [128, 26], F32)
        vec = {v: P.sbuf('v_' + v, [128, 8], F32) for v in VEC_NAMES}
        oka = P.sbuf('oka', [128, 8], F32)
        lsp = P.sbuf('lsp', [128, 8], F32)
        cw = P.sbuf('cw', [128, 4, 8], F32)
        w2a2 = P.sbuf('w2a2', [128, D], BF16)
        g2 = P.sbuf('g2', [128, D], BF16)
        wrbd = P.sbuf('wrbd', [128, 8, 128], BF16); wibd = P.sbuf('wibd', [128, 8, 128], BF16)
        WCAP = 5632
        NWB = 3
        wbuf = [P.sbuf('wbuf%d' % i, [128, WCAP], BF16) for i in range(NWB)]
        wsem = [P.dma_sem() for i in range(NWB)]
        h32 = P.sbuf('h32', [128, 8, n], F32)
        hb = P.sbuf('hb', [128, 8, n], BF16)
        FB_ = [P.sbuf('F%d' % i, [128, 8, n], F32) for i in range(5)]
        HX = P.sbuf('HX', [128, 24, n], BF16)
        HB_ = [P.sbuf('H%d' % i, [128, 8, n], BF16) for i in range(4)]
        memT = HB_[3]
        pl = P.sbuf('pl', [128, 8, n + 3], F32)
        st1 = [P.sbuf('st%d' % i, [128, n], F32) for i in range(5)]
        carry_sh = P.sbuf('carry_sh', [128, 26], F32)
        carry_h = P.sbuf('carry_h', [128, 8], F32)
        A32 = P.sbuf('A32', [128, 8, 64], F32)
        A0b = [P.sbuf('A0b%d' % i, [128, 8, 64], BF16) for i in range(2)]
        RHSb = P.sbuf('RHSb', [128, 8, 64], BF16)
        Ub = P.sbuf('Ub', [128, 8, 64], BF16)
        WCs = P.sbuf('WCs', [128, 8, NCH], F32)
        X32 = [P.sbuf('X32_%d' % i, [128, 8, 64], F32) for i in range(2)]
        Xb = [P.sbuf('Xb_%d' % i, [128, 8, 64], BF16) for i in range(2)]
        PP = [[P.sbuf('PP_%d_%d' % (i, j), [128, 8, 128], BF16) for j in range(2)] for i in range(2)]
        LP = P.sbuf('LP', [128, 8, NCH, 128], BF16)
        PN = P.sbuf('PN', [128, 8, NCH, 128], BF16)
        XT = P.sbuf('XT', [128, 8, NCH, 64], BF16)
        mkT = P.sbuf('mkT', [128, 8, NMEM], BF16)
        mvb = P.sbuf('mvb', [128, 2, D], BF16)
        tok32 = [P.sbuf('tok32_%d' % i, [128, D], F32) for i in range(2)]
        osml = P.sbuf('osml', [128, 8, 8], F32)
        osh = P.sbuf('osh', [128, 26], F32)
        sgb = [P.sbuf('sgb%d' % i, [128, NT], F32) for i in range(2)]
        ps = P.psum('ps', [128, 8, 512], F32)
        dsem_c = [P.dma_sem() for i in range(4)]
        dsem_in = [P.dma_sem() for i in range(2)]
        dsem_out = [P.dma_sem() for i in range(2)]
        dsem_misc = P.dma_sem()

        bank_ctr = [0]
        tokctr = [0]

        def bank():
            b = bank_ctr[0] % 8
            bank_ctr[0] += 1
            return b

        def mm(out, lhsT, rhs, start, stop, reads, writes):
            P.op('pe', lambda e: e.matmul(out, lhsT=lhsT, rhs=rhs, start=start, stop=stop), reads=reads, writes=writes)

        def tr(out, in_, idn, reads, writes):
            P.op('pe', lambda e: e.transpose(out, in_, idn), reads=reads, writes=writes)

        def act(out, in_, func, reads, writes, bias=None, scale=None):
            kw = {}
            if bias is not None:
                kw['bias'] = bias
            if scale is not None:
                kw['scale'] = scale
            P.op('act', lambda e: e.activation(out=out, in_=in_, func=func, **kw), reads=reads, writes=writes)

        def tt(out, in0, in1, op, reads, writes, eng='dve'):
            P.op(eng, lambda e: e.tensor_tensor(out=out, in0=in0, in1=in1, op=op), reads=reads, writes=writes)

        def ts(out, in0, s1, s2, op0, op1, reads, writes, eng='dve'):
            if s2 is None:
                P.op(eng, lambda e: e.tensor_scalar(out=out, in0=in0, scalar1=s1, scalar2=None, op0=op0), reads=reads, writes=writes)
            else:
                P.op(eng, lambda e: e.tensor_scalar(out=out, in0=in0, scalar1=s1, scalar2=s2, op0=op0, op1=op1), reads=reads, writes=writes)

        def stt(out, in0, scalar, in1, op0, op1, reads, writes):
            P.op('dve', lambda e: e.scalar_tensor_tensor(out=out, in0=in0, scalar=scalar, in1=in1, op0=op0, op1=op1), reads=reads, writes=writes)

        def cp(out, in_, reads, writes, eng='dve'):
            if eng == 'act':
                act(out, in_, AF.Copy, reads, writes)
            else:
                P.op(eng, lambda e: e.tensor_copy(out=out, in_=in_), reads=reads, writes=writes)

        def recip(out, in_, reads, writes):
            P.op('dve', lambda e: e.reciprocal(out=out, in_=in_), reads=reads, writes=writes)

        def dma(eng, out, in_, reads, writes, dsem, is_out=False, **kw):
            P.op(eng, lambda e: e.dma_start(out=out, in_=in_, **kw), reads=reads, writes=writes, dsem=dsem, is_out=is_out)

        def dump(name, src_ap, key):
            if name in dbg:
                dma('sp', di['dbg_' + name], src_ap, [key], [], P.dma_sem(), is_out=True)

        PARTS = os.environ.get('PARTS', 'abcdefg')
        dma('sp', ident[:], di['c_all'][:, 0:128], [], ['ident'], dsem_c[0])
        dma('sp', mask[:], di['c_all'][:, 384:640], [], ['mask'], dsem_c[0])
        dma('sp', reset[:], di['c_all'][:, 640:640 + NT], [], ['reset'], dsem_c[0])
        P.barrier(['ident', 'mask', 'reset'])
        if 'b' in PARTS:
            dma('pool', identb[:], di['c_all'][:, 0:128], [], ['identb'], dsem_c[1])
            dma('pool', onesb[:], di['c_all'][:, 128:256], [], ['onesb'], dsem_c[1])
            dma('pool', bdb[:], di['c_all'][:, 256:384], [], ['bdb'], dsem_c[1])
            dma('pool', w2a2[0:64, :], di['decay_w2'], [], ['w2a2'], dsem_c[1])
            dma('pool', w2a2[64:128, :], di['aaa_a2'], [], ['w2a2'], dsem_c[1])
            dma('pool', g2[:], di['gate_g2'], [], ['g2'], dsem_c[1])
        if 'm' in PARTS or PARTS == 'abcdefg':
            P.op('dve', lambda e: e.memset(wrbd[:], 0.0), writes=['wrbd'])
            P.op('dve', lambda e: e.memset(wibd[:], 0.0), writes=['wibd'])
        for (wt, nm, key) in ([(wrbd, 'lru_wr', 'wrbd'), (wibd, 'lru_wi', 'wibd')] if 'c' in PARTS else []):
            src = di[nm].rearrange('(c two) i o -> two i c o', two=2)
            for par in range(2):
                dma('pool', wt[par * 64:(par + 1) * 64, :, par * 64:(par + 1) * 64], src[par], [], [key], dsem_c[1])
        P.barrier(['identb', 'onesb', 'bdb', 'w2a2', 'g2', 'wrbd', 'wibd'])
        prm = [tok32[0], tok32[1]]
        rows = []
        rows.append((lng[:].rearrange('p l c -> p (l c)'), di['prm'][0:32, :], 'lng'))
        rows.append((lnb[:].rearrange('p l c -> p (l c)'), di['prm'][32:64, :], 'lnb'))
        rows.append((mu[:], di['prm'][64:90, :], 'mu'))
        rows.append((cw[:].rearrange('p l c -> p (l c)'), di['prm'][90:122, :], 'cw'))
        for vi_, v in enumerate(VEC_NAMES):
            rows.append((vec[v][:], di['prm'][122 + 8 * vi_:130 + 8 * vi_, :], 'v_' + v))
        groups = [[]]
        cnt = 0
        for r_ in rows:
            k_ = r_[1].shape[0]
            if cnt + k_ > 128:
                groups.append([]); cnt = 0
            groups[-1].append((cnt, k_) + r_)
            cnt += k_
        for gi_, grp in enumerate(groups if 'd' in PARTS else []):
            tk = prm[gi_ % 2]
            tot = 0
            for (o_, k_, dst, src, key) in grp:
                dma('sp', tk[o_:o_ + k_, 0:128], src, [], [('tok32', gi_ % 2)], dsem_c[2 + gi_ % 2])
                tot = o_ + k_
            pb = bank()
            tr(ps[:, pb, 0:tot], tk[0:tot, 0:128], ident[0:tot, 0:tot], [('tok32', gi_ % 2), 'ident'], [('ps', pb)])
            for (o_, k_, dst, src, key) in grp:
                cp(dst, ps[:, pb, o_:o_ + k_], [('ps', pb)], [key])
        if 'e' in PARTS:
            ts(omu[:], mu[:], -1.0, 1.0, ALU.mult, ALU.add, ['mu'], ['omu'])
            ts(oka[:], vec['k_a'][:], -1.0, 1.0, ALU.mult, ALU.add, ['v_k_a'], ['oka'])
            act(lsp[:], vec['lru_lambda'][:], AF.Exp, ['v_lru_lambda'], ['lsp'], scale=-1.0)
            act(lsp[:], lsp[:], AF.Ln, ['lsp'], ['lsp'], bias=1.0)
            ts(lsp[:], lsp[:], -8.0, None, ALU.mult, None, ['lsp'], ['lsp'])

        def wblocks():
            def ffn_blocks(wi, wo):
                for g in range(11):
                    def f(buf, g=g, wi=wi):
                        v = buf[:, 0:4096].rearrange('p (k c) -> p k c', k=8)
                        return [(v[:, :, 0:256], di[wi][:, g * 256:(g + 1) * 256].rearrange('(k p) c -> p k c', p=128)),
                                (v[:, :, 256:512], di[wi][:, DFF + g * 256:DFF + (g + 1) * 256].rearrange('(k p) c -> p k c', p=128))]
                    yield ((wi, g), f)
                for mp in range(4):
                    def f(buf, mp=mp, wo=wo):
                        v = buf[:, 0:NJ * 256].rearrange('p (j c) -> p j c', j=NJ)
                        return [(v, di[wo][:, mp * 256:(mp + 1) * 256].rearrange('(j p) c -> p j c', p=128))]
                    yield ((wo, mp), f)

            def sq_blocks(w, ncols):
                nb = (ncols + 511) // 512
                for b in range(nb):
                    c0 = b * 512
                    cn = min(512, ncols - c0)
                    def f(buf, c0=c0, cn=cn, w=w):
                        v = buf[:, 0:8 * cn].rearrange('p (k c) -> p k c', k=8)
                        return [(v, di[w][:, c0:c0 + cn].rearrange('(k p) c -> p k c', p=128))]
                    yield ((w, b), f)
            yield from sq_blocks('xa_wk', D)
            yield from sq_blocks('xa_wv', D)
            def one_pass():
                yield from ffn_blocks('ffn1_wi', 'ffn1_wo')
                yield from sq_blocks('w_in', PW)
                yield from sq_blocks('w_mix_out', D)
                yield from sq_blocks('xa_wq', D)
                yield from sq_blocks('xa_wo', D)
                yield from ffn_blocks('ffn2_wi', 'ffn2_wo')
            for it in range(int(os.environ.get('NTI', NTILES)) + (1 if do_sample else 0)):
                for blk, (tag, f) in enumerate(one_pass()):
                    yield (tag, f, it, blk)

        wgen = wblocks()
        wstate = dict(issued=0, consumed=0, pending=[])

        NBLK = 51
        wsc = nc.dram_tensor('wsc', [NBLK, 128, WCAP], BF16, kind='Internal').ap()
        wbsem = [P.dma_sem() for i in range(NWB)]

        def w_used(tag):
            if tag[0].endswith('_wi'):
                return 4096
            if tag[0].endswith('_wo') and tag[0].startswith('ffn'):
                return NJ * 256
            ncols = PW if tag[0] == 'w_in' else D
            return 8 * min(512, ncols - tag[1] * 512)

        def w_issue():
            try:
                item = next(wgen)
            except StopIteration:
                return False
            i = wstate['issued'] % NWB
            if len(item) == 2:
                tag, f = item
                for (dst, src) in f(wbuf[i]):
                    dma('pool', dst, src, [], [('wbuf', i)], wsem[i])
            else:
                tag, f, it, blk = item
                used = w_used(tag)
                if it == 0:
                    for (dst, src) in f(wbuf[i]):
                        dma('pool', dst, src, [], [('wbuf', i)], wsem[i])
                    dma('sp', wsc[blk, :, 0:used], wbuf[i][:, 0:used], [('wbuf', i)], [('wsc', blk)], wbsem[i])
                else:
                    dma('pool', wbuf[i][:, 0:used], wsc[blk, :, 0:used], [('wsc', blk)], [('wbuf', i)], wsem[i])
            wstate['pending'].append((tag, i))
            wstate['issued'] += 1
            return True

        def w_next(tag):
            while wstate['issued'] - wstate['consumed'] < NWB:
                if not w_issue():
                    break
            t, i = wstate['pending'].pop(0)
            assert t == tag, (t, tag)
            wstate['consumed'] += 1
            return wbuf[i], ('wbuf', i)

        def sqview(buf, cn):
            return buf[:, 0:8 * cn].rearrange('p (k c) -> p k c', k=8)

        def load_tokens_fm(src_rows_ap, nrows, dst32, dstb, col0, dkey):
            tokctr[0] += 1
            i = tokctr[0] % 2 if 'A' in os.environ.get('TOG', 'A') else 0
            tk = tok32[i]
            dma('sp', tk[0:nrows, :], src_rows_ap, [], [('tok32', i)], dsem_in[i])
            for half in range(2):
                b = bank()
                for q in range(4):
                    kc = half * 4 + q
                    P.op('pe', lambda e, b=b, q=q, kc=kc: e.transpose(ps[:, b, q * 128:q * 128 + nrows], tk[0:nrows, kc * 128:(kc + 1) * 128], ident[0:nrows, 0:nrows]),
                         reads=[('tok32', i), 'ident'], writes=[('ps', b)], track=('W' not in os.environ.get('TOG', '')))
                src = ps[:, b, :].rearrange('p (q t) -> p q t', q=4)[:, :, 0:nrows]
                if dst32 is not None:
                    cp(dst32[:, half * 4:half * 4 + 4, col0:col0 + nrows], src, [('ps', b), ('tok32', i)], [dkey], eng='act')
                if dstb is not None:
                    cp(dstb[:, half * 4:half * 4 + 4, col0:col0 + nrows], src, [('ps', b)], ['hb' if dkey == 'h32' else 'H3'], eng='dve')

        def store_fm_tokens(src32, skey, col0, nrows, dst_rows_ap, nch=8, feat0=0):
            i = bank_ctr[0] % 2
            tk = tok32[i]
            for g0 in range(0, nch, 4):
                b = bank()
                gn = min(4, nch - g0)
                for q in range(gn):
                    tr(ps[0:nrows, b, q * 128:(q + 1) * 128], src32[:, g0 + q, col0:col0 + nrows], ident[:, :],
                       [skey, 'ident'], [('ps', b)])
                cp(tk[0:nrows, g0 * 128:(g0 + gn) * 128], ps[0:nrows, b, 0:gn * 128], [('ps', b)], [('tok32', i)], eng='act')
            dma('sp', dst_rows_ap, tk[0:nrows, 0:nch * 128], [('tok32', i)], [], dsem_out[i], is_out=True)

        def layernorm(idx, n, eps):
            P.phase = 'layernorm'
            zsq = HB_[0]
            cp(hb[:, :, 0:n], h32[:, :, 0:n], ['h32'], ['hb'], eng='act')
            act(zsq[:, :, 0:n], h32[:, :, 0:n], AF.Square, ['h32'], ['H0'])
            b1 = bank(); b2 = bank()
            for kc in range(8):
                mm(ps[:, b1, 0:n], onesb[:], hb[:, kc, 0:n], kc == 0, kc == 7, ['onesb', 'hb'], [('ps', b1)])
            for kc in range(8):
                mm(ps[:, b2, 0:n], onesb[:], zsq[:, kc, 0:n], kc == 0, kc == 7, ['onesb', 'H0'], [('ps', b2)])
            mean, msq, var, rstd, nmr = [s[:, 0:n] for s in st1]
            ts(mean, ps[:, b1, 0:n], 1.0 / D, None, ALU.mult, None, [('ps', b1)], ['st0'])
            tt(msq, mean, mean, ALU.mult, ['st0'], ['st1'])
            stt(var, ps[:, b2, 0:n], 1.0 / D, msq, ALU.mult, ALU.subtract, [('ps', b2), 'st1'], ['st2'])
            ts(var, var, 0.0, eps, ALU.max, ALU.add, ['st2'], ['st2'])
            act(var, var, AF.Sqrt, ['st2'], ['st2'])
            recip(rstd, var, ['st2'], ['st3'])
            tt(nmr, mean, rstd, ALU.mult, ['st0', 'st3'], ['st4'])
            tt(h32[:, :, 0:n], h32[:, :, 0:n], rstd.unsqueeze(1).to_broadcast([128, 8, n]), ALU.mult, ['h32', 'st3'], ['h32'])
            tt(h32[:, :, 0:n], h32[:, :, 0:n], nmr.unsqueeze(1).to_broadcast([128, 8, n]), ALU.subtract, ['h32', 'st4'], ['h32'])
            for kc in range(8):
                act(h32[:, kc, 0:n], h32[:, kc, 0:n], AF.Identity, ['h32', 'lng', 'lnb'], ['h32'],
                    bias=lnb[:, idx, kc:kc + 1], scale=lng[:, idx, kc:kc + 1])
            cp(hb[:, :, 0:n], h32[:, :, 0:n], ['h32'], ['hb'], eng='act')

        def ffn(wi, wo, ln_idx, n):
            P.phase = 'ffn'
            actb = HX
            sg = [sgb[0][:, 0:n], sgb[1][:, 0:n]]
            for g in range(11):
                wb, wk = w_next((wi, g))
                wv = sqview(wb, 512)
                for jj in range(2):
                    j = 2 * g + jj
                    pg = bank(); pu = bank()
                    for kc in range(8):
                        mm(ps[:, pg, 0:n], wv[:, kc, jj * 128:(jj + 1) * 128], hb[:, kc, 0:n], kc == 0, kc == 7, [wk, 'hb'], [('ps', pg)])
                    for kc in range(8):
                        mm(ps[:, pu, 0:n], wv[:, kc, 256 + jj * 128:256 + (jj + 1) * 128], hb[:, kc, 0:n], kc == 0, kc == 7, [wk, 'hb'], [('ps', pu)])
                    act(sg[jj], ps[:, pg, 0:n], AF.Silu, [('ps', pg)], [('sg', jj)])
                    tt(actb[:, j, 0:n], sg[jj], ps[:, pu, 0:n], ALU.mult, [('sg', jj), ('ps', pu)], ['HX%d' % (j // 8)])
            for mp in range(4):
                wb, wk = w_next((wo, mp))
                wv = wb[:, 0:NJ * 256].rearrange('p (j c) -> p j c', j=NJ)
                for m2 in range(2):
                    m = 2 * mp + m2
                    po = bank()
                    for j in range(NJ):
                        mm(ps[:, po, 0:n], wv[:, j, m2 * 128:(m2 + 1) * 128], actb[:, j, 0:n], j == 0, j == NJ - 1, [wk, 'HX%d' % (j // 8)], [('ps', po)])
                    stt(h32[:, m, 0:n], ps[:, po, 0:n], 0.5 / ALPHA, h32[:, m, 0:n], ALU.mult, ALU.add, [('ps', po), 'h32'], ['h32'])
            layernorm(ln_idx, n, LN_EPS / (ALPHA * ALPHA))

        def proj(w, ncols, xin, xkey, n, evac):
            nb = (ncols + 511) // 512
            for b in range(nb):
                c0 = b * 512
                cn = min(512, ncols - c0)
                wb, wk = w_next((w, b))
                wv = sqview(wb, cn)
                for q in range(cn // 128):
                    m = c0 // 128 + q
                    pb = bank()
                    for kc in range(8):
                        mm(ps[:, pb, 0:n], wv[:, kc, q * 128:(q + 1) * 128], xin[:, kc, 0:n], kc == 0, kc == 7, [wk, xkey], [('ps', pb)])
                    evac(m, pb)

        def mem_kv():
            P.phase = 'mem_kv'
            for r in range(2):
                load_tokens_fm(di['mem'][r * 128:(r + 1) * 128, :], 128, None, memT, r * 128, 'memT')
            for (w, outname, isk) in [('xa_wk', 'pmk', True), ('xa_wv', 'pmv', False)]:
                for b in range(2):
                    wb, wk = w_next((w, b))
                    wv = sqview(wb, 512)
                    for r in range(2):
                        pb = bank()
                        for kc in range(8):
                            mm(ps[:, pb, :], memT[:, kc, r * 128:(r + 1) * 128], wv[:, kc, :], kc == 0, kc == 7, ['H3', wk], [('ps', pb)])
                        i = bank_ctr[0] % 2
                        cp(tok32[i][:, 0:512], ps[:, pb, :], [('ps', pb)], [('tok32', i)], eng='act')
                        if not isk:
                            cp(mvb[:, r, b * 512:(b + 1) * 512], ps[:, pb, :], [('ps', pb)], ['mvb'], eng='dve')
                        dma('sp', di[outname][r * 128:(r + 1) * 128, b * 512:(b + 1) * 512], tok32[i][:, 0:512], [('tok32', i)], [], dsem_out[i], is_out=True)
                    if isk:
                        for q in range(4):
                            m = b * 4 + q
                            pb = bank()
                            for kc in range(8):
                                mm(ps[:, pb, 0:NMEM], wv[:, kc, q * 128:(q + 1) * 128], memT[:, kc, :], kc == 0, kc == 7, [wk, 'H3'], [('ps', pb)])
                            cp(mkT[:, m, :], ps[:, pb, 0:NMEM], [('ps', pb)], ['mkT'], eng='act')

        def mixer_prompt(ti):
            P.phase = 'mixer_prompt'
            n = NT
            r32, k32, v32, a32, ls32 = FB_
            glb = HB_[1]; geb = HB_[2]; g0b = HB_[3]
            g1b = HX[:, 0:8, :]; xcb = HX[:, 8:16, :]
            first = (ti == 0)
            tmp = st1[0]

            def evac(m, pb):
                psn = ps[:, pb, 0:n]
                if m < 26:
                    act(tmp[:, 1:n], ps[:, pb, 0:n - 1], AF.Copy, [('ps', pb), 'mu'], ['st0'], scale=mu[:, m:m + 1])
                    if first:
                        P.op('dve', lambda e: e.memset(tmp[:, 0:1], 0.0), writes=['st0'])
                    else:
                        tt(tmp[:, 0:1], carry_sh[:, m:m + 1], mu[:, m:m + 1], ALU.mult, ['carry_sh', 'mu'], ['st0'])
                    cp(carry_sh[:, m:m + 1], ps[:, pb, n - 1:n], [('ps', pb)], ['carry_sh'])
                    if m < 24:
                        dst = [r32, k32, v32][m // 8][:, m % 8, :]
                        dkey = ['F0', 'F1', 'F2'][m // 8]
                        stt(dst, psn, omu[:, m:m + 1], tmp[:, 0:n], ALU.mult, ALU.add, [('ps', pb), 'omu', 'st0'], [dkey])
                    else:
                        xs_ = st1[1][:, 0:n]
                        stt(xs_, psn, omu[:, m:m + 1], tmp[:, 0:n], ALU.mult, ALU.add, [('ps', pb), 'omu', 'st0'], ['st1'])
                        lb = st1[2][:, 0:n].bitcast(BF16)[:, 0:n]
                        if m == 24:
                            act(lb[0:64, :], xs_[0:64, :], AF.Tanh, ['st1'], ['st2'])
                            cp(lb[64:128, :], xs_[64:128, :], ['st1'], ['st2'])
                            for (lo, dstt, dk, bvec) in [(0, ls32, 'F4', 'decay_w0'), (64, a32, 'F3', 'aaa_a0')]:
                                for q in range(8):
                                    p2 = bank()
                                    mm(ps[:, p2, 0:n], w2a2[lo:lo + 64, q * 128:(q + 1) * 128], lb[lo:lo + 64, :], True, True, ['w2a2', 'st2'], [('ps', p2)])
                                    act(dstt[:, q, :], ps[:, p2, 0:n], AF.Sigmoid, [('ps', p2), 'v_' + bvec], [dk], bias=vec[bvec][:, q:q + 1])
                        else:
                            act(lb, xs_, AF.Sigmoid, ['st1'], ['st2'])
                            for q in range(8):
                                p2 = bank()
                                mm(ps[:, p2, 0:n], g2[:, q * 128:(q + 1) * 128], lb, True, True, ['g2', 'st2'], [('ps', p2)])
                                cp(glb[:, q, :], ps[:, p2, 0:n], [('ps', p2)], ['H1'], eng='act')
                elif m < 34:
                    cp(pl[:, m - 26, 3:3 + n], psn, [('ps', pb)], ['pl'], eng='act')
                elif m < 42:
                    act(geb[:, m - 34, :], psn, AF.Gelu, [('ps', pb)], ['H2'])
                elif m < 50:
                    act(g0b[:, m - 42, :], psn, AF.Sigmoid, [('ps', pb)], ['H3'])
                else:
                    act(g1b[:, m - 50, :], psn, AF.Sigmoid, [('ps', pb)], ['HX0'])

            if first:
                P.op('dve', lambda e: e.memset(pl[:, :, 0:3], 0.0), writes=['pl'])
            else:
                cp(pl[:, :, 0:3], osml[:, :, 4:7], ['osml'], ['pl'])
            proj('w_in', PW, hb, 'hb', n, evac)
            cp(osml[:, :, 4:7], pl[:, :, n:n + 3], ['pl'], ['osml'])
            dump('r32', r32[:], 'F0'); dump('k32', k32[:], 'F1'); dump('v32', v32[:], 'F2'); dump('a32', a32[:], 'F3'); dump('ls32', ls32[:], 'F4')
            if ti == int(os.environ.get('NTI', NTILES)) - 1:
                cp(osml[:, :, 0:3], pl[:, :, n:n + 3], ['pl'], ['osml'])
                cp(osh[:], carry_sh[:], ['carry_sh'], ['osh'])
            return dict(r32=r32, k32=k32, v32=v32, a32=a32, ls32=ls32, glb=glb, geb=geb, g0b=g0b, g1b=g1b, xcb=xcb)

        def lru_prompt(ti, B):
            P.phase = 'lru_prompt'
            n = NT
            geb, g1b, xcb = B['geb'], B['g1b'], B['xcb']
            lmb = HX[:, 16:24, :]
            xc = st1[0][:, 0:n]; gr = st1[1][:, 0:n]; gi = st1[2][:, 0:n]; t3 = st1[3][:, 0:n]; hs = st1[4][:, 0:n]
            for c in range(8):
                act(xc, pl[:, c, 3:3 + n], AF.Identity, ['pl', 'cw', 'v_conv_b'], ['st0'], bias=vec['conv_b'][:, c:c + 1], scale=cw[:, 3, c:c + 1])
                for j in range(3):
                    stt(xc, pl[:, c, j:j + n], cw[:, j, c:c + 1], xc, ALU.mult, ALU.add, ['pl', 'cw', 'st0'], ['st0'])
                cp(xcb[:, c, :], xc, ['st0'], ['HX1'], eng='act')
                p1 = bank(); p2 = bank()
                mm(ps[:, p1, 0:n], wrbd[:, c, :], xcb[:, c, :], True, True, ['wrbd', 'HX1'], [('ps', p1)])
                mm(ps[:, p2, 0:n], wibd[:, c, :], xcb[:, c, :], True, True, ['wibd', 'HX1'], [('ps', p2)])
                act(gr, ps[:, p1, 0:n], AF.Sigmoid, [('ps', p1), 'v_lru_br'], ['st1'], bias=vec['lru_br'][:, c:c + 1])
                act(gi, ps[:, p2, 0:n], AF.Sigmoid, [('ps', p2), 'v_lru_bi'], ['st2'], bias=vec['lru_bi'][:, c:c + 1])
                act(gr, gr, AF.Exp, ['st1', 'lsp'], ['st1'], scale=lsp[:, c:c + 1])
                act(t3, gr, AF.Square, ['st1'], ['st3'])
                ts(t3, t3, -1.0, 1.0, ALU.mult, ALU.add, ['st3'], ['st3'])
                ts(t3, t3, 0.0, None, ALU.max, None, ['st3'], ['st3'])
                act(t3, t3, AF.Sqrt, ['st3'], ['st3'])
                tt(gi, gi, xc, ALU.mult, ['st2', 'st0'], ['st2'])
                tt(gi, gi, t3, ALU.mult, ['st2', 'st3'], ['st2'])
                if ti == 0:
                    P.op('dve', lambda e: e.tensor_tensor_scan(out=hs, data0=gr, data1=gi, initial=0.0, op0=ALU.mult, op1=ALU.add),
                         reads=['st1', 'st2'], writes=['st4'])
                else:
                    P.op('dve', lambda e, c=c: e.tensor_tensor_scan(out=hs, data0=gr, data1=gi, initial=carry_h[:, c:c + 1], op0=ALU.mult, op1=ALU.add),
                         reads=['st1', 'st2', 'carry_h'], writes=['st4'])
                cp(carry_h[:, c:c + 1], hs[:, n - 1:n], ['st4'], ['carry_h'])
                tt(t3, hs, geb[:, c, :], ALU.mult, ['st4', 'H2'], ['st3'])
                tt(lmb[:, c, :], t3, g1b[:, c, :], ALU.mult, ['st3', 'HX0'], ['HX2'])
            if ti == int(os.environ.get('NTI', NTILES)) - 1:
                cp(osml[:, :, 3:4], carry_h[:].unsqueeze(2), ['carry_h'], ['osml'])
            return lmb


        plb = pl[:].rearrange('p a b -> p (a b)').bitcast(BF16)
        plA = plb[:, 0:8 * NT].rearrange('p (a b) -> p a b', a=8)
        plB = plb[:, 8 * NT:16 * NT].rearrange('p (a b) -> p a b', a=8)

        def rwkv_core(B, state_in, state_out, skip_inverse=False):
            P.phase = 'rwkv_core'
            n = NT
            r32, k32, v32, a32, ls32 = B['r32'], B['k32'], B['v32'], B['a32'], B['ls32']
            bc8 = lambda v: v[:].unsqueeze(2).to_broadcast([128, 8, n])
            QR = HX[:, 0:16, :].rearrange('p (k two) (c t) -> p k two c t', two=2, t=C)
            KT = HB_[0]; NB = hb
            kk32 = pl[:, :, 0:n]
            tt(kk32, k32[:], bc8(vec['k_k']), ALU.mult, ['F1', 'v_k_k'], ['pl'])
            act(KT[:], kk32, AF.Square, ['pl'], ['H0'])
            for kc in range(8):
                pb = bank()
                mm(ps[:, pb, 0:n], bdb[:], KT[:, kc, :], True, True, ['bdb', 'H0'], [('ps', pb)])
                s_ = st1[kc % 2][:, 0:n]; sk = 'st%d' % (kc % 2)
                act(s_, ps[:, pb, 0:n], AF.Sqrt, [('ps', pb)], [sk])
                ts(s_, s_, 1e-12, None, ALU.max, None, [sk], [sk])
                recip(s_, s_, [sk], [sk])
                tt(kk32[:, kc, :], kk32[:, kc, :], s_, ALU.mult, ['pl', sk], ['pl'])
            for kc in range(8):
                u_ = st1[2 + kc % 2][:, 0:n]; sk = 'st%d' % (2 + kc % 2)
                ts(u_, a32[:, kc, :], vec['k_a'][:, kc:kc + 1], oka[:, kc:kc + 1], ALU.mult, ALU.add, ['F3', 'v_k_a', 'oka'], [sk])
                tt(k32[:, kc, :], k32[:, kc, :], u_, ALU.mult, ['F1', sk], ['F1'])
            tt(a32[:], a32[:], kk32, ALU.mult, ['F3', 'pl'], ['F3'])
            rkb = KT
            for kc in range(8):
                u_ = st1[kc % 2][:, 0:n]; sk = 'st%d' % (kc % 2)
                tt(u_, r32[:, kc, :], k32[:, kc, :], ALU.mult, ['F0', 'F1'], [sk])
                ts(rkb[:, kc, :], u_, vec['r_k'][:, kc:kc + 1], None, ALU.mult, None, [sk, 'v_r_k'], ['H0'])
            for kc in range(8):
                cs = st1[0][:, 0:n]; dd = st1[1][:, 0:n]; Wi = st1[2][:, 0:n]; We = st1[3][:, 0:n]; Wv = st1[4][:, 0:n]
                P.op('dve', lambda e, kc=kc, cs=cs: e.tensor_tensor_scan(out=cs, data0=reset[:, 0:n], data1=ls32[:, kc, :], initial=0.0, op0=ALU.mult, op1=ALU.add),
                     reads=['reset', 'F4'], writes=['st0'])
                tt(dd, cs, ls32[:, kc, :], ALU.subtract, ['st0', 'F4'], ['st1'])
                act(Wi, cs, AF.Exp, ['st0'], ['st2'], scale=-C0)
                act(We, dd, AF.Exp, ['st1'], ['st3'], scale=-C0)
                act(Wv, cs, AF.Exp, ['st0'], ['st4'], scale=C0)
                c4 = lambda a: a.rearrange('p (c t) -> p c t', t=C)
                tt(QR[:, kc, 1, :, :], c4(r32[:, kc, :]), c4(Wi), ALU.mult, ['F0', 'st2'], ['HX0', 'HX1'])
                tt(QR[:, kc, 0, :, :], c4(kk32[:, kc, :]), c4(We), ALU.mult, ['pl', 'st3'], ['HX0', 'HX1'])
                cp(WCs[:, kc, :], c4(Wi)[:, :, C - 1], ['st2'], ['WCs'])
                tt(Wi, k32[:, kc, :], Wv, ALU.mult, ['F1', 'st4', 'st2'], ['st2'])
                stt(We, a32[:, kc, :], -1.0, Wv, ALU.mult, ALU.mult, ['F3', 'st4', 'st3'], ['st3'])
                cp(NB[:, kc, :], We, ['st3'], ['hb'], eng='act')
                cp(FB_[4][:, kc, :], Wi, ['st2'], ['F4'], eng='act')
            vb = plB
            cp(vb[:, :, 0:n], v32[:], ['F2'], ['pl'], eng='act')
            for kc in range(8):
                pb = bank()
                mm(ps[:, pb, 0:n], bdb[:], rkb[:, kc, :], True, True, ['bdb', 'H0'], [('ps', pb)])
                tt(v32[:, kc, :], v32[:, kc, :], ps[:, pb, 0:n], ALU.mult, ['F2', ('ps', pb)], ['F2'])
            cp(KT[:], FB_[4][:], ['F4'], ['H0'], eng='act')
            tmv = lambda a: a.rearrange('p a b -> p (a b)').rearrange('p (c x) -> p c x', c=NCH)
            vT = tmv(HB_[2][:]); kTt = tmv(plA); nbT = tmv(plB)

            def to_tokmajor(src, skey, dst, dkey):
                for c in range(NCH):
                    pb = bank()
                    pbv = ps[:, pb, :].bitcast(BF16)
                    for hp in range(8):
                        for par in range(2):
                            lo = par * 64
                            tr(pbv[lo:lo + 64, hp * 64:(hp + 1) * 64], src[lo:lo + 64, hp, c * C:(c + 1) * C], identb[lo:lo + 64, lo:lo + 64],
                               [skey, 'identb'], [('ps', pb)])
                    cp(dst[:, c, :], pbv[:, 0:512], [('ps', pb)], [dkey], eng=('act' if c % 2 else 'dve'))
            to_tokmajor(vb, 'pl', vT, 'H2')
            to_tokmajor(KT, 'H0', kTt, 'pl')
            to_tokmajor(NB, 'hb', nbT, 'pl')
            vTv = lambda c, hp, lo: vT[lo:lo + 64, c, hp * 64:(hp + 1) * 64]
            kTv = lambda c, hp, lo: kTt[lo:lo + 64, c, hp * 64:(hp + 1) * 64]
            nTv = lambda c, hp, lo: nbT[lo:lo + 64, c, hp * 64:(hp + 1) * 64]
            m_su_ui = mask[:, 0:128].unsqueeze(1).to_broadcast([128, 8, 128])
            m_sl = mask[:, 128:192].unsqueeze(1).to_broadcast([128, 8, 64])
            m_eye = mask[:, 192:256].unsqueeze(1).to_broadcast([128, 8, 64])
            for c0 in range(0, NCH, 2):
                ctx = []
                for s in range(2):
                    c = c0 + s
                    b1a = bank(); b1b = bank()
                    for hp in range(8):
                        for par in range(2):
                            lo = par * 64
                            bsel = b1a if hp < 4 else b1b
                            mm(ps[lo:lo + 64, bsel, (hp % 4) * 128:(hp % 4 + 1) * 128], KT[lo:lo + 64, hp, c * C:(c + 1) * C],
                               QR[lo:lo + 64, hp, :, c, :], True, True, ['H0', 'HX0', 'HX1'], [('ps', bsel)])
                    for (bsel, h0) in [(b1a, 0), (b1b, 4)]:
                        tt(LP[:, h0:h0 + 4, c, :], ps[:, bsel, :].rearrange('p (h x) -> p h x', h=4), m_su_ui[:, 0:4, :], ALU.mult,
                           [('ps', bsel), 'mask'], [('LP', c)])
                    b2a = bank(); b2b = bank()
                    for hp in range(8):
                        for par in range(2):
                            lo = par * 64
                            bsel = b2a if hp < 4 else b2b
                            mm(ps[lo:lo + 64, bsel, (hp % 4) * 128:(hp % 4 + 1) * 128], NB[lo:lo + 64, hp, c * C:(c + 1) * C],
                               QR[lo:lo + 64, hp, :, c, :], True, True, ['hb', 'HX0', 'HX1'], [('ps', bsel)])
                    for (bsel, h0) in [(b2a, 0), (b2b, 4)]:
                        tt(PN[:, h0:h0 + 4, c, :], ps[:, bsel, :].rearrange('p (h x) -> p h x', h=4), m_su_ui[:, 0:4, :], ALU.mult,
                           [('ps', bsel), 'mask'], [('PN', c)])
                    b3 = bank()
                    for hp in range(8):
                        for par in range(2):
                            lo = par * 64
                            mm(ps[lo:lo + 64, b3, hp * 64:(hp + 1) * 64], QR[lo:lo + 64, hp, 0, c, :], NB[lo:lo + 64, hp, c * C:(c + 1) * C],
                               True, True, ['HX0', 'HX1', 'hb'], [('ps', b3)])
                    pp = PP[s][0]
                    cp(pp[:, :, 0:64], PN[:, :, c, 0:64], [('PN', c)], [('PP', s, 0)], eng='act')
                    tt(pp[:, :, 64:128], ps[:, b3, :].rearrange('p (h x) -> p h x', h=8), m_sl, ALU.mult, [('ps', b3), 'mask'], [('PP', s, 0)])
                    tt(X32[s][:], PN[:, :, c, 0:64], m_eye, ALU.add, [('PN', c), 'mask'], [('X32', s)])
                    cp(Xb[s][:], X32[s][:], [('X32', s)], [('Xb', s)], eng='act')
                    ctx.append(c)
                if skip_inverse:
                    for s_ in range(2):
                        cp(XT[:, :, ctx[s_], :], X32[s_][:], [('X32', s_)], [('XT', ctx[s_])], eng='act')
                for lvl in ([] if skip_inverse else range(1, 6)):
                    cur = (lvl - 1) % 2; nxt = lvl % 2
                    banks = []
                    for s in range(2):
                        ba = bank(); bb = bank()
                        src = PP[s][cur]
                        for hp in range(8):
                            for par in range(2):
                                lo = par * 64
                                bsel = ba if hp < 4 else bb
                                o0 = (hp % 4) * 128
                                mm(ps[lo:lo + 64, bsel, o0:o0 + 64], src[lo:lo + 64, hp, 64:128], src[lo:lo + 64, hp, 0:64], True, True,
                                   [('PP', s, cur)], [('ps', bsel)])
                                mm(ps[lo:lo + 64, bsel, o0 + 64:o0 + 128], src[lo:lo + 64, hp, 0:64], src[lo:lo + 64, hp, 64:128], True, True,
                                   [('PP', s, cur)], [('ps', bsel)])
                        banks.append((ba, bb))
                    for s in range(2):
                        ba, bb = banks[s]
                        dst = PP[s][nxt]
                        cp(dst[:, 0:4, :], ps[:, ba, :].rearrange('p (h x) -> p h x', h=4), [('ps', ba)], [('PP', s, nxt)], eng='act')
                        cp(dst[:, 4:8, :], ps[:, bb, :].rearrange('p (h x) -> p h x', h=4), [('ps', bb)], [('PP', s, nxt)], eng='dve')
                    xb_ = []
                    for s in range(2):
                        bx = bank()
                        src = PP[s][nxt]
                        for hp in range(8):
                            for par in range(2):
                                lo = par * 64
                                mm(ps[lo:lo + 64, bx, hp * 64:(hp + 1) * 64], src[lo:lo + 64, hp, 64:128], Xb[s][lo:lo + 64, hp, :], True, True,
                                   [('PP', s, nxt), ('Xb', s)], [('ps', bx)])
                        xb_.append(bx)
                    for s in range(2):
                        bx = xb_[s]
                        tt(X32[s][:], X32[s][:], ps[:, bx, :].rearrange('p (h x) -> p h x', h=8), ALU.add, [('X32', s), ('ps', bx)], [('X32', s)])
                        if lvl < 5:
                            cp(Xb[s][:], X32[s][:], [('X32', s)], [('Xb', s)], eng='act')
                        else:
                            cp(XT[:, :, ctx[s], :], X32[s][:], [('X32', s)], [('XT', ctx[s])], eng='act')
            Y32 = FB_[0]
            for c in range(NCH):
                state_in(c)
                cur = c % 2
                a0 = A0b[cur]
                bR = bank()
                for hp in range(8):
                    for par in range(2):
                        lo = par * 64
                        o = ps[lo:lo + 64, bR, hp * 64:(hp + 1) * 64]
                        mm(o, QR[lo:lo + 64, hp, 0, c, :], a0[lo:lo + 64, hp, :], True, False, ['HX0', 'HX1', ('A0b', cur)], [('ps', bR)])
                        mm(o, LP[lo:lo + 64, hp, c, 0:64], vTv(c, hp, lo), False, True, [('LP', c), 'H2'], [('ps', bR)])
                cp(RHSb[:], ps[:, bR, :].rearrange('p (h x) -> p h x', h=8), [('ps', bR)], ['RHSb'], eng='act')
                bU = bank()
                for hp in range(8):
                    for par in range(2):
                        lo = par * 64
                        mm(ps[lo:lo + 64, bU, hp * 64:(hp + 1) * 64], XT[lo:lo + 64, hp, c, :], RHSb[lo:lo + 64, hp, :], True, True,
                           [('XT', c), 'RHSb'], [('ps', bU)])
                cp(Ub[:], ps[:, bU, :].rearrange('p (h x) -> p h x', h=8), [('ps', bU)], ['Ub'], eng='act')
                bD = bank()
                for hp in range(8):
                    for par in range(2):
                        lo = par * 64
                        o = ps[lo:lo + 64, bD, hp * 64:(hp + 1) * 64]
                        mm(o, kTv(c, hp, lo), vTv(c, hp, lo), True, False, ['pl', 'H2'], [('ps', bD)])
                        mm(o, nTv(c, hp, lo), Ub[lo:lo + 64, hp, :], False, True, ['pl', 'Ub'], [('ps', bD)])
                bY = bank()
                for hp in range(8):
                    for par in range(2):
                        lo = par * 64
                        o = ps[lo:lo + 64, bY, hp * 64:(hp + 1) * 64]
                        mm(o, a0[lo:lo + 64, hp, :], QR[lo:lo + 64, hp, 1, c, :], True, False, [('A0b', cur), 'HX0', 'HX1'], [('ps', bY)])
                        mm(o, vTv(c, hp, lo), LP[lo:lo + 64, hp, c, 64:128], False, False, ['H2', ('LP', c)], [('ps', bY)])
                        mm(o, Ub[lo:lo + 64, hp, :], PN[lo:lo + 64, hp, c, 64:128], False, True, ['Ub', ('PN', c)], [('ps', bY)])
                tt(A32[:], A32[:], ps[:, bD, :].rearrange('p (h x) -> p h x', h=8), ALU.add, ['A32', ('ps', bD)], ['A32'])
                tt(A32[:], A32[:], WCs[:, :, c:c + 1].to_broadcast([128, 8, 64]), ALU.mult, ['A32', 'WCs'], ['A32'])
                cp(A0b[1 - cur][:], A32[:], ['A32'], [('A0b', 1 - cur)], eng='act')
                cp(Y32[:, :, c * C:(c + 1) * C], ps[:, bY, :].rearrange('p (h x) -> p h x', h=8), [('ps', bY)], ['F0'], eng='dve')
                state_out(c)
            return Y32, v32

        def rwkv_post(Y, Ykey, bonus, bkey, glb, g0b, lmb, mb, n):
            P.phase = 'rwkv_post'
            Yb = HB_[0]; ysq = hb
            cp(Yb[:, :, 0:n], Y[:, :, 0:n], [Ykey], ['H0'], eng='act')
            act(ysq[:, :, 0:n], Y[:, :, 0:n], AF.Square, [Ykey], ['hb'])
            for kc in range(8):
                b1 = bank(); b2 = bank()
                mm(ps[:, b1, 0:n], bdb[:], Yb[:, kc, 0:n], True, True, ['bdb', 'H0'], [('ps', b1)])
                mm(ps[:, b2, 0:n], bdb[:], ysq[:, kc, 0:n], True, True, ['bdb', 'hb'], [('ps', b2)])
                mean = st1[0][:, 0:n]; var = st1[1][:, 0:n]; t_ = st1[2][:, 0:n]
                ts(mean, ps[:, b1, 0:n], 1.0 / 64, None, ALU.mult, None, [('ps', b1)], ['st0'])
                tt(var, mean, mean, ALU.mult, ['st0'], ['st1'])
                stt(var, ps[:, b2, 0:n], 1.0 / 64, var, ALU.mult, ALU.subtract, [('ps', b2), 'st1'], ['st1'])
                ts(var, var, 0.0, GN_EPS, ALU.max, ALU.add, ['st1'], ['st1'])
                act(var, var, AF.Sqrt, ['st1'], ['st1'])
                recip(var, var, ['st1'], ['st1'])
                tt(t_, Y[:, kc, 0:n], mean, ALU.subtract, [Ykey, 'st0'], ['st2'])
                tt(t_, t_, var, ALU.mult, ['st2', 'st1'], ['st2'])
                act(t_, t_, AF.Identity, ['st2', 'v_gn_g', 'v_gn_b'], ['st2'], bias=vec['gn_b'][:, kc:kc + 1], scale=vec['gn_g'][:, kc:kc + 1])
                tt(t_, t_, bonus[:, kc, 0:n], ALU.add, ['st2', bkey], ['st2'])
                tt(t_, t_, glb[:, kc, 0:n], ALU.mult, ['st2', 'H1'], ['st2'])
                tt(t_, t_, g0b[:, kc, 0:n], ALU.mult, ['st2', 'H3'], ['st2'])
                tt(mb[:, kc, 0:n], t_, lmb[:, kc, 0:n], ALU.add, ['st2', 'HX2'], ['H2'])

        def resid_ln(w, xin, xkey, ln_idx, n):
            def evac(m, pb):
                stt(h32[:, m, 0:n], ps[:, pb, 0:n], 1.0 / ALPHA, h32[:, m, 0:n], ALU.mult, ALU.add, [('ps', pb), 'h32'], ['h32'])
            proj(w, D, xin, xkey, n, evac)
            layernorm(ln_idx, n, LN_EPS / (ALPHA * ALPHA))

        def xattn_prompt(n):
            P.phase = 'xattn_prompt'
            qb = HB_[0]; ob = HB_[1]; pT = HX[:, 0:8, :]
            def evq(m, pb):
                cp(qb[:, m, 0:n], ps[:, pb, 0:n], [('ps', pb)], ['H0'], eng='act')
            proj('xa_wq', D, hb, 'hb', n, evq)
            for h in range(4):
                for mc in range(2):
                    pb = bank()
                    for dc in range(2):
                        mm(ps[:, pb, 0:n], mkT[:, 2 * h + dc, mc * 128:(mc + 1) * 128], qb[:, 2 * h + dc, 0:n], dc == 0, dc == 1, ['mkT', 'H0'], [('ps', pb)])
                    act(pT[:, 2 * h + mc, 0:n], ps[:, pb, 0:n], AF.Exp, [('ps', pb)], ['HX0'], scale=1.0 / 16.0)
                pd = bank()
                for mc in range(2):
                    mm(ps[:, pd, 0:n], onesb[:], pT[:, 2 * h + mc, 0:n], mc == 0, mc == 1, ['onesb', 'HX0'], [('ps', pd)])
                rd = st1[h % 2][:, 0:n]; rk_ = 'st%d' % (h % 2)
                recip(rd, ps[:, pd, 0:n], [('ps', pd)], [rk_])
                for dc in range(2):
                    po = bank()
                    for mc in range(2):
                        mm(ps[:, po, 0:n], mvb[:, mc, (2 * h + dc) * 128:(2 * h + dc + 1) * 128], pT[:, 2 * h + mc, 0:n], mc == 0, mc == 1, ['mvb', 'HX0'], [('ps', po)])
                    tt(ob[:, 2 * h + dc, 0:n], ps[:, po, 0:n], rd, ALU.mult, [('ps', po), rk_], ['H1'])
            resid_ln('xa_wo', ob, 'H1', 2, n)

        if stage >= 1:
            mem_kv()
        if 'm' in PARTS or PARTS == 'abcdefg':
            P.op('dve', lambda e: e.memset(A32[:], 0.0), writes=['A32'])
            P.op('dve', lambda e: e.memset(A0b[0][:], 0.0), writes=[('A0b', 0)])
        for ti in range(int(os.environ.get('NTI', NTILES)) if stage >= 9 else 1):
            TOG = os.environ.get('TOG', '')
            for r in range(1 if '1' in TOG else NT // 128):
                load_tokens_fm(di['xp'][ti * NT + r * 128: ti * NT + (r + 1) * 128, :], 128, h32, None if 'D' in TOG else hb, r * 128, 'h32')
            if stage >= 2:
                ffn('ffn1_wi', 'ffn1_wo', 0, NT)
            if ti == 0:
                dump('h1', h32[:], 'h32')
            if stage < 3:
                break
            B = mixer_prompt(ti)
            if stage < 4:
                break
            lmb = lru_prompt(ti, B)
            if ti == 0:
                dump('lm', lmb, 'HX2')
            if stage < 5:
                break
            Y32, bonus = rwkv_core(B, lambda c: None, lambda c: None)
            if ti == 0:
                dump('Y', Y32[:], 'F0')
            if stage < 6:
                break
            mb = HB_[2]
            rwkv_post(Y32, 'F0', bonus, 'F2', B['glb'], B['g0b'], lmb, mb, NT)
            resid_ln('w_mix_out', mb, 'H2', 1, NT)
            if ti == 0:
                dump('h2', h32[:], 'h32')
            if stage < 7:
                break
            xattn_prompt(NT)
            if ti == 0:
                dump('h3', h32[:], 'h32')
            if stage < 8:
                break
            ffn('ffn2_wi', 'ffn2_wo', 3, NT)
            for r in range(NT // 128):
                store_fm_tokens(h32, 'h32', r * 128, 128, di['yp'][ti * NT + r * 128: ti * NT + (r + 1) * 128, :])
        if stage < 9:
            P.emit()
            return nc, P
        def prompt_outputs():
            pass
            Sout = FB_[1].rearrange('p a b -> p (a b)')[:, 0:1024].rearrange('p (hp par k) -> p hp par k', hp=8, par=2)
            for g in range(2):
                pb = bank()
                for q in range(4):
                    hp = g * 4 + q
                    tr(ps[0:64, pb, q * 128:(q + 1) * 128], A32[:, hp, :], ident[:, :], ['A32', 'ident'], [('ps', pb)])
                cp(Sout[0:64, g * 4:(g + 1) * 4, :, :], ps[0:64, pb, :].rearrange('p (q par k) -> p q par k', q=4, par=2), [('ps', pb)], ['F1'], eng='act')
            dma('sp', di['prw'].rearrange('(hp par v) k -> v hp par k', hp=8, par=2), Sout[0:64, :, :, :], ['F1'], [], dsem_misc, is_out=True)
            osm2 = FB_[2]
            cp(osm2[:, 0:8, 0:3], osml[:, :, 0:3], ['osml'], ['F2'])
            cp(osm2[:, 0:8, 3:4], osml[:, :, 3:4], ['osml'], ['F2'])
            store_fm_tokens(osm2, 'F2', 0, 3, di['pcv'][:, :])
            store_fm_tokens(osm2, 'F2', 3, 1, di['plru'].rearrange('(o d) -> o d', o=1))
            osh3 = FB_[3]
            cp(osh3[:, 0:8, 0:1], osh[:, 0:8].unsqueeze(2), ['osh'], ['F3'])
            cp(osh3[:, 0:8, 1:2], osh[:, 8:16].unsqueeze(2), ['osh'], ['F3'])
            cp(osh3[:, 0:8, 2:3], osh[:, 16:24].unsqueeze(2), ['osh'], ['F3'])
            cp(osh3[:, 0:2, 3:4], osh[:, 24:26].unsqueeze(2), ['osh'], ['F3'])
            pshv = di['psh'].rearrange('(o d) -> o d', o=1)
            for q in range(3):
                store_fm_tokens(osh3, 'F3', q, 1, pshv[:, q * 1024:(q + 1) * 1024])
            store_fm_tokens(osh3, 'F3', 3, 1, pshv[:, 3072:3328], nch=2)


        if os.environ.get('NTI') != '0':
            prompt_outputs()
        def sample_path():
            P.phase = 'sample_path'
            n = NS
            sm = lambda nm, shp, dt=F32: P.sbuf(nm, shp, dt)
            rc = sm('s_rc', [128, 8, n]); kc_ = sm('s_kc', [128, 8, n]); vc = sm('s_vc', [128, 8, n]); ac = sm('s_ac', [128, 8, n]); lsc = sm('s_lsc', [128, 8, n])
            yc = sm('s_yc', [128, 8, n]); bonc = sm('s_bonc', [128, 8, n]); plc = sm('s_plc', [128, 8, n]); hsc = sm('s_hsc', [128, 8, n])
            prevS = sm('s_prev', [128, 26, n]); praw = sm('s_praw', [128, 26, n]); h0S = sm('s_h0', [128, 8, n]); scvT = sm('s_scvT', [128, 8, 3 * n])
            BD = tok32[1][:].rearrange('p (h x) -> p h x', h=8); ones32 = st1[4][:, 0:128]
            glb = HB_[1]; geb = HB_[2]; g0b = HB_[3]; g1b = HX[:, 0:8, :]; lmb = HX[:, 16:24, :]
            dsS = [P.dma_sem() for _ in range(7)]

            def load_fm(src_rows_ap, nrows, nchunks, dst, dkey, tki, sem):
                tk = tok32[tki]
                dma('sp', tk[0:nrows, 0:nchunks * 128], src_rows_ap, [], [('tok32', tki)], sem)
                for g0 in range(0, nchunks, 4):
                    gn = min(4, nchunks - g0)
                    b = bank()
                    for q in range(gn):
                        tr(ps[:, b, q * 128:q * 128 + nrows], tk[0:nrows, (g0 + q) * 128:(g0 + q + 1) * 128], ident[0:nrows, 0:nrows],
                           [('tok32', tki), 'ident'], [('ps', b)])
                    cp(dst[:, g0:g0 + gn, 0:nrows], ps[:, b, :].rearrange('p (q t) -> p q t', q=4)[:, 0:gn, 0:nrows], [('ps', b)], [dkey], eng='act')

            load_tokens_fm(di['xs'], n, h32, hb, 0, 'h32')
            for q in range(4):
                c0 = q * 8; cn = min(8, 26 - c0)
                tmpd = FB_[0] if q % 2 == 0 else FB_[1]
                load_fm(di['ssh'][:, c0 * 128:(c0 + cn) * 128], n, cn, tmpd, 'F%d' % (q % 2), q % 2, dsS[q % 2])
                cp(prevS[:, c0:c0 + cn, :], tmpd[:, 0:cn, 0:n], ['F%d' % (q % 2)], ['s_prev'])
            load_fm(di['slru'], n, 8, h0S, 's_h0', 0, dsS[0])
            load_fm(di['scv'].rearrange('b j d -> (b j) d'), 3 * n, 8, scvT, 's_scvT', 1, dsS[1])
            dma('sp', di['scv_o'][:, 0:2, :], di['scv'][:, 1:3, :], [], [], P.dma_sem(), is_out=True)

            ffn('ffn1_wi', 'ffn1_wo', 0, n)

            tmp = st1[0]

            def evac(m, pb):
                psn = ps[:, pb, 0:n]
                if m < 26:
                    cp(praw[:, m, :], psn, [('ps', pb)], ['s_praw'], eng='act')
                    ts(tmp[:, 0:n], prevS[:, m, :], mu[:, m:m + 1], None, ALU.mult, None, ['s_prev', 'mu'], ['st0'])
                    if m < 24:
                        dst = [rc, kc_, vc][m // 8][:, m % 8, :]
                        dkey = ['s_rc', 's_kc', 's_vc'][m // 8]
                        stt(dst, psn, omu[:, m:m + 1], tmp[:, 0:n], ALU.mult, ALU.add, [('ps', pb), 'omu', 'st0'], [dkey])
                    else:
                        xs_ = st1[1][:, 0:n]
                        stt(xs_, psn, omu[:, m:m + 1], tmp[:, 0:n], ALU.mult, ALU.add, [('ps', pb), 'omu', 'st0'], ['st1'])
                        lb = st1[2][:, 0:NT].bitcast(BF16)[:, 0:n]
                        if m == 24:
                            act(lb[0:64, :], xs_[0:64, :], AF.Tanh, ['st1'], ['st2'])
                            cp(lb[64:128, :], xs_[64:128, :], ['st1'], ['st2'])
                            for (lo, dstt, dk, bvec) in [(0, lsc, 's_lsc', 'decay_w0'), (64, ac, 's_ac', 'aaa_a0')]:
                                for q in range(8):
                                    p2 = bank()
                                    mm(ps[:, p2, 0:n], w2a2[lo:lo + 64, q * 128:(q + 1) * 128], lb[lo:lo + 64, :], True, True, ['w2a2', 'st2'], [('ps', p2)])
                                    act(dstt[:, q, :], ps[:, p2, 0:n], AF.Sigmoid, [('ps', p2), 'v_' + bvec], [dk], bias=vec[bvec][:, q:q + 1])
                        else:
                            act(lb, xs_, AF.Sigmoid, ['st1'], ['st2'])
                            for q in range(8):
                                p2 = bank()
                                mm(ps[:, p2, 0:n], g2[:, q * 128:(q + 1) * 128], lb, True, True, ['g2', 'st2'], [('ps', p2)])
                                cp(glb[:, q, 0:n], ps[:, p2, 0:n], [('ps', p2)], ['H1'], eng='act')
                elif m < 34:
                    cp(plc[:, m - 26, :], psn, [('ps', pb)], ['s_plc'], eng='act')
                elif m < 42:
                    act(geb[:, m - 34, 0:n], psn, AF.Gelu, [('ps', pb)], ['H2'])
                elif m < 50:
                    act(g0b[:, m - 42, 0:n], psn, AF.Sigmoid, [('ps', pb)], ['H3'])
                else:
                    act(g1b[:, m - 50, 0:n], psn, AF.Sigmoid, [('ps', pb)], ['HX0'])
            proj('w_in', PW, hb, 'hb', n, evac)
            for q in range(4):
                c0 = q * 8; cn = min(8, 26 - c0)
                store_fm_tokens(praw[:, c0:c0 + cn, :], 's_praw', 0, n, di['ssh_o'][:, c0 * 128:(c0 + cn) * 128], nch=cn)
            store_fm_tokens(plc, 's_plc', 0, n, di['scv_o'][:, 2, :])

            sc3 = scvT[:].rearrange('p c (b j) -> p c b j', j=3)
            xc = st1[0][:, 0:n]; gr = st1[1][:, 0:n]; gi = st1[2][:, 0:n]; t3 = st1[3][:, 0:n]; hs = st1[4][:, 0:n]
            xcb = HX[:, 8:16, :]
            for c in range(8):
                act(xc, plc[:, c, :], AF.Identity, ['s_plc', 'cw', 'v_conv_b'], ['st0'], bias=vec['conv_b'][:, c:c + 1], scale=cw[:, 3, c:c + 1])
                for j in range(3):
                    stt(xc, sc3[:, c, :, j], cw[:, j, c:c + 1], xc, ALU.mult, ALU.add, ['s_scvT', 'cw', 'st0'], ['st0'])
                cp(xcb[:, c, 0:n], xc, ['st0'], ['HX1'], eng='act')
                p1 = bank(); p2 = bank()
                mm(ps[:, p1, 0:n], wrbd[:, c, :], xcb[:, c, 0:n], True, True, ['wrbd', 'HX1'], [('ps', p1)])
                mm(ps[:, p2, 0:n], wibd[:, c, :], xcb[:, c, 0:n], True, True, ['wibd', 'HX1'], [('ps', p2)])
                act(gr, ps[:, p1, 0:n], AF.Sigmoid, [('ps', p1), 'v_lru_br'], ['st1'], bias=vec['lru_br'][:, c:c + 1])
                act(gi, ps[:, p2, 0:n], AF.Sigmoid, [('ps', p2), 'v_lru_bi'], ['st2'], bias=vec['lru_bi'][:, c:c + 1])
                act(gr, gr, AF.Exp, ['st1', 'lsp'], ['st1'], scale=lsp[:, c:c + 1])
                act(t3, gr, AF.Square, ['st1'], ['st3'])
                ts(t3, t3, -1.0, 1.0, ALU.mult, ALU.add, ['st3'], ['st3'])
                ts(t3, t3, 0.0, None, ALU.max, None, ['st3'], ['st3'])
                act(t3, t3, AF.Sqrt, ['st3'], ['st3'])
                tt(gi, gi, xc, ALU.mult, ['st2', 'st0'], ['st2'])
                tt(gi, gi, t3, ALU.mult, ['st2', 'st3'], ['st2'])
                tt(hs, gr, h0S[:, c, :], ALU.mult, ['st1', 's_h0'], ['st4'])
                tt(hsc[:, c, :], hs, gi, ALU.add, ['st4', 'st2'], ['s_hsc'])
                tt(t3, hsc[:, c, :], geb[:, c, 0:n], ALU.mult, ['s_hsc', 'H2'], ['st3'])
                tt(lmb[:, c, 0:n], t3, g1b[:, c, 0:n], ALU.mult, ['st3', 'HX0'], ['HX2'])
            store_fm_tokens(hsc, 's_hsc', 0, n, di['slru_o'])

            Sout = FB_[1].rearrange('p a b -> p (a b)')[:, 0:1024].rearrange('p (hp par k) -> p hp par k', hp=8, par=2)
            P.op('dve', lambda e: e.memset(tok32[1][:], 0.0), writes=[('tok32', 1)])
            for g in range(NS // NCH):
                Bp = dict(r32=FB_[0], k32=FB_[1], v32=FB_[2], a32=FB_[3], ls32=FB_[4])
                for (dstF, fk, srcc, sk) in [(FB_[0], 'F0', rc, 's_rc'), (FB_[1], 'F1', kc_, 's_kc'), (FB_[2], 'F2', vc, 's_vc'), (FB_[3], 'F3', ac, 's_ac'), (FB_[4], 'F4', lsc, 's_lsc')]:
                    P.op('dve', lambda e, dstF=dstF: e.memset(dstF[:], 0.0), writes=[fk])
                    cp(dstF[:].rearrange('p k (c t) -> p k c t', t=C)[:, :, :, 0], srcc[:, :, g * NCH:(g + 1) * NCH], [sk], [fk])

                def state_in(c, g=g):
                    b_ = g * NCH + c
                    src = di['srw'][b_].rearrange('(hp par v) k -> par v hp k', hp=8, par=2)
                    for par in range(2):
                        dma('sp', BD[par * 64:(par + 1) * 64, :, par * 64:(par + 1) * 64], src[par], [], [('tok32', 1)], dsS[3])
                    pb = bank()
                    for hp in range(8):
                        mm(ps[:, pb, hp * 64:(hp + 1) * 64], BD[:, hp, :], mask[:, 192:256], True, True, [('tok32', 1), 'mask'], [('ps', pb)])
                    v3 = ps[:, pb, :].rearrange('p (h x) -> p h x', h=8)
                    cp(A32[:], v3, [('ps', pb)], ['A32'], eng='dve')
                    cp(A0b[c % 2][:], v3, [('ps', pb)], [('A0b', c % 2)], eng='act')

                def state_out(c, g=g):
                    b_ = g * NCH + c
                    for gg in range(2):
                        pb = bank()
                        for q in range(4):
                            hp = gg * 4 + q
                            tr(ps[0:64, pb, q * 128:(q + 1) * 128], A32[:, hp, :], ident[:, :], ['A32', 'ident'], [('ps', pb)])
                        cp(Sout[0:64, gg * 4:(gg + 1) * 4, :, :], ps[0:64, pb, :].rearrange('p (q par k) -> p q par k', q=4, par=2), [('ps', pb)], ['F1'], eng='act')
                    dma('sp', di['srw_o'][b_].rearrange('(hp par v) k -> v hp par k', hp=8, par=2), Sout[0:64, :, :, :], ['F1'], [], dsS[4], is_out=True)
                Y32, bon = rwkv_core(Bp, state_in, state_out, skip_inverse=True)
                cp(yc[:, :, g * NCH:(g + 1) * NCH], Y32[:].rearrange('p k (c t) -> p k c t', t=C)[:, :, :, 0], ['F0'], ['s_yc'])
                cp(bonc[:, :, g * NCH:(g + 1) * NCH], bon[:].rearrange('p k (c t) -> p k c t', t=C)[:, :, :, 0], ['F2'], ['s_bonc'])
            mb = HB_[2]
            rwkv_post(yc, 's_yc', bonc, 's_bonc', glb, g0b, lmb, mb, n)
            resid_ln('w_mix_out', mb, 'H2', 1, n)

            qc = FB_[0]; qT = FB_[1].rearrange('p a b -> p (a b)')[:, 0:1024]; sel = FB_[2].rearrange('p a b -> p (a b)')[:, 0:NS * 128].rearrange('p (b m) -> p b m', b=NS)
            Kb = FB_[3].rearrange('p a b -> p (a b)').rearrange('p (mc f) -> p mc f', mc=2)
            prod = FB_[4].rearrange('p a b -> p (a b)')[:, 0:1024]
            Vb = HB_[0][:].rearrange('p a b -> p (a b)').rearrange('p (mc f) -> p mc f', mc=2)
            ob = HB_[1]
            sc = st1[0][:, 0:128]; ex = st1[1][:, 0:128]; den = st1[2][:, 0:64]; pbf = st1[3][:, 0:NT].bitcast(BF16)[:, 0:128]

            def evq(m, pb):
                cp(qc[:, m, 0:n], ps[:, pb, 0:n], [('ps', pb)], ['F0'], eng='act')
            proj('xa_wq', D, hb, 'hb', n, evq)
            for g0 in range(0, 8, 4):
                pb = bank()
                for q in range(4):
                    tr(ps[0:n, pb, q * 128:(q + 1) * 128], qc[:, g0 + q, 0:n], ident[:, :], ['F0', 'ident'], [('ps', pb)])
                cp(qT[0:n, g0 * 128:(g0 + 4) * 128], ps[0:n, pb, :], [('ps', pb)], ['F1'], eng='act')
            cp(sel[0:n, :, :], ident[0:n, 0:n].unsqueeze(2).to_broadcast([n, n, 128]), ['ident'], ['F2'])
            for b_ in range(NS):
                dma('sp', Kb, di['cmk'][b_].rearrange('(mc p) f -> p mc f', p=128), [], ['F3'], dsS[5])
                pq = [bank(), bank()]
                for hf in range(2):
                    mm(ps[:, pq[hf], :], sel[0:n, b_, :], qT[0:n, hf * 512:(hf + 1) * 512], True, True, ['F2', 'F1'], [('ps', pq[hf])])
                for mc in range(2):
                    for hf in range(2):
                        tt(prod[:, hf * 512:(hf + 1) * 512], Kb[:, mc, hf * 512:(hf + 1) * 512], ps[:, pq[hf], :], ALU.mult, ['F3', ('ps', pq[hf])], ['F4'])
                    P.op('dve', lambda e, b_=b_, mc=mc: e.tensor_reduce(out=sc[:, (b_ * 2 + mc) * 4:(b_ * 2 + mc) * 4 + 4], in_=prod.rearrange('p (h d) -> p h d', h=4), axis=AX.X, op=ALU.add),
                         reads=['F4'], writes=['st0'])
            act(ex, sc, AF.Exp, ['st0'], ['st1'], scale=1.0 / 16.0)
            dma('sp', ones32, di['c_all'][:, 128:256], [], ['st4'], dsS[2])
            pdn = bank()
            mm(ps[:, pdn, 0:128], ones32, ex, True, True, ['st4', 'st1'], [('ps', pdn)])
            d4 = ps[:, pdn, 0:128].rearrange('p (b mc h) -> p b mc h', mc=2, h=4)
            den3 = den.rearrange('p (b h) -> p b h', h=4)
            cp(den3, d4[:, :, 0, :], [('ps', pdn)], ['st2'])
            tt(den3, den3, d4[:, :, 1, :], ALU.add, ['st2', ('ps', pdn)], ['st2'])
            recip(den, den, ['st2'], ['st2'])
            tt(pbf.rearrange('p (b mc h) -> p b mc h', mc=2, h=4), ex.rearrange('p (b mc h) -> p b mc h', mc=2, h=4),
               den3.unsqueeze(2).to_broadcast([128, NS, 2, 4]), ALU.mult, ['st1', 'st2'], ['st3'])
            po = bank()
            for b_ in range(NS):
                dma('pool', Vb, di['cmv'][b_].rearrange('(mc p) f -> p mc f', p=128), [], ['H0'], dsS[6])
                for c in range(8):
                    for mc in range(2):
                        col = (b_ * 2 + mc) * 4 + c // 2
                        mm(ps[:, po, c * NS + b_:c * NS + b_ + 1], Vb[:, mc, c * 128:(c + 1) * 128], pbf[:, col:col + 1], mc == 0, mc == 1, ['H0', 'st3'], [('ps', po)])
            cp(ob[:, :, 0:n], ps[:, po, 0:8 * NS].rearrange('p (c b) -> p c b', c=8), [('ps', po)], ['H1'], eng='act')
            resid_ln('xa_wo', ob, 'H1', 2, n)
            ffn('ffn2_wi', 'ffn2_wo', 3, n)
            store_fm_tokens(h32, 'h32', 0, n, di['ys'])

        if do_sample:
            sample_path()
        P.emit()
    return nc, P


_CACHE = {}


def _consts():
    a = np.arange(128) % 64
    b = np.arange(64)
    su = (a[:, None] < b[None, :]).astype(np.float32)
    ui = (a[:, None] <= b[None, :]).astype(np.float32)
    sl = (a[:, None] > b[None, :]).astype(np.float32)
    ey = (a[:, None] == b[None, :]).astype(np.float32)
    bd = np.zeros((128, 128), np.float32)
    bd[:64, :64] = 1.0
    bd[64:, 64:] = 1.0
    rs = np.ones((128, NT), np.float32)
    rs[:, ::C] = 0.0
    return {'c_all': np.ascontiguousarray(np.concatenate([np.eye(128, dtype=np.float32), np.ones((128, 128), np.float32), bd, su, ui, sl, ey, rs], axis=1))}


def make_in_maps(inputs):
    f = lambda a: np.ascontiguousarray(np.asarray(a, dtype=np.float32))
    shared = {}
    for nm in ['ffn1_wi', 'ffn1_wo', 'ffn2_wi', 'ffn2_wo', 'w_in', 'decay_w2', 'aaa_a2', 'gate_g2',
               'lru_wr', 'lru_wi', 'w_mix_out', 'xa_wq', 'xa_wk', 'xa_wv', 'xa_wo']:
        shared[nm] = np.ascontiguousarray(f(inputs[nm])[0])
    shared['prm'] = np.ascontiguousarray(np.concatenate(
        [f(inputs[nm])[0].reshape(-1, 128) for nm in ['ln_g', 'ln_b', 'shift_mu', 'conv_w'] + VEC_NAMES], axis=0))
    shared.update(_consts())
    maps = []
    for c in range(8):
        m = dict(shared)
        sl = slice(c * NS, (c + 1) * NS)
        m['xp'] = f(inputs['x_prompt'][c])
        m['mem'] = f(inputs['mem_prompt'][c])
        m['xs'] = f(inputs['x_sample'][sl, 0])
        m['cmk'] = f(inputs['cache_mem_k'][0, sl]).reshape(NS, NMEM, D)
        m['cmv'] = f(inputs['cache_mem_v'][0, sl]).reshape(NS, NMEM, D)
        m['srw'] = f(inputs['state_rwkv'][0, sl]).reshape(NS, D, 64)
        m['ssh'] = f(inputs['state_rwkv_shift'][0, sl])
        m['slru'] = f(inputs['state_lru'][0, sl])
        m['scv'] = f(inputs['state_conv'][0, sl])
        maps.append(m)
    return maps


def kernel(**inputs):
    if 'nc' not in _CACHE:
        _CACHE['nc'] = build()[0]
    nc = _CACHE['nc']
    maps = make_in_maps(inputs)
    res = run_bass_kernel_spmd(nc, maps, core_ids=list(range(8)))
    R = res.results
    cat = lambda k: np.stack([np.asarray(r[k], dtype=np.float32) for r in R])
    catc = lambda k: np.concatenate([np.asarray(r[k], dtype=np.float32) for r in R], axis=0)
    yp = cat('yp')
    ys = catc('ys').reshape(8 * NS, 1, D)
    pmk = cat('pmk').reshape(1, 8, NMEM, 4, 256)
    pmv = cat('pmv').reshape(1, 8, NMEM, 4, 256)
    prw = cat('prw').reshape(1, 8, 16, 64, 64)
    psh = cat('psh').reshape(1, 8, RP)
    plru = cat('plru').reshape(1, 8, D)
    pcv = cat('pcv').reshape(1, 8, 3, D)
    srw = catc('srw_o').reshape(1, 8 * NS, 16, 64, 64)
    ssh = catc('ssh_o').reshape(1, 8 * NS, RP)
    slru = catc('slru_o').reshape(1, 8 * NS, D)
    scv = catc('scv_o').reshape(1, 8 * NS, 3, D)
    return (yp, ys, pmk, pmv, prw, psh, plru, pcv, srw, ssh, slru, scv)
```

```python
import math
import os
import numpy as np
from contextlib import ExitStack
import concourse.bass as bass
import concourse.mybir as mybir
from concourse.bass_utils import run_bass_kernel_spmd

F32 = mybir.dt.float32
BF16 = mybir.dt.bfloat16
AF = mybir.ActivationFunctionType
ALU = mybir.AluOpType
AX = mybir.AxisListType

ENGS = ['pe', 'dve', 'act', 'pool', 'sp']

D = 1024
T = 2048
NT = 256
NTILES = T // NT
C = 64
NCH = NT // C
DFF = 2816
NJ = DFF // 128
RP = 3328
PW = 7424
NMEM = 256
NS = 16
ALPHA = 2.0 ** 0.25
LN_EPS = 1e-5
GN_EPS = 64e-5
C0 = math.exp(-0.5)


class DmaSem:
    def __init__(self, sem):
        self.sem = sem
        self.count = 0


class Prog:
    def __init__(self, nc, stack):
        self.nc = nc
        self.stack = stack
        self.ops = {e: [] for e in ENGS}
        self.last_w = {}
        self.readers = {}
        self.seen = {e: {} for e in ENGS}
        self.dsems = []
        self.out_tokens = []

    def dma_sem(self):
        s = DmaSem(self.stack.enter_context(self.nc.semaphore('dsem%d' % len(self.dsems))))
        self.dsems.append(s)
        return s

    def sbuf(self, name, shape, dt):
        return self.stack.enter_context(self.nc.sbuf_tensor(name, list(shape), dt))

    def psum(self, name, shape, dt):
        return self.stack.enter_context(self.nc.psum_tensor(name, list(shape), dt))

    def barrier(self, keys, engines=ENGS):
        if 'B' in os.environ.get('TOG', ''):
            return
        for e in engines:
            self.op(e, None, reads=keys, track=False)

    def op(self, eng, fn, reads=(), writes=(), dsem=None, is_out=False, track=True):
        isps = lambda k: isinstance(k, tuple) and k[0] == 'ps'
        writes = list(writes) + [k for k in reads if isps(k)]
        reads = [k for k in reads if not isps(k)]
        deps = []
        for k in reads:
            t = self.last_w.get(k)
            if t is not None:
                deps.append(t)
        for k in writes:
            t = self.last_w.get(k)
            if t is not None:
                deps.append(t)
            deps.extend(self.readers.get(k, {}).values())
        need = {}
        for t in deps:
            if t[0] == 'eng':
                if t[1] == eng and dsem is None and (eng == 'pe' or os.environ.get('NOSELF')):
                    continue
                key = ('eng', t[1])
            else:
                key = ('dma', id(t[1]))
            if need.get(key, (None, -1))[1] < t[2]:
                need[key] = (t[1], t[2])
        waits = []
        for key, (src, v) in need.items():
            if self.seen[eng].get(key, -1) >= v:
                continue
            self.seen[eng][key] = v
            waits.append((key[0], src, v))
        idx = len(self.ops[eng])
        self.ops[eng].append(dict(fn=fn, waits=waits, dsem=dsem, target=False, phase=getattr(self, 'phase', '')))
        if dsem is not None:
            dsem.count += 16
            tok = ('dma', dsem, dsem.count)
        else:
            tok = ('eng', eng, idx)
        for k in writes:
            self.last_w[k] = tok
            self.readers[k] = {}
        for k in (reads if track else ()):
            r = self.readers.setdefault(k, {})
            rk = (tok[0], tok[1] if tok[0] == 'eng' else id(tok[1]))
            if rk not in r or r[rk][2] < tok[2]:
                r[rk] = tok
        if is_out:
            self.out_tokens.append(tok)
        return tok

    def emit(self):
        nc = self.nc
        fin = {}
        for t in self.out_tokens:
            fin[id(t[1])] = (t[1], max(fin.get(id(t[1]), (None, 0))[1], t[2]))
        self.ops['sp'].append(dict(fn=None, waits=[('dma', s, v) for s, v in fin.values()], dsem=None, target=False))
        for e in ENGS:
            for o in self.ops[e]:
                for kind, src, v in o['waits']:
                    if kind == 'eng':
                        self.ops[src][v]['target'] = True
        semval = {}
        for e in ENGS:
            c = 0
            vals = []
            for o in self.ops[e]:
                if o['target']:
                    c += 1
                vals.append(c)
            semval[e] = vals
        esem = {e: self.stack.enter_context(nc.semaphore('esem_' + e)) for e in ENGS}
        handles = {'pe': 'tensor', 'dve': 'vector', 'act': 'scalar', 'pool': 'gpsimd', 'sp': 'sync'}
        with nc.Block() as block:
            def make(e):
                def body(eng):
                    for o in self.ops[e]:
                        for kind, src, v in o['waits']:
                            if kind == 'eng':
                                eng.wait_ge(esem[src], semval[src][v])
                            else:
                                eng.wait_ge(src.sem, v)
                        if o['fn'] is None:
                            continue
                        inst = o['fn'](eng)
                        if os.environ.get('ANNOT') and o.get('phase'):
                            inst.annotate(o['phase'])
                        if o['dsem'] is not None:
                            inst.then_inc(o['dsem'].sem, 16)
                        elif o['target']:
                            inst.then_inc(esem[e], 1)
                return body
            for e in ENGS:
                getattr(block, handles[e])(make(e))
        self.stats = {e: len(self.ops[e]) for e in ENGS}


VEC_NAMES = ['decay_w0', 'aaa_a0', 'k_k', 'k_a', 'r_k', 'gn_g', 'gn_b', 'conv_b', 'lru_br', 'lru_bi', 'lru_lambda']
W_NAMES = ['ffn1_wi', 'ffn1_wo', 'ffn2_wi', 'ffn2_wo', 'w_in', 'w_mix_out', 'xa_wq', 'xa_wk', 'xa_wv', 'xa_wo']


def build(dbg=None, do_sample=True, stage=99):
    nc = bass.Bass('TRN2', target_bir_lowering=False)
    di = {}

    DECL = os.environ.get('DECL')

    def din(name, shape):
        if DECL and name not in DECL.split(','):
            return None
        di[name] = nc.dram_tensor(name, list(shape), F32, kind='ExternalInput').ap()
        return di[name]

    def dout(name, shape):
        if DECL and name not in DECL.split(','):
            return None
        di[name] = nc.dram_tensor(name, list(shape), F32, kind='ExternalOutput').ap()
        return di[name]

    din('xp', [T, D]); din('mem', [NMEM, D])
    din('xs', [NS, D]); din('cmk', [NS, NMEM, D]); din('cmv', [NS, NMEM, D])
    din('srw', [NS, D, 64]); din('ssh', [NS, RP]); din('slru', [NS, D]); din('scv', [NS, 3, D])
    din('prm', [210, 128])
    din('ffn1_wi', [D, 2 * DFF]); din('ffn1_wo', [DFF, D]); din('ffn2_wi', [D, 2 * DFF]); din('ffn2_wo', [DFF, D])
    din('w_in', [D, PW])
    din('decay_w2', [64, D]); din('aaa_a2', [64, D]); din('gate_g2', [128, D])
    din('lru_wr', [16, 64, 64]); din('lru_wi', [16, 64, 64])
    for w in ['w_mix_out', 'xa_wq', 'xa_wk', 'xa_wv', 'xa_wo']:
        din(w, [D, D])
    din('c_all', [128, 640 + NT])
    dout('yp', [T, D]); dout('ys', [NS, D]); dout('pmk', [NMEM, D]); dout('pmv', [NMEM, D])
    dout('prw', [D, 64]); dout('psh', [RP]); dout('plru', [D]); dout('pcv', [3, D])
    dout('srw_o', [NS, D, 64]); dout('ssh_o', [NS, RP]); dout('slru_o', [NS, D]); dout('scv_o', [NS, 3, D])
    dbg = dbg or {}
    for k, shp in dbg.items():
        dout('dbg_' + k, shp)

    with ExitStack() as st:
        P = Prog(nc, st)
        n = NT
        ident = P.sbuf('ident', [128, 128], F32)
        identb = P.sbuf('identb', [128, 128], BF16)
        onesb = P.sbuf('onesb', [128, 128], BF16)
        bdb = P.sbuf('bdb', [128, 128], BF16)
        mask = P.sbuf('mask', [128, 256], F32)
        reset = P.sbuf('reset', [128, NT], F32)
        lng = P.sbuf('lng', [128, 4, 8], F32); lnb = P.sbuf('lnb', [128, 4, 8], F32)
        mu = P.sbuf('mu', [128, 26], F32); omu = P.sbuf('omu', [128, 26], F32)
        vec = {v: P.sbuf('v_' + v, [128, 8], F32) for v in VEC_NAMES}
        oka = P.sbuf('oka', [128, 8], F32)
        lsp = P.sbuf('lsp', [128, 8], F32)
        cw = P.sbuf('cw', [128, 4, 8], F32)
        w2a2 = P.sbuf('w2a2', [128, D], BF16)
        g2 = P.sbuf('g2', [128, D], BF16)
        wrbd = P.sbuf('wrbd', [128, 8, 128], BF16); wibd = P.sbuf('wibd', [128, 8, 128], BF16)
        WCAP = 4096
        NWB = 3
        wbuf = [P.sbuf('wbuf%d' % i, [128, WCAP], BF16) for i in range(NWB)]
        wsem = [P.dma_sem() for i in range(NWB)]
        h32 = P.sbuf('h32', [128, 8, n], F32)
        hb = P.sbuf('hb', [128, 8, n], BF16)
        FB_ = [P.sbuf('F%d' % i, [128, 8, n], F32) for i in range(5)]
        HX = P.sbuf('HX', [128, 24, n], BF16)
        HB_ = [P.sbuf('H%d' % i, [128, 8, n], BF16) for i in range(4)]
        memT = HB_[3]
        pl = P.sbuf('pl', [128, 8, n + 3], F32)
        st1 = [P.sbuf('st%d' % i, [128, n], F32) for i in range(5)]
        st2 = [P.sbuf('su%d' % i, [128, n], F32) for i in range(5)]
        stsel = lambda i: ((st1, 'st') if i % 2 == 0 else (st2, 'su'))
        carry_sh = P.sbuf('carry_sh', [128, 26], F32)
        carry_h = P.sbuf('carry_h', [128, 8], F32)
        A32 = P.sbuf('A32', [128, 8, 64], F32)
        A0b = [P.sbuf('A0b%d' % i, [128, 8, 64], BF16) for i in range(2)]
        RHSb = P.sbuf('RHSb', [128, 8, 64], BF16)
        Ub = P.sbuf('Ub', [128, 8, 64], BF16)
        WCs = P.sbuf('WCs', [128, 8, NCH], F32)
        X32 = [P.sbuf('X32_%d' % i, [128, 8, 64], F32) for i in range(2)]
        Xb = [P.sbuf('Xb_%d' % i, [128, 8, 64], BF16) for i in range(2)]
        PP = [[P.sbuf('PP_%d_%d' % (i, j), [128, 8, 128], BF16) for j in range(2)] for i in range(2)]
        LP = P.sbuf('LP', [128, 8, NCH, 128], BF16)
        PN = P.sbuf('PN', [128, 8, NCH, 128], BF16)
        XT = P.sbuf('XT', [128, 8, NCH, 64], BF16)
        mkT = P.sbuf('mkT', [128, 8, NMEM], BF16)
        mvb = P.sbuf('mvb', [128, 2, D], BF16)
        tok32 = [P.sbuf('tok32_%d' % i, [128, D], F32) for i in range(2)]
        osml = P.sbuf('osml', [128, 8, 8], F32)
        osh = P.sbuf('osh', [128, 26], F32)
        sgb = [P.sbuf('sgb%d' % i, [128, NT], F32) for i in range(2)]
        ps = P.psum('ps', [128, 8, 512], F32)
        dsem_c = [P.dma_sem() for i in range(4)]
        dsem_in = [P.dma_sem() for i in range(2)]
        dsem_out = [P.dma_sem() for i in range(2)]
        dsem_misc = P.dma_sem()

        bank_ctr = [0]
        tokctr = [0]

        def bank():
            b = bank_ctr[0] % 8
            bank_ctr[0] += 1
            return b

        def mm(out, lhsT, rhs, start, stop, reads, writes):
            P.op('pe', lambda e: e.matmul(out, lhsT=lhsT, rhs=rhs, start=start, stop=stop), reads=reads, writes=writes)

        def tr(out, in_, idn, reads, writes):
            P.op('pe', lambda e: e.transpose(out, in_, idn), reads=reads, writes=writes)

        def act(out, in_, func, reads, writes, bias=None, scale=None):
            kw = {}
            if bias is not None:
                kw['bias'] = bias
            if scale is not None:
                kw['scale'] = scale
            P.op('act', lambda e: e.activation(out=out, in_=in_, func=func, **kw), reads=reads, writes=writes)

        def tt(out, in0, in1, op, reads, writes, eng='dve'):
            P.op(eng, lambda e: e.tensor_tensor(out=out, in0=in0, in1=in1, op=op), reads=reads, writes=writes)

        def ts(out, in0, s1, s2, op0, op1, reads, writes, eng='dve'):
            if s2 is None:
                P.op(eng, lambda e: e.tensor_scalar(out=out, in0=in0, scalar1=s1, scalar2=None, op0=op0), reads=reads, writes=writes)
            else:
                P.op(eng, lambda e: e.tensor_scalar(out=out, in0=in0, scalar1=s1, scalar2=s2, op0=op0, op1=op1), reads=reads, writes=writes)

        def stt(out, in0, scalar, in1, op0, op1, reads, writes):
            P.op('dve', lambda e: e.scalar_tensor_tensor(out=out, in0=in0, scalar=scalar, in1=in1, op0=op0, op1=op1), reads=reads, writes=writes)

        def cp(out, in_, reads, writes, eng='dve'):
            if eng == 'act':
                act(out, in_, AF.Copy, reads, writes)
            else:
                P.op(eng, lambda e: e.tensor_copy(out=out, in_=in_), reads=reads, writes=writes)

        def recip(out, in_, reads, writes):
            P.op('dve', lambda e: e.reciprocal(out=out, in_=in_), reads=reads, writes=writes)

        def dma(eng, out, in_, reads, writes, dsem, is_out=False, **kw):
            P.op(eng, lambda e: e.dma_start(out=out, in_=in_, **kw), reads=reads, writes=writes, dsem=dsem, is_out=is_out)

        def dump(name, src_ap, key):
            if name in dbg:
                dma('sp', di['dbg_' + name], src_ap, [key], [], P.dma_sem(), is_out=True)

        PARTS = os.environ.get('PARTS', 'abcdefg')
        dma('sp', ident[:], di['c_all'][:, 0:128], [], ['ident'], dsem_c[0])
        dma('sp', mask[:], di['c_all'][:, 384:640], [], ['mask'], dsem_c[0])
        dma('sp', reset[:], di['c_all'][:, 640:640 + NT], [], ['reset'], dsem_c[0])
        P.barrier(['ident', 'mask', 'reset'])
        if 'b' in PARTS:
            dma('pool', identb[:], di['c_all'][:, 0:128], [], ['identb'], dsem_c[1])
            dma('pool', onesb[:], di['c_all'][:, 128:256], [], ['onesb'], dsem_c[1])
            dma('pool', bdb[:], di['c_all'][:, 256:384], [], ['bdb'], dsem_c[1])
            dma('pool', w2a2[0:64, :], di['decay_w2'], [], ['w2a2'], dsem_c[1])
            dma('pool', w2a2[64:128, :], di['aaa_a2'], [], ['w2a2'], dsem_c[1])
            dma('pool', g2[:], di['gate_g2'], [], ['g2'], dsem_c[1])
        if 'm' in PARTS or PARTS == 'abcdefg':
            P.op('dve', lambda e: e.memset(wrbd[:], 0.0), writes=['wrbd'])
            P.op('dve', lambda e: e.memset(wibd[:], 0.0), writes=['wibd'])
        for (wt, nm, key) in ([(wrbd, 'lru_wr', 'wrbd'), (wibd, 'lru_wi', 'wibd')] if 'c' in PARTS else []):
            src = di[nm].rearrange('(c two) i o -> two i c o', two=2)
            for par in range(2):
                dma('pool', wt[par * 64:(par + 1) * 64, :, par * 64:(par + 1) * 64], src[par], [], [key], dsem_c[1])
        P.barrier(['identb', 'onesb', 'bdb', 'w2a2', 'g2', 'wrbd', 'wibd'])
        prm = [tok32[0], tok32[1]]
        rows = []
        rows.append((lng[:].rearrange('p l c -> p (l c)'), di['prm'][0:32, :], 'lng'))
        rows.append((lnb[:].rearrange('p l c -> p (l c)'), di['prm'][32:64, :], 'lnb'))
        rows.append((mu[:], di['prm'][64:90, :], 'mu'))
        rows.append((cw[:].rearrange('p l c -> p (l c)'), di['prm'][90:122, :], 'cw'))
        for vi_, v in enumerate(VEC_NAMES):
            rows.append((vec[v][:], di['prm'][122 + 8 * vi_:130 + 8 * vi_, :], 'v_' + v))
        groups = [[]]
        cnt = 0
        for r_ in rows:
            k_ = r_[1].shape[0]
            if cnt + k_ > 128:
                groups.append([]); cnt = 0
            groups[-1].append((cnt, k_) + r_)
            cnt += k_
        for gi_, grp in enumerate(groups if 'd' in PARTS else []):
            tk = prm[gi_ % 2]
            tot = 0
            for (o_, k_, dst, src, key) in grp:
                dma('sp', tk[o_:o_ + k_, 0:128], src, [], [('tok32', gi_ % 2)], dsem_c[2 + gi_ % 2])
                tot = o_ + k_
            pb = bank()
            tr(ps[:, pb, 0:tot], tk[0:tot, 0:128], ident[0:tot, 0:tot], [('tok32', gi_ % 2), 'ident'], [('ps', pb)])
            for (o_, k_, dst, src, key) in grp:
                cp(dst, ps[:, pb, o_:o_ + k_], [('ps', pb)], [key])
        if 'e' in PARTS:
            ts(omu[:], mu[:], -1.0, 1.0, ALU.mult, ALU.add, ['mu'], ['omu'])
            ts(oka[:], vec['k_a'][:], -1.0, 1.0, ALU.mult, ALU.add, ['v_k_a'], ['oka'])
            act(lsp[:], vec['lru_lambda'][:], AF.Exp, ['v_lru_lambda'], ['lsp'], scale=-1.0)
            act(lsp[:], lsp[:], AF.Ln, ['lsp'], ['lsp'], bias=1.0)
            ts(lsp[:], lsp[:], -8.0, None, ALU.mult, None, ['lsp'], ['lsp'])

        def wblocks():
            def ffn_blocks(wi, wo):
                for g in range(11):
                    def f(buf, g=g, wi=wi):
                        v = buf[:, 0:4096].rearrange('p (k c) -> p k c', k=8)
                        return [(v[:, :, 0:256], di[wi][:, g * 256:(g + 1) * 256].rearrange('(k p) c -> p k c', p=128)),
                                (v[:, :, 256:512], di[wi][:, DFF + g * 256:DFF + (g + 1) * 256].rearrange('(k p) c -> p k c', p=128))]
                    yield ((wi, g), f)
                for mp in range(8):
                    def f(buf, mp=mp, wo=wo):
                        v = buf[:, 0:NJ * 128].rearrange('p (j c) -> p j c', j=NJ)
                        return [(v, di[wo][:, mp * 128:(mp + 1) * 128].rearrange('(j p) c -> p j c', p=128))]
                    yield ((wo, mp), f)

            def sq_blocks(w, ncols):
                nb = (ncols + 511) // 512
                for b in range(nb):
                    c0 = b * 512
                    cn = min(512, ncols - c0)
                    def f(buf, c0=c0, cn=cn, w=w):
                        v = buf[:, 0:8 * cn].rearrange('p (k c) -> p k c', k=8)
                        return [(v, di[w][:, c0:c0 + cn].rearrange('(k p) c -> p k c', p=128))]
                    yield ((w, b), f)
            yield from sq_blocks('xa_wk', D)
            yield from sq_blocks('xa_wv', D)
            def one_pass():
                yield from ffn_blocks('ffn1_wi', 'ffn1_wo')
                yield from sq_blocks('w_in', PW)
                yield from sq_blocks('w_mix_out', D)
                yield from sq_blocks('xa_wq', D)
                yield from sq_blocks('xa_wo', D)
                yield from ffn_blocks('ffn2_wi', 'ffn2_wo')
            for it in range(int(os.environ.get('NTI', NTILES)) + (1 if do_sample else 0)):
                for blk, (tag, f) in enumerate(one_pass()):
                    yield (tag, f, it, blk)

        wgen = wblocks()
        wstate = dict(issued=0, consumed=0, pending=[])

        NBLK = 59
        wsc = nc.dram_tensor('wsc', [NBLK, 128, WCAP], BF16, kind='Internal').ap()
        wbsem = [P.dma_sem() for i in range(NWB)]

        def w_used(tag):
            if tag[0].endswith('_wi'):
                return 4096
            if tag[0].endswith('_wo') and tag[0].startswith('ffn'):
                return NJ * 128
            ncols = PW if tag[0] == 'w_in' else D
            return 8 * min(512, ncols - tag[1] * 512)

        def w_issue():
            try:
                item = next(wgen)
            except StopIteration:
                return False
            i = wstate['issued'] % NWB
            if len(item) == 2:
                tag, f = item
                for (dst, src) in f(wbuf[i]):
                    dma('pool', dst, src, [], [('wbuf', i)], wsem[i])
            else:
                tag, f, it, blk = item
                used = w_used(tag)
                if it == 0:
                    for (dst, src) in f(wbuf[i]):
                        dma('pool', dst, src, [], [('wbuf', i)], wsem[i])
                    dma('sp', wsc[blk, :, 0:used], wbuf[i][:, 0:used], [('wbuf', i)], [('wsc', blk)], wbsem[i])
                else:
                    dma('pool', wbuf[i][:, 0:used], wsc[blk, :, 0:used], [('wsc', blk)], [('wbuf', i)], wsem[i])
            wstate['pending'].append((tag, i))
            wstate['issued'] += 1
            return True

        def w_next(tag):
            while wstate['issued'] - wstate['consumed'] < NWB:
                if not w_issue():
                    break
            t, i = wstate['pending'].pop(0)
            assert t == tag, (t, tag)
            wstate['consumed'] += 1
            return wbuf[i], ('wbuf', i)

        def sqview(buf, cn):
            return buf[:, 0:8 * cn].rearrange('p (k c) -> p k c', k=8)

        def load_tokens_fm(src_rows_ap, nrows, dst32, dstb, col0, dkey):
            tokctr[0] += 1
            i = tokctr[0] % 2 if 'A' in os.environ.get('TOG', 'A') else 0
            tk = tok32[i]
            dma('sp', tk[0:nrows, :], src_rows_ap, [], [('tok32', i)], dsem_in[i])
            for half in range(2):
                b = bank()
                for q in range(4):
                    kc = half * 4 + q
                    P.op('pe', lambda e, b=b, q=q, kc=kc: e.transpose(ps[:, b, q * 128:q * 128 + nrows], tk[0:nrows, kc * 128:(kc + 1) * 128], ident[0:nrows, 0:nrows]),
                         reads=[('tok32', i), 'ident'], writes=[('ps', b)], track=('W' not in os.environ.get('TOG', '')))
                src = ps[:, b, :].rearrange('p (q t) -> p q t', q=4)[:, :, 0:nrows]
                if dst32 is not None:
                    cp(dst32[:, half * 4:half * 4 + 4, col0:col0 + nrows], src, [('ps', b), ('tok32', i)], [dkey], eng='act')
                if dstb is not None:
                    cp(dstb[:, half * 4:half * 4 + 4, col0:col0 + nrows], src, [('ps', b)], ['hb' if dkey == 'h32' else 'H3'], eng='dve')

        def store_fm_tokens(src32, skey, col0, nrows, dst_rows_ap, nch=8, feat0=0):
            i = bank_ctr[0] % 2
            tk = tok32[i]
            for g0 in range(0, nch, 4):
                b = bank()
                gn = min(4, nch - g0)
                for q in range(gn):
                    tr(ps[0:nrows, b, q * 128:(q + 1) * 128], src32[:, g0 + q, col0:col0 + nrows], ident[:, :],
                       [skey, 'ident'], [('ps', b)])
                cp(tk[0:nrows, g0 * 128:(g0 + gn) * 128], ps[0:nrows, b, 0:gn * 128], [('ps', b)], [('tok32', i)], eng='act')
            dma('sp', dst_rows_ap, tk[0:nrows, 0:nch * 128], [('tok32', i)], [], dsem_out[i], is_out=True)

        def layernorm(idx, n, eps):
            P.phase = 'layernorm'
            zsq = HB_[0]
            cp(hb[:, :, 0:n], h32[:, :, 0:n], ['h32'], ['hb'], eng='act')
            act(zsq[:, :, 0:n], h32[:, :, 0:n], AF.Square, ['h32'], ['H0'])
            b1 = bank(); b2 = bank()
            for kc in range(8):
                mm(ps[:, b1, 0:n], onesb[:], hb[:, kc, 0:n], kc == 0, kc == 7, ['onesb', 'hb'], [('ps', b1)])
            for kc in range(8):
                mm(ps[:, b2, 0:n], onesb[:], zsq[:, kc, 0:n], kc == 0, kc == 7, ['onesb', 'H0'], [('ps', b2)])
            mean, msq, var, rstd, nmr = [s[:, 0:n] for s in st1]
            ts(mean, ps[:, b1, 0:n], 1.0 / D, None, ALU.mult, None, [('ps', b1)], ['st0'])
            tt(msq, mean, mean, ALU.mult, ['st0'], ['st1'])
            stt(var, ps[:, b2, 0:n], 1.0 / D, msq, ALU.mult, ALU.subtract, [('ps', b2), 'st1'], ['st2'])
            ts(var, var, 0.0, eps, ALU.max, ALU.add, ['st2'], ['st2'])
            act(var, var, AF.Sqrt, ['st2'], ['st2'])
            recip(rstd, var, ['st2'], ['st3'])
            tt(nmr, mean, rstd, ALU.mult, ['st0', 'st3'], ['st4'])
            tt(h32[:, :, 0:n], h32[:, :, 0:n], rstd.unsqueeze(1).to_broadcast([128, 8, n]), ALU.mult, ['h32', 'st3'], ['h32'])
            tt(h32[:, :, 0:n], h32[:, :, 0:n], nmr.unsqueeze(1).to_broadcast([128, 8, n]), ALU.subtract, ['h32', 'st4'], ['h32'])
            for kc in range(8):
                act(h32[:, kc, 0:n], h32[:, kc, 0:n], AF.Identity, ['h32', 'lng', 'lnb'], ['h32'],
                    bias=lnb[:, idx, kc:kc + 1], scale=lng[:, idx, kc:kc + 1])
            cp(hb[:, :, 0:n], h32[:, :, 0:n], ['h32'], ['hb'], eng='act')

        def ffn(wi, wo, ln_idx, n):
            P.phase = 'ffn'
            actb = HX
            sg = [sgb[0][:, 0:n], sgb[1][:, 0:n]]
            for g in range(11):
                wb, wk = w_next((wi, g))
                wv = sqview(wb, 512)
                for jj in range(2):
                    j = 2 * g + jj
                    pg = bank(); pu = bank()
                    for kc in range(8):
                        mm(ps[:, pg, 0:n], wv[:, kc, jj * 128:(jj + 1) * 128], hb[:, kc, 0:n], kc == 0, kc == 7, [wk, 'hb'], [('ps', pg)])
                    for kc in range(8):
                        mm(ps[:, pu, 0:n], wv[:, kc, 256 + jj * 128:256 + (jj + 1) * 128], hb[:, kc, 0:n], kc == 0, kc == 7, [wk, 'hb'], [('ps', pu)])
                    act(sg[jj], ps[:, pg, 0:n], AF.Silu, [('ps', pg)], [('sg', jj)])
                    tt(actb[:, j, 0:n], sg[jj], ps[:, pu, 0:n], ALU.mult, [('sg', jj), ('ps', pu)], ['HX%d' % (j // 8)])
            for m in range(8):
                wb, wk = w_next((wo, m))
                wv = wb[:, 0:NJ * 128].rearrange('p (j c) -> p j c', j=NJ)
                po = bank()
                for j in range(NJ):
                    mm(ps[:, po, 0:n], wv[:, j, :], actb[:, j, 0:n], j == 0, j == NJ - 1, [wk, 'HX%d' % (j // 8)], [('ps', po)])
                stt(h32[:, m, 0:n], ps[:, po, 0:n], 0.5 / ALPHA, h32[:, m, 0:n], ALU.mult, ALU.add, [('ps', po), 'h32'], ['h32'])
            layernorm(ln_idx, n, LN_EPS / (ALPHA * ALPHA))

        def proj(w, ncols, xin, xkey, n, evac):
            nb = (ncols + 511) // 512
            for b in range(nb):
                c0 = b * 512
                cn = min(512, ncols - c0)
                wb, wk = w_next((w, b))
                wv = sqview(wb, cn)
                for q in range(cn // 128):
                    m = c0 // 128 + q
                    pb = bank()
                    for kc in range(8):
                        mm(ps[:, pb, 0:n], wv[:, kc, q * 128:(q + 1) * 128], xin[:, kc, 0:n], kc == 0, kc == 7, [wk, xkey], [('ps', pb)])
                    evac(m, pb)

        def mem_kv():
            P.phase = 'mem_kv'
            for r in range(2):
                load_tokens_fm(di['mem'][r * 128:(r + 1) * 128, :], 128, None, memT, r * 128, 'memT')
            for (w, outname, isk) in [('xa_wk', 'pmk', True), ('xa_wv', 'pmv', False)]:
                for b in range(2):
                    wb, wk = w_next((w, b))
                    wv = sqview(wb, 512)
                    for r in range(2):
                        pb = bank()
                        for kc in range(8):
                            mm(ps[:, pb, :], memT[:, kc, r * 128:(r + 1) * 128], wv[:, kc, :], kc == 0, kc == 7, ['H3', wk], [('ps', pb)])
                        i = bank_ctr[0] % 2
                        cp(tok32[i][:, 0:512], ps[:, pb, :], [('ps', pb)], [('tok32', i)], eng='act')
                        if not isk:
                            cp(mvb[:, r, b * 512:(b + 1) * 512], ps[:, pb, :], [('ps', pb)], ['mvb'], eng='dve')
                        dma('sp', di[outname][r * 128:(r + 1) * 128, b * 512:(b + 1) * 512], tok32[i][:, 0:512], [('tok32', i)], [], dsem_out[i], is_out=True)
                    if isk:
                        for q in range(4):
                            m = b * 4 + q
                            pb = bank()
                            for kc in range(8):
                                mm(ps[:, pb, 0:NMEM], wv[:, kc, q * 128:(q + 1) * 128], memT[:, kc, :], kc == 0, kc == 7, [wk, 'H3'], [('ps', pb)])
                            cp(mkT[:, m, :], ps[:, pb, 0:NMEM], [('ps', pb)], ['mkT'], eng='act')

        def mixer_prompt(ti):
            P.phase = 'mixer_prompt'
            n = NT
            r32, k32, v32, a32, ls32 = FB_
            glb = HB_[1]; geb = HB_[2]; g0b = HB_[3]
            g1b = HX[:, 0:8, :]; xcb = HX[:, 8:16, :]
            first = (ti == 0)
            tmp = st1[0]

            def evac(m, pb):
                psn = ps[:, pb, 0:n]
                SS, KP = stsel(m)
                tmp = SS[0]
                if m < 26:
                    act(tmp[:, 1:n], ps[:, pb, 0:n - 1], AF.Copy, [('ps', pb), 'mu'], [KP + '0'], scale=mu[:, m:m + 1])
                    if first:
                        P.op('dve', lambda e, tmp=tmp: e.memset(tmp[:, 0:1], 0.0), writes=[KP + '0'])
                    else:
                        tt(tmp[:, 0:1], carry_sh[:, m:m + 1], mu[:, m:m + 1], ALU.mult, ['carry_sh', 'mu'], [KP + '0'])
                    cp(carry_sh[:, m:m + 1], ps[:, pb, n - 1:n], [('ps', pb)], ['carry_sh'])
                    if m < 24:
                        dst = [r32, k32, v32][m // 8][:, m % 8, :]
                        dkey = ['F0', 'F1', 'F2'][m // 8]
                        stt(dst, psn, omu[:, m:m + 1], tmp[:, 0:n], ALU.mult, ALU.add, [('ps', pb), 'omu', KP + '0'], [dkey])
                    else:
                        xs_ = SS[1][:, 0:n]
                        stt(xs_, psn, omu[:, m:m + 1], tmp[:, 0:n], ALU.mult, ALU.add, [('ps', pb), 'omu', KP + '0'], [KP + '1'])
                        lb = SS[2][:, 0:n].bitcast(BF16)[:, 0:n]
                        if m == 24:
                            act(lb[0:64, :], xs_[0:64, :], AF.Tanh, [KP + '1'], [KP + '2'])
                            cp(lb[64:128, :], xs_[64:128, :], [KP + '1'], [KP + '2'])
                            for (lo, dstt, dk, bvec) in [(0, ls32, 'F4', 'decay_w0'), (64, a32, 'F3', 'aaa_a0')]:
                                for q in range(8):
                                    p2 = bank()
                                    mm(ps[:, p2, 0:n], w2a2[lo:lo + 64, q * 128:(q + 1) * 128], lb[lo:lo + 64, :], True, True, ['w2a2', KP + '2'], [('ps', p2)])
                                    act(dstt[:, q, :], ps[:, p2, 0:n], AF.Sigmoid, [('ps', p2), 'v_' + bvec], [dk], bias=vec[bvec][:, q:q + 1])
                        else:
                            act(lb, xs_, AF.Sigmoid, [KP + '1'], [KP + '2'])
                            for q in range(8):
                                p2 = bank()
                                mm(ps[:, p2, 0:n], g2[:, q * 128:(q + 1) * 128], lb, True, True, ['g2', KP + '2'], [('ps', p2)])
                                cp(glb[:, q, :], ps[:, p2, 0:n], [('ps', p2)], ['H1'], eng='act')
                elif m < 34:
                    cp(pl[:, m - 26, 3:3 + n], psn, [('ps', pb)], ['pl'], eng='act')
                elif m < 42:
                    act(geb[:, m - 34, :], psn, AF.Gelu, [('ps', pb)], ['H2'])
                elif m < 50:
                    act(g0b[:, m - 42, :], psn, AF.Sigmoid, [('ps', pb)], ['H3'])
                else:
                    act(g1b[:, m - 50, :], psn, AF.Sigmoid, [('ps', pb)], ['HX0'])

            if first:
                P.op('dve', lambda e: e.memset(pl[:, :, 0:3], 0.0), writes=['pl'])
            else:
                cp(pl[:, :, 0:3], osml[:, :, 4:7], ['osml'], ['pl'])
            proj('w_in', PW, hb, 'hb', n, evac)
            cp(osml[:, :, 4:7], pl[:, :, n:n + 3], ['pl'], ['osml'])
            dump('r32', r32[:], 'F0'); dump('k32', k32[:], 'F1'); dump('v32', v32[:], 'F2'); dump('a32', a32[:], 'F3'); dump('ls32', ls32[:], 'F4')
            if ti == int(os.environ.get('NTI', NTILES)) - 1:
                cp(osml[:, :, 0:3], pl[:, :, n:n + 3], ['pl'], ['osml'])
                cp(osh[:], carry_sh[:], ['carry_sh'], ['osh'])
            return dict(r32=r32, k32=k32, v32=v32, a32=a32, ls32=ls32, glb=glb, geb=geb, g0b=g0b, g1b=g1b, xcb=xcb)

        def lru_prompt(ti, B):
            P.phase = 'lru_prompt'
            n = NT
            geb, g1b, xcb = B['geb'], B['g1b'], B['xcb']
            lmb = HX[:, 16:24, :]
            for c in range(8):
                SS, KP = stsel(c)
                xc = SS[0][:, 0:n]; gr = SS[1][:, 0:n]; gi = SS[2][:, 0:n]; t3 = SS[3][:, 0:n]; hs = SS[4][:, 0:n]
                act(xc, pl[:, c, 3:3 + n], AF.Identity, ['pl', 'cw', 'v_conv_b'], [KP + '0'], bias=vec['conv_b'][:, c:c + 1], scale=cw[:, 3, c:c + 1])
                for j in range(3):
                    stt(xc, pl[:, c, j:j + n], cw[:, j, c:c + 1], xc, ALU.mult, ALU.add, ['pl', 'cw', KP + '0'], [KP + '0'])
                cp(xcb[:, c, :], xc, [KP + '0'], ['HX1'], eng='act')
                p1 = bank(); p2 = bank()
                mm(ps[:, p1, 0:n], wrbd[:, c, :], xcb[:, c, :], True, True, ['wrbd', 'HX1'], [('ps', p1)])
                mm(ps[:, p2, 0:n], wibd[:, c, :], xcb[:, c, :], True, True, ['wibd', 'HX1'], [('ps', p2)])
                act(gr, ps[:, p1, 0:n], AF.Sigmoid, [('ps', p1), 'v_lru_br'], [KP + '1'], bias=vec['lru_br'][:, c:c + 1])
                act(gi, ps[:, p2, 0:n], AF.Sigmoid, [('ps', p2), 'v_lru_bi'], [KP + '2'], bias=vec['lru_bi'][:, c:c + 1])
                act(gr, gr, AF.Exp, [KP + '1', 'lsp'], [KP + '1'], scale=lsp[:, c:c + 1])
                act(t3, gr, AF.Square, [KP + '1'], [KP + '3'])
                ts(t3, t3, -1.0, 1.0, ALU.mult, ALU.add, [KP + '3'], [KP + '3'])
                ts(t3, t3, 0.0, None, ALU.max, None, [KP + '3'], [KP + '3'])
                act(t3, t3, AF.Sqrt, [KP + '3'], [KP + '3'])
                tt(gi, gi, xc, ALU.mult, [KP + '2', KP + '0'], [KP + '2'])
                tt(gi, gi, t3, ALU.mult, [KP + '2', KP + '3'], [KP + '2'])
                if ti == 0:
                    P.op('dve', lambda e, hs=hs, gr=gr, gi=gi: e.tensor_tensor_scan(out=hs, data0=gr, data1=gi, initial=0.0, op0=ALU.mult, op1=ALU.add),
                         reads=[KP + '1', KP + '2'], writes=[KP + '4'])
                else:
                    P.op('dve', lambda e, c=c, hs=hs, gr=gr, gi=gi: e.tensor_tensor_scan(out=hs, data0=gr, data1=gi, initial=carry_h[:, c:c + 1], op0=ALU.mult, op1=ALU.add),
                         reads=[KP + '1', KP + '2', 'carry_h'], writes=[KP + '4'])
                cp(carry_h[:, c:c + 1], hs[:, n - 1:n], [KP + '4'], ['carry_h'])
                tt(t3, hs, geb[:, c, :], ALU.mult, [KP + '4', 'H2'], [KP + '3'])
                tt(lmb[:, c, :], t3, g1b[:, c, :], ALU.mult, [KP + '3', 'HX0'], ['HX2'])
            if ti == int(os.environ.get('NTI', NTILES)) - 1:
                cp(osml[:, :, 3:4], carry_h[:].unsqueeze(2), ['carry_h'], ['osml'])
            return lmb


        plb = pl[:].rearrange('p a b -> p (a b)').bitcast(BF16)
        plA = plb[:, 0:8 * NT].rearrange('p (a b) -> p a b', a=8)
        plB = plb[:, 8 * NT:16 * NT].rearrange('p (a b) -> p a b', a=8)

        def rwkv_core(B, state_in, state_out, skip_inverse=False):
            P.phase = 'rwkv_core'
            n = NT
            r32, k32, v32, a32, ls32 = B['r32'], B['k32'], B['v32'], B['a32'], B['ls32']
            bc8 = lambda v: v[:].unsqueeze(2).to_broadcast([128, 8, n])
            QR = HX[:, 0:16, :].rearrange('p (k two) (c t) -> p k two c t', two=2, t=C)
            KT = HB_[0]; NB = hb
            kk32 = pl[:, :, 0:n]
            tt(kk32, k32[:], bc8(vec['k_k']), ALU.mult, ['F1', 'v_k_k'], ['pl'])
            act(KT[:], kk32, AF.Square, ['pl'], ['H0'])
            for kc in range(8):
                pb = bank()
                mm(ps[:, pb, 0:n], bdb[:], KT[:, kc, :], True, True, ['bdb', 'H0'], [('ps', pb)])
                s_ = st1[kc % 2][:, 0:n]; sk = 'st%d' % (kc % 2)
                act(s_, ps[:, pb, 0:n], AF.Sqrt, [('ps', pb)], [sk])
                ts(s_, s_, 1e-12, None, ALU.max, None, [sk], [sk])
                recip(s_, s_, [sk], [sk])
                tt(kk32[:, kc, :], kk32[:, kc, :], s_, ALU.mult, ['pl', sk], ['pl'])
            for kc in range(8):
                u_ = st1[2 + kc % 2][:, 0:n]; sk = 'st%d' % (2 + kc % 2)
                ts(u_, a32[:, kc, :], vec['k_a'][:, kc:kc + 1], oka[:, kc:kc + 1], ALU.mult, ALU.add, ['F3', 'v_k_a', 'oka'], [sk])
                tt(k32[:, kc, :], k32[:, kc, :], u_, ALU.mult, ['F1', sk], ['F1'])
            tt(a32[:], a32[:], kk32, ALU.mult, ['F3', 'pl'], ['F3'])
            rkb = KT
            for kc in range(8):
                u_ = st1[kc % 2][:, 0:n]; sk = 'st%d' % (kc % 2)
                tt(u_, r32[:, kc, :], k32[:, kc, :], ALU.mult, ['F0', 'F1'], [sk])
                ts(rkb[:, kc, :], u_, vec['r_k'][:, kc:kc + 1], None, ALU.mult, None, [sk, 'v_r_k'], ['H0'])
            for kc in range(8):
                SS, KP = stsel(kc)
                cs = SS[0][:, 0:n]; dd = SS[1][:, 0:n]; Wi = SS[2][:, 0:n]; We = SS[3][:, 0:n]; Wv = SS[4][:, 0:n]
                P.op('dve', lambda e, kc=kc, cs=cs: e.tensor_tensor_scan(out=cs, data0=reset[:, 0:n], data1=ls32[:, kc, :], initial=0.0, op0=ALU.mult, op1=ALU.add),
                     reads=['reset', 'F4'], writes=[KP + '0'])
                tt(dd, cs, ls32[:, kc, :], ALU.subtract, [KP + '0', 'F4'], [KP + '1'])
                act(Wi, cs, AF.Exp, [KP + '0'], [KP + '2'], scale=-C0)
                act(We, dd, AF.Exp, [KP + '1'], [KP + '3'], scale=-C0)
                act(Wv, cs, AF.Exp, [KP + '0'], [KP + '4'], scale=C0)
                c4 = lambda a: a.rearrange('p (c t) -> p c t', t=C)
                tt(QR[:, kc, 1, :, :], c4(r32[:, kc, :]), c4(Wi), ALU.mult, ['F0', KP + '2'], ['HX0', 'HX1'])
                tt(QR[:, kc, 0, :, :], c4(kk32[:, kc, :]), c4(We), ALU.mult, ['pl', KP + '3'], ['HX0', 'HX1'])
                cp(WCs[:, kc, :], c4(Wi)[:, :, C - 1], [KP + '2'], ['WCs'])
                tt(Wi, k32[:, kc, :], Wv, ALU.mult, ['F1', KP + '4', KP + '2'], [KP + '2'])
                stt(We, a32[:, kc, :], -1.0, Wv, ALU.mult, ALU.mult, ['F3', KP + '4', KP + '3'], [KP + '3'])
                cp(NB[:, kc, :], We, [KP + '3'], ['hb'], eng='act')
                cp(FB_[4][:, kc, :], Wi, [KP + '2'], ['F4'], eng='act')
            vb = plB
            cp(vb[:, :, 0:n], v32[:], ['F2'], ['pl'], eng='act')
            for kc in range(8):
                pb = bank()
                mm(ps[:, pb, 0:n], bdb[:], rkb[:, kc, :], True, True, ['bdb', 'H0'], [('ps', pb)])
                tt(v32[:, kc, :], v32[:, kc, :], ps[:, pb, 0:n], ALU.mult, ['F2', ('ps', pb)], ['F2'])
            cp(KT[:], FB_[4][:], ['F4'], ['H0'], eng='act')
            tmv = lambda a: a.rearrange('p a b -> p (a b)').rearrange('p (c x) -> p c x', c=NCH)
            vT = tmv(HB_[2][:]); kTt = tmv(plA); nbT = tmv(plB)

            def to_tokmajor(src, skey, dst, dkey):
                for c in range(NCH):
                    pb = bank()
                    pbv = ps[:, pb, :].bitcast(BF16)
                    for hp in range(8):
                        for par in range(2):
                            lo = par * 64
                            tr(pbv[lo:lo + 64, hp * 64:(hp + 1) * 64], src[lo:lo + 64, hp, c * C:(c + 1) * C], identb[lo:lo + 64, lo:lo + 64],
                               [skey, 'identb'], [('ps', pb)])
                    cp(dst[:, c, :], pbv[:, 0:512], [('ps', pb)], [dkey], eng=('act' if c % 2 else 'dve'))
            to_tokmajor(vb, 'pl', vT, 'H2')
            to_tokmajor(KT, 'H0', kTt, 'pl')
            to_tokmajor(NB, 'hb', nbT, 'pl')
            vTv = lambda c, hp, lo: vT[lo:lo + 64, c, hp * 64:(hp + 1) * 64]
            kTv = lambda c, hp, lo: kTt[lo:lo + 64, c, hp * 64:(hp + 1) * 64]
            nTv = lambda c, hp, lo: nbT[lo:lo + 64, c, hp * 64:(hp + 1) * 64]
            m_su_ui = mask[:, 0:128].unsqueeze(1).to_broadcast([128, 8, 128])
            m_sl = mask[:, 128:192].unsqueeze(1).to_broadcast([128, 8, 64])
            m_eye = mask[:, 192:256].unsqueeze(1).to_broadcast([128, 8, 64])
            for c0 in range(0, NCH, 2):
                ctx = []
                for s in range(2):
                    c = c0 + s
                    b1a = bank(); b1b = bank()
                    for hp in range(8):
                        for par in range(2):
                            lo = par * 64
                            bsel = b1a if hp < 4 else b1b
                            mm(ps[lo:lo + 64, bsel, (hp % 4) * 128:(hp % 4 + 1) * 128], KT[lo:lo + 64, hp, c * C:(c + 1) * C],
                               QR[lo:lo + 64, hp, :, c, :], True, True, ['H0', 'HX0', 'HX1'], [('ps', bsel)])
                    for (bsel, h0) in [(b1a, 0), (b1b, 4)]:
                        tt(LP[:, h0:h0 + 4, c, :], ps[:, bsel, :].rearrange('p (h x) -> p h x', h=4), m_su_ui[:, 0:4, :], ALU.mult,
                           [('ps', bsel), 'mask'], [('LP', c)])
                    b2a = bank(); b2b = bank()
                    for hp in range(8):
                        for par in range(2):
                            lo = par * 64
                            bsel = b2a if hp < 4 else b2b
                            mm(ps[lo:lo + 64, bsel, (hp % 4) * 128:(hp % 4 + 1) * 128], NB[lo:lo + 64, hp, c * C:(c + 1) * C],
                               QR[lo:lo + 64, hp, :, c, :], True, True, ['hb', 'HX0', 'HX1'], [('ps', bsel)])
                    for (bsel, h0) in [(b2a, 0), (b2b, 4)]:
                        tt(PN[:, h0:h0 + 4, c, :], ps[:, bsel, :].rearrange('p (h x) -> p h x', h=4), m_su_ui[:, 0:4, :], ALU.mult,
                           [('ps', bsel), 'mask'], [('PN', c)])
                    b3 = bank()
                    for hp in range(8):
                        for par in range(2):
                            lo = par * 64
                            mm(ps[lo:lo + 64, b3, hp * 64:(hp + 1) * 64], QR[lo:lo + 64, hp, 0, c, :], NB[lo:lo + 64, hp, c * C:(c + 1) * C],
                               True, True, ['HX0', 'HX1', 'hb'], [('ps', b3)])
                    pp = PP[s][0]
                    cp(pp[:, :, 0:64], PN[:, :, c, 0:64], [('PN', c)], [('PP', s, 0)], eng='act')
                    tt(pp[:, :, 64:128], ps[:, b3, :].rearrange('p (h x) -> p h x', h=8), m_sl, ALU.mult, [('ps', b3), 'mask'], [('PP', s, 0)])
                    tt(X32[s][:], PN[:, :, c, 0:64], m_eye, ALU.add, [('PN', c), 'mask'], [('X32', s)])
                    cp(Xb[s][:], X32[s][:], [('X32', s)], [('Xb', s)], eng='act')
                    ctx.append(c)
                if skip_inverse:
                    for s_ in range(2):
                        cp(XT[:, :, ctx[s_], :], X32[s_][:], [('X32', s_)], [('XT', ctx[s_])], eng='act')
                for lvl in ([] if skip_inverse else range(1, 6)):
                    cur = (lvl - 1) % 2; nxt = lvl % 2
                    banks = []
                    for s in range(2):
                        ba = bank(); bb = bank()
                        src = PP[s][cur]
                        for hp in range(8):
                            for par in range(2):
                                lo = par * 64
                                bsel = ba if hp < 4 else bb
                                o0 = (hp % 4) * 128
                                mm(ps[lo:lo + 64, bsel, o0:o0 + 64], src[lo:lo + 64, hp, 64:128], src[lo:lo + 64, hp, 0:64], True, True,
                                   [('PP', s, cur)], [('ps', bsel)])
                                mm(ps[lo:lo + 64, bsel, o0 + 64:o0 + 128], src[lo:lo + 64, hp, 0:64], src[lo:lo + 64, hp, 64:128], True, True,
                                   [('PP', s, cur)], [('ps', bsel)])
                        banks.append((ba, bb))
                    for s in range(2):
                        ba, bb = banks[s]
                        dst = PP[s][nxt]
                        cp(dst[:, 0:4, :], ps[:, ba, :].rearrange('p (h x) -> p h x', h=4), [('ps', ba)], [('PP', s, nxt)], eng='act')
                        cp(dst[:, 4:8, :], ps[:, bb, :].rearrange('p (h x) -> p h x', h=4), [('ps', bb)], [('PP', s, nxt)], eng='dve')
                    xb_ = []
                    for s in range(2):
                        bx = bank()
                        src = PP[s][nxt]
                        for hp in range(8):
                            for par in range(2):
                                lo = par * 64
                                mm(ps[lo:lo + 64, bx, hp * 64:(hp + 1) * 64], src[lo:lo + 64, hp, 64:128], Xb[s][lo:lo + 64, hp, :], True, True,
                                   [('PP', s, nxt), ('Xb', s)], [('ps', bx)])
                        xb_.append(bx)
                    for s in range(2):
                        bx = xb_[s]
                        tt(X32[s][:], X32[s][:], ps[:, bx, :].rearrange('p (h x) -> p h x', h=8), ALU.add, [('X32', s), ('ps', bx)], [('X32', s)])
                        if lvl < 5:
                            cp(Xb[s][:], X32[s][:], [('X32', s)], [('Xb', s)], eng='act')
                        else:
                            cp(XT[:, :, ctx[s], :], X32[s][:], [('X32', s)], [('XT', ctx[s])], eng='act')
            Y32 = FB_[0]
            for c in range(NCH):
                state_in(c)
                cur = c % 2
                a0 = A0b[cur]
                bR = bank()
                for hp in range(8):
                    for par in range(2):
                        lo = par * 64
                        o = ps[lo:lo + 64, bR, hp * 64:(hp + 1) * 64]
                        mm(o, QR[lo:lo + 64, hp, 0, c, :], a0[lo:lo + 64, hp, :], True, False, ['HX0', 'HX1', ('A0b', cur)], [('ps', bR)])
                        mm(o, LP[lo:lo + 64, hp, c, 0:64], vTv(c, hp, lo), False, True, [('LP', c), 'H2'], [('ps', bR)])
                cp(RHSb[:], ps[:, bR, :].rearrange('p (h x) -> p h x', h=8), [('ps', bR)], ['RHSb'], eng='act')
                bU = bank()
                for hp in range(8):
                    for par in range(2):
                        lo = par * 64
                        mm(ps[lo:lo + 64, bU, hp * 64:(hp + 1) * 64], XT[lo:lo + 64, hp, c, :], RHSb[lo:lo + 64, hp, :], True, True,
                           [('XT', c), 'RHSb'], [('ps', bU)])
                cp(Ub[:], ps[:, bU, :].rearrange('p (h x) -> p h x', h=8), [('ps', bU)], ['Ub'], eng='act')
                bD = bank()
                for hp in range(8):
                    for par in range(2):
                        lo = par * 64
                        o = ps[lo:lo + 64, bD, hp * 64:(hp + 1) * 64]
                        mm(o, kTv(c, hp, lo), vTv(c, hp, lo), True, False, ['pl', 'H2'], [('ps', bD)])
                        mm(o, nTv(c, hp, lo), Ub[lo:lo + 64, hp, :], False, True, ['pl', 'Ub'], [('ps', bD)])
                bY = bank()
                for hp in range(8):
                    for par in range(2):
                        lo = par * 64
                        o = ps[lo:lo + 64, bY, hp * 64:(hp + 1) * 64]
                        mm(o, a0[lo:lo + 64, hp, :], QR[lo:lo + 64, hp, 1, c, :], True, False, [('A0b', cur), 'HX0', 'HX1'], [('ps', bY)])
                        mm(o, vTv(c, hp, lo), LP[lo:lo + 64, hp, c, 64:128], False, False, ['H2', ('LP', c)], [('ps', bY)])
                        mm(o, Ub[lo:lo + 64, hp, :], PN[lo:lo + 64, hp, c, 64:128], False, True, ['Ub', ('PN', c)], [('ps', bY)])
                tt(A32[:], A32[:], ps[:, bD, :].rearrange('p (h x) -> p h x', h=8), ALU.add, ['A32', ('ps', bD)], ['A32'])
                tt(A32[:], A32[:], WCs[:, :, c:c + 1].to_broadcast([128, 8, 64]), ALU.mult, ['A32', 'WCs'], ['A32'])
                cp(A0b[1 - cur][:], A32[:], ['A32'], [('A0b', 1 - cur)], eng='act')
                cp(Y32[:, :, c * C:(c + 1) * C], ps[:, bY, :].rearrange('p (h x) -> p h x', h=8), [('ps', bY)], ['F0'], eng='dve')
                state_out(c)
            return Y32, v32

        def rwkv_post(Y, Ykey, bonus, bkey, glb, g0b, lmb, mb, n):
            P.phase = 'rwkv_post'
            Yb = HB_[0]; ysq = hb
            cp(Yb[:, :, 0:n], Y[:, :, 0:n], [Ykey], ['H0'], eng='act')
            act(ysq[:, :, 0:n], Y[:, :, 0:n], AF.Square, [Ykey], ['hb'])
            for kc in range(8):
                b1 = bank(); b2 = bank()
                mm(ps[:, b1, 0:n], bdb[:], Yb[:, kc, 0:n], True, True, ['bdb', 'H0'], [('ps', b1)])
                mm(ps[:, b2, 0:n], bdb[:], ysq[:, kc, 0:n], True, True, ['bdb', 'hb'], [('ps', b2)])
                SS, KP = stsel(kc)
                mean = SS[0][:, 0:n]; var = SS[1][:, 0:n]; t_ = SS[2][:, 0:n]
                ts(mean, ps[:, b1, 0:n], 1.0 / 64, None, ALU.mult, None, [('ps', b1)], [KP + '0'])
                tt(var, mean, mean, ALU.mult, [KP + '0'], [KP + '1'])
                stt(var, ps[:, b2, 0:n], 1.0 / 64, var, ALU.mult, ALU.subtract, [('ps', b2), KP + '1'], [KP + '1'])
                ts(var, var, 0.0, GN_EPS, ALU.max, ALU.add, [KP + '1'], [KP + '1'])
                act(var, var, AF.Sqrt, [KP + '1'], [KP + '1'])
                recip(var, var, [KP + '1'], [KP + '1'])
                tt(t_, Y[:, kc, 0:n], mean, ALU.subtract, [Ykey, KP + '0'], [KP + '2'])
                tt(t_, t_, var, ALU.mult, [KP + '2', KP + '1'], [KP + '2'])
                act(t_, t_, AF.Identity, [KP + '2', 'v_gn_g', 'v_gn_b'], [KP + '2'], bias=vec['gn_b'][:, kc:kc + 1], scale=vec['gn_g'][:, kc:kc + 1])
                tt(t_, t_, bonus[:, kc, 0:n], ALU.add, [KP + '2', bkey], [KP + '2'])
                tt(t_, t_, glb[:, kc, 0:n], ALU.mult, [KP + '2', 'H1'], [KP + '2'])
                tt(t_, t_, g0b[:, kc, 0:n], ALU.mult, [KP + '2', 'H3'], [KP + '2'])
                tt(mb[:, kc, 0:n], t_, lmb[:, kc, 0:n], ALU.add, [KP + '2', 'HX2'], ['H2'])

        def resid_ln(w, xin, xkey, ln_idx, n):
            def evac(m, pb):
                stt(h32[:, m, 0:n], ps[:, pb, 0:n], 1.0 / ALPHA, h32[:, m, 0:n], ALU.mult, ALU.add, [('ps', pb), 'h32'], ['h32'])
            proj(w, D, xin, xkey, n, evac)
            layernorm(ln_idx, n, LN_EPS / (ALPHA * ALPHA))

        def xattn_prompt(n):
            P.phase = 'xattn_prompt'
            qb = HB_[0]; ob = HB_[1]; pT = HX[:, 0:8, :]
            def evq(m, pb):
                cp(qb[:, m, 0:n], ps[:, pb, 0:n], [('ps', pb)], ['H0'], eng='act')
            proj('xa_wq', D, hb, 'hb', n, evq)
            for h in range(4):
                for mc in range(2):
                    pb = bank()
                    for dc in range(2):
                        mm(ps[:, pb, 0:n], mkT[:, 2 * h + dc, mc * 128:(mc + 1) * 128], qb[:, 2 * h + dc, 0:n], dc == 0, dc == 1, ['mkT', 'H0'], [('ps', pb)])
                    act(pT[:, 2 * h + mc, 0:n], ps[:, pb, 0:n], AF.Exp, [('ps', pb)], ['HX0'], scale=1.0 / 16.0)
                pd = bank()
                for mc in range(2):
                    mm(ps[:, pd, 0:n], onesb[:], pT[:, 2 * h + mc, 0:n], mc == 0, mc == 1, ['onesb', 'HX0'], [('ps', pd)])
                rd = st1[h % 2][:, 0:n]; rk_ = 'st%d' % (h % 2)
                recip(rd, ps[:, pd, 0:n], [('ps', pd)], [rk_])
                for dc in range(2):
                    po = bank()
                    for mc in range(2):
                        mm(ps[:, po, 0:n], mvb[:, mc, (2 * h + dc) * 128:(2 * h + dc + 1) * 128], pT[:, 2 * h + mc, 0:n], mc == 0, mc == 1, ['mvb', 'HX0'], [('ps', po)])
                    tt(ob[:, 2 * h + dc, 0:n], ps[:, po, 0:n], rd, ALU.mult, [('ps', po), rk_], ['H1'])
            resid_ln('xa_wo', ob, 'H1', 2, n)

        if stage >= 1:
            mem_kv()
        if 'm' in PARTS or PARTS == 'abcdefg':
            P.op('dve', lambda e: e.memset(A32[:], 0.0), writes=['A32'])
            P.op('dve', lambda e: e.memset(A0b[0][:], 0.0), writes=[('A0b', 0)])
        for ti in range(int(os.environ.get('NTI', NTILES)) if stage >= 9 else 1):
            TOG = os.environ.get('TOG', '')
            for r in range(1 if '1' in TOG else NT // 128):
                load_tokens_fm(di['xp'][ti * NT + r * 128: ti * NT + (r + 1) * 128, :], 128, h32, None if 'D' in TOG else hb, r * 128, 'h32')
            if stage >= 2:
                ffn('ffn1_wi', 'ffn1_wo', 0, NT)
            if ti == 0:
                dump('h1', h32[:], 'h32')
            if stage < 3:
                break
            B = mixer_prompt(ti)
            if stage < 4:
                break
            lmb = lru_prompt(ti, B)
            if ti == 0:
                dump('lm', lmb, 'HX2')
            if stage < 5:
                break
            Y32, bonus = rwkv_core(B, lambda c: None, lambda c: None)
            if ti == 0:
                dump('Y', Y32[:], 'F0')
            if stage < 6:
                break
            mb = HB_[2]
            rwkv_post(Y32, 'F0', bonus, 'F2', B['glb'], B['g0b'], lmb, mb, NT)
            resid_ln('w_mix_out', mb, 'H2', 1, NT)
            if ti == 0:
                dump('h2', h32[:], 'h32')
            if stage < 7:
                break
            xattn_prompt(NT)
            if ti == 0:
                dump('h3', h32[:], 'h32')
            if stage < 8:
                break
            ffn('ffn2_wi', 'ffn2_wo', 3, NT)
            for r in range(NT // 128):
                store_fm_tokens(h32, 'h32', r * 128, 128, di['yp'][ti * NT + r * 128: ti * NT + (r + 1) * 128, :])
        if stage < 9:
            P.emit()
            return nc, P
        def prompt_outputs():
            pass
            Sout = FB_[1].rearrange('p a b -> p (a b)')[:, 0:1024].rearrange('p (hp par k) -> p hp par k', hp=8, par=2)
            for g in range(2):
                pb = bank()
                for q in range(4):
                    hp = g * 4 + q
                    tr(ps[0:64, pb, q * 128:(q + 1) * 128], A32[:, hp, :], ident[:, :], ['A32', 'ident'], [('ps', pb)])
                cp(Sout[0:64, g * 4:(g + 1) * 4, :, :], ps[0:64, pb, :].rearrange('p (q par k) -> p q par k', q=4, par=2), [('ps', pb)], ['F1'], eng='act')
            dma('sp', di['prw'].rearrange('(hp par v) k -> v hp par k', hp=8, par=2), Sout[0:64, :, :, :], ['F1'], [], dsem_misc, is_out=True)
            osm2 = FB_[2]
            cp(osm2[:, 0:8, 0:3], osml[:, :, 0:3], ['osml'], ['F2'])
            cp(osm2[:, 0:8, 3:4], osml[:, :, 3:4], ['osml'], ['F2'])
            store_fm_tokens(osm2, 'F2', 0, 3, di['pcv'][:, :])
            store_fm_tokens(osm2, 'F2', 3, 1, di['plru'].rearrange('(o d) -> o d', o=1))
            osh3 = FB_[3]
            cp(osh3[:, 0:8, 0:1], osh[:, 0:8].unsqueeze(2), ['osh'], ['F3'])
            cp(osh3[:, 0:8, 1:2], osh[:, 8:16].unsqueeze(2), ['osh'], ['F3'])
            cp(osh3[:, 0:8, 2:3], osh[:, 16:24].unsqueeze(2), ['osh'], ['F3'])
            cp(osh3[:, 0:2, 3:4], osh[:, 24:26].unsqueeze(2), ['osh'], ['F3'])
            pshv = di['psh'].rearrange('(o d) -> o d', o=1)
            for q in range(3):
                store_fm_tokens(osh3, 'F3', q, 1, pshv[:, q * 1024:(q + 1) * 1024])
            store_fm_tokens(osh3, 'F3', 3, 1, pshv[:, 3072:3328], nch=2)


        if os.environ.get('NTI') != '0':
            prompt_outputs()
        def sample_path():
            P.phase = 'sample_path'
            n = NS
            sm = lambda nm, shp, dt=F32: P.sbuf(nm, shp, dt)
            rc = sm('s_rc', [128, 8, n]); kc_ = sm('s_kc', [128, 8, n]); vc = sm('s_vc', [128, 8, n]); ac = sm('s_ac', [128, 8, n]); lsc = sm('s_lsc', [128, 8, n])
            yc = sm('s_yc', [128, 8, n]); bonc = sm('s_bonc', [128, 8, n]); plc = sm('s_plc', [128, 8, n]); hsc = sm('s_hsc', [128, 8, n])
            prevS = sm('s_prev', [128, 26, n]); praw = sm('s_praw', [128, 26, n]); h0S = sm('s_h0', [128, 8, n]); scvT = sm('s_scvT', [128, 8, 3 * n])
            BD = tok32[1][:].rearrange('p (h x) -> p h x', h=8); ones32 = st1[4][:, 0:128]
            glb = HB_[1]; geb = HB_[2]; g0b = HB_[3]; g1b = HX[:, 0:8, :]; lmb = HX[:, 16:24, :]
            dsS = [P.dma_sem() for _ in range(7)]

            def load_fm(src_rows_ap, nrows, nchunks, dst, dkey, tki, sem):
                tk = tok32[tki]
                dma('sp', tk[0:nrows, 0:nchunks * 128], src_rows_ap, [], [('tok32', tki)], sem)
                for g0 in range(0, nchunks, 4):
                    gn = min(4, nchunks - g0)
                    b = bank()
                    for q in range(gn):
                        tr(ps[:, b, q * 128:q * 128 + nrows], tk[0:nrows, (g0 + q) * 128:(g0 + q + 1) * 128], ident[0:nrows, 0:nrows],
                           [('tok32', tki), 'ident'], [('ps', b)])
                    cp(dst[:, g0:g0 + gn, 0:nrows], ps[:, b, :].rearrange('p (q t) -> p q t', q=4)[:, 0:gn, 0:nrows], [('ps', b)], [dkey], eng='act')

            load_tokens_fm(di['xs'], n, h32, hb, 0, 'h32')
            for q in range(4):
                c0 = q * 8; cn = min(8, 26 - c0)
                tmpd = FB_[0] if q % 2 == 0 else FB_[1]
                load_fm(di['ssh'][:, c0 * 128:(c0 + cn) * 128], n, cn, tmpd, 'F%d' % (q % 2), q % 2, dsS[q % 2])
                cp(prevS[:, c0:c0 + cn, :], tmpd[:, 0:cn, 0:n], ['F%d' % (q % 2)], ['s_prev'])
            load_fm(di['slru'], n, 8, h0S, 's_h0', 0, dsS[0])
            load_fm(di['scv'].rearrange('b j d -> (b j) d'), 3 * n, 8, scvT, 's_scvT', 1, dsS[1])
            dma('sp', di['scv_o'][:, 0:2, :], di['scv'][:, 1:3, :], [], [], P.dma_sem(), is_out=True)

            ffn('ffn1_wi', 'ffn1_wo', 0, n)

            tmp = st1[0]

            def evac(m, pb):
                psn = ps[:, pb, 0:n]
                if m < 26:
                    cp(praw[:, m, :], psn, [('ps', pb)], ['s_praw'], eng='act')
                    ts(tmp[:, 0:n], prevS[:, m, :], mu[:, m:m + 1], None, ALU.mult, None, ['s_prev', 'mu'], ['st0'])
                    if m < 24:
                        dst = [rc, kc_, vc][m // 8][:, m % 8, :]
                        dkey = ['s_rc', 's_kc', 's_vc'][m // 8]
                        stt(dst, psn, omu[:, m:m + 1], tmp[:, 0:n], ALU.mult, ALU.add, [('ps', pb), 'omu', 'st0'], [dkey])
                    else:
                        xs_ = st1[1][:, 0:n]
                        stt(xs_, psn, omu[:, m:m + 1], tmp[:, 0:n], ALU.mult, ALU.add, [('ps', pb), 'omu', 'st0'], ['st1'])
                        lb = st1[2][:, 0:NT].bitcast(BF16)[:, 0:n]
                        if m == 24:
                            act(lb[0:64, :], xs_[0:64, :], AF.Tanh, ['st1'], ['st2'])
                            cp(lb[64:128, :], xs_[64:128, :], ['st1'], ['st2'])
                            for (lo, dstt, dk, bvec) in [(0, lsc, 's_lsc', 'decay_w0'), (64, ac, 's_ac', 'aaa_a0')]:
                                for q in range(8):
                                    p2 = bank()
                                    mm(ps[:, p2, 0:n], w2a2[lo:lo + 64, q * 128:(q + 1) * 128], lb[lo:lo + 64, :], True, True, ['w2a2', 'st2'], [('ps', p2)])
                                    act(dstt[:, q, :], ps[:, p2, 0:n], AF.Sigmoid, [('ps', p2), 'v_' + bvec], [dk], bias=vec[bvec][:, q:q + 1])
                        else:
                            act(lb, xs_, AF.Sigmoid, ['st1'], ['st2'])
                            for q in range(8):
                                p2 = bank()
                                mm(ps[:, p2, 0:n], g2[:, q * 128:(q + 1) * 128], lb, True, True, ['g2', 'st2'], [('ps', p2)])
                                cp(glb[:, q, 0:n], ps[:, p2, 0:n], [('ps', p2)], ['H1'], eng='act')
                elif m < 34:
                    cp(plc[:, m - 26, :], psn, [('ps', pb)], ['s_plc'], eng='act')
                elif m < 42:
                    act(geb[:, m - 34, 0:n], psn, AF.Gelu, [('ps', pb)], ['H2'])
                elif m < 50:
                    act(g0b[:, m - 42, 0:n], psn, AF.Sigmoid, [('ps', pb)], ['H3'])
                else:
                    act(g1b[:, m - 50, 0:n], psn, AF.Sigmoid, [('ps', pb)], ['HX0'])
            proj('w_in', PW, hb, 'hb', n, evac)
            for q in range(4):
                c0 = q * 8; cn = min(8, 26 - c0)
                store_fm_tokens(praw[:, c0:c0 + cn, :], 's_praw', 0, n, di['ssh_o'][:, c0 * 128:(c0 + cn) * 128], nch=cn)
            store_fm_tokens(plc, 's_plc', 0, n, di['scv_o'][:, 2, :])

            sc3 = scvT[:].rearrange('p c (b j) -> p c b j', j=3)
            xc = st1[0][:, 0:n]; gr = st1[1][:, 0:n]; gi = st1[2][:, 0:n]; t3 = st1[3][:, 0:n]; hs = st1[4][:, 0:n]
            xcb = HX[:, 8:16, :]
            for c in range(8):
                act(xc, plc[:, c, :], AF.Identity, ['s_plc', 'cw', 'v_conv_b'], ['st0'], bias=vec['conv_b'][:, c:c + 1], scale=cw[:, 3, c:c + 1])
                for j in range(3):
                    stt(xc, sc3[:, c, :, j], cw[:, j, c:c + 1], xc, ALU.mult, ALU.add, ['s_scvT', 'cw', 'st0'], ['st0'])
                cp(xcb[:, c, 0:n], xc, ['st0'], ['HX1'], eng='act')
                p1 = bank(); p2 = bank()
                mm(ps[:, p1, 0:n], wrbd[:, c, :], xcb[:, c, 0:n], True, True, ['wrbd', 'HX1'], [('ps', p1)])
                mm(ps[:, p2, 0:n], wibd[:, c, :], xcb[:, c, 0:n], True, True, ['wibd', 'HX1'], [('ps', p2)])
                act(gr, ps[:, p1, 0:n], AF.Sigmoid, [('ps', p1), 'v_lru_br'], ['st1'], bias=vec['lru_br'][:, c:c + 1])
                act(gi, ps[:, p2, 0:n], AF.Sigmoid, [('ps', p2), 'v_lru_bi'], ['st2'], bias=vec['lru_bi'][:, c:c + 1])
                act(gr, gr, AF.Exp, ['st1', 'lsp'], ['st1'], scale=lsp[:, c:c + 1])
                act(t3, gr, AF.Square, ['st1'], ['st3'])
                ts(t3, t3, -1.0, 1.0, ALU.mult, ALU.add, ['st3'], ['st3'])
                ts(t3, t3, 0.0, None, ALU.max, None, ['st3'], ['st3'])
                act(t3, t3, AF.Sqrt, ['st3'], ['st3'])
                tt(gi, gi, xc, ALU.mult, ['st2', 'st0'], ['st2'])
                tt(gi, gi, t3, ALU.mult, ['st2', 'st3'], ['st2'])
                tt(hs, gr, h0S[:, c, :], ALU.mult, ['st1', 's_h0'], ['st4'])
                tt(hsc[:, c, :], hs, gi, ALU.add, ['st4', 'st2'], ['s_hsc'])
                tt(t3, hsc[:, c, :], geb[:, c, 0:n], ALU.mult, ['s_hsc', 'H2'], ['st3'])
                tt(lmb[:, c, 0:n], t3, g1b[:, c, 0:n], ALU.mult, ['st3', 'HX0'], ['HX2'])
            store_fm_tokens(hsc, 's_hsc', 0, n, di['slru_o'])

            Sout = FB_[1].rearrange('p a b -> p (a b)')[:, 0:1024].rearrange('p (hp par k) -> p hp par k', hp=8, par=2)
            P.op('dve', lambda e: e.memset(tok32[1][:], 0.0), writes=[('tok32', 1)])
            for g in range(NS // NCH):
                Bp = dict(r32=FB_[0], k32=FB_[1], v32=FB_[2], a32=FB_[3], ls32=FB_[4])
                for (dstF, fk, srcc, sk) in [(FB_[0], 'F0', rc, 's_rc'), (FB_[1], 'F1', kc_, 's_kc'), (FB_[2], 'F2', vc, 's_vc'), (FB_[3], 'F3', ac, 's_ac'), (FB_[4], 'F4', lsc, 's_lsc')]:
                    P.op('dve', lambda e, dstF=dstF: e.memset(dstF[:], 0.0), writes=[fk])
                    cp(dstF[:].rearrange('p k (c t) -> p k c t', t=C)[:, :, :, 0], srcc[:, :, g * NCH:(g + 1) * NCH], [sk], [fk])

                def state_in(c, g=g):
                    b_ = g * NCH + c
                    src = di['srw'][b_].rearrange('(hp par v) k -> par v hp k', hp=8, par=2)
                    for par in range(2):
                        dma('sp', BD[par * 64:(par + 1) * 64, :, par * 64:(par + 1) * 64], src[par], [], [('tok32', 1)], dsS[3])
                    pb = bank()
                    for hp in range(8):
                        mm(ps[:, pb, hp * 64:(hp + 1) * 64], BD[:, hp, :], mask[:, 192:256], True, True, [('tok32', 1), 'mask'], [('ps', pb)])
                    v3 = ps[:, pb, :].rearrange('p (h x) -> p h x', h=8)
                    cp(A32[:], v3, [('ps', pb)], ['A32'], eng='dve')
                    cp(A0b[c % 2][:], v3, [('ps', pb)], [('A0b', c % 2)], eng='act')

                def state_out(c, g=g):
                    b_ = g * NCH + c
                    for gg in range(2):
                        pb = bank()
                        for q in range(4):
                            hp = gg * 4 + q
                            tr(ps[0:64, pb, q * 128:(q + 1) * 128], A32[:, hp, :], ident[:, :], ['A32', 'ident'], [('ps', pb)])
                        cp(Sout[0:64, gg * 4:(gg + 1) * 4, :, :], ps[0:64, pb, :].rearrange('p (q par k) -> p q par k', q=4, par=2), [('ps', pb)], ['F1'], eng='act')
                    dma('sp', di['srw_o'][b_].rearrange('(hp par v) k -> v hp par k', hp=8, par=2), Sout[0:64, :, :, :], ['F1'], [], dsS[4], is_out=True)
                Y32, bon = rwkv_core(Bp, state_in, state_out, skip_inverse=True)
                cp(yc[:, :, g * NCH:(g + 1) * NCH], Y32[:].rearrange('p k (c t) -> p k c t', t=C)[:, :, :, 0], ['F0'], ['s_yc'])
                cp(bonc[:, :, g * NCH:(g + 1) * NCH], bon[:].rearrange('p k (c t) -> p k c t', t=C)[:, :, :, 0], ['F2'], ['s_bonc'])
            mb = HB_[2]
            rwkv_post(yc, 's_yc', bonc, 's_bonc', glb, g0b, lmb, mb, n)
            resid_ln('w_mix_out', mb, 'H2', 1, n)

            qc = FB_[0]; qT = FB_[1].rearrange('p a b -> p (a b)')[:, 0:1024]; sel = FB_[2].rearrange('p a b -> p (a b)')[:, 0:NS * 128].rearrange('p (b m) -> p b m', b=NS)
            Kb = FB_[3].rearrange('p a b -> p (a b)').rearrange('p (mc f) -> p mc f', mc=2)
            prod = FB_[4].rearrange('p a b -> p (a b)')[:, 0:1024]
            Vb = HB_[0][:].rearrange('p a b -> p (a b)').rearrange('p (mc f) -> p mc f', mc=2)
            ob = HB_[1]
            sc = st1[0][:, 0:128]; ex = st1[1][:, 0:128]; den = st1[2][:, 0:64]; pbf = st1[3][:, 0:NT].bitcast(BF16)[:, 0:128]

            def evq(m, pb):
                cp(qc[:, m, 0:n], ps[:, pb, 0:n], [('ps', pb)], ['F0'], eng='act')
            proj('xa_wq', D, hb, 'hb', n, evq)
            for g0 in range(0, 8, 4):
                pb = bank()
                for q in range(4):
                    tr(ps[0:n, pb, q * 128:(q + 1) * 128], qc[:, g0 + q, 0:n], ident[:, :], ['F0', 'ident'], [('ps', pb)])
                cp(qT[0:n, g0 * 128:(g0 + 4) * 128], ps[0:n, pb, :], [('ps', pb)], ['F1'], eng='act')
            cp(sel[0:n, :, :], ident[0:n, 0:n].unsqueeze(2).to_broadcast([n, n, 128]), ['ident'], ['F2'])
            for b_ in range(NS):
                dma('sp', Kb, di['cmk'][b_].rearrange('(mc p) f -> p mc f', p=128), [], ['F3'], dsS[5])
                pq = [bank(), bank()]
                for hf in range(2):
                    mm(ps[:, pq[hf], :], sel[0:n, b_, :], qT[0:n, hf * 512:(hf + 1) * 512], True, True, ['F2', 'F1'], [('ps', pq[hf])])
                for mc in range(2):
                    for hf in range(2):
                        tt(prod[:, hf * 512:(hf + 1) * 512], Kb[:, mc, hf * 512:(hf + 1) * 512], ps[:, pq[hf], :], ALU.mult, ['F3', ('ps', pq[hf])], ['F4'])
                    P.op('dve', lambda e, b_=b_, mc=mc: e.tensor_reduce(out=sc[:, (b_ * 2 + mc) * 4:(b_ * 2 + mc) * 4 + 4], in_=prod.rearrange('p (h d) -> p h d', h=4), axis=AX.X, op=ALU.add),
                         reads=['F4'], writes=['st0'])
            act(ex, sc, AF.Exp, ['st0'], ['st1'], scale=1.0 / 16.0)
            dma('sp', ones32, di['c_all'][:, 128:256], [], ['st4'], dsS[2])
            pdn = bank()
            mm(ps[:, pdn, 0:128], ones32, ex, True, True, ['st4', 'st1'], [('ps', pdn)])
            d4 = ps[:, pdn, 0:128].rearrange('p (b mc h) -> p b mc h', mc=2, h=4)
            den3 = den.rearrange('p (b h) -> p b h', h=4)
            cp(den3, d4[:, :, 0, :], [('ps', pdn)], ['st2'])
            tt(den3, den3, d4[:, :, 1, :], ALU.add, ['st2', ('ps', pdn)], ['st2'])
            recip(den, den, ['st2'], ['st2'])
            tt(pbf.rearrange('p (b mc h) -> p b mc h', mc=2, h=4), ex.rearrange('p (b mc h) -> p b mc h', mc=2, h=4),
               den3.unsqueeze(2).to_broadcast([128, NS, 2, 4]), ALU.mult, ['st1', 'st2'], ['st3'])
            po = bank()
            for b_ in range(NS):
                dma('pool', Vb, di['cmv'][b_].rearrange('(mc p) f -> p mc f', p=128), [], ['H0'], dsS[6])
                for c in range(8):
                    for mc in range(2):
                        col = (b_ * 2 + mc) * 4 + c // 2
                        mm(ps[:, po, c * NS + b_:c * NS + b_ + 1], Vb[:, mc, c * 128:(c + 1) * 128], pbf[:, col:col + 1], mc == 0, mc == 1, ['H0', 'st3'], [('ps', po)])
            cp(ob[:, :, 0:n], ps[:, po, 0:8 * NS].rearrange('p (c b) -> p c b', c=8), [('ps', po)], ['H1'], eng='act')
            resid_ln('xa_wo', ob, 'H1', 2, n)
            ffn('ffn2_wi', 'ffn2_wo', 3, n)
            store_fm_tokens(h32, 'h32', 0, n, di['ys'])

        if do_sample:
            sample_path()
        P.emit()
    return nc, P


_CACHE = {}


def _consts():
    a = np.arange(128) % 64
    b = np.arange(64)
    su = (a[:, None] < b[None, :]).astype(np.float32)
    ui = (a[:, None] <= b[None, :]).astype(np.float32)
    sl = (a[:, None] > b[None, :]).astype(np.float32)
    ey = (a[:, None] == b[None, :]).astype(np.float32)
    bd = np.zeros((128, 128), np.float32)
    bd[:64, :64] = 1.0
    bd[64:, 64:] = 1.0
    rs = np.ones((128, NT), np.float32)
    rs[:, ::C] = 0.0
    return {'c_all': np.ascontiguousarray(np.concatenate([np.eye(128, dtype=np.float32), np.ones((128, 128), np.float32), bd, su, ui, sl, ey, rs], axis=1))}


def make_in_maps(inputs):
    f = lambda a: np.ascontiguousarray(np.asarray(a, dtype=np.float32))
    shared = {}
    for nm in ['ffn1_wi', 'ffn1_wo', 'ffn2_wi', 'ffn2_wo', 'w_in', 'decay_w2', 'aaa_a2', 'gate_g2',
               'lru_wr', 'lru_wi', 'w_mix_out', 'xa_wq', 'xa_wk', 'xa_wv', 'xa_wo']:
        shared[nm] = np.ascontiguousarray(f(inputs[nm])[0])
    shared['prm'] = np.ascontiguousarray(np.concatenate(
        [f(inputs[nm])[0].reshape(-1, 128) for nm in ['ln_g', 'ln_b', 'shift_mu', 'conv_w'] + VEC_NAMES], axis=0))
    shared.update(_consts())
    maps = []
    for c in range(8):
        m = dict(shared)
        sl = slice(c * NS, (c + 1) * NS)
        m['xp'] = f(inputs['x_prompt'][c])
        m['mem'] = f(inputs['mem_prompt'][c])
        m['xs'] = f(inputs['x_sample'][sl, 0])
        m['cmk'] = f(inputs['cache_mem_k'][0, sl]).reshape(NS, NMEM, D)
        m['cmv'] = f(inputs['cache_mem_v'][0, sl]).reshape(NS, NMEM, D)
        m['srw'] = f(inputs['state_rwkv'][0, sl]).reshape(NS, D, 64)
        m['ssh'] = f(inputs['state_rwkv_shift'][0, sl])
        m['slru'] = f(inputs['state_lru'][0, sl])
        m['scv'] = f(inputs['state_conv'][0, sl])
        maps.append(m)
    return maps


def kernel(**inputs):
    if 'nc' not in _CACHE:
        _CACHE['nc'] = build()[0]
    nc = _CACHE['nc']
    maps = make_in_maps(inputs)
    res = run_bass_kernel_spmd(nc, maps, core_ids=list(range(8)))
    R = res.results
    cat = lambda k: np.stack([np.asarray(r[k], dtype=np.float32) for r in R])
    catc = lambda k: np.concatenate([np.asarray(r[k], dtype=np.float32) for r in R], axis=0)
    yp = cat('yp')
    ys = catc('ys').reshape(8 * NS, 1, D)
    pmk = cat('pmk').reshape(1, 8, NMEM, 4, 256)
    pmv = cat('pmv').reshape(1, 8, NMEM, 4, 256)
    prw = cat('prw').reshape(1, 8, 16, 64, 64)
    psh = cat('psh').reshape(1, 8, RP)
    plru = cat('plru').reshape(1, 8, D)
    pcv = cat('pcv').reshape(1, 8, 3, D)
    srw = catc('srw_o').reshape(1, 8 * NS, 16, 64, 64)
    ssh = catc('ssh_o').reshape(1, 8 * NS, RP)
    slru = catc('slru_o').reshape(1, 8 * NS, D)
    scv = catc('scv_o').reshape(1, 8 * NS, 3, D)
    return (yp, ys, pmk, pmv, prw, psh, plru, pcv, srw, ssh, slru, scv)
```

```python
import math
import os
import numpy as np
from contextlib import ExitStack
import concourse.bass as bass
import concourse.mybir as mybir
from concourse.bass_utils import run_bass_kernel_spmd

F32 = mybir.dt.float32
BF16 = mybir.dt.bfloat16
AF = mybir.ActivationFunctionType
ALU = mybir.AluOpType
AX = mybir.AxisListType

ENGS = ['pe', 'dve', 'act', 'pool', 'sp']

D = 1024
T = 2048
NT = 256
NTILES = T // NT
C = 64
NCH = NT // C
DFF = 2816
NJ = DFF // 128
RP = 3328
PW = 7424
NMEM = 256
NS = 16
ALPHA = 2.0 ** 0.25
LN_EPS = 1e-5
GN_EPS = 64e-5
C0 = math.exp(-0.5)


class DmaSem:
    def __init__(self, sem):
        self.sem = sem
        self.count = 0


class Prog:
    def __init__(self, nc, stack):
        self.nc = nc
        self.stack = stack
        self.ops = {e: [] for e in ENGS}
        self.last_w = {}
        self.readers = {}
        self.seen = {e: {} for e in ENGS}
        self.dsems = []
        self.out_tokens = []

    def dma_sem(self):
        s = DmaSem(self.stack.enter_context(self.nc.semaphore('dsem%d' % len(self.dsems))))
        self.dsems.append(s)
        return s

    def sbuf(self, name, shape, dt):
        return self.stack.enter_context(self.nc.sbuf_tensor(name, list(shape), dt))

    def psum(self, name, shape, dt):
        return self.stack.enter_context(self.nc.psum_tensor(name, list(shape), dt))

    def barrier(self, keys, engines=ENGS):
        if 'B' in os.environ.get('TOG', ''):
            return
        for e in engines:
            self.op(e, None, reads=keys, track=False)

    def op(self, eng, fn, reads=(), writes=(), dsem=None, is_out=False, track=True):
        isps = lambda k: isinstance(k, tuple) and k[0] == 'ps'
        writes = list(writes) + [k for k in reads if isps(k)]
        reads = [k for k in reads if not isps(k)]
        deps = []
        for k in reads:
            t = self.last_w.get(k)
            if t is not None:
                deps.append(t)
        for k in writes:
            t = self.last_w.get(k)
            if t is not None:
                deps.append(t)
            deps.extend(self.readers.get(k, {}).values())
        need = {}
        for t in deps:
            if t[0] == 'eng':
                if t[1] == eng and dsem is None and (eng == 'pe' or os.environ.get('NOSELF')):
                    continue
                key = ('eng', t[1])
            else:
                key = ('dma', id(t[1]))
            if need.get(key, (None, -1))[1] < t[2]:
                need[key] = (t[1], t[2])
        waits = []
        for key, (src, v) in need.items():
            if self.seen[eng].get(key, -1) >= v:
                continue
            self.seen[eng][key] = v
            waits.append((key[0], src, v))
        idx = len(self.ops[eng])
        self.ops[eng].append(dict(fn=fn, waits=waits, dsem=dsem, target=False, phase=getattr(self, 'phase', '')))
        if dsem is not None:
            dsem.count += 16
            tok = ('dma', dsem, dsem.count)
        else:
            tok = ('eng', eng, idx)
        for k in writes:
            self.last_w[k] = tok
            self.readers[k] = {}
        for k in (reads if track else ()):
            r = self.readers.setdefault(k, {})
            rk = (tok[0], tok[1] if tok[0] == 'eng' else id(tok[1]))
            if rk not in r or r[rk][2] < tok[2]:
                r[rk] = tok
        if is_out:
            self.out_tokens.append(tok)
        return tok

    def emit(self):
        nc = self.nc
        fin = {}
        for t in self.out_tokens:
            fin[id(t[1])] = (t[1], max(fin.get(id(t[1]), (None, 0))[1], t[2]))
        self.ops['sp'].append(dict(fn=None, waits=[('dma', s, v) for s, v in fin.values()], dsem=None, target=False))
        for e in ENGS:
            for o in self.ops[e]:
                for kind, src, v in o['waits']:
                    if kind == 'eng':
                        self.ops[src][v]['target'] = True
        semval = {}
        for e in ENGS:
            c = 0
            vals = []
            for o in self.ops[e]:
                if o['target']:
                    c += 1
                vals.append(c)
            semval[e] = vals
        esem = {e: self.stack.enter_context(nc.semaphore('esem_' + e)) for e in ENGS}
        handles = {'pe': 'tensor', 'dve': 'vector', 'act': 'scalar', 'pool': 'gpsimd', 'sp': 'sync'}
        with nc.Block() as block:
            def make(e):
                def body(eng):
                    for o in self.ops[e]:
                        for kind, src, v in o['waits']:
                            if kind == 'eng':
                                eng.wait_ge(esem[src], semval[src][v])
                            else:
                                eng.wait_ge(src.sem, v)
                        if o['fn'] is None:
                            continue
                        inst = o['fn'](eng)
                        if os.environ.get('ANNOT') and o.get('phase'):
                            inst.annotate(o['phase'])
                        if o['dsem'] is not None:
                            inst.then_inc(o['dsem'].sem, 16)
                        elif o['target']:
                            inst.then_inc(esem[e], 1)
                return body
            for e in ENGS:
                getattr(block, handles[e])(make(e))
        self.stats = {e: len(self.ops[e]) for e in ENGS}


VEC_NAMES = ['decay_w0', 'aaa_a0', 'k_k', 'k_a', 'r_k', 'gn_g', 'gn_b', 'conv_b', 'lru_br', 'lru_bi', 'lru_lambda']
W_NAMES = ['ffn1_wi', 'ffn1_wo', 'ffn2_wi', 'ffn2_wo', 'w_in', 'w_mix_out', 'xa_wq', 'xa_wk', 'xa_wv', 'xa_wo']


def build(dbg=None, do_sample=True, stage=99):
    nc = bass.Bass('TRN2', target_bir_lowering=False)
    di = {}

    DECL = os.environ.get('DECL')

    def din(name, shape):
        if DECL and name not in DECL.split(','):
            return None
        di[name] = nc.dram_tensor(name, list(shape), F32, kind='ExternalInput').ap()
        return di[name]

    def dout(name, shape):
        if DECL and name not in DECL.split(','):
            return None
        di[name] = nc.dram_tensor(name, list(shape), F32, kind='ExternalOutput').ap()
        return di[name]

    din('xp', [T, D]); din('mem', [NMEM, D])
    din('xs', [NS, D]); din('cmk', [NS, NMEM, D]); din('cmv', [NS, NMEM, D])
    din('srw', [NS, D, 64]); din('ssh', [NS, RP]); din('slru', [NS, D]); din('scv', [NS, 3, D])
    din('prm', [210, 128])
    din('ffn1_wi', [D, 2 * DFF]); din('ffn1_wo', [DFF, D]); din('ffn2_wi', [D, 2 * DFF]); din('ffn2_wo', [DFF, D])
    din('w_in', [D, PW])
    din('decay_w2', [64, D]); din('aaa_a2', [64, D]); din('gate_g2', [128, D])
    din('lru_wr', [16, 64, 64]); din('lru_wi', [16, 64, 64])
    for w in ['w_mix_out', 'xa_wq', 'xa_wk', 'xa_wv', 'xa_wo']:
        din(w, [D, D])
    din('c_all', [128, 640 + NT])
    dout('yp', [T, D]); dout('ys', [NS, D]); dout('pmk', [NMEM, D]); dout('pmv', [NMEM, D])
    dout('prw', [D, 64]); dout('psh', [RP]); dout('plru', [D]); dout('pcv', [3, D])
    dout('srw_o', [NS, D, 64]); dout('ssh_o', [NS, RP]); dout('slru_o', [NS, D]); dout('scv_o', [NS, 3, D])
    dbg = dbg or {}
    for k, shp in dbg.items():
        dout('dbg_' + k, shp)

    with ExitStack() as st:
        P = Prog(nc, st)
        n = NT
        ident = P.sbuf('ident', [128, 128], F32)
        identb = P.sbuf('identb', [128, 128], BF16)
        onesb = P.sbuf('onesb', [128, 128], BF16)
        bdb = P.sbuf('bdb', [128, 128], BF16)
        mask = P.sbuf('mask', [128, 256], F32)
        reset = P.sbuf('reset', [128, NT], F32)
        lng = P.sbuf('lng', [128, 4, 8], F32); lnb = P.sbuf('lnb', [128, 4, 8], F32)
        mu = P.sbuf('mu', [128, 26], F32); omu = P.sbuf('omu', [128, 26], F32)
        vec = {v: P.sbuf('v_' + v, [128, 8], F32) for v in VEC_NAMES}
        oka = P.sbuf('oka', [128, 8], F32)
        lsp = P.sbuf('lsp', [128, 8], F32)
        cw = P.sbuf('cw', [128, 4, 8], F32)
        w2a2 = P.sbuf('w2a2', [128, D], BF16)
        g2 = P.sbuf('g2', [128, D], BF16)
        wrbd = P.sbuf('wrbd', [128, 8, 128], BF16); wibd = P.sbuf('wibd', [128, 8, 128], BF16)
        WCAP = 4096
        NWB = 3
        wbuf = [P.sbuf('wbuf%d' % i, [128, WCAP], BF16) for i in range(NWB)]
        wsem = [P.dma_sem() for i in range(NWB)]
        h32 = P.sbuf('h32', [128, 8, n], F32)
        hb = P.sbuf('hb', [128, 8, n], BF16)
        FB_ = [P.sbuf('F%d' % i, [128, 8, n], F32) for i in range(5)]
        HX = P.sbuf('HX', [128, 24, n], BF16)
        HB_ = [P.sbuf('H%d' % i, [128, 8, n], BF16) for i in range(4)]
        memT = HB_[3]
        pl = P.sbuf('pl', [128, 8, n + 3], F32)
        st1 = [P.sbuf('st%d' % i, [128, n], F32) for i in range(5)]
        st2 = [P.sbuf('su%d' % i, [128, n], F32) for i in range(5)]
        stsel = lambda i: ((st1, 'st') if i % 2 == 0 else (st2, 'su'))
        carry_sh = P.sbuf('carry_sh', [128, 26], F32)
        carry_h = P.sbuf('carry_h', [128, 8], F32)
        A32 = P.sbuf('A32', [128, 8, 64], F32)
        A0b = [P.sbuf('A0b%d' % i, [128, 8, 64], BF16) for i in range(2)]
        RHSb = P.sbuf('RHSb', [128, 8, 64], BF16)
        Ub = P.sbuf('Ub', [128, 8, 64], BF16)
        WCs = P.sbuf('WCs', [128, 8, NCH], F32)
        X32 = [P.sbuf('X32_%d' % i, [128, 8, 64], F32) for i in range(2)]
        Xb = [P.sbuf('Xb_%d' % i, [128, 8, 64], BF16) for i in range(2)]
        PP = [[P.sbuf('PP_%d_%d' % (i, j), [128, 8, 128], BF16) for j in range(2)] for i in range(2)]
        LP = P.sbuf('LP', [128, 8, NCH, 128], BF16)
        PN = P.sbuf('PN', [128, 8, NCH, 128], BF16)
        XT = P.sbuf('XT', [128, 8, NCH, 64], BF16)
        mkT = P.sbuf('mkT', [128, 8, NMEM], BF16)
        mvb = P.sbuf('mvb', [128, 2, D], BF16)
        tok32 = [P.sbuf('tok32_%d' % i, [128, D], F32) for i in range(2)]
        osml = P.sbuf('osml', [128, 8, 8], F32)
        osh = P.sbuf('osh', [128, 26], F32)
        sgb = [P.sbuf('sgb%d' % i, [128, NT], F32) for i in range(2)]
        ps = P.psum('ps', [128, 8, 512], F32)
        dsem_c = [P.dma_sem() for i in range(4)]
        dsem_in = [P.dma_sem() for i in range(2)]
        dsem_out = [P.dma_sem() for i in range(2)]
        dsem_misc = P.dma_sem()

        bank_ctr = [0]
        tokctr = [0]

        def bank():
            b = bank_ctr[0] % 8
            bank_ctr[0] += 1
            return b

        def mm(out, lhsT, rhs, start, stop, reads, writes):
            P.op('pe', lambda e: e.matmul(out, lhsT=lhsT, rhs=rhs, start=start, stop=stop), reads=reads, writes=writes)

        def tr(out, in_, idn, reads, writes):
            P.op('pe', lambda e: e.transpose(out, in_, idn), reads=reads, writes=writes)

        def act(out, in_, func, reads, writes, bias=None, scale=None):
            kw = {}
            if bias is not None:
                kw['bias'] = bias
            if scale is not None:
                kw['scale'] = scale
            P.op('act', lambda e: e.activation(out=out, in_=in_, func=func, **kw), reads=reads, writes=writes)

        def tt(out, in0, in1, op, reads, writes, eng='dve'):
            P.op(eng, lambda e: e.tensor_tensor(out=out, in0=in0, in1=in1, op=op), reads=reads, writes=writes)

        def ts(out, in0, s1, s2, op0, op1, reads, writes, eng='dve'):
            if s2 is None:
                P.op(eng, lambda e: e.tensor_scalar(out=out, in0=in0, scalar1=s1, scalar2=None, op0=op0), reads=reads, writes=writes)
            else:
                P.op(eng, lambda e: e.tensor_scalar(out=out, in0=in0, scalar1=s1, scalar2=s2, op0=op0, op1=op1), reads=reads, writes=writes)

        def stt(out, in0, scalar, in1, op0, op1, reads, writes):
            P.op('dve', lambda e: e.scalar_tensor_tensor(out=out, in0=in0, scalar=scalar, in1=in1, op0=op0, op1=op1), reads=reads, writes=writes)

        def cp(out, in_, reads, writes, eng='dve'):
            if eng == 'act':
                act(out, in_, AF.Copy, reads, writes)
            else:
                P.op(eng, lambda e: e.tensor_copy(out=out, in_=in_), reads=reads, writes=writes)

        def recip(out, in_, reads, writes):
            P.op('dve', lambda e: e.reciprocal(out=out, in_=in_), reads=reads, writes=writes)

        def dma(eng, out, in_, reads, writes, dsem, is_out=False, **kw):
            P.op(eng, lambda e: e.dma_start(out=out, in_=in_, **kw), reads=reads, writes=writes, dsem=dsem, is_out=is_out)

        def interleave2(ga, gb):
            a_ok = b_ok = True
            while a_ok or b_ok:
                if a_ok:
                    try:
                        next(ga)
                    except StopIteration:
                        a_ok = False
                if b_ok:
                    try:
                        next(gb)
                    except StopIteration:
                        b_ok = False

        def dump(name, src_ap, key):
            if name in dbg:
                dma('sp', di['dbg_' + name], src_ap, [key], [], P.dma_sem(), is_out=True)

        PARTS = os.environ.get('PARTS', 'abcdefg')
        dma('sp', ident[:], di['c_all'][:, 0:128], [], ['ident'], dsem_c[0])
        dma('sp', mask[:], di['c_all'][:, 384:640], [], ['mask'], dsem_c[0])
        dma('sp', reset[:], di['c_all'][:, 640:640 + NT], [], ['reset'], dsem_c[0])
        P.barrier(['ident', 'mask', 'reset'])
        if 'b' in PARTS:
            dma('pool', identb[:], di['c_all'][:, 0:128], [], ['identb'], dsem_c[1])
            dma('pool', onesb[:], di['c_all'][:, 128:256], [], ['onesb'], dsem_c[1])
            dma('pool', bdb[:], di['c_all'][:, 256:384], [], ['bdb'], dsem_c[1])
            dma('pool', w2a2[0:64, :], di['decay_w2'], [], ['w2a2'], dsem_c[1])
            dma('pool', w2a2[64:128, :], di['aaa_a2'], [], ['w2a2'], dsem_c[1])
            dma('pool', g2[:], di['gate_g2'], [], ['g2'], dsem_c[1])
        if 'm' in PARTS or PARTS == 'abcdefg':
            P.op('dve', lambda e: e.memset(wrbd[:], 0.0), writes=['wrbd'])
            P.op('dve', lambda e: e.memset(wibd[:], 0.0), writes=['wibd'])
        for (wt, nm, key) in ([(wrbd, 'lru_wr', 'wrbd'), (wibd, 'lru_wi', 'wibd')] if 'c' in PARTS else []):
            src = di[nm].rearrange('(c two) i o -> two i c o', two=2)
            for par in range(2):
                dma('pool', wt[par * 64:(par + 1) * 64, :, par * 64:(par + 1) * 64], src[par], [], [key], dsem_c[1])
        P.barrier(['identb', 'onesb', 'bdb', 'w2a2', 'g2', 'wrbd', 'wibd'])
        prm = [tok32[0], tok32[1]]
        rows = []
        rows.append((lng[:].rearrange('p l c -> p (l c)'), di['prm'][0:32, :], 'lng'))
        rows.append((lnb[:].rearrange('p l c -> p (l c)'), di['prm'][32:64, :], 'lnb'))
        rows.append((mu[:], di['prm'][64:90, :], 'mu'))
        rows.append((cw[:].rearrange('p l c -> p (l c)'), di['prm'][90:122, :], 'cw'))
        for vi_, v in enumerate(VEC_NAMES):
            rows.append((vec[v][:], di['prm'][122 + 8 * vi_:130 + 8 * vi_, :], 'v_' + v))
        groups = [[]]
        cnt = 0
        for r_ in rows:
            k_ = r_[1].shape[0]
            if cnt + k_ > 128:
                groups.append([]); cnt = 0
            groups[-1].append((cnt, k_) + r_)
            cnt += k_
        for gi_, grp in enumerate(groups if 'd' in PARTS else []):
            tk = prm[gi_ % 2]
            tot = 0
            for (o_, k_, dst, src, key) in grp:
                dma('sp', tk[o_:o_ + k_, 0:128], src, [], [('tok32', gi_ % 2)], dsem_c[2 + gi_ % 2])
                tot = o_ + k_
            pb = bank()
            tr(ps[:, pb, 0:tot], tk[0:tot, 0:128], ident[0:tot, 0:tot], [('tok32', gi_ % 2), 'ident'], [('ps', pb)])
            for (o_, k_, dst, src, key) in grp:
                cp(dst, ps[:, pb, o_:o_ + k_], [('ps', pb)], [key])
        if 'e' in PARTS:
            ts(omu[:], mu[:], -1.0, 1.0, ALU.mult, ALU.add, ['mu'], ['omu'])
            ts(oka[:], vec['k_a'][:], -1.0, 1.0, ALU.mult, ALU.add, ['v_k_a'], ['oka'])
            act(lsp[:], vec['lru_lambda'][:], AF.Exp, ['v_lru_lambda'], ['lsp'], scale=-1.0)
            act(lsp[:], lsp[:], AF.Ln, ['lsp'], ['lsp'], bias=1.0)
            ts(lsp[:], lsp[:], -8.0, None, ALU.mult, None, ['lsp'], ['lsp'])

        def wblocks():
            def ffn_blocks(wi, wo):
                for g in range(11):
                    def f(buf, g=g, wi=wi):
                        v = buf[:, 0:4096].rearrange('p (k c) -> p k c', k=8)
                        return [(v[:, :, 0:256], di[wi][:, g * 256:(g + 1) * 256].rearrange('(k p) c -> p k c', p=128)),
                                (v[:, :, 256:512], di[wi][:, DFF + g * 256:DFF + (g + 1) * 256].rearrange('(k p) c -> p k c', p=128))]
                    yield ((wi, g), f)
                for mp in range(8):
                    def f(buf, mp=mp, wo=wo):
                        v = buf[:, 0:NJ * 128].rearrange('p (j c) -> p j c', j=NJ)
                        return [(v, di[wo][:, mp * 128:(mp + 1) * 128].rearrange('(j p) c -> p j c', p=128))]
                    yield ((wo, mp), f)

            def sq_blocks(w, ncols):
                nb = (ncols + 511) // 512
                for b in range(nb):
                    c0 = b * 512
                    cn = min(512, ncols - c0)
                    def f(buf, c0=c0, cn=cn, w=w):
                        v = buf[:, 0:8 * cn].rearrange('p (k c) -> p k c', k=8)
                        return [(v, di[w][:, c0:c0 + cn].rearrange('(k p) c -> p k c', p=128))]
                    yield ((w, b), f)
            yield from sq_blocks('xa_wk', D)
            yield from sq_blocks('xa_wv', D)
            def one_pass():
                yield from ffn_blocks('ffn1_wi', 'ffn1_wo')
                yield from sq_blocks('w_in', PW)
                yield from sq_blocks('w_mix_out', D)
                yield from sq_blocks('xa_wq', D)
                yield from sq_blocks('xa_wo', D)
                yield from ffn_blocks('ffn2_wi', 'ffn2_wo')
            for it in range(int(os.environ.get('NTI', NTILES)) + (1 if do_sample else 0)):
                for blk, (tag, f) in enumerate(one_pass()):
                    yield (tag, f, it, blk)

        wgen = wblocks()
        wstate = dict(issued=0, consumed=0, pending=[])

        NBLK = 59
        wsc = nc.dram_tensor('wsc', [NBLK, 128, WCAP], BF16, kind='Internal').ap()
        wbsem = [P.dma_sem() for i in range(NWB)]

        def w_used(tag):
            if tag[0].endswith('_wi'):
                return 4096
            if tag[0].endswith('_wo') and tag[0].startswith('ffn'):
                return NJ * 128
            ncols = PW if tag[0] == 'w_in' else D
            return 8 * min(512, ncols - tag[1] * 512)

        def w_issue():
            try:
                item = next(wgen)
            except StopIteration:
                return False
            i = wstate['issued'] % NWB
            if len(item) == 2:
                tag, f = item
                for (dst, src) in f(wbuf[i]):
                    dma('pool', dst, src, [], [('wbuf', i)], wsem[i])
            else:
                tag, f, it, blk = item
                used = w_used(tag)
                if it == 0:
                    for (dst, src) in f(wbuf[i]):
                        dma('pool', dst, src, [], [('wbuf', i)], wsem[i])
                    dma('sp', wsc[blk, :, 0:used], wbuf[i][:, 0:used], [('wbuf', i)], [('wsc', blk)], wbsem[i])
                else:
                    dma('pool', wbuf[i][:, 0:used], wsc[blk, :, 0:used], [('wsc', blk)], [('wbuf', i)], wsem[i])
            wstate['pending'].append((tag, i))
            wstate['issued'] += 1
            return True

        def w_next(tag):
            while wstate['issued'] - wstate['consumed'] < NWB:
                if not w_issue():
                    break
            t, i = wstate['pending'].pop(0)
            assert t == tag, (t, tag)
            wstate['consumed'] += 1
            return wbuf[i], ('wbuf', i)

        def sqview(buf, cn):
            return buf[:, 0:8 * cn].rearrange('p (k c) -> p k c', k=8)

        def load_tokens_fm(src_rows_ap, nrows, dst32, dstb, col0, dkey):
            tokctr[0] += 1
            i = tokctr[0] % 2 if 'A' in os.environ.get('TOG', 'A') else 0
            tk = tok32[i]
            dma('sp', tk[0:nrows, :], src_rows_ap, [], [('tok32', i)], dsem_in[i])
            for half in range(2):
                b = bank()
                for q in range(4):
                    kc = half * 4 + q
                    P.op('pe', lambda e, b=b, q=q, kc=kc: e.transpose(ps[:, b, q * 128:q * 128 + nrows], tk[0:nrows, kc * 128:(kc + 1) * 128], ident[0:nrows, 0:nrows]),
                         reads=[('tok32', i), 'ident'], writes=[('ps', b)], track=('W' not in os.environ.get('TOG', '')))
                src = ps[:, b, :].rearrange('p (q t) -> p q t', q=4)[:, :, 0:nrows]
                if dst32 is not None:
                    cp(dst32[:, half * 4:half * 4 + 4, col0:col0 + nrows], src, [('ps', b), ('tok32', i)], [dkey], eng='act')
                if dstb is not None:
                    cp(dstb[:, half * 4:half * 4 + 4, col0:col0 + nrows], src, [('ps', b)], ['hb' if dkey == 'h32' else 'H3'], eng='dve')

        def store_fm_tokens(src32, skey, col0, nrows, dst_rows_ap, nch=8, feat0=0):
            i = bank_ctr[0] % 2
            tk = tok32[i]
            for g0 in range(0, nch, 4):
                b = bank()
                gn = min(4, nch - g0)
                for q in range(gn):
                    tr(ps[0:nrows, b, q * 128:(q + 1) * 128], src32[:, g0 + q, col0:col0 + nrows], ident[:, :],
                       [skey, 'ident'], [('ps', b)])
                cp(tk[0:nrows, g0 * 128:(g0 + gn) * 128], ps[0:nrows, b, 0:gn * 128], [('ps', b)], [('tok32', i)], eng='act')
            dma('sp', dst_rows_ap, tk[0:nrows, 0:nch * 128], [('tok32', i)], [], dsem_out[i], is_out=True)

        def layernorm(idx, n, eps):
            P.phase = 'layernorm'
            zsq = HB_[0]
            cp(hb[:, :, 0:n], h32[:, :, 0:n], ['h32'], ['hb'], eng='act')
            act(zsq[:, :, 0:n], h32[:, :, 0:n], AF.Square, ['h32'], ['H0'])
            b1 = bank(); b2 = bank()
            for kc in range(8):
                mm(ps[:, b1, 0:n], onesb[:], hb[:, kc, 0:n], kc == 0, kc == 7, ['onesb', 'hb'], [('ps', b1)])
            for kc in range(8):
                mm(ps[:, b2, 0:n], onesb[:], zsq[:, kc, 0:n], kc == 0, kc == 7, ['onesb', 'H0'], [('ps', b2)])
            mean, msq, var, rstd, nmr = [s[:, 0:n] for s in st1]
            ts(mean, ps[:, b1, 0:n], 1.0 / D, None, ALU.mult, None, [('ps', b1)], ['st0'])
            tt(msq, mean, mean, ALU.mult, ['st0'], ['st1'])
            stt(var, ps[:, b2, 0:n], 1.0 / D, msq, ALU.mult, ALU.subtract, [('ps', b2), 'st1'], ['st2'])
            ts(var, var, 0.0, eps, ALU.max, ALU.add, ['st2'], ['st2'])
            act(var, var, AF.Sqrt, ['st2'], ['st2'])
            recip(rstd, var, ['st2'], ['st3'])
            tt(nmr, mean, rstd, ALU.mult, ['st0', 'st3'], ['st4'])
            tt(h32[:, :, 0:n], h32[:, :, 0:n], rstd.unsqueeze(1).to_broadcast([128, 8, n]), ALU.mult, ['h32', 'st3'], ['h32'])
            tt(h32[:, :, 0:n], h32[:, :, 0:n], nmr.unsqueeze(1).to_broadcast([128, 8, n]), ALU.subtract, ['h32', 'st4'], ['h32'])
            for kc in range(8):
                act(h32[:, kc, 0:n], h32[:, kc, 0:n], AF.Identity, ['h32', 'lng', 'lnb'], ['h32'],
                    bias=lnb[:, idx, kc:kc + 1], scale=lng[:, idx, kc:kc + 1])
            cp(hb[:, :, 0:n], h32[:, :, 0:n], ['h32'], ['hb'], eng='act')

        def ffn(wi, wo, ln_idx, n):
            P.phase = 'ffn'
            actb = HX
            sg = [sgb[0][:, 0:n], sgb[1][:, 0:n]]
            for g in range(11):
                wb, wk = w_next((wi, g))
                wv = sqview(wb, 512)
                for jj in range(2):
                    j = 2 * g + jj
                    pg = bank(); pu = bank()
                    for kc in range(8):
                        mm(ps[:, pg, 0:n], wv[:, kc, jj * 128:(jj + 1) * 128], hb[:, kc, 0:n], kc == 0, kc == 7, [wk, 'hb'], [('ps', pg)])
                    for kc in range(8):
                        mm(ps[:, pu, 0:n], wv[:, kc, 256 + jj * 128:256 + (jj + 1) * 128], hb[:, kc, 0:n], kc == 0, kc == 7, [wk, 'hb'], [('ps', pu)])
                    act(sg[jj], ps[:, pg, 0:n], AF.Silu, [('ps', pg)], [('sg', jj)])
                    tt(actb[:, j, 0:n], sg[jj], ps[:, pu, 0:n], ALU.mult, [('sg', jj), ('ps', pu)], ['HX%d' % (j // 8)])
            for m in range(8):
                wb, wk = w_next((wo, m))
                wv = wb[:, 0:NJ * 128].rearrange('p (j c) -> p j c', j=NJ)
                po = bank()
                for j in range(NJ):
                    mm(ps[:, po, 0:n], wv[:, j, :], actb[:, j, 0:n], j == 0, j == NJ - 1, [wk, 'HX%d' % (j // 8)], [('ps', po)])
                stt(h32[:, m, 0:n], ps[:, po, 0:n], 0.5 / ALPHA, h32[:, m, 0:n], ALU.mult, ALU.add, [('ps', po), 'h32'], ['h32'])
            layernorm(ln_idx, n, LN_EPS / (ALPHA * ALPHA))

        def proj(w, ncols, xin, xkey, n, evac):
            nb = (ncols + 511) // 512
            for b in range(nb):
                c0 = b * 512
                cn = min(512, ncols - c0)
                wb, wk = w_next((w, b))
                wv = sqview(wb, cn)
                for q in range(cn // 128):
                    m = c0 // 128 + q
                    pb = bank()
                    for kc in range(8):
                        mm(ps[:, pb, 0:n], wv[:, kc, q * 128:(q + 1) * 128], xin[:, kc, 0:n], kc == 0, kc == 7, [wk, xkey], [('ps', pb)])
                    evac(m, pb)

        def mem_kv():
            P.phase = 'mem_kv'
            for r in range(2):
                load_tokens_fm(di['mem'][r * 128:(r + 1) * 128, :], 128, None, memT, r * 128, 'memT')
            for (w, outname, isk) in [('xa_wk', 'pmk', True), ('xa_wv', 'pmv', False)]:
                for b in range(2):
                    wb, wk = w_next((w, b))
                    wv = sqview(wb, 512)
                    for r in range(2):
                        pb = bank()
                        for kc in range(8):
                            mm(ps[:, pb, :], memT[:, kc, r * 128:(r + 1) * 128], wv[:, kc, :], kc == 0, kc == 7, ['H3', wk], [('ps', pb)])
                        i = bank_ctr[0] % 2
                        cp(tok32[i][:, 0:512], ps[:, pb, :], [('ps', pb)], [('tok32', i)], eng='act')
                        if not isk:
                            cp(mvb[:, r, b * 512:(b + 1) * 512], ps[:, pb, :], [('ps', pb)], ['mvb'], eng='dve')
                        dma('sp', di[outname][r * 128:(r + 1) * 128, b * 512:(b + 1) * 512], tok32[i][:, 0:512], [('tok32', i)], [], dsem_out[i], is_out=True)
                    if isk:
                        for q in range(4):
                            m = b * 4 + q
                            pb = bank()
                            for kc in range(8):
                                mm(ps[:, pb, 0:NMEM], wv[:, kc, q * 128:(q + 1) * 128], memT[:, kc, :], kc == 0, kc == 7, [wk, 'H3'], [('ps', pb)])
                            cp(mkT[:, m, :], ps[:, pb, 0:NMEM], [('ps', pb)], ['mkT'], eng='act')

        def mixer_prompt(ti):
            P.phase = 'mixer_prompt'
            n = NT
            r32, k32, v32, a32, ls32 = FB_
            glb = HB_[1]; geb = HB_[2]; g0b = HB_[3]
            g1b = HX[:, 0:8, :]; xcb = HX[:, 8:16, :]
            first = (ti == 0)
            tmp = st1[0]

            def evac(m, pb):
                psn = ps[:, pb, 0:n]
                SS, KP = stsel(m)
                tmp = SS[0]
                if m < 26:
                    act(tmp[:, 1:n], ps[:, pb, 0:n - 1], AF.Copy, [('ps', pb), 'mu'], [KP + '0'], scale=mu[:, m:m + 1])
                    if first:
                        P.op('dve', lambda e, tmp=tmp: e.memset(tmp[:, 0:1], 0.0), writes=[KP + '0'])
                    else:
                        tt(tmp[:, 0:1], carry_sh[:, m:m + 1], mu[:, m:m + 1], ALU.mult, ['carry_sh', 'mu'], [KP + '0'])
                    cp(carry_sh[:, m:m + 1], ps[:, pb, n - 1:n], [('ps', pb)], ['carry_sh'])
                    if m < 24:
                        dst = [r32, k32, v32][m // 8][:, m % 8, :]
                        dkey = ['F0', 'F1', 'F2'][m // 8]
                        stt(dst, psn, omu[:, m:m + 1], tmp[:, 0:n], ALU.mult, ALU.add, [('ps', pb), 'omu', KP + '0'], [dkey])
                    else:
                        xs_ = SS[1][:, 0:n]
                        stt(xs_, psn, omu[:, m:m + 1], tmp[:, 0:n], ALU.mult, ALU.add, [('ps', pb), 'omu', KP + '0'], [KP + '1'])
                        lb = SS[2][:, 0:n].bitcast(BF16)[:, 0:n]
                        if m == 24:
                            act(lb[0:64, :], xs_[0:64, :], AF.Tanh, [KP + '1'], [KP + '2'])
                            cp(lb[64:128, :], xs_[64:128, :], [KP + '1'], [KP + '2'])
                            for (lo, dstt, dk, bvec) in [(0, ls32, 'F4', 'decay_w0'), (64, a32, 'F3', 'aaa_a0')]:
                                for q in range(8):
                                    p2 = bank()
                                    mm(ps[:, p2, 0:n], w2a2[lo:lo + 64, q * 128:(q + 1) * 128], lb[lo:lo + 64, :], True, True, ['w2a2', KP + '2'], [('ps', p2)])
                                    act(dstt[:, q, :], ps[:, p2, 0:n], AF.Sigmoid, [('ps', p2), 'v_' + bvec], [dk], bias=vec[bvec][:, q:q + 1])
                        else:
                            act(lb, xs_, AF.Sigmoid, [KP + '1'], [KP + '2'])
                            for q in range(8):
                                p2 = bank()
                                mm(ps[:, p2, 0:n], g2[:, q * 128:(q + 1) * 128], lb, True, True, ['g2', KP + '2'], [('ps', p2)])
                                cp(glb[:, q, :], ps[:, p2, 0:n], [('ps', p2)], ['H1'], eng='act')
                elif m < 34:
                    cp(pl[:, m - 26, 3:3 + n], psn, [('ps', pb)], ['pl'], eng='act')
                elif m < 42:
                    act(geb[:, m - 34, :], psn, AF.Gelu, [('ps', pb)], ['H2'])
                elif m < 50:
                    act(g0b[:, m - 42, :], psn, AF.Sigmoid, [('ps', pb)], ['H3'])
                else:
                    act(g1b[:, m - 50, :], psn, AF.Sigmoid, [('ps', pb)], ['HX0'])

            if first:
                P.op('dve', lambda e: e.memset(pl[:, :, 0:3], 0.0), writes=['pl'])
            else:
                cp(pl[:, :, 0:3], osml[:, :, 4:7], ['osml'], ['pl'])
            proj('w_in', PW, hb, 'hb', n, evac)
            cp(osml[:, :, 4:7], pl[:, :, n:n + 3], ['pl'], ['osml'])
            dump('r32', r32[:], 'F0'); dump('k32', k32[:], 'F1'); dump('v32', v32[:], 'F2'); dump('a32', a32[:], 'F3'); dump('ls32', ls32[:], 'F4')
            if ti == int(os.environ.get('NTI', NTILES)) - 1:
                cp(osml[:, :, 0:3], pl[:, :, n:n + 3], ['pl'], ['osml'])
                cp(osh[:], carry_sh[:], ['carry_sh'], ['osh'])
            return dict(r32=r32, k32=k32, v32=v32, a32=a32, ls32=ls32, glb=glb, geb=geb, g0b=g0b, g1b=g1b, xcb=xcb)

        def lru_prompt(ti, B):
            P.phase = 'lru_prompt'
            n = NT
            geb, g1b, xcb = B['geb'], B['g1b'], B['xcb']
            lmb = HX[:, 16:24, :]
            def lru_chunk(c):
                SS, KP = stsel(c)
                xc = SS[0][:, 0:n]; gr = SS[1][:, 0:n]; gi = SS[2][:, 0:n]; t3 = SS[3][:, 0:n]; hs = SS[4][:, 0:n]
                act(xc, pl[:, c, 3:3 + n], AF.Identity, ['pl', 'cw', 'v_conv_b'], [KP + '0'], bias=vec['conv_b'][:, c:c + 1], scale=cw[:, 3, c:c + 1])
                for j in range(3):
                    stt(xc, pl[:, c, j:j + n], cw[:, j, c:c + 1], xc, ALU.mult, ALU.add, ['pl', 'cw', KP + '0'], [KP + '0'])
                yield
                cp(xcb[:, c, :], xc, [KP + '0'], ['HX1'], eng='act')
                p1 = bank(); p2 = bank()
                yield
                mm(ps[:, p1, 0:n], wrbd[:, c, :], xcb[:, c, :], True, True, ['wrbd', 'HX1'], [('ps', p1)])
                yield
                mm(ps[:, p2, 0:n], wibd[:, c, :], xcb[:, c, :], True, True, ['wibd', 'HX1'], [('ps', p2)])
                yield
                act(gr, ps[:, p1, 0:n], AF.Sigmoid, [('ps', p1), 'v_lru_br'], [KP + '1'], bias=vec['lru_br'][:, c:c + 1])
                yield
                act(gi, ps[:, p2, 0:n], AF.Sigmoid, [('ps', p2), 'v_lru_bi'], [KP + '2'], bias=vec['lru_bi'][:, c:c + 1])
                yield
                act(gr, gr, AF.Exp, [KP + '1', 'lsp'], [KP + '1'], scale=lsp[:, c:c + 1])
                yield
                act(t3, gr, AF.Square, [KP + '1'], [KP + '3'])
                yield
                ts(t3, t3, -1.0, 1.0, ALU.mult, ALU.add, [KP + '3'], [KP + '3'])
                yield
                ts(t3, t3, 0.0, None, ALU.max, None, [KP + '3'], [KP + '3'])
                yield
                act(t3, t3, AF.Sqrt, [KP + '3'], [KP + '3'])
                yield
                tt(gi, gi, xc, ALU.mult, [KP + '2', KP + '0'], [KP + '2'])
                yield
                tt(gi, gi, t3, ALU.mult, [KP + '2', KP + '3'], [KP + '2'])
                yield
                if ti == 0:
                    P.op('dve', lambda e, hs=hs, gr=gr, gi=gi: e.tensor_tensor_scan(out=hs, data0=gr, data1=gi, initial=0.0, op0=ALU.mult, op1=ALU.add),
                         reads=[KP + '1', KP + '2'], writes=[KP + '4'])
                else:
                    P.op('dve', lambda e, c=c, hs=hs, gr=gr, gi=gi: e.tensor_tensor_scan(out=hs, data0=gr, data1=gi, initial=carry_h[:, c:c + 1], op0=ALU.mult, op1=ALU.add),
                         reads=[KP + '1', KP + '2', ('carry_h', c)], writes=[KP + '4'])
                yield
                cp(carry_h[:, c:c + 1], hs[:, n - 1:n], [KP + '4'], [('carry_h', c)])
                yield
                tt(t3, hs, geb[:, c, :], ALU.mult, [KP + '4', 'H2'], [KP + '3'])
                yield
                tt(lmb[:, c, :], t3, g1b[:, c, :], ALU.mult, [KP + '3', 'HX0'], ['HX2'])
            for _c0 in range(0, 8, 2):
                interleave2(lru_chunk(_c0), lru_chunk(_c0 + 1))
            if ti == int(os.environ.get('NTI', NTILES)) - 1:
                cp(osml[:, :, 3:4], carry_h[:].unsqueeze(2), [('carry_h', c_) for c_ in range(8)], ['osml'])
            return lmb


        plb = pl[:].rearrange('p a b -> p (a b)').bitcast(BF16)
        plA = plb[:, 0:8 * NT].rearrange('p (a b) -> p a b', a=8)
        plB = plb[:, 8 * NT:16 * NT].rearrange('p (a b) -> p a b', a=8)

        def rwkv_core(B, state_in, state_out, skip_inverse=False):
            P.phase = 'rwkv_core'
            n = NT
            r32, k32, v32, a32, ls32 = B['r32'], B['k32'], B['v32'], B['a32'], B['ls32']
            bc8 = lambda v: v[:].unsqueeze(2).to_broadcast([128, 8, n])
            QR = HX[:, 0:16, :].rearrange('p (k two) (c t) -> p k two c t', two=2, t=C)
            KT = HB_[0]; NB = hb
            kk32 = pl[:, :, 0:n]
            tt(kk32, k32[:], bc8(vec['k_k']), ALU.mult, ['F1', 'v_k_k'], ['pl'])
            act(KT[:], kk32, AF.Square, ['pl'], ['H0'])
            for kc in range(8):
                pb = bank()
                mm(ps[:, pb, 0:n], bdb[:], KT[:, kc, :], True, True, ['bdb', 'H0'], [('ps', pb)])
                s_ = st1[kc % 2][:, 0:n]; sk = 'st%d' % (kc % 2)
                act(s_, ps[:, pb, 0:n], AF.Sqrt, [('ps', pb)], [sk])
                ts(s_, s_, 1e-12, None, ALU.max, None, [sk], [sk])
                recip(s_, s_, [sk], [sk])
                tt(kk32[:, kc, :], kk32[:, kc, :], s_, ALU.mult, ['pl', sk], ['pl'])
            for kc in range(8):
                u_ = st1[2 + kc % 2][:, 0:n]; sk = 'st%d' % (2 + kc % 2)
                ts(u_, a32[:, kc, :], vec['k_a'][:, kc:kc + 1], oka[:, kc:kc + 1], ALU.mult, ALU.add, ['F3', 'v_k_a', 'oka'], [sk])
                tt(k32[:, kc, :], k32[:, kc, :], u_, ALU.mult, ['F1', sk], ['F1'])
            tt(a32[:], a32[:], kk32, ALU.mult, ['F3', 'pl'], ['F3'])
            rkb = HB_[2]
            for kc in range(8):
                u_ = st1[kc % 2][:, 0:n]; sk = 'st%d' % (kc % 2)
                tt(u_, r32[:, kc, :], k32[:, kc, :], ALU.mult, ['F0', 'F1'], [sk])
                ts(rkb[:, kc, :], u_, vec['r_k'][:, kc:kc + 1], None, ALU.mult, None, [sk, 'v_r_k'], ['H2'])
            P.phase = 'rw_decay'
            def decay_chunk(kc):
                SS, KP = stsel(kc)
                cs = SS[0][:, 0:n]; dd = SS[1][:, 0:n]; Wi = SS[2][:, 0:n]; We = SS[3][:, 0:n]; Wv = SS[4][:, 0:n]
                P.op('dve', lambda e, kc=kc, cs=cs: e.tensor_tensor_scan(out=cs, data0=reset[:, 0:n], data1=ls32[:, kc, :], initial=0.0, op0=ALU.mult, op1=ALU.add),
                     reads=['reset', 'F4'], writes=[KP + '0'])
                yield
                tt(dd, cs, ls32[:, kc, :], ALU.subtract, [KP + '0', 'F4'], [KP + '1'])
                yield
                act(Wi, cs, AF.Exp, [KP + '0'], [KP + '2'], scale=-C0)
                yield
                act(We, dd, AF.Exp, [KP + '1'], [KP + '3'], scale=-C0)
                yield
                act(Wv, cs, AF.Exp, [KP + '0'], [KP + '4'], scale=C0)
                c4 = lambda a: a.rearrange('p (c t) -> p c t', t=C)
                yield
                tt(QR[:, kc, 1, :, :], c4(r32[:, kc, :]), c4(Wi), ALU.mult, ['F0', KP + '2'], ['HX0', 'HX1'])
                yield
                tt(QR[:, kc, 0, :, :], c4(kk32[:, kc, :]), c4(We), ALU.mult, ['pl', KP + '3'], ['HX0', 'HX1'])
                yield
                cp(WCs[:, kc, :], c4(Wi)[:, :, C - 1], [KP + '2'], ['WCs'])
                yield
                tt(Wi, k32[:, kc, :], Wv, ALU.mult, ['F1', KP + '4', KP + '2'], [KP + '2'])
                yield
                stt(We, a32[:, kc, :], -1.0, Wv, ALU.mult, ALU.mult, ['F3', KP + '4', KP + '3'], [KP + '3'])
                yield
                cp(NB[:, kc, :], We, [KP + '3'], ['hb'], eng='act')
                yield
                cp(KT[:, kc, :], Wi, [KP + '2'], ['H0'], eng='act')
            for _c0 in range(0, 8, 2):
                interleave2(decay_chunk(_c0), decay_chunk(_c0 + 1))
            P.phase = 'rw_tok'
            vb = plB
            cp(vb[:, :, 0:n], v32[:], ['F2'], ['pl'], eng='act')
            for kc in range(8):
                pb = bank()
                mm(ps[:, pb, 0:n], bdb[:], rkb[:, kc, :], True, True, ['bdb', 'H2'], [('ps', pb)])
                tt(v32[:, kc, :], v32[:, kc, :], ps[:, pb, 0:n], ALU.mult, ['F2', ('ps', pb)], ['F2'])
            tmv = lambda a: a.rearrange('p a b -> p (a b)').rearrange('p (c x) -> p c x', c=NCH)
            vT = tmv(HB_[2][:]); kTt = tmv(plA); nbT = tmv(plB)

            def to_tokmajor(src, skey, dst, dkey):
                for c in range(NCH):
                    pb = bank()
                    pbv = ps[:, pb, :].bitcast(BF16)
                    for hp in range(8):
                        for par in range(2):
                            lo = par * 64
                            tr(pbv[lo:lo + 64, hp * 64:(hp + 1) * 64], src[lo:lo + 64, hp, c * C:(c + 1) * C], identb[lo:lo + 64, lo:lo + 64],
                               [skey, 'identb'], [('ps', pb)])
                    cp(dst[:, c, :], pbv[:, 0:512], [('ps', pb)], [dkey], eng=('act' if c % 2 else 'dve'))
            to_tokmajor(vb, 'pl', vT, 'H2')
            to_tokmajor(KT, 'H0', kTt, 'pl')
            to_tokmajor(NB, 'hb', nbT, 'pl')
            vTv = lambda c, hp, lo: vT[lo:lo + 64, c, hp * 64:(hp + 1) * 64]
            kTv = lambda c, hp, lo: kTt[lo:lo + 64, c, hp * 64:(hp + 1) * 64]
            nTv = lambda c, hp, lo: nbT[lo:lo + 64, c, hp * 64:(hp + 1) * 64]
            m_su_ui = mask[:, 0:128].unsqueeze(1).to_broadcast([128, 8, 128])
            m_sl = mask[:, 128:192].unsqueeze(1).to_broadcast([128, 8, 64])
            m_eye = mask[:, 192:256].unsqueeze(1).to_broadcast([128, 8, 64])
            def ph1(c0):
                P.phase = 'rw_ph1'
                ctx = []
                for s in range(2):
                    c = c0 + s
                    b1a = bank(); b1b = bank()
                    for hp in range(8):
                        for par in range(2):
                            lo = par * 64
                            bsel = b1a if hp < 4 else b1b
                            mm(ps[lo:lo + 64, bsel, (hp % 4) * 128:(hp % 4 + 1) * 128], KT[lo:lo + 64, hp, c * C:(c + 1) * C],
                               QR[lo:lo + 64, hp, :, c, :], True, True, ['H0', 'HX0', 'HX1'], [('ps', bsel)])
                    for (bsel, h0) in [(b1a, 0), (b1b, 4)]:
                        tt(LP[:, h0:h0 + 4, c, :], ps[:, bsel, :].rearrange('p (h x) -> p h x', h=4), m_su_ui[:, 0:4, :], ALU.mult,
                           [('ps', bsel), 'mask'], [('LP', c)])
                    b2a = bank(); b2b = bank()
                    for hp in range(8):
                        for par in range(2):
                            lo = par * 64
                            bsel = b2a if hp < 4 else b2b
                            mm(ps[lo:lo + 64, bsel, (hp % 4) * 128:(hp % 4 + 1) * 128], NB[lo:lo + 64, hp, c * C:(c + 1) * C],
                               QR[lo:lo + 64, hp, :, c, :], True, True, ['hb', 'HX0', 'HX1'], [('ps', bsel)])
                    for (bsel, h0) in [(b2a, 0), (b2b, 4)]:
                        tt(PN[:, h0:h0 + 4, c, :], ps[:, bsel, :].rearrange('p (h x) -> p h x', h=4), m_su_ui[:, 0:4, :], ALU.mult,
                           [('ps', bsel), 'mask'], [('PN', c)])
                    b3 = bank()
                    for hp in range(8):
                        for par in range(2):
                            lo = par * 64
                            mm(ps[lo:lo + 64, b3, hp * 64:(hp + 1) * 64], QR[lo:lo + 64, hp, 0, c, :], NB[lo:lo + 64, hp, c * C:(c + 1) * C],
                               True, True, ['HX0', 'HX1', 'hb'], [('ps', b3)])
                    pp = PP[s][0]
                    cp(pp[:, :, 0:64], PN[:, :, c, 0:64], [('PN', c)], [('PP', s, 0)], eng='act')
                    tt(pp[:, :, 64:128], ps[:, b3, :].rearrange('p (h x) -> p h x', h=8), m_sl, ALU.mult, [('ps', b3), 'mask'], [('PP', s, 0)])
                    tt(X32[s][:], PN[:, :, c, 0:64], m_eye, ALU.add, [('PN', c), 'mask'], [('X32', s)])
                    cp(Xb[s][:], X32[s][:], [('X32', s)], [('Xb', s)], eng='act')
                    ctx.append(c)
                if skip_inverse:
                    for s_ in range(2):
                        cp(XT[:, :, ctx[s_], :], X32[s_][:], [('X32', s_)], [('XT', ctx[s_])], eng='act')
                for lvl in ([] if skip_inverse else range(1, 6)):
                    yield
                    P.phase = 'rw_ph1'
                    cur = (lvl - 1) % 2; nxt = lvl % 2
                    banks = []
                    for s in range(2):
                        ba = bank(); bb = bank()
                        src = PP[s][cur]
                        for hp in range(8):
                            for par in range(2):
                                lo = par * 64
                                bsel = ba if hp < 4 else bb
                                o0 = (hp % 4) * 128
                                mm(ps[lo:lo + 64, bsel, o0:o0 + 64], src[lo:lo + 64, hp, 64:128], src[lo:lo + 64, hp, 0:64], True, True,
                                   [('PP', s, cur)], [('ps', bsel)])
                                mm(ps[lo:lo + 64, bsel, o0 + 64:o0 + 128], src[lo:lo + 64, hp, 0:64], src[lo:lo + 64, hp, 64:128], True, True,
                                   [('PP', s, cur)], [('ps', bsel)])
                        banks.append((ba, bb))
                    for s in range(2):
                        ba, bb = banks[s]
                        dst = PP[s][nxt]
                        cp(dst[:, 0:4, :], ps[:, ba, :].rearrange('p (h x) -> p h x', h=4), [('ps', ba)], [('PP', s, nxt)], eng='act')
                        cp(dst[:, 4:8, :], ps[:, bb, :].rearrange('p (h x) -> p h x', h=4), [('ps', bb)], [('PP', s, nxt)], eng='dve')
                    xb_ = []
                    for s in range(2):
                        bx = bank()
                        src = PP[s][nxt]
                        for hp in range(8):
                            for par in range(2):
                                lo = par * 64
                                mm(ps[lo:lo + 64, bx, hp * 64:(hp + 1) * 64], src[lo:lo + 64, hp, 64:128], Xb[s][lo:lo + 64, hp, :], True, True,
                                   [('PP', s, nxt), ('Xb', s)], [('ps', bx)])
                        xb_.append(bx)
                    for s in range(2):
                        bx = xb_[s]
                        tt(X32[s][:], X32[s][:], ps[:, bx, :].rearrange('p (h x) -> p h x', h=8), ALU.add, [('X32', s), ('ps', bx)], [('X32', s)])
                        if lvl < 5:
                            cp(Xb[s][:], X32[s][:], [('X32', s)], [('Xb', s)], eng='act')
                        else:
                            cp(XT[:, :, ctx[s], :], X32[s][:], [('X32', s)], [('XT', ctx[s])], eng='act')
            Y32 = FB_[0]

            def ph2(c):
                P.phase = 'rw_ph2'
                state_in(c)
                cur = c % 2
                a0 = A0b[cur]
                bR = bank()
                for hp in range(8):
                    for par in range(2):
                        lo = par * 64
                        o = ps[lo:lo + 64, bR, hp * 64:(hp + 1) * 64]
                        mm(o, QR[lo:lo + 64, hp, 0, c, :], a0[lo:lo + 64, hp, :], True, False, ['HX0', 'HX1', ('A0b', cur)], [('ps', bR)])
                        mm(o, LP[lo:lo + 64, hp, c, 0:64], vTv(c, hp, lo), False, True, [('LP', c), 'H2'], [('ps', bR)])
                cp(RHSb[:], ps[:, bR, :].rearrange('p (h x) -> p h x', h=8), [('ps', bR)], ['RHSb'], eng='act')
                yield
                P.phase = 'rw_ph2'
                bU = bank()
                for hp in range(8):
                    for par in range(2):
                        lo = par * 64
                        mm(ps[lo:lo + 64, bU, hp * 64:(hp + 1) * 64], XT[lo:lo + 64, hp, c, :], RHSb[lo:lo + 64, hp, :], True, True,
                           [('XT', c), 'RHSb'], [('ps', bU)])
                cp(Ub[:], ps[:, bU, :].rearrange('p (h x) -> p h x', h=8), [('ps', bU)], ['Ub'], eng='act')
                yield
                P.phase = 'rw_ph2'
                bD = bank()
                for hp in range(8):
                    for par in range(2):
                        lo = par * 64
                        o = ps[lo:lo + 64, bD, hp * 64:(hp + 1) * 64]
                        mm(o, kTv(c, hp, lo), vTv(c, hp, lo), True, False, ['pl', 'H2'], [('ps', bD)])
                        mm(o, nTv(c, hp, lo), Ub[lo:lo + 64, hp, :], False, True, ['pl', 'Ub'], [('ps', bD)])
                bY = bank()
                for hp in range(8):
                    for par in range(2):
                        lo = par * 64
                        o = ps[lo:lo + 64, bY, hp * 64:(hp + 1) * 64]
                        mm(o, a0[lo:lo + 64, hp, :], QR[lo:lo + 64, hp, 1, c, :], True, False, [('A0b', cur), 'HX0', 'HX1'], [('ps', bY)])
                        mm(o, vTv(c, hp, lo), LP[lo:lo + 64, hp, c, 64:128], False, False, ['H2', ('LP', c)], [('ps', bY)])
                        mm(o, Ub[lo:lo + 64, hp, :], PN[lo:lo + 64, hp, c, 64:128], False, True, ['Ub', ('PN', c)], [('ps', bY)])
                tt(A32[:], A32[:], ps[:, bD, :].rearrange('p (h x) -> p h x', h=8), ALU.add, ['A32', ('ps', bD)], ['A32'])
                tt(A32[:], A32[:], WCs[:, :, c:c + 1].to_broadcast([128, 8, 64]), ALU.mult, ['A32', 'WCs'], ['A32'])
                cp(A0b[1 - cur][:], A32[:], ['A32'], [('A0b', 1 - cur)], eng='act')
                cp(Y32[:, :, c * C:(c + 1) * C], ps[:, bY, :].rearrange('p (h x) -> p h x', h=8), [('ps', bY)], ['F0'], eng='dve')
                state_out(c)
                yield

            def drain(g):
                for _ in g:
                    pass

            def interleave(ga, gb):
                a_ok = b_ok = True
                while a_ok or b_ok:
                    if a_ok:
                        try:
                            next(ga)
                        except StopIteration:
                            a_ok = False
                    if b_ok:
                        try:
                            next(gb)
                        except StopIteration:
                            b_ok = False

            def chain(*gs):
                for g in gs:
                    yield from g
            drain(ph1(0))
            if NCH == 4:
                interleave(ph1(2), chain(ph2(0), ph2(1)))
                drain(chain(ph2(2), ph2(3)))
            else:
                for c0 in range(2, NCH, 2):
                    drain(ph1(c0))
                for c in range(NCH):
                    drain(ph2(c))
            return Y32, v32

        def rwkv_post(Y, Ykey, bonus, bkey, glb, g0b, lmb, mb, n):
            P.phase = 'rwkv_post'
            Yb = HB_[0]; ysq = hb
            cp(Yb[:, :, 0:n], Y[:, :, 0:n], [Ykey], ['H0'], eng='act')
            act(ysq[:, :, 0:n], Y[:, :, 0:n], AF.Square, [Ykey], ['hb'])
            def post_chunk(kc):
                b1 = bank(); b2 = bank()
                mm(ps[:, b1, 0:n], bdb[:], Yb[:, kc, 0:n], True, True, ['bdb', 'H0'], [('ps', b1)])
                yield
                mm(ps[:, b2, 0:n], bdb[:], ysq[:, kc, 0:n], True, True, ['bdb', 'hb'], [('ps', b2)])
                SS, KP = stsel(kc)
                mean = SS[0][:, 0:n]; var = SS[1][:, 0:n]; t_ = SS[2][:, 0:n]
                yield
                ts(mean, ps[:, b1, 0:n], 1.0 / 64, None, ALU.mult, None, [('ps', b1)], [KP + '0'])
                yield
                tt(var, mean, mean, ALU.mult, [KP + '0'], [KP + '1'])
                yield
                stt(var, ps[:, b2, 0:n], 1.0 / 64, var, ALU.mult, ALU.subtract, [('ps', b2), KP + '1'], [KP + '1'])
                yield
                ts(var, var, 0.0, GN_EPS, ALU.max, ALU.add, [KP + '1'], [KP + '1'])
                yield
                act(var, var, AF.Sqrt, [KP + '1'], [KP + '1'])
                yield
                recip(var, var, [KP + '1'], [KP + '1'])
                yield
                tt(t_, Y[:, kc, 0:n], mean, ALU.subtract, [Ykey, KP + '0'], [KP + '2'])
                yield
                tt(t_, t_, var, ALU.mult, [KP + '2', KP + '1'], [KP + '2'])
                yield
                act(t_, t_, AF.Identity, [KP + '2', 'v_gn_g', 'v_gn_b'], [KP + '2'], bias=vec['gn_b'][:, kc:kc + 1], scale=vec['gn_g'][:, kc:kc + 1])
                yield
                tt(t_, t_, bonus[:, kc, 0:n], ALU.add, [KP + '2', bkey], [KP + '2'])
                yield
                tt(t_, t_, glb[:, kc, 0:n], ALU.mult, [KP + '2', 'H1'], [KP + '2'])
                yield
                tt(t_, t_, g0b[:, kc, 0:n], ALU.mult, [KP + '2', 'H3'], [KP + '2'])
                yield
                tt(mb[:, kc, 0:n], t_, lmb[:, kc, 0:n], ALU.add, [KP + '2', 'HX2'], ['H2'])
            for _c0 in range(0, 8, 2):
                interleave2(post_chunk(_c0), post_chunk(_c0 + 1))

        def resid_ln(w, xin, xkey, ln_idx, n):
            def evac(m, pb):
                stt(h32[:, m, 0:n], ps[:, pb, 0:n], 1.0 / ALPHA, h32[:, m, 0:n], ALU.mult, ALU.add, [('ps', pb), 'h32'], ['h32'])
            proj(w, D, xin, xkey, n, evac)
            layernorm(ln_idx, n, LN_EPS / (ALPHA * ALPHA))

        def xattn_prompt(n):
            P.phase = 'xattn_prompt'
            qb = HB_[0]; ob = HB_[1]; pT = HX[:, 0:8, :]
            def evq(m, pb):
                cp(qb[:, m, 0:n], ps[:, pb, 0:n], [('ps', pb)], ['H0'], eng='act')
            proj('xa_wq', D, hb, 'hb', n, evq)
            for h in range(4):
                for mc in range(2):
                    pb = bank()
                    for dc in range(2):
                        mm(ps[:, pb, 0:n], mkT[:, 2 * h + dc, mc * 128:(mc + 1) * 128], qb[:, 2 * h + dc, 0:n], dc == 0, dc == 1, ['mkT', 'H0'], [('ps', pb)])
                    act(pT[:, 2 * h + mc, 0:n], ps[:, pb, 0:n], AF.Exp, [('ps', pb)], ['HX0'], scale=1.0 / 16.0)
                pd = bank()
                for mc in range(2):
                    mm(ps[:, pd, 0:n], onesb[:], pT[:, 2 * h + mc, 0:n], mc == 0, mc == 1, ['onesb', 'HX0'], [('ps', pd)])
                rd = st1[h % 2][:, 0:n]; rk_ = 'st%d' % (h % 2)
                recip(rd, ps[:, pd, 0:n], [('ps', pd)], [rk_])
                for dc in range(2):
                    po = bank()
                    for mc in range(2):
                        mm(ps[:, po, 0:n], mvb[:, mc, (2 * h + dc) * 128:(2 * h + dc + 1) * 128], pT[:, 2 * h + mc, 0:n], mc == 0, mc == 1, ['mvb', 'HX0'], [('ps', po)])
                    tt(ob[:, 2 * h + dc, 0:n], ps[:, po, 0:n], rd, ALU.mult, [('ps', po), rk_], ['H1'])
            resid_ln('xa_wo', ob, 'H1', 2, n)

        if stage >= 1:
            mem_kv()
        if 'm' in PARTS or PARTS == 'abcdefg':
            P.op('dve', lambda e: e.memset(A32[:], 0.0), writes=['A32'])
            P.op('dve', lambda e: e.memset(A0b[0][:], 0.0), writes=[('A0b', 0)])
        for ti in range(int(os.environ.get('NTI', NTILES)) if stage >= 9 else 1):
            TOG = os.environ.get('TOG', '')
            for r in range(1 if '1' in TOG else NT // 128):
                load_tokens_fm(di['xp'][ti * NT + r * 128: ti * NT + (r + 1) * 128, :], 128, h32, None if 'D' in TOG else hb, r * 128, 'h32')
            if stage >= 2:
                ffn('ffn1_wi', 'ffn1_wo', 0, NT)
            if ti == 0:
                dump('h1', h32[:], 'h32')
            if stage < 3:
                break
            B = mixer_prompt(ti)
            if stage < 4:
                break
            lmb = lru_prompt(ti, B)
            if ti == 0:
                dump('lm', lmb, 'HX2')
            if stage < 5:
                break
            Y32, bonus = rwkv_core(B, lambda c: None, lambda c: None)
            if ti == 0:
                dump('Y', Y32[:], 'F0')
            if stage < 6:
                break
            mb = HB_[2]
            rwkv_post(Y32, 'F0', bonus, 'F2', B['glb'], B['g0b'], lmb, mb, NT)
            resid_ln('w_mix_out', mb, 'H2', 1, NT)
            if ti == 0:
                dump('h2', h32[:], 'h32')
            if stage < 7:
                break
            xattn_prompt(NT)
            if ti == 0:
                dump('h3', h32[:], 'h32')
            if stage < 8:
                break
            ffn('ffn2_wi', 'ffn2_wo', 3, NT)
            for r in range(NT // 128):
                store_fm_tokens(h32, 'h32', r * 128, 128, di['yp'][ti * NT + r * 128: ti * NT + (r + 1) * 128, :])
        if stage < 9:
            P.emit()
            return nc, P
        def prompt_outputs():
            pass
            Sout = FB_[1].rearrange('p a b -> p (a b)')[:, 0:1024].rearrange('p (hp par k) -> p hp par k', hp=8, par=2)
            for g in range(2):
                pb = bank()
                for q in range(4):
                    hp = g * 4 + q
                    tr(ps[0:64, pb, q * 128:(q + 1) * 128], A32[:, hp, :], ident[:, :], ['A32', 'ident'], [('ps', pb)])
                cp(Sout[0:64, g * 4:(g + 1) * 4, :, :], ps[0:64, pb, :].rearrange('p (q par k) -> p q par k', q=4, par=2), [('ps', pb)], ['F1'], eng='act')
            dma('sp', di['prw'].rearrange('(hp par v) k -> v hp par k', hp=8, par=2), Sout[0:64, :, :, :], ['F1'], [], dsem_misc, is_out=True)
            osm2 = FB_[2]
            cp(osm2[:, 0:8, 0:3], osml[:, :, 0:3], ['osml'], ['F2'])
            cp(osm2[:, 0:8, 3:4], osml[:, :, 3:4], ['osml'], ['F2'])
            store_fm_tokens(osm2, 'F2', 0, 3, di['pcv'][:, :])
            store_fm_tokens(osm2, 'F2', 3, 1, di['plru'].rearrange('(o d) -> o d', o=1))
            osh3 = FB_[3]
            cp(osh3[:, 0:8, 0:1], osh[:, 0:8].unsqueeze(2), ['osh'], ['F3'])
            cp(osh3[:, 0:8, 1:2], osh[:, 8:16].unsqueeze(2), ['osh'], ['F3'])
            cp(osh3[:, 0:8, 2:3], osh[:, 16:24].unsqueeze(2), ['osh'], ['F3'])
            cp(osh3[:, 0:2, 3:4], osh[:, 24:26].unsqueeze(2), ['osh'], ['F3'])
            pshv = di['psh'].rearrange('(o d) -> o d', o=1)
            for q in range(3):
                store_fm_tokens(osh3, 'F3', q, 1, pshv[:, q * 1024:(q + 1) * 1024])
            store_fm_tokens(osh3, 'F3', 3, 1, pshv[:, 3072:3328], nch=2)


        if os.environ.get('NTI') != '0':
            prompt_outputs()
        def sample_path():
            P.phase = 'sample_path'
            n = NS
            sm = lambda nm, shp, dt=F32: P.sbuf(nm, shp, dt)
            rc = sm('s_rc', [128, 8, n]); kc_ = sm('s_kc', [128, 8, n]); vc = sm('s_vc', [128, 8, n]); ac = sm('s_ac', [128, 8, n]); lsc = sm('s_lsc', [128, 8, n])
            yc = sm('s_yc', [128, 8, n]); bonc = sm('s_bonc', [128, 8, n]); plc = sm('s_plc', [128, 8, n]); hsc = sm('s_hsc', [128, 8, n])
            prevS = sm('s_prev', [128, 26, n]); praw = sm('s_praw', [128, 26, n]); h0S = sm('s_h0', [128, 8, n]); scvT = sm('s_scvT', [128, 8, 3 * n])
            BD = tok32[1][:].rearrange('p (h x) -> p h x', h=8); ones32 = st1[4][:, 0:128]
            glb = HB_[1]; geb = HB_[2]; g0b = HB_[3]; g1b = HX[:, 0:8, :]; lmb = HX[:, 16:24, :]
            dsS = [P.dma_sem() for _ in range(7)]

            def load_fm(src_rows_ap, nrows, nchunks, dst, dkey, tki, sem):
                tk = tok32[tki]
                dma('sp', tk[0:nrows, 0:nchunks * 128], src_rows_ap, [], [('tok32', tki)], sem)
                for g0 in range(0, nchunks, 4):
                    gn = min(4, nchunks - g0)
                    b = bank()
                    for q in range(gn):
                        tr(ps[:, b, q * 128:q * 128 + nrows], tk[0:nrows, (g0 + q) * 128:(g0 + q + 1) * 128], ident[0:nrows, 0:nrows],
                           [('tok32', tki), 'ident'], [('ps', b)])
                    cp(dst[:, g0:g0 + gn, 0:nrows], ps[:, b, :].rearrange('p (q t) -> p q t', q=4)[:, 0:gn, 0:nrows], [('ps', b)], [dkey], eng='act')

            load_tokens_fm(di['xs'], n, h32, hb, 0, 'h32')
            for q in range(4):
                c0 = q * 8; cn = min(8, 26 - c0)
                tmpd = FB_[0] if q % 2 == 0 else FB_[1]
                load_fm(di['ssh'][:, c0 * 128:(c0 + cn) * 128], n, cn, tmpd, 'F%d' % (q % 2), q % 2, dsS[q % 2])
                cp(prevS[:, c0:c0 + cn, :], tmpd[:, 0:cn, 0:n], ['F%d' % (q % 2)], ['s_prev'])
            load_fm(di['slru'], n, 8, h0S, 's_h0', 0, dsS[0])
            load_fm(di['scv'].rearrange('b j d -> (b j) d'), 3 * n, 8, scvT, 's_scvT', 1, dsS[1])
            dma('sp', di['scv_o'][:, 0:2, :], di['scv'][:, 1:3, :], [], [], P.dma_sem(), is_out=True)

            ffn('ffn1_wi', 'ffn1_wo', 0, n)

            tmp = st1[0]

            def evac(m, pb):
                psn = ps[:, pb, 0:n]
                if m < 26:
                    cp(praw[:, m, :], psn, [('ps', pb)], ['s_praw'], eng='act')
                    ts(tmp[:, 0:n], prevS[:, m, :], mu[:, m:m + 1], None, ALU.mult, None, ['s_prev', 'mu'], ['st0'])
                    if m < 24:
                        dst = [rc, kc_, vc][m // 8][:, m % 8, :]
                        dkey = ['s_rc', 's_kc', 's_vc'][m // 8]
                        stt(dst, psn, omu[:, m:m + 1], tmp[:, 0:n], ALU.mult, ALU.add, [('ps', pb), 'omu', 'st0'], [dkey])
                    else:
                        xs_ = st1[1][:, 0:n]
                        stt(xs_, psn, omu[:, m:m + 1], tmp[:, 0:n], ALU.mult, ALU.add, [('ps', pb), 'omu', 'st0'], ['st1'])
                        lb = st1[2][:, 0:NT].bitcast(BF16)[:, 0:n]
                        if m == 24:
                            act(lb[0:64, :], xs_[0:64, :], AF.Tanh, ['st1'], ['st2'])
                            cp(lb[64:128, :], xs_[64:128, :], ['st1'], ['st2'])
                            for (lo, dstt, dk, bvec) in [(0, lsc, 's_lsc', 'decay_w0'), (64, ac, 's_ac', 'aaa_a0')]:
                                for q in range(8):
                                    p2 = bank()
                                    mm(ps[:, p2, 0:n], w2a2[lo:lo + 64, q * 128:(q + 1) * 128], lb[lo:lo + 64, :], True, True, ['w2a2', 'st2'], [('ps', p2)])
                                    act(dstt[:, q, :], ps[:, p2, 0:n], AF.Sigmoid, [('ps', p2), 'v_' + bvec], [dk], bias=vec[bvec][:, q:q + 1])
                        else:
                            act(lb, xs_, AF.Sigmoid, ['st1'], ['st2'])
                            for q in range(8):
                                p2 = bank()
                                mm(ps[:, p2, 0:n], g2[:, q * 128:(q + 1) * 128], lb, True, True, ['g2', 'st2'], [('ps', p2)])
                                cp(glb[:, q, 0:n], ps[:, p2, 0:n], [('ps', p2)], ['H1'], eng='act')
                elif m < 34:
                    cp(plc[:, m - 26, :], psn, [('ps', pb)], ['s_plc'], eng='act')
                elif m < 42:
                    act(geb[:, m - 34, 0:n], psn, AF.Gelu, [('ps', pb)], ['H2'])
                elif m < 50:
                    act(g0b[:, m - 42, 0:n], psn, AF.Sigmoid, [('ps', pb)], ['H3'])
                else:
                    act(g1b[:, m - 50, 0:n], psn, AF.Sigmoid, [('ps', pb)], ['HX0'])
            proj('w_in', PW, hb, 'hb', n, evac)
            for q in range(4):
                c0 = q * 8; cn = min(8, 26 - c0)
                store_fm_tokens(praw[:, c0:c0 + cn, :], 's_praw', 0, n, di['ssh_o'][:, c0 * 128:(c0 + cn) * 128], nch=cn)
            store_fm_tokens(plc, 's_plc', 0, n, di['scv_o'][:, 2, :])

            sc3 = scvT[:].rearrange('p c (b j) -> p c b j', j=3)
            xc = st1[0][:, 0:n]; gr = st1[1][:, 0:n]; gi = st1[2][:, 0:n]; t3 = st1[3][:, 0:n]; hs = st1[4][:, 0:n]
            xcb = HX[:, 8:16, :]
            for c in range(8):
                act(xc, plc[:, c, :], AF.Identity, ['s_plc', 'cw', 'v_conv_b'], ['st0'], bias=vec['conv_b'][:, c:c + 1], scale=cw[:, 3, c:c + 1])
                for j in range(3):
                    stt(xc, sc3[:, c, :, j], cw[:, j, c:c + 1], xc, ALU.mult, ALU.add, ['s_scvT', 'cw', 'st0'], ['st0'])
                cp(xcb[:, c, 0:n], xc, ['st0'], ['HX1'], eng='act')
                p1 = bank(); p2 = bank()
                mm(ps[:, p1, 0:n], wrbd[:, c, :], xcb[:, c, 0:n], True, True, ['wrbd', 'HX1'], [('ps', p1)])
                mm(ps[:, p2, 0:n], wibd[:, c, :], xcb[:, c, 0:n], True, True, ['wibd', 'HX1'], [('ps', p2)])
                act(gr, ps[:, p1, 0:n], AF.Sigmoid, [('ps', p1), 'v_lru_br'], ['st1'], bias=vec['lru_br'][:, c:c + 1])
                act(gi, ps[:, p2, 0:n], AF.Sigmoid, [('ps', p2), 'v_lru_bi'], ['st2'], bias=vec['lru_bi'][:, c:c + 1])
                act(gr, gr, AF.Exp, ['st1', 'lsp'], ['st1'], scale=lsp[:, c:c + 1])
                act(t3, gr, AF.Square, ['st1'], ['st3'])
                ts(t3, t3, -1.0, 1.0, ALU.mult, ALU.add, ['st3'], ['st3'])
                ts(t3, t3, 0.0, None, ALU.max, None, ['st3'], ['st3'])
                act(t3, t3, AF.Sqrt, ['st3'], ['st3'])
                tt(gi, gi, xc, ALU.mult, ['st2', 'st0'], ['st2'])
                tt(gi, gi, t3, ALU.mult, ['st2', 'st3'], ['st2'])
                tt(hs, gr, h0S[:, c, :], ALU.mult, ['st1', 's_h0'], ['st4'])
                tt(hsc[:, c, :], hs, gi, ALU.add, ['st4', 'st2'], ['s_hsc'])
                tt(t3, hsc[:, c, :], geb[:, c, 0:n], ALU.mult, ['s_hsc', 'H2'], ['st3'])
                tt(lmb[:, c, 0:n], t3, g1b[:, c, 0:n], ALU.mult, ['st3', 'HX0'], ['HX2'])
            store_fm_tokens(hsc, 's_hsc', 0, n, di['slru_o'])

            Sout = FB_[1].rearrange('p a b -> p (a b)')[:, 0:1024].rearrange('p (hp par k) -> p hp par k', hp=8, par=2)
            P.op('dve', lambda e: e.memset(tok32[1][:], 0.0), writes=[('tok32', 1)])
            for g in range(NS // NCH):
                Bp = dict(r32=FB_[0], k32=FB_[1], v32=FB_[2], a32=FB_[3], ls32=FB_[4])
                for (dstF, fk, srcc, sk) in [(FB_[0], 'F0', rc, 's_rc'), (FB_[1], 'F1', kc_, 's_kc'), (FB_[2], 'F2', vc, 's_vc'), (FB_[3], 'F3', ac, 's_ac'), (FB_[4], 'F4', lsc, 's_lsc')]:
                    P.op('dve', lambda e, dstF=dstF: e.memset(dstF[:], 0.0), writes=[fk])
                    cp(dstF[:].rearrange('p k (c t) -> p k c t', t=C)[:, :, :, 0], srcc[:, :, g * NCH:(g + 1) * NCH], [sk], [fk])

                def state_in(c, g=g):
                    b_ = g * NCH + c
                    src = di['srw'][b_].rearrange('(hp par v) k -> par v hp k', hp=8, par=2)
                    for par in range(2):
                        dma('sp', BD[par * 64:(par + 1) * 64, :, par * 64:(par + 1) * 64], src[par], [], [('tok32', 1)], dsS[3])
                    pb = bank()
                    for hp in range(8):
                        mm(ps[:, pb, hp * 64:(hp + 1) * 64], BD[:, hp, :], mask[:, 192:256], True, True, [('tok32', 1), 'mask'], [('ps', pb)])
                    v3 = ps[:, pb, :].rearrange('p (h x) -> p h x', h=8)
                    cp(A32[:], v3, [('ps', pb)], ['A32'], eng='dve')
                    cp(A0b[c % 2][:], v3, [('ps', pb)], [('A0b', c % 2)], eng='act')

                def state_out(c, g=g):
                    b_ = g * NCH + c
                    for gg in range(2):
                        pb = bank()
                        for q in range(4):
                            hp = gg * 4 + q
                            tr(ps[0:64, pb, q * 128:(q + 1) * 128], A32[:, hp, :], ident[:, :], ['A32', 'ident'], [('ps', pb)])
                        cp(Sout[0:64, gg * 4:(gg + 1) * 4, :, :], ps[0:64, pb, :].rearrange('p (q par k) -> p q par k', q=4, par=2), [('ps', pb)], ['F1'], eng='act')
                    dma('sp', di['srw_o'][b_].rearrange('(hp par v) k -> v hp par k', hp=8, par=2), Sout[0:64, :, :, :], ['F1'], [], dsS[4], is_out=True)
                Y32, bon = rwkv_core(Bp, state_in, state_out, skip_inverse=True)
                cp(yc[:, :, g * NCH:(g + 1) * NCH], Y32[:].rearrange('p k (c t) -> p k c t', t=C)[:, :, :, 0], ['F0'], ['s_yc'])
                cp(bonc[:, :, g * NCH:(g + 1) * NCH], bon[:].rearrange('p k (c t) -> p k c t', t=C)[:, :, :, 0], ['F2'], ['s_bonc'])
            mb = HB_[2]
            rwkv_post(yc, 's_yc', bonc, 's_bonc', glb, g0b, lmb, mb, n)
            resid_ln('w_mix_out', mb, 'H2', 1, n)

            qc = FB_[0]; qT = FB_[1].rearrange('p a b -> p (a b)')[:, 0:1024]; sel = FB_[2].rearrange('p a b -> p (a b)')[:, 0:NS * 128].rearrange('p (b m) -> p b m', b=NS)
            Kbs = [FB_[3].rearrange('p a b -> p (a b)').rearrange('p (mc f) -> p mc f', mc=2),
                   pl[:].rearrange('p a b -> p (a b)')[:, 0:2048].rearrange('p (mc f) -> p mc f', mc=2)]
            Kkeys = ['F3', 'pl']; Ksems = [dsS[5], P.dma_sem()]
            prod = FB_[4].rearrange('p a b -> p (a b)')[:, 0:1024]
            Vbs = [HB_[0][:].rearrange('p a b -> p (a b)').rearrange('p (mc f) -> p mc f', mc=2),
                   HB_[2][:].rearrange('p a b -> p (a b)').rearrange('p (mc f) -> p mc f', mc=2)]
            Vkeys = ['H0', 'H2']; Vsems = [dsS[6], P.dma_sem()]
            ob = HB_[1]
            sc = st1[0][:, 0:128]; ex = st1[1][:, 0:128]; den = st1[2][:, 0:64]; pbf = st1[3][:, 0:NT].bitcast(BF16)[:, 0:128]

            def evq(m, pb):
                cp(qc[:, m, 0:n], ps[:, pb, 0:n], [('ps', pb)], ['F0'], eng='act')
            proj('xa_wq', D, hb, 'hb', n, evq)
            for g0 in range(0, 8, 4):
                pb = bank()
                for q in range(4):
                    tr(ps[0:n, pb, q * 128:(q + 1) * 128], qc[:, g0 + q, 0:n], ident[:, :], ['F0', 'ident'], [('ps', pb)])
                cp(qT[0:n, g0 * 128:(g0 + 4) * 128], ps[0:n, pb, :], [('ps', pb)], ['F1'], eng='act')
            cp(sel[0:n, :, :], ident[0:n, 0:n].unsqueeze(2).to_broadcast([n, n, 128]), ['ident'], ['F2'])
            for b_ in range(NS):
                Kb = Kbs[b_ % 2]; kkey = Kkeys[b_ % 2]
                dma('sp', Kb, di['cmk'][b_].rearrange('(mc p) f -> p mc f', p=128), [], [kkey], Ksems[b_ % 2])
                pq = [bank(), bank()]
                for hf in range(2):
                    mm(ps[:, pq[hf], :], sel[0:n, b_, :], qT[0:n, hf * 512:(hf + 1) * 512], True, True, ['F2', 'F1'], [('ps', pq[hf])])
                for mc in range(2):
                    for hf in range(2):
                        tt(prod[:, hf * 512:(hf + 1) * 512], Kb[:, mc, hf * 512:(hf + 1) * 512], ps[:, pq[hf], :], ALU.mult, [kkey, ('ps', pq[hf])], ['F4'])
                    P.op('dve', lambda e, b_=b_, mc=mc: e.tensor_reduce(out=sc[:, (b_ * 2 + mc) * 4:(b_ * 2 + mc) * 4 + 4], in_=prod.rearrange('p (h d) -> p h d', h=4), axis=AX.X, op=ALU.add),
                         reads=['F4'], writes=['st0'])
            act(ex, sc, AF.Exp, ['st0'], ['st1'], scale=1.0 / 16.0)
            dma('sp', ones32, di['c_all'][:, 128:256], [], ['st4'], dsS[2])
            pdn = bank()
            mm(ps[:, pdn, 0:128], ones32, ex, True, True, ['st4', 'st1'], [('ps', pdn)])
            d4 = ps[:, pdn, 0:128].rearrange('p (b mc h) -> p b mc h', mc=2, h=4)
            den3 = den.rearrange('p (b h) -> p b h', h=4)
            cp(den3, d4[:, :, 0, :], [('ps', pdn)], ['st2'])
            tt(den3, den3, d4[:, :, 1, :], ALU.add, ['st2', ('ps', pdn)], ['st2'])
            recip(den, den, ['st2'], ['st2'])
            tt(pbf.rearrange('p (b mc h) -> p b mc h', mc=2, h=4), ex.rearrange('p (b mc h) -> p b mc h', mc=2, h=4),
               den3.unsqueeze(2).to_broadcast([128, NS, 2, 4]), ALU.mult, ['st1', 'st2'], ['st3'])
            po = bank()
            for b_ in range(NS):
                Vb = Vbs[b_ % 2]; vkey = Vkeys[b_ % 2]
                dma('pool', Vb, di['cmv'][b_].rearrange('(mc p) f -> p mc f', p=128), [], [vkey], Vsems[b_ % 2])
                for c in range(8):
                    for mc in range(2):
                        col = (b_ * 2 + mc) * 4 + c // 2
                        mm(ps[:, po, c * NS + b_:c * NS + b_ + 1], Vb[:, mc, c * 128:(c + 1) * 128], pbf[:, col:col + 1], mc == 0, mc == 1, [vkey, 'st3'], [('ps', po)])
            cp(ob[:, :, 0:n], ps[:, po, 0:8 * NS].rearrange('p (c b) -> p c b', c=8), [('ps', po)], ['H1'], eng='act')
            resid_ln('xa_wo', ob, 'H1', 2, n)
            ffn('ffn2_wi', 'ffn2_wo', 3, n)
            store_fm_tokens(h32, 'h32', 0, n, di['ys'])

        if do_sample:
            sample_path()
        P.emit()
    return nc, P


_CACHE = {}


def _consts():
    a = np.arange(128) % 64
    b = np.arange(64)
    su = (a[:, None] < b[None, :]).astype(np.float32)
    ui = (a[:, None] <= b[None, :]).astype(np.float32)
    sl = (a[:, None] > b[None, :]).astype(np.float32)
    ey = (a[:, None] == b[None, :]).astype(np.float32)
    bd = np.zeros((128, 128), np.float32)
    bd[:64, :64] = 1.0
    bd[64:, 64:] = 1.0
    rs = np.ones((128, NT), np.float32)
    rs[:, ::C] = 0.0
    return {'c_all': np.ascontiguousarray(np.concatenate([np.eye(128, dtype=np.float32), np.ones((128, 128), np.float32), bd, su, ui, sl, ey, rs], axis=1))}


def make_in_maps(inputs):
    f = lambda a: np.ascontiguousarray(np.asarray(a, dtype=np.float32))
    shared = {}
    for nm in ['ffn1_wi', 'ffn1_wo', 'ffn2_wi', 'ffn2_wo', 'w_in', 'decay_w2', 'aaa_a2', 'gate_g2',
               'lru_wr', 'lru_wi', 'w_mix_out', 'xa_wq', 'xa_wk', 'xa_wv', 'xa_wo']:
        shared[nm] = np.ascontiguousarray(f(inputs[nm])[0])
    shared['prm'] = np.ascontiguousarray(np.concatenate(
        [f(inputs[nm])[0].reshape(-1, 128) for nm in ['ln_g', 'ln_b', 'shift_mu', 'conv_w'] + VEC_NAMES], axis=0))
    shared.update(_consts())
    maps = []
    for c in range(8):
        m = dict(shared)
        sl = slice(c * NS, (c + 1) * NS)
        m['xp'] = f(inputs['x_prompt'][c])
        m['mem'] = f(inputs['mem_prompt'][c])
        m['xs'] = f(inputs['x_sample'][sl, 0])
        m['cmk'] = f(inputs['cache_mem_k'][0, sl]).reshape(NS, NMEM, D)
        m['cmv'] = f(inputs['cache_mem_v'][0, sl]).reshape(NS, NMEM, D)
        m['srw'] = f(inputs['state_rwkv'][0, sl]).reshape(NS, D, 64)
        m['ssh'] = f(inputs['state_rwkv_shift'][0, sl])
        m['slru'] = f(inputs['state_lru'][0, sl])
        m['scv'] = f(inputs['state_conv'][0, sl])
        maps.append(m)
    return maps


def kernel(**inputs):
    if 'nc' not in _CACHE:
        _CACHE['nc'] = build()[0]
    nc = _CACHE['nc']
    maps = make_in_maps(inputs)
    res = run_bass_kernel_spmd(nc, maps, core_ids=list(range(8)))
    R = res.results
    cat = lambda k: np.stack([np.asarray(r[k], dtype=np.float32) for r in R])
    catc = lambda k: np.concatenate([np.asarray(r[k], dtype=np.float32) for r in R], axis=0)
    yp = cat('yp')
    ys = catc('ys').reshape(8 * NS, 1, D)
    pmk = cat('pmk').reshape(1, 8, NMEM, 4, 256)
    pmv = cat('pmv').reshape(1, 8, NMEM, 4, 256)
    prw = cat('prw').reshape(1, 8, 16, 64, 64)
    psh = cat('psh').reshape(1, 8, RP)
    plru = cat('plru').reshape(1, 8, D)
    pcv = cat('pcv').reshape(1, 8, 3, D)
    srw = catc('srw_o').reshape(1, 8 * NS, 16, 64, 64)
    ssh = catc('ssh_o').reshape(1, 8 * NS, RP)
    slru = catc('slru_o').reshape(1, 8 * NS, D)
    scv = catc('scv_o').reshape(1, 8 * NS, 3, D)
    return (yp, ys, pmk, pmv, prw, psh, plru, pcv, srw, ssh, slru, scv)
```

```python
import math
import os
import numpy as np
from contextlib import ExitStack
import concourse.bass as bass
import concourse.mybir as mybir
from concourse.bass_utils import run_bass_kernel_spmd

F32 = mybir.dt.float32
BF16 = mybir.dt.bfloat16
AF = mybir.ActivationFunctionType
ALU = mybir.AluOpType
AX = mybir.AxisListType

ENGS = ['pe', 'dve', 'act', 'pool', 'sp']

D = 1024
T = 2048
NT = 256
NTILES = T // NT
C = 64
NCH = NT // C
DFF = 2816
NJ = DFF // 128
RP = 3328
PW = 7424
NMEM = 256
NS = 16
ALPHA = 2.0 ** 0.25
LN_EPS = 1e-5
GN_EPS = 64e-5
C0 = math.exp(-0.5)


class DmaSem:
    def __init__(self, sem):
        self.sem = sem
        self.count = 0


class Prog:
    def __init__(self, nc, stack):
        self.nc = nc
        self.stack = stack
        self.ops = {e: [] for e in ENGS}
        self.last_w = {}
        self.readers = {}
        self.seen = {e: {} for e in ENGS}
        self.dsems = []
        self.out_tokens = []

    def dma_sem(self):
        s = DmaSem(self.stack.enter_context(self.nc.semaphore('dsem%d' % len(self.dsems))))
        self.dsems.append(s)
        return s

    def sbuf(self, name, shape, dt):
        return self.stack.enter_context(self.nc.sbuf_tensor(name, list(shape), dt))

    def psum(self, name, shape, dt):
        return self.stack.enter_context(self.nc.psum_tensor(name, list(shape), dt))

    def barrier(self, keys, engines=ENGS):
        if 'B' in os.environ.get('TOG', ''):
            return
        for e in engines:
            self.op(e, None, reads=keys, track=False)

    def op(self, eng, fn, reads=(), writes=(), dsem=None, is_out=False, track=True):
        isps = lambda k: isinstance(k, tuple) and k[0] == 'ps'
        writes = list(writes) + [k for k in reads if isps(k)]
        reads = [k for k in reads if not isps(k)]
        deps = []
        for k in reads:
            t = self.last_w.get(k)
            if t is not None:
                deps.append(t)
        for k in writes:
            t = self.last_w.get(k)
            if t is not None:
                deps.append(t)
            deps.extend(self.readers.get(k, {}).values())
        need = {}
        for t in deps:
            if t[0] == 'eng':
                if t[1] == eng and dsem is None and (eng == 'pe' or os.environ.get('NOSELF')):
                    continue
                key = ('eng', t[1])
            else:
                key = ('dma', id(t[1]))
            if need.get(key, (None, -1))[1] < t[2]:
                need[key] = (t[1], t[2])
        waits = []
        for key, (src, v) in need.items():
            if self.seen[eng].get(key, -1) >= v:
                continue
            self.seen[eng][key] = v
            waits.append((key[0], src, v))
        idx = len(self.ops[eng])
        self.ops[eng].append(dict(fn=fn, waits=waits, dsem=dsem, target=False, phase=getattr(self, 'phase', '')))
        if dsem is not None:
            dsem.count += 16
            tok = ('dma', dsem, dsem.count)
        else:
            tok = ('eng', eng, idx)
        for k in writes:
            self.last_w[k] = tok
            self.readers[k] = {}
        for k in (reads if track else ()):
            r = self.readers.setdefault(k, {})
            rk = (tok[0], tok[1] if tok[0] == 'eng' else id(tok[1]))
            if rk not in r or r[rk][2] < tok[2]:
                r[rk] = tok
        if is_out:
            self.out_tokens.append(tok)
        return tok

    def emit(self):
        nc = self.nc
        fin = {}
        for t in self.out_tokens:
            fin[id(t[1])] = (t[1], max(fin.get(id(t[1]), (None, 0))[1], t[2]))
        self.ops['sp'].append(dict(fn=None, waits=[('dma', s, v) for s, v in fin.values()], dsem=None, target=False))
        for e in ENGS:
            for o in self.ops[e]:
                for kind, src, v in o['waits']:
                    if kind == 'eng':
                        self.ops[src][v]['target'] = True
        semval = {}
        for e in ENGS:
            c = 0
            vals = []
            for o in self.ops[e]:
                if o['target']:
                    c += 1
                vals.append(c)
            semval[e] = vals
        esem = {e: self.stack.enter_context(nc.semaphore('esem_' + e)) for e in ENGS}
        handles = {'pe': 'tensor', 'dve': 'vector', 'act': 'scalar', 'pool': 'gpsimd', 'sp': 'sync'}
        with nc.Block() as block:
            def make(e):
                def body(eng):
                    for o in self.ops[e]:
                        for kind, src, v in o['waits']:
                            if kind == 'eng':
                                eng.wait_ge(esem[src], semval[src][v])
                            else:
                                eng.wait_ge(src.sem, v)
                        if o['fn'] is None:
                            continue
                        inst = o['fn'](eng)
                        if os.environ.get('ANNOT') and o.get('phase'):
                            inst.annotate(o['phase'])
                        if o['dsem'] is not None:
                            inst.then_inc(o['dsem'].sem, 16)
                        elif o['target']:
                            inst.then_inc(esem[e], 1)
                return body
            for e in ENGS:
                getattr(block, handles[e])(make(e))
        self.stats = {e: len(self.ops[e]) for e in ENGS}


VEC_NAMES = ['decay_w0', 'aaa_a0', 'k_k', 'k_a', 'r_k', 'gn_g', 'gn_b', 'conv_b', 'lru_br', 'lru_bi', 'lru_lambda']
W_NAMES = ['ffn1_wi', 'ffn1_wo', 'ffn2_wi', 'ffn2_wo', 'w_in', 'w_mix_out', 'xa_wq', 'xa_wk', 'xa_wv', 'xa_wo']


def build(dbg=None, do_sample=True, stage=99):
    nc = bass.Bass('TRN2', target_bir_lowering=False)
    di = {}

    DECL = os.environ.get('DECL')

    def din(name, shape):
        if DECL and name not in DECL.split(','):
            return None
        di[name] = nc.dram_tensor(name, list(shape), F32, kind='ExternalInput').ap()
        return di[name]

    def dout(name, shape):
        if DECL and name not in DECL.split(','):
            return None
        di[name] = nc.dram_tensor(name, list(shape), F32, kind='ExternalOutput').ap()
        return di[name]

    din('xp', [T, D]); din('mem', [NMEM, D])
    din('xs', [NS, D]); din('cmk', [NS, NMEM, D]); din('cmv', [NS, NMEM, D])
    din('srw', [NS, D, 64]); din('ssh', [NS, RP]); din('slru', [NS, D]); din('scv', [NS, 3, D])
    din('prm', [210, 128])
    din('ffn1_wi', [D, 2 * DFF]); din('ffn1_wo', [DFF, D]); din('ffn2_wi', [D, 2 * DFF]); din('ffn2_wo', [DFF, D])
    din('w_in', [D, PW])
    din('decay_w2', [64, D]); din('aaa_a2', [64, D]); din('gate_g2', [128, D])
    din('lru_wr', [16, 64, 64]); din('lru_wi', [16, 64, 64])
    for w in ['w_mix_out', 'xa_wq', 'xa_wk', 'xa_wv', 'xa_wo']:
        din(w, [D, D])
    din('c_all', [128, 640 + NT])
    dout('yp', [T, D]); dout('ys', [NS, D]); dout('pmk', [NMEM, D]); dout('pmv', [NMEM, D])
    dout('prw', [D, 64]); dout('psh', [RP]); dout('plru', [D]); dout('pcv', [3, D])
    dout('srw_o', [NS, D, 64]); dout('ssh_o', [NS, RP]); dout('slru_o', [NS, D]); dout('scv_o', [NS, 3, D])
    dbg = dbg or {}
    for k, shp in dbg.items():
        dout('dbg_' + k, shp)

    with ExitStack() as st:
        P = Prog(nc, st)
        n = NT
        ident = P.sbuf('ident', [128, 128], F32)
        identb = P.sbuf('identb', [128, 128], BF16)
        onesb = P.sbuf('onesb', [128, 128], BF16)
        bdb = P.sbuf('bdb', [128, 128], BF16)
        mask = P.sbuf('mask', [128, 256], F32)
        reset = P.sbuf('reset', [128, NT], F32)
        lng = P.sbuf('lng', [128, 4, 8], F32); lnb = P.sbuf('lnb', [128, 4, 8], F32)
        mu = P.sbuf('mu', [128, 26], F32); omu = P.sbuf('omu', [128, 26], F32)
        vec = {v: P.sbuf('v_' + v, [128, 8], F32) for v in VEC_NAMES}
        oka = P.sbuf('oka', [128, 8], F32)
        lsp = P.sbuf('lsp', [128, 8], F32)
        cw = P.sbuf('cw', [128, 4, 8], F32)
        w2a2 = P.sbuf('w2a2', [128, D], BF16)
        g2 = P.sbuf('g2', [128, D], BF16)
        wrbd = P.sbuf('wrbd', [128, 8, 128], BF16); wibd = P.sbuf('wibd', [128, 8, 128], BF16)
        WCAP = 4096
        NWB = 3
        wbuf = [P.sbuf('wbuf%d' % i, [128, WCAP], BF16) for i in range(NWB)]
        wsem = [P.dma_sem() for i in range(NWB)]
        h32 = P.sbuf('h32', [128, 8, n], F32)
        hb = P.sbuf('hb', [128, 8, n], BF16)
        FB_ = [P.sbuf('F%d' % i, [128, 8, n], F32) for i in range(5)]
        HX = P.sbuf('HX', [128, 24, n], BF16)
        HB_ = [P.sbuf('H%d' % i, [128, 8, n], BF16) for i in range(4)]
        memT = HB_[3]
        pl = P.sbuf('pl', [128, 8, n + 3], F32)
        st1 = [P.sbuf('st%d' % i, [128, n], F32) for i in range(5)]
        st2 = [P.sbuf('su%d' % i, [128, n], F32) for i in range(5)]
        stsel = lambda i: ((st1, 'st') if i % 2 == 0 else (st2, 'su'))
        carry_sh = P.sbuf('carry_sh', [128, 26], F32)
        carry_h = P.sbuf('carry_h', [128, 8], F32)
        A32 = P.sbuf('A32', [128, 8, 64], F32)
        A0b = [P.sbuf('A0b%d' % i, [128, 8, 64], BF16) for i in range(2)]
        RHSb = P.sbuf('RHSb', [128, 8, 64], BF16)
        Ub = P.sbuf('Ub', [128, 8, 64], BF16)
        WCs = P.sbuf('WCs', [128, 8, NCH], F32)
        X32 = [P.sbuf('X32_%d' % i, [128, 8, 64], F32) for i in range(2)]
        Xb = [P.sbuf('Xb_%d' % i, [128, 8, 64], BF16) for i in range(2)]
        PP = [[P.sbuf('PP_%d_%d' % (i, j), [128, 8, 128], BF16) for j in range(2)] for i in range(2)]
        LP = P.sbuf('LP', [128, 8, NCH, 128], BF16)
        PN = P.sbuf('PN', [128, 8, NCH, 128], BF16)
        XT = P.sbuf('XT', [128, 8, NCH, 64], BF16)
        mkT = P.sbuf('mkT', [128, 8, NMEM], BF16)
        mvb = P.sbuf('mvb', [128, 2, D], BF16)
        tok32 = [P.sbuf('tok32_%d' % i, [128, D], F32) for i in range(2)]
        osml = P.sbuf('osml', [128, 8, 8], F32)
        osh = P.sbuf('osh', [128, 26], F32)
        sgb = [P.sbuf('sgb%d' % i, [128, NT], F32) for i in range(2)]
        ps = P.psum('ps', [128, 8, 512], F32)
        dsem_c = [P.dma_sem() for i in range(4)]
        dsem_in = [P.dma_sem() for i in range(2)]
        dsem_out = [P.dma_sem() for i in range(2)]
        dsem_misc = P.dma_sem()

        bank_ctr = [0]
        tokctr = [0]

        def bank():
            b = bank_ctr[0] % 8
            bank_ctr[0] += 1
            return b

        def mm(out, lhsT, rhs, start, stop, reads, writes):
            P.op('pe', lambda e: e.matmul(out, lhsT=lhsT, rhs=rhs, start=start, stop=stop), reads=reads, writes=writes)

        def tr(out, in_, idn, reads, writes):
            P.op('pe', lambda e: e.transpose(out, in_, idn), reads=reads, writes=writes)

        def act(out, in_, func, reads, writes, bias=None, scale=None):
            kw = {}
            if bias is not None:
                kw['bias'] = bias
            if scale is not None:
                kw['scale'] = scale
            P.op('act', lambda e: e.activation(out=out, in_=in_, func=func, **kw), reads=reads, writes=writes)

        def tt(out, in0, in1, op, reads, writes, eng='dve'):
            P.op(eng, lambda e: e.tensor_tensor(out=out, in0=in0, in1=in1, op=op), reads=reads, writes=writes)

        def ts(out, in0, s1, s2, op0, op1, reads, writes, eng='dve'):
            if s2 is None:
                P.op(eng, lambda e: e.tensor_scalar(out=out, in0=in0, scalar1=s1, scalar2=None, op0=op0), reads=reads, writes=writes)
            else:
                P.op(eng, lambda e: e.tensor_scalar(out=out, in0=in0, scalar1=s1, scalar2=s2, op0=op0, op1=op1), reads=reads, writes=writes)

        def stt(out, in0, scalar, in1, op0, op1, reads, writes):
            P.op('dve', lambda e: e.scalar_tensor_tensor(out=out, in0=in0, scalar=scalar, in1=in1, op0=op0, op1=op1), reads=reads, writes=writes)

        def cp(out, in_, reads, writes, eng='dve'):
            if eng == 'act':
                act(out, in_, AF.Copy, reads, writes)
            else:
                P.op(eng, lambda e: e.tensor_copy(out=out, in_=in_), reads=reads, writes=writes)

        def recip(out, in_, reads, writes):
            P.op('dve', lambda e: e.reciprocal(out=out, in_=in_), reads=reads, writes=writes)

        def dma(eng, out, in_, reads, writes, dsem, is_out=False, **kw):
            P.op(eng, lambda e: e.dma_start(out=out, in_=in_, **kw), reads=reads, writes=writes, dsem=dsem, is_out=is_out)

        def interleave2(ga, gb):
            a_ok = b_ok = True
            while a_ok or b_ok:
                if a_ok:
                    try:
                        next(ga)
                    except StopIteration:
                        a_ok = False
                if b_ok:
                    try:
                        next(gb)
                    except StopIteration:
                        b_ok = False

        def dump(name, src_ap, key):
            if name in dbg:
                dma('sp', di['dbg_' + name], src_ap, [key], [], P.dma_sem(), is_out=True)

        PARTS = os.environ.get('PARTS', 'abcdefg')
        dma('sp', ident[:], di['c_all'][:, 0:128], [], ['ident'], dsem_c[0])
        dma('sp', mask[:], di['c_all'][:, 384:640], [], ['mask'], dsem_c[0])
        dma('sp', reset[:], di['c_all'][:, 640:640 + NT], [], ['reset'], dsem_c[0])
        P.barrier(['ident', 'mask', 'reset'])
        if 'b' in PARTS:
            dma('pool', identb[:], di['c_all'][:, 0:128], [], ['identb'], dsem_c[1])
            dma('pool', onesb[:], di['c_all'][:, 128:256], [], ['onesb'], dsem_c[1])
            dma('pool', bdb[:], di['c_all'][:, 256:384], [], ['bdb'], dsem_c[1])
            dma('pool', w2a2[0:64, :], di['decay_w2'], [], ['w2a2'], dsem_c[1])
            dma('pool', w2a2[64:128, :], di['aaa_a2'], [], ['w2a2'], dsem_c[1])
            dma('pool', g2[:], di['gate_g2'], [], ['g2'], dsem_c[1])
        if 'm' in PARTS or PARTS == 'abcdefg':
            P.op('dve', lambda e: e.memset(wrbd[:], 0.0), writes=['wrbd'])
            P.op('dve', lambda e: e.memset(wibd[:], 0.0), writes=['wibd'])
        for (wt, nm, key) in ([(wrbd, 'lru_wr', 'wrbd'), (wibd, 'lru_wi', 'wibd')] if 'c' in PARTS else []):
            src = di[nm].rearrange('(c two) i o -> two i c o', two=2)
            for par in range(2):
                dma('pool', wt[par * 64:(par + 1) * 64, :, par * 64:(par + 1) * 64], src[par], [], [key], dsem_c[1])
        P.barrier(['identb', 'onesb', 'bdb', 'w2a2', 'g2', 'wrbd', 'wibd'])
        prm = [tok32[0], tok32[1]]
        rows = []
        rows.append((lng[:].rearrange('p l c -> p (l c)'), di['prm'][0:32, :], 'lng'))
        rows.append((lnb[:].rearrange('p l c -> p (l c)'), di['prm'][32:64, :], 'lnb'))
        rows.append((mu[:], di['prm'][64:90, :], 'mu'))
        rows.append((cw[:].rearrange('p l c -> p (l c)'), di['prm'][90:122, :], 'cw'))
        for vi_, v in enumerate(VEC_NAMES):
            rows.append((vec[v][:], di['prm'][122 + 8 * vi_:130 + 8 * vi_, :], 'v_' + v))
        groups = [[]]
        cnt = 0
        for r_ in rows:
            k_ = r_[1].shape[0]
            if cnt + k_ > 128:
                groups.append([]); cnt = 0
            groups[-1].append((cnt, k_) + r_)
            cnt += k_
        for gi_, grp in enumerate(groups if 'd' in PARTS else []):
            tk = prm[gi_ % 2]
            tot = 0
            for (o_, k_, dst, src, key) in grp:
                dma('sp', tk[o_:o_ + k_, 0:128], src, [], [('tok32', gi_ % 2)], dsem_c[2 + gi_ % 2])
                tot = o_ + k_
            pb = bank()
            tr(ps[:, pb, 0:tot], tk[0:tot, 0:128], ident[0:tot, 0:tot], [('tok32', gi_ % 2), 'ident'], [('ps', pb)])
            for (o_, k_, dst, src, key) in grp:
                cp(dst, ps[:, pb, o_:o_ + k_], [('ps', pb)], [key])
        if 'e' in PARTS:
            ts(omu[:], mu[:], -1.0, 1.0, ALU.mult, ALU.add, ['mu'], ['omu'])
            ts(oka[:], vec['k_a'][:], -1.0, 1.0, ALU.mult, ALU.add, ['v_k_a'], ['oka'])
            act(lsp[:], vec['lru_lambda'][:], AF.Exp, ['v_lru_lambda'], ['lsp'], scale=-1.0)
            act(lsp[:], lsp[:], AF.Ln, ['lsp'], ['lsp'], bias=1.0)
            ts(lsp[:], lsp[:], -8.0, None, ALU.mult, None, ['lsp'], ['lsp'])

        def wblocks():
            def ffn_blocks(wi, wo):
                for g in range(11):
                    def f(buf, g=g, wi=wi):
                        v = buf[:, 0:4096].rearrange('p (k c) -> p k c', k=8)
                        return [(v[:, :, 0:256], di[wi][:, g * 256:(g + 1) * 256].rearrange('(k p) c -> p k c', p=128)),
                                (v[:, :, 256:512], di[wi][:, DFF + g * 256:DFF + (g + 1) * 256].rearrange('(k p) c -> p k c', p=128))]
                    yield ((wi, g), f)
                for mp in range(8):
                    def f(buf, mp=mp, wo=wo):
                        v = buf[:, 0:NJ * 128].rearrange('p (j c) -> p j c', j=NJ)
                        return [(v, di[wo][:, mp * 128:(mp + 1) * 128].rearrange('(j p) c -> p j c', p=128))]
                    yield ((wo, mp), f)

            def sq_blocks(w, ncols):
                nb = (ncols + 511) // 512
                for b in range(nb):
                    c0 = b * 512
                    cn = min(512, ncols - c0)
                    def f(buf, c0=c0, cn=cn, w=w):
                        v = buf[:, 0:8 * cn].rearrange('p (k c) -> p k c', k=8)
                        return [(v, di[w][:, c0:c0 + cn].rearrange('(k p) c -> p k c', p=128))]
                    yield ((w, b), f)
            yield from sq_blocks('xa_wk', D)
            yield from sq_blocks('xa_wv', D)
            def one_pass():
                yield from ffn_blocks('ffn1_wi', 'ffn1_wo')
                yield from sq_blocks('w_in', PW)
                yield from sq_blocks('w_mix_out', D)
                yield from sq_blocks('xa_wq', D)
                yield from sq_blocks('xa_wo', D)
                yield from ffn_blocks('ffn2_wi', 'ffn2_wo')
            for it in range(int(os.environ.get('NTI', NTILES)) + (1 if do_sample else 0)):
                for blk, (tag, f) in enumerate(one_pass()):
                    yield (tag, f, it, blk)

        wgen = wblocks()
        wstate = dict(issued=0, consumed=0, pending=[])

        NBLK = 59
        wsc = nc.dram_tensor('wsc', [NBLK, 128, WCAP], BF16, kind='Internal').ap()
        wbsem = [P.dma_sem() for i in range(NWB)]

        def w_used(tag):
            if tag[0].endswith('_wi'):
                return 4096
            if tag[0].endswith('_wo') and tag[0].startswith('ffn'):
                return NJ * 128
            ncols = PW if tag[0] == 'w_in' else D
            return 8 * min(512, ncols - tag[1] * 512)

        def w_issue():
            try:
                item = next(wgen)
            except StopIteration:
                return False
            i = wstate['issued'] % NWB
            if len(item) == 2:
                tag, f = item
                for (dst, src) in f(wbuf[i]):
                    dma('pool', dst, src, [], [('wbuf', i)], wsem[i])
            else:
                tag, f, it, blk = item
                used = w_used(tag)
                if it == 0:
                    for (dst, src) in f(wbuf[i]):
                        dma('pool', dst, src, [], [('wbuf', i)], wsem[i])
                    dma('sp', wsc[blk, :, 0:used], wbuf[i][:, 0:used], [('wbuf', i)], [('wsc', blk)], wbsem[i])
                else:
                    dma('pool', wbuf[i][:, 0:used], wsc[blk, :, 0:used], [('wsc', blk)], [('wbuf', i)], wsem[i])
            wstate['pending'].append((tag, i))
            wstate['issued'] += 1
            return True

        def w_next(tag):
            while wstate['issued'] - wstate['consumed'] < NWB:
                if not w_issue():
                    break
            t, i = wstate['pending'].pop(0)
            assert t == tag, (t, tag)
            wstate['consumed'] += 1
            return wbuf[i], ('wbuf', i)

        def sqview(buf, cn):
            return buf[:, 0:8 * cn].rearrange('p (k c) -> p k c', k=8)

        def load_tokens_fm(src_rows_ap, nrows, dst32, dstb, col0, dkey):
            tokctr[0] += 1
            i = tokctr[0] % 2 if 'A' in os.environ.get('TOG', 'A') else 0
            tk = tok32[i]
            dma('sp', tk[0:nrows, :], src_rows_ap, [], [('tok32', i)], dsem_in[i])
            for half in range(2):
                b = bank()
                for q in range(4):
                    kc = half * 4 + q
                    P.op('pe', lambda e, b=b, q=q, kc=kc: e.transpose(ps[:, b, q * 128:q * 128 + nrows], tk[0:nrows, kc * 128:(kc + 1) * 128], ident[0:nrows, 0:nrows]),
                         reads=[('tok32', i), 'ident'], writes=[('ps', b)], track=('W' not in os.environ.get('TOG', '')))
                src = ps[:, b, :].rearrange('p (q t) -> p q t', q=4)[:, :, 0:nrows]
                if dst32 is not None:
                    cp(dst32[:, half * 4:half * 4 + 4, col0:col0 + nrows], src, [('ps', b), ('tok32', i)], [dkey], eng='act')
                if dstb is not None:
                    cp(dstb[:, half * 4:half * 4 + 4, col0:col0 + nrows], src, [('ps', b)], ['hb' if dkey == 'h32' else 'H3'], eng='dve')

        def store_fm_tokens(src32, skey, col0, nrows, dst_rows_ap, nch=8, feat0=0):
            i = bank_ctr[0] % 2
            tk = tok32[i]
            for g0 in range(0, nch, 4):
                b = bank()
                gn = min(4, nch - g0)
                for q in range(gn):
                    tr(ps[0:nrows, b, q * 128:(q + 1) * 128], src32[:, g0 + q, col0:col0 + nrows], ident[:, :],
                       [skey, 'ident'], [('ps', b)])
                cp(tk[0:nrows, g0 * 128:(g0 + gn) * 128], ps[0:nrows, b, 0:gn * 128], [('ps', b)], [('tok32', i)], eng='act')
            dma('sp', dst_rows_ap, tk[0:nrows, 0:nch * 128], [('tok32', i)], [], dsem_out[i], is_out=True)

        def layernorm(idx, n, eps):
            P.phase = 'layernorm'
            zsq = HB_[0]
            cp(hb[:, :, 0:n], h32[:, :, 0:n], ['h32'], ['hb'], eng='act')
            act(zsq[:, :, 0:n], h32[:, :, 0:n], AF.Square, ['h32'], ['H0'])
            b1 = bank(); b2 = bank()
            for kc in range(8):
                mm(ps[:, b1, 0:n], onesb[:], hb[:, kc, 0:n], kc == 0, kc == 7, ['onesb', 'hb'], [('ps', b1)])
            for kc in range(8):
                mm(ps[:, b2, 0:n], onesb[:], zsq[:, kc, 0:n], kc == 0, kc == 7, ['onesb', 'H0'], [('ps', b2)])
            mean, msq, var, rstd, nmr = [s[:, 0:n] for s in st1]
            ts(mean, ps[:, b1, 0:n], 1.0 / D, None, ALU.mult, None, [('ps', b1)], ['st0'])
            tt(msq, mean, mean, ALU.mult, ['st0'], ['st1'])
            stt(var, ps[:, b2, 0:n], 1.0 / D, msq, ALU.mult, ALU.subtract, [('ps', b2), 'st1'], ['st2'])
            ts(var, var, 0.0, eps, ALU.max, ALU.add, ['st2'], ['st2'])
            act(var, var, AF.Sqrt, ['st2'], ['st2'])
            recip(rstd, var, ['st2'], ['st3'])
            tt(nmr, mean, rstd, ALU.mult, ['st0', 'st3'], ['st4'])
            tt(h32[:, :, 0:n], h32[:, :, 0:n], rstd.unsqueeze(1).to_broadcast([128, 8, n]), ALU.mult, ['h32', 'st3'], ['h32'])
            tt(h32[:, :, 0:n], h32[:, :, 0:n], nmr.unsqueeze(1).to_broadcast([128, 8, n]), ALU.subtract, ['h32', 'st4'], ['h32'])
            for kc in range(8):
                act(h32[:, kc, 0:n], h32[:, kc, 0:n], AF.Identity, ['h32', 'lng', 'lnb'], ['h32'],
                    bias=lnb[:, idx, kc:kc + 1], scale=lng[:, idx, kc:kc + 1])
            cp(hb[:, :, 0:n], h32[:, :, 0:n], ['h32'], ['hb'], eng='act')

        def ffn(wi, wo, ln_idx, n):
            P.phase = 'ffn'
            actb = HX
            sg = [sgb[0][:, 0:n], sgb[1][:, 0:n]]
            for g in range(11):
                wb, wk = w_next((wi, g))
                wv = sqview(wb, 512)
                for jj in range(2):
                    j = 2 * g + jj
                    pg = bank(); pu = bank()
                    for kc in range(8):
                        mm(ps[:, pg, 0:n], wv[:, kc, jj * 128:(jj + 1) * 128], hb[:, kc, 0:n], kc == 0, kc == 7, [wk, 'hb'], [('ps', pg)])
                    for kc in range(8):
                        mm(ps[:, pu, 0:n], wv[:, kc, 256 + jj * 128:256 + (jj + 1) * 128], hb[:, kc, 0:n], kc == 0, kc == 7, [wk, 'hb'], [('ps', pu)])
                    act(sg[jj], ps[:, pg, 0:n], AF.Silu, [('ps', pg)], [('sg', jj)])
                    tt(actb[:, j, 0:n], sg[jj], ps[:, pu, 0:n], ALU.mult, [('sg', jj), ('ps', pu)], ['HX%d' % (j // 8)])
            for m in range(8):
                wb, wk = w_next((wo, m))
                wv = wb[:, 0:NJ * 128].rearrange('p (j c) -> p j c', j=NJ)
                po = bank()
                for j in range(NJ):
                    mm(ps[:, po, 0:n], wv[:, j, :], actb[:, j, 0:n], j == 0, j == NJ - 1, [wk, 'HX%d' % (j // 8)], [('ps', po)])
                stt(h32[:, m, 0:n], ps[:, po, 0:n], 0.5 / ALPHA, h32[:, m, 0:n], ALU.mult, ALU.add, [('ps', po), 'h32'], ['h32'])
            layernorm(ln_idx, n, LN_EPS / (ALPHA * ALPHA))

        def proj(w, ncols, xin, xkey, n, evac):
            nb = (ncols + 511) // 512
            for b in range(nb):
                c0 = b * 512
                cn = min(512, ncols - c0)
                wb, wk = w_next((w, b))
                wv = sqview(wb, cn)
                for q in range(cn // 128):
                    m = c0 // 128 + q
                    pb = bank()
                    for kc in range(8):
                        mm(ps[:, pb, 0:n], wv[:, kc, q * 128:(q + 1) * 128], xin[:, kc, 0:n], kc == 0, kc == 7, [wk, xkey], [('ps', pb)])
                    evac(m, pb)

        def mem_kv():
            P.phase = 'mem_kv'
            for r in range(2):
                load_tokens_fm(di['mem'][r * 128:(r + 1) * 128, :], 128, None, memT, r * 128, 'memT')
            for (w, outname, isk) in [('xa_wk', 'pmk', True), ('xa_wv', 'pmv', False)]:
                for b in range(2):
                    wb, wk = w_next((w, b))
                    wv = sqview(wb, 512)
                    for r in range(2):
                        pb = bank()
                        for kc in range(8):
                            mm(ps[:, pb, :], memT[:, kc, r * 128:(r + 1) * 128], wv[:, kc, :], kc == 0, kc == 7, ['H3', wk], [('ps', pb)])
                        i = bank_ctr[0] % 2
                        cp(tok32[i][:, 0:512], ps[:, pb, :], [('ps', pb)], [('tok32', i)], eng='act')
                        if not isk:
                            cp(mvb[:, r, b * 512:(b + 1) * 512], ps[:, pb, :], [('ps', pb)], ['mvb'], eng='dve')
                        dma('sp', di[outname][r * 128:(r + 1) * 128, b * 512:(b + 1) * 512], tok32[i][:, 0:512], [('tok32', i)], [], dsem_out[i], is_out=True)
                    if isk:
                        for q in range(4):
                            m = b * 4 + q
                            pb = bank()
                            for kc in range(8):
                                mm(ps[:, pb, 0:NMEM], wv[:, kc, q * 128:(q + 1) * 128], memT[:, kc, :], kc == 0, kc == 7, [wk, 'H3'], [('ps', pb)])
                            cp(mkT[:, m, :], ps[:, pb, 0:NMEM], [('ps', pb)], ['mkT'], eng='act')

        def mixer_prompt(ti):
            P.phase = 'mixer_prompt'
            n = NT
            r32, k32, v32, a32, ls32 = FB_
            glb = HB_[1]; geb = HB_[2]; g0b = HB_[3]
            g1b = HX[:, 0:8, :]; xcb = HX[:, 8:16, :]
            first = (ti == 0)
            tmp = st1[0]

            def evac(m, pb):
                psn = ps[:, pb, 0:n]
                SS, KP = stsel(m)
                tmp = SS[0]
                if m < 26:
                    act(tmp[:, 1:n], ps[:, pb, 0:n - 1], AF.Copy, [('ps', pb), 'mu'], [KP + '0'], scale=mu[:, m:m + 1])
                    if first:
                        P.op('dve', lambda e, tmp=tmp: e.memset(tmp[:, 0:1], 0.0), writes=[KP + '0'])
                    else:
                        tt(tmp[:, 0:1], carry_sh[:, m:m + 1], mu[:, m:m + 1], ALU.mult, ['carry_sh', 'mu'], [KP + '0'])
                    cp(carry_sh[:, m:m + 1], ps[:, pb, n - 1:n], [('ps', pb)], ['carry_sh'])
                    if m < 24:
                        dst = [r32, k32, v32][m // 8][:, m % 8, :]
                        dkey = ['F0', 'F1', 'F2'][m // 8]
                        stt(dst, psn, omu[:, m:m + 1], tmp[:, 0:n], ALU.mult, ALU.add, [('ps', pb), 'omu', KP + '0'], [dkey])
                    else:
                        xs_ = SS[1][:, 0:n]
                        stt(xs_, psn, omu[:, m:m + 1], tmp[:, 0:n], ALU.mult, ALU.add, [('ps', pb), 'omu', KP + '0'], [KP + '1'])
                        lb = SS[2][:, 0:n].bitcast(BF16)[:, 0:n]
                        if m == 24:
                            act(lb[0:64, :], xs_[0:64, :], AF.Tanh, [KP + '1'], [KP + '2'])
                            cp(lb[64:128, :], xs_[64:128, :], [KP + '1'], [KP + '2'])
                            for (lo, dstt, dk, bvec) in [(0, ls32, 'F4', 'decay_w0'), (64, a32, 'F3', 'aaa_a0')]:
                                for q in range(8):
                                    p2 = bank()
                                    mm(ps[:, p2, 0:n], w2a2[lo:lo + 64, q * 128:(q + 1) * 128], lb[lo:lo + 64, :], True, True, ['w2a2', KP + '2'], [('ps', p2)])
                                    act(dstt[:, q, :], ps[:, p2, 0:n], AF.Sigmoid, [('ps', p2), 'v_' + bvec], [dk], bias=vec[bvec][:, q:q + 1])
                        else:
                            act(lb, xs_, AF.Sigmoid, [KP + '1'], [KP + '2'])
                            for q in range(8):
                                p2 = bank()
                                mm(ps[:, p2, 0:n], g2[:, q * 128:(q + 1) * 128], lb, True, True, ['g2', KP + '2'], [('ps', p2)])
                                cp(glb[:, q, :], ps[:, p2, 0:n], [('ps', p2)], ['H1'], eng='act')
                elif m < 34:
                    cp(pl[:, m - 26, 3:3 + n], psn, [('ps', pb)], ['pl'], eng='act')
                elif m < 42:
                    act(geb[:, m - 34, :], psn, AF.Gelu, [('ps', pb)], ['H2'])
                elif m < 50:
                    act(g0b[:, m - 42, :], psn, AF.Sigmoid, [('ps', pb)], ['H3'])
                else:
                    act(g1b[:, m - 50, :], psn, AF.Sigmoid, [('ps', pb)], ['HX0'])

            if first:
                P.op('dve', lambda e: e.memset(pl[:, :, 0:3], 0.0), writes=['pl'])
            else:
                cp(pl[:, :, 0:3], osml[:, :, 4:7], ['osml'], ['pl'])
            proj('w_in', PW, hb, 'hb', n, evac)
            cp(osml[:, :, 4:7], pl[:, :, n:n + 3], ['pl'], ['osml'])
            dump('r32', r32[:], 'F0'); dump('k32', k32[:], 'F1'); dump('v32', v32[:], 'F2'); dump('a32', a32[:], 'F3'); dump('ls32', ls32[:], 'F4')
            if ti == int(os.environ.get('NTI', NTILES)) - 1:
                cp(osml[:, :, 0:3], pl[:, :, n:n + 3], ['pl'], ['osml'])
                cp(osh[:], carry_sh[:], ['carry_sh'], ['osh'])
            return dict(r32=r32, k32=k32, v32=v32, a32=a32, ls32=ls32, glb=glb, geb=geb, g0b=g0b, g1b=g1b, xcb=xcb)

        def lru_prompt(ti, B):
            P.phase = 'lru_prompt'
            n = NT
            geb, g1b, xcb = B['geb'], B['g1b'], B['xcb']
            lmb = HX[:, 16:24, :]
            def lru_chunk(c):
                SS, KP = stsel(c)
                xc = SS[0][:, 0:n]; gr = SS[1][:, 0:n]; gi = SS[2][:, 0:n]; t3 = SS[3][:, 0:n]; hs = SS[4][:, 0:n]
                act(xc, pl[:, c, 3:3 + n], AF.Identity, ['pl', 'cw', 'v_conv_b'], [KP + '0'], bias=vec['conv_b'][:, c:c + 1], scale=cw[:, 3, c:c + 1])
                for j in range(3):
                    stt(xc, pl[:, c, j:j + n], cw[:, j, c:c + 1], xc, ALU.mult, ALU.add, ['pl', 'cw', KP + '0'], [KP + '0'])
                yield
                cp(xcb[:, c, :], xc, [KP + '0'], ['HX1'], eng='act')
                p1 = bank(); p2 = bank()
                yield
                mm(ps[:, p1, 0:n], wrbd[:, c, :], xcb[:, c, :], True, True, ['wrbd', 'HX1'], [('ps', p1)])
                yield
                mm(ps[:, p2, 0:n], wibd[:, c, :], xcb[:, c, :], True, True, ['wibd', 'HX1'], [('ps', p2)])
                yield
                act(gr, ps[:, p1, 0:n], AF.Sigmoid, [('ps', p1), 'v_lru_br'], [KP + '1'], bias=vec['lru_br'][:, c:c + 1])
                yield
                act(gi, ps[:, p2, 0:n], AF.Sigmoid, [('ps', p2), 'v_lru_bi'], [KP + '2'], bias=vec['lru_bi'][:, c:c + 1])
                yield
                act(gr, gr, AF.Exp, [KP + '1', 'lsp'], [KP + '1'], scale=lsp[:, c:c + 1])
                yield
                act(t3, gr, AF.Square, [KP + '1'], [KP + '3'])
                yield
                ts(t3, t3, -1.0, 1.0, ALU.mult, ALU.add, [KP + '3'], [KP + '3'])
                yield
                ts(t3, t3, 0.0, None, ALU.max, None, [KP + '3'], [KP + '3'])
                yield
                act(t3, t3, AF.Sqrt, [KP + '3'], [KP + '3'])
                yield
                tt(gi, gi, xc, ALU.mult, [KP + '2', KP + '0'], [KP + '2'])
                yield
                tt(gi, gi, t3, ALU.mult, [KP + '2', KP + '3'], [KP + '2'])
                yield
                if ti == 0:
                    P.op('dve', lambda e, hs=hs, gr=gr, gi=gi: e.tensor_tensor_scan(out=hs, data0=gr, data1=gi, initial=0.0, op0=ALU.mult, op1=ALU.add),
                         reads=[KP + '1', KP + '2'], writes=[KP + '4'])
                else:
                    P.op('dve', lambda e, c=c, hs=hs, gr=gr, gi=gi: e.tensor_tensor_scan(out=hs, data0=gr, data1=gi, initial=carry_h[:, c:c + 1], op0=ALU.mult, op1=ALU.add),
                         reads=[KP + '1', KP + '2', ('carry_h', c)], writes=[KP + '4'])
                yield
                cp(carry_h[:, c:c + 1], hs[:, n - 1:n], [KP + '4'], [('carry_h', c)])
                yield
                tt(t3, hs, geb[:, c, :], ALU.mult, [KP + '4', 'H2'], [KP + '3'])
                yield
                tt(lmb[:, c, :], t3, g1b[:, c, :], ALU.mult, [KP + '3', 'HX0'], ['HX2'])
            for _c0 in range(0, 8, 2):
                interleave2(lru_chunk(_c0), lru_chunk(_c0 + 1))
            if ti == int(os.environ.get('NTI', NTILES)) - 1:
                cp(osml[:, :, 3:4], carry_h[:].unsqueeze(2), [('carry_h', c_) for c_ in range(8)], ['osml'])
            return lmb


        plb = pl[:].rearrange('p a b -> p (a b)').bitcast(BF16)
        plA = plb[:, 0:8 * NT].rearrange('p (a b) -> p a b', a=8)
        plB = plb[:, 8 * NT:16 * NT].rearrange('p (a b) -> p a b', a=8)

        def rwkv_core(B, state_in, state_out, skip_inverse=False):
            P.phase = 'rwkv_core'
            n = NT
            r32, k32, v32, a32, ls32 = B['r32'], B['k32'], B['v32'], B['a32'], B['ls32']
            bc8 = lambda v: v[:].unsqueeze(2).to_broadcast([128, 8, n])
            QR = HX[:, 0:16, :].rearrange('p (k two) (c t) -> p k two c t', two=2, t=C)
            KT = HB_[0]; NB = hb
            kk32 = pl[:, :, 0:n]
            tt(kk32, k32[:], bc8(vec['k_k']), ALU.mult, ['F1', 'v_k_k'], ['pl'])
            act(KT[:], kk32, AF.Square, ['pl'], ['H0'])
            rkb = HB_[2]

            def prep_chunk(kc):
                SS, KP = stsel(kc)
                s_ = SS[0][:, 0:n]; u_ = SS[1][:, 0:n]; u2 = SS[2][:, 0:n]
                pb = bank()
                mm(ps[:, pb, 0:n], bdb[:], KT[:, kc, :], True, True, ['bdb', 'H0'], [('ps', pb)])
                act(s_, ps[:, pb, 0:n], AF.Sqrt, [('ps', pb)], [KP + '0'])
                act(u_, a32[:, kc, :], AF.Identity, ['F3', 'v_k_a', 'oka'], [KP + '1'], bias=oka[:, kc:kc + 1], scale=vec['k_a'][:, kc:kc + 1])
                yield
                ts(s_, s_, 1e-12, None, ALU.max, None, [KP + '0'], [KP + '0'])
                yield
                recip(s_, s_, [KP + '0'], [KP + '0'])
                yield
                tt(kk32[:, kc, :], kk32[:, kc, :], s_, ALU.mult, ['pl', KP + '0'], ['pl'])
                yield
                tt(k32[:, kc, :], k32[:, kc, :], u_, ALU.mult, ['F1', KP + '1'], ['F1'])
                yield
                tt(a32[:, kc, :], a32[:, kc, :], kk32[:, kc, :], ALU.mult, ['F3', 'pl'], ['F3'])
                yield
                tt(u2, r32[:, kc, :], k32[:, kc, :], ALU.mult, ['F0', 'F1'], [KP + '2'])
                yield
                act(rkb[:, kc, :], u2, AF.Copy, [KP + '2', 'v_r_k'], ['H2'], scale=vec['r_k'][:, kc:kc + 1])
            for _c0 in range(0, 8, 2):
                interleave2(prep_chunk(_c0), prep_chunk(_c0 + 1))
            P.phase = 'rw_decay'
            def decay_chunk(kc):
                SS, KP = stsel(kc)
                cs = SS[0][:, 0:n]; dd = SS[1][:, 0:n]; Wi = SS[2][:, 0:n]; We = SS[3][:, 0:n]; Wv = SS[4][:, 0:n]
                P.op('dve', lambda e, kc=kc, cs=cs: e.tensor_tensor_scan(out=cs, data0=reset[:, 0:n], data1=ls32[:, kc, :], initial=0.0, op0=ALU.mult, op1=ALU.add),
                     reads=['reset', 'F4'], writes=[KP + '0'])
                yield
                tt(dd, cs, ls32[:, kc, :], ALU.subtract, [KP + '0', 'F4'], [KP + '1'])
                yield
                act(Wi, cs, AF.Exp, [KP + '0'], [KP + '2'], scale=-C0)
                yield
                act(We, dd, AF.Exp, [KP + '1'], [KP + '3'], scale=-C0)
                yield
                act(Wv, cs, AF.Exp, [KP + '0'], [KP + '4'], scale=C0)
                c4 = lambda a: a.rearrange('p (c t) -> p c t', t=C)
                yield
                tt(QR[:, kc, 1, :, :], c4(r32[:, kc, :]), c4(Wi), ALU.mult, ['F0', KP + '2'], ['HX0', 'HX1'])
                yield
                tt(QR[:, kc, 0, :, :], c4(kk32[:, kc, :]), c4(We), ALU.mult, ['pl', KP + '3'], ['HX0', 'HX1'])
                yield
                cp(WCs[:, kc, :], c4(Wi)[:, :, C - 1], [KP + '2'], ['WCs'])
                yield
                tt(Wi, k32[:, kc, :], Wv, ALU.mult, ['F1', KP + '4', KP + '2'], [KP + '2'])
                yield
                stt(We, a32[:, kc, :], -1.0, Wv, ALU.mult, ALU.mult, ['F3', KP + '4', KP + '3'], [KP + '3'])
                yield
                cp(NB[:, kc, :], We, [KP + '3'], ['hb'], eng='act')
                yield
                cp(KT[:, kc, :], Wi, [KP + '2'], ['H0'], eng='act')
            for _c0 in range(0, 8, 2):
                interleave2(decay_chunk(_c0), decay_chunk(_c0 + 1))
            P.phase = 'rw_tok'
            vb = plB
            cp(vb[:, :, 0:n], v32[:], ['F2'], ['pl'], eng='act')
            for kc in range(8):
                pb = bank()
                mm(ps[:, pb, 0:n], bdb[:], rkb[:, kc, :], True, True, ['bdb', 'H2'], [('ps', pb)])
                tt(v32[:, kc, :], v32[:, kc, :], ps[:, pb, 0:n], ALU.mult, ['F2', ('ps', pb)], ['F2'])
            tmv = lambda a: a.rearrange('p a b -> p (a b)').rearrange('p (c x) -> p c x', c=NCH)
            vT = tmv(HB_[2][:]); kTt = tmv(plA); nbT = tmv(plB)

            def to_tokmajor(src, skey, dst, dkey):
                for c in range(NCH):
                    pb = bank()
                    pbv = ps[:, pb, :].bitcast(BF16)
                    for hp in range(8):
                        for par in range(2):
                            lo = par * 64
                            tr(pbv[lo:lo + 64, hp * 64:(hp + 1) * 64], src[lo:lo + 64, hp, c * C:(c + 1) * C], identb[lo:lo + 64, lo:lo + 64],
                               [skey, 'identb'], [('ps', pb)])
                    cp(dst[:, c, :], pbv[:, 0:512], [('ps', pb)], [dkey], eng=('act' if c % 2 else 'dve'))
            to_tokmajor(vb, 'pl', vT, 'H2')
            to_tokmajor(KT, 'H0', kTt, 'pl')
            to_tokmajor(NB, 'hb', nbT, 'pl')
            vTv = lambda c, hp, lo: vT[lo:lo + 64, c, hp * 64:(hp + 1) * 64]
            kTv = lambda c, hp, lo: kTt[lo:lo + 64, c, hp * 64:(hp + 1) * 64]
            nTv = lambda c, hp, lo: nbT[lo:lo + 64, c, hp * 64:(hp + 1) * 64]
            m_su_ui = mask[:, 0:128].unsqueeze(1).to_broadcast([128, 8, 128])
            m_sl = mask[:, 128:192].unsqueeze(1).to_broadcast([128, 8, 64])
            m_eye = mask[:, 192:256].unsqueeze(1).to_broadcast([128, 8, 64])
            def ph1(c0):
                P.phase = 'rw_ph1'
                ctx = []
                for s in range(2):
                    c = c0 + s
                    b1a = bank(); b1b = bank()
                    for hp in range(8):
                        for par in range(2):
                            lo = par * 64
                            bsel = b1a if hp < 4 else b1b
                            mm(ps[lo:lo + 64, bsel, (hp % 4) * 128:(hp % 4 + 1) * 128], KT[lo:lo + 64, hp, c * C:(c + 1) * C],
                               QR[lo:lo + 64, hp, :, c, :], True, True, ['H0', 'HX0', 'HX1'], [('ps', bsel)])
                    for (bsel, h0) in [(b1a, 0), (b1b, 4)]:
                        tt(LP[:, h0:h0 + 4, c, :], ps[:, bsel, :].rearrange('p (h x) -> p h x', h=4), m_su_ui[:, 0:4, :], ALU.mult,
                           [('ps', bsel), 'mask'], [('LP', c)])
                    b2a = bank(); b2b = bank()
                    for hp in range(8):
                        for par in range(2):
                            lo = par * 64
                            bsel = b2a if hp < 4 else b2b
                            mm(ps[lo:lo + 64, bsel, (hp % 4) * 128:(hp % 4 + 1) * 128], NB[lo:lo + 64, hp, c * C:(c + 1) * C],
                               QR[lo:lo + 64, hp, :, c, :], True, True, ['hb', 'HX0', 'HX1'], [('ps', bsel)])
                    for (bsel, h0) in [(b2a, 0), (b2b, 4)]:
                        tt(PN[:, h0:h0 + 4, c, :], ps[:, bsel, :].rearrange('p (h x) -> p h x', h=4), m_su_ui[:, 0:4, :], ALU.mult,
                           [('ps', bsel), 'mask'], [('PN', c)])
                    b3 = bank()
                    for hp in range(8):
                        for par in range(2):
                            lo = par * 64
                            mm(ps[lo:lo + 64, b3, hp * 64:(hp + 1) * 64], QR[lo:lo + 64, hp, 0, c, :], NB[lo:lo + 64, hp, c * C:(c + 1) * C],
                               True, True, ['HX0', 'HX1', 'hb'], [('ps', b3)])
                    pp = PP[s][0]
                    cp(pp[:, :, 0:64], PN[:, :, c, 0:64], [('PN', c)], [('PP', s, 0)], eng='act')
                    tt(pp[:, :, 64:128], ps[:, b3, :].rearrange('p (h x) -> p h x', h=8), m_sl, ALU.mult, [('ps', b3), 'mask'], [('PP', s, 0)])
                    tt(X32[s][:], PN[:, :, c, 0:64], m_eye, ALU.add, [('PN', c), 'mask'], [('X32', s)])
                    cp(Xb[s][:], X32[s][:], [('X32', s)], [('Xb', s)], eng='act')
                    ctx.append(c)
                if skip_inverse:
                    for s_ in range(2):
                        cp(XT[:, :, ctx[s_], :], X32[s_][:], [('X32', s_)], [('XT', ctx[s_])], eng='act')
                for lvl in ([] if skip_inverse else range(1, 6)):
                    yield
                    P.phase = 'rw_ph1'
                    cur = (lvl - 1) % 2; nxt = lvl % 2
                    banks = []
                    for s in range(2):
                        ba = bank(); bb = bank()
                        src = PP[s][cur]
                        for hp in range(8):
                            for par in range(2):
                                lo = par * 64
                                bsel = ba if hp < 4 else bb
                                o0 = (hp % 4) * 128
                                mm(ps[lo:lo + 64, bsel, o0:o0 + 64], src[lo:lo + 64, hp, 64:128], src[lo:lo + 64, hp, 0:64], True, True,
                                   [('PP', s, cur)], [('ps', bsel)])
                                mm(ps[lo:lo + 64, bsel, o0 + 64:o0 + 128], src[lo:lo + 64, hp, 0:64], src[lo:lo + 64, hp, 64:128], True, True,
                                   [('PP', s, cur)], [('ps', bsel)])
                        banks.append((ba, bb))
                    for s in range(2):
                        ba, bb = banks[s]
                        dst = PP[s][nxt]
                        cp(dst[:, 0:4, :], ps[:, ba, :].rearrange('p (h x) -> p h x', h=4), [('ps', ba)], [('PP', s, nxt)], eng='act')
                        cp(dst[:, 4:8, :], ps[:, bb, :].rearrange('p (h x) -> p h x', h=4), [('ps', bb)], [('PP', s, nxt)], eng='dve')
                    xb_ = []
                    for s in range(2):
                        bx = bank()
                        src = PP[s][nxt]
                        for hp in range(8):
                            for par in range(2):
                                lo = par * 64
                                mm(ps[lo:lo + 64, bx, hp * 64:(hp + 1) * 64], src[lo:lo + 64, hp, 64:128], Xb[s][lo:lo + 64, hp, :], True, True,
                                   [('PP', s, nxt), ('Xb', s)], [('ps', bx)])
                        xb_.append(bx)
                    for s in range(2):
                        bx = xb_[s]
                        tt(X32[s][:], X32[s][:], ps[:, bx, :].rearrange('p (h x) -> p h x', h=8), ALU.add, [('X32', s), ('ps', bx)], [('X32', s)])
                        if lvl < 5:
                            cp(Xb[s][:], X32[s][:], [('X32', s)], [('Xb', s)], eng='act')
                        else:
                            cp(XT[:, :, ctx[s], :], X32[s][:], [('X32', s)], [('XT', ctx[s])], eng='act')
            Y32 = FB_[0]

            def ph2(c):
                P.phase = 'rw_ph2'
                state_in(c)
                cur = c % 2
                a0 = A0b[cur]
                bR = bank()
                for hp in range(8):
                    for par in range(2):
                        lo = par * 64
                        o = ps[lo:lo + 64, bR, hp * 64:(hp + 1) * 64]
                        mm(o, QR[lo:lo + 64, hp, 0, c, :], a0[lo:lo + 64, hp, :], True, False, ['HX0', 'HX1', ('A0b', cur)], [('ps', bR)])
                        mm(o, LP[lo:lo + 64, hp, c, 0:64], vTv(c, hp, lo), False, True, [('LP', c), 'H2'], [('ps', bR)])
                cp(RHSb[:], ps[:, bR, :].rearrange('p (h x) -> p h x', h=8), [('ps', bR)], ['RHSb'], eng='act')
                yield
                P.phase = 'rw_ph2'
                bU = bank()
                for hp in range(8):
                    for par in range(2):
                        lo = par * 64
                        mm(ps[lo:lo + 64, bU, hp * 64:(hp + 1) * 64], XT[lo:lo + 64, hp, c, :], RHSb[lo:lo + 64, hp, :], True, True,
                           [('XT', c), 'RHSb'], [('ps', bU)])
                cp(Ub[:], ps[:, bU, :].rearrange('p (h x) -> p h x', h=8), [('ps', bU)], ['Ub'], eng='act')
                yield
                P.phase = 'rw_ph2'
                bD = bank()
                for hp in range(8):
                    for par in range(2):
                        lo = par * 64
                        o = ps[lo:lo + 64, bD, hp * 64:(hp + 1) * 64]
                        mm(o, kTv(c, hp, lo), vTv(c, hp, lo), True, False, ['pl', 'H2'], [('ps', bD)])
                        mm(o, nTv(c, hp, lo), Ub[lo:lo + 64, hp, :], False, True, ['pl', 'Ub'], [('ps', bD)])
                bY = bank()
                for hp in range(8):
                    for par in range(2):
                        lo = par * 64
                        o = ps[lo:lo + 64, bY, hp * 64:(hp + 1) * 64]
                        mm(o, a0[lo:lo + 64, hp, :], QR[lo:lo + 64, hp, 1, c, :], True, False, [('A0b', cur), 'HX0', 'HX1'], [('ps', bY)])
                        mm(o, vTv(c, hp, lo), LP[lo:lo + 64, hp, c, 64:128], False, False, ['H2', ('LP', c)], [('ps', bY)])
                        mm(o, Ub[lo:lo + 64, hp, :], PN[lo:lo + 64, hp, c, 64:128], False, True, ['Ub', ('PN', c)], [('ps', bY)])
                tt(A32[:], A32[:], ps[:, bD, :].rearrange('p (h x) -> p h x', h=8), ALU.add, ['A32', ('ps', bD)], ['A32'])
                tt(A32[:], A32[:], WCs[:, :, c:c + 1].to_broadcast([128, 8, 64]), ALU.mult, ['A32', 'WCs'], ['A32'])
                cp(A0b[1 - cur][:], A32[:], ['A32'], [('A0b', 1 - cur)], eng='act')
                cp(Y32[:, :, c * C:(c + 1) * C], ps[:, bY, :].rearrange('p (h x) -> p h x', h=8), [('ps', bY)], ['F0'], eng='dve')
                state_out(c)
                yield

            def drain(g):
                for _ in g:
                    pass

            def interleave(ga, gb):
                a_ok = b_ok = True
                while a_ok or b_ok:
                    if a_ok:
                        try:
                            next(ga)
                        except StopIteration:
                            a_ok = False
                    if b_ok:
                        try:
                            next(gb)
                        except StopIteration:
                            b_ok = False

            def chain(*gs):
                for g in gs:
                    yield from g
            drain(ph1(0))
            if NCH == 4:
                interleave(ph1(2), chain(ph2(0), ph2(1)))
                drain(chain(ph2(2), ph2(3)))
            else:
                for c0 in range(2, NCH, 2):
                    drain(ph1(c0))
                for c in range(NCH):
                    drain(ph2(c))
            return Y32, v32

        def rwkv_post(Y, Ykey, bonus, bkey, glb, g0b, lmb, mb, n):
            P.phase = 'rwkv_post'
            Yb = HB_[0]; ysq = hb
            cp(Yb[:, :, 0:n], Y[:, :, 0:n], [Ykey], ['H0'], eng='act')
            act(ysq[:, :, 0:n], Y[:, :, 0:n], AF.Square, [Ykey], ['hb'])
            def post_chunk(kc):
                b1 = bank(); b2 = bank()
                mm(ps[:, b1, 0:n], bdb[:], Yb[:, kc, 0:n], True, True, ['bdb', 'H0'], [('ps', b1)])
                yield
                mm(ps[:, b2, 0:n], bdb[:], ysq[:, kc, 0:n], True, True, ['bdb', 'hb'], [('ps', b2)])
                SS, KP = stsel(kc)
                mean = SS[0][:, 0:n]; var = SS[1][:, 0:n]; t_ = SS[2][:, 0:n]
                yield
                ts(mean, ps[:, b1, 0:n], 1.0 / 64, None, ALU.mult, None, [('ps', b1)], [KP + '0'])
                yield
                tt(var, mean, mean, ALU.mult, [KP + '0'], [KP + '1'])
                yield
                stt(var, ps[:, b2, 0:n], 1.0 / 64, var, ALU.mult, ALU.subtract, [('ps', b2), KP + '1'], [KP + '1'])
                yield
                ts(var, var, 0.0, GN_EPS, ALU.max, ALU.add, [KP + '1'], [KP + '1'])
                yield
                act(var, var, AF.Sqrt, [KP + '1'], [KP + '1'])
                yield
                recip(var, var, [KP + '1'], [KP + '1'])
                yield
                tt(t_, Y[:, kc, 0:n], mean, ALU.subtract, [Ykey, KP + '0'], [KP + '2'])
                yield
                tt(t_, t_, var, ALU.mult, [KP + '2', KP + '1'], [KP + '2'])
                yield
                act(t_, t_, AF.Identity, [KP + '2', 'v_gn_g', 'v_gn_b'], [KP + '2'], bias=vec['gn_b'][:, kc:kc + 1], scale=vec['gn_g'][:, kc:kc + 1])
                yield
                tt(t_, t_, bonus[:, kc, 0:n], ALU.add, [KP + '2', bkey], [KP + '2'])
                yield
                tt(t_, t_, glb[:, kc, 0:n], ALU.mult, [KP + '2', 'H1'], [KP + '2'])
                yield
                tt(t_, t_, g0b[:, kc, 0:n], ALU.mult, [KP + '2', 'H3'], [KP + '2'])
                yield
                tt(mb[:, kc, 0:n], t_, lmb[:, kc, 0:n], ALU.add, [KP + '2', 'HX2'], ['H2'])
            for _c0 in range(0, 8, 2):
                interleave2(post_chunk(_c0), post_chunk(_c0 + 1))

        def resid_ln(w, xin, xkey, ln_idx, n):
            def evac(m, pb):
                stt(h32[:, m, 0:n], ps[:, pb, 0:n], 1.0 / ALPHA, h32[:, m, 0:n], ALU.mult, ALU.add, [('ps', pb), 'h32'], ['h32'])
            proj(w, D, xin, xkey, n, evac)
            layernorm(ln_idx, n, LN_EPS / (ALPHA * ALPHA))

        def xattn_prompt(n):
            P.phase = 'xattn_prompt'
            qb = HB_[0]; ob = HB_[1]; pT = HX[:, 0:8, :]
            def evq(m, pb):
                cp(qb[:, m, 0:n], ps[:, pb, 0:n], [('ps', pb)], ['H0'], eng='act')
            proj('xa_wq', D, hb, 'hb', n, evq)
            for h in range(4):
                for mc in range(2):
                    pb = bank()
                    for dc in range(2):
                        mm(ps[:, pb, 0:n], mkT[:, 2 * h + dc, mc * 128:(mc + 1) * 128], qb[:, 2 * h + dc, 0:n], dc == 0, dc == 1, ['mkT', 'H0'], [('ps', pb)])
                    act(pT[:, 2 * h + mc, 0:n], ps[:, pb, 0:n], AF.Exp, [('ps', pb)], ['HX0'], scale=1.0 / 16.0)
            rds = []
            for h in range(4):
                pd = bank()
                for mc in range(2):
                    mm(ps[:, pd, 0:n], onesb[:], pT[:, 2 * h + mc, 0:n], mc == 0, mc == 1, ['onesb', 'HX0'], [('ps', pd)])
                SS, KP = stsel(h)
                rd = SS[h // 2][:, 0:n]; rk_ = KP + str(h // 2)
                recip(rd, ps[:, pd, 0:n], [('ps', pd)], [rk_])
                rds.append((rd, rk_))
            for h in range(4):
                rd, rk_ = rds[h]
                for dc in range(2):
                    po = bank()
                    for mc in range(2):
                        mm(ps[:, po, 0:n], mvb[:, mc, (2 * h + dc) * 128:(2 * h + dc + 1) * 128], pT[:, 2 * h + mc, 0:n], mc == 0, mc == 1, ['mvb', 'HX0'], [('ps', po)])
                    tt(ob[:, 2 * h + dc, 0:n], ps[:, po, 0:n], rd, ALU.mult, [('ps', po), rk_], ['H1'])
            resid_ln('xa_wo', ob, 'H1', 2, n)

        if stage >= 1:
            mem_kv()
        if 'm' in PARTS or PARTS == 'abcdefg':
            P.op('dve', lambda e: e.memset(A32[:], 0.0), writes=['A32'])
            P.op('dve', lambda e: e.memset(A0b[0][:], 0.0), writes=[('A0b', 0)])
        for ti in range(int(os.environ.get('NTI', NTILES)) if stage >= 9 else 1):
            TOG = os.environ.get('TOG', '')
            for r in range(1 if '1' in TOG else NT // 128):
                load_tokens_fm(di['xp'][ti * NT + r * 128: ti * NT + (r + 1) * 128, :], 128, h32, None if 'D' in TOG else hb, r * 128, 'h32')
            if stage >= 2:
                ffn('ffn1_wi', 'ffn1_wo', 0, NT)
            if ti == 0:
                dump('h1', h32[:], 'h32')
            if stage < 3:
                break
            B = mixer_prompt(ti)
            if stage < 4:
                break
            lmb = lru_prompt(ti, B)
            if ti == 0:
                dump('lm', lmb, 'HX2')
            if stage < 5:
                break
            Y32, bonus = rwkv_core(B, lambda c: None, lambda c: None)
            if ti == 0:
                dump('Y', Y32[:], 'F0')
            if stage < 6:
                break
            mb = HB_[2]
            rwkv_post(Y32, 'F0', bonus, 'F2', B['glb'], B['g0b'], lmb, mb, NT)
            resid_ln('w_mix_out', mb, 'H2', 1, NT)
            if ti == 0:
                dump('h2', h32[:], 'h32')
            if stage < 7:
                break
            xattn_prompt(NT)
            if ti == 0:
                dump('h3', h32[:], 'h32')
            if stage < 8:
                break
            ffn('ffn2_wi', 'ffn2_wo', 3, NT)
            for r in range(NT // 128):
                store_fm_tokens(h32, 'h32', r * 128, 128, di['yp'][ti * NT + r * 128: ti * NT + (r + 1) * 128, :])
        if stage < 9:
            P.emit()
            return nc, P
        def prompt_outputs():
            pass
            Sout = FB_[1].rearrange('p a b -> p (a b)')[:, 0:1024].rearrange('p (hp par k) -> p hp par k', hp=8, par=2)
            for g in range(2):
                pb = bank()
                for q in range(4):
                    hp = g * 4 + q
                    tr(ps[0:64, pb, q * 128:(q + 1) * 128], A32[:, hp, :], ident[:, :], ['A32', 'ident'], [('ps', pb)])
                cp(Sout[0:64, g * 4:(g + 1) * 4, :, :], ps[0:64, pb, :].rearrange('p (q par k) -> p q par k', q=4, par=2), [('ps', pb)], ['F1'], eng='act')
            dma('sp', di['prw'].rearrange('(hp par v) k -> v hp par k', hp=8, par=2), Sout[0:64, :, :, :], ['F1'], [], dsem_misc, is_out=True)
            osm2 = FB_[2]
            cp(osm2[:, 0:8, 0:3], osml[:, :, 0:3], ['osml'], ['F2'])
            cp(osm2[:, 0:8, 3:4], osml[:, :, 3:4], ['osml'], ['F2'])
            store_fm_tokens(osm2, 'F2', 0, 3, di['pcv'][:, :])
            store_fm_tokens(osm2, 'F2', 3, 1, di['plru'].rearrange('(o d) -> o d', o=1))
            osh3 = FB_[3]
            cp(osh3[:, 0:8, 0:1], osh[:, 0:8].unsqueeze(2), ['osh'], ['F3'])
            cp(osh3[:, 0:8, 1:2], osh[:, 8:16].unsqueeze(2), ['osh'], ['F3'])
            cp(osh3[:, 0:8, 2:3], osh[:, 16:24].unsqueeze(2), ['osh'], ['F3'])
            cp(osh3[:, 0:2, 3:4], osh[:, 24:26].unsqueeze(2), ['osh'], ['F3'])
            pshv = di['psh'].rearrange('(o d) -> o d', o=1)
            for q in range(3):
                store_fm_tokens(osh3, 'F3', q, 1, pshv[:, q * 1024:(q + 1) * 1024])
            store_fm_tokens(osh3, 'F3', 3, 1, pshv[:, 3072:3328], nch=2)


        if os.environ.get('NTI') != '0':
            prompt_outputs()
        def sample_path():
            P.phase = 'sample_path'
            n = NS
            sm = lambda nm, shp, dt=F32: P.sbuf(nm, shp, dt)
            rc = sm('s_rc', [128, 8, n]); kc_ = sm('s_kc', [128, 8, n]); vc = sm('s_vc', [128, 8, n]); ac = sm('s_ac', [128, 8, n]); lsc = sm('s_lsc', [128, 8, n])
            yc = sm('s_yc', [128, 8, n]); bonc = sm('s_bonc', [128, 8, n]); plc = sm('s_plc', [128, 8, n]); hsc = sm('s_hsc', [128, 8, n])
            prevS = sm('s_prev', [128, 26, n]); praw = sm('s_praw', [128, 26, n]); h0S = sm('s_h0', [128, 8, n]); scvT = sm('s_scvT', [128, 8, 3 * n])
            BD = tok32[1][:].rearrange('p (h x) -> p h x', h=8); ones32 = st1[4][:, 0:128]
            glb = HB_[1]; geb = HB_[2]; g0b = HB_[3]; g1b = HX[:, 0:8, :]; lmb = HX[:, 16:24, :]
            dsS = [P.dma_sem() for _ in range(7)]

            def load_fm(src_rows_ap, nrows, nchunks, dst, dkey, tki, sem):
                tk = tok32[tki]
                dma('sp', tk[0:nrows, 0:nchunks * 128], src_rows_ap, [], [('tok32', tki)], sem)
                for g0 in range(0, nchunks, 4):
                    gn = min(4, nchunks - g0)
                    b = bank()
                    for q in range(gn):
                        tr(ps[:, b, q * 128:q * 128 + nrows], tk[0:nrows, (g0 + q) * 128:(g0 + q + 1) * 128], ident[0:nrows, 0:nrows],
                           [('tok32', tki), 'ident'], [('ps', b)])
                    cp(dst[:, g0:g0 + gn, 0:nrows], ps[:, b, :].rearrange('p (q t) -> p q t', q=4)[:, 0:gn, 0:nrows], [('ps', b)], [dkey], eng='act')

            load_tokens_fm(di['xs'], n, h32, hb, 0, 'h32')
            for q in range(4):
                c0 = q * 8; cn = min(8, 26 - c0)
                tmpd = FB_[0] if q % 2 == 0 else FB_[1]
                load_fm(di['ssh'][:, c0 * 128:(c0 + cn) * 128], n, cn, tmpd, 'F%d' % (q % 2), q % 2, dsS[q % 2])
                cp(prevS[:, c0:c0 + cn, :], tmpd[:, 0:cn, 0:n], ['F%d' % (q % 2)], ['s_prev'])
            load_fm(di['slru'], n, 8, h0S, 's_h0', 0, dsS[0])
            load_fm(di['scv'].rearrange('b j d -> (b j) d'), 3 * n, 8, scvT, 's_scvT', 1, dsS[1])
            dma('sp', di['scv_o'][:, 0:2, :], di['scv'][:, 1:3, :], [], [], P.dma_sem(), is_out=True)

            ffn('ffn1_wi', 'ffn1_wo', 0, n)

            tmp = st1[0]

            def evac(m, pb):
                psn = ps[:, pb, 0:n]
                if m < 26:
                    cp(praw[:, m, :], psn, [('ps', pb)], ['s_praw'], eng='act')
                    ts(tmp[:, 0:n], prevS[:, m, :], mu[:, m:m + 1], None, ALU.mult, None, ['s_prev', 'mu'], ['st0'])
                    if m < 24:
                        dst = [rc, kc_, vc][m // 8][:, m % 8, :]
                        dkey = ['s_rc', 's_kc', 's_vc'][m // 8]
                        stt(dst, psn, omu[:, m:m + 1], tmp[:, 0:n], ALU.mult, ALU.add, [('ps', pb), 'omu', 'st0'], [dkey])
                    else:
                        xs_ = st1[1][:, 0:n]
                        stt(xs_, psn, omu[:, m:m + 1], tmp[:, 0:n], ALU.mult, ALU.add, [('ps', pb), 'omu', 'st0'], ['st1'])
                        lb = st1[2][:, 0:NT].bitcast(BF16)[:, 0:n]
                        if m == 24:
                            act(lb[0:64, :], xs_[0:64, :], AF.Tanh, ['st1'], ['st2'])
                            cp(lb[64:128, :], xs_[64:128, :], ['st1'], ['st2'])
                            for (lo, dstt, dk, bvec) in [(0, lsc, 's_lsc', 'decay_w0'), (64, ac, 's_ac', 'aaa_a0')]:
                                for q in range(8):
                                    p2 = bank()
                                    mm(ps[:, p2, 0:n], w2a2[lo:lo + 64, q * 128:(q + 1) * 128], lb[lo:lo + 64, :], True, True, ['w2a2', 'st2'], [('ps', p2)])
                                    act(dstt[:, q, :], ps[:, p2, 0:n], AF.Sigmoid, [('ps', p2), 'v_' + bvec], [dk], bias=vec[bvec][:, q:q + 1])
                        else:
                            act(lb, xs_, AF.Sigmoid, ['st1'], ['st2'])
                            for q in range(8):
                                p2 = bank()
                                mm(ps[:, p2, 0:n], g2[:, q * 128:(q + 1) * 128], lb, True, True, ['g2', 'st2'], [('ps', p2)])
                                cp(glb[:, q, 0:n], ps[:, p2, 0:n], [('ps', p2)], ['H1'], eng='act')
                elif m < 34:
                    cp(plc[:, m - 26, :], psn, [('ps', pb)], ['s_plc'], eng='act')
                elif m < 42:
                    act(geb[:, m - 34, 0:n], psn, AF.Gelu, [('ps', pb)], ['H2'])
                elif m < 50:
                    act(g0b[:, m - 42, 0:n], psn, AF.Sigmoid, [('ps', pb)], ['H3'])
                else:
                    act(g1b[:, m - 50, 0:n], psn, AF.Sigmoid, [('ps', pb)], ['HX0'])
            proj('w_in', PW, hb, 'hb', n, evac)
            for q in range(4):
                c0 = q * 8; cn = min(8, 26 - c0)
                store_fm_tokens(praw[:, c0:c0 + cn, :], 's_praw', 0, n, di['ssh_o'][:, c0 * 128:(c0 + cn) * 128], nch=cn)
            store_fm_tokens(plc, 's_plc', 0, n, di['scv_o'][:, 2, :])

            sc3 = scvT[:].rearrange('p c (b j) -> p c b j', j=3)
            xc = st1[0][:, 0:n]; gr = st1[1][:, 0:n]; gi = st1[2][:, 0:n]; t3 = st1[3][:, 0:n]; hs = st1[4][:, 0:n]
            xcb = HX[:, 8:16, :]
            for c in range(8):
                act(xc, plc[:, c, :], AF.Identity, ['s_plc', 'cw', 'v_conv_b'], ['st0'], bias=vec['conv_b'][:, c:c + 1], scale=cw[:, 3, c:c + 1])
                for j in range(3):
                    stt(xc, sc3[:, c, :, j], cw[:, j, c:c + 1], xc, ALU.mult, ALU.add, ['s_scvT', 'cw', 'st0'], ['st0'])
                cp(xcb[:, c, 0:n], xc, ['st0'], ['HX1'], eng='act')
                p1 = bank(); p2 = bank()
                mm(ps[:, p1, 0:n], wrbd[:, c, :], xcb[:, c, 0:n], True, True, ['wrbd', 'HX1'], [('ps', p1)])
                mm(ps[:, p2, 0:n], wibd[:, c, :], xcb[:, c, 0:n], True, True, ['wibd', 'HX1'], [('ps', p2)])
                act(gr, ps[:, p1, 0:n], AF.Sigmoid, [('ps', p1), 'v_lru_br'], ['st1'], bias=vec['lru_br'][:, c:c + 1])
                act(gi, ps[:, p2, 0:n], AF.Sigmoid, [('ps', p2), 'v_lru_bi'], ['st2'], bias=vec['lru_bi'][:, c:c + 1])
                act(gr, gr, AF.Exp, ['st1', 'lsp'], ['st1'], scale=lsp[:, c:c + 1])
                act(t3, gr, AF.Square, ['st1'], ['st3'])
                ts(t3, t3, -1.0, 1.0, ALU.mult, ALU.add, ['st3'], ['st3'])
                ts(t3, t3, 0.0, None, ALU.max, None, ['st3'], ['st3'])
                act(t3, t3, AF.Sqrt, ['st3'], ['st3'])
                tt(gi, gi, xc, ALU.mult, ['st2', 'st0'], ['st2'])
                tt(gi, gi, t3, ALU.mult, ['st2', 'st3'], ['st2'])
                tt(hs, gr, h0S[:, c, :], ALU.mult, ['st1', 's_h0'], ['st4'])
                tt(hsc[:, c, :], hs, gi, ALU.add, ['st4', 'st2'], ['s_hsc'])
                tt(t3, hsc[:, c, :], geb[:, c, 0:n], ALU.mult, ['s_hsc', 'H2'], ['st3'])
                tt(lmb[:, c, 0:n], t3, g1b[:, c, 0:n], ALU.mult, ['st3', 'HX0'], ['HX2'])
            store_fm_tokens(hsc, 's_hsc', 0, n, di['slru_o'])

            Sout = FB_[1].rearrange('p a b -> p (a b)')[:, 0:1024].rearrange('p (hp par k) -> p hp par k', hp=8, par=2)
            P.op('dve', lambda e: e.memset(tok32[1][:], 0.0), writes=[('tok32', 1)])
            for g in range(NS // NCH):
                Bp = dict(r32=FB_[0], k32=FB_[1], v32=FB_[2], a32=FB_[3], ls32=FB_[4])
                for (dstF, fk, srcc, sk) in [(FB_[0], 'F0', rc, 's_rc'), (FB_[1], 'F1', kc_, 's_kc'), (FB_[2], 'F2', vc, 's_vc'), (FB_[3], 'F3', ac, 's_ac'), (FB_[4], 'F4', lsc, 's_lsc')]:
                    P.op('dve', lambda e, dstF=dstF: e.memset(dstF[:], 0.0), writes=[fk])
                    cp(dstF[:].rearrange('p k (c t) -> p k c t', t=C)[:, :, :, 0], srcc[:, :, g * NCH:(g + 1) * NCH], [sk], [fk])

                def state_in(c, g=g):
                    b_ = g * NCH + c
                    src = di['srw'][b_].rearrange('(hp par v) k -> par v hp k', hp=8, par=2)
                    for par in range(2):
                        dma('sp', BD[par * 64:(par + 1) * 64, :, par * 64:(par + 1) * 64], src[par], [], [('tok32', 1)], dsS[3])
                    pb = bank()
                    for hp in range(8):
                        mm(ps[:, pb, hp * 64:(hp + 1) * 64], BD[:, hp, :], mask[:, 192:256], True, True, [('tok32', 1), 'mask'], [('ps', pb)])
                    v3 = ps[:, pb, :].rearrange('p (h x) -> p h x', h=8)
                    cp(A32[:], v3, [('ps', pb)], ['A32'], eng='dve')
                    cp(A0b[c % 2][:], v3, [('ps', pb)], [('A0b', c % 2)], eng='act')

                def state_out(c, g=g):
                    b_ = g * NCH + c
                    for gg in range(2):
                        pb = bank()
                        for q in range(4):
                            hp = gg * 4 + q
                            tr(ps[0:64, pb, q * 128:(q + 1) * 128], A32[:, hp, :], ident[:, :], ['A32', 'ident'], [('ps', pb)])
                        cp(Sout[0:64, gg * 4:(gg + 1) * 4, :, :], ps[0:64, pb, :].rearrange('p (q par k) -> p q par k', q=4, par=2), [('ps', pb)], ['F1'], eng='act')
                    dma('sp', di['srw_o'][b_].rearrange('(hp par v) k -> v hp par k', hp=8, par=2), Sout[0:64, :, :, :], ['F1'], [], dsS[4], is_out=True)
                Y32, bon = rwkv_core(Bp, state_in, state_out, skip_inverse=True)
                cp(yc[:, :, g * NCH:(g + 1) * NCH], Y32[:].rearrange('p k (c t) -> p k c t', t=C)[:, :, :, 0], ['F0'], ['s_yc'])
                cp(bonc[:, :, g * NCH:(g + 1) * NCH], bon[:].rearrange('p k (c t) -> p k c t', t=C)[:, :, :, 0], ['F2'], ['s_bonc'])
            mb = HB_[2]
            rwkv_post(yc, 's_yc', bonc, 's_bonc', glb, g0b, lmb, mb, n)
            resid_ln('w_mix_out', mb, 'H2', 1, n)

            qc = FB_[0]; qT = FB_[1].rearrange('p a b -> p (a b)')[:, 0:1024]; sel = FB_[2].rearrange('p a b -> p (a b)')[:, 0:NS * 128].rearrange('p (b m) -> p b m', b=NS)
            Kbs = [FB_[3].rearrange('p a b -> p (a b)').rearrange('p (mc f) -> p mc f', mc=2),
                   pl[:].rearrange('p a b -> p (a b)')[:, 0:2048].rearrange('p (mc f) -> p mc f', mc=2)]
            Kkeys = ['F3', 'pl']; Ksems = [dsS[5], P.dma_sem()]
            prod = FB_[4].rearrange('p a b -> p (a b)')[:, 0:1024]
            Vbs = [HB_[0][:].rearrange('p a b -> p (a b)').rearrange('p (mc f) -> p mc f', mc=2),
                   HB_[2][:].rearrange('p a b -> p (a b)').rearrange('p (mc f) -> p mc f', mc=2)]
            Vkeys = ['H0', 'H2']; Vsems = [dsS[6], P.dma_sem()]
            ob = HB_[1]
            sc = st1[0][:, 0:128]; ex = st1[1][:, 0:128]; den = st1[2][:, 0:64]; pbf = st1[3][:, 0:NT].bitcast(BF16)[:, 0:128]

            def evq(m, pb):
                cp(qc[:, m, 0:n], ps[:, pb, 0:n], [('ps', pb)], ['F0'], eng='act')
            proj('xa_wq', D, hb, 'hb', n, evq)
            for g0 in range(0, 8, 4):
                pb = bank()
                for q in range(4):
                    tr(ps[0:n, pb, q * 128:(q + 1) * 128], qc[:, g0 + q, 0:n], ident[:, :], ['F0', 'ident'], [('ps', pb)])
                cp(qT[0:n, g0 * 128:(g0 + 4) * 128], ps[0:n, pb, :], [('ps', pb)], ['F1'], eng='act')
            cp(sel[0:n, :, :], ident[0:n, 0:n].unsqueeze(2).to_broadcast([n, n, 128]), ['ident'], ['F2'])
            for b_ in range(NS):
                Kb = Kbs[b_ % 2]; kkey = Kkeys[b_ % 2]
                dma('sp', Kb, di['cmk'][b_].rearrange('(mc p) f -> p mc f', p=128), [], [kkey], Ksems[b_ % 2])
                pq = [bank(), bank()]
                for hf in range(2):
                    mm(ps[:, pq[hf], :], sel[0:n, b_, :], qT[0:n, hf * 512:(hf + 1) * 512], True, True, ['F2', 'F1'], [('ps', pq[hf])])
                for mc in range(2):
                    for hf in range(2):
                        tt(prod[:, hf * 512:(hf + 1) * 512], Kb[:, mc, hf * 512:(hf + 1) * 512], ps[:, pq[hf], :], ALU.mult, [kkey, ('ps', pq[hf])], ['F4'])
                    P.op('dve', lambda e, b_=b_, mc=mc: e.tensor_reduce(out=sc[:, (b_ * 2 + mc) * 4:(b_ * 2 + mc) * 4 + 4], in_=prod.rearrange('p (h d) -> p h d', h=4), axis=AX.X, op=ALU.add),
                         reads=['F4'], writes=['st0'])
            act(ex, sc, AF.Exp, ['st0'], ['st1'], scale=1.0 / 16.0)
            dma('sp', ones32, di['c_all'][:, 128:256], [], ['st4'], dsS[2])
            pdn = bank()
            mm(ps[:, pdn, 0:128], ones32, ex, True, True, ['st4', 'st1'], [('ps', pdn)])
            d4 = ps[:, pdn, 0:128].rearrange('p (b mc h) -> p b mc h', mc=2, h=4)
            den3 = den.rearrange('p (b h) -> p b h', h=4)
            cp(den3, d4[:, :, 0, :], [('ps', pdn)], ['st2'])
            tt(den3, den3, d4[:, :, 1, :], ALU.add, ['st2', ('ps', pdn)], ['st2'])
            recip(den, den, ['st2'], ['st2'])
            tt(pbf.rearrange('p (b mc h) -> p b mc h', mc=2, h=4), ex.rearrange('p (b mc h) -> p b mc h', mc=2, h=4),
               den3.unsqueeze(2).to_broadcast([128, NS, 2, 4]), ALU.mult, ['st1', 'st2'], ['st3'])
            po = bank()
            for b_ in range(NS):
                Vb = Vbs[b_ % 2]; vkey = Vkeys[b_ % 2]
                dma('pool', Vb, di['cmv'][b_].rearrange('(mc p) f -> p mc f', p=128), [], [vkey], Vsems[b_ % 2])
                for c in range(8):
                    for mc in range(2):
                        col = (b_ * 2 + mc) * 4 + c // 2
                        mm(ps[:, po, c * NS + b_:c * NS + b_ + 1], Vb[:, mc, c * 128:(c + 1) * 128], pbf[:, col:col + 1], mc == 0, mc == 1, [vkey, 'st3'], [('ps', po)])
            cp(ob[:, :, 0:n], ps[:, po, 0:8 * NS].rearrange('p (c b) -> p c b', c=8), [('ps', po)], ['H1'], eng='act')
            resid_ln('xa_wo', ob, 'H1', 2, n)
            ffn('ffn2_wi', 'ffn2_wo', 3, n)
            store_fm_tokens(h32, 'h32', 0, n, di['ys'])

        if do_sample:
            sample_path()
        P.emit()
    return nc, P


_CACHE = {}


def _consts():
    a = np.arange(128) % 64
    b = np.arange(64)
    su = (a[:, None] < b[None, :]).astype(np.float32)
    ui = (a[:, None] <= b[None, :]).astype(np.float32)
    sl = (a[:, None] > b[None, :]).astype(np.float32)
    ey = (a[:, None] == b[None, :]).astype(np.float32)
    bd = np.zeros((128, 128), np.float32)
    bd[:64, :64] = 1.0
    bd[64:, 64:] = 1.0
    rs = np.ones((128, NT), np.float32)
    rs[:, ::C] = 0.0
    return {'c_all': np.ascontiguousarray(np.concatenate([np.eye(128, dtype=np.float32), np.ones((128, 128), np.float32), bd, su, ui, sl, ey, rs], axis=1))}


def make_in_maps(inputs):
    f = lambda a: np.ascontiguousarray(np.asarray(a, dtype=np.float32))
    shared = {}
    for nm in ['ffn1_wi', 'ffn1_wo', 'ffn2_wi', 'ffn2_wo', 'w_in', 'decay_w2', 'aaa_a2', 'gate_g2',
               'lru_wr', 'lru_wi', 'w_mix_out', 'xa_wq', 'xa_wk', 'xa_wv', 'xa_wo']:
        shared[nm] = np.ascontiguousarray(f(inputs[nm])[0])
    shared['prm'] = np.ascontiguousarray(np.concatenate(
        [f(inputs[nm])[0].reshape(-1, 128) for nm in ['ln_g', 'ln_b', 'shift_mu', 'conv_w'] + VEC_NAMES], axis=0))
    shared.update(_consts())
    maps = []
    for c in range(8):
        m = dict(shared)
        sl = slice(c * NS, (c + 1) * NS)
        m['xp'] = f(inputs['x_prompt'][c])
        m['mem'] = f(inputs['mem_prompt'][c])
        m['xs'] = f(inputs['x_sample'][sl, 0])
        m['cmk'] = f(inputs['cache_mem_k'][0, sl]).reshape(NS, NMEM, D)
        m['cmv'] = f(inputs['cache_mem_v'][0, sl]).reshape(NS, NMEM, D)
        m['srw'] = f(inputs['state_rwkv'][0, sl]).reshape(NS, D, 64)
        m['ssh'] = f(inputs['state_rwkv_shift'][0, sl])
        m['slru'] = f(inputs['state_lru'][0, sl])
        m['scv'] = f(inputs['state_conv'][0, sl])
        maps.append(m)
    return maps


def kernel(**inputs):
    if 'nc' not in _CACHE:
        _CACHE['nc'] = build()[0]
    nc = _CACHE['nc']
    maps = make_in_maps(inputs)
    res = run_bass_kernel_spmd(nc, maps, core_ids=list(range(8)))
    R = res.results
    cat = lambda k: np.stack([np.asarray(r[k], dtype=np.float32) for r in R])
    catc = lambda k: np.concatenate([np.asarray(r[k], dtype=np.float32) for r in R], axis=0)
    yp = cat('yp')
    ys = catc('ys').reshape(8 * NS, 1, D)
    pmk = cat('pmk').reshape(1, 8, NMEM, 4, 256)
    pmv = cat('pmv').reshape(1, 8, NMEM, 4, 256)
    prw = cat('prw').reshape(1, 8, 16, 64, 64)
    psh = cat('psh').reshape(1, 8, RP)
    plru = cat('plru').reshape(1, 8, D)
    pcv = cat('pcv').reshape(1, 8, 3, D)
    srw = catc('srw_o').reshape(1, 8 * NS, 16, 64, 64)
    ssh = catc('ssh_o').reshape(1, 8 * NS, RP)
    slru = catc('slru_o').reshape(1, 8 * NS, D)
    scv = catc('scv_o').reshape(1, 8 * NS, 3, D)
    return (yp, ys, pmk, pmv, prw, psh, plru, pcv, srw, ssh, slru, scv)
```

```python
import math
import os
import numpy as np
from contextlib import ExitStack
import concourse.bass as bass
import concourse.mybir as mybir
from concourse.bass_utils import run_bass_kernel_spmd

F32 = mybir.dt.float32
BF16 = mybir.dt.bfloat16
AF = mybir.ActivationFunctionType
ALU = mybir.AluOpType
AX = mybir.AxisListType

ENGS = ['pe', 'dve', 'act', 'pool', 'sp']

D = 1024
T = 2048
NT = 256
NTILES = T // NT
C = 64
NCH = NT // C
DFF = 2816
NJ = DFF // 128
RP = 3328
PW = 7424
NMEM = 256
NS = 16
ALPHA = 2.0 ** 0.25
LN_EPS = 1e-5
GN_EPS = 64e-5
C0 = math.exp(-0.5)


class DmaSem:
    def __init__(self, sem):
        self.sem = sem
        self.count = 0


class Prog:
    def __init__(self, nc, stack):
        self.nc = nc
        self.stack = stack
        self.ops = {e: [] for e in ENGS}
        self.last_w = {}
        self.readers = {}
        self.seen = {e: {} for e in ENGS}
        self.dsems = []
        self.out_tokens = []

    def dma_sem(self):
        s = DmaSem(self.stack.enter_context(self.nc.semaphore('dsem%d' % len(self.dsems))))
        self.dsems.append(s)
        return s

    def sbuf(self, name, shape, dt):
        return self.stack.enter_context(self.nc.sbuf_tensor(name, list(shape), dt))

    def psum(self, name, shape, dt):
        return self.stack.enter_context(self.nc.psum_tensor(name, list(shape), dt))

    def barrier(self, keys, engines=ENGS):
        if 'B' in os.environ.get('TOG', ''):
            return
        for e in engines:
            self.op(e, None, reads=keys, track=False)

    def op(self, eng, fn, reads=(), writes=(), dsem=None, is_out=False, track=True):
        isps = lambda k: isinstance(k, tuple) and k[0] == 'ps'
        writes = list(writes) + [k for k in reads if isps(k)]
        reads = [k for k in reads if not isps(k)]
        deps = []
        for k in reads:
            t = self.last_w.get(k)
            if t is not None:
                deps.append(t)
        for k in writes:
            t = self.last_w.get(k)
            if t is not None:
                deps.append(t)
            deps.extend(self.readers.get(k, {}).values())
        need = {}
        for t in deps:
            if t[0] == 'eng':
                if t[1] == eng and dsem is None and (eng == 'pe' or os.environ.get('NOSELF')):
                    continue
                key = ('eng', t[1])
            else:
                key = ('dma', id(t[1]))
            if need.get(key, (None, -1))[1] < t[2]:
                need[key] = (t[1], t[2])
        waits = []
        for key, (src, v) in need.items():
            if self.seen[eng].get(key, -1) >= v:
                continue
            self.seen[eng][key] = v
            waits.append((key[0], src, v))
        idx = len(self.ops[eng])
        self.ops[eng].append(dict(fn=fn, waits=waits, dsem=dsem, target=False, phase=getattr(self, 'phase', '')))
        if dsem is not None:
            dsem.count += 16
            tok = ('dma', dsem, dsem.count)
        else:
            tok = ('eng', eng, idx)
        for k in writes:
            self.last_w[k] = tok
            self.readers[k] = {}
        for k in (reads if track else ()):
            r = self.readers.setdefault(k, {})
            rk = (tok[0], tok[1] if tok[0] == 'eng' else id(tok[1]))
            if rk not in r or r[rk][2] < tok[2]:
                r[rk] = tok
        if is_out:
            self.out_tokens.append(tok)
        return tok

    def emit(self):
        nc = self.nc
        fin = {}
        for t in self.out_tokens:
            fin[id(t[1])] = (t[1], max(fin.get(id(t[1]), (None, 0))[1], t[2]))
        self.ops['sp'].append(dict(fn=None, waits=[('dma', s, v) for s, v in fin.values()], dsem=None, target=False))
        for e in ENGS:
            for o in self.ops[e]:
                for kind, src, v in o['waits']:
                    if kind == 'eng':
                        self.ops[src][v]['target'] = True
        semval = {}
        for e in ENGS:
            c = 0
            vals = []
            for o in self.ops[e]:
                if o['target']:
                    c += 1
                vals.append(c)
            semval[e] = vals
        esem = {e: self.stack.enter_context(nc.semaphore('esem_' + e)) for e in ENGS}
        handles = {'pe': 'tensor', 'dve': 'vector', 'act': 'scalar', 'pool': 'gpsimd', 'sp': 'sync'}
        with nc.Block() as block:
            def make(e):
                def body(eng):
                    for o in self.ops[e]:
                        for kind, src, v in o['waits']:
                            if kind == 'eng':
                                eng.wait_ge(esem[src], semval[src][v])
                            else:
                                eng.wait_ge(src.sem, v)
                        if o['fn'] is None:
                            continue
                        inst = o['fn'](eng)
                        if os.environ.get('ANNOT') and o.get('phase'):
                            inst.annotate(o['phase'])
                        if o['dsem'] is not None:
                            inst.then_inc(o['dsem'].sem, 16)
                        elif o['target']:
                            inst.then_inc(esem[e], 1)
                return body
            for e in ENGS:
                getattr(block, handles[e])(make(e))
        self.stats = {e: len(self.ops[e]) for e in ENGS}


VEC_NAMES = ['decay_w0', 'aaa_a0', 'k_k', 'k_a', 'r_k', 'gn_g', 'gn_b', 'conv_b', 'lru_br', 'lru_bi', 'lru_lambda']
W_NAMES = ['ffn1_wi', 'ffn1_wo', 'ffn2_wi', 'ffn2_wo', 'w_in', 'w_mix_out', 'xa_wq', 'xa_wk', 'xa_wv', 'xa_wo']


def build(dbg=None, do_sample=True, stage=99):
    nc = bass.Bass('TRN2', target_bir_lowering=False)
    di = {}

    DECL = os.environ.get('DECL')

    def din(name, shape):
        if DECL and name not in DECL.split(','):
            return None
        di[name] = nc.dram_tensor(name, list(shape), F32, kind='ExternalInput').ap()
        return di[name]

    def dout(name, shape):
        if DECL and name not in DECL.split(','):
            return None
        di[name] = nc.dram_tensor(name, list(shape), F32, kind='ExternalOutput').ap()
        return di[name]

    din('xp', [T, D]); din('mem', [NMEM, D])
    din('xs', [NS, D]); din('cmk', [NS, NMEM, D]); din('cmv', [NS, NMEM, D])
    din('srw', [NS, D, 64]); din('ssh', [NS, RP]); din('slru', [NS, D]); din('scv', [NS, 3, D])
    din('prm', [210, 128])
    din('ffn1_wi', [D, 2 * DFF]); din('ffn1_wo', [DFF, D]); din('ffn2_wi', [D, 2 * DFF]); din('ffn2_wo', [DFF, D])
    din('w_in', [D, PW])
    din('decay_w2', [64, D]); din('aaa_a2', [64, D]); din('gate_g2', [128, D])
    din('lru_wr', [16, 64, 64]); din('lru_wi', [16, 64, 64])
    for w in ['w_mix_out', 'xa_wq', 'xa_wk', 'xa_wv', 'xa_wo']:
        din(w, [D, D])
    din('c_all', [128, 640 + NT])
    dout('yp', [T, D]); dout('ys', [NS, D]); dout('pmk', [NMEM, D]); dout('pmv', [NMEM, D])
    dout('prw', [D, 64]); dout('psh', [RP]); dout('plru', [D]); dout('pcv', [3, D])
    dout('srw_o', [NS, D, 64]); dout('ssh_o', [NS, RP]); dout('slru_o', [NS, D]); dout('scv_o', [NS, 3, D])
    dbg = dbg or {}
    for k, shp in dbg.items():
        dout('dbg_' + k, shp)

    with ExitStack() as st:
        P = Prog(nc, st)
        n = NT
        ident = P.sbuf('ident', [128, 128], F32)
        identb = P.sbuf('identb', [128, 128], BF16)
        onesb = P.sbuf('onesb', [128, 128], BF16)
        bdb = P.sbuf('bdb', [128, 128], BF16)
        mask = P.sbuf('mask', [128, 256], F32)
        reset = P.sbuf('reset', [128, NT], F32)
        lng = P.sbuf('lng', [128, 4, 8], F32); lnb = P.sbuf('lnb', [128, 4, 8], F32)
        mu = P.sbuf('mu', [128, 26], F32); omu = P.sbuf('omu', [128, 26], F32)
        vec = {v: P.sbuf('v_' + v, [128, 8], F32) for v in VEC_NAMES}
        oka = P.sbuf('oka', [128, 8], F32)
        lsp = P.sbuf('lsp', [128, 8], F32)
        cw = P.sbuf('cw', [128, 4, 8], F32)
        w2a2 = P.sbuf('w2a2', [128, D], BF16)
        g2 = P.sbuf('g2', [128, D], BF16)
        wrbd = P.sbuf('wrbd', [128, 8, 128], BF16); wibd = P.sbuf('wibd', [128, 8, 128], BF16)
        WCAP = 4096
        NWB = 3
        wbuf = [P.sbuf('wbuf%d' % i, [128, WCAP], BF16) for i in range(NWB)]
        wsem = [P.dma_sem() for i in range(NWB)]
        h32 = P.sbuf('h32', [128, 8, n], F32)
        hb = P.sbuf('hb', [128, 8, n], BF16)
        FB_ = [P.sbuf('F%d' % i, [128, 8, n], F32) for i in range(5)]
        HX = P.sbuf('HX', [128, 24, n], BF16)
        HB_ = [P.sbuf('H%d' % i, [128, 8, n], BF16) for i in range(4)]
        memT = HB_[3]
        pl = P.sbuf('pl', [128, 8, n + 3], F32)
        st1 = [P.sbuf('st%d' % i, [128, n], F32) for i in range(5)]
        st2 = [P.sbuf('su%d' % i, [128, n], F32) for i in range(5)]
        stsel = lambda i: ((st1, 'st') if i % 2 == 0 else (st2, 'su'))
        carry_sh = P.sbuf('carry_sh', [128, 26], F32)
        carry_h = P.sbuf('carry_h', [128, 8], F32)
        A32 = P.sbuf('A32', [128, 8, 64], F32)
        A0b = [P.sbuf('A0b%d' % i, [128, 8, 64], BF16) for i in range(2)]
        RHSb = P.sbuf('RHSb', [128, 8, 64], BF16)
        Ub = P.sbuf('Ub', [128, 8, 64], BF16)
        WCs = P.sbuf('WCs', [128, 8, NCH], F32)
        X32 = [P.sbuf('X32_%d' % i, [128, 8, 64], F32) for i in range(2)]
        Xb = [P.sbuf('Xb_%d' % i, [128, 8, 64], BF16) for i in range(2)]
        PP = [[P.sbuf('PP_%d_%d' % (i, j), [128, 8, 128], BF16) for j in range(2)] for i in range(2)]
        LP = P.sbuf('LP', [128, 8, NCH, 128], BF16)
        PN = P.sbuf('PN', [128, 8, NCH, 128], BF16)
        XT = P.sbuf('XT', [128, 8, NCH, 64], BF16)
        mkT = P.sbuf('mkT', [128, 8, NMEM], BF16)
        mvb = P.sbuf('mvb', [128, 2, D], BF16)
        tok32 = [P.sbuf('tok32_%d' % i, [128, D], F32) for i in range(2)]
        osml = P.sbuf('osml', [128, 8, 8], F32)
        osh = P.sbuf('osh', [128, 26], F32)
        sgb = [P.sbuf('sgb%d' % i, [128, NT], F32) for i in range(2)]
        ps = P.psum('ps', [128, 8, 512], F32)
        dsem_c = [P.dma_sem() for i in range(4)]
        dsem_in = [P.dma_sem() for i in range(2)]
        dsem_out = [P.dma_sem() for i in range(2)]
        dsem_misc = P.dma_sem()

        bank_ctr = [0]
        tokctr = [0]

        def bank():
            b = bank_ctr[0] % 8
            bank_ctr[0] += 1
            return b

        def mm(out, lhsT, rhs, start, stop, reads, writes):
            P.op('pe', lambda e: e.matmul(out, lhsT=lhsT, rhs=rhs, start=start, stop=stop), reads=reads, writes=writes)

        def tr(out, in_, idn, reads, writes):
            P.op('pe', lambda e: e.transpose(out, in_, idn), reads=reads, writes=writes)

        def act(out, in_, func, reads, writes, bias=None, scale=None):
            kw = {}
            if bias is not None:
                kw['bias'] = bias
            if scale is not None:
                kw['scale'] = scale
            P.op('act', lambda e: e.activation(out=out, in_=in_, func=func, **kw), reads=reads, writes=writes)

        def tt(out, in0, in1, op, reads, writes, eng='dve'):
            P.op(eng, lambda e: e.tensor_tensor(out=out, in0=in0, in1=in1, op=op), reads=reads, writes=writes)

        def ts(out, in0, s1, s2, op0, op1, reads, writes, eng='dve'):
            if s2 is None:
                P.op(eng, lambda e: e.tensor_scalar(out=out, in0=in0, scalar1=s1, scalar2=None, op0=op0), reads=reads, writes=writes)
            else:
                P.op(eng, lambda e: e.tensor_scalar(out=out, in0=in0, scalar1=s1, scalar2=s2, op0=op0, op1=op1), reads=reads, writes=writes)

        def stt(out, in0, scalar, in1, op0, op1, reads, writes):
            P.op('dve', lambda e: e.scalar_tensor_tensor(out=out, in0=in0, scalar=scalar, in1=in1, op0=op0, op1=op1), reads=reads, writes=writes)

        def cp(out, in_, reads, writes, eng='dve'):
            if eng == 'act':
                act(out, in_, AF.Copy, reads, writes)
            else:
                P.op(eng, lambda e: e.tensor_copy(out=out, in_=in_), reads=reads, writes=writes)

        def recip(out, in_, reads, writes):
            P.op('dve', lambda e: e.reciprocal(out=out, in_=in_), reads=reads, writes=writes)

        def dma(eng, out, in_, reads, writes, dsem, is_out=False, **kw):
            P.op(eng, lambda e: e.dma_start(out=out, in_=in_, **kw), reads=reads, writes=writes, dsem=dsem, is_out=is_out)

        def interleave2(ga, gb):
            a_ok = b_ok = True
            while a_ok or b_ok:
                if a_ok:
                    try:
                        next(ga)
                    except StopIteration:
                        a_ok = False
                if b_ok:
                    try:
                        next(gb)
                    except StopIteration:
                        b_ok = False

        def dump(name, src_ap, key):
            if name in dbg:
                dma('sp', di['dbg_' + name], src_ap, [key], [], P.dma_sem(), is_out=True)

        PARTS = os.environ.get('PARTS', 'abcdefg')
        dma('sp', ident[:], di['c_all'][:, 0:128], [], ['ident'], dsem_c[0])
        dma('sp', mask[:], di['c_all'][:, 384:640], [], ['mask'], dsem_c[0])
        dma('sp', reset[:], di['c_all'][:, 640:640 + NT], [], ['reset'], dsem_c[0])
        P.barrier(['ident', 'mask', 'reset'])
        if 'b' in PARTS:
            dma('pool', identb[:], di['c_all'][:, 0:128], [], ['identb'], dsem_c[1])
            dma('pool', onesb[:], di['c_all'][:, 128:256], [], ['onesb'], dsem_c[1])
            dma('pool', bdb[:], di['c_all'][:, 256:384], [], ['bdb'], dsem_c[1])
            dma('pool', w2a2[0:64, :], di['decay_w2'], [], ['w2a2'], dsem_c[1])
            dma('pool', w2a2[64:128, :], di['aaa_a2'], [], ['w2a2'], dsem_c[1])
            dma('pool', g2[:], di['gate_g2'], [], ['g2'], dsem_c[1])
        if 'm' in PARTS or PARTS == 'abcdefg':
            P.op('dve', lambda e: e.memset(wrbd[:], 0.0), writes=['wrbd'])
            P.op('dve', lambda e: e.memset(wibd[:], 0.0), writes=['wibd'])
        for (wt, nm, key) in ([(wrbd, 'lru_wr', 'wrbd'), (wibd, 'lru_wi', 'wibd')] if 'c' in PARTS else []):
            src = di[nm].rearrange('(c two) i o -> two i c o', two=2)
            for par in range(2):
                dma('pool', wt[par * 64:(par + 1) * 64, :, par * 64:(par + 1) * 64], src[par], [], [key], dsem_c[1])
        P.barrier(['identb', 'onesb', 'bdb', 'w2a2', 'g2', 'wrbd', 'wibd'])
        prm = [tok32[0], tok32[1]]
        rows = []
        rows.append((lng[:].rearrange('p l c -> p (l c)'), di['prm'][0:32, :], 'lng'))
        rows.append((lnb[:].rearrange('p l c -> p (l c)'), di['prm'][32:64, :], 'lnb'))
        rows.append((mu[:], di['prm'][64:90, :], 'mu'))
        rows.append((cw[:].rearrange('p l c -> p (l c)'), di['prm'][90:122, :], 'cw'))
        for vi_, v in enumerate(VEC_NAMES):
            rows.append((vec[v][:], di['prm'][122 + 8 * vi_:130 + 8 * vi_, :], 'v_' + v))
        groups = [[]]
        cnt = 0
        for r_ in rows:
            k_ = r_[1].shape[0]
            if cnt + k_ > 128:
                groups.append([]); cnt = 0
            groups[-1].append((cnt, k_) + r_)
            cnt += k_
        for gi_, grp in enumerate(groups if 'd' in PARTS else []):
            tk = prm[gi_ % 2]
            tot = 0
            for (o_, k_, dst, src, key) in grp:
                dma('sp', tk[o_:o_ + k_, 0:128], src, [], [('tok32', gi_ % 2)], dsem_c[2 + gi_ % 2])
                tot = o_ + k_
            pb = bank()
            tr(ps[:, pb, 0:tot], tk[0:tot, 0:128], ident[0:tot, 0:tot], [('tok32', gi_ % 2), 'ident'], [('ps', pb)])
            for (o_, k_, dst, src, key) in grp:
                cp(dst, ps[:, pb, o_:o_ + k_], [('ps', pb)], [key])
        if 'e' in PARTS:
            ts(omu[:], mu[:], -1.0, 1.0, ALU.mult, ALU.add, ['mu'], ['omu'])
            ts(oka[:], vec['k_a'][:], -1.0, 1.0, ALU.mult, ALU.add, ['v_k_a'], ['oka'])
            act(lsp[:], vec['lru_lambda'][:], AF.Exp, ['v_lru_lambda'], ['lsp'], scale=-1.0)
            act(lsp[:], lsp[:], AF.Ln, ['lsp'], ['lsp'], bias=1.0)
            ts(lsp[:], lsp[:], -8.0, None, ALU.mult, None, ['lsp'], ['lsp'])

        def wblocks():
            def ffn_blocks(wi, wo):
                for g in range(11):
                    def f(buf, g=g, wi=wi):
                        v = buf[:, 0:4096].rearrange('p (k c) -> p k c', k=8)
                        return [(v[:, :, 0:256], di[wi][:, g * 256:(g + 1) * 256].rearrange('(k p) c -> p k c', p=128)),
                                (v[:, :, 256:512], di[wi][:, DFF + g * 256:DFF + (g + 1) * 256].rearrange('(k p) c -> p k c', p=128))]
                    yield ((wi, g), f)
                for mp in range(8):
                    def f(buf, mp=mp, wo=wo):
                        v = buf[:, 0:NJ * 128].rearrange('p (j c) -> p j c', j=NJ)
                        return [(v, di[wo][:, mp * 128:(mp + 1) * 128].rearrange('(j p) c -> p j c', p=128))]
                    yield ((wo, mp), f)

            def sq_blocks(w, ncols):
                nb = (ncols + 511) // 512
                for b in range(nb):
                    c0 = b * 512
                    cn = min(512, ncols - c0)
                    def f(buf, c0=c0, cn=cn, w=w):
                        v = buf[:, 0:8 * cn].rearrange('p (k c) -> p k c', k=8)
                        return [(v, di[w][:, c0:c0 + cn].rearrange('(k p) c -> p k c', p=128))]
                    yield ((w, b), f)
            yield from sq_blocks('xa_wk', D)
            yield from sq_blocks('xa_wv', D)
            def one_pass():
                yield from ffn_blocks('ffn1_wi', 'ffn1_wo')
                yield from sq_blocks('w_in', PW)
                yield from sq_blocks('w_mix_out', D)
                yield from sq_blocks('xa_wq', D)
                yield from sq_blocks('xa_wo', D)
                yield from ffn_blocks('ffn2_wi', 'ffn2_wo')
            for it in range(int(os.environ.get('NTI', NTILES)) + (1 if do_sample else 0)):
                for blk, (tag, f) in enumerate(one_pass()):
                    yield (tag, f, it, blk)

        wgen = wblocks()
        wstate = dict(issued=0, consumed=0, pending=[])

        NBLK = 59
        wsc = nc.dram_tensor('wsc', [NBLK, 128, WCAP], BF16, kind='Internal').ap()
        wbsem = [P.dma_sem() for i in range(NWB)]

        def w_used(tag):
            if tag[0].endswith('_wi'):
                return 4096
            if tag[0].endswith('_wo') and tag[0].startswith('ffn'):
                return NJ * 128
            ncols = PW if tag[0] == 'w_in' else D
            return 8 * min(512, ncols - tag[1] * 512)

        def w_issue():
            try:
                item = next(wgen)
            except StopIteration:
                return False
            i = wstate['issued'] % NWB
            if len(item) == 2:
                tag, f = item
                for (dst, src) in f(wbuf[i]):
                    dma('pool', dst, src, [], [('wbuf', i)], wsem[i])
            else:
                tag, f, it, blk = item
                used = w_used(tag)
                if it == 0:
                    for (dst, src) in f(wbuf[i]):
                        dma('pool', dst, src, [], [('wbuf', i)], wsem[i])
                    dma('sp', wsc[blk, :, 0:used], wbuf[i][:, 0:used], [('wbuf', i)], [('wsc', blk)], wbsem[i])
                else:
                    dma('pool', wbuf[i][:, 0:used], wsc[blk, :, 0:used], [('wsc', blk)], [('wbuf', i)], wsem[i])
            wstate['pending'].append((tag, i))
            wstate['issued'] += 1
            return True

        def w_next(tag):
            while wstate['issued'] - wstate['consumed'] < NWB:
                if not w_issue():
                    break
            t, i = wstate['pending'].pop(0)
            assert t == tag, (t, tag)
            wstate['consumed'] += 1
            return wbuf[i], ('wbuf', i)

        def sqview(buf, cn):
            return buf[:, 0:8 * cn].rearrange('p (k c) -> p k c', k=8)

        def load_tokens_fm(src_rows_ap, nrows, dst32, dstb, col0, dkey):
            tokctr[0] += 1
            i = tokctr[0] % 2 if 'A' in os.environ.get('TOG', 'A') else 0
            tk = tok32[i]
            dma('sp', tk[0:nrows, :], src_rows_ap, [], [('tok32', i)], dsem_in[i])
            for half in range(2):
                b = bank()
                for q in range(4):
                    kc = half * 4 + q
                    P.op('pe', lambda e, b=b, q=q, kc=kc: e.transpose(ps[:, b, q * 128:q * 128 + nrows], tk[0:nrows, kc * 128:(kc + 1) * 128], ident[0:nrows, 0:nrows]),
                         reads=[('tok32', i), 'ident'], writes=[('ps', b)], track=('W' not in os.environ.get('TOG', '')))
                src = ps[:, b, :].rearrange('p (q t) -> p q t', q=4)[:, :, 0:nrows]
                if dst32 is not None:
                    cp(dst32[:, half * 4:half * 4 + 4, col0:col0 + nrows], src, [('ps', b), ('tok32', i)], [dkey], eng='act')
                if dstb is not None:
                    cp(dstb[:, half * 4:half * 4 + 4, col0:col0 + nrows], src, [('ps', b)], ['hb' if dkey == 'h32' else 'H3'], eng='dve')

        def store_fm_tokens(src32, skey, col0, nrows, dst_rows_ap, nch=8, feat0=0):
            i = bank_ctr[0] % 2
            tk = tok32[i]
            for g0 in range(0, nch, 4):
                b = bank()
                gn = min(4, nch - g0)
                for q in range(gn):
                    tr(ps[0:nrows, b, q * 128:(q + 1) * 128], src32[:, g0 + q, col0:col0 + nrows], ident[:, :],
                       [skey, 'ident'], [('ps', b)])
                cp(tk[0:nrows, g0 * 128:(g0 + gn) * 128], ps[0:nrows, b, 0:gn * 128], [('ps', b)], [('tok32', i)], eng='act')
            dma('sp', dst_rows_ap, tk[0:nrows, 0:nch * 128], [('tok32', i)], [], dsem_out[i], is_out=True)

        def layernorm(idx, n, eps):
            P.phase = 'layernorm'
            zsq = HB_[0]
            cp(hb[:, :, 0:n], h32[:, :, 0:n], ['h32'], ['hb'], eng='dve')
            act(zsq[:, :, 0:n], h32[:, :, 0:n], AF.Square, ['h32'], ['H0'])
            b1 = bank(); b2 = bank()
            for kc in range(8):
                mm(ps[:, b1, 0:n], onesb[:], hb[:, kc, 0:n], kc == 0, kc == 7, ['onesb', 'hb'], [('ps', b1)])
            for kc in range(8):
                mm(ps[:, b2, 0:n], onesb[:], zsq[:, kc, 0:n], kc == 0, kc == 7, ['onesb', 'H0'], [('ps', b2)])
            mean, msq, var, rstd, nmr = [s[:, 0:n] for s in st1]
            ts(mean, ps[:, b1, 0:n], 1.0 / D, None, ALU.mult, None, [('ps', b1)], ['st0'])
            tt(msq, mean, mean, ALU.mult, ['st0'], ['st1'])
            stt(var, ps[:, b2, 0:n], 1.0 / D, msq, ALU.mult, ALU.subtract, [('ps', b2), 'st1'], ['st2'])
            ts(var, var, 0.0, eps, ALU.max, ALU.add, ['st2'], ['st2'])
            act(var, var, AF.Sqrt, ['st2'], ['st2'])
            recip(rstd, var, ['st2'], ['st3'])
            tt(nmr, mean, rstd, ALU.mult, ['st0', 'st3'], ['st4'])
            fine = [('h32', kc) for kc in range(8)]
            P.op('dve', lambda e: e.engine_nop(), writes=['h32'] + fine)
            tpool = [(st1[1], 'st1'), (st1[2], 'st2'), (st2[0], 'su0'), (st2[1], 'su1'), (st2[2], 'su2'), (st2[3], 'su3'), (st2[4], 'su4'), (sgb[0], ('sg', 0))]
            for kc in range(8):
                T_, tk_ = tpool[kc]
                T_ = T_[:, 0:n]
                tt(T_, h32[:, kc, 0:n], rstd, ALU.mult, [('h32', kc), 'st3'], [tk_])
                tt(T_, T_, nmr, ALU.subtract, [tk_, 'st4'], [tk_])
                act(h32[:, kc, 0:n], T_, AF.Identity, [tk_, 'lng', 'lnb'], [('h32', kc)],
                    bias=lnb[:, idx, kc:kc + 1], scale=lng[:, idx, kc:kc + 1])
                act(hb[:, kc, 0:n], T_, AF.Identity, [tk_, 'lng', 'lnb'], ['hb'],
                    bias=lnb[:, idx, kc:kc + 1], scale=lng[:, idx, kc:kc + 1])
            P.op('dve', lambda e: e.engine_nop(), writes=['h32'] + fine)

        def ffn(wi, wo, ln_idx, n):
            P.phase = 'ffn'
            actb = HX
            sg = [sgb[0][:, 0:n], sgb[1][:, 0:n]]
            for g in range(11):
                wb, wk = w_next((wi, g))
                wv = sqview(wb, 512)
                for jj in range(2):
                    j = 2 * g + jj
                    pg = bank(); pu = bank()
                    for kc in range(8):
                        mm(ps[:, pg, 0:n], wv[:, kc, jj * 128:(jj + 1) * 128], hb[:, kc, 0:n], kc == 0, kc == 7, [wk, 'hb'], [('ps', pg)])
                    for kc in range(8):
                        mm(ps[:, pu, 0:n], wv[:, kc, 256 + jj * 128:256 + (jj + 1) * 128], hb[:, kc, 0:n], kc == 0, kc == 7, [wk, 'hb'], [('ps', pu)])
                    act(sg[jj], ps[:, pg, 0:n], AF.Silu, [('ps', pg)], [('sg', jj)])
                    tt(actb[:, j, 0:n], sg[jj], ps[:, pu, 0:n], ALU.mult, [('sg', jj), ('ps', pu)], ['HX%d' % (j // 8)])
            for m in range(8):
                wb, wk = w_next((wo, m))
                wv = wb[:, 0:NJ * 128].rearrange('p (j c) -> p j c', j=NJ)
                po = bank()
                for j in range(NJ):
                    mm(ps[:, po, 0:n], wv[:, j, :], actb[:, j, 0:n], j == 0, j == NJ - 1, [wk, 'HX%d' % (j // 8)], [('ps', po)])
                stt(h32[:, m, 0:n], ps[:, po, 0:n], 0.5 / ALPHA, h32[:, m, 0:n], ALU.mult, ALU.add, [('ps', po), 'h32'], ['h32'])
            layernorm(ln_idx, n, LN_EPS / (ALPHA * ALPHA))

        def proj(w, ncols, xin, xkey, n, evac):
            nb = (ncols + 511) // 512
            for b in range(nb):
                c0 = b * 512
                cn = min(512, ncols - c0)
                wb, wk = w_next((w, b))
                wv = sqview(wb, cn)
                for q in range(cn // 128):
                    m = c0 // 128 + q
                    pb = bank()
                    for kc in range(8):
                        mm(ps[:, pb, 0:n], wv[:, kc, q * 128:(q + 1) * 128], xin[:, kc, 0:n], kc == 0, kc == 7, [wk, xkey], [('ps', pb)])
                    evac(m, pb)

        def mem_kv():
            P.phase = 'mem_kv'
            for r in range(2):
                load_tokens_fm(di['mem'][r * 128:(r + 1) * 128, :], 128, None, memT, r * 128, 'memT')
            for (w, outname, isk) in [('xa_wk', 'pmk', True), ('xa_wv', 'pmv', False)]:
                for b in range(2):
                    wb, wk = w_next((w, b))
                    wv = sqview(wb, 512)
                    for r in range(2):
                        pb = bank()
                        for kc in range(8):
                            mm(ps[:, pb, :], memT[:, kc, r * 128:(r + 1) * 128], wv[:, kc, :], kc == 0, kc == 7, ['H3', wk], [('ps', pb)])
                        i = bank_ctr[0] % 2
                        cp(tok32[i][:, 0:512], ps[:, pb, :], [('ps', pb)], [('tok32', i)], eng='act')
                        if not isk:
                            cp(mvb[:, r, b * 512:(b + 1) * 512], ps[:, pb, :], [('ps', pb)], ['mvb'], eng='dve')
                        dma('sp', di[outname][r * 128:(r + 1) * 128, b * 512:(b + 1) * 512], tok32[i][:, 0:512], [('tok32', i)], [], dsem_out[i], is_out=True)
                    if isk:
                        for q in range(4):
                            m = b * 4 + q
                            pb = bank()
                            for kc in range(8):
                                mm(ps[:, pb, 0:NMEM], wv[:, kc, q * 128:(q + 1) * 128], memT[:, kc, :], kc == 0, kc == 7, [wk, 'H3'], [('ps', pb)])
                            cp(mkT[:, m, :], ps[:, pb, 0:NMEM], [('ps', pb)], ['mkT'], eng='act')

        def mixer_prompt(ti):
            P.phase = 'mixer_prompt'
            n = NT
            r32, k32, v32, a32, ls32 = FB_
            glb = HB_[1]; geb = HB_[2]; g0b = HB_[3]
            g1b = HX[:, 0:8, :]; xcb = HX[:, 8:16, :]
            first = (ti == 0)
            tmp = st1[0]

            def evac(m, pb):
                psn = ps[:, pb, 0:n]
                SS, KP = stsel(m)
                tmp = SS[0]
                if m < 26:
                    act(tmp[:, 1:n], ps[:, pb, 0:n - 1], AF.Copy, [('ps', pb), 'mu'], [KP + '0'], scale=mu[:, m:m + 1])
                    if first:
                        P.op('dve', lambda e, tmp=tmp: e.memset(tmp[:, 0:1], 0.0), writes=[KP + '0'])
                    else:
                        tt(tmp[:, 0:1], carry_sh[:, m:m + 1], mu[:, m:m + 1], ALU.mult, ['carry_sh', 'mu'], [KP + '0'])
                    cp(carry_sh[:, m:m + 1], ps[:, pb, n - 1:n], [('ps', pb)], ['carry_sh'])
                    if m < 24:
                        dst = [r32, k32, v32][m // 8][:, m % 8, :]
                        dkey = ['F0', 'F1', 'F2'][m // 8]
                        stt(dst, psn, omu[:, m:m + 1], tmp[:, 0:n], ALU.mult, ALU.add, [('ps', pb), 'omu', KP + '0'], [dkey])
                    else:
                        xs_ = SS[1][:, 0:n]
                        stt(xs_, psn, omu[:, m:m + 1], tmp[:, 0:n], ALU.mult, ALU.add, [('ps', pb), 'omu', KP + '0'], [KP + '1'])
                        lb = SS[2][:, 0:n].bitcast(BF16)[:, 0:n]
                        if m == 24:
                            act(lb[0:64, :], xs_[0:64, :], AF.Tanh, [KP + '1'], [KP + '2'])
                            cp(lb[64:128, :], xs_[64:128, :], [KP + '1'], [KP + '2'])
                            for (lo, dstt, dk, bvec) in [(0, ls32, 'F4', 'decay_w0'), (64, a32, 'F3', 'aaa_a0')]:
                                for q in range(8):
                                    p2 = bank()
                                    mm(ps[:, p2, 0:n], w2a2[lo:lo + 64, q * 128:(q + 1) * 128], lb[lo:lo + 64, :], True, True, ['w2a2', KP + '2'], [('ps', p2)])
                                    act(dstt[:, q, :], ps[:, p2, 0:n], AF.Sigmoid, [('ps', p2), 'v_' + bvec], [dk], bias=vec[bvec][:, q:q + 1])
                        else:
                            act(lb, xs_, AF.Sigmoid, [KP + '1'], [KP + '2'])
                            for q in range(8):
                                p2 = bank()
                                mm(ps[:, p2, 0:n], g2[:, q * 128:(q + 1) * 128], lb, True, True, ['g2', KP + '2'], [('ps', p2)])
                                cp(glb[:, q, :], ps[:, p2, 0:n], [('ps', p2)], ['H1'], eng='act')
                elif m < 34:
                    cp(pl[:, m - 26, 3:3 + n], psn, [('ps', pb)], ['pl'], eng='act')
                elif m < 42:
                    act(geb[:, m - 34, :], psn, AF.Gelu, [('ps', pb)], ['H2'])
                elif m < 50:
                    act(g0b[:, m - 42, :], psn, AF.Sigmoid, [('ps', pb)], ['H3'])
                else:
                    act(g1b[:, m - 50, :], psn, AF.Sigmoid, [('ps', pb)], ['HX0'])

            if first:
                P.op('dve', lambda e: e.memset(pl[:, :, 0:3], 0.0), writes=['pl'])
            else:
                cp(pl[:, :, 0:3], osml[:, :, 4:7], ['osml'], ['pl'])
            proj('w_in', PW, hb, 'hb', n, evac)
            cp(osml[:, :, 4:7], pl[:, :, n:n + 3], ['pl'], ['osml'])
            dump('r32', r32[:], 'F0'); dump('k32', k32[:], 'F1'); dump('v32', v32[:], 'F2'); dump('a32', a32[:], 'F3'); dump('ls32', ls32[:], 'F4')
            if ti == int(os.environ.get('NTI', NTILES)) - 1:
                cp(osml[:, :, 0:3], pl[:, :, n:n + 3], ['pl'], ['osml'])
                cp(osh[:], carry_sh[:], ['carry_sh'], ['osh'])
            return dict(r32=r32, k32=k32, v32=v32, a32=a32, ls32=ls32, glb=glb, geb=geb, g0b=g0b, g1b=g1b, xcb=xcb)

        def lru_prompt(ti, B):
            P.phase = 'lru_prompt'
            n = NT
            geb, g1b, xcb = B['geb'], B['g1b'], B['xcb']
            lmb = HX[:, 16:24, :]
            def lru_chunk(c):
                SS, KP = stsel(c)
                xc = SS[0][:, 0:n]; gr = SS[1][:, 0:n]; gi = SS[2][:, 0:n]; t3 = SS[3][:, 0:n]; hs = SS[4][:, 0:n]
                act(xc, pl[:, c, 3:3 + n], AF.Identity, ['pl', 'cw', 'v_conv_b'], [KP + '0'], bias=vec['conv_b'][:, c:c + 1], scale=cw[:, 3, c:c + 1])
                for j in range(3):
                    stt(xc, pl[:, c, j:j + n], cw[:, j, c:c + 1], xc, ALU.mult, ALU.add, ['pl', 'cw', KP + '0'], [KP + '0'])
                yield
                cp(xcb[:, c, :], xc, [KP + '0'], ['HX1'], eng='act')
                p1 = bank(); p2 = bank()
                yield
                mm(ps[:, p1, 0:n], wrbd[:, c, :], xcb[:, c, :], True, True, ['wrbd', 'HX1'], [('ps', p1)])
                yield
                mm(ps[:, p2, 0:n], wibd[:, c, :], xcb[:, c, :], True, True, ['wibd', 'HX1'], [('ps', p2)])
                yield
                act(gr, ps[:, p1, 0:n], AF.Sigmoid, [('ps', p1), 'v_lru_br'], [KP + '1'], bias=vec['lru_br'][:, c:c + 1])
                yield
                act(gi, ps[:, p2, 0:n], AF.Sigmoid, [('ps', p2), 'v_lru_bi'], [KP + '2'], bias=vec['lru_bi'][:, c:c + 1])
                yield
                act(gr, gr, AF.Exp, [KP + '1', 'lsp'], [KP + '1'], scale=lsp[:, c:c + 1])
                yield
                act(t3, gr, AF.Square, [KP + '1'], [KP + '3'])
                yield
                ts(t3, t3, -1.0, 1.0, ALU.mult, ALU.add, [KP + '3'], [KP + '3'])
                yield
                ts(t3, t3, 0.0, None, ALU.max, None, [KP + '3'], [KP + '3'])
                yield
                act(t3, t3, AF.Sqrt, [KP + '3'], [KP + '3'])
                yield
                tt(gi, gi, xc, ALU.mult, [KP + '2', KP + '0'], [KP + '2'])
                yield
                tt(gi, gi, t3, ALU.mult, [KP + '2', KP + '3'], [KP + '2'])
                yield
                if ti == 0:
                    P.op('dve', lambda e, hs=hs, gr=gr, gi=gi: e.tensor_tensor_scan(out=hs, data0=gr, data1=gi, initial=0.0, op0=ALU.mult, op1=ALU.add),
                         reads=[KP + '1', KP + '2'], writes=[KP + '4'])
                else:
                    P.op('dve', lambda e, c=c, hs=hs, gr=gr, gi=gi: e.tensor_tensor_scan(out=hs, data0=gr, data1=gi, initial=carry_h[:, c:c + 1], op0=ALU.mult, op1=ALU.add),
                         reads=[KP + '1', KP + '2', ('carry_h', c)], writes=[KP + '4'])
                yield
                cp(carry_h[:, c:c + 1], hs[:, n - 1:n], [KP + '4'], [('carry_h', c)])
                yield
                tt(t3, hs, geb[:, c, :], ALU.mult, [KP + '4', 'H2'], [KP + '3'])
                yield
                tt(lmb[:, c, :], t3, g1b[:, c, :], ALU.mult, [KP + '3', 'HX0'], ['HX2'])
            for _c0 in range(0, 8, 2):
                interleave2(lru_chunk(_c0), lru_chunk(_c0 + 1))
            if ti == int(os.environ.get('NTI', NTILES)) - 1:
                cp(osml[:, :, 3:4], carry_h[:].unsqueeze(2), [('carry_h', c_) for c_ in range(8)], ['osml'])
            return lmb


        plb = pl[:].rearrange('p a b -> p (a b)').bitcast(BF16)
        plA = plb[:, 0:8 * NT].rearrange('p (a b) -> p a b', a=8)
        plB = plb[:, 8 * NT:16 * NT].rearrange('p (a b) -> p a b', a=8)

        def rwkv_core(B, state_in, state_out, skip_inverse=False):
            P.phase = 'rwkv_core'
            n = NT
            r32, k32, v32, a32, ls32 = B['r32'], B['k32'], B['v32'], B['a32'], B['ls32']
            bc8 = lambda v: v[:].unsqueeze(2).to_broadcast([128, 8, n])
            QR = HX[:, 0:16, :].rearrange('p (k two) (c t) -> p k two c t', two=2, t=C)
            KT = HB_[0]; NB = hb
            kk32 = pl[:, :, 0:n]
            tt(kk32, k32[:], bc8(vec['k_k']), ALU.mult, ['F1', 'v_k_k'], ['pl'])
            act(KT[:], kk32, AF.Square, ['pl'], ['H0'])
            rkb = HB_[2]

            def prep_chunk(kc):
                SS, KP = stsel(kc)
                s_ = SS[0][:, 0:n]; u_ = SS[1][:, 0:n]; u2 = SS[2][:, 0:n]
                pb = bank()
                mm(ps[:, pb, 0:n], bdb[:], KT[:, kc, :], True, True, ['bdb', 'H0'], [('ps', pb)])
                act(s_, ps[:, pb, 0:n], AF.Sqrt, [('ps', pb)], [KP + '0'])
                act(u_, a32[:, kc, :], AF.Identity, ['F3', 'v_k_a', 'oka'], [KP + '1'], bias=oka[:, kc:kc + 1], scale=vec['k_a'][:, kc:kc + 1])
                yield
                ts(s_, s_, 1e-12, None, ALU.max, None, [KP + '0'], [KP + '0'])
                yield
                recip(s_, s_, [KP + '0'], [KP + '0'])
                yield
                tt(kk32[:, kc, :], kk32[:, kc, :], s_, ALU.mult, ['pl', KP + '0'], ['pl'])
                yield
                tt(k32[:, kc, :], k32[:, kc, :], u_, ALU.mult, ['F1', KP + '1'], ['F1'])
                yield
                tt(a32[:, kc, :], a32[:, kc, :], kk32[:, kc, :], ALU.mult, ['F3', 'pl'], ['F3'])
                yield
                tt(u2, r32[:, kc, :], k32[:, kc, :], ALU.mult, ['F0', 'F1'], [KP + '2'])
                yield
                act(rkb[:, kc, :], u2, AF.Copy, [KP + '2', 'v_r_k'], ['H2'], scale=vec['r_k'][:, kc:kc + 1])
            for _c0 in range(0, 8, 2):
                interleave2(prep_chunk(_c0), prep_chunk(_c0 + 1))
            P.phase = 'rw_decay'
            def decay_chunk(kc):
                SS, KP = stsel(kc)
                cs = SS[0][:, 0:n]; dd = SS[1][:, 0:n]; Wi = SS[2][:, 0:n]; We = SS[3][:, 0:n]; Wv = SS[4][:, 0:n]
                P.op('dve', lambda e, kc=kc, cs=cs: e.tensor_tensor_scan(out=cs, data0=reset[:, 0:n], data1=ls32[:, kc, :], initial=0.0, op0=ALU.mult, op1=ALU.add),
                     reads=['reset', 'F4'], writes=[KP + '0'])
                yield
                tt(dd, cs, ls32[:, kc, :], ALU.subtract, [KP + '0', 'F4'], [KP + '1'])
                yield
                act(Wi, cs, AF.Exp, [KP + '0'], [KP + '2'], scale=-C0)
                yield
                act(We, dd, AF.Exp, [KP + '1'], [KP + '3'], scale=-C0)
                yield
                act(Wv, cs, AF.Exp, [KP + '0'], [KP + '4'], scale=C0)
                c4 = lambda a: a.rearrange('p (c t) -> p c t', t=C)
                yield
                tt(QR[:, kc, 1, :, :], c4(r32[:, kc, :]), c4(Wi), ALU.mult, ['F0', KP + '2'], ['HX0', 'HX1'])
                yield
                tt(QR[:, kc, 0, :, :], c4(kk32[:, kc, :]), c4(We), ALU.mult, ['pl', KP + '3'], ['HX0', 'HX1'])
                yield
                cp(WCs[:, kc, :], c4(Wi)[:, :, C - 1], [KP + '2'], ['WCs'])
                yield
                tt(Wi, k32[:, kc, :], Wv, ALU.mult, ['F1', KP + '4', KP + '2'], [KP + '2'])
                yield
                stt(We, a32[:, kc, :], -1.0, Wv, ALU.mult, ALU.mult, ['F3', KP + '4', KP + '3'], [KP + '3'])
                yield
                cp(NB[:, kc, :], We, [KP + '3'], ['hb'], eng='act')
                yield
                cp(KT[:, kc, :], Wi, [KP + '2'], ['H0'], eng='act')
            for _c0 in range(0, 8, 2):
                interleave2(decay_chunk(_c0), decay_chunk(_c0 + 1))
            P.phase = 'rw_tok'
            vb = plB
            cp(vb[:, :, 0:n], v32[:], ['F2'], ['pl'], eng='act')
            for kc in range(8):
                pb = bank()
                mm(ps[:, pb, 0:n], bdb[:], rkb[:, kc, :], True, True, ['bdb', 'H2'], [('ps', pb)])
                tt(v32[:, kc, :], v32[:, kc, :], ps[:, pb, 0:n], ALU.mult, ['F2', ('ps', pb)], ['F2'])
            tmv = lambda a: a.rearrange('p a b -> p (a b)').rearrange('p (c x) -> p c x', c=NCH)
            vT = tmv(HB_[2][:]); kTt = tmv(plA); nbT = tmv(plB)

            def to_tokmajor(src, skey, dst, dkey):
                for c in range(NCH):
                    pb = bank()
                    pbv = ps[:, pb, :].bitcast(BF16)
                    for hp in range(8):
                        for par in range(2):
                            lo = par * 64
                            tr(pbv[lo:lo + 64, hp * 64:(hp + 1) * 64], src[lo:lo + 64, hp, c * C:(c + 1) * C], identb[lo:lo + 64, lo:lo + 64],
                               [skey, 'identb'], [('ps', pb)])
                    cp(dst[:, c, :], pbv[:, 0:512], [('ps', pb)], [dkey], eng=('act' if c % 2 else 'dve'))
            to_tokmajor(vb, 'pl', vT, 'H2')
            to_tokmajor(KT, 'H0', kTt, 'pl')
            to_tokmajor(NB, 'hb', nbT, 'pl')
            vTv = lambda c, hp, lo: vT[lo:lo + 64, c, hp * 64:(hp + 1) * 64]
            kTv = lambda c, hp, lo: kTt[lo:lo + 64, c, hp * 64:(hp + 1) * 64]
            nTv = lambda c, hp, lo: nbT[lo:lo + 64, c, hp * 64:(hp + 1) * 64]
            m_su_ui = mask[:, 0:128].unsqueeze(1).to_broadcast([128, 8, 128])
            m_sl = mask[:, 128:192].unsqueeze(1).to_broadcast([128, 8, 64])
            m_eye = mask[:, 192:256].unsqueeze(1).to_broadcast([128, 8, 64])
            def ph1(c0):
                P.phase = 'rw_ph1'
                ctx = []
                for s in range(2):
                    c = c0 + s
                    b1a = bank(); b1b = bank()
                    for hp in range(8):
                        for par in range(2):
                            lo = par * 64
                            bsel = b1a if hp < 4 else b1b
                            mm(ps[lo:lo + 64, bsel, (hp % 4) * 128:(hp % 4 + 1) * 128], KT[lo:lo + 64, hp, c * C:(c + 1) * C],
                               QR[lo:lo + 64, hp, :, c, :], True, True, ['H0', 'HX0', 'HX1'], [('ps', bsel)])
                    for (bsel, h0) in [(b1a, 0), (b1b, 4)]:
                        tt(LP[:, h0:h0 + 4, c, :], ps[:, bsel, :].rearrange('p (h x) -> p h x', h=4), m_su_ui[:, 0:4, :], ALU.mult,
                           [('ps', bsel), 'mask'], [('LP', c)])
                    b2a = bank(); b2b = bank()
                    for hp in range(8):
                        for par in range(2):
                            lo = par * 64
                            bsel = b2a if hp < 4 else b2b
                            mm(ps[lo:lo + 64, bsel, (hp % 4) * 128:(hp % 4 + 1) * 128], NB[lo:lo + 64, hp, c * C:(c + 1) * C],
                               QR[lo:lo + 64, hp, :, c, :], True, True, ['hb', 'HX0', 'HX1'], [('ps', bsel)])
                    for (bsel, h0) in [(b2a, 0), (b2b, 4)]:
                        tt(PN[:, h0:h0 + 4, c, :], ps[:, bsel, :].rearrange('p (h x) -> p h x', h=4), m_su_ui[:, 0:4, :], ALU.mult,
                           [('ps', bsel), 'mask'], [('PN', c)])
                    b3 = bank()
                    for hp in range(8):
                        for par in range(2):
                            lo = par * 64
                            mm(ps[lo:lo + 64, b3, hp * 64:(hp + 1) * 64], QR[lo:lo + 64, hp, 0, c, :], NB[lo:lo + 64, hp, c * C:(c + 1) * C],
                               True, True, ['HX0', 'HX1', 'hb'], [('ps', b3)])
                    pp = PP[s][0]
                    cp(pp[:, :, 0:64], PN[:, :, c, 0:64], [('PN', c)], [('PP', s, 0)], eng='act')
                    tt(pp[:, :, 64:128], ps[:, b3, :].rearrange('p (h x) -> p h x', h=8), m_sl, ALU.mult, [('ps', b3), 'mask'], [('PP', s, 0)])
                    tt(X32[s][:], PN[:, :, c, 0:64], m_eye, ALU.add, [('PN', c), 'mask'], [('X32', s)])
                    cp(Xb[s][:], X32[s][:], [('X32', s)], [('Xb', s)], eng='act')
                    ctx.append(c)
                if skip_inverse:
                    for s_ in range(2):
                        cp(XT[:, :, ctx[s_], :], X32[s_][:], [('X32', s_)], [('XT', ctx[s_])], eng='act')
                for lvl in ([] if skip_inverse else range(1, 6)):
                    yield
                    P.phase = 'rw_ph1'
                    cur = (lvl - 1) % 2; nxt = lvl % 2
                    banks = []
                    for s in range(2):
                        ba = bank(); bb = bank()
                        src = PP[s][cur]
                        for hp in range(8):
                            for par in range(2):
                                lo = par * 64
                                bsel = ba if hp < 4 else bb
                                o0 = (hp % 4) * 128
                                mm(ps[lo:lo + 64, bsel, o0:o0 + 64], src[lo:lo + 64, hp, 64:128], src[lo:lo + 64, hp, 0:64], True, True,
                                   [('PP', s, cur)], [('ps', bsel)])
                                mm(ps[lo:lo + 64, bsel, o0 + 64:o0 + 128], src[lo:lo + 64, hp, 0:64], src[lo:lo + 64, hp, 64:128], True, True,
                                   [('PP', s, cur)], [('ps', bsel)])
                        banks.append((ba, bb))
                    for s in range(2):
                        ba, bb = banks[s]
                        dst = PP[s][nxt]
                        cp(dst[:, 0:4, :], ps[:, ba, :].rearrange('p (h x) -> p h x', h=4), [('ps', ba)], [('PP', s, nxt)], eng='act')
                        cp(dst[:, 4:8, :], ps[:, bb, :].rearrange('p (h x) -> p h x', h=4), [('ps', bb)], [('PP', s, nxt)], eng='dve')
                    xb_ = []
                    for s in range(2):
                        bx = bank()
                        src = PP[s][nxt]
                        for hp in range(8):
                            for par in range(2):
                                lo = par * 64
                                mm(ps[lo:lo + 64, bx, hp * 64:(hp + 1) * 64], src[lo:lo + 64, hp, 64:128], Xb[s][lo:lo + 64, hp, :], True, True,
                                   [('PP', s, nxt), ('Xb', s)], [('ps', bx)])
                        xb_.append(bx)
                    for s in range(2):
                        bx = xb_[s]
                        tt(X32[s][:], X32[s][:], ps[:, bx, :].rearrange('p (h x) -> p h x', h=8), ALU.add, [('X32', s), ('ps', bx)], [('X32', s)])
                        if lvl < 5:
                            cp(Xb[s][:], X32[s][:], [('X32', s)], [('Xb', s)], eng='act')
                        else:
                            cp(XT[:, :, ctx[s], :], X32[s][:], [('X32', s)], [('XT', ctx[s])], eng='act')
            Y32 = FB_[0]

            def ph2(c):
                P.phase = 'rw_ph2'
                state_in(c)
                cur = c % 2
                a0 = A0b[cur]
                bR = bank()
                for hp in range(8):
                    for par in range(2):
                        lo = par * 64
                        o = ps[lo:lo + 64, bR, hp * 64:(hp + 1) * 64]
                        mm(o, QR[lo:lo + 64, hp, 0, c, :], a0[lo:lo + 64, hp, :], True, False, ['HX0', 'HX1', ('A0b', cur)], [('ps', bR)])
                        mm(o, LP[lo:lo + 64, hp, c, 0:64], vTv(c, hp, lo), False, True, [('LP', c), 'H2'], [('ps', bR)])
                cp(RHSb[:], ps[:, bR, :].rearrange('p (h x) -> p h x', h=8), [('ps', bR)], ['RHSb'], eng='act')
                yield
                P.phase = 'rw_ph2'
                bU = bank()
                for hp in range(8):
                    for par in range(2):
                        lo = par * 64
                        mm(ps[lo:lo + 64, bU, hp * 64:(hp + 1) * 64], XT[lo:lo + 64, hp, c, :], RHSb[lo:lo + 64, hp, :], True, True,
                           [('XT', c), 'RHSb'], [('ps', bU)])
                cp(Ub[:], ps[:, bU, :].rearrange('p (h x) -> p h x', h=8), [('ps', bU)], ['Ub'], eng='act')
                yield
                P.phase = 'rw_ph2'
                bD = bank()
                for hp in range(8):
                    for par in range(2):
                        lo = par * 64
                        o = ps[lo:lo + 64, bD, hp * 64:(hp + 1) * 64]
                        mm(o, kTv(c, hp, lo), vTv(c, hp, lo), True, False, ['pl', 'H2'], [('ps', bD)])
                        mm(o, nTv(c, hp, lo), Ub[lo:lo + 64, hp, :], False, True, ['pl', 'Ub'], [('ps', bD)])
                bY = bank()
                for hp in range(8):
                    for par in range(2):
                        lo = par * 64
                        o = ps[lo:lo + 64, bY, hp * 64:(hp + 1) * 64]
                        mm(o, a0[lo:lo + 64, hp, :], QR[lo:lo + 64, hp, 1, c, :], True, False, [('A0b', cur), 'HX0', 'HX1'], [('ps', bY)])
                        mm(o, vTv(c, hp, lo), LP[lo:lo + 64, hp, c, 64:128], False, False, ['H2', ('LP', c)], [('ps', bY)])
                        mm(o, Ub[lo:lo + 64, hp, :], PN[lo:lo + 64, hp, c, 64:128], False, True, ['Ub', ('PN', c)], [('ps', bY)])
                tt(A32[:], A32[:], ps[:, bD, :].rearrange('p (h x) -> p h x', h=8), ALU.add, ['A32', ('ps', bD)], ['A32'])
                tt(A32[:], A32[:], WCs[:, :, c:c + 1].to_broadcast([128, 8, 64]), ALU.mult, ['A32', 'WCs'], ['A32'])
                cp(A0b[1 - cur][:], A32[:], ['A32'], [('A0b', 1 - cur)], eng='act')
                cp(Y32[:, :, c * C:(c + 1) * C], ps[:, bY, :].rearrange('p (h x) -> p h x', h=8), [('ps', bY)], ['F0'], eng='dve')
                state_out(c)
                yield

            def drain(g):
                for _ in g:
                    pass

            def interleave(ga, gb):
                a_ok = b_ok = True
                while a_ok or b_ok:
                    if a_ok:
                        try:
                            next(ga)
                        except StopIteration:
                            a_ok = False
                    if b_ok:
                        try:
                            next(gb)
                        except StopIteration:
                            b_ok = False

            def chain(*gs):
                for g in gs:
                    yield from g
            drain(ph1(0))
            if NCH == 4:
                interleave(ph1(2), chain(ph2(0), ph2(1)))
                drain(chain(ph2(2), ph2(3)))
            else:
                for c0 in range(2, NCH, 2):
                    drain(ph1(c0))
                for c in range(NCH):
                    drain(ph2(c))
            return Y32, v32

        def rwkv_post(Y, Ykey, bonus, bkey, glb, g0b, lmb, mb, n):
            P.phase = 'rwkv_post'
            Yb = HB_[0]; ysq = hb
            cp(Yb[:, :, 0:n], Y[:, :, 0:n], [Ykey], ['H0'], eng='act')
            act(ysq[:, :, 0:n], Y[:, :, 0:n], AF.Square, [Ykey], ['hb'])
            def post_chunk(kc):
                b1 = bank(); b2 = bank()
                mm(ps[:, b1, 0:n], bdb[:], Yb[:, kc, 0:n], True, True, ['bdb', 'H0'], [('ps', b1)])
                yield
                mm(ps[:, b2, 0:n], bdb[:], ysq[:, kc, 0:n], True, True, ['bdb', 'hb'], [('ps', b2)])
                SS, KP = stsel(kc)
                mean = SS[0][:, 0:n]; var = SS[1][:, 0:n]; t_ = SS[2][:, 0:n]
                yield
                ts(mean, ps[:, b1, 0:n], 1.0 / 64, None, ALU.mult, None, [('ps', b1)], [KP + '0'])
                yield
                tt(var, mean, mean, ALU.mult, [KP + '0'], [KP + '1'])
                yield
                stt(var, ps[:, b2, 0:n], 1.0 / 64, var, ALU.mult, ALU.subtract, [('ps', b2), KP + '1'], [KP + '1'])
                yield
                ts(var, var, 0.0, GN_EPS, ALU.max, ALU.add, [KP + '1'], [KP + '1'])
                yield
                act(var, var, AF.Sqrt, [KP + '1'], [KP + '1'])
                yield
                recip(var, var, [KP + '1'], [KP + '1'])
                yield
                tt(t_, Y[:, kc, 0:n], mean, ALU.subtract, [Ykey, KP + '0'], [KP + '2'])
                yield
                tt(t_, t_, var, ALU.mult, [KP + '2', KP + '1'], [KP + '2'])
                yield
                act(t_, t_, AF.Identity, [KP + '2', 'v_gn_g', 'v_gn_b'], [KP + '2'], bias=vec['gn_b'][:, kc:kc + 1], scale=vec['gn_g'][:, kc:kc + 1])
                yield
                tt(t_, t_, bonus[:, kc, 0:n], ALU.add, [KP + '2', bkey], [KP + '2'])
                yield
                tt(t_, t_, glb[:, kc, 0:n], ALU.mult, [KP + '2', 'H1'], [KP + '2'])
                yield
                tt(t_, t_, g0b[:, kc, 0:n], ALU.mult, [KP + '2', 'H3'], [KP + '2'])
                yield
                tt(mb[:, kc, 0:n], t_, lmb[:, kc, 0:n], ALU.add, [KP + '2', 'HX2'], ['H2'])
            for _c0 in range(0, 8, 2):
                interleave2(post_chunk(_c0), post_chunk(_c0 + 1))

        def resid_ln(w, xin, xkey, ln_idx, n):
            def evac(m, pb):
                stt(h32[:, m, 0:n], ps[:, pb, 0:n], 1.0 / ALPHA, h32[:, m, 0:n], ALU.mult, ALU.add, [('ps', pb), 'h32'], ['h32'])
            proj(w, D, xin, xkey, n, evac)
            layernorm(ln_idx, n, LN_EPS / (ALPHA * ALPHA))

        def xattn_prompt(n):
            P.phase = 'xattn_prompt'
            qb = HB_[0]; ob = HB_[1]; pT = HX[:, 0:8, :]
            def evq(m, pb):
                cp(qb[:, m, 0:n], ps[:, pb, 0:n], [('ps', pb)], ['H0'], eng='act')
            proj('xa_wq', D, hb, 'hb', n, evq)
            for h in range(4):
                for mc in range(2):
                    pb = bank()
                    for dc in range(2):
                        mm(ps[:, pb, 0:n], mkT[:, 2 * h + dc, mc * 128:(mc + 1) * 128], qb[:, 2 * h + dc, 0:n], dc == 0, dc == 1, ['mkT', 'H0'], [('ps', pb)])
                    act(pT[:, 2 * h + mc, 0:n], ps[:, pb, 0:n], AF.Exp, [('ps', pb)], ['HX0'], scale=1.0 / 16.0)
            rds = []
            for h in range(4):
                pd = bank()
                for mc in range(2):
                    mm(ps[:, pd, 0:n], onesb[:], pT[:, 2 * h + mc, 0:n], mc == 0, mc == 1, ['onesb', 'HX0'], [('ps', pd)])
                SS, KP = stsel(h)
                rd = SS[h // 2][:, 0:n]; rk_ = KP + str(h // 2)
                recip(rd, ps[:, pd, 0:n], [('ps', pd)], [rk_])
                rds.append((rd, rk_))
            for h in range(4):
                rd, rk_ = rds[h]
                for dc in range(2):
                    po = bank()
                    for mc in range(2):
                        mm(ps[:, po, 0:n], mvb[:, mc, (2 * h + dc) * 128:(2 * h + dc + 1) * 128], pT[:, 2 * h + mc, 0:n], mc == 0, mc == 1, ['mvb', 'HX0'], [('ps', po)])
                    tt(ob[:, 2 * h + dc, 0:n], ps[:, po, 0:n], rd, ALU.mult, [('ps', po), rk_], ['H1'])
            resid_ln('xa_wo', ob, 'H1', 2, n)

        if stage >= 1:
            mem_kv()
        if 'm' in PARTS or PARTS == 'abcdefg':
            P.op('dve', lambda e: e.memset(A32[:], 0.0), writes=['A32'])
            P.op('dve', lambda e: e.memset(A0b[0][:], 0.0), writes=[('A0b', 0)])
        for ti in range(int(os.environ.get('NTI', NTILES)) if stage >= 9 else 1):
            TOG = os.environ.get('TOG', '')
            for r in range(1 if '1' in TOG else NT // 128):
                load_tokens_fm(di['xp'][ti * NT + r * 128: ti * NT + (r + 1) * 128, :], 128, h32, None if 'D' in TOG else hb, r * 128, 'h32')
            if stage >= 2:
                ffn('ffn1_wi', 'ffn1_wo', 0, NT)
            if ti == 0:
                dump('h1', h32[:], 'h32')
            if stage < 3:
                break
            B = mixer_prompt(ti)
            if stage < 4:
                break
            lmb = lru_prompt(ti, B)
            if ti == 0:
                dump('lm', lmb, 'HX2')
            if stage < 5:
                break
            Y32, bonus = rwkv_core(B, lambda c: None, lambda c: None)
            if ti == 0:
                dump('Y', Y32[:], 'F0')
            if stage < 6:
                break
            mb = HB_[2]
            rwkv_post(Y32, 'F0', bonus, 'F2', B['glb'], B['g0b'], lmb, mb, NT)
            resid_ln('w_mix_out', mb, 'H2', 1, NT)
            if ti == 0:
                dump('h2', h32[:], 'h32')
            if stage < 7:
                break
            xattn_prompt(NT)
            if ti == 0:
                dump('h3', h32[:], 'h32')
            if stage < 8:
                break
            ffn('ffn2_wi', 'ffn2_wo', 3, NT)
            for r in range(NT // 128):
                store_fm_tokens(h32, 'h32', r * 128, 128, di['yp'][ti * NT + r * 128: ti * NT + (r + 1) * 128, :])
        if stage < 9:
            P.emit()
            return nc, P
        def prompt_outputs():
            pass
            Sout = FB_[1].rearrange('p a b -> p (a b)')[:, 0:1024].rearrange('p (hp par k) -> p hp par k', hp=8, par=2)
            for g in range(2):
                pb = bank()
                for q in range(4):
                    hp = g * 4 + q
                    tr(ps[0:64, pb, q * 128:(q + 1) * 128], A32[:, hp, :], ident[:, :], ['A32', 'ident'], [('ps', pb)])
                cp(Sout[0:64, g * 4:(g + 1) * 4, :, :], ps[0:64, pb, :].rearrange('p (q par k) -> p q par k', q=4, par=2), [('ps', pb)], ['F1'], eng='act')
            dma('sp', di['prw'].rearrange('(hp par v) k -> v hp par k', hp=8, par=2), Sout[0:64, :, :, :], ['F1'], [], dsem_misc, is_out=True)
            osm2 = FB_[2]
            cp(osm2[:, 0:8, 0:3], osml[:, :, 0:3], ['osml'], ['F2'])
            cp(osm2[:, 0:8, 3:4], osml[:, :, 3:4], ['osml'], ['F2'])
            store_fm_tokens(osm2, 'F2', 0, 3, di['pcv'][:, :])
            store_fm_tokens(osm2, 'F2', 3, 1, di['plru'].rearrange('(o d) -> o d', o=1))
            osh3 = FB_[3]
            cp(osh3[:, 0:8, 0:1], osh[:, 0:8].unsqueeze(2), ['osh'], ['F3'])
            cp(osh3[:, 0:8, 1:2], osh[:, 8:16].unsqueeze(2), ['osh'], ['F3'])
            cp(osh3[:, 0:8, 2:3], osh[:, 16:24].unsqueeze(2), ['osh'], ['F3'])
            cp(osh3[:, 0:2, 3:4], osh[:, 24:26].unsqueeze(2), ['osh'], ['F3'])
            pshv = di['psh'].rearrange('(o d) -> o d', o=1)
            for q in range(3):
                store_fm_tokens(osh3, 'F3', q, 1, pshv[:, q * 1024:(q + 1) * 1024])
            store_fm_tokens(osh3, 'F3', 3, 1, pshv[:, 3072:3328], nch=2)


        if os.environ.get('NTI') != '0':
            prompt_outputs()
        def sample_path():
            P.phase = 'sample_path'
            n = NS
            sm = lambda nm, shp, dt=F32: P.sbuf(nm, shp, dt)
            rc = sm('s_rc', [128, 8, n]); kc_ = sm('s_kc', [128, 8, n]); vc = sm('s_vc', [128, 8, n]); ac = sm('s_ac', [128, 8, n]); lsc = sm('s_lsc', [128, 8, n])
            yc = sm('s_yc', [128, 8, n]); bonc = sm('s_bonc', [128, 8, n]); plc = sm('s_plc', [128, 8, n]); hsc = sm('s_hsc', [128, 8, n])
            prevS = sm('s_prev', [128, 26, n]); praw = sm('s_praw', [128, 26, n]); h0S = sm('s_h0', [128, 8, n]); scvT = sm('s_scvT', [128, 8, 3 * n])
            BD = tok32[1][:].rearrange('p (h x) -> p h x', h=8); ones32 = st1[4][:, 0:128]
            glb = HB_[1]; geb = HB_[2]; g0b = HB_[3]; g1b = HX[:, 0:8, :]; lmb = HX[:, 16:24, :]
            dsS = [P.dma_sem() for _ in range(7)]

            def load_fm(src_rows_ap, nrows, nchunks, dst, dkey, tki, sem):
                tk = tok32[tki]
                dma('sp', tk[0:nrows, 0:nchunks * 128], src_rows_ap, [], [('tok32', tki)], sem)
                for g0 in range(0, nchunks, 4):
                    gn = min(4, nchunks - g0)
                    b = bank()
                    for q in range(gn):
                        tr(ps[:, b, q * 128:q * 128 + nrows], tk[0:nrows, (g0 + q) * 128:(g0 + q + 1) * 128], ident[0:nrows, 0:nrows],
                           [('tok32', tki), 'ident'], [('ps', b)])
                    cp(dst[:, g0:g0 + gn, 0:nrows], ps[:, b, :].rearrange('p (q t) -> p q t', q=4)[:, 0:gn, 0:nrows], [('ps', b)], [dkey], eng='act')

            load_tokens_fm(di['xs'], n, h32, hb, 0, 'h32')
            for q in range(4):
                c0 = q * 8; cn = min(8, 26 - c0)
                tmpd = FB_[0] if q % 2 == 0 else FB_[1]
                load_fm(di['ssh'][:, c0 * 128:(c0 + cn) * 128], n, cn, tmpd, 'F%d' % (q % 2), q % 2, dsS[q % 2])
                cp(prevS[:, c0:c0 + cn, :], tmpd[:, 0:cn, 0:n], ['F%d' % (q % 2)], ['s_prev'])
            load_fm(di['slru'], n, 8, h0S, 's_h0', 0, dsS[0])
            load_fm(di['scv'].rearrange('b j d -> (b j) d'), 3 * n, 8, scvT, 's_scvT', 1, dsS[1])
            dma('sp', di['scv_o'][:, 0:2, :], di['scv'][:, 1:3, :], [], [], P.dma_sem(), is_out=True)

            ffn('ffn1_wi', 'ffn1_wo', 0, n)

            tmp = st1[0]

            def evac(m, pb):
                psn = ps[:, pb, 0:n]
                if m < 26:
                    cp(praw[:, m, :], psn, [('ps', pb)], ['s_praw'], eng='act')
                    ts(tmp[:, 0:n], prevS[:, m, :], mu[:, m:m + 1], None, ALU.mult, None, ['s_prev', 'mu'], ['st0'])
                    if m < 24:
                        dst = [rc, kc_, vc][m // 8][:, m % 8, :]
                        dkey = ['s_rc', 's_kc', 's_vc'][m // 8]
                        stt(dst, psn, omu[:, m:m + 1], tmp[:, 0:n], ALU.mult, ALU.add, [('ps', pb), 'omu', 'st0'], [dkey])
                    else:
                        xs_ = st1[1][:, 0:n]
                        stt(xs_, psn, omu[:, m:m + 1], tmp[:, 0:n], ALU.mult, ALU.add, [('ps', pb), 'omu', 'st0'], ['st1'])
                        lb = st1[2][:, 0:NT].bitcast(BF16)[:, 0:n]
                        if m == 24:
                            act(lb[0:64, :], xs_[0:64, :], AF.Tanh, ['st1'], ['st2'])
                            cp(lb[64:128, :], xs_[64:128, :], ['st1'], ['st2'])
                            for (lo, dstt, dk, bvec) in [(0, lsc, 's_lsc', 'decay_w0'), (64, ac, 's_ac', 'aaa_a0')]:
                                for q in range(8):
                                    p2 = bank()
                                    mm(ps[:, p2, 0:n], w2a2[lo:lo + 64, q * 128:(q + 1) * 128], lb[lo:lo + 64, :], True, True, ['w2a2', 'st2'], [('ps', p2)])
                                    act(dstt[:, q, :], ps[:, p2, 0:n], AF.Sigmoid, [('ps', p2), 'v_' + bvec], [dk], bias=vec[bvec][:, q:q + 1])
                        else:
                            act(lb, xs_, AF.Sigmoid, ['st1'], ['st2'])
                            for q in range(8):
                                p2 = bank()
                                mm(ps[:, p2, 0:n], g2[:, q * 128:(q + 1) * 128], lb, True, True, ['g2', 'st2'], [('ps', p2)])
                                cp(glb[:, q, 0:n], ps[:, p2, 0:n], [('ps', p2)], ['H1'], eng='act')
                elif m < 34:
                    cp(plc[:, m - 26, :], psn, [('ps', pb)], ['s_plc'], eng='act')
                elif m < 42:
                    act(geb[:, m - 34, 0:n], psn, AF.Gelu, [('ps', pb)], ['H2'])
                elif m < 50:
                    act(g0b[:, m - 42, 0:n], psn, AF.Sigmoid, [('ps', pb)], ['H3'])
                else:
                    act(g1b[:, m - 50, 0:n], psn, AF.Sigmoid, [('ps', pb)], ['HX0'])
            proj('w_in', PW, hb, 'hb', n, evac)
            for q in range(4):
                c0 = q * 8; cn = min(8, 26 - c0)
                store_fm_tokens(praw[:, c0:c0 + cn, :], 's_praw', 0, n, di['ssh_o'][:, c0 * 128:(c0 + cn) * 128], nch=cn)
            store_fm_tokens(plc, 's_plc', 0, n, di['scv_o'][:, 2, :])

            sc3 = scvT[:].rearrange('p c (b j) -> p c b j', j=3)
            xc = st1[0][:, 0:n]; gr = st1[1][:, 0:n]; gi = st1[2][:, 0:n]; t3 = st1[3][:, 0:n]; hs = st1[4][:, 0:n]
            xcb = HX[:, 8:16, :]
            for c in range(8):
                act(xc, plc[:, c, :], AF.Identity, ['s_plc', 'cw', 'v_conv_b'], ['st0'], bias=vec['conv_b'][:, c:c + 1], scale=cw[:, 3, c:c + 1])
                for j in range(3):
                    stt(xc, sc3[:, c, :, j], cw[:, j, c:c + 1], xc, ALU.mult, ALU.add, ['s_scvT', 'cw', 'st0'], ['st0'])
                cp(xcb[:, c, 0:n], xc, ['st0'], ['HX1'], eng='act')
                p1 = bank(); p2 = bank()
                mm(ps[:, p1, 0:n], wrbd[:, c, :], xcb[:, c, 0:n], True, True, ['wrbd', 'HX1'], [('ps', p1)])
                mm(ps[:, p2, 0:n], wibd[:, c, :], xcb[:, c, 0:n], True, True, ['wibd', 'HX1'], [('ps', p2)])
                act(gr, ps[:, p1, 0:n], AF.Sigmoid, [('ps', p1), 'v_lru_br'], ['st1'], bias=vec['lru_br'][:, c:c + 1])
                act(gi, ps[:, p2, 0:n], AF.Sigmoid, [('ps', p2), 'v_lru_bi'], ['st2'], bias=vec['lru_bi'][:, c:c + 1])
                act(gr, gr, AF.Exp, ['st1', 'lsp'], ['st1'], scale=lsp[:, c:c + 1])
                act(t3, gr, AF.Square, ['st1'], ['st3'])
                ts(t3, t3, -1.0, 1.0, ALU.mult, ALU.add, ['st3'], ['st3'])
                ts(t3, t3, 0.0, None, ALU.max, None, ['st3'], ['st3'])
                act(t3, t3, AF.Sqrt, ['st3'], ['st3'])
                tt(gi, gi, xc, ALU.mult, ['st2', 'st0'], ['st2'])
                tt(gi, gi, t3, ALU.mult, ['st2', 'st3'], ['st2'])
                tt(hs, gr, h0S[:, c, :], ALU.mult, ['st1', 's_h0'], ['st4'])
                tt(hsc[:, c, :], hs, gi, ALU.add, ['st4', 'st2'], ['s_hsc'])
                tt(t3, hsc[:, c, :], geb[:, c, 0:n], ALU.mult, ['s_hsc', 'H2'], ['st3'])
                tt(lmb[:, c, 0:n], t3, g1b[:, c, 0:n], ALU.mult, ['st3', 'HX0'], ['HX2'])
            store_fm_tokens(hsc, 's_hsc', 0, n, di['slru_o'])

            F1flat = FB_[1].rearrange('p a b -> p (a b)')
            Souts = [F1flat[:, 0:1024].rearrange('p (hp par k) -> p hp par k', hp=8, par=2),
                     F1flat[:, 1024:2048].rearrange('p (hp par k) -> p hp par k', hp=8, par=2)]
            Skeys = ['F1', ('F1', 'b')]; Ssems = [dsS[4], P.dma_sem()]
            BDs = [tok32[0][:].rearrange('p (h x) -> p h x', h=8), tok32[1][:].rearrange('p (h x) -> p h x', h=8)]
            Bsems = [P.dma_sem(), dsS[3]]
            for i_ in range(2):
                P.op('dve', lambda e, i_=i_: e.memset(tok32[i_][:], 0.0), writes=[('tok32', i_)])

            def prefetch_state(b_):
                if b_ >= NS:
                    return
                src = di['srw'][b_].rearrange('(hp par v) k -> par v hp k', hp=8, par=2)
                for par in range(2):
                    dma('sp', BDs[b_ % 2][par * 64:(par + 1) * 64, :, par * 64:(par + 1) * 64], src[par], [], [('tok32', b_ % 2)], Bsems[b_ % 2])
            prefetch_state(0)
            for g in range(NS // NCH):
                Bp = dict(r32=FB_[0], k32=FB_[1], v32=FB_[2], a32=FB_[3], ls32=FB_[4])
                for (dstF, fk, srcc, sk) in [(FB_[0], 'F0', rc, 's_rc'), (FB_[1], 'F1', kc_, 's_kc'), (FB_[2], 'F2', vc, 's_vc'), (FB_[3], 'F3', ac, 's_ac'), (FB_[4], 'F4', lsc, 's_lsc')]:
                    P.op('dve', lambda e, dstF=dstF: e.memset(dstF[:], 0.0), writes=([fk, ('F1', 'b')] if fk == 'F1' else [fk]))
                    cp(dstF[:].rearrange('p k (c t) -> p k c t', t=C)[:, :, :, 0], srcc[:, :, g * NCH:(g + 1) * NCH], [sk], [fk])

                def state_in(c, g=g):
                    b_ = g * NCH + c
                    prefetch_state(b_ + 1)
                    BD = BDs[b_ % 2]
                    pb = bank()
                    for hp in range(8):
                        mm(ps[:, pb, hp * 64:(hp + 1) * 64], BD[:, hp, :], mask[:, 192:256], True, True, [('tok32', b_ % 2), 'mask'], [('ps', pb)])
                    v3 = ps[:, pb, :].rearrange('p (h x) -> p h x', h=8)
                    cp(A32[:], v3, [('ps', pb)], ['A32'], eng='dve')
                    cp(A0b[c % 2][:], v3, [('ps', pb)], [('A0b', c % 2)], eng='act')

                def state_out(c, g=g):
                    b_ = g * NCH + c
                    Sout = Souts[b_ % 2]; skey = Skeys[b_ % 2]
                    for gg in range(2):
                        pb = bank()
                        for q in range(4):
                            hp = gg * 4 + q
                            tr(ps[0:64, pb, q * 128:(q + 1) * 128], A32[:, hp, :], ident[:, :], ['A32', 'ident'], [('ps', pb)])
                        cp(Sout[0:64, gg * 4:(gg + 1) * 4, :, :], ps[0:64, pb, :].rearrange('p (q par k) -> p q par k', q=4, par=2), [('ps', pb)], [skey], eng='act')
                    dma('sp', di['srw_o'][b_].rearrange('(hp par v) k -> v hp par k', hp=8, par=2), Sout[0:64, :, :, :], [skey], [], Ssems[b_ % 2], is_out=True)
                Y32, bon = rwkv_core(Bp, state_in, state_out, skip_inverse=True)
                cp(yc[:, :, g * NCH:(g + 1) * NCH], Y32[:].rearrange('p k (c t) -> p k c t', t=C)[:, :, :, 0], ['F0'], ['s_yc'])
                cp(bonc[:, :, g * NCH:(g + 1) * NCH], bon[:].rearrange('p k (c t) -> p k c t', t=C)[:, :, :, 0], ['F2'], ['s_bonc'])
            mb = HB_[2]
            rwkv_post(yc, 's_yc', bonc, 's_bonc', glb, g0b, lmb, mb, n)
            resid_ln('w_mix_out', mb, 'H2', 1, n)

            qc = FB_[0]; qT = FB_[1].rearrange('p a b -> p (a b)')[:, 0:1024]; sel = FB_[2].rearrange('p a b -> p (a b)')[:, 0:NS * 128].rearrange('p (b m) -> p b m', b=NS)
            Kbs = [FB_[3].rearrange('p a b -> p (a b)').rearrange('p (mc f) -> p mc f', mc=2),
                   pl[:].rearrange('p a b -> p (a b)')[:, 0:2048].rearrange('p (mc f) -> p mc f', mc=2)]
            Kkeys = ['F3', 'pl']; Ksems = [dsS[5], P.dma_sem()]
            prod = FB_[4].rearrange('p a b -> p (a b)')[:, 0:1024]
            Vbs = [HB_[0][:].rearrange('p a b -> p (a b)').rearrange('p (mc f) -> p mc f', mc=2),
                   HB_[2][:].rearrange('p a b -> p (a b)').rearrange('p (mc f) -> p mc f', mc=2)]
            Vkeys = ['H0', 'H2']; Vsems = [dsS[6], P.dma_sem()]
            ob = HB_[1]
            sc = st1[0][:, 0:128]; ex = st1[1][:, 0:128]; den = st1[2][:, 0:64]; pbf = st1[3][:, 0:NT].bitcast(BF16)[:, 0:128]

            def evq(m, pb):
                cp(qc[:, m, 0:n], ps[:, pb, 0:n], [('ps', pb)], ['F0'], eng='act')
            proj('xa_wq', D, hb, 'hb', n, evq)
            for g0 in range(0, 8, 4):
                pb = bank()
                for q in range(4):
                    tr(ps[0:n, pb, q * 128:(q + 1) * 128], qc[:, g0 + q, 0:n], ident[:, :], ['F0', 'ident'], [('ps', pb)])
                cp(qT[0:n, g0 * 128:(g0 + 4) * 128], ps[0:n, pb, :], [('ps', pb)], ['F1'], eng='act')
            cp(sel[0:n, :, :], ident[0:n, 0:n].unsqueeze(2).to_broadcast([n, n, 128]), ['ident'], ['F2'])
            for b_ in range(NS):
                Kb = Kbs[b_ % 2]; kkey = Kkeys[b_ % 2]
                dma('sp', Kb, di['cmk'][b_].rearrange('(mc p) f -> p mc f', p=128), [], [kkey], Ksems[b_ % 2])
                pq = [bank(), bank()]
                for hf in range(2):
                    mm(ps[:, pq[hf], :], sel[0:n, b_, :], qT[0:n, hf * 512:(hf + 1) * 512], True, True, ['F2', 'F1'], [('ps', pq[hf])])
                for mc in range(2):
                    for hf in range(2):
                        tt(prod[:, hf * 512:(hf + 1) * 512], Kb[:, mc, hf * 512:(hf + 1) * 512], ps[:, pq[hf], :], ALU.mult, [kkey, ('ps', pq[hf])], ['F4'])
                    P.op('dve', lambda e, b_=b_, mc=mc: e.tensor_reduce(out=sc[:, (b_ * 2 + mc) * 4:(b_ * 2 + mc) * 4 + 4], in_=prod.rearrange('p (h d) -> p h d', h=4), axis=AX.X, op=ALU.add),
                         reads=['F4'], writes=['st0'])
            act(ex, sc, AF.Exp, ['st0'], ['st1'], scale=1.0 / 16.0)
            dma('sp', ones32, di['c_all'][:, 128:256], [], ['st4'], dsS[2])
            pdn = bank()
            mm(ps[:, pdn, 0:128], ones32, ex, True, True, ['st4', 'st1'], [('ps', pdn)])
            d4 = ps[:, pdn, 0:128].rearrange('p (b mc h) -> p b mc h', mc=2, h=4)
            den3 = den.rearrange('p (b h) -> p b h', h=4)
            cp(den3, d4[:, :, 0, :], [('ps', pdn)], ['st2'])
            tt(den3, den3, d4[:, :, 1, :], ALU.add, ['st2', ('ps', pdn)], ['st2'])
            recip(den, den, ['st2'], ['st2'])
            tt(pbf.rearrange('p (b mc h) -> p b mc h', mc=2, h=4), ex.rearrange('p (b mc h) -> p b mc h', mc=2, h=4),
               den3.unsqueeze(2).to_broadcast([128, NS, 2, 4]), ALU.mult, ['st1', 'st2'], ['st3'])
            po = bank()
            for b_ in range(NS):
                Vb = Vbs[b_ % 2]; vkey = Vkeys[b_ % 2]
                dma('pool', Vb, di['cmv'][b_].rearrange('(mc p) f -> p mc f', p=128), [], [vkey], Vsems[b_ % 2])
                for c in range(8):
                    for mc in range(2):
                        col = (b_ * 2 + mc) * 4 + c // 2
                        mm(ps[:, po, c * NS + b_:c * NS + b_ + 1], Vb[:, mc, c * 128:(c + 1) * 128], pbf[:, col:col + 1], mc == 0, mc == 1, [vkey, 'st3'], [('ps', po)])
            cp(ob[:, :, 0:n], ps[:, po, 0:8 * NS].rearrange('p (c b) -> p c b', c=8), [('ps', po)], ['H1'], eng='act')
            resid_ln('xa_wo', ob, 'H1', 2, n)
            ffn('ffn2_wi', 'ffn2_wo', 3, n)
            store_fm_tokens(h32, 'h32', 0, n, di['ys'])

        if do_sample:
            sample_path()
        P.emit()
    return nc, P


_CACHE = {}


def _consts():
    a = np.arange(128) % 64
    b = np.arange(64)
    su = (a[:, None] < b[None, :]).astype(np.float32)
    ui = (a[:, None] <= b[None, :]).astype(np.float32)
    sl = (a[:, None] > b[None, :]).astype(np.float32)
    ey = (a[:, None] == b[None, :]).astype(np.float32)
    bd = np.zeros((128, 128), np.float32)
    bd[:64, :64] = 1.0
    bd[64:, 64:] = 1.0
    rs = np.ones((128, NT), np.float32)
    rs[:, ::C] = 0.0
    return {'c_all': np.ascontiguousarray(np.concatenate([np.eye(128, dtype=np.float32), np.ones((128, 128), np.float32), bd, su, ui, sl, ey, rs], axis=1))}


def make_in_maps(inputs):
    f = lambda a: np.ascontiguousarray(np.asarray(a, dtype=np.float32))
    shared = {}
    for nm in ['ffn1_wi', 'ffn1_wo', 'ffn2_wi', 'ffn2_wo', 'w_in', 'decay_w2', 'aaa_a2', 'gate_g2',
               'lru_wr', 'lru_wi', 'w_mix_out', 'xa_wq', 'xa_wk', 'xa_wv', 'xa_wo']:
        shared[nm] = np.ascontiguousarray(f(inputs[nm])[0])
    shared['prm'] = np.ascontiguousarray(np.concatenate(
        [f(inputs[nm])[0].reshape(-1, 128) for nm in ['ln_g', 'ln_b', 'shift_mu', 'conv_w'] + VEC_NAMES], axis=0))
    shared.update(_consts())
    maps = []
    for c in range(8):
        m = dict(shared)
        sl = slice(c * NS, (c + 1) * NS)
        m['xp'] = f(inputs['x_prompt'][c])
        m['mem'] = f(inputs['mem_prompt'][c])
        m['xs'] = f(inputs['x_sample'][sl, 0])
        m['cmk'] = f(inputs['cache_mem_k'][0, sl]).reshape(NS, NMEM, D)
        m['cmv'] = f(inputs['cache_mem_v'][0, sl]).reshape(NS, NMEM, D)
        m['srw'] = f(inputs['state_rwkv'][0, sl]).reshape(NS, D, 64)
        m['ssh'] = f(inputs['state_rwkv_shift'][0, sl])
        m['slru'] = f(inputs['state_lru'][0, sl])
        m['scv'] = f(inputs['state_conv'][0, sl])
        maps.append(m)
    return maps


def kernel(**inputs):
    if 'nc' not in _CACHE:
        _CACHE['nc'] = build()[0]
    nc = _CACHE['nc']
    maps = make_in_maps(inputs)
    res = run_bass_kernel_spmd(nc, maps, core_ids=list(range(8)))
    R = res.results
    cat = lambda k: np.stack([np.asarray(r[k], dtype=np.float32) for r in R])
    catc = lambda k: np.concatenate([np.asarray(r[k], dtype=np.float32) for r in R], axis=0)
    yp = cat('yp')
    ys = catc('ys').reshape(8 * NS, 1, D)
    pmk = cat('pmk').reshape(1, 8, NMEM, 4, 256)
    pmv = cat('pmv').reshape(1, 8, NMEM, 4, 256)
    prw = cat('prw').reshape(1, 8, 16, 64, 64)
    psh = cat('psh').reshape(1, 8, RP)
    plru = cat('plru').reshape(1, 8, D)
    pcv = cat('pcv').reshape(1, 8, 3, D)
    srw = catc('srw_o').reshape(1, 8 * NS, 16, 64, 64)
    ssh = catc('ssh_o').reshape(1, 8 * NS, RP)
    slru = catc('slru_o').reshape(1, 8 * NS, D)
    scv = catc('scv_o').reshape(1, 8 * NS, 3, D)
    return (yp, ys, pmk, pmv, prw, psh, plru, pcv, srw, ssh, slru, scv)
```

```python
import math
import os
import numpy as np
from contextlib import ExitStack
import concourse.bass as bass
import concourse.mybir as mybir
from concourse.bass_utils import run_bass_kernel_spmd

F32 = mybir.dt.float32
BF16 = mybir.dt.bfloat16
AF = mybir.ActivationFunctionType
ALU = mybir.AluOpType
AX = mybir.AxisListType

ENGS = ['pe', 'dve', 'act', 'pool', 'sp']

D = 1024
T = 2048
NT = 256
NTILES = T // NT
C = 64
NCH = NT // C
DFF = 2816
NJ = DFF // 128
RP = 3328
PW = 7424
NMEM = 256
NS = 16
ALPHA = 2.0 ** 0.25
LN_EPS = 1e-5
GN_EPS = 64e-5
C0 = math.exp(-0.5)


class DmaSem:
    def __init__(self, sem):
        self.sem = sem
        self.count = 0


class Prog:
    def __init__(self, nc, stack):
        self.nc = nc
        self.stack = stack
        self.ops = {e: [] for e in ENGS}
        self.last_w = {}
        self.readers = {}
        self.seen = {e: {} for e in ENGS}
        self.dsems = []
        self.out_tokens = []

    def dma_sem(self):
        s = DmaSem(self.stack.enter_context(self.nc.semaphore('dsem%d' % len(self.dsems))))
        self.dsems.append(s)
        return s

    def sbuf(self, name, shape, dt):
        return self.stack.enter_context(self.nc.sbuf_tensor(name, list(shape), dt))

    def psum(self, name, shape, dt):
        return self.stack.enter_context(self.nc.psum_tensor(name, list(shape), dt))

    def barrier(self, keys, engines=ENGS):
        if 'B' in os.environ.get('TOG', ''):
            return
        for e in engines:
            self.op(e, None, reads=keys, track=False)

    def op(self, eng, fn, reads=(), writes=(), dsem=None, is_out=False, track=True):
        isps = lambda k: isinstance(k, tuple) and k[0] == 'ps'
        writes = list(writes) + [k for k in reads if isps(k)]
        reads = [k for k in reads if not isps(k)]
        deps = []
        for k in reads:
            t = self.last_w.get(k)
            if t is not None:
                deps.append(t)
        for k in writes:
            t = self.last_w.get(k)
            if t is not None:
                deps.append(t)
            deps.extend(self.readers.get(k, {}).values())
        need = {}
        for t in deps:
            if t[0] == 'eng':
                if t[1] == eng and dsem is None and (eng == 'pe' or os.environ.get('NOSELF')):
                    continue
                key = ('eng', t[1])
            else:
                key = ('dma', id(t[1]))
            if need.get(key, (None, -1))[1] < t[2]:
                need[key] = (t[1], t[2])
        waits = []
        for key, (src, v) in need.items():
            if self.seen[eng].get(key, -1) >= v:
                continue
            self.seen[eng][key] = v
            waits.append((key[0], src, v))
        idx = len(self.ops[eng])
        self.ops[eng].append(dict(fn=fn, waits=waits, dsem=dsem, target=False, phase=getattr(self, 'phase', '')))
        if dsem is not None:
            dsem.count += 16
            tok = ('dma', dsem, dsem.count)
        else:
            tok = ('eng', eng, idx)
        for k in writes:
            self.last_w[k] = tok
            self.readers[k] = {}
        for k in (reads if track else ()):
            r = self.readers.setdefault(k, {})
            rk = (tok[0], tok[1] if tok[0] == 'eng' else id(tok[1]))
            if rk not in r or r[rk][2] < tok[2]:
                r[rk] = tok
        if is_out:
            self.out_tokens.append(tok)
        return tok

    def emit(self):
        nc = self.nc
        fin = {}
        for t in self.out_tokens:
            fin[id(t[1])] = (t[1], max(fin.get(id(t[1]), (None, 0))[1], t[2]))
        self.ops['sp'].append(dict(fn=None, waits=[('dma', s, v) for s, v in fin.values()], dsem=None, target=False))
        for e in ENGS:
            for o in self.ops[e]:
                for kind, src, v in o['waits']:
                    if kind == 'eng':
                        self.ops[src][v]['target'] = True
        semval = {}
        for e in ENGS:
            c = 0
            vals = []
            for o in self.ops[e]:
                if o['target']:
                    c += 1
                vals.append(c)
            semval[e] = vals
        esem = {e: self.stack.enter_context(nc.semaphore('esem_' + e)) for e in ENGS}
        handles = {'pe': 'tensor', 'dve': 'vector', 'act': 'scalar', 'pool': 'gpsimd', 'sp': 'sync'}
        with nc.Block() as block:
            def make(e):
                def body(eng):
                    for o in self.ops[e]:
                        for kind, src, v in o['waits']:
                            if kind == 'eng':
                                eng.wait_ge(esem[src], semval[src][v])
                            else:
                                eng.wait_ge(src.sem, v)
                        if o['fn'] is None:
                            continue
                        inst = o['fn'](eng)
                        if os.environ.get('ANNOT') and o.get('phase'):
                            inst.annotate(o['phase'])
                        if o['dsem'] is not None:
                            inst.then_inc(o['dsem'].sem, 16)
                        elif o['target']:
                            inst.then_inc(esem[e], 1)
                return body
            for e in ENGS:
                getattr(block, handles[e])(make(e))
        self.stats = {e: len(self.ops[e]) for e in ENGS}


VEC_NAMES = ['decay_w0', 'aaa_a0', 'k_k', 'k_a', 'r_k', 'gn_g', 'gn_b', 'conv_b', 'lru_br', 'lru_bi', 'lru_lambda']
W_NAMES = ['ffn1_wi', 'ffn1_wo', 'ffn2_wi', 'ffn2_wo', 'w_in', 'w_mix_out', 'xa_wq', 'xa_wk', 'xa_wv', 'xa_wo']


def build(dbg=None, do_sample=True, stage=99):
    nc = bass.Bass('TRN2', target_bir_lowering=False)
    di = {}

    DECL = os.environ.get('DECL')

    def din(name, shape):
        if DECL and name not in DECL.split(','):
            return None
        di[name] = nc.dram_tensor(name, list(shape), F32, kind='ExternalInput').ap()
        return di[name]

    def dout(name, shape):
        if DECL and name not in DECL.split(','):
            return None
        di[name] = nc.dram_tensor(name, list(shape), F32, kind='ExternalOutput').ap()
        return di[name]

    din('xp', [T, D]); din('mem', [NMEM, D])
    din('xs', [NS, D]); din('cmk', [NS, NMEM, D]); din('cmv', [NS, NMEM, D])
    din('srw', [NS, D, 64]); din('ssh', [NS, RP]); din('slru', [NS, D]); din('scv', [NS, 3, D])
    din('prm', [210, 128])
    din('ffn1_wi', [D, 2 * DFF]); din('ffn1_wo', [DFF, D]); din('ffn2_wi', [D, 2 * DFF]); din('ffn2_wo', [DFF, D])
    din('w_in', [D, PW])
    din('decay_w2', [64, D]); din('aaa_a2', [64, D]); din('gate_g2', [128, D])
    din('lru_wr', [16, 64, 64]); din('lru_wi', [16, 64, 64])
    for w in ['w_mix_out', 'xa_wq', 'xa_wk', 'xa_wv', 'xa_wo']:
        din(w, [D, D])
    din('c_all', [128, 640 + NT])
    dout('yp', [T, D]); dout('ys', [NS, D]); dout('pmk', [NMEM, D]); dout('pmv', [NMEM, D])
    dout('prw', [D, 64]); dout('psh', [RP]); dout('plru', [D]); dout('pcv', [3, D])
    dout('srw_o', [NS, D, 64]); dout('ssh_o', [NS, RP]); dout('slru_o', [NS, D]); dout('scv_o', [NS, 3, D])
    dbg = dbg or {}
    for k, shp in dbg.items():
        dout('dbg_' + k, shp)

    with ExitStack() as st:
        P = Prog(nc, st)
        n = NT
        ident = P.sbuf('ident', [128, 128], F32)
        identb = P.sbuf('identb', [128, 128], BF16)
        onesb = P.sbuf('onesb', [128, 128], BF16)
        bdb = P.sbuf('bdb', [128, 128], BF16)
        mask = P.sbuf('mask', [128, 256], F32)
        reset = P.sbuf('reset', [128, NT], F32)
        lng = P.sbuf('lng', [128, 4, 8], F32); lnb = P.sbuf('lnb', [128, 4, 8], F32)
        mu = P.sbuf('mu', [128, 26], F32); omu = P.sbuf('omu', [128, 26], F32)
        vec = {v: P.sbuf('v_' + v, [128, 8], F32) for v in VEC_NAMES}
        oka = P.sbuf('oka', [128, 8], F32)
        lsp = P.sbuf('lsp', [128, 8], F32)
        cw = P.sbuf('cw', [128, 4, 8], F32)
        w2a2 = P.sbuf('w2a2', [128, D], BF16)
        g2 = P.sbuf('g2', [128, D], BF16)
        wrbd = P.sbuf('wrbd', [128, 8, 128], BF16); wibd = P.sbuf('wibd', [128, 8, 128], BF16)
        WCAP = 4096
        NWB = 3
        wbuf = [P.sbuf('wbuf%d' % i, [128, WCAP], BF16) for i in range(NWB)]
        wsem = [P.dma_sem() for i in range(NWB)]
        h32 = P.sbuf('h32', [128, 8, n], F32)
        hb = P.sbuf('hb', [128, 8, n], BF16)
        FB_ = [P.sbuf('F%d' % i, [128, 8, n], F32) for i in range(5)]
        HX = P.sbuf('HX', [128, 24, n], BF16)
        HB_ = [P.sbuf('H%d' % i, [128, 8, n], BF16) for i in range(4)]
        memT = HB_[3]
        pl = P.sbuf('pl', [128, 8, n + 3], F32)
        st1 = [P.sbuf('st%d' % i, [128, n], F32) for i in range(5)]
        st2 = [P.sbuf('su%d' % i, [128, n], F32) for i in range(5)]
        stsel = lambda i: ((st1, 'st') if i % 2 == 0 else (st2, 'su'))
        carry_sh = P.sbuf('carry_sh', [128, 26], F32)
        carry_h = P.sbuf('carry_h', [128, 8], F32)
        A32 = P.sbuf('A32', [128, 8, 64], F32)
        A0b = [P.sbuf('A0b%d' % i, [128, 8, 64], BF16) for i in range(2)]
        RHSb = P.sbuf('RHSb', [128, 8, 64], BF16)
        Ub = P.sbuf('Ub', [128, 8, 64], BF16)
        WCs = P.sbuf('WCs', [128, 8, NCH], F32)
        X32 = [P.sbuf('X32_%d' % i, [128, 8, 64], F32) for i in range(2)]
        Xb = [P.sbuf('Xb_%d' % i, [128, 8, 64], BF16) for i in range(2)]
        PP = [[P.sbuf('PP_%d_%d' % (i, j), [128, 8, 128], BF16) for j in range(2)] for i in range(2)]
        LP = P.sbuf('LP', [128, 8, NCH, 128], BF16)
        PN = P.sbuf('PN', [128, 8, NCH, 128], BF16)
        XT = P.sbuf('XT', [128, 8, NCH, 64], BF16)
        mkT = P.sbuf('mkT', [128, 8, NMEM], BF16)
        mvb = P.sbuf('mvb', [128, 2, D], BF16)
        tok32 = [P.sbuf('tok32_%d' % i, [128, D], F32) for i in range(2)]
        osml = P.sbuf('osml', [128, 8, 8], F32)
        osh = P.sbuf('osh', [128, 26], F32)
        sgb = [P.sbuf('sgb%d' % i, [128, NT], F32) for i in range(2)]
        ps = P.psum('ps', [128, 8, 512], F32)
        dsem_c = [P.dma_sem() for i in range(4)]
        dsem_in = [P.dma_sem() for i in range(2)]
        dsem_out = [P.dma_sem() for i in range(2)]
        dsem_misc = P.dma_sem()

        bank_ctr = [0]
        tokctr = [0]

        def bank():
            b = bank_ctr[0] % 8
            bank_ctr[0] += 1
            return b

        def mm(out, lhsT, rhs, start, stop, reads, writes):
            P.op('pe', lambda e: e.matmul(out, lhsT=lhsT, rhs=rhs, start=start, stop=stop), reads=reads, writes=writes)

        def tr(out, in_, idn, reads, writes):
            P.op('pe', lambda e: e.transpose(out, in_, idn), reads=reads, writes=writes)

        def act(out, in_, func, reads, writes, bias=None, scale=None):
            kw = {}
            if bias is not None:
                kw['bias'] = bias
            if scale is not None:
                kw['scale'] = scale
            P.op('act', lambda e: e.activation(out=out, in_=in_, func=func, **kw), reads=reads, writes=writes)

        def tt(out, in0, in1, op, reads, writes, eng='dve'):
            P.op(eng, lambda e: e.tensor_tensor(out=out, in0=in0, in1=in1, op=op), reads=reads, writes=writes)

        def ts(out, in0, s1, s2, op0, op1, reads, writes, eng='dve'):
            if s2 is None:
                P.op(eng, lambda e: e.tensor_scalar(out=out, in0=in0, scalar1=s1, scalar2=None, op0=op0), reads=reads, writes=writes)
            else:
                P.op(eng, lambda e: e.tensor_scalar(out=out, in0=in0, scalar1=s1, scalar2=s2, op0=op0, op1=op1), reads=reads, writes=writes)

        def stt(out, in0, scalar, in1, op0, op1, reads, writes):
            P.op('dve', lambda e: e.scalar_tensor_tensor(out=out, in0=in0, scalar=scalar, in1=in1, op0=op0, op1=op1), reads=reads, writes=writes)

        def cp(out, in_, reads, writes, eng='dve'):
            if eng == 'act':
                act(out, in_, AF.Copy, reads, writes)
            else:
                P.op(eng, lambda e: e.tensor_copy(out=out, in_=in_), reads=reads, writes=writes)

        def recip(out, in_, reads, writes):
            P.op('dve', lambda e: e.reciprocal(out=out, in_=in_), reads=reads, writes=writes)

        def dma(eng, out, in_, reads, writes, dsem, is_out=False, **kw):
            P.op(eng, lambda e: e.dma_start(out=out, in_=in_, **kw), reads=reads, writes=writes, dsem=dsem, is_out=is_out)

        def interleave2(ga, gb):
            a_ok = b_ok = True
            while a_ok or b_ok:
                if a_ok:
                    try:
                        next(ga)
                    except StopIteration:
                        a_ok = False
                if b_ok:
                    try:
                        next(gb)
                    except StopIteration:
                        b_ok = False

        def dump(name, src_ap, key):
            if name in dbg:
                dma('sp', di['dbg_' + name], src_ap, [key], [], P.dma_sem(), is_out=True)

        PARTS = os.environ.get('PARTS', 'abcdefg')
        dma('sp', ident[:], di['c_all'][:, 0:128], [], ['ident'], dsem_c[0])
        dma('sp', mask[:], di['c_all'][:, 384:640], [], ['mask'], dsem_c[0])
        dma('sp', reset[:], di['c_all'][:, 640:640 + NT], [], ['reset'], dsem_c[0])
        P.barrier(['ident', 'mask', 'reset'])
        if 'b' in PARTS:
            dma('pool', identb[:], di['c_all'][:, 0:128], [], ['identb'], dsem_c[1])
            dma('pool', onesb[:], di['c_all'][:, 128:256], [], ['onesb'], dsem_c[1])
            dma('pool', bdb[:], di['c_all'][:, 256:384], [], ['bdb'], dsem_c[1])
            dma('pool', w2a2[0:64, :], di['decay_w2'], [], ['w2a2'], dsem_c[1])
            dma('pool', w2a2[64:128, :], di['aaa_a2'], [], ['w2a2'], dsem_c[1])
            dma('pool', g2[:], di['gate_g2'], [], ['g2'], dsem_c[1])
        if 'm' in PARTS or PARTS == 'abcdefg':
            P.op('dve', lambda e: e.memset(wrbd[:], 0.0), writes=['wrbd'])
            P.op('dve', lambda e: e.memset(wibd[:], 0.0), writes=['wibd'])
        for (wt, nm, key) in ([(wrbd, 'lru_wr', 'wrbd'), (wibd, 'lru_wi', 'wibd')] if 'c' in PARTS else []):
            src = di[nm].rearrange('(c two) i o -> two i c o', two=2)
            for par in range(2):
                dma('pool', wt[par * 64:(par + 1) * 64, :, par * 64:(par + 1) * 64], src[par], [], [key], dsem_c[1])
        P.barrier(['identb', 'onesb', 'bdb', 'w2a2', 'g2', 'wrbd', 'wibd'])
        prm = [tok32[0], tok32[1]]
        rows = []
        rows.append((lng[:].rearrange('p l c -> p (l c)'), di['prm'][0:32, :], 'lng'))
        rows.append((lnb[:].rearrange('p l c -> p (l c)'), di['prm'][32:64, :], 'lnb'))
        rows.append((mu[:], di['prm'][64:90, :], 'mu'))
        rows.append((cw[:].rearrange('p l c -> p (l c)'), di['prm'][90:122, :], 'cw'))
        for vi_, v in enumerate(VEC_NAMES):
            rows.append((vec[v][:], di['prm'][122 + 8 * vi_:130 + 8 * vi_, :], 'v_' + v))
        groups = [[]]
        cnt = 0
        for r_ in rows:
            k_ = r_[1].shape[0]
            if cnt + k_ > 128:
                groups.append([]); cnt = 0
            groups[-1].append((cnt, k_) + r_)
            cnt += k_
        for gi_, grp in enumerate(groups if 'd' in PARTS else []):
            tk = prm[gi_ % 2]
            tot = 0
            for (o_, k_, dst, src, key) in grp:
                dma('sp', tk[o_:o_ + k_, 0:128], src, [], [('tok32', gi_ % 2)], dsem_c[2 + gi_ % 2])
                tot = o_ + k_
            pb = bank()
            tr(ps[:, pb, 0:tot], tk[0:tot, 0:128], ident[0:tot, 0:tot], [('tok32', gi_ % 2), 'ident'], [('ps', pb)])
            for (o_, k_, dst, src, key) in grp:
                cp(dst, ps[:, pb, o_:o_ + k_], [('ps', pb)], [key])
        if 'e' in PARTS:
            ts(omu[:], mu[:], -1.0, 1.0, ALU.mult, ALU.add, ['mu'], ['omu'])
            ts(oka[:], vec['k_a'][:], -1.0, 1.0, ALU.mult, ALU.add, ['v_k_a'], ['oka'])
            act(lsp[:], vec['lru_lambda'][:], AF.Exp, ['v_lru_lambda'], ['lsp'], scale=-1.0)
            act(lsp[:], lsp[:], AF.Ln, ['lsp'], ['lsp'], bias=1.0)
            ts(lsp[:], lsp[:], -8.0, None, ALU.mult, None, ['lsp'], ['lsp'])

        def wblocks():
            def ffn_blocks(wi, wo):
                for g in range(11):
                    def f(buf, g=g, wi=wi):
                        v = buf[:, 0:4096].rearrange('p (k c) -> p k c', k=8)
                        return [(v[:, :, 0:256], di[wi][:, g * 256:(g + 1) * 256].rearrange('(k p) c -> p k c', p=128)),
                                (v[:, :, 256:512], di[wi][:, DFF + g * 256:DFF + (g + 1) * 256].rearrange('(k p) c -> p k c', p=128))]
                    yield ((wi, g), f)
                for mp in range(8):
                    def f(buf, mp=mp, wo=wo):
                        v = buf[:, 0:NJ * 128].rearrange('p (j c) -> p j c', j=NJ)
                        return [(v, di[wo][:, mp * 128:(mp + 1) * 128].rearrange('(j p) c -> p j c', p=128))]
                    yield ((wo, mp), f)

            def sq_blocks(w, ncols):
                nb = (ncols + 511) // 512
                for b in range(nb):
                    c0 = b * 512
                    cn = min(512, ncols - c0)
                    def f(buf, c0=c0, cn=cn, w=w):
                        v = buf[:, 0:8 * cn].rearrange('p (k c) -> p k c', k=8)
                        return [(v, di[w][:, c0:c0 + cn].rearrange('(k p) c -> p k c', p=128))]
                    yield ((w, b), f)
            yield from sq_blocks('xa_wk', D)
            yield from sq_blocks('xa_wv', D)
            def one_pass():
                yield from ffn_blocks('ffn1_wi', 'ffn1_wo')
                yield from sq_blocks('w_in', PW)
                yield from sq_blocks('w_mix_out', D)
                yield from sq_blocks('xa_wq', D)
                yield from sq_blocks('xa_wo', D)
                yield from ffn_blocks('ffn2_wi', 'ffn2_wo')
            for it in range(int(os.environ.get('NTI', NTILES)) + (1 if do_sample else 0)):
                for blk, (tag, f) in enumerate(one_pass()):
                    yield (tag, f, it, blk)

        wgen = wblocks()
        wstate = dict(issued=0, consumed=0, pending=[])

        NBLK = 59
        wsc = nc.dram_tensor('wsc', [NBLK, 128, WCAP], BF16, kind='Internal').ap()
        wbsem = [P.dma_sem() for i in range(NWB)]

        def w_used(tag):
            if tag[0].endswith('_wi'):
                return 4096
            if tag[0].endswith('_wo') and tag[0].startswith('ffn'):
                return NJ * 128
            ncols = PW if tag[0] == 'w_in' else D
            return 8 * min(512, ncols - tag[1] * 512)

        def w_issue():
            try:
                item = next(wgen)
            except StopIteration:
                return False
            i = wstate['issued'] % NWB
            if len(item) == 2:
                tag, f = item
                for (dst, src) in f(wbuf[i]):
                    dma('pool', dst, src, [], [('wbuf', i)], wsem[i])
            else:
                tag, f, it, blk = item
                used = w_used(tag)
                if it == 0:
                    for (dst, src) in f(wbuf[i]):
                        dma('pool', dst, src, [], [('wbuf', i)], wsem[i])
                    dma('sp', wsc[blk, :, 0:used], wbuf[i][:, 0:used], [('wbuf', i)], [('wsc', blk)], wbsem[i])
                else:
                    dma('pool', wbuf[i][:, 0:used], wsc[blk, :, 0:used], [('wsc', blk)], [('wbuf', i)], wsem[i])
            wstate['pending'].append((tag, i))
            wstate['issued'] += 1
            return True

        def w_next(tag):
            while wstate['issued'] - wstate['consumed'] < NWB:
                if not w_issue():
                    break
            t, i = wstate['pending'].pop(0)
            assert t == tag, (t, tag)
            wstate['consumed'] += 1
            return wbuf[i], ('wbuf', i)

        def sqview(buf, cn):
            return buf[:, 0:8 * cn].rearrange('p (k c) -> p k c', k=8)

        def load_tokens_fm(src_rows_ap, nrows, dst32, dstb, col0, dkey):
            tokctr[0] += 1
            i = tokctr[0] % 2 if 'A' in os.environ.get('TOG', 'A') else 0
            tk = tok32[i]
            dma('sp', tk[0:nrows, :], src_rows_ap, [], [('tok32', i)], dsem_in[i])
            for half in range(2):
                b = bank()
                for q in range(4):
                    kc = half * 4 + q
                    P.op('pe', lambda e, b=b, q=q, kc=kc: e.transpose(ps[:, b, q * 128:q * 128 + nrows], tk[0:nrows, kc * 128:(kc + 1) * 128], ident[0:nrows, 0:nrows]),
                         reads=[('tok32', i), 'ident'], writes=[('ps', b)], track=('W' not in os.environ.get('TOG', '')))
                src = ps[:, b, :].rearrange('p (q t) -> p q t', q=4)[:, :, 0:nrows]
                if dst32 is not None:
                    cp(dst32[:, half * 4:half * 4 + 4, col0:col0 + nrows], src, [('ps', b), ('tok32', i)], [dkey], eng='act')
                if dstb is not None:
                    cp(dstb[:, half * 4:half * 4 + 4, col0:col0 + nrows], src, [('ps', b)], ['hb' if dkey == 'h32' else 'H3'], eng='dve')

        def store_fm_tokens(src32, skey, col0, nrows, dst_rows_ap, nch=8, feat0=0):
            i = bank_ctr[0] % 2
            tk = tok32[i]
            for g0 in range(0, nch, 4):
                b = bank()
                gn = min(4, nch - g0)
                for q in range(gn):
                    tr(ps[0:nrows, b, q * 128:(q + 1) * 128], src32[:, g0 + q, col0:col0 + nrows], ident[:, :],
                       [skey, 'ident'], [('ps', b)])
                cp(tk[0:nrows, g0 * 128:(g0 + gn) * 128], ps[0:nrows, b, 0:gn * 128], [('ps', b)], [('tok32', i)], eng='act')
            dma('sp', dst_rows_ap, tk[0:nrows, 0:nch * 128], [('tok32', i)], [], dsem_out[i], is_out=True)

        def layernorm(idx, n, eps):
            P.phase = 'layernorm'
            zsq = HB_[0]
            cp(hb[:, :, 0:n], h32[:, :, 0:n], ['h32'], ['hb'], eng='dve')
            act(zsq[:, :, 0:n], h32[:, :, 0:n], AF.Square, ['h32'], ['H0'])
            b1 = bank(); b2 = bank()
            for kc in range(8):
                mm(ps[:, b1, 0:n], onesb[:], hb[:, kc, 0:n], kc == 0, kc == 7, ['onesb', 'hb'], [('ps', b1)])
            for kc in range(8):
                mm(ps[:, b2, 0:n], onesb[:], zsq[:, kc, 0:n], kc == 0, kc == 7, ['onesb', 'H0'], [('ps', b2)])
            mean, msq, var, rstd, nmr = [s[:, 0:n] for s in st1]
            ts(mean, ps[:, b1, 0:n], 1.0 / D, None, ALU.mult, None, [('ps', b1)], ['st0'])
            tt(msq, mean, mean, ALU.mult, ['st0'], ['st1'])
            stt(var, ps[:, b2, 0:n], 1.0 / D, msq, ALU.mult, ALU.subtract, [('ps', b2), 'st1'], ['st2'])
            ts(var, var, 0.0, eps, ALU.max, ALU.add, ['st2'], ['st2'])
            act(var, var, AF.Sqrt, ['st2'], ['st2'])
            recip(rstd, var, ['st2'], ['st3'])
            tt(nmr, mean, rstd, ALU.mult, ['st0', 'st3'], ['st4'])
            fine = [('h32', kc) for kc in range(8)]
            P.op('dve', lambda e: e.engine_nop(), writes=['h32'] + fine)
            tpool = [(st1[1], 'st1'), (st1[2], 'st2'), (st2[0], 'su0'), (st2[1], 'su1'), (st2[2], 'su2'), (st2[3], 'su3'), (st2[4], 'su4'), (sgb[0], ('sg', 0))]
            for kc in range(8):
                T_, tk_ = tpool[kc]
                T_ = T_[:, 0:n]
                tt(T_, h32[:, kc, 0:n], rstd, ALU.mult, [('h32', kc), 'st3'], [tk_])
                tt(T_, T_, nmr, ALU.subtract, [tk_, 'st4'], [tk_])
                act(h32[:, kc, 0:n], T_, AF.Identity, [tk_, 'lng', 'lnb'], [('h32', kc)],
                    bias=lnb[:, idx, kc:kc + 1], scale=lng[:, idx, kc:kc + 1])
                act(hb[:, kc, 0:n], T_, AF.Identity, [tk_, 'lng', 'lnb'], ['hb'],
                    bias=lnb[:, idx, kc:kc + 1], scale=lng[:, idx, kc:kc + 1])
            P.op('dve', lambda e: e.engine_nop(), writes=['h32'] + fine)

        def ffn(wi, wo, ln_idx, n):
            P.phase = 'ffn'
            actb = HX
            sg = [sgb[0][:, 0:n], sgb[1][:, 0:n]]
            for g in range(11):
                wb, wk = w_next((wi, g))
                wv = sqview(wb, 512)
                for jj in range(2):
                    j = 2 * g + jj
                    pg = bank(); pu = bank()
                    for kc in range(8):
                        mm(ps[:, pg, 0:n], wv[:, kc, jj * 128:(jj + 1) * 128], hb[:, kc, 0:n], kc == 0, kc == 7, [wk, 'hb'], [('ps', pg)])
                    for kc in range(8):
                        mm(ps[:, pu, 0:n], wv[:, kc, 256 + jj * 128:256 + (jj + 1) * 128], hb[:, kc, 0:n], kc == 0, kc == 7, [wk, 'hb'], [('ps', pu)])
                    act(sg[jj], ps[:, pg, 0:n], AF.Silu, [('ps', pg)], [('sg', jj)])
                    tt(actb[:, j, 0:n], sg[jj], ps[:, pu, 0:n], ALU.mult, [('sg', jj), ('ps', pu)], ['HX%d' % (j // 8)])
            for m in range(8):
                wb, wk = w_next((wo, m))
                wv = wb[:, 0:NJ * 128].rearrange('p (j c) -> p j c', j=NJ)
                po = bank()
                for j in range(NJ):
                    mm(ps[:, po, 0:n], wv[:, j, :], actb[:, j, 0:n], j == 0, j == NJ - 1, [wk, 'HX%d' % (j // 8)], [('ps', po)])
                stt(h32[:, m, 0:n], ps[:, po, 0:n], 0.5 / ALPHA, h32[:, m, 0:n], ALU.mult, ALU.add, [('ps', po), 'h32'], ['h32'])
            layernorm(ln_idx, n, LN_EPS / (ALPHA * ALPHA))

        def proj(w, ncols, xin, xkey, n, evac):
            nb = (ncols + 511) // 512
            for b in range(nb):
                c0 = b * 512
                cn = min(512, ncols - c0)
                wb, wk = w_next((w, b))
                wv = sqview(wb, cn)
                for q in range(cn // 128):
                    m = c0 // 128 + q
                    pb = bank()
                    for kc in range(8):
                        mm(ps[:, pb, 0:n], wv[:, kc, q * 128:(q + 1) * 128], xin[:, kc, 0:n], kc == 0, kc == 7, [wk, xkey], [('ps', pb)])
                    evac(m, pb)

        def mem_kv():
            P.phase = 'mem_kv'
            for r in range(2):
                load_tokens_fm(di['mem'][r * 128:(r + 1) * 128, :], 128, None, memT, r * 128, 'memT')
            for (w, outname, isk) in [('xa_wk', 'pmk', True), ('xa_wv', 'pmv', False)]:
                for b in range(2):
                    wb, wk = w_next((w, b))
                    wv = sqview(wb, 512)
                    for r in range(2):
                        pb = bank()
                        for kc in range(8):
                            mm(ps[:, pb, :], memT[:, kc, r * 128:(r + 1) * 128], wv[:, kc, :], kc == 0, kc == 7, ['H3', wk], [('ps', pb)])
                        i = bank_ctr[0] % 2
                        cp(tok32[i][:, 0:512], ps[:, pb, :], [('ps', pb)], [('tok32', i)], eng='act')
                        if not isk:
                            cp(mvb[:, r, b * 512:(b + 1) * 512], ps[:, pb, :], [('ps', pb)], ['mvb'], eng='dve')
                        dma('sp', di[outname][r * 128:(r + 1) * 128, b * 512:(b + 1) * 512], tok32[i][:, 0:512], [('tok32', i)], [], dsem_out[i], is_out=True)
                    if isk:
                        for q in range(4):
                            m = b * 4 + q
                            pb = bank()
                            for kc in range(8):
                                mm(ps[:, pb, 0:NMEM], wv[:, kc, q * 128:(q + 1) * 128], memT[:, kc, :], kc == 0, kc == 7, [wk, 'H3'], [('ps', pb)])
                            cp(mkT[:, m, :], ps[:, pb, 0:NMEM], [('ps', pb)], ['mkT'], eng='act')

        def mixer_prompt(ti):
            P.phase = 'mixer_prompt'
            n = NT
            r32, k32, v32, a32, ls32 = FB_
            glb = HB_[1]; geb = HB_[2]; g0b = HB_[3]
            g1b = HX[:, 0:8, :]; xcb = HX[:, 8:16, :]
            first = (ti == 0)
            tmp = st1[0]

            def evac(m, pb):
                psn = ps[:, pb, 0:n]
                SS, KP = stsel(m)
                tmp = SS[0]
                if m < 26:
                    act(tmp[:, 1:n], ps[:, pb, 0:n - 1], AF.Copy, [('ps', pb), 'mu'], [KP + '0'], scale=mu[:, m:m + 1])
                    if first:
                        P.op('dve', lambda e, tmp=tmp: e.memset(tmp[:, 0:1], 0.0), writes=[KP + '0'])
                    else:
                        tt(tmp[:, 0:1], carry_sh[:, m:m + 1], mu[:, m:m + 1], ALU.mult, ['carry_sh', 'mu'], [KP + '0'])
                    cp(carry_sh[:, m:m + 1], ps[:, pb, n - 1:n], [('ps', pb)], ['carry_sh'])
                    if m < 24:
                        dst = [r32, k32, v32][m // 8][:, m % 8, :]
                        dkey = ['F0', 'F1', 'F2'][m // 8]
                        stt(dst, psn, omu[:, m:m + 1], tmp[:, 0:n], ALU.mult, ALU.add, [('ps', pb), 'omu', KP + '0'], [dkey])
                    else:
                        xs_ = SS[1][:, 0:n]
                        stt(xs_, psn, omu[:, m:m + 1], tmp[:, 0:n], ALU.mult, ALU.add, [('ps', pb), 'omu', KP + '0'], [KP + '1'])
                        lb = SS[2][:, 0:n].bitcast(BF16)[:, 0:n]
                        if m == 24:
                            act(lb[0:64, :], xs_[0:64, :], AF.Tanh, [KP + '1'], [KP + '2'])
                            cp(lb[64:128, :], xs_[64:128, :], [KP + '1'], [KP + '2'])
                            for (lo, dstt, dk, bvec) in [(0, ls32, 'F4', 'decay_w0'), (64, a32, 'F3', 'aaa_a0')]:
                                for q in range(8):
                                    p2 = bank()
                                    mm(ps[:, p2, 0:n], w2a2[lo:lo + 64, q * 128:(q + 1) * 128], lb[lo:lo + 64, :], True, True, ['w2a2', KP + '2'], [('ps', p2)])
                                    act(dstt[:, q, :], ps[:, p2, 0:n], AF.Sigmoid, [('ps', p2), 'v_' + bvec], [dk], bias=vec[bvec][:, q:q + 1])
                        else:
                            act(lb, xs_, AF.Sigmoid, [KP + '1'], [KP + '2'])
                            for q in range(8):
                                p2 = bank()
                                mm(ps[:, p2, 0:n], g2[:, q * 128:(q + 1) * 128], lb, True, True, ['g2', KP + '2'], [('ps', p2)])
                                cp(glb[:, q, :], ps[:, p2, 0:n], [('ps', p2)], ['H1'], eng='act')
                elif m < 34:
                    cp(pl[:, m - 26, 3:3 + n], psn, [('ps', pb)], ['pl'], eng='act')
                elif m < 42:
                    act(geb[:, m - 34, :], psn, AF.Gelu, [('ps', pb)], ['H2'])
                elif m < 50:
                    act(g0b[:, m - 42, :], psn, AF.Sigmoid, [('ps', pb)], ['H3'])
                else:
                    act(g1b[:, m - 50, :], psn, AF.Sigmoid, [('ps', pb)], ['HX0'])

            if first:
                P.op('dve', lambda e: e.memset(pl[:, :, 0:3], 0.0), writes=['pl'])
            else:
                cp(pl[:, :, 0:3], osml[:, :, 4:7], ['osml'], ['pl'])
            proj('w_in', PW, hb, 'hb', n, evac)
            cp(osml[:, :, 4:7], pl[:, :, n:n + 3], ['pl'], ['osml'])
            dump('r32', r32[:], 'F0'); dump('k32', k32[:], 'F1'); dump('v32', v32[:], 'F2'); dump('a32', a32[:], 'F3'); dump('ls32', ls32[:], 'F4')
            if ti == int(os.environ.get('NTI', NTILES)) - 1:
                cp(osml[:, :, 0:3], pl[:, :, n:n + 3], ['pl'], ['osml'])
                cp(osh[:], carry_sh[:], ['carry_sh'], ['osh'])
            return dict(r32=r32, k32=k32, v32=v32, a32=a32, ls32=ls32, glb=glb, geb=geb, g0b=g0b, g1b=g1b, xcb=xcb)

        def lru_prompt(ti, B):
            P.phase = 'lru_prompt'
            n = NT
            geb, g1b, xcb = B['geb'], B['g1b'], B['xcb']
            lmb = HX[:, 16:24, :]
            def lru_chunk(c):
                SS, KP = stsel(c)
                xc = SS[0][:, 0:n]; gr = SS[1][:, 0:n]; gi = SS[2][:, 0:n]; t3 = SS[3][:, 0:n]; hs = SS[4][:, 0:n]
                act(xc, pl[:, c, 3:3 + n], AF.Identity, ['pl', 'cw', 'v_conv_b'], [KP + '0'], bias=vec['conv_b'][:, c:c + 1], scale=cw[:, 3, c:c + 1])
                for j in range(3):
                    stt(xc, pl[:, c, j:j + n], cw[:, j, c:c + 1], xc, ALU.mult, ALU.add, ['pl', 'cw', KP + '0'], [KP + '0'])
                yield
                cp(xcb[:, c, :], xc, [KP + '0'], ['HX1'], eng='act')
                p1 = bank(); p2 = bank()
                yield
                mm(ps[:, p1, 0:n], wrbd[:, c, :], xcb[:, c, :], True, True, ['wrbd', 'HX1'], [('ps', p1)])
                yield
                mm(ps[:, p2, 0:n], wibd[:, c, :], xcb[:, c, :], True, True, ['wibd', 'HX1'], [('ps', p2)])
                yield
                act(gr, ps[:, p1, 0:n], AF.Sigmoid, [('ps', p1), 'v_lru_br'], [KP + '1'], bias=vec['lru_br'][:, c:c + 1])
                yield
                act(gi, ps[:, p2, 0:n], AF.Sigmoid, [('ps', p2), 'v_lru_bi'], [KP + '2'], bias=vec['lru_bi'][:, c:c + 1])
                yield
                act(gr, gr, AF.Exp, [KP + '1', 'lsp'], [KP + '1'], scale=lsp[:, c:c + 1])
                yield
                act(t3, gr, AF.Square, [KP + '1'], [KP + '3'])
                yield
                ts(t3, t3, -1.0, 1.0, ALU.mult, ALU.add, [KP + '3'], [KP + '3'])
                yield
                ts(t3, t3, 0.0, None, ALU.max, None, [KP + '3'], [KP + '3'])
                yield
                act(t3, t3, AF.Sqrt, [KP + '3'], [KP + '3'])
                yield
                tt(gi, gi, xc, ALU.mult, [KP + '2', KP + '0'], [KP + '2'])
                yield
                tt(gi, gi, t3, ALU.mult, [KP + '2', KP + '3'], [KP + '2'])
                yield
                if ti == 0:
                    P.op('dve', lambda e, hs=hs, gr=gr, gi=gi: e.tensor_tensor_scan(out=hs, data0=gr, data1=gi, initial=0.0, op0=ALU.mult, op1=ALU.add),
                         reads=[KP + '1', KP + '2'], writes=[KP + '4'])
                else:
                    P.op('dve', lambda e, c=c, hs=hs, gr=gr, gi=gi: e.tensor_tensor_scan(out=hs, data0=gr, data1=gi, initial=carry_h[:, c:c + 1], op0=ALU.mult, op1=ALU.add),
                         reads=[KP + '1', KP + '2', ('carry_h', c)], writes=[KP + '4'])
                yield
                cp(carry_h[:, c:c + 1], hs[:, n - 1:n], [KP + '4'], [('carry_h', c)])
                yield
                tt(t3, hs, geb[:, c, :], ALU.mult, [KP + '4', 'H2'], [KP + '3'])
                yield
                tt(lmb[:, c, :], t3, g1b[:, c, :], ALU.mult, [KP + '3', 'HX0'], ['HX2'])
            for _c0 in range(0, 8, 2):
                interleave2(lru_chunk(_c0), lru_chunk(_c0 + 1))
            if ti == int(os.environ.get('NTI', NTILES)) - 1:
                cp(osml[:, :, 3:4], carry_h[:].unsqueeze(2), [('carry_h', c_) for c_ in range(8)], ['osml'])
            return lmb


        plb = pl[:].rearrange('p a b -> p (a b)').bitcast(BF16)
        plA = plb[:, 0:8 * NT].rearrange('p (a b) -> p a b', a=8)
        plB = plb[:, 8 * NT:16 * NT].rearrange('p (a b) -> p a b', a=8)

        def rwkv_core(B, state_in, state_out, skip_inverse=False, prepared=None):
            P.phase = 'rwkv_core'
            n = NT
            r32, k32, v32, a32, ls32 = B['r32'], B['k32'], B['v32'], B['a32'], B['ls32']
            bc8 = lambda v: v[:].unsqueeze(2).to_broadcast([128, 8, n])
            QR = HX[:, 0:16, :].rearrange('p (k two) (c t) -> p k two c t', two=2, t=C)
            KT = HB_[0]; NB = hb
            vb = plB
            if prepared is not None:
                prepared(QR, KT, NB, vb, WCs)
            else:
                kk32 = pl[:, :, 0:n]
                tt(kk32, k32[:], bc8(vec['k_k']), ALU.mult, ['F1', 'v_k_k'], ['pl'])
                act(KT[:], kk32, AF.Square, ['pl'], ['H0'])
                rkb = HB_[2]

                def prep_chunk(kc):
                    SS, KP = stsel(kc)
                    s_ = SS[0][:, 0:n]; u_ = SS[1][:, 0:n]; u2 = SS[2][:, 0:n]
                    pb = bank()
                    mm(ps[:, pb, 0:n], bdb[:], KT[:, kc, :], True, True, ['bdb', 'H0'], [('ps', pb)])
                    act(s_, ps[:, pb, 0:n], AF.Sqrt, [('ps', pb)], [KP + '0'])
                    act(u_, a32[:, kc, :], AF.Identity, ['F3', 'v_k_a', 'oka'], [KP + '1'], bias=oka[:, kc:kc + 1], scale=vec['k_a'][:, kc:kc + 1])
                    yield
                    ts(s_, s_, 1e-12, None, ALU.max, None, [KP + '0'], [KP + '0'])
                    yield
                    recip(s_, s_, [KP + '0'], [KP + '0'])
                    yield
                    tt(kk32[:, kc, :], kk32[:, kc, :], s_, ALU.mult, ['pl', KP + '0'], ['pl'])
                    yield
                    tt(k32[:, kc, :], k32[:, kc, :], u_, ALU.mult, ['F1', KP + '1'], ['F1'])
                    yield
                    tt(a32[:, kc, :], a32[:, kc, :], kk32[:, kc, :], ALU.mult, ['F3', 'pl'], ['F3'])
                    yield
                    tt(u2, r32[:, kc, :], k32[:, kc, :], ALU.mult, ['F0', 'F1'], [KP + '2'])
                    yield
                    act(rkb[:, kc, :], u2, AF.Copy, [KP + '2', 'v_r_k'], ['H2'], scale=vec['r_k'][:, kc:kc + 1])
                for _c0 in range(0, 8, 2):
                    interleave2(prep_chunk(_c0), prep_chunk(_c0 + 1))
                P.phase = 'rw_decay'
                def decay_chunk(kc):
                    SS, KP = stsel(kc)
                    cs = SS[0][:, 0:n]; dd = SS[1][:, 0:n]; Wi = SS[2][:, 0:n]; We = SS[3][:, 0:n]; Wv = SS[4][:, 0:n]
                    P.op('dve', lambda e, kc=kc, cs=cs: e.tensor_tensor_scan(out=cs, data0=reset[:, 0:n], data1=ls32[:, kc, :], initial=0.0, op0=ALU.mult, op1=ALU.add),
                         reads=['reset', 'F4'], writes=[KP + '0'])
                    yield
                    tt(dd, cs, ls32[:, kc, :], ALU.subtract, [KP + '0', 'F4'], [KP + '1'])
                    yield
                    act(Wi, cs, AF.Exp, [KP + '0'], [KP + '2'], scale=-C0)
                    yield
                    act(We, dd, AF.Exp, [KP + '1'], [KP + '3'], scale=-C0)
                    yield
                    act(Wv, cs, AF.Exp, [KP + '0'], [KP + '4'], scale=C0)
                    c4 = lambda a: a.rearrange('p (c t) -> p c t', t=C)
                    yield
                    tt(QR[:, kc, 1, :, :], c4(r32[:, kc, :]), c4(Wi), ALU.mult, ['F0', KP + '2'], ['HX0', 'HX1'])
                    yield
                    tt(QR[:, kc, 0, :, :], c4(kk32[:, kc, :]), c4(We), ALU.mult, ['pl', KP + '3'], ['HX0', 'HX1'])
                    yield
                    cp(WCs[:, kc, :], c4(Wi)[:, :, C - 1], [KP + '2'], ['WCs'])
                    yield
                    tt(Wi, k32[:, kc, :], Wv, ALU.mult, ['F1', KP + '4', KP + '2'], [KP + '2'])
                    yield
                    stt(We, a32[:, kc, :], -1.0, Wv, ALU.mult, ALU.mult, ['F3', KP + '4', KP + '3'], [KP + '3'])
                    yield
                    cp(NB[:, kc, :], We, [KP + '3'], ['hb'], eng='act')
                    yield
                    cp(KT[:, kc, :], Wi, [KP + '2'], ['H0'], eng='act')
                for _c0 in range(0, 8, 2):
                    interleave2(decay_chunk(_c0), decay_chunk(_c0 + 1))
                P.phase = 'rw_tok'
                vb = plB
                cp(vb[:, :, 0:n], v32[:], ['F2'], ['pl'], eng='act')
                for kc in range(8):
                    pb = bank()
                    mm(ps[:, pb, 0:n], bdb[:], rkb[:, kc, :], True, True, ['bdb', 'H2'], [('ps', pb)])
                    tt(v32[:, kc, :], v32[:, kc, :], ps[:, pb, 0:n], ALU.mult, ['F2', ('ps', pb)], ['F2'])
            tmv = lambda a: a.rearrange('p a b -> p (a b)').rearrange('p (c x) -> p c x', c=NCH)
            vT = tmv(HB_[2][:]); kTt = tmv(plA); nbT = tmv(plB)

            def to_tokmajor(src, skey, dst, dkey):
                for c in range(NCH):
                    pb = bank()
                    pbv = ps[:, pb, :].bitcast(BF16)
                    for hp in range(8):
                        for par in range(2):
                            lo = par * 64
                            tr(pbv[lo:lo + 64, hp * 64:(hp + 1) * 64], src[lo:lo + 64, hp, c * C:(c + 1) * C], identb[lo:lo + 64, lo:lo + 64],
                               [skey, 'identb'], [('ps', pb)])
                    cp(dst[:, c, :], pbv[:, 0:512], [('ps', pb)], [dkey], eng=('act' if c % 2 else 'dve'))
            to_tokmajor(vb, 'pl', vT, 'H2')
            to_tokmajor(KT, 'H0', kTt, 'pl')
            to_tokmajor(NB, 'hb', nbT, 'pl')
            vTv = lambda c, hp, lo: vT[lo:lo + 64, c, hp * 64:(hp + 1) * 64]
            kTv = lambda c, hp, lo: kTt[lo:lo + 64, c, hp * 64:(hp + 1) * 64]
            nTv = lambda c, hp, lo: nbT[lo:lo + 64, c, hp * 64:(hp + 1) * 64]
            m_su_ui = mask[:, 0:128].unsqueeze(1).to_broadcast([128, 8, 128])
            m_sl = mask[:, 128:192].unsqueeze(1).to_broadcast([128, 8, 64])
            m_eye = mask[:, 192:256].unsqueeze(1).to_broadcast([128, 8, 64])
            def ph1(c0):
                P.phase = 'rw_ph1'
                ctx = []
                for s in range(2):
                    c = c0 + s
                    b1a = bank(); b1b = bank()
                    for hp in range(8):
                        for par in range(2):
                            lo = par * 64
                            bsel = b1a if hp < 4 else b1b
                            mm(ps[lo:lo + 64, bsel, (hp % 4) * 128:(hp % 4 + 1) * 128], KT[lo:lo + 64, hp, c * C:(c + 1) * C],
                               QR[lo:lo + 64, hp, :, c, :], True, True, ['H0', 'HX0', 'HX1'], [('ps', bsel)])
                    for (bsel, h0) in [(b1a, 0), (b1b, 4)]:
                        tt(LP[:, h0:h0 + 4, c, :], ps[:, bsel, :].rearrange('p (h x) -> p h x', h=4), m_su_ui[:, 0:4, :], ALU.mult,
                           [('ps', bsel), 'mask'], [('LP', c)])
                    b2a = bank(); b2b = bank()
                    for hp in range(8):
                        for par in range(2):
                            lo = par * 64
                            bsel = b2a if hp < 4 else b2b
                            mm(ps[lo:lo + 64, bsel, (hp % 4) * 128:(hp % 4 + 1) * 128], NB[lo:lo + 64, hp, c * C:(c + 1) * C],
                               QR[lo:lo + 64, hp, :, c, :], True, True, ['hb', 'HX0', 'HX1'], [('ps', bsel)])
                    for (bsel, h0) in [(b2a, 0), (b2b, 4)]:
                        tt(PN[:, h0:h0 + 4, c, :], ps[:, bsel, :].rearrange('p (h x) -> p h x', h=4), m_su_ui[:, 0:4, :], ALU.mult,
                           [('ps', bsel), 'mask'], [('PN', c)])
                    b3 = bank()
                    for hp in range(8):
                        for par in range(2):
                            lo = par * 64
                            mm(ps[lo:lo + 64, b3, hp * 64:(hp + 1) * 64], QR[lo:lo + 64, hp, 0, c, :], NB[lo:lo + 64, hp, c * C:(c + 1) * C],
                               True, True, ['HX0', 'HX1', 'hb'], [('ps', b3)])
                    pp = PP[s][0]
                    cp(pp[:, :, 0:64], PN[:, :, c, 0:64], [('PN', c)], [('PP', s, 0)], eng='act')
                    tt(pp[:, :, 64:128], ps[:, b3, :].rearrange('p (h x) -> p h x', h=8), m_sl, ALU.mult, [('ps', b3), 'mask'], [('PP', s, 0)])
                    tt(X32[s][:], PN[:, :, c, 0:64], m_eye, ALU.add, [('PN', c), 'mask'], [('X32', s)])
                    cp(Xb[s][:], X32[s][:], [('X32', s)], [('Xb', s)], eng='act')
                    ctx.append(c)
                if skip_inverse:
                    for s_ in range(2):
                        cp(XT[:, :, ctx[s_], :], X32[s_][:], [('X32', s_)], [('XT', ctx[s_])], eng='act')
                for lvl in ([] if skip_inverse else range(1, 6)):
                    yield
                    P.phase = 'rw_ph1'
                    cur = (lvl - 1) % 2; nxt = lvl % 2
                    banks = []
                    for s in range(2):
                        ba = bank(); bb = bank()
                        src = PP[s][cur]
                        for hp in range(8):
                            for par in range(2):
                                lo = par * 64
                                bsel = ba if hp < 4 else bb
                                o0 = (hp % 4) * 128
                                mm(ps[lo:lo + 64, bsel, o0:o0 + 64], src[lo:lo + 64, hp, 64:128], src[lo:lo + 64, hp, 0:64], True, True,
                                   [('PP', s, cur)], [('ps', bsel)])
                                mm(ps[lo:lo + 64, bsel, o0 + 64:o0 + 128], src[lo:lo + 64, hp, 0:64], src[lo:lo + 64, hp, 64:128], True, True,
                                   [('PP', s, cur)], [('ps', bsel)])
                        banks.append((ba, bb))
                    for s in range(2):
                        ba, bb = banks[s]
                        dst = PP[s][nxt]
                        cp(dst[:, 0:4, :], ps[:, ba, :].rearrange('p (h x) -> p h x', h=4), [('ps', ba)], [('PP', s, nxt)], eng='act')
                        cp(dst[:, 4:8, :], ps[:, bb, :].rearrange('p (h x) -> p h x', h=4), [('ps', bb)], [('PP', s, nxt)], eng='dve')
                    xb_ = []
                    for s in range(2):
                        bx = bank()
                        src = PP[s][nxt]
                        for hp in range(8):
                            for par in range(2):
                                lo = par * 64
                                mm(ps[lo:lo + 64, bx, hp * 64:(hp + 1) * 64], src[lo:lo + 64, hp, 64:128], Xb[s][lo:lo + 64, hp, :], True, True,
                                   [('PP', s, nxt), ('Xb', s)], [('ps', bx)])
                        xb_.append(bx)
                    for s in range(2):
                        bx = xb_[s]
                        tt(X32[s][:], X32[s][:], ps[:, bx, :].rearrange('p (h x) -> p h x', h=8), ALU.add, [('X32', s), ('ps', bx)], [('X32', s)])
                        if lvl < 5:
                            cp(Xb[s][:], X32[s][:], [('X32', s)], [('Xb', s)], eng='act')
                        else:
                            cp(XT[:, :, ctx[s], :], X32[s][:], [('X32', s)], [('XT', ctx[s])], eng='act')
            Y32 = FB_[0]

            def ph2(c):
                P.phase = 'rw_ph2'
                state_in(c)
                cur = c % 2
                a0 = A0b[cur]
                bR = bank()
                for hp in range(8):
                    for par in range(2):
                        lo = par * 64
                        o = ps[lo:lo + 64, bR, hp * 64:(hp + 1) * 64]
                        mm(o, QR[lo:lo + 64, hp, 0, c, :], a0[lo:lo + 64, hp, :], True, False, ['HX0', 'HX1', ('A0b', cur)], [('ps', bR)])
                        mm(o, LP[lo:lo + 64, hp, c, 0:64], vTv(c, hp, lo), False, True, [('LP', c), 'H2'], [('ps', bR)])
                cp(RHSb[:], ps[:, bR, :].rearrange('p (h x) -> p h x', h=8), [('ps', bR)], ['RHSb'], eng='act')
                yield
                P.phase = 'rw_ph2'
                bU = bank()
                for hp in range(8):
                    for par in range(2):
                        lo = par * 64
                        mm(ps[lo:lo + 64, bU, hp * 64:(hp + 1) * 64], XT[lo:lo + 64, hp, c, :], RHSb[lo:lo + 64, hp, :], True, True,
                           [('XT', c), 'RHSb'], [('ps', bU)])
                cp(Ub[:], ps[:, bU, :].rearrange('p (h x) -> p h x', h=8), [('ps', bU)], ['Ub'], eng='act')
                yield
                P.phase = 'rw_ph2'
                bD = bank()
                for hp in range(8):
                    for par in range(2):
                        lo = par * 64
                        o = ps[lo:lo + 64, bD, hp * 64:(hp + 1) * 64]
                        mm(o, kTv(c, hp, lo), vTv(c, hp, lo), True, False, ['pl', 'H2'], [('ps', bD)])
                        mm(o, nTv(c, hp, lo), Ub[lo:lo + 64, hp, :], False, True, ['pl', 'Ub'], [('ps', bD)])
                bY = bank()
                for hp in range(8):
                    for par in range(2):
                        lo = par * 64
                        o = ps[lo:lo + 64, bY, hp * 64:(hp + 1) * 64]
                        mm(o, a0[lo:lo + 64, hp, :], QR[lo:lo + 64, hp, 1, c, :], True, False, [('A0b', cur), 'HX0', 'HX1'], [('ps', bY)])
                        mm(o, vTv(c, hp, lo), LP[lo:lo + 64, hp, c, 64:128], False, False, ['H2', ('LP', c)], [('ps', bY)])
                        mm(o, Ub[lo:lo + 64, hp, :], PN[lo:lo + 64, hp, c, 64:128], False, True, ['Ub', ('PN', c)], [('ps', bY)])
                tt(A32[:], A32[:], ps[:, bD, :].rearrange('p (h x) -> p h x', h=8), ALU.add, ['A32', ('ps', bD)], ['A32'])
                tt(A32[:], A32[:], WCs[:, :, c:c + 1].to_broadcast([128, 8, 64]), ALU.mult, ['A32', 'WCs'], ['A32'])
                cp(A0b[1 - cur][:], A32[:], ['A32'], [('A0b', 1 - cur)], eng='act')
                cp(Y32[:, :, c * C:(c + 1) * C], ps[:, bY, :].rearrange('p (h x) -> p h x', h=8), [('ps', bY)], ['F0'], eng='dve')
                state_out(c)
                yield

            def drain(g):
                for _ in g:
                    pass

            def interleave(ga, gb):
                a_ok = b_ok = True
                while a_ok or b_ok:
                    if a_ok:
                        try:
                            next(ga)
                        except StopIteration:
                            a_ok = False
                    if b_ok:
                        try:
                            next(gb)
                        except StopIteration:
                            b_ok = False

            def chain(*gs):
                for g in gs:
                    yield from g
            drain(ph1(0))
            if NCH == 4:
                interleave(ph1(2), chain(ph2(0), ph2(1)))
                drain(chain(ph2(2), ph2(3)))
            else:
                for c0 in range(2, NCH, 2):
                    drain(ph1(c0))
                for c in range(NCH):
                    drain(ph2(c))
            return Y32, v32

        def rwkv_post(Y, Ykey, bonus, bkey, glb, g0b, lmb, mb, n):
            P.phase = 'rwkv_post'
            Yb = HB_[0]; ysq = hb
            cp(Yb[:, :, 0:n], Y[:, :, 0:n], [Ykey], ['H0'], eng='act')
            act(ysq[:, :, 0:n], Y[:, :, 0:n], AF.Square, [Ykey], ['hb'])
            def post_chunk(kc):
                b1 = bank(); b2 = bank()
                mm(ps[:, b1, 0:n], bdb[:], Yb[:, kc, 0:n], True, True, ['bdb', 'H0'], [('ps', b1)])
                yield
                mm(ps[:, b2, 0:n], bdb[:], ysq[:, kc, 0:n], True, True, ['bdb', 'hb'], [('ps', b2)])
                SS, KP = stsel(kc)
                mean = SS[0][:, 0:n]; var = SS[1][:, 0:n]; t_ = SS[2][:, 0:n]
                yield
                ts(mean, ps[:, b1, 0:n], 1.0 / 64, None, ALU.mult, None, [('ps', b1)], [KP + '0'])
                yield
                tt(var, mean, mean, ALU.mult, [KP + '0'], [KP + '1'])
                yield
                stt(var, ps[:, b2, 0:n], 1.0 / 64, var, ALU.mult, ALU.subtract, [('ps', b2), KP + '1'], [KP + '1'])
                yield
                ts(var, var, 0.0, GN_EPS, ALU.max, ALU.add, [KP + '1'], [KP + '1'])
                yield
                act(var, var, AF.Sqrt, [KP + '1'], [KP + '1'])
                yield
                recip(var, var, [KP + '1'], [KP + '1'])
                yield
                tt(t_, Y[:, kc, 0:n], mean, ALU.subtract, [Ykey, KP + '0'], [KP + '2'])
                yield
                tt(t_, t_, var, ALU.mult, [KP + '2', KP + '1'], [KP + '2'])
                yield
                act(t_, t_, AF.Identity, [KP + '2', 'v_gn_g', 'v_gn_b'], [KP + '2'], bias=vec['gn_b'][:, kc:kc + 1], scale=vec['gn_g'][:, kc:kc + 1])
                yield
                tt(t_, t_, bonus[:, kc, 0:n], ALU.add, [KP + '2', bkey], [KP + '2'])
                yield
                tt(t_, t_, glb[:, kc, 0:n], ALU.mult, [KP + '2', 'H1'], [KP + '2'])
                yield
                tt(t_, t_, g0b[:, kc, 0:n], ALU.mult, [KP + '2', 'H3'], [KP + '2'])
                yield
                tt(mb[:, kc, 0:n], t_, lmb[:, kc, 0:n], ALU.add, [KP + '2', 'HX2'], ['H2'])
            for _c0 in range(0, 8, 2):
                interleave2(post_chunk(_c0), post_chunk(_c0 + 1))

        def resid_ln(w, xin, xkey, ln_idx, n):
            def evac(m, pb):
                stt(h32[:, m, 0:n], ps[:, pb, 0:n], 1.0 / ALPHA, h32[:, m, 0:n], ALU.mult, ALU.add, [('ps', pb), 'h32'], ['h32'])
            proj(w, D, xin, xkey, n, evac)
            layernorm(ln_idx, n, LN_EPS / (ALPHA * ALPHA))

        def xattn_prompt(n):
            P.phase = 'xattn_prompt'
            qb = HB_[0]; ob = HB_[1]; pT = HX[:, 0:8, :]
            def evq(m, pb):
                cp(qb[:, m, 0:n], ps[:, pb, 0:n], [('ps', pb)], ['H0'], eng='act')
            proj('xa_wq', D, hb, 'hb', n, evq)
            for h in range(4):
                for mc in range(2):
                    pb = bank()
                    for dc in range(2):
                        mm(ps[:, pb, 0:n], mkT[:, 2 * h + dc, mc * 128:(mc + 1) * 128], qb[:, 2 * h + dc, 0:n], dc == 0, dc == 1, ['mkT', 'H0'], [('ps', pb)])
                    act(pT[:, 2 * h + mc, 0:n], ps[:, pb, 0:n], AF.Exp, [('ps', pb)], ['HX0'], scale=1.0 / 16.0)
            rds = []
            for h in range(4):
                pd = bank()
                for mc in range(2):
                    mm(ps[:, pd, 0:n], onesb[:], pT[:, 2 * h + mc, 0:n], mc == 0, mc == 1, ['onesb', 'HX0'], [('ps', pd)])
                SS, KP = stsel(h)
                rd = SS[h // 2][:, 0:n]; rk_ = KP + str(h // 2)
                recip(rd, ps[:, pd, 0:n], [('ps', pd)], [rk_])
                rds.append((rd, rk_))
            for h in range(4):
                rd, rk_ = rds[h]
                for dc in range(2):
                    po = bank()
                    for mc in range(2):
                        mm(ps[:, po, 0:n], mvb[:, mc, (2 * h + dc) * 128:(2 * h + dc + 1) * 128], pT[:, 2 * h + mc, 0:n], mc == 0, mc == 1, ['mvb', 'HX0'], [('ps', po)])
                    tt(ob[:, 2 * h + dc, 0:n], ps[:, po, 0:n], rd, ALU.mult, [('ps', po), rk_], ['H1'])
            resid_ln('xa_wo', ob, 'H1', 2, n)

        if stage >= 1:
            mem_kv()
        if 'm' in PARTS or PARTS == 'abcdefg':
            P.op('dve', lambda e: e.memset(A32[:], 0.0), writes=['A32'])
            P.op('dve', lambda e: e.memset(A0b[0][:], 0.0), writes=[('A0b', 0)])
        for ti in range(int(os.environ.get('NTI', NTILES)) if stage >= 9 else 1):
            TOG = os.environ.get('TOG', '')
            for r in range(1 if '1' in TOG else NT // 128):
                load_tokens_fm(di['xp'][ti * NT + r * 128: ti * NT + (r + 1) * 128, :], 128, h32, None if 'D' in TOG else hb, r * 128, 'h32')
            if stage >= 2:
                ffn('ffn1_wi', 'ffn1_wo', 0, NT)
            if ti == 0:
                dump('h1', h32[:], 'h32')
            if stage < 3:
                break
            B = mixer_prompt(ti)
            if stage < 4:
                break
            lmb = lru_prompt(ti, B)
            if ti == 0:
                dump('lm', lmb, 'HX2')
            if stage < 5:
                break
            Y32, bonus = rwkv_core(B, lambda c: None, lambda c: None)
            if ti == 0:
                dump('Y', Y32[:], 'F0')
            if stage < 6:
                break
            mb = HB_[2]
            rwkv_post(Y32, 'F0', bonus, 'F2', B['glb'], B['g0b'], lmb, mb, NT)
            resid_ln('w_mix_out', mb, 'H2', 1, NT)
            if ti == 0:
                dump('h2', h32[:], 'h32')
            if stage < 7:
                break
            xattn_prompt(NT)
            if ti == 0:
                dump('h3', h32[:], 'h32')
            if stage < 8:
                break
            ffn('ffn2_wi', 'ffn2_wo', 3, NT)
            for r in range(NT // 128):
                store_fm_tokens(h32, 'h32', r * 128, 128, di['yp'][ti * NT + r * 128: ti * NT + (r + 1) * 128, :])
        if stage < 9:
            P.emit()
            return nc, P
        def prompt_outputs():
            pass
            Sout = FB_[1].rearrange('p a b -> p (a b)')[:, 0:1024].rearrange('p (hp par k) -> p hp par k', hp=8, par=2)
            for g in range(2):
                pb = bank()
                for q in range(4):
                    hp = g * 4 + q
                    tr(ps[0:64, pb, q * 128:(q + 1) * 128], A32[:, hp, :], ident[:, :], ['A32', 'ident'], [('ps', pb)])
                cp(Sout[0:64, g * 4:(g + 1) * 4, :, :], ps[0:64, pb, :].rearrange('p (q par k) -> p q par k', q=4, par=2), [('ps', pb)], ['F1'], eng='act')
            dma('sp', di['prw'].rearrange('(hp par v) k -> v hp par k', hp=8, par=2), Sout[0:64, :, :, :], ['F1'], [], dsem_misc, is_out=True)
            osm2 = FB_[2]
            cp(osm2[:, 0:8, 0:3], osml[:, :, 0:3], ['osml'], ['F2'])
            cp(osm2[:, 0:8, 3:4], osml[:, :, 3:4], ['osml'], ['F2'])
            store_fm_tokens(osm2, 'F2', 0, 3, di['pcv'][:, :])
            store_fm_tokens(osm2, 'F2', 3, 1, di['plru'].rearrange('(o d) -> o d', o=1))
            osh3 = FB_[3]
            cp(osh3[:, 0:8, 0:1], osh[:, 0:8].unsqueeze(2), ['osh'], ['F3'])
            cp(osh3[:, 0:8, 1:2], osh[:, 8:16].unsqueeze(2), ['osh'], ['F3'])
            cp(osh3[:, 0:8, 2:3], osh[:, 16:24].unsqueeze(2), ['osh'], ['F3'])
            cp(osh3[:, 0:2, 3:4], osh[:, 24:26].unsqueeze(2), ['osh'], ['F3'])
            pshv = di['psh'].rearrange('(o d) -> o d', o=1)
            for q in range(3):
                store_fm_tokens(osh3, 'F3', q, 1, pshv[:, q * 1024:(q + 1) * 1024])
            store_fm_tokens(osh3, 'F3', 3, 1, pshv[:, 3072:3328], nch=2)


        if os.environ.get('NTI') != '0':
            prompt_outputs()
        def sample_path():
            P.phase = 'sample_path'
            n = NS
            sm = lambda nm, shp, dt=F32: P.sbuf(nm, shp, dt)
            rc = sm('s_rc', [128, 8, n]); kc_ = sm('s_kc', [128, 8, n]); vc = sm('s_vc', [128, 8, n]); ac = sm('s_ac', [128, 8, n]); lsc = sm('s_lsc', [128, 8, n])
            yc = sm('s_yc', [128, 8, n]); bonc = sm('s_bonc', [128, 8, n]); plc = sm('s_plc', [128, 8, n]); hsc = sm('s_hsc', [128, 8, n])
            prevS = sm('s_prev', [128, 26, n]); praw = sm('s_praw', [128, 26, n]); h0S = sm('s_h0', [128, 8, n]); scvT = sm('s_scvT', [128, 8, 3 * n])
            BD = tok32[1][:].rearrange('p (h x) -> p h x', h=8); ones32 = st1[4][:, 0:128]
            glb = HB_[1]; geb = HB_[2]; g0b = HB_[3]; g1b = HX[:, 0:8, :]; lmb = HX[:, 16:24, :]
            dsS = [P.dma_sem() for _ in range(7)]

            def load_fm(src_rows_ap, nrows, nchunks, dst, dkey, tki, sem):
                tk = tok32[tki]
                dma('sp', tk[0:nrows, 0:nchunks * 128], src_rows_ap, [], [('tok32', tki)], sem)
                for g0 in range(0, nchunks, 4):
                    gn = min(4, nchunks - g0)
                    b = bank()
                    for q in range(gn):
                        tr(ps[:, b, q * 128:q * 128 + nrows], tk[0:nrows, (g0 + q) * 128:(g0 + q + 1) * 128], ident[0:nrows, 0:nrows],
                           [('tok32', tki), 'ident'], [('ps', b)])
                    cp(dst[:, g0:g0 + gn, 0:nrows], ps[:, b, :].rearrange('p (q t) -> p q t', q=4)[:, 0:gn, 0:nrows], [('ps', b)], [dkey], eng='act')

            load_tokens_fm(di['xs'], n, h32, hb, 0, 'h32')
            for q in range(4):
                c0 = q * 8; cn = min(8, 26 - c0)
                tmpd = FB_[0] if q % 2 == 0 else FB_[1]
                load_fm(di['ssh'][:, c0 * 128:(c0 + cn) * 128], n, cn, tmpd, 'F%d' % (q % 2), q % 2, dsS[q % 2])
                cp(prevS[:, c0:c0 + cn, :], tmpd[:, 0:cn, 0:n], ['F%d' % (q % 2)], ['s_prev'])
            load_fm(di['slru'], n, 8, h0S, 's_h0', 0, dsS[0])
            load_fm(di['scv'].rearrange('b j d -> (b j) d'), 3 * n, 8, scvT, 's_scvT', 1, dsS[1])
            dma('sp', di['scv_o'][:, 0:2, :], di['scv'][:, 1:3, :], [], [], P.dma_sem(), is_out=True)

            ffn('ffn1_wi', 'ffn1_wo', 0, n)

            tmp = st1[0]

            def evac(m, pb):
                psn = ps[:, pb, 0:n]
                if m < 26:
                    cp(praw[:, m, :], psn, [('ps', pb)], ['s_praw'], eng='act')
                    ts(tmp[:, 0:n], prevS[:, m, :], mu[:, m:m + 1], None, ALU.mult, None, ['s_prev', 'mu'], ['st0'])
                    if m < 24:
                        dst = [rc, kc_, vc][m // 8][:, m % 8, :]
                        dkey = ['s_rc', 's_kc', 's_vc'][m // 8]
                        stt(dst, psn, omu[:, m:m + 1], tmp[:, 0:n], ALU.mult, ALU.add, [('ps', pb), 'omu', 'st0'], [dkey])
                    else:
                        xs_ = st1[1][:, 0:n]
                        stt(xs_, psn, omu[:, m:m + 1], tmp[:, 0:n], ALU.mult, ALU.add, [('ps', pb), 'omu', 'st0'], ['st1'])
                        lb = st1[2][:, 0:NT].bitcast(BF16)[:, 0:n]
                        if m == 24:
                            act(lb[0:64, :], xs_[0:64, :], AF.Tanh, ['st1'], ['st2'])
                            cp(lb[64:128, :], xs_[64:128, :], ['st1'], ['st2'])
                            for (lo, dstt, dk, bvec) in [(0, lsc, 's_lsc', 'decay_w0'), (64, ac, 's_ac', 'aaa_a0')]:
                                for q in range(8):
                                    p2 = bank()
                                    mm(ps[:, p2, 0:n], w2a2[lo:lo + 64, q * 128:(q + 1) * 128], lb[lo:lo + 64, :], True, True, ['w2a2', 'st2'], [('ps', p2)])
                                    act(dstt[:, q, :], ps[:, p2, 0:n], AF.Sigmoid, [('ps', p2), 'v_' + bvec], [dk], bias=vec[bvec][:, q:q + 1])
                        else:
                            act(lb, xs_, AF.Sigmoid, ['st1'], ['st2'])
                            for q in range(8):
                                p2 = bank()
                                mm(ps[:, p2, 0:n], g2[:, q * 128:(q + 1) * 128], lb, True, True, ['g2', 'st2'], [('ps', p2)])
                                cp(glb[:, q, 0:n], ps[:, p2, 0:n], [('ps', p2)], ['H1'], eng='act')
                elif m < 34:
                    cp(plc[:, m - 26, :], psn, [('ps', pb)], ['s_plc'], eng='act')
                elif m < 42:
                    act(geb[:, m - 34, 0:n], psn, AF.Gelu, [('ps', pb)], ['H2'])
                elif m < 50:
                    act(g0b[:, m - 42, 0:n], psn, AF.Sigmoid, [('ps', pb)], ['H3'])
                else:
                    act(g1b[:, m - 50, 0:n], psn, AF.Sigmoid, [('ps', pb)], ['HX0'])
            proj('w_in', PW, hb, 'hb', n, evac)
            for q in range(4):
                c0 = q * 8; cn = min(8, 26 - c0)
                store_fm_tokens(praw[:, c0:c0 + cn, :], 's_praw', 0, n, di['ssh_o'][:, c0 * 128:(c0 + cn) * 128], nch=cn)
            store_fm_tokens(plc, 's_plc', 0, n, di['scv_o'][:, 2, :])

            sc3 = scvT[:].rearrange('p c (b j) -> p c b j', j=3)
            xc = st1[0][:, 0:n]; gr = st1[1][:, 0:n]; gi = st1[2][:, 0:n]; t3 = st1[3][:, 0:n]; hs = st1[4][:, 0:n]
            xcb = HX[:, 8:16, :]
            for c in range(8):
                act(xc, plc[:, c, :], AF.Identity, ['s_plc', 'cw', 'v_conv_b'], ['st0'], bias=vec['conv_b'][:, c:c + 1], scale=cw[:, 3, c:c + 1])
                for j in range(3):
                    stt(xc, sc3[:, c, :, j], cw[:, j, c:c + 1], xc, ALU.mult, ALU.add, ['s_scvT', 'cw', 'st0'], ['st0'])
                cp(xcb[:, c, 0:n], xc, ['st0'], ['HX1'], eng='act')
                p1 = bank(); p2 = bank()
                mm(ps[:, p1, 0:n], wrbd[:, c, :], xcb[:, c, 0:n], True, True, ['wrbd', 'HX1'], [('ps', p1)])
                mm(ps[:, p2, 0:n], wibd[:, c, :], xcb[:, c, 0:n], True, True, ['wibd', 'HX1'], [('ps', p2)])
                act(gr, ps[:, p1, 0:n], AF.Sigmoid, [('ps', p1), 'v_lru_br'], ['st1'], bias=vec['lru_br'][:, c:c + 1])
                act(gi, ps[:, p2, 0:n], AF.Sigmoid, [('ps', p2), 'v_lru_bi'], ['st2'], bias=vec['lru_bi'][:, c:c + 1])
                act(gr, gr, AF.Exp, ['st1', 'lsp'], ['st1'], scale=lsp[:, c:c + 1])
                act(t3, gr, AF.Square, ['st1'], ['st3'])
                ts(t3, t3, -1.0, 1.0, ALU.mult, ALU.add, ['st3'], ['st3'])
                ts(t3, t3, 0.0, None, ALU.max, None, ['st3'], ['st3'])
                act(t3, t3, AF.Sqrt, ['st3'], ['st3'])
                tt(gi, gi, xc, ALU.mult, ['st2', 'st0'], ['st2'])
                tt(gi, gi, t3, ALU.mult, ['st2', 'st3'], ['st2'])
                tt(hs, gr, h0S[:, c, :], ALU.mult, ['st1', 's_h0'], ['st4'])
                tt(hsc[:, c, :], hs, gi, ALU.add, ['st4', 'st2'], ['s_hsc'])
                tt(t3, hsc[:, c, :], geb[:, c, 0:n], ALU.mult, ['s_hsc', 'H2'], ['st3'])
                tt(lmb[:, c, 0:n], t3, g1b[:, c, 0:n], ALU.mult, ['st3', 'HX0'], ['HX2'])
            store_fm_tokens(hsc, 's_hsc', 0, n, di['slru_o'])

            F1flat = FB_[1].rearrange('p a b -> p (a b)')
            Souts = [F1flat[:, 0:1024].rearrange('p (hp par k) -> p hp par k', hp=8, par=2),
                     F1flat[:, 1024:2048].rearrange('p (hp par k) -> p hp par k', hp=8, par=2)]
            Skeys = ['F1', ('F1', 'b')]; Ssems = [dsS[4], P.dma_sem()]
            BDs = [tok32[0][:].rearrange('p (h x) -> p h x', h=8), tok32[1][:].rearrange('p (h x) -> p h x', h=8)]
            Bsems = [P.dma_sem(), dsS[3]]
            for i_ in range(2):
                P.op('dve', lambda e, i_=i_: e.memset(tok32[i_][:], 0.0), writes=[('tok32', i_)])

            def prefetch_state(b_):
                if b_ >= NS:
                    return
                src = di['srw'][b_].rearrange('(hp par v) k -> par v hp k', hp=8, par=2)
                for par in range(2):
                    dma('sp', BDs[b_ % 2][par * 64:(par + 1) * 64, :, par * 64:(par + 1) * 64], src[par], [], [('tok32', b_ % 2)], Bsems[b_ % 2])
            prefetch_state(0)
            bc16 = lambda v: v[:].unsqueeze(2).to_broadcast([128, 8, n])
            v816 = lambda t: t[:, 0:8 * n].rearrange('p (k b) -> p k b', k=8)
            kkn = plc; Wvc = hsc
            sqb = HB_[0][:, :, 0:n]
            tt(kkn[:], kc_[:], bc16(vec['k_k']), ALU.mult, ['s_kc', 'v_k_k', 's_plc'], ['s_plc'])
            act(sqb, kkn[:], AF.Square, ['s_plc'], ['H0'])
            pb = bank()
            for kc in range(8):
                mm(ps[:, pb, kc * n:(kc + 1) * n], bdb[:], sqb[:, kc, :], True, True, ['bdb', 'H0'], [('ps', pb)])
            s16 = v816(st1[0])
            act(s16, ps[:, pb, 0:8 * n].rearrange('p (k b) -> p k b', k=8), AF.Sqrt, [('ps', pb)], ['st0'])
            ts(s16, s16, 1e-12, None, ALU.max, None, ['st0'], ['st0'])
            recip(s16, s16, ['st0'], ['st0'])
            tt(kkn[:], kkn[:], s16, ALU.mult, ['s_plc', 'st0'], ['s_plc'])
            u16 = v816(st1[1])
            tt(u16, ac[:], bc16(vec['k_a']), ALU.mult, ['s_ac', 'v_k_a'], ['st1'])
            tt(u16, u16, bc16(oka), ALU.add, ['st1', 'oka'], ['st1'])
            tt(kc_[:], kc_[:], u16, ALU.mult, ['s_kc', 'st1'], ['s_kc'])
            tt(ac[:], ac[:], kkn[:], ALU.mult, ['s_ac', 's_plc'], ['s_ac'])
            r16 = v816(st1[2])
            tt(r16, rc[:], kc_[:], ALU.mult, ['s_rc', 's_kc'], ['st2'])
            tt(r16, r16, bc16(vec['r_k']), ALU.mult, ['st2', 'v_r_k'], ['st2'])
            cp(sqb, r16, ['st2'], ['H0'], eng='act')
            pb = bank()
            for kc in range(8):
                mm(ps[:, pb, kc * n:(kc + 1) * n], bdb[:], sqb[:, kc, :], True, True, ['bdb', 'H0'], [('ps', pb)])
            tt(bonc[:], vc[:], ps[:, pb, 0:8 * n].rearrange('p (k b) -> p k b', k=8), ALU.mult, ['s_vc', ('ps', pb)], ['s_bonc'])
            act(Wvc[:], lsc[:], AF.Exp, ['s_lsc', 's_hsc'], ['s_hsc'], scale=C0)
            act(lsc[:], lsc[:], AF.Exp, ['s_lsc'], ['s_lsc'], scale=-C0)
            tt(rc[:], rc[:], lsc[:], ALU.mult, ['s_rc', 's_lsc'], ['s_rc'])
            tt(kc_[:], kc_[:], Wvc[:], ALU.mult, ['s_kc', 's_hsc'], ['s_kc'])
            stt(ac[:], ac[:], -1.0, Wvc[:], ALU.mult, ALU.mult, ['s_ac', 's_hsc'], ['s_ac'])

            for g in range(NS // NCH):
                Bp = dict(r32=FB_[0], k32=FB_[1], v32=FB_[2], a32=FB_[3], ls32=FB_[4])
                gs = slice(g * NCH, (g + 1) * NCH)

                def prepared(QR, KT, NB, vb, WCs_, gs=gs):
                    c0v = lambda t: t.rearrange('p k (c t) -> p k c t', t=C)[:, :, :, 0]
                    P.op('dve', lambda e: e.memset(HX[:, 0:16, :], 0.0), writes=['HX0', 'HX1'])
                    P.op('dve', lambda e: e.memset(KT[:], 0.0), writes=['H0'])
                    P.op('dve', lambda e: e.memset(NB[:], 0.0), writes=['hb'])
                    P.op('dve', lambda e: e.memset(vb[:, :, 0:NT], 0.0), writes=['pl'])
                    cp(QR[:, :, 0, :, 0], kkn[:, :, gs], ['s_plc'], ['HX0', 'HX1'])
                    cp(QR[:, :, 1, :, 0], rc[:, :, gs], ['s_rc'], ['HX0', 'HX1'])
                    cp(c0v(KT[:]), kc_[:, :, gs], ['s_kc'], ['H0'])
                    cp(c0v(NB[:]), ac[:, :, gs], ['s_ac'], ['hb'])
                    cp(c0v(vb[:, :, 0:NT]), vc[:, :, gs], ['s_vc'], ['pl'])
                    cp(WCs_[:, :, :], lsc[:, :, gs], ['s_lsc'], ['WCs'])

                def state_in(c, g=g):
                    b_ = g * NCH + c
                    prefetch_state(b_ + 1)
                    BD = BDs[b_ % 2]
                    pb = bank()
                    for hp in range(8):
                        mm(ps[:, pb, hp * 64:(hp + 1) * 64], BD[:, hp, :], mask[:, 192:256], True, True, [('tok32', b_ % 2), 'mask'], [('ps', pb)])
                    v3 = ps[:, pb, :].rearrange('p (h x) -> p h x', h=8)
                    cp(A32[:], v3, [('ps', pb)], ['A32'], eng='dve')
                    cp(A0b[c % 2][:], v3, [('ps', pb)], [('A0b', c % 2)], eng='act')

                def state_out(c, g=g):
                    b_ = g * NCH + c
                    Sout = Souts[b_ % 2]; skey = Skeys[b_ % 2]
                    for gg in range(2):
                        pb = bank()
                        for q in range(4):
                            hp = gg * 4 + q
                            tr(ps[0:64, pb, q * 128:(q + 1) * 128], A32[:, hp, :], ident[:, :], ['A32', 'ident'], [('ps', pb)])
                        cp(Sout[0:64, gg * 4:(gg + 1) * 4, :, :], ps[0:64, pb, :].rearrange('p (q par k) -> p q par k', q=4, par=2), [('ps', pb)], [skey], eng='act')
                    dma('sp', di['srw_o'][b_].rearrange('(hp par v) k -> v hp par k', hp=8, par=2), Sout[0:64, :, :, :], [skey], [], Ssems[b_ % 2], is_out=True)
                Y32, bon = rwkv_core(Bp, state_in, state_out, skip_inverse=True, prepared=prepared)
                cp(yc[:, :, g * NCH:(g + 1) * NCH], Y32[:].rearrange('p k (c t) -> p k c t', t=C)[:, :, :, 0], ['F0'], ['s_yc'])
            mb = HB_[2]
            rwkv_post(yc, 's_yc', bonc, 's_bonc', glb, g0b, lmb, mb, n)
            resid_ln('w_mix_out', mb, 'H2', 1, n)

            qc = FB_[0]; qT = FB_[1].rearrange('p a b -> p (a b)')[:, 0:1024]; sel = FB_[2].rearrange('p a b -> p (a b)')[:, 0:NS * 128].rearrange('p (b m) -> p b m', b=NS)
            Kbs = [FB_[3].rearrange('p a b -> p (a b)').rearrange('p (mc f) -> p mc f', mc=2),
                   pl[:].rearrange('p a b -> p (a b)')[:, 0:2048].rearrange('p (mc f) -> p mc f', mc=2)]
            Kkeys = ['F3', 'pl']; Ksems = [dsS[5], P.dma_sem()]
            prod = FB_[4].rearrange('p a b -> p (a b)')[:, 0:1024]
            Vbs = [HB_[0][:].rearrange('p a b -> p (a b)').rearrange('p (mc f) -> p mc f', mc=2),
                   HB_[2][:].rearrange('p a b -> p (a b)').rearrange('p (mc f) -> p mc f', mc=2)]
            Vkeys = ['H0', 'H2']; Vsems = [dsS[6], P.dma_sem()]
            ob = HB_[1]
            sc = st1[0][:, 0:128]; ex = st1[1][:, 0:128]; den = st1[2][:, 0:64]; pbf = st1[3][:, 0:NT].bitcast(BF16)[:, 0:128]

            def evq(m, pb):
                cp(qc[:, m, 0:n], ps[:, pb, 0:n], [('ps', pb)], ['F0'], eng='act')
            proj('xa_wq', D, hb, 'hb', n, evq)
            for g0 in range(0, 8, 4):
                pb = bank()
                for q in range(4):
                    tr(ps[0:n, pb, q * 128:(q + 1) * 128], qc[:, g0 + q, 0:n], ident[:, :], ['F0', 'ident'], [('ps', pb)])
                cp(qT[0:n, g0 * 128:(g0 + 4) * 128], ps[0:n, pb, :], [('ps', pb)], ['F1'], eng='act')
            cp(sel[0:n, :, :], ident[0:n, 0:n].unsqueeze(2).to_broadcast([n, n, 128]), ['ident'], ['F2'])
            for b_ in range(NS):
                Kb = Kbs[b_ % 2]; kkey = Kkeys[b_ % 2]
                dma('sp', Kb, di['cmk'][b_].rearrange('(mc p) f -> p mc f', p=128), [], [kkey], Ksems[b_ % 2])
                pq = [bank(), bank()]
                for hf in range(2):
                    mm(ps[:, pq[hf], :], sel[0:n, b_, :], qT[0:n, hf * 512:(hf + 1) * 512], True, True, ['F2', 'F1'], [('ps', pq[hf])])
                for mc in range(2):
                    for hf in range(2):
                        tt(prod[:, hf * 512:(hf + 1) * 512], Kb[:, mc, hf * 512:(hf + 1) * 512], ps[:, pq[hf], :], ALU.mult, [kkey, ('ps', pq[hf])], ['F4'])
                    P.op('dve', lambda e, b_=b_, mc=mc: e.tensor_reduce(out=sc[:, (b_ * 2 + mc) * 4:(b_ * 2 + mc) * 4 + 4], in_=prod.rearrange('p (h d) -> p h d', h=4), axis=AX.X, op=ALU.add),
                         reads=['F4'], writes=['st0'])
            act(ex, sc, AF.Exp, ['st0'], ['st1'], scale=1.0 / 16.0)
            dma('sp', ones32, di['c_all'][:, 128:256], [], ['st4'], dsS[2])
            pdn = bank()
            mm(ps[:, pdn, 0:128], ones32, ex, True, True, ['st4', 'st1'], [('ps', pdn)])
            d4 = ps[:, pdn, 0:128].rearrange('p (b mc h) -> p b mc h', mc=2, h=4)
            den3 = den.rearrange('p (b h) -> p b h', h=4)
            cp(den3, d4[:, :, 0, :], [('ps', pdn)], ['st2'])
            tt(den3, den3, d4[:, :, 1, :], ALU.add, ['st2', ('ps', pdn)], ['st2'])
            recip(den, den, ['st2'], ['st2'])
            tt(pbf.rearrange('p (b mc h) -> p b mc h', mc=2, h=4), ex.rearrange('p (b mc h) -> p b mc h', mc=2, h=4),
               den3.unsqueeze(2).to_broadcast([128, NS, 2, 4]), ALU.mult, ['st1', 'st2'], ['st3'])
            po = bank()
            for b_ in range(NS):
                Vb = Vbs[b_ % 2]; vkey = Vkeys[b_ % 2]
                dma('pool', Vb, di['cmv'][b_].rearrange('(mc p) f -> p mc f', p=128), [], [vkey], Vsems[b_ % 2])
                for c in range(8):
                    for mc in range(2):
                        col = (b_ * 2 + mc) * 4 + c // 2
                        mm(ps[:, po, c * NS + b_:c * NS + b_ + 1], Vb[:, mc, c * 128:(c + 1) * 128], pbf[:, col:col + 1], mc == 0, mc == 1, [vkey, 'st3'], [('ps', po)])
            cp(ob[:, :, 0:n], ps[:, po, 0:8 * NS].rearrange('p (c b) -> p c b', c=8), [('ps', po)], ['H1'], eng='act')
            resid_ln('xa_wo', ob, 'H1', 2, n)
            ffn('ffn2_wi', 'ffn2_wo', 3, n)
            store_fm_tokens(h32, 'h32', 0, n, di['ys'])

        if do_sample:
            sample_path()
        P.emit()
    return nc, P


_CACHE = {}


def _consts():
    a = np.arange(128) % 64
    b = np.arange(64)
    su = (a[:, None] < b[None, :]).astype(np.float32)
    ui = (a[:, None] <= b[None, :]).astype(np.float32)
    sl = (a[:, None] > b[None, :]).astype(np.float32)
    ey = (a[:, None] == b[None, :]).astype(np.float32)
    bd = np.zeros((128, 128), np.float32)
    bd[:64, :64] = 1.0
    bd[64:, 64:] = 1.0
    rs = np.ones((128, NT), np.float32)
    rs[:, ::C] = 0.0
    return {'c_all': np.ascontiguousarray(np.concatenate([np.eye(128, dtype=np.float32), np.ones((128, 128), np.float32), bd, su, ui, sl, ey, rs], axis=1))}


def make_in_maps(inputs):
    f = lambda a: np.ascontiguousarray(np.asarray(a, dtype=np.float32))
    shared = {}
    for nm in ['ffn1_wi', 'ffn1_wo', 'ffn2_wi', 'ffn2_wo', 'w_in', 'decay_w2', 'aaa_a2', 'gate_g2',
               'lru_wr', 'lru_wi', 'w_mix_out', 'xa_wq', 'xa_wk', 'xa_wv', 'xa_wo']:
        shared[nm] = np.ascontiguousarray(f(inputs[nm])[0])
    shared['prm'] = np.ascontiguousarray(np.concatenate(
        [f(inputs[nm])[0].reshape(-1, 128) for nm in ['ln_g', 'ln_b', 'shift_mu', 'conv_w'] + VEC_NAMES], axis=0))
    shared.update(_consts())
    maps = []
    for c in range(8):
        m = dict(shared)
        sl = slice(c * NS, (c + 1) * NS)
        m['xp'] = f(inputs['x_prompt'][c])
        m['mem'] = f(inputs['mem_prompt'][c])
        m['xs'] = f(inputs['x_sample'][sl, 0])
        m['cmk'] = f(inputs['cache_mem_k'][0, sl]).reshape(NS, NMEM, D)
        m['cmv'] = f(inputs['cache_mem_v'][0, sl]).reshape(NS, NMEM, D)
        m['srw'] = f(inputs['state_rwkv'][0, sl]).reshape(NS, D, 64)
        m['ssh'] = f(inputs['state_rwkv_shift'][0, sl])
        m['slru'] = f(inputs['state_lru'][0, sl])
        m['scv'] = f(inputs['state_conv'][0, sl])
        maps.append(m)
    return maps


def kernel(**inputs):
    if 'nc' not in _CACHE:
        _CACHE['nc'] = build()[0]
    nc = _CACHE['nc']
    maps = make_in_maps(inputs)
    res = run_bass_kernel_spmd(nc, maps, core_ids=list(range(8)))
    R = res.results
    cat = lambda k: np.stack([np.asarray(r[k], dtype=np.float32) for r in R])
    catc = lambda k: np.concatenate([np.asarray(r[k], dtype=np.float32) for r in R], axis=0)
    yp = cat('yp')
    ys = catc('ys').reshape(8 * NS, 1, D)
    pmk = cat('pmk').reshape(1, 8, NMEM, 4, 256)
    pmv = cat('pmv').reshape(1, 8, NMEM, 4, 256)
    prw = cat('prw').reshape(1, 8, 16, 64, 64)
    psh = cat('psh').reshape(1, 8, RP)
    plru = cat('plru').reshape(1, 8, D)
    pcv = cat('pcv').reshape(1, 8, 3, D)
    srw = catc('srw_o').reshape(1, 8 * NS, 16, 64, 64)
    ssh = catc('ssh_o').reshape(1, 8 * NS, RP)
    slru = catc('slru_o').reshape(1, 8 * NS, D)
    scv = catc('scv_o').reshape(1, 8 * NS, 3, D)
    return (yp, ys, pmk, pmv, prw, psh, plru, pcv, srw, ssh, slru, scv)
```

```python
import math
import os
import numpy as np
from contextlib import ExitStack
import concourse.bass as bass
import concourse.mybir as mybir
from concourse.bass_utils import run_bass_kernel_spmd

F32 = mybir.dt.float32
BF16 = mybir.dt.bfloat16
AF = mybir.ActivationFunctionType
ALU = mybir.AluOpType
AX = mybir.AxisListType

ENGS = ['pe', 'dve', 'act', 'pool', 'sp']

D = 1024
T = 2048
NT = 256
NTILES = T // NT
C = 64
NCH = NT // C
DFF = 2816
NJ = DFF // 128
RP = 3328
PW = 7424
NMEM = 256
NS = 16
ALPHA = 2.0 ** 0.25
LN_EPS = 1e-5
GN_EPS = 64e-5
C0 = math.exp(-0.5)


class DmaSem:
    def __init__(self, sem):
        self.sem = sem
        self.count = 0


class Prog:
    def __init__(self, nc, stack):
        self.nc = nc
        self.stack = stack
        self.ops = {e: [] for e in ENGS}
        self.last_w = {}
        self.readers = {}
        self.seen = {e: {} for e in ENGS}
        self.dsems = []
        self.out_tokens = []

    def dma_sem(self):
        s = DmaSem(self.stack.enter_context(self.nc.semaphore('dsem%d' % len(self.dsems))))
        self.dsems.append(s)
        return s

    def sbuf(self, name, shape, dt):
        return self.stack.enter_context(self.nc.sbuf_tensor(name, list(shape), dt))

    def psum(self, name, shape, dt):
        return self.stack.enter_context(self.nc.psum_tensor(name, list(shape), dt))

    def barrier(self, keys, engines=ENGS):
        if 'B' in os.environ.get('TOG', ''):
            return
        for e in engines:
            self.op(e, None, reads=keys, track=False)

    def op(self, eng, fn, reads=(), writes=(), dsem=None, is_out=False, track=True):
        isps = lambda k: isinstance(k, tuple) and k[0] == 'ps'
        writes = list(writes) + [k for k in reads if isps(k)]
        reads = [k for k in reads if not isps(k)]
        deps = []
        for k in reads:
            t = self.last_w.get(k)
            if t is not None:
                deps.append(t)
        for k in writes:
            t = self.last_w.get(k)
            if t is not None:
                deps.append(t)
            deps.extend(self.readers.get(k, {}).values())
        need = {}
        for t in deps:
            if t[0] == 'eng':
                if t[1] == eng and dsem is None and (eng == 'pe' or os.environ.get('NOSELF')):
                    continue
                key = ('eng', t[1])
            else:
                key = ('dma', id(t[1]))
            if need.get(key, (None, -1))[1] < t[2]:
                need[key] = (t[1], t[2])
        waits = []
        for key, (src, v) in need.items():
            if self.seen[eng].get(key, -1) >= v:
                continue
            self.seen[eng][key] = v
            waits.append((key[0], src, v))
        idx = len(self.ops[eng])
        self.ops[eng].append(dict(fn=fn, waits=waits, dsem=dsem, target=False, phase=getattr(self, 'phase', '')))
        if dsem is not None:
            dsem.count += 16
            tok = ('dma', dsem, dsem.count)
        else:
            tok = ('eng', eng, idx)
        for k in writes:
            self.last_w[k] = tok
            self.readers[k] = {}
        for k in (reads if track else ()):
            r = self.readers.setdefault(k, {})
            rk = (tok[0], tok[1] if tok[0] == 'eng' else id(tok[1]))
            if rk not in r or r[rk][2] < tok[2]:
                r[rk] = tok
        if is_out:
            self.out_tokens.append(tok)
        return tok

    def emit(self):
        nc = self.nc
        fin = {}
        for t in self.out_tokens:
            fin[id(t[1])] = (t[1], max(fin.get(id(t[1]), (None, 0))[1], t[2]))
        self.ops['sp'].append(dict(fn=None, waits=[('dma', s, v) for s, v in fin.values()], dsem=None, target=False))
        for e in ENGS:
            for o in self.ops[e]:
                for kind, src, v in o['waits']:
                    if kind == 'eng':
                        self.ops[src][v]['target'] = True
        semval = {}
        for e in ENGS:
            c = 0
            vals = []
            for o in self.ops[e]:
                if o['target']:
                    c += 1
                vals.append(c)
            semval[e] = vals
        esem = {e: self.stack.enter_context(nc.semaphore('esem_' + e)) for e in ENGS}
        handles = {'pe': 'tensor', 'dve': 'vector', 'act': 'scalar', 'pool': 'gpsimd', 'sp': 'sync'}
        with nc.Block() as block:
            def make(e):
                def body(eng):
                    for o in self.ops[e]:
                        for kind, src, v in o['waits']:
                            if kind == 'eng':
                                eng.wait_ge(esem[src], semval[src][v])
                            else:
                                eng.wait_ge(src.sem, v)
                        if o['fn'] is None:
                            continue
                        inst = o['fn'](eng)
                        if os.environ.get('ANNOT') and o.get('phase'):
                            inst.annotate(o['phase'])
                        if o['dsem'] is not None:
                            inst.then_inc(o['dsem'].sem, 16)
                        elif o['target']:
                            inst.then_inc(esem[e], 1)
                return body
            for e in ENGS:
                getattr(block, handles[e])(make(e))
        self.stats = {e: len(self.ops[e]) for e in ENGS}


VEC_NAMES = ['decay_w0', 'aaa_a0', 'k_k', 'k_a', 'r_k', 'gn_g', 'gn_b', 'conv_b', 'lru_br', 'lru_bi', 'lru_lambda']
W_NAMES = ['ffn1_wi', 'ffn1_wo', 'ffn2_wi', 'ffn2_wo', 'w_in', 'w_mix_out', 'xa_wq', 'xa_wk', 'xa_wv', 'xa_wo']


def build(dbg=None, do_sample=True, stage=99):
    nc = bass.Bass('TRN2', target_bir_lowering=False)
    di = {}

    DECL = os.environ.get('DECL')

    def din(name, shape):
        if DECL and name not in DECL.split(','):
            return None
        di[name] = nc.dram_tensor(name, list(shape), F32, kind='ExternalInput').ap()
        return di[name]

    def dout(name, shape):
        if DECL and name not in DECL.split(','):
            return None
        di[name] = nc.dram_tensor(name, list(shape), F32, kind='ExternalOutput').ap()
        return di[name]

    din('xp', [T, D]); din('mem', [NMEM, D])
    din('xs', [NS, D]); din('cmk', [NS, NMEM, D]); din('cmv', [NS, NMEM, D])
    din('srw', [NS, D, 64]); din('ssh', [NS, RP]); din('slru', [NS, D]); din('scv', [NS, 3, D])
    din('prm', [210, 128])
    din('ffn1_wi', [D, 2 * DFF]); din('ffn1_wo', [DFF, D]); din('ffn2_wi', [D, 2 * DFF]); din('ffn2_wo', [DFF, D])
    din('w_in', [D, PW])
    din('decay_w2', [64, D]); din('aaa_a2', [64, D]); din('gate_g2', [128, D])
    din('lru_wr', [16, 64, 64]); din('lru_wi', [16, 64, 64])
    for w in ['w_mix_out', 'xa_wq', 'xa_wk', 'xa_wv', 'xa_wo']:
        din(w, [D, D])
    din('c_all', [128, 640 + NT])
    dout('yp', [T, D]); dout('ys', [NS, D]); dout('pmk', [NMEM, D]); dout('pmv', [NMEM, D])
    dout('prw', [D, 64]); dout('psh', [RP]); dout('plru', [D]); dout('pcv', [3, D])
    dout('srw_o', [NS, D, 64]); dout('ssh_o', [NS, RP]); dout('slru_o', [NS, D]); dout('scv_o', [NS, 3, D])
    dbg = dbg or {}
    for k, shp in dbg.items():
        dout('dbg_' + k, shp)

    with ExitStack() as st:
        P = Prog(nc, st)
        n = NT
        ident = P.sbuf('ident', [128, 128], F32)
        identb = P.sbuf('identb', [128, 128], BF16)
        onesb = P.sbuf('onesb', [128, 128], BF16)
        bdb = P.sbuf('bdb', [128, 128], BF16)
        mask = P.sbuf('mask', [128, 256], F32)
        reset = P.sbuf('reset', [128, NT], F32)
        lng = P.sbuf('lng', [128, 4, 8], F32); lnb = P.sbuf('lnb', [128, 4, 8], F32)
        mu = P.sbuf('mu', [128, 26], F32); omu = P.sbuf('omu', [128, 26], F32)
        vec = {v: P.sbuf('v_' + v, [128, 8], F32) for v in VEC_NAMES}
        oka = P.sbuf('oka', [128, 8], F32)
        lsp = P.sbuf('lsp', [128, 8], F32)
        cw = P.sbuf('cw', [128, 4, 8], F32)
        w2a2 = P.sbuf('w2a2', [128, D], BF16)
        g2 = P.sbuf('g2', [128, D], BF16)
        wrbd = P.sbuf('wrbd', [128, 8, 128], BF16); wibd = P.sbuf('wibd', [128, 8, 128], BF16)
        WCAP = 4096
        NWB = 4
        wbuf = [P.sbuf('wbuf%d' % i, [128, WCAP], BF16) for i in range(NWB)]
        wsem = [P.dma_sem() for i in range(NWB)]
        h32 = P.sbuf('h32', [128, 8, n], F32)
        hb = P.sbuf('hb', [128, 8, n], BF16)
        FB_ = [P.sbuf('F%d' % i, [128, 8, n], F32) for i in range(5)]
        HX = P.sbuf('HX', [128, 24, n], BF16)
        HB_ = [P.sbuf('H%d' % i, [128, 8, n], BF16) for i in range(4)]
        memT = HB_[3]
        pl = P.sbuf('pl', [128, 8, n + 3], F32)
        st1 = [P.sbuf('st%d' % i, [128, n], F32) for i in range(5)]
        st2 = [P.sbuf('su%d' % i, [128, n], F32) for i in range(5)]
        stsel = lambda i: ((st1, 'st') if i % 2 == 0 else (st2, 'su'))
        carry_sh = P.sbuf('carry_sh', [128, 26], F32)
        carry_h = P.sbuf('carry_h', [128, 8], F32)
        A32 = P.sbuf('A32', [128, 8, 64], F32)
        A0b = [P.sbuf('A0b%d' % i, [128, 8, 64], BF16) for i in range(2)]
        RHSb = P.sbuf('RHSb', [128, 8, 64], BF16)
        Ub = P.sbuf('Ub', [128, 8, 64], BF16)
        WCs = P.sbuf('WCs', [128, 8, NCH], F32)
        X32 = [P.sbuf('X32_%d' % i, [128, 8, 64], F32) for i in range(2)]
        Xb = [P.sbuf('Xb_%d' % i, [128, 8, 64], BF16) for i in range(2)]
        PP = [[P.sbuf('PP_%d_%d' % (i, j), [128, 8, 128], BF16) for j in range(2)] for i in range(2)]
        LP = P.sbuf('LP', [128, 8, NCH, 128], BF16)
        PN = P.sbuf('PN', [128, 8, NCH, 128], BF16)
        XT = P.sbuf('XT', [128, 8, NCH, 64], BF16)
        mkT = P.sbuf('mkT', [128, 8, NMEM], BF16)
        mvb = P.sbuf('mvb', [128, 2, D], BF16)
        tok32 = [P.sbuf('tok32_%d' % i, [128, D], F32) for i in range(2)]
        osml = P.sbuf('osml', [128, 8, 8], F32)
        osh = P.sbuf('osh', [128, 26], F32)
        sgb = [P.sbuf('sgb%d' % i, [128, NT], F32) for i in range(2)]
        ps = P.psum('ps', [128, 8, 512], F32)
        dsem_c = [P.dma_sem() for i in range(4)]
        dsem_in = [P.dma_sem() for i in range(2)]
        dsem_out = [P.dma_sem() for i in range(2)]
        dsem_misc = P.dma_sem()

        bank_ctr = [0]
        tokctr = [0]

        def bank():
            b = bank_ctr[0] % 8
            bank_ctr[0] += 1
            return b

        def mm(out, lhsT, rhs, start, stop, reads, writes):
            P.op('pe', lambda e: e.matmul(out, lhsT=lhsT, rhs=rhs, start=start, stop=stop), reads=reads, writes=writes)

        def tr(out, in_, idn, reads, writes):
            P.op('pe', lambda e: e.transpose(out, in_, idn), reads=reads, writes=writes)

        def act(out, in_, func, reads, writes, bias=None, scale=None):
            kw = {}
            if bias is not None:
                kw['bias'] = bias
            if scale is not None:
                kw['scale'] = scale
            P.op('act', lambda e: e.activation(out=out, in_=in_, func=func, **kw), reads=reads, writes=writes)

        def tt(out, in0, in1, op, reads, writes, eng='dve'):
            P.op(eng, lambda e: e.tensor_tensor(out=out, in0=in0, in1=in1, op=op), reads=reads, writes=writes)

        def ts(out, in0, s1, s2, op0, op1, reads, writes, eng='dve'):
            if s2 is None:
                P.op(eng, lambda e: e.tensor_scalar(out=out, in0=in0, scalar1=s1, scalar2=None, op0=op0), reads=reads, writes=writes)
            else:
                P.op(eng, lambda e: e.tensor_scalar(out=out, in0=in0, scalar1=s1, scalar2=s2, op0=op0, op1=op1), reads=reads, writes=writes)

        def stt(out, in0, scalar, in1, op0, op1, reads, writes):
            P.op('dve', lambda e: e.scalar_tensor_tensor(out=out, in0=in0, scalar=scalar, in1=in1, op0=op0, op1=op1), reads=reads, writes=writes)

        def cp(out, in_, reads, writes, eng='dve'):
            if eng == 'act':
                act(out, in_, AF.Copy, reads, writes)
            else:
                P.op(eng, lambda e: e.tensor_copy(out=out, in_=in_), reads=reads, writes=writes)

        def recip(out, in_, reads, writes):
            P.op('dve', lambda e: e.reciprocal(out=out, in_=in_), reads=reads, writes=writes)

        def dma(eng, out, in_, reads, writes, dsem, is_out=False, **kw):
            P.op(eng, lambda e: e.dma_start(out=out, in_=in_, **kw), reads=reads, writes=writes, dsem=dsem, is_out=is_out)

        def interleave2(ga, gb):
            a_ok = b_ok = True
            while a_ok or b_ok:
                if a_ok:
                    try:
                        next(ga)
                    except StopIteration:
                        a_ok = False
                if b_ok:
                    try:
                        next(gb)
                    except StopIteration:
                        b_ok = False

        def dump(name, src_ap, key):
            if name in dbg:
                dma('sp', di['dbg_' + name], src_ap, [key], [], P.dma_sem(), is_out=True)

        PARTS = os.environ.get('PARTS', 'abcdefg')
        dma('sp', ident[:], di['c_all'][:, 0:128], [], ['ident'], dsem_c[0])
        dma('sp', mask[:], di['c_all'][:, 384:640], [], ['mask'], dsem_c[0])
        dma('sp', reset[:], di['c_all'][:, 640:640 + NT], [], ['reset'], dsem_c[0])
        P.barrier(['ident', 'mask', 'reset'])
        if 'b' in PARTS:
            dma('pool', identb[:], di['c_all'][:, 0:128], [], ['identb'], dsem_c[1])
            dma('pool', onesb[:], di['c_all'][:, 128:256], [], ['onesb'], dsem_c[1])
            dma('pool', bdb[:], di['c_all'][:, 256:384], [], ['bdb'], dsem_c[1])
            dma('pool', w2a2[0:64, :], di['decay_w2'], [], ['w2a2'], dsem_c[1])
            dma('pool', w2a2[64:128, :], di['aaa_a2'], [], ['w2a2'], dsem_c[1])
            dma('pool', g2[:], di['gate_g2'], [], ['g2'], dsem_c[1])
        if 'm' in PARTS or PARTS == 'abcdefg':
            P.op('dve', lambda e: e.memset(wrbd[:], 0.0), writes=['wrbd'])
            P.op('dve', lambda e: e.memset(wibd[:], 0.0), writes=['wibd'])
        for (wt, nm, key) in ([(wrbd, 'lru_wr', 'wrbd'), (wibd, 'lru_wi', 'wibd')] if 'c' in PARTS else []):
            src = di[nm].rearrange('(c two) i o -> two i c o', two=2)
            for par in range(2):
                dma('pool', wt[par * 64:(par + 1) * 64, :, par * 64:(par + 1) * 64], src[par], [], [key], dsem_c[1])
        P.barrier(['identb', 'onesb', 'bdb', 'w2a2', 'g2', 'wrbd', 'wibd'])
        prm = [tok32[0], tok32[1]]
        rows = []
        rows.append((lng[:].rearrange('p l c -> p (l c)'), di['prm'][0:32, :], 'lng'))
        rows.append((lnb[:].rearrange('p l c -> p (l c)'), di['prm'][32:64, :], 'lnb'))
        rows.append((mu[:], di['prm'][64:90, :], 'mu'))
        rows.append((cw[:].rearrange('p l c -> p (l c)'), di['prm'][90:122, :], 'cw'))
        for vi_, v in enumerate(VEC_NAMES):
            rows.append((vec[v][:], di['prm'][122 + 8 * vi_:130 + 8 * vi_, :], 'v_' + v))
        groups = [[]]
        cnt = 0
        for r_ in rows:
            k_ = r_[1].shape[0]
            if cnt + k_ > 128:
                groups.append([]); cnt = 0
            groups[-1].append((cnt, k_) + r_)
            cnt += k_
        for gi_, grp in enumerate(groups if 'd' in PARTS else []):
            tk = prm[gi_ % 2]
            tot = 0
            for (o_, k_, dst, src, key) in grp:
                dma('sp', tk[o_:o_ + k_, 0:128], src, [], [('tok32', gi_ % 2)], dsem_c[2 + gi_ % 2])
                tot = o_ + k_
            pb = bank()
            tr(ps[:, pb, 0:tot], tk[0:tot, 0:128], ident[0:tot, 0:tot], [('tok32', gi_ % 2), 'ident'], [('ps', pb)])
            for (o_, k_, dst, src, key) in grp:
                cp(dst, ps[:, pb, o_:o_ + k_], [('ps', pb)], [key])
        if 'e' in PARTS:
            ts(omu[:], mu[:], -1.0, 1.0, ALU.mult, ALU.add, ['mu'], ['omu'])
            ts(oka[:], vec['k_a'][:], -1.0, 1.0, ALU.mult, ALU.add, ['v_k_a'], ['oka'])
            act(lsp[:], vec['lru_lambda'][:], AF.Exp, ['v_lru_lambda'], ['lsp'], scale=-1.0)
            act(lsp[:], lsp[:], AF.Ln, ['lsp'], ['lsp'], bias=1.0)
            ts(lsp[:], lsp[:], -8.0, None, ALU.mult, None, ['lsp'], ['lsp'])

        def wblocks():
            def ffn_blocks(wi, wo):
                for g in range(11):
                    def f(buf, g=g, wi=wi):
                        v = buf[:, 0:4096].rearrange('p (k c) -> p k c', k=8)
                        return [(v[:, :, 0:256], di[wi][:, g * 256:(g + 1) * 256].rearrange('(k p) c -> p k c', p=128)),
                                (v[:, :, 256:512], di[wi][:, DFF + g * 256:DFF + (g + 1) * 256].rearrange('(k p) c -> p k c', p=128))]
                    yield ((wi, g), f)
                for mp in range(8):
                    def f(buf, mp=mp, wo=wo):
                        v = buf[:, 0:NJ * 128].rearrange('p (j c) -> p j c', j=NJ)
                        return [(v, di[wo][:, mp * 128:(mp + 1) * 128].rearrange('(j p) c -> p j c', p=128))]
                    yield ((wo, mp), f)

            def sq_blocks(w, ncols):
                nb = (ncols + 511) // 512
                for b in range(nb):
                    c0 = b * 512
                    cn = min(512, ncols - c0)
                    def f(buf, c0=c0, cn=cn, w=w):
                        v = buf[:, 0:8 * cn].rearrange('p (k c) -> p k c', k=8)
                        return [(v, di[w][:, c0:c0 + cn].rearrange('(k p) c -> p k c', p=128))]
                    yield ((w, b), f)
            yield from sq_blocks('xa_wk', D)
            yield from sq_blocks('xa_wv', D)
            def one_pass():
                yield from ffn_blocks('ffn1_wi', 'ffn1_wo')
                yield from sq_blocks('w_in', PW)
                yield from sq_blocks('w_mix_out', D)
                yield from sq_blocks('xa_wq', D)
                yield from sq_blocks('xa_wo', D)
                yield from ffn_blocks('ffn2_wi', 'ffn2_wo')
            for it in range(int(os.environ.get('NTI', NTILES)) + (1 if do_sample else 0)):
                for blk, (tag, f) in enumerate(one_pass()):
                    yield (tag, f, it, blk)

        wgen = wblocks()
        wstate = dict(issued=0, consumed=0, pending=[])

        NBLK = 59
        wsc = nc.dram_tensor('wsc', [NBLK, 128, WCAP], BF16, kind='Internal').ap()
        wbsem = [P.dma_sem() for i in range(NWB)]

        def w_used(tag):
            if tag[0].endswith('_wi'):
                return 4096
            if tag[0].endswith('_wo') and tag[0].startswith('ffn'):
                return NJ * 128
            ncols = PW if tag[0] == 'w_in' else D
            return 8 * min(512, ncols - tag[1] * 512)

        def w_issue():
            try:
                item = next(wgen)
            except StopIteration:
                return False
            i = wstate['issued'] % NWB
            if len(item) == 2:
                tag, f = item
                for (dst, src) in f(wbuf[i]):
                    dma('pool', dst, src, [], [('wbuf', i)], wsem[i])
            else:
                tag, f, it, blk = item
                used = w_used(tag)
                if it == 0:
                    for (dst, src) in f(wbuf[i]):
                        dma('pool', dst, src, [], [('wbuf', i)], wsem[i])
                    dma('sp', wsc[blk, :, 0:used], wbuf[i][:, 0:used], [('wbuf', i)], [('wsc', blk)], wbsem[i])
                else:
                    dma('pool', wbuf[i][:, 0:used], wsc[blk, :, 0:used], [('wsc', blk)], [('wbuf', i)], wsem[i])
            wstate['pending'].append((tag, i))
            wstate['issued'] += 1
            return True

        def w_next(tag):
            while wstate['issued'] - wstate['consumed'] < NWB:
                if not w_issue():
                    break
            t, i = wstate['pending'].pop(0)
            assert t == tag, (t, tag)
            wstate['consumed'] += 1
            return wbuf[i], ('wbuf', i)

        def sqview(buf, cn):
            return buf[:, 0:8 * cn].rearrange('p (k c) -> p k c', k=8)

        def load_tokens_fm(src_rows_ap, nrows, dst32, dstb, col0, dkey):
            tokctr[0] += 1
            i = tokctr[0] % 2 if 'A' in os.environ.get('TOG', 'A') else 0
            tk = tok32[i]
            dma('sp', tk[0:nrows, :], src_rows_ap, [], [('tok32', i)], dsem_in[i])
            for half in range(2):
                b = bank()
                for q in range(4):
                    kc = half * 4 + q
                    P.op('pe', lambda e, b=b, q=q, kc=kc: e.transpose(ps[:, b, q * 128:q * 128 + nrows], tk[0:nrows, kc * 128:(kc + 1) * 128], ident[0:nrows, 0:nrows]),
                         reads=[('tok32', i), 'ident'], writes=[('ps', b)], track=('W' not in os.environ.get('TOG', '')))
                src = ps[:, b, :].rearrange('p (q t) -> p q t', q=4)[:, :, 0:nrows]
                if dst32 is not None:
                    cp(dst32[:, half * 4:half * 4 + 4, col0:col0 + nrows], src, [('ps', b), ('tok32', i)], [dkey], eng='act')
                if dstb is not None:
                    cp(dstb[:, half * 4:half * 4 + 4, col0:col0 + nrows], src, [('ps', b)], ['hb' if dkey == 'h32' else 'H3'], eng='dve')

        def store_fm_tokens(src32, skey, col0, nrows, dst_rows_ap, nch=8, feat0=0):
            i = bank_ctr[0] % 2
            tk = tok32[i]
            for g0 in range(0, nch, 4):
                b = bank()
                gn = min(4, nch - g0)
                for q in range(gn):
                    tr(ps[0:nrows, b, q * 128:(q + 1) * 128], src32[:, g0 + q, col0:col0 + nrows], ident[:, :],
                       [skey, 'ident'], [('ps', b)])
                cp(tk[0:nrows, g0 * 128:(g0 + gn) * 128], ps[0:nrows, b, 0:gn * 128], [('ps', b)], [('tok32', i)], eng='act')
            dma('sp', dst_rows_ap, tk[0:nrows, 0:nch * 128], [('tok32', i)], [], dsem_out[i], is_out=True)

        def layernorm(idx, n, eps):
            P.phase = 'layernorm'
            zsq = HB_[0]
            cp(hb[:, :, 0:n], h32[:, :, 0:n], ['h32'], ['hb'], eng='dve')
            act(zsq[:, :, 0:n], h32[:, :, 0:n], AF.Square, ['h32'], ['H0'])
            b1 = bank(); b2 = bank()
            for kc in range(8):
                mm(ps[:, b1, 0:n], onesb[:], hb[:, kc, 0:n], kc == 0, kc == 7, ['onesb', 'hb'], [('ps', b1)])
            for kc in range(8):
                mm(ps[:, b2, 0:n], onesb[:], zsq[:, kc, 0:n], kc == 0, kc == 7, ['onesb', 'H0'], [('ps', b2)])
            mean, msq, var, rstd, nmr = [s[:, 0:n] for s in st1]
            ts(mean, ps[:, b1, 0:n], 1.0 / D, None, ALU.mult, None, [('ps', b1)], ['st0'])
            tt(msq, mean, mean, ALU.mult, ['st0'], ['st1'])
            stt(var, ps[:, b2, 0:n], 1.0 / D, msq, ALU.mult, ALU.subtract, [('ps', b2), 'st1'], ['st2'])
            ts(var, var, 0.0, eps, ALU.max, ALU.add, ['st2'], ['st2'])
            act(var, var, AF.Sqrt, ['st2'], ['st2'])
            recip(rstd, var, ['st2'], ['st3'])
            tt(nmr, mean, rstd, ALU.mult, ['st0', 'st3'], ['st4'])
            fine = [('h32', kc) for kc in range(8)]
            P.op('dve', lambda e: e.engine_nop(), writes=['h32'] + fine)
            tpool = [(st1[1], 'st1'), (st1[2], 'st2'), (st2[0], 'su0'), (st2[1], 'su1'), (st2[2], 'su2'), (st2[3], 'su3'), (st2[4], 'su4'), (sgb[0], ('sg', 0))]
            for kc in range(8):
                T_, tk_ = tpool[kc]
                T_ = T_[:, 0:n]
                tt(T_, h32[:, kc, 0:n], rstd, ALU.mult, [('h32', kc), 'st3'], [tk_])
                tt(T_, T_, nmr, ALU.subtract, [tk_, 'st4'], [tk_])
                act(h32[:, kc, 0:n], T_, AF.Identity, [tk_, 'lng', 'lnb'], [('h32', kc)],
                    bias=lnb[:, idx, kc:kc + 1], scale=lng[:, idx, kc:kc + 1])
                act(hb[:, kc, 0:n], T_, AF.Identity, [tk_, 'lng', 'lnb'], ['hb'],
                    bias=lnb[:, idx, kc:kc + 1], scale=lng[:, idx, kc:kc + 1])
            P.op('dve', lambda e: e.engine_nop(), writes=['h32'] + fine)

        def ffn(wi, wo, ln_idx, n):
            P.phase = 'ffn'
            actb = HX
            sg = [sgb[0][:, 0:n], sgb[1][:, 0:n]]
            for g in range(11):
                wb, wk = w_next((wi, g))
                wv = sqview(wb, 512)
                for jj in range(2):
                    j = 2 * g + jj
                    pg = bank(); pu = bank()
                    for kc in range(8):
                        mm(ps[:, pg, 0:n], wv[:, kc, jj * 128:(jj + 1) * 128], hb[:, kc, 0:n], kc == 0, kc == 7, [wk, 'hb'], [('ps', pg)])
                    for kc in range(8):
                        mm(ps[:, pu, 0:n], wv[:, kc, 256 + jj * 128:256 + (jj + 1) * 128], hb[:, kc, 0:n], kc == 0, kc == 7, [wk, 'hb'], [('ps', pu)])
                    act(sg[jj], ps[:, pg, 0:n], AF.Silu, [('ps', pg)], [('sg', jj)])
                    tt(actb[:, j, 0:n], sg[jj], ps[:, pu, 0:n], ALU.mult, [('sg', jj), ('ps', pu)], ['HX%d' % (j // 8)])
            for m in range(8):
                wb, wk = w_next((wo, m))
                wv = wb[:, 0:NJ * 128].rearrange('p (j c) -> p j c', j=NJ)
                po = bank()
                for j in range(NJ):
                    mm(ps[:, po, 0:n], wv[:, j, :], actb[:, j, 0:n], j == 0, j == NJ - 1, [wk, 'HX%d' % (j // 8)], [('ps', po)])
                stt(h32[:, m, 0:n], ps[:, po, 0:n], 0.5 / ALPHA, h32[:, m, 0:n], ALU.mult, ALU.add, [('ps', po), 'h32'], ['h32'])
            layernorm(ln_idx, n, LN_EPS / (ALPHA * ALPHA))

        def proj(w, ncols, xin, xkey, n, evac):
            nb = (ncols + 511) // 512
            for b in range(nb):
                c0 = b * 512
                cn = min(512, ncols - c0)
                wb, wk = w_next((w, b))
                wv = sqview(wb, cn)
                for q in range(cn // 128):
                    m = c0 // 128 + q
                    pb = bank()
                    for kc in range(8):
                        mm(ps[:, pb, 0:n], wv[:, kc, q * 128:(q + 1) * 128], xin[:, kc, 0:n], kc == 0, kc == 7, [wk, xkey], [('ps', pb)])
                    evac(m, pb)

        def mem_kv():
            P.phase = 'mem_kv'
            for r in range(2):
                load_tokens_fm(di['mem'][r * 128:(r + 1) * 128, :], 128, None, memT, r * 128, 'memT')
            for (w, outname, isk) in [('xa_wk', 'pmk', True), ('xa_wv', 'pmv', False)]:
                for b in range(2):
                    wb, wk = w_next((w, b))
                    wv = sqview(wb, 512)
                    for r in range(2):
                        pb = bank()
                        for kc in range(8):
                            mm(ps[:, pb, :], memT[:, kc, r * 128:(r + 1) * 128], wv[:, kc, :], kc == 0, kc == 7, ['H3', wk], [('ps', pb)])
                        i = bank_ctr[0] % 2
                        cp(tok32[i][:, 0:512], ps[:, pb, :], [('ps', pb)], [('tok32', i)], eng='act')
                        if not isk:
                            cp(mvb[:, r, b * 512:(b + 1) * 512], ps[:, pb, :], [('ps', pb)], ['mvb'], eng='dve')
                        dma('sp', di[outname][r * 128:(r + 1) * 128, b * 512:(b + 1) * 512], tok32[i][:, 0:512], [('tok32', i)], [], dsem_out[i], is_out=True)
                    if isk:
                        for q in range(4):
                            m = b * 4 + q
                            pb = bank()
                            for kc in range(8):
                                mm(ps[:, pb, 0:NMEM], wv[:, kc, q * 128:(q + 1) * 128], memT[:, kc, :], kc == 0, kc == 7, [wk, 'H3'], [('ps', pb)])
                            cp(mkT[:, m, :], ps[:, pb, 0:NMEM], [('ps', pb)], ['mkT'], eng='act')

        def mixer_prompt(ti):
            P.phase = 'mixer_prompt'
            n = NT
            r32, k32, v32, a32, ls32 = FB_
            glb = HB_[1]; geb = HB_[2]; g0b = HB_[3]
            g1b = HX[:, 0:8, :]; xcb = HX[:, 8:16, :]
            first = (ti == 0)
            tmp = st1[0]

            def evac(m, pb):
                psn = ps[:, pb, 0:n]
                SS, KP = stsel(m)
                tmp = SS[0]
                if m < 26:
                    act(tmp[:, 1:n], ps[:, pb, 0:n - 1], AF.Copy, [('ps', pb), 'mu'], [KP + '0'], scale=mu[:, m:m + 1])
                    if first:
                        P.op('dve', lambda e, tmp=tmp: e.memset(tmp[:, 0:1], 0.0), writes=[KP + '0'])
                    else:
                        tt(tmp[:, 0:1], carry_sh[:, m:m + 1], mu[:, m:m + 1], ALU.mult, ['carry_sh', 'mu'], [KP + '0'])
                    cp(carry_sh[:, m:m + 1], ps[:, pb, n - 1:n], [('ps', pb)], ['carry_sh'])
                    if m < 24:
                        dst = [r32, k32, v32][m // 8][:, m % 8, :]
                        dkey = ['F0', 'F1', 'F2'][m // 8]
                        stt(dst, psn, omu[:, m:m + 1], tmp[:, 0:n], ALU.mult, ALU.add, [('ps', pb), 'omu', KP + '0'], [dkey])
                    else:
                        xs_ = SS[1][:, 0:n]
                        stt(xs_, psn, omu[:, m:m + 1], tmp[:, 0:n], ALU.mult, ALU.add, [('ps', pb), 'omu', KP + '0'], [KP + '1'])
                        lb = SS[2][:, 0:n].bitcast(BF16)[:, 0:n]
                        if m == 24:
                            act(lb[0:64, :], xs_[0:64, :], AF.Tanh, [KP + '1'], [KP + '2'])
                            cp(lb[64:128, :], xs_[64:128, :], [KP + '1'], [KP + '2'])
                            for (lo, dstt, dk, bvec) in [(0, ls32, 'F4', 'decay_w0'), (64, a32, 'F3', 'aaa_a0')]:
                                for q in range(8):
                                    p2 = bank()
                                    mm(ps[:, p2, 0:n], w2a2[lo:lo + 64, q * 128:(q + 1) * 128], lb[lo:lo + 64, :], True, True, ['w2a2', KP + '2'], [('ps', p2)])
                                    act(dstt[:, q, :], ps[:, p2, 0:n], AF.Sigmoid, [('ps', p2), 'v_' + bvec], [dk], bias=vec[bvec][:, q:q + 1])
                        else:
                            act(lb, xs_, AF.Sigmoid, [KP + '1'], [KP + '2'])
                            for q in range(8):
                                p2 = bank()
                                mm(ps[:, p2, 0:n], g2[:, q * 128:(q + 1) * 128], lb, True, True, ['g2', KP + '2'], [('ps', p2)])
                                cp(glb[:, q, :], ps[:, p2, 0:n], [('ps', p2)], ['H1'], eng='act')
                elif m < 34:
                    cp(pl[:, m - 26, 3:3 + n], psn, [('ps', pb)], ['pl'], eng='act')
                elif m < 42:
                    act(geb[:, m - 34, :], psn, AF.Gelu, [('ps', pb)], ['H2'])
                elif m < 50:
                    act(g0b[:, m - 42, :], psn, AF.Sigmoid, [('ps', pb)], ['H3'])
                else:
                    act(g1b[:, m - 50, :], psn, AF.Sigmoid, [('ps', pb)], ['HX0'])

            if first:
                P.op('dve', lambda e: e.memset(pl[:, :, 0:3], 0.0), writes=['pl'])
            else:
                cp(pl[:, :, 0:3], osml[:, :, 4:7], ['osml'], ['pl'])
            proj('w_in', PW, hb, 'hb', n, evac)
            cp(osml[:, :, 4:7], pl[:, :, n:n + 3], ['pl'], ['osml'])
            dump('r32', r32[:], 'F0'); dump('k32', k32[:], 'F1'); dump('v32', v32[:], 'F2'); dump('a32', a32[:], 'F3'); dump('ls32', ls32[:], 'F4')
            if ti == int(os.environ.get('NTI', NTILES)) - 1:
                cp(osml[:, :, 0:3], pl[:, :, n:n + 3], ['pl'], ['osml'])
                cp(osh[:], carry_sh[:], ['carry_sh'], ['osh'])
            return dict(r32=r32, k32=k32, v32=v32, a32=a32, ls32=ls32, glb=glb, geb=geb, g0b=g0b, g1b=g1b, xcb=xcb)

        def lru_prompt(ti, B):
            P.phase = 'lru_prompt'
            n = NT
            geb, g1b, xcb = B['geb'], B['g1b'], B['xcb']
            lmb = HX[:, 16:24, :]
            def lru_chunk(c):
                SS, KP = stsel(c)
                xc = SS[0][:, 0:n]; gr = SS[1][:, 0:n]; gi = SS[2][:, 0:n]; t3 = SS[3][:, 0:n]; hs = SS[4][:, 0:n]
                act(xc, pl[:, c, 3:3 + n], AF.Identity, ['pl', 'cw', 'v_conv_b'], [KP + '0'], bias=vec['conv_b'][:, c:c + 1], scale=cw[:, 3, c:c + 1])
                for j in range(3):
                    stt(xc, pl[:, c, j:j + n], cw[:, j, c:c + 1], xc, ALU.mult, ALU.add, ['pl', 'cw', KP + '0'], [KP + '0'])
                yield
                cp(xcb[:, c, :], xc, [KP + '0'], ['HX1'], eng='act')
                p1 = bank(); p2 = bank()
                yield
                mm(ps[:, p1, 0:n], wrbd[:, c, :], xcb[:, c, :], True, True, ['wrbd', 'HX1'], [('ps', p1)])
                yield
                mm(ps[:, p2, 0:n], wibd[:, c, :], xcb[:, c, :], True, True, ['wibd', 'HX1'], [('ps', p2)])
                yield
                act(gr, ps[:, p1, 0:n], AF.Sigmoid, [('ps', p1), 'v_lru_br'], [KP + '1'], bias=vec['lru_br'][:, c:c + 1])
                yield
                act(gi, ps[:, p2, 0:n], AF.Sigmoid, [('ps', p2), 'v_lru_bi'], [KP + '2'], bias=vec['lru_bi'][:, c:c + 1])
                yield
                act(gr, gr, AF.Exp, [KP + '1', 'lsp'], [KP + '1'], scale=lsp[:, c:c + 1])
                yield
                act(t3, gr, AF.Square, [KP + '1'], [KP + '3'])
                yield
                ts(t3, t3, -1.0, 1.0, ALU.mult, ALU.add, [KP + '3'], [KP + '3'])
                yield
                ts(t3, t3, 0.0, None, ALU.max, None, [KP + '3'], [KP + '3'])
                yield
                act(t3, t3, AF.Sqrt, [KP + '3'], [KP + '3'])
                yield
                tt(gi, gi, xc, ALU.mult, [KP + '2', KP + '0'], [KP + '2'])
                yield
                tt(gi, gi, t3, ALU.mult, [KP + '2', KP + '3'], [KP + '2'])
                yield
                if ti == 0:
                    P.op('dve', lambda e, hs=hs, gr=gr, gi=gi: e.tensor_tensor_scan(out=hs, data0=gr, data1=gi, initial=0.0, op0=ALU.mult, op1=ALU.add),
                         reads=[KP + '1', KP + '2'], writes=[KP + '4'])
                else:
                    P.op('dve', lambda e, c=c, hs=hs, gr=gr, gi=gi: e.tensor_tensor_scan(out=hs, data0=gr, data1=gi, initial=carry_h[:, c:c + 1], op0=ALU.mult, op1=ALU.add),
                         reads=[KP + '1', KP + '2', ('carry_h', c)], writes=[KP + '4'])
                yield
                cp(carry_h[:, c:c + 1], hs[:, n - 1:n], [KP + '4'], [('carry_h', c)])
                yield
                tt(t3, hs, geb[:, c, :], ALU.mult, [KP + '4', 'H2'], [KP + '3'])
                yield
                tt(lmb[:, c, :], t3, g1b[:, c, :], ALU.mult, [KP + '3', 'HX0'], ['HX2'])
            for _c0 in range(0, 8, 2):
                interleave2(lru_chunk(_c0), lru_chunk(_c0 + 1))
            if ti == int(os.environ.get('NTI', NTILES)) - 1:
                cp(osml[:, :, 3:4], carry_h[:].unsqueeze(2), [('carry_h', c_) for c_ in range(8)], ['osml'])
            return lmb


        plb = pl[:].rearrange('p a b -> p (a b)').bitcast(BF16)
        plA = plb[:, 0:8 * NT].rearrange('p (a b) -> p a b', a=8)
        plB = plb[:, 8 * NT:16 * NT].rearrange('p (a b) -> p a b', a=8)

        def rwkv_core(B, state_in, state_out, skip_inverse=False, prepared=None):
            P.phase = 'rwkv_core'
            n = NT
            r32, k32, v32, a32, ls32 = B['r32'], B['k32'], B['v32'], B['a32'], B['ls32']
            bc8 = lambda v: v[:].unsqueeze(2).to_broadcast([128, 8, n])
            QR = HX[:, 0:16, :].rearrange('p (k two) (c t) -> p k two c t', two=2, t=C)
            KT = HB_[0]; NB = hb
            vb = plB
            if prepared is not None:
                prepared(QR, KT, NB, vb, WCs)
            else:
                kk32 = pl[:, :, 0:n]
                tt(kk32, k32[:], bc8(vec['k_k']), ALU.mult, ['F1', 'v_k_k'], ['pl'])
                act(KT[:], kk32, AF.Square, ['pl'], ['H0'])
                rkb = HB_[2]

                def prep_chunk(kc):
                    SS, KP = stsel(kc)
                    s_ = SS[0][:, 0:n]; u_ = SS[1][:, 0:n]; u2 = SS[2][:, 0:n]
                    pb = bank()
                    mm(ps[:, pb, 0:n], bdb[:], KT[:, kc, :], True, True, ['bdb', 'H0'], [('ps', pb)])
                    act(s_, ps[:, pb, 0:n], AF.Sqrt, [('ps', pb)], [KP + '0'])
                    act(u_, a32[:, kc, :], AF.Identity, ['F3', 'v_k_a', 'oka'], [KP + '1'], bias=oka[:, kc:kc + 1], scale=vec['k_a'][:, kc:kc + 1])
                    yield
                    ts(s_, s_, 1e-12, None, ALU.max, None, [KP + '0'], [KP + '0'])
                    yield
                    recip(s_, s_, [KP + '0'], [KP + '0'])
                    yield
                    tt(kk32[:, kc, :], kk32[:, kc, :], s_, ALU.mult, ['pl', KP + '0'], ['pl'])
                    yield
                    tt(k32[:, kc, :], k32[:, kc, :], u_, ALU.mult, ['F1', KP + '1'], ['F1'])
                    yield
                    tt(a32[:, kc, :], a32[:, kc, :], kk32[:, kc, :], ALU.mult, ['F3', 'pl'], ['F3'])
                    yield
                    tt(u2, r32[:, kc, :], k32[:, kc, :], ALU.mult, ['F0', 'F1'], [KP + '2'])
                    yield
                    act(rkb[:, kc, :], u2, AF.Copy, [KP + '2', 'v_r_k'], ['H2'], scale=vec['r_k'][:, kc:kc + 1])
                for _c0 in range(0, 8, 2):
                    interleave2(prep_chunk(_c0), prep_chunk(_c0 + 1))
                P.phase = 'rw_decay'
                def decay_chunk(kc):
                    SS, KP = stsel(kc)
                    cs = SS[0][:, 0:n]; dd = SS[1][:, 0:n]; Wi = SS[2][:, 0:n]; We = SS[3][:, 0:n]; Wv = SS[4][:, 0:n]
                    P.op('dve', lambda e, kc=kc, cs=cs: e.tensor_tensor_scan(out=cs, data0=reset[:, 0:n], data1=ls32[:, kc, :], initial=0.0, op0=ALU.mult, op1=ALU.add),
                         reads=['reset', 'F4'], writes=[KP + '0'])
                    yield
                    tt(dd, cs, ls32[:, kc, :], ALU.subtract, [KP + '0', 'F4'], [KP + '1'])
                    yield
                    act(Wi, cs, AF.Exp, [KP + '0'], [KP + '2'], scale=-C0)
                    yield
                    act(We, dd, AF.Exp, [KP + '1'], [KP + '3'], scale=-C0)
                    yield
                    act(Wv, cs, AF.Exp, [KP + '0'], [KP + '4'], scale=C0)
                    c4 = lambda a: a.rearrange('p (c t) -> p c t', t=C)
                    yield
                    tt(QR[:, kc, 1, :, :], c4(r32[:, kc, :]), c4(Wi), ALU.mult, ['F0', KP + '2'], ['HX0', 'HX1'])
                    yield
                    tt(QR[:, kc, 0, :, :], c4(kk32[:, kc, :]), c4(We), ALU.mult, ['pl', KP + '3'], ['HX0', 'HX1'])
                    yield
                    cp(WCs[:, kc, :], c4(Wi)[:, :, C - 1], [KP + '2'], ['WCs'])
                    yield
                    tt(Wi, k32[:, kc, :], Wv, ALU.mult, ['F1', KP + '4', KP + '2'], [KP + '2'])
                    yield
                    stt(We, a32[:, kc, :], -1.0, Wv, ALU.mult, ALU.mult, ['F3', KP + '4', KP + '3'], [KP + '3'])
                    yield
                    cp(NB[:, kc, :], We, [KP + '3'], ['hb'], eng='act')
                    yield
                    cp(KT[:, kc, :], Wi, [KP + '2'], ['H0'], eng='act')
                for _c0 in range(0, 8, 2):
                    interleave2(decay_chunk(_c0), decay_chunk(_c0 + 1))
                P.phase = 'rw_tok'
                vb = plB
                cp(vb[:, :, 0:n], v32[:], ['F2'], ['pl'], eng='act')
                for kc in range(8):
                    pb = bank()
                    mm(ps[:, pb, 0:n], bdb[:], rkb[:, kc, :], True, True, ['bdb', 'H2'], [('ps', pb)])
                    tt(v32[:, kc, :], v32[:, kc, :], ps[:, pb, 0:n], ALU.mult, ['F2', ('ps', pb)], ['F2'])
            tmv = lambda a: a.rearrange('p a b -> p (a b)').rearrange('p (c x) -> p c x', c=NCH)
            vT = tmv(HB_[2][:]); kTt = tmv(plA); nbT = tmv(plB)

            def to_tokmajor(src, skey, dst, dkey):
                for c in range(NCH):
                    pb = bank()
                    pbv = ps[:, pb, :].bitcast(BF16)
                    for hp in range(8):
                        for par in range(2):
                            lo = par * 64
                            tr(pbv[lo:lo + 64, hp * 64:(hp + 1) * 64], src[lo:lo + 64, hp, c * C:(c + 1) * C], identb[lo:lo + 64, lo:lo + 64],
                               [skey, 'identb'], [('ps', pb)])
                    cp(dst[:, c, :], pbv[:, 0:512], [('ps', pb)], [dkey], eng=('act' if c % 2 else 'dve'))
            to_tokmajor(vb, 'pl', vT, 'H2')
            to_tokmajor(KT, 'H0', kTt, 'pl')
            to_tokmajor(NB, 'hb', nbT, 'pl')
            vTv = lambda c, hp, lo: vT[lo:lo + 64, c, hp * 64:(hp + 1) * 64]
            kTv = lambda c, hp, lo: kTt[lo:lo + 64, c, hp * 64:(hp + 1) * 64]
            nTv = lambda c, hp, lo: nbT[lo:lo + 64, c, hp * 64:(hp + 1) * 64]
            m_su_ui = mask[:, 0:128].unsqueeze(1).to_broadcast([128, 8, 128])
            m_sl = mask[:, 128:192].unsqueeze(1).to_broadcast([128, 8, 64])
            m_eye = mask[:, 192:256].unsqueeze(1).to_broadcast([128, 8, 64])
            def ph1(c0):
                P.phase = 'rw_ph1'
                ctx = []
                for s in range(2):
                    c = c0 + s
                    b1a = bank(); b1b = bank()
                    for hp in range(8):
                        for par in range(2):
                            lo = par * 64
                            bsel = b1a if hp < 4 else b1b
                            mm(ps[lo:lo + 64, bsel, (hp % 4) * 128:(hp % 4 + 1) * 128], KT[lo:lo + 64, hp, c * C:(c + 1) * C],
                               QR[lo:lo + 64, hp, :, c, :], True, True, ['H0', 'HX0', 'HX1'], [('ps', bsel)])
                    for (bsel, h0) in [(b1a, 0), (b1b, 4)]:
                        tt(LP[:, h0:h0 + 4, c, :], ps[:, bsel, :].rearrange('p (h x) -> p h x', h=4), m_su_ui[:, 0:4, :], ALU.mult,
                           [('ps', bsel), 'mask'], [('LP', c)])
                    b2a = bank(); b2b = bank()
                    for hp in range(8):
                        for par in range(2):
                            lo = par * 64
                            bsel = b2a if hp < 4 else b2b
                            mm(ps[lo:lo + 64, bsel, (hp % 4) * 128:(hp % 4 + 1) * 128], NB[lo:lo + 64, hp, c * C:(c + 1) * C],
                               QR[lo:lo + 64, hp, :, c, :], True, True, ['hb', 'HX0', 'HX1'], [('ps', bsel)])
                    for (bsel, h0) in [(b2a, 0), (b2b, 4)]:
                        tt(PN[:, h0:h0 + 4, c, :], ps[:, bsel, :].rearrange('p (h x) -> p h x', h=4), m_su_ui[:, 0:4, :], ALU.mult,
                           [('ps', bsel), 'mask'], [('PN', c)])
                    b3 = bank()
                    for hp in range(8):
                        for par in range(2):
                            lo = par * 64
                            mm(ps[lo:lo + 64, b3, hp * 64:(hp + 1) * 64], QR[lo:lo + 64, hp, 0, c, :], NB[lo:lo + 64, hp, c * C:(c + 1) * C],
                               True, True, ['HX0', 'HX1', 'hb'], [('ps', b3)])
                    pp = PP[s][0]
                    cp(pp[:, :, 0:64], PN[:, :, c, 0:64], [('PN', c)], [('PP', s, 0)], eng='act')
                    tt(pp[:, :, 64:128], ps[:, b3, :].rearrange('p (h x) -> p h x', h=8), m_sl, ALU.mult, [('ps', b3), 'mask'], [('PP', s, 0)])
                    tt(X32[s][:], PN[:, :, c, 0:64], m_eye, ALU.add, [('PN', c), 'mask'], [('X32', s)])
                    cp(Xb[s][:], X32[s][:], [('X32', s)], [('Xb', s)], eng='act')
                    ctx.append(c)
                if skip_inverse:
                    for s_ in range(2):
                        cp(XT[:, :, ctx[s_], :], X32[s_][:], [('X32', s_)], [('XT', ctx[s_])], eng='act')
                for lvl in ([] if skip_inverse else range(1, 6)):
                    yield
                    P.phase = 'rw_ph1'
                    cur = (lvl - 1) % 2; nxt = lvl % 2
                    banks = []
                    for s in range(2):
                        ba = bank(); bb = bank()
                        src = PP[s][cur]
                        for hp in range(8):
                            for par in range(2):
                                lo = par * 64
                                bsel = ba if hp < 4 else bb
                                o0 = (hp % 4) * 128
                                mm(ps[lo:lo + 64, bsel, o0:o0 + 64], src[lo:lo + 64, hp, 64:128], src[lo:lo + 64, hp, 0:64], True, True,
                                   [('PP', s, cur)], [('ps', bsel)])
                                mm(ps[lo:lo + 64, bsel, o0 + 64:o0 + 128], src[lo:lo + 64, hp, 0:64], src[lo:lo + 64, hp, 64:128], True, True,
                                   [('PP', s, cur)], [('ps', bsel)])
                        banks.append((ba, bb))
                    for s in range(2):
                        ba, bb = banks[s]
                        dst = PP[s][nxt]
                        cp(dst[:, 0:4, :], ps[:, ba, :].rearrange('p (h x) -> p h x', h=4), [('ps', ba)], [('PP', s, nxt)], eng='act')
                        cp(dst[:, 4:8, :], ps[:, bb, :].rearrange('p (h x) -> p h x', h=4), [('ps', bb)], [('PP', s, nxt)], eng='dve')
                    xb_ = []
                    for s in range(2):
                        bx = bank()
                        src = PP[s][nxt]
                        for hp in range(8):
                            for par in range(2):
                                lo = par * 64
                                mm(ps[lo:lo + 64, bx, hp * 64:(hp + 1) * 64], src[lo:lo + 64, hp, 64:128], Xb[s][lo:lo + 64, hp, :], True, True,
                                   [('PP', s, nxt), ('Xb', s)], [('ps', bx)])
                        xb_.append(bx)
                    for s in range(2):
                        bx = xb_[s]
                        tt(X32[s][:], X32[s][:], ps[:, bx, :].rearrange('p (h x) -> p h x', h=8), ALU.add, [('X32', s), ('ps', bx)], [('X32', s)])
                        if lvl < 5:
                            cp(Xb[s][:], X32[s][:], [('X32', s)], [('Xb', s)], eng='act')
                        else:
                            cp(XT[:, :, ctx[s], :], X32[s][:], [('X32', s)], [('XT', ctx[s])], eng='act')
            Y32 = FB_[0]

            def ph2(c):
                P.phase = 'rw_ph2'
                state_in(c)
                cur = c % 2
                a0 = A0b[cur]
                bR = bank()
                for hp in range(8):
                    for par in range(2):
                        lo = par * 64
                        o = ps[lo:lo + 64, bR, hp * 64:(hp + 1) * 64]
                        mm(o, QR[lo:lo + 64, hp, 0, c, :], a0[lo:lo + 64, hp, :], True, False, ['HX0', 'HX1', ('A0b', cur)], [('ps', bR)])
                        mm(o, LP[lo:lo + 64, hp, c, 0:64], vTv(c, hp, lo), False, True, [('LP', c), 'H2'], [('ps', bR)])
                cp(RHSb[:], ps[:, bR, :].rearrange('p (h x) -> p h x', h=8), [('ps', bR)], ['RHSb'], eng='act')
                yield
                P.phase = 'rw_ph2'
                bU = bank()
                for hp in range(8):
                    for par in range(2):
                        lo = par * 64
                        mm(ps[lo:lo + 64, bU, hp * 64:(hp + 1) * 64], XT[lo:lo + 64, hp, c, :], RHSb[lo:lo + 64, hp, :], True, True,
                           [('XT', c), 'RHSb'], [('ps', bU)])
                cp(Ub[:], ps[:, bU, :].rearrange('p (h x) -> p h x', h=8), [('ps', bU)], ['Ub'], eng='act')
                yield
                P.phase = 'rw_ph2'
                bD = bank()
                for hp in range(8):
                    for par in range(2):
                        lo = par * 64
                        o = ps[lo:lo + 64, bD, hp * 64:(hp + 1) * 64]
                        mm(o, kTv(c, hp, lo), vTv(c, hp, lo), True, False, ['pl', 'H2'], [('ps', bD)])
                        mm(o, nTv(c, hp, lo), Ub[lo:lo + 64, hp, :], False, True, ['pl', 'Ub'], [('ps', bD)])
                bY = bank()
                for hp in range(8):
                    for par in range(2):
                        lo = par * 64
                        o = ps[lo:lo + 64, bY, hp * 64:(hp + 1) * 64]
                        mm(o, a0[lo:lo + 64, hp, :], QR[lo:lo + 64, hp, 1, c, :], True, False, [('A0b', cur), 'HX0', 'HX1'], [('ps', bY)])
                        mm(o, vTv(c, hp, lo), LP[lo:lo + 64, hp, c, 64:128], False, False, ['H2', ('LP', c)], [('ps', bY)])
                        mm(o, Ub[lo:lo + 64, hp, :], PN[lo:lo + 64, hp, c, 64:128], False, True, ['Ub', ('PN', c)], [('ps', bY)])
                tt(A32[:], A32[:], ps[:, bD, :].rearrange('p (h x) -> p h x', h=8), ALU.add, ['A32', ('ps', bD)], ['A32'])
                tt(A32[:], A32[:], WCs[:, :, c:c + 1].to_broadcast([128, 8, 64]), ALU.mult, ['A32', 'WCs'], ['A32'])
                cp(A0b[1 - cur][:], A32[:], ['A32'], [('A0b', 1 - cur)], eng='act')
                cp(Y32[:, :, c * C:(c + 1) * C], ps[:, bY, :].rearrange('p (h x) -> p h x', h=8), [('ps', bY)], ['F0'], eng='dve')
                state_out(c)
                yield

            def drain(g):
                for _ in g:
                    pass

            def interleave(ga, gb):
                a_ok = b_ok = True
                while a_ok or b_ok:
                    if a_ok:
                        try:
                            next(ga)
                        except StopIteration:
                            a_ok = False
                    if b_ok:
                        try:
                            next(gb)
                        except StopIteration:
                            b_ok = False

            def chain(*gs):
                for g in gs:
                    yield from g
            drain(ph1(0))
            if NCH == 4:
                interleave(ph1(2), chain(ph2(0), ph2(1)))
                drain(chain(ph2(2), ph2(3)))
            else:
                for c0 in range(2, NCH, 2):
                    drain(ph1(c0))
                for c in range(NCH):
                    drain(ph2(c))
            return Y32, v32

        def rwkv_post(Y, Ykey, bonus, bkey, glb, g0b, lmb, mb, n):
            P.phase = 'rwkv_post'
            Yb = HB_[0]; ysq = hb
            cp(Yb[:, :, 0:n], Y[:, :, 0:n], [Ykey], ['H0'], eng='act')
            act(ysq[:, :, 0:n], Y[:, :, 0:n], AF.Square, [Ykey], ['hb'])
            def post_chunk(kc):
                b1 = bank(); b2 = bank()
                mm(ps[:, b1, 0:n], bdb[:], Yb[:, kc, 0:n], True, True, ['bdb', 'H0'], [('ps', b1)])
                yield
                mm(ps[:, b2, 0:n], bdb[:], ysq[:, kc, 0:n], True, True, ['bdb', 'hb'], [('ps', b2)])
                SS, KP = stsel(kc)
                mean = SS[0][:, 0:n]; var = SS[1][:, 0:n]; t_ = SS[2][:, 0:n]
                yield
                ts(mean, ps[:, b1, 0:n], 1.0 / 64, None, ALU.mult, None, [('ps', b1)], [KP + '0'])
                yield
                tt(var, mean, mean, ALU.mult, [KP + '0'], [KP + '1'])
                yield
                stt(var, ps[:, b2, 0:n], 1.0 / 64, var, ALU.mult, ALU.subtract, [('ps', b2), KP + '1'], [KP + '1'])
                yield
                ts(var, var, 0.0, GN_EPS, ALU.max, ALU.add, [KP + '1'], [KP + '1'])
                yield
                act(var, var, AF.Sqrt, [KP + '1'], [KP + '1'])
                yield
                recip(var, var, [KP + '1'], [KP + '1'])
                yield
                tt(t_, Y[:, kc, 0:n], mean, ALU.subtract, [Ykey, KP + '0'], [KP + '2'])
                yield
                tt(t_, t_, var, ALU.mult, [KP + '2', KP + '1'], [KP + '2'])
                yield
                act(t_, t_, AF.Identity, [KP + '2', 'v_gn_g', 'v_gn_b'], [KP + '2'], bias=vec['gn_b'][:, kc:kc + 1], scale=vec['gn_g'][:, kc:kc + 1])
                yield
                tt(t_, t_, bonus[:, kc, 0:n], ALU.add, [KP + '2', bkey], [KP + '2'])
                yield
                tt(t_, t_, glb[:, kc, 0:n], ALU.mult, [KP + '2', 'H1'], [KP + '2'])
                yield
                tt(t_, t_, g0b[:, kc, 0:n], ALU.mult, [KP + '2', 'H3'], [KP + '2'])
                yield
                tt(mb[:, kc, 0:n], t_, lmb[:, kc, 0:n], ALU.add, [KP + '2', 'HX2'], ['H2'])
            for _c0 in range(0, 8, 2):
                interleave2(post_chunk(_c0), post_chunk(_c0 + 1))

        def resid_ln(w, xin, xkey, ln_idx, n):
            def evac(m, pb):
                stt(h32[:, m, 0:n], ps[:, pb, 0:n], 1.0 / ALPHA, h32[:, m, 0:n], ALU.mult, ALU.add, [('ps', pb), 'h32'], ['h32'])
            proj(w, D, xin, xkey, n, evac)
            layernorm(ln_idx, n, LN_EPS / (ALPHA * ALPHA))

        def xattn_prompt(n):
            P.phase = 'xattn_prompt'
            qb = HB_[0]; ob = HB_[1]; pT = HX[:, 0:8, :]
            def evq(m, pb):
                cp(qb[:, m, 0:n], ps[:, pb, 0:n], [('ps', pb)], ['H0'], eng='act')
            proj('xa_wq', D, hb, 'hb', n, evq)
            for h in range(4):
                for mc in range(2):
                    pb = bank()
                    for dc in range(2):
                        mm(ps[:, pb, 0:n], mkT[:, 2 * h + dc, mc * 128:(mc + 1) * 128], qb[:, 2 * h + dc, 0:n], dc == 0, dc == 1, ['mkT', 'H0'], [('ps', pb)])
                    act(pT[:, 2 * h + mc, 0:n], ps[:, pb, 0:n], AF.Exp, [('ps', pb)], ['HX0'], scale=1.0 / 16.0)
            rds = []
            for h in range(4):
                pd = bank()
                for mc in range(2):
                    mm(ps[:, pd, 0:n], onesb[:], pT[:, 2 * h + mc, 0:n], mc == 0, mc == 1, ['onesb', 'HX0'], [('ps', pd)])
                SS, KP = stsel(h)
                rd = SS[h // 2][:, 0:n]; rk_ = KP + str(h // 2)
                recip(rd, ps[:, pd, 0:n], [('ps', pd)], [rk_])
                rds.append((rd, rk_))
            for h in range(4):
                rd, rk_ = rds[h]
                for dc in range(2):
                    po = bank()
                    for mc in range(2):
                        mm(ps[:, po, 0:n], mvb[:, mc, (2 * h + dc) * 128:(2 * h + dc + 1) * 128], pT[:, 2 * h + mc, 0:n], mc == 0, mc == 1, ['mvb', 'HX0'], [('ps', po)])
                    tt(ob[:, 2 * h + dc, 0:n], ps[:, po, 0:n], rd, ALU.mult, [('ps', po), rk_], ['H1'])
            resid_ln('xa_wo', ob, 'H1', 2, n)

        if stage >= 1:
            mem_kv()
        if 'm' in PARTS or PARTS == 'abcdefg':
            P.op('dve', lambda e: e.memset(A32[:], 0.0), writes=['A32'])
            P.op('dve', lambda e: e.memset(A0b[0][:], 0.0), writes=[('A0b', 0)])
        for ti in range(int(os.environ.get('NTI', NTILES)) if stage >= 9 else 1):
            TOG = os.environ.get('TOG', '')
            for r in range(1 if '1' in TOG else NT // 128):
                load_tokens_fm(di['xp'][ti * NT + r * 128: ti * NT + (r + 1) * 128, :], 128, h32, None if 'D' in TOG else hb, r * 128, 'h32')
            if stage >= 2:
                ffn('ffn1_wi', 'ffn1_wo', 0, NT)
            if ti == 0:
                dump('h1', h32[:], 'h32')
            if stage < 3:
                break
            B = mixer_prompt(ti)
            if stage < 4:
                break
            lmb = lru_prompt(ti, B)
            if ti == 0:
                dump('lm', lmb, 'HX2')
            if stage < 5:
                break
            Y32, bonus = rwkv_core(B, lambda c: None, lambda c: None)
            if ti == 0:
                dump('Y', Y32[:], 'F0')
            if stage < 6:
                break
            mb = HB_[2]
            rwkv_post(Y32, 'F0', bonus, 'F2', B['glb'], B['g0b'], lmb, mb, NT)
            resid_ln('w_mix_out', mb, 'H2', 1, NT)
            if ti == 0:
                dump('h2', h32[:], 'h32')
            if stage < 7:
                break
            xattn_prompt(NT)
            if ti == 0:
                dump('h3', h32[:], 'h32')
            if stage < 8:
                break
            ffn('ffn2_wi', 'ffn2_wo', 3, NT)
            for r in range(NT // 128):
                store_fm_tokens(h32, 'h32', r * 128, 128, di['yp'][ti * NT + r * 128: ti * NT + (r + 1) * 128, :])
        if stage < 9:
            P.emit()
            return nc, P
        def prompt_outputs():
            pass
            Sout = FB_[1].rearrange('p a b -> p (a b)')[:, 0:1024].rearrange('p (hp par k) -> p hp par k', hp=8, par=2)
            for g in range(2):
                pb = bank()
                for q in range(4):
                    hp = g * 4 + q
                    tr(ps[0:64, pb, q * 128:(q + 1) * 128], A32[:, hp, :], ident[:, :], ['A32', 'ident'], [('ps', pb)])
                cp(Sout[0:64, g * 4:(g + 1) * 4, :, :], ps[0:64, pb, :].rearrange('p (q par k) -> p q par k', q=4, par=2), [('ps', pb)], ['F1'], eng='act')
            dma('sp', di['prw'].rearrange('(hp par v) k -> v hp par k', hp=8, par=2), Sout[0:64, :, :, :], ['F1'], [], dsem_misc, is_out=True)
            osm2 = FB_[2]
            cp(osm2[:, 0:8, 0:3], osml[:, :, 0:3], ['osml'], ['F2'])
            cp(osm2[:, 0:8, 3:4], osml[:, :, 3:4], ['osml'], ['F2'])
            store_fm_tokens(osm2, 'F2', 0, 3, di['pcv'][:, :])
            store_fm_tokens(osm2, 'F2', 3, 1, di['plru'].rearrange('(o d) -> o d', o=1))
            osh3 = FB_[3]
            cp(osh3[:, 0:8, 0:1], osh[:, 0:8].unsqueeze(2), ['osh'], ['F3'])
            cp(osh3[:, 0:8, 1:2], osh[:, 8:16].unsqueeze(2), ['osh'], ['F3'])
            cp(osh3[:, 0:8, 2:3], osh[:, 16:24].unsqueeze(2), ['osh'], ['F3'])
            cp(osh3[:, 0:2, 3:4], osh[:, 24:26].unsqueeze(2), ['osh'], ['F3'])
            pshv = di['psh'].rearrange('(o d) -> o d', o=1)
            for q in range(3):
                store_fm_tokens(osh3, 'F3', q, 1, pshv[:, q * 1024:(q + 1) * 1024])
            store_fm_tokens(osh3, 'F3', 3, 1, pshv[:, 3072:3328], nch=2)


        if os.environ.get('NTI') != '0':
            prompt_outputs()
        def sample_path():
            P.phase = 'sample_path'
            n = NS
            sm = lambda nm, shp, dt=F32: P.sbuf(nm, shp, dt)
            rc = sm('s_rc', [128, 8, n]); kc_ = sm('s_kc', [128, 8, n]); vc = sm('s_vc', [128, 8, n]); ac = sm('s_ac', [128, 8, n]); lsc = sm('s_lsc', [128, 8, n])
            mk32 = mkT[:].rearrange('p a b -> p (a b)').bitcast(F32)
            mv32 = mvb[:].rearrange('p a b -> p (a b)').bitcast(F32)
            prevS = mk32[:, 0:26 * n].rearrange('p (c b) -> p c b', c=26)
            praw = mk32[:, 26 * n:52 * n].rearrange('p (c b) -> p c b', c=26)
            scvT = mv32[:, 0:24 * n].rearrange('p (c b) -> p c b', c=8)
            h0S = mv32[:, 24 * n:32 * n].rearrange('p (c b) -> p c b', c=8)
            yc = mv32[:, 32 * n:40 * n].rearrange('p (c b) -> p c b', c=8)
            bonc = mv32[:, 40 * n:48 * n].rearrange('p (c b) -> p c b', c=8)
            plc = mv32[:, 48 * n:56 * n].rearrange('p (c b) -> p c b', c=8)
            hsc = mv32[:, 56 * n:64 * n].rearrange('p (c b) -> p c b', c=8)
            P.op('dve', lambda e: e.engine_nop(), writes=['mkT', 'mvb', 's_prev', 's_praw', 's_scvT', 's_h0', 's_yc', 's_bonc', 's_plc', 's_hsc'])
            BD = tok32[1][:].rearrange('p (h x) -> p h x', h=8); ones32 = st1[4][:, 0:128]
            glb = HB_[1]; geb = HB_[2]; g0b = HB_[3]; g1b = HX[:, 0:8, :]; lmb = HX[:, 16:24, :]
            dsS = [P.dma_sem() for _ in range(7)]

            def load_fm(src_rows_ap, nrows, nchunks, dst, dkey, tki, sem):
                tk = tok32[tki]
                dma('sp', tk[0:nrows, 0:nchunks * 128], src_rows_ap, [], [('tok32', tki)], sem)
                for g0 in range(0, nchunks, 4):
                    gn = min(4, nchunks - g0)
                    b = bank()
                    for q in range(gn):
                        tr(ps[:, b, q * 128:q * 128 + nrows], tk[0:nrows, (g0 + q) * 128:(g0 + q + 1) * 128], ident[0:nrows, 0:nrows],
                           [('tok32', tki), 'ident'], [('ps', b)])
                    cp(dst[:, g0:g0 + gn, 0:nrows], ps[:, b, :].rearrange('p (q t) -> p q t', q=4)[:, 0:gn, 0:nrows], [('ps', b)], [dkey], eng='act')

            load_tokens_fm(di['xs'], n, h32, hb, 0, 'h32')
            for q in range(4):
                c0 = q * 8; cn = min(8, 26 - c0)
                tmpd = FB_[0] if q % 2 == 0 else FB_[1]
                load_fm(di['ssh'][:, c0 * 128:(c0 + cn) * 128], n, cn, tmpd, 'F%d' % (q % 2), q % 2, dsS[q % 2])
                cp(prevS[:, c0:c0 + cn, :], tmpd[:, 0:cn, 0:n], ['F%d' % (q % 2)], ['s_prev'])
            load_fm(di['slru'], n, 8, h0S, 's_h0', 0, dsS[0])
            load_fm(di['scv'].rearrange('b j d -> (b j) d'), 3 * n, 8, scvT, 's_scvT', 1, dsS[1])
            dma('sp', di['scv_o'][:, 0:2, :], di['scv'][:, 1:3, :], [], [], P.dma_sem(), is_out=True)

            ffn('ffn1_wi', 'ffn1_wo', 0, n)

            tmp = st1[0]

            def evac(m, pb):
                psn = ps[:, pb, 0:n]
                if m < 26:
                    cp(praw[:, m, :], psn, [('ps', pb)], ['s_praw'], eng='act')
                    ts(tmp[:, 0:n], prevS[:, m, :], mu[:, m:m + 1], None, ALU.mult, None, ['s_prev', 'mu'], ['st0'])
                    if m < 24:
                        dst = [rc, kc_, vc][m // 8][:, m % 8, :]
                        dkey = ['s_rc', 's_kc', 's_vc'][m // 8]
                        stt(dst, psn, omu[:, m:m + 1], tmp[:, 0:n], ALU.mult, ALU.add, [('ps', pb), 'omu', 'st0'], [dkey])
                    else:
                        xs_ = st1[1][:, 0:n]
                        stt(xs_, psn, omu[:, m:m + 1], tmp[:, 0:n], ALU.mult, ALU.add, [('ps', pb), 'omu', 'st0'], ['st1'])
                        lb = st1[2][:, 0:NT].bitcast(BF16)[:, 0:n]
                        if m == 24:
                            act(lb[0:64, :], xs_[0:64, :], AF.Tanh, ['st1'], ['st2'])
                            cp(lb[64:128, :], xs_[64:128, :], ['st1'], ['st2'])
                            for (lo, dstt, dk, bvec) in [(0, lsc, 's_lsc', 'decay_w0'), (64, ac, 's_ac', 'aaa_a0')]:
                                for q in range(8):
                                    p2 = bank()
                                    mm(ps[:, p2, 0:n], w2a2[lo:lo + 64, q * 128:(q + 1) * 128], lb[lo:lo + 64, :], True, True, ['w2a2', 'st2'], [('ps', p2)])
                                    act(dstt[:, q, :], ps[:, p2, 0:n], AF.Sigmoid, [('ps', p2), 'v_' + bvec], [dk], bias=vec[bvec][:, q:q + 1])
                        else:
                            act(lb, xs_, AF.Sigmoid, ['st1'], ['st2'])
                            for q in range(8):
                                p2 = bank()
                                mm(ps[:, p2, 0:n], g2[:, q * 128:(q + 1) * 128], lb, True, True, ['g2', 'st2'], [('ps', p2)])
                                cp(glb[:, q, 0:n], ps[:, p2, 0:n], [('ps', p2)], ['H1'], eng='act')
                elif m < 34:
                    cp(plc[:, m - 26, :], psn, [('ps', pb)], ['s_plc'], eng='act')
                elif m < 42:
                    act(geb[:, m - 34, 0:n], psn, AF.Gelu, [('ps', pb)], ['H2'])
                elif m < 50:
                    act(g0b[:, m - 42, 0:n], psn, AF.Sigmoid, [('ps', pb)], ['H3'])
                else:
                    act(g1b[:, m - 50, 0:n], psn, AF.Sigmoid, [('ps', pb)], ['HX0'])
            proj('w_in', PW, hb, 'hb', n, evac)
            for q in range(4):
                c0 = q * 8; cn = min(8, 26 - c0)
                store_fm_tokens(praw[:, c0:c0 + cn, :], 's_praw', 0, n, di['ssh_o'][:, c0 * 128:(c0 + cn) * 128], nch=cn)
            store_fm_tokens(plc, 's_plc', 0, n, di['scv_o'][:, 2, :])

            sc3 = scvT[:].rearrange('p c (b j) -> p c b j', j=3)
            xc = st1[0][:, 0:n]; gr = st1[1][:, 0:n]; gi = st1[2][:, 0:n]; t3 = st1[3][:, 0:n]; hs = st1[4][:, 0:n]
            xcb = HX[:, 8:16, :]
            for c in range(8):
                act(xc, plc[:, c, :], AF.Identity, ['s_plc', 'cw', 'v_conv_b'], ['st0'], bias=vec['conv_b'][:, c:c + 1], scale=cw[:, 3, c:c + 1])
                for j in range(3):
                    stt(xc, sc3[:, c, :, j], cw[:, j, c:c + 1], xc, ALU.mult, ALU.add, ['s_scvT', 'cw', 'st0'], ['st0'])
                cp(xcb[:, c, 0:n], xc, ['st0'], ['HX1'], eng='act')
                p1 = bank(); p2 = bank()
                mm(ps[:, p1, 0:n], wrbd[:, c, :], xcb[:, c, 0:n], True, True, ['wrbd', 'HX1'], [('ps', p1)])
                mm(ps[:, p2, 0:n], wibd[:, c, :], xcb[:, c, 0:n], True, True, ['wibd', 'HX1'], [('ps', p2)])
                act(gr, ps[:, p1, 0:n], AF.Sigmoid, [('ps', p1), 'v_lru_br'], ['st1'], bias=vec['lru_br'][:, c:c + 1])
                act(gi, ps[:, p2, 0:n], AF.Sigmoid, [('ps', p2), 'v_lru_bi'], ['st2'], bias=vec['lru_bi'][:, c:c + 1])
                act(gr, gr, AF.Exp, ['st1', 'lsp'], ['st1'], scale=lsp[:, c:c + 1])
                act(t3, gr, AF.Square, ['st1'], ['st3'])
                ts(t3, t3, -1.0, 1.0, ALU.mult, ALU.add, ['st3'], ['st3'])
                ts(t3, t3, 0.0, None, ALU.max, None, ['st3'], ['st3'])
                act(t3, t3, AF.Sqrt, ['st3'], ['st3'])
                tt(gi, gi, xc, ALU.mult, ['st2', 'st0'], ['st2'])
                tt(gi, gi, t3, ALU.mult, ['st2', 'st3'], ['st2'])
                tt(hs, gr, h0S[:, c, :], ALU.mult, ['st1', 's_h0'], ['st4'])
                tt(hsc[:, c, :], hs, gi, ALU.add, ['st4', 'st2'], ['s_hsc'])
                tt(t3, hsc[:, c, :], geb[:, c, 0:n], ALU.mult, ['s_hsc', 'H2'], ['st3'])
                tt(lmb[:, c, 0:n], t3, g1b[:, c, 0:n], ALU.mult, ['st3', 'HX0'], ['HX2'])
            store_fm_tokens(hsc, 's_hsc', 0, n, di['slru_o'])

            F1flat = FB_[1].rearrange('p a b -> p (a b)')
            Souts = [F1flat[:, 0:1024].rearrange('p (hp par k) -> p hp par k', hp=8, par=2),
                     F1flat[:, 1024:2048].rearrange('p (hp par k) -> p hp par k', hp=8, par=2)]
            Skeys = ['F1', ('F1', 'b')]; Ssems = [dsS[4], P.dma_sem()]
            BDs = [tok32[0][:].rearrange('p (h x) -> p h x', h=8), tok32[1][:].rearrange('p (h x) -> p h x', h=8)]
            Bsems = [P.dma_sem(), dsS[3]]
            for i_ in range(2):
                P.op('dve', lambda e, i_=i_: e.memset(tok32[i_][:], 0.0), writes=[('tok32', i_)])

            def prefetch_state(b_):
                if b_ >= NS:
                    return
                src = di['srw'][b_].rearrange('(hp par v) k -> par v hp k', hp=8, par=2)
                for par in range(2):
                    dma('sp', BDs[b_ % 2][par * 64:(par + 1) * 64, :, par * 64:(par + 1) * 64], src[par], [], [('tok32', b_ % 2)], Bsems[b_ % 2])
            prefetch_state(0)
            bc16 = lambda v: v[:].unsqueeze(2).to_broadcast([128, 8, n])
            v816 = lambda t: t[:, 0:8 * n].rearrange('p (k b) -> p k b', k=8)
            kkn = plc; Wvc = hsc
            sqb = HB_[0][:, :, 0:n]
            tt(kkn[:], kc_[:], bc16(vec['k_k']), ALU.mult, ['s_kc', 'v_k_k', 's_plc'], ['s_plc'])
            act(sqb, kkn[:], AF.Square, ['s_plc'], ['H0'])
            pb = bank()
            for kc in range(8):
                mm(ps[:, pb, kc * n:(kc + 1) * n], bdb[:], sqb[:, kc, :], True, True, ['bdb', 'H0'], [('ps', pb)])
            s16 = v816(st1[0])
            act(s16, ps[:, pb, 0:8 * n].rearrange('p (k b) -> p k b', k=8), AF.Sqrt, [('ps', pb)], ['st0'])
            ts(s16, s16, 1e-12, None, ALU.max, None, ['st0'], ['st0'])
            recip(s16, s16, ['st0'], ['st0'])
            tt(kkn[:], kkn[:], s16, ALU.mult, ['s_plc', 'st0'], ['s_plc'])
            u16 = v816(st1[1])
            tt(u16, ac[:], bc16(vec['k_a']), ALU.mult, ['s_ac', 'v_k_a'], ['st1'])
            tt(u16, u16, bc16(oka), ALU.add, ['st1', 'oka'], ['st1'])
            tt(kc_[:], kc_[:], u16, ALU.mult, ['s_kc', 'st1'], ['s_kc'])
            tt(ac[:], ac[:], kkn[:], ALU.mult, ['s_ac', 's_plc'], ['s_ac'])
            r16 = v816(st1[2])
            tt(r16, rc[:], kc_[:], ALU.mult, ['s_rc', 's_kc'], ['st2'])
            tt(r16, r16, bc16(vec['r_k']), ALU.mult, ['st2', 'v_r_k'], ['st2'])
            cp(sqb, r16, ['st2'], ['H0'], eng='act')
            pb = bank()
            for kc in range(8):
                mm(ps[:, pb, kc * n:(kc + 1) * n], bdb[:], sqb[:, kc, :], True, True, ['bdb', 'H0'], [('ps', pb)])
            tt(bonc[:], vc[:], ps[:, pb, 0:8 * n].rearrange('p (k b) -> p k b', k=8), ALU.mult, ['s_vc', ('ps', pb)], ['s_bonc'])
            act(Wvc[:], lsc[:], AF.Exp, ['s_lsc', 's_hsc'], ['s_hsc'], scale=C0)
            act(lsc[:], lsc[:], AF.Exp, ['s_lsc'], ['s_lsc'], scale=-C0)
            tt(rc[:], rc[:], lsc[:], ALU.mult, ['s_rc', 's_lsc'], ['s_rc'])
            tt(kc_[:], kc_[:], Wvc[:], ALU.mult, ['s_kc', 's_hsc'], ['s_kc'])
            stt(ac[:], ac[:], -1.0, Wvc[:], ALU.mult, ALU.mult, ['s_ac', 's_hsc'], ['s_ac'])

            for g in range(NS // NCH):
                Bp = dict(r32=FB_[0], k32=FB_[1], v32=FB_[2], a32=FB_[3], ls32=FB_[4])
                gs = slice(g * NCH, (g + 1) * NCH)

                def prepared(QR, KT, NB, vb, WCs_, gs=gs):
                    c0v = lambda t: t.rearrange('p k (c t) -> p k c t', t=C)[:, :, :, 0]
                    P.op('dve', lambda e: e.memset(HX[:, 0:16, :], 0.0), writes=['HX0', 'HX1'])
                    P.op('dve', lambda e: e.memset(KT[:], 0.0), writes=['H0'])
                    P.op('dve', lambda e: e.memset(NB[:], 0.0), writes=['hb'])
                    P.op('dve', lambda e: e.memset(vb[:, :, 0:NT], 0.0), writes=['pl'])
                    cp(QR[:, :, 0, :, 0], kkn[:, :, gs], ['s_plc'], ['HX0', 'HX1'])
                    cp(QR[:, :, 1, :, 0], rc[:, :, gs], ['s_rc'], ['HX0', 'HX1'])
                    cp(c0v(KT[:]), kc_[:, :, gs], ['s_kc'], ['H0'])
                    cp(c0v(NB[:]), ac[:, :, gs], ['s_ac'], ['hb'])
                    cp(c0v(vb[:, :, 0:NT]), vc[:, :, gs], ['s_vc'], ['pl'])
                    cp(WCs_[:, :, :], lsc[:, :, gs], ['s_lsc'], ['WCs'])

                def state_in(c, g=g):
                    b_ = g * NCH + c
                    prefetch_state(b_ + 1)
                    BD = BDs[b_ % 2]
                    pb = bank()
                    for hp in range(8):
                        mm(ps[:, pb, hp * 64:(hp + 1) * 64], BD[:, hp, :], mask[:, 192:256], True, True, [('tok32', b_ % 2), 'mask'], [('ps', pb)])
                    v3 = ps[:, pb, :].rearrange('p (h x) -> p h x', h=8)
                    cp(A32[:], v3, [('ps', pb)], ['A32'], eng='dve')
                    cp(A0b[c % 2][:], v3, [('ps', pb)], [('A0b', c % 2)], eng='act')

                def state_out(c, g=g):
                    b_ = g * NCH + c
                    Sout = Souts[b_ % 2]; skey = Skeys[b_ % 2]
                    for gg in range(2):
                        pb = bank()
                        for q in range(4):
                            hp = gg * 4 + q
                            tr(ps[0:64, pb, q * 128:(q + 1) * 128], A32[:, hp, :], ident[:, :], ['A32', 'ident'], [('ps', pb)])
                        cp(Sout[0:64, gg * 4:(gg + 1) * 4, :, :], ps[0:64, pb, :].rearrange('p (q par k) -> p q par k', q=4, par=2), [('ps', pb)], [skey], eng='act')
                    dma('sp', di['srw_o'][b_].rearrange('(hp par v) k -> v hp par k', hp=8, par=2), Sout[0:64, :, :, :], [skey], [], Ssems[b_ % 2], is_out=True)
                Y32, bon = rwkv_core(Bp, state_in, state_out, skip_inverse=True, prepared=prepared)
                cp(yc[:, :, g * NCH:(g + 1) * NCH], Y32[:].rearrange('p k (c t) -> p k c t', t=C)[:, :, :, 0], ['F0'], ['s_yc'])
            mb = HB_[2]
            rwkv_post(yc, 's_yc', bonc, 's_bonc', glb, g0b, lmb, mb, n)
            resid_ln('w_mix_out', mb, 'H2', 1, n)

            qc = FB_[0]; qT = FB_[1].rearrange('p a b -> p (a b)')[:, 0:1024]; sel = FB_[2].rearrange('p a b -> p (a b)')[:, 0:NS * 128].rearrange('p (b m) -> p b m', b=NS)
            Kbs = [FB_[3].rearrange('p a b -> p (a b)').rearrange('p (mc f) -> p mc f', mc=2),
                   pl[:].rearrange('p a b -> p (a b)')[:, 0:2048].rearrange('p (mc f) -> p mc f', mc=2)]
            Kkeys = ['F3', 'pl']; Ksems = [dsS[5], P.dma_sem()]
            prod = FB_[4].rearrange('p a b -> p (a b)')[:, 0:1024]
            Vbs = [HB_[0][:].rearrange('p a b -> p (a b)').rearrange('p (mc f) -> p mc f', mc=2),
                   HB_[2][:].rearrange('p a b -> p (a b)').rearrange('p (mc f) -> p mc f', mc=2)]
            Vkeys = ['H0', 'H2']; Vsems = [dsS[6], P.dma_sem()]
            ob = HB_[1]
            sc = st1[0][:, 0:128]; ex = st1[1][:, 0:128]; den = st1[2][:, 0:64]; pbf = st1[3][:, 0:NT].bitcast(BF16)[:, 0:128]

            def evq(m, pb):
                cp(qc[:, m, 0:n], ps[:, pb, 0:n], [('ps', pb)], ['F0'], eng='act')
            proj('xa_wq', D, hb, 'hb', n, evq)
            for g0 in range(0, 8, 4):
                pb = bank()
                for q in range(4):
                    tr(ps[0:n, pb, q * 128:(q + 1) * 128], qc[:, g0 + q, 0:n], ident[:, :], ['F0', 'ident'], [('ps', pb)])
                cp(qT[0:n, g0 * 128:(g0 + 4) * 128], ps[0:n, pb, :], [('ps', pb)], ['F1'], eng='act')
            cp(sel[0:n, :, :], ident[0:n, 0:n].unsqueeze(2).to_broadcast([n, n, 128]), ['ident'], ['F2'])
            for b_ in range(NS):
                Kb = Kbs[b_ % 2]; kkey = Kkeys[b_ % 2]
                dma('sp', Kb, di['cmk'][b_].rearrange('(mc p) f -> p mc f', p=128), [], [kkey], Ksems[b_ % 2])
                pq = [bank(), bank()]
                for hf in range(2):
                    mm(ps[:, pq[hf], :], sel[0:n, b_, :], qT[0:n, hf * 512:(hf + 1) * 512], True, True, ['F2', 'F1'], [('ps', pq[hf])])
                for mc in range(2):
                    for hf in range(2):
                        tt(prod[:, hf * 512:(hf + 1) * 512], Kb[:, mc, hf * 512:(hf + 1) * 512], ps[:, pq[hf], :], ALU.mult, [kkey, ('ps', pq[hf])], ['F4'])
                    P.op('dve', lambda e, b_=b_, mc=mc: e.tensor_reduce(out=sc[:, (b_ * 2 + mc) * 4:(b_ * 2 + mc) * 4 + 4], in_=prod.rearrange('p (h d) -> p h d', h=4), axis=AX.X, op=ALU.add),
                         reads=['F4'], writes=['st0'])
            act(ex, sc, AF.Exp, ['st0'], ['st1'], scale=1.0 / 16.0)
            dma('sp', ones32, di['c_all'][:, 128:256], [], ['st4'], dsS[2])
            pdn = bank()
            mm(ps[:, pdn, 0:128], ones32, ex, True, True, ['st4', 'st1'], [('ps', pdn)])
            d4 = ps[:, pdn, 0:128].rearrange('p (b mc h) -> p b mc h', mc=2, h=4)
            den3 = den.rearrange('p (b h) -> p b h', h=4)
            cp(den3, d4[:, :, 0, :], [('ps', pdn)], ['st2'])
            tt(den3, den3, d4[:, :, 1, :], ALU.add, ['st2', ('ps', pdn)], ['st2'])
            recip(den, den, ['st2'], ['st2'])
            tt(pbf.rearrange('p (b mc h) -> p b mc h', mc=2, h=4), ex.rearrange('p (b mc h) -> p b mc h', mc=2, h=4),
               den3.unsqueeze(2).to_broadcast([128, NS, 2, 4]), ALU.mult, ['st1', 'st2'], ['st3'])
            po = bank()
            for b_ in range(NS):
                Vb = Vbs[b_ % 2]; vkey = Vkeys[b_ % 2]
                dma('pool', Vb, di['cmv'][b_].rearrange('(mc p) f -> p mc f', p=128), [], [vkey], Vsems[b_ % 2])
                for c in range(8):
                    for mc in range(2):
                        col = (b_ * 2 + mc) * 4 + c // 2
                        mm(ps[:, po, c * NS + b_:c * NS + b_ + 1], Vb[:, mc, c * 128:(c + 1) * 128], pbf[:, col:col + 1], mc == 0, mc == 1, [vkey, 'st3'], [('ps', po)])
            cp(ob[:, :, 0:n], ps[:, po, 0:8 * NS].rearrange('p (c b) -> p c b', c=8), [('ps', po)], ['H1'], eng='act')
            resid_ln('xa_wo', ob, 'H1', 2, n)
            ffn('ffn2_wi', 'ffn2_wo', 3, n)
            store_fm_tokens(h32, 'h32', 0, n, di['ys'])

        if do_sample:
            sample_path()
        P.emit()
    return nc, P


_CACHE = {}


def _consts():
    a = np.arange(128) % 64
    b = np.arange(64)
    su = (a[:, None] < b[None, :]).astype(np.float32)
    ui = (a[:, None] <= b[None, :]).astype(np.float32)
    sl = (a[:, None] > b[None, :]).astype(np.float32)
    ey = (a[:, None] == b[None, :]).astype(np.float32)
    bd = np.zeros((128, 128), np.float32)
    bd[:64, :64] = 1.0
    bd[64:, 64:] = 1.0
    rs = np.ones((128, NT), np.float32)
    rs[:, ::C] = 0.0
    return {'c_all': np.ascontiguousarray(np.concatenate([np.eye(128, dtype=np.float32), np.ones((128, 128), np.float32), bd, su, ui, sl, ey, rs], axis=1))}


def make_in_maps(inputs):
    f = lambda a: np.ascontiguousarray(np.asarray(a, dtype=np.float32))
    shared = {}
    for nm in ['ffn1_wi', 'ffn1_wo', 'ffn2_wi', 'ffn2_wo', 'w_in', 'decay_w2', 'aaa_a2', 'gate_g2',
               'lru_wr', 'lru_wi', 'w_mix_out', 'xa_wq', 'xa_wk', 'xa_wv', 'xa_wo']:
        shared[nm] = np.ascontiguousarray(f(inputs[nm])[0])
    shared['prm'] = np.ascontiguousarray(np.concatenate(
        [f(inputs[nm])[0].reshape(-1, 128) for nm in ['ln_g', 'ln_b', 'shift_mu', 'conv_w'] + VEC_NAMES], axis=0))
    shared.update(_consts())
    maps = []
    for c in range(8):
        m = dict(shared)
        sl = slice(c * NS, (c + 1) * NS)
        m['xp'] = f(inputs['x_prompt'][c])
        m['mem'] = f(inputs['mem_prompt'][c])
        m['xs'] = f(inputs['x_sample'][sl, 0])
        m['cmk'] = f(inputs['cache_mem_k'][0, sl]).reshape(NS, NMEM, D)
        m['cmv'] = f(inputs['cache_mem_v'][0, sl]).reshape(NS, NMEM, D)
        m['srw'] = f(inputs['state_rwkv'][0, sl]).reshape(NS, D, 64)
        m['ssh'] = f(inputs['state_rwkv_shift'][0, sl])
        m['slru'] = f(inputs['state_lru'][0, sl])
        m['scv'] = f(inputs['state_conv'][0, sl])
        maps.append(m)
    return maps


def kernel(**inputs):
    if 'nc' not in _CACHE:
        _CACHE['nc'] = build()[0]
    nc = _CACHE['nc']
    maps = make_in_maps(inputs)
    res = run_bass_kernel_spmd(nc, maps, core_ids=list(range(8)))
    R = res.results
    cat = lambda k: np.stack([np.asarray(r[k], dtype=np.float32) for r in R])
    catc = lambda k: np.concatenate([np.asarray(r[k], dtype=np.float32) for r in R], axis=0)
    yp = cat('yp')
    ys = catc('ys').reshape(8 * NS, 1, D)
    pmk = cat('pmk').reshape(1, 8, NMEM, 4, 256)
    pmv = cat('pmv').reshape(1, 8, NMEM, 4, 256)
    prw = cat('prw').reshape(1, 8, 16, 64, 64)
    psh = cat('psh').reshape(1, 8, RP)
    plru = cat('plru').reshape(1, 8, D)
    pcv = cat('pcv').reshape(1, 8, 3, D)
    srw = catc('srw_o').reshape(1, 8 * NS, 16, 64, 64)
    ssh = catc('ssh_o').reshape(1, 8 * NS, RP)
    slru = catc('slru_o').reshape(1, 8 * NS, D)
    scv = catc('scv_o').reshape(1, 8 * NS, 3, D)
    return (yp, ys, pmk, pmv, prw, psh, plru, pcv, srw, ssh, slru, scv)
```

```python
import math
import os
import numpy as np
from contextlib import ExitStack
import concourse.bass as bass
import concourse.mybir as mybir
from concourse.bass_utils import run_bass_kernel_spmd

F32 = mybir.dt.float32
BF16 = mybir.dt.bfloat16
AF = mybir.ActivationFunctionType
ALU = mybir.AluOpType
AX = mybir.AxisListType

ENGS = ['pe', 'dve', 'act', 'pool', 'sp']

D = 1024
T = 2048
NT = 256
NTILES = T // NT
C = 64
NCH = NT // C
DFF = 2816
NJ = DFF // 128
RP = 3328
PW = 7424
NMEM = 256
NS = 16
ALPHA = 2.0 ** 0.25
LN_EPS = 1e-5
GN_EPS = 64e-5
C0 = math.exp(-0.5)


class DmaSem:
    def __init__(self, sem):
        self.sem = sem
        self.count = 0


class Prog:
    def __init__(self, nc, stack):
        self.nc = nc
        self.stack = stack
        self.ops = {e: [] for e in ENGS}
        self.last_w = {}
        self.readers = {}
        self.seen = {e: {} for e in ENGS}
        self.dsems = []
        self.out_tokens = []

    def dma_sem(self):
        s = DmaSem(self.stack.enter_context(self.nc.semaphore('dsem%d' % len(self.dsems))))
        self.dsems.append(s)
        return s

    def sbuf(self, name, shape, dt):
        return self.stack.enter_context(self.nc.sbuf_tensor(name, list(shape), dt))

    def psum(self, name, shape, dt):
        return self.stack.enter_context(self.nc.psum_tensor(name, list(shape), dt))

    def barrier(self, keys, engines=ENGS):
        if 'B' in os.environ.get('TOG', ''):
            return
        for e in engines:
            self.op(e, None, reads=keys, track=False)

    def op(self, eng, fn, reads=(), writes=(), dsem=None, is_out=False, track=True):
        isps = lambda k: isinstance(k, tuple) and k[0] == 'ps'
        writes = list(writes) + [k for k in reads if isps(k)]
        reads = [k for k in reads if not isps(k)]
        deps = []
        for k in reads:
            t = self.last_w.get(k)
            if t is not None:
                deps.append(t)
        for k in writes:
            t = self.last_w.get(k)
            if t is not None:
                deps.append(t)
            deps.extend(self.readers.get(k, {}).values())
        need = {}
        for t in deps:
            if t[0] == 'eng':
                if t[1] == eng and dsem is None and (eng == 'pe' or os.environ.get('NOSELF')):
                    continue
                key = ('eng', t[1])
            else:
                key = ('dma', id(t[1]))
            if need.get(key, (None, -1))[1] < t[2]:
                need[key] = (t[1], t[2])
        waits = []
        for key, (src, v) in need.items():
            if self.seen[eng].get(key, -1) >= v:
                continue
            self.seen[eng][key] = v
            waits.append((key[0], src, v))
        idx = len(self.ops[eng])
        self.ops[eng].append(dict(fn=fn, waits=waits, dsem=dsem, target=False, phase=getattr(self, 'phase', '')))
        if dsem is not None:
            dsem.count += 16
            tok = ('dma', dsem, dsem.count)
        else:
            tok = ('eng', eng, idx)
        for k in writes:
            self.last_w[k] = tok
            self.readers[k] = {}
        for k in (reads if track else ()):
            r = self.readers.setdefault(k, {})
            rk = (tok[0], tok[1] if tok[0] == 'eng' else id(tok[1]))
            if rk not in r or r[rk][2] < tok[2]:
                r[rk] = tok
        if is_out:
            self.out_tokens.append(tok)
        return tok

    def emit(self):
        nc = self.nc
        fin = {}
        for t in self.out_tokens:
            fin[id(t[1])] = (t[1], max(fin.get(id(t[1]), (None, 0))[1], t[2]))
        self.ops['sp'].append(dict(fn=None, waits=[('dma', s, v) for s, v in fin.values()], dsem=None, target=False))
        for e in ENGS:
            for o in self.ops[e]:
                for kind, src, v in o['waits']:
                    if kind == 'eng':
                        self.ops[src][v]['target'] = True
        semval = {}
        for e in ENGS:
            c = 0
            vals = []
            for o in self.ops[e]:
                if o['target']:
                    c += 1
                vals.append(c)
            semval[e] = vals
        esem = {e: self.stack.enter_context(nc.semaphore('esem_' + e)) for e in ENGS}
        handles = {'pe': 'tensor', 'dve': 'vector', 'act': 'scalar', 'pool': 'gpsimd', 'sp': 'sync'}
        with nc.Block() as block:
            def make(e):
                def body(eng):
                    for o in self.ops[e]:
                        for kind, src, v in o['waits']:
                            if kind == 'eng':
                                eng.wait_ge(esem[src], semval[src][v])
                            else:
                                eng.wait_ge(src.sem, v)
                        if o['fn'] is None:
                            continue
                        inst = o['fn'](eng)
                        if os.environ.get('ANNOT') and o.get('phase'):
                            inst.annotate(o['phase'])
                        if o['dsem'] is not None:
                            inst.then_inc(o['dsem'].sem, 16)
                        elif o['target']:
                            inst.then_inc(esem[e], 1)
                return body
            for e in ENGS:
                getattr(block, handles[e])(make(e))
        self.stats = {e: len(self.ops[e]) for e in ENGS}


VEC_NAMES = ['decay_w0', 'aaa_a0', 'k_k', 'k_a', 'r_k', 'gn_g', 'gn_b', 'conv_b', 'lru_br', 'lru_bi', 'lru_lambda']
W_NAMES = ['ffn1_wi', 'ffn1_wo', 'ffn2_wi', 'ffn2_wo', 'w_in', 'w_mix_out', 'xa_wq', 'xa_wk', 'xa_wv', 'xa_wo']


def build(dbg=None, do_sample=True, stage=99):
    nc = bass.Bass('TRN2', target_bir_lowering=False)
    di = {}

    DECL = os.environ.get('DECL')

    def din(name, shape):
        if DECL and name not in DECL.split(','):
            return None
        di[name] = nc.dram_tensor(name, list(shape), F32, kind='ExternalInput').ap()
        return di[name]

    def dout(name, shape):
        if DECL and name not in DECL.split(','):
            return None
        di[name] = nc.dram_tensor(name, list(shape), F32, kind='ExternalOutput').ap()
        return di[name]

    din('xp', [T, D]); din('mem', [NMEM, D])
    din('xs', [NS, D]); din('cmk', [NS, NMEM, D]); din('cmv', [NS, NMEM, D])
    din('srw', [NS, D, 64]); din('ssh', [NS, RP]); din('slru', [NS, D]); din('scv', [NS, 3, D])
    din('prm', [210, 128])
    din('ffn1_wi', [D, 2 * DFF]); din('ffn1_wo', [DFF, D]); din('ffn2_wi', [D, 2 * DFF]); din('ffn2_wo', [DFF, D])
    din('w_in', [D, PW])
    din('decay_w2', [64, D]); din('aaa_a2', [64, D]); din('gate_g2', [128, D])
    din('lru_wr', [16, 64, 64]); din('lru_wi', [16, 64, 64])
    for w in ['w_mix_out', 'xa_wq', 'xa_wk', 'xa_wv', 'xa_wo']:
        din(w, [D, D])
    din('c_all', [128, 640 + NT])
    dout('yp', [T, D]); dout('ys', [NS, D]); dout('pmk', [NMEM, D]); dout('pmv', [NMEM, D])
    dout('prw', [D, 64]); dout('psh', [RP]); dout('plru', [D]); dout('pcv', [3, D])
    dout('srw_o', [NS, D, 64]); dout('ssh_o', [NS, RP]); dout('slru_o', [NS, D]); dout('scv_o', [NS, 3, D])
    dbg = dbg or {}
    for k, shp in dbg.items():
        dout('dbg_' + k, shp)

    with ExitStack() as st:
        P = Prog(nc, st)
        n = NT
        ident = P.sbuf('ident', [128, 128], F32)
        identb = P.sbuf('identb', [128, 128], BF16)
        onesb = P.sbuf('onesb', [128, 128], BF16)
        bdb = P.sbuf('bdb', [128, 128], BF16)
        mask = P.sbuf('mask', [128, 256], F32)
        reset = P.sbuf('reset', [128, NT], F32)
        lng = P.sbuf('lng', [128, 4, 8], F32); lnb = P.sbuf('lnb', [128, 4, 8], F32)
        mu = P.sbuf('mu', [128, 26], F32); omu = P.sbuf('omu', [128, 26], F32)
        vec = {v: P.sbuf('v_' + v, [128, 8], F32) for v in VEC_NAMES}
        oka = P.sbuf('oka', [128, 8], F32)
        lsp = P.sbuf('lsp', [128, 8], F32)
        cw = P.sbuf('cw', [128, 4, 8], F32)
        w2a2 = P.sbuf('w2a2', [128, D], BF16)
        g2 = P.sbuf('g2', [128, D], BF16)
        wrbd = P.sbuf('wrbd', [128, 8, 128], BF16); wibd = P.sbuf('wibd', [128, 8, 128], BF16)
        WCAP = 4096
        NWB = 4
        wbuf = [P.sbuf('wbuf%d' % i, [128, WCAP], BF16) for i in range(NWB)]
        wsem = [P.dma_sem() for i in range(NWB)]
        h32 = P.sbuf('h32', [128, 8, n], F32)
        hb = P.sbuf('hb', [128, 8, n], BF16)
        FB_ = [P.sbuf('F%d' % i, [128, 8, n], F32) for i in range(5)]
        HX = P.sbuf('HX', [128, 24, n], BF16)
        HB_ = [P.sbuf('H%d' % i, [128, 8, n], BF16) for i in range(4)]
        memT = HB_[3]
        pl = P.sbuf('pl', [128, 8, n + 3], F32)
        st1 = [P.sbuf('st%d' % i, [128, n], F32) for i in range(5)]
        st2 = [P.sbuf('su%d' % i, [128, n], F32) for i in range(5)]
        stsel = lambda i: ((st1, 'st') if i % 2 == 0 else (st2, 'su'))
        carry_sh = P.sbuf('carry_sh', [128, 26], F32)
        carry_h = P.sbuf('carry_h', [128, 8], F32)
        A32 = P.sbuf('A32', [128, 8, 64], F32)
        A0b = [P.sbuf('A0b%d' % i, [128, 8, 64], BF16) for i in range(2)]
        RHSb = P.sbuf('RHSb', [128, 8, 64], BF16)
        Ub = P.sbuf('Ub', [128, 8, 64], BF16)
        WCs = P.sbuf('WCs', [128, 8, NCH], F32)
        X32 = [P.sbuf('X32_%d' % i, [128, 8, 64], F32) for i in range(2)]
        Xb = [P.sbuf('Xb_%d' % i, [128, 8, 64], BF16) for i in range(2)]
        PP = [[P.sbuf('PP_%d_%d' % (i, j), [128, 8, 128], BF16) for j in range(2)] for i in range(2)]
        LP = P.sbuf('LP', [128, 8, NCH, 128], BF16)
        PN = P.sbuf('PN', [128, 8, NCH, 128], BF16)
        XT = P.sbuf('XT', [128, 8, NCH, 64], BF16)
        mkT = P.sbuf('mkT', [128, 8, NMEM], BF16)
        mvb = P.sbuf('mvb', [128, 2, D], BF16)
        tok32 = [P.sbuf('tok32_%d' % i, [128, D], F32) for i in range(2)]
        osml = P.sbuf('osml', [128, 8, 8], F32)
        osh = P.sbuf('osh', [128, 26], F32)
        sgb = [P.sbuf('sgb%d' % i, [128, NT], F32) for i in range(2)]
        ps = P.psum('ps', [128, 8, 512], F32)
        dsem_c = [P.dma_sem() for i in range(4)]
        dsem_in = [P.dma_sem() for i in range(2)]
        dsem_out = [P.dma_sem() for i in range(2)]
        dsem_misc = P.dma_sem()

        bank_ctr = [0]
        tokctr = [0]

        def bank():
            b = bank_ctr[0] % 8
            bank_ctr[0] += 1
            return b

        def mm(out, lhsT, rhs, start, stop, reads, writes):
            P.op('pe', lambda e: e.matmul(out, lhsT=lhsT, rhs=rhs, start=start, stop=stop), reads=reads, writes=writes)

        def tr(out, in_, idn, reads, writes):
            P.op('pe', lambda e: e.transpose(out, in_, idn), reads=reads, writes=writes)

        def act(out, in_, func, reads, writes, bias=None, scale=None):
            kw = {}
            if bias is not None:
                kw['bias'] = bias
            if scale is not None:
                kw['scale'] = scale
            P.op('act', lambda e: e.activation(out=out, in_=in_, func=func, **kw), reads=reads, writes=writes)

        def tt(out, in0, in1, op, reads, writes, eng='dve'):
            P.op(eng, lambda e: e.tensor_tensor(out=out, in0=in0, in1=in1, op=op), reads=reads, writes=writes)

        def ts(out, in0, s1, s2, op0, op1, reads, writes, eng='dve'):
            if s2 is None:
                P.op(eng, lambda e: e.tensor_scalar(out=out, in0=in0, scalar1=s1, scalar2=None, op0=op0), reads=reads, writes=writes)
            else:
                P.op(eng, lambda e: e.tensor_scalar(out=out, in0=in0, scalar1=s1, scalar2=s2, op0=op0, op1=op1), reads=reads, writes=writes)

        def stt(out, in0, scalar, in1, op0, op1, reads, writes):
            P.op('dve', lambda e: e.scalar_tensor_tensor(out=out, in0=in0, scalar=scalar, in1=in1, op0=op0, op1=op1), reads=reads, writes=writes)

        def cp(out, in_, reads, writes, eng='dve'):
            if eng == 'act':
                act(out, in_, AF.Copy, reads, writes)
            else:
                P.op(eng, lambda e: e.tensor_copy(out=out, in_=in_), reads=reads, writes=writes)

        def recip(out, in_, reads, writes):
            P.op('dve', lambda e: e.reciprocal(out=out, in_=in_), reads=reads, writes=writes)

        def dma(eng, out, in_, reads, writes, dsem, is_out=False, **kw):
            P.op(eng, lambda e: e.dma_start(out=out, in_=in_, **kw), reads=reads, writes=writes, dsem=dsem, is_out=is_out)

        def interleave2(ga, gb):
            a_ok = b_ok = True
            while a_ok or b_ok:
                if a_ok:
                    try:
                        next(ga)
                    except StopIteration:
                        a_ok = False
                if b_ok:
                    try:
                        next(gb)
                    except StopIteration:
                        b_ok = False

        def dump(name, src_ap, key):
            if name in dbg:
                dma('sp', di['dbg_' + name], src_ap, [key], [], P.dma_sem(), is_out=True)

        PARTS = os.environ.get('PARTS', 'abcdefg')
        dma('sp', ident[:], di['c_all'][:, 0:128], [], ['ident'], dsem_c[0])
        dma('sp', mask[:], di['c_all'][:, 384:640], [], ['mask'], dsem_c[0])
        dma('sp', reset[:], di['c_all'][:, 640:640 + NT], [], ['reset'], dsem_c[0])
        P.barrier(['ident', 'mask', 'reset'])
        if 'b' in PARTS:
            dma('pool', identb[:], di['c_all'][:, 0:128], [], ['identb'], dsem_c[1])
            dma('pool', onesb[:], di['c_all'][:, 128:256], [], ['onesb'], dsem_c[1])
            dma('pool', bdb[:], di['c_all'][:, 256:384], [], ['bdb'], dsem_c[1])
            dma('pool', w2a2[0:64, :], di['decay_w2'], [], ['w2a2'], dsem_c[1])
            dma('pool', w2a2[64:128, :], di['aaa_a2'], [], ['w2a2'], dsem_c[1])
            dma('pool', g2[:], di['gate_g2'], [], ['g2'], dsem_c[1])
        if 'm' in PARTS or PARTS == 'abcdefg':
            P.op('dve', lambda e: e.memset(wrbd[:], 0.0), writes=['wrbd'])
            P.op('dve', lambda e: e.memset(wibd[:], 0.0), writes=['wibd'])
        for (wt, nm, key) in ([(wrbd, 'lru_wr', 'wrbd'), (wibd, 'lru_wi', 'wibd')] if 'c' in PARTS else []):
            src = di[nm].rearrange('(c two) i o -> two i c o', two=2)
            for par in range(2):
                dma('pool', wt[par * 64:(par + 1) * 64, :, par * 64:(par + 1) * 64], src[par], [], [key], dsem_c[1])
        P.barrier(['identb', 'onesb', 'bdb', 'w2a2', 'g2', 'wrbd', 'wibd'])
        prm = [tok32[0], tok32[1]]
        rows = []
        rows.append((lng[:].rearrange('p l c -> p (l c)'), di['prm'][0:32, :], 'lng'))
        rows.append((lnb[:].rearrange('p l c -> p (l c)'), di['prm'][32:64, :], 'lnb'))
        rows.append((mu[:], di['prm'][64:90, :], 'mu'))
        rows.append((cw[:].rearrange('p l c -> p (l c)'), di['prm'][90:122, :], 'cw'))
        for vi_, v in enumerate(VEC_NAMES):
            rows.append((vec[v][:], di['prm'][122 + 8 * vi_:130 + 8 * vi_, :], 'v_' + v))
        groups = [[]]
        cnt = 0
        for r_ in rows:
            k_ = r_[1].shape[0]
            if cnt + k_ > 128:
                groups.append([]); cnt = 0
            groups[-1].append((cnt, k_) + r_)
            cnt += k_
        for gi_, grp in enumerate(groups if 'd' in PARTS else []):
            tk = prm[gi_ % 2]
            tot = 0
            for (o_, k_, dst, src, key) in grp:
                dma('sp', tk[o_:o_ + k_, 0:128], src, [], [('tok32', gi_ % 2)], dsem_c[2 + gi_ % 2])
                tot = o_ + k_
            pb = bank()
            tr(ps[:, pb, 0:tot], tk[0:tot, 0:128], ident[0:tot, 0:tot], [('tok32', gi_ % 2), 'ident'], [('ps', pb)])
            for (o_, k_, dst, src, key) in grp:
                cp(dst, ps[:, pb, o_:o_ + k_], [('ps', pb)], [key])
        if 'e' in PARTS:
            ts(omu[:], mu[:], -1.0, 1.0, ALU.mult, ALU.add, ['mu'], ['omu'])
            ts(oka[:], vec['k_a'][:], -1.0, 1.0, ALU.mult, ALU.add, ['v_k_a'], ['oka'])
            act(lsp[:], vec['lru_lambda'][:], AF.Exp, ['v_lru_lambda'], ['lsp'], scale=-1.0)
            act(lsp[:], lsp[:], AF.Ln, ['lsp'], ['lsp'], bias=1.0)
            ts(lsp[:], lsp[:], -8.0, None, ALU.mult, None, ['lsp'], ['lsp'])

        def wblocks():
            def ffn_blocks(wi, wo):
                for g in range(11):
                    def f(buf, g=g, wi=wi):
                        v = buf[:, 0:4096].rearrange('p (k c) -> p k c', k=8)
                        return [(v[:, :, 0:256], di[wi][:, g * 256:(g + 1) * 256].rearrange('(k p) c -> p k c', p=128)),
                                (v[:, :, 256:512], di[wi][:, DFF + g * 256:DFF + (g + 1) * 256].rearrange('(k p) c -> p k c', p=128))]
                    yield ((wi, g), f)
                for mp in range(8):
                    def f(buf, mp=mp, wo=wo):
                        v = buf[:, 0:NJ * 128].rearrange('p (j c) -> p j c', j=NJ)
                        return [(v, di[wo][:, mp * 128:(mp + 1) * 128].rearrange('(j p) c -> p j c', p=128))]
                    yield ((wo, mp), f)

            def sq_blocks(w, ncols):
                nb = (ncols + 511) // 512
                for b in range(nb):
                    c0 = b * 512
                    cn = min(512, ncols - c0)
                    def f(buf, c0=c0, cn=cn, w=w):
                        v = buf[:, 0:8 * cn].rearrange('p (k c) -> p k c', k=8)
                        return [(v, di[w][:, c0:c0 + cn].rearrange('(k p) c -> p k c', p=128))]
                    yield ((w, b), f)
            yield from sq_blocks('xa_wk', D)
            yield from sq_blocks('xa_wv', D)
            def one_pass():
                yield from ffn_blocks('ffn1_wi', 'ffn1_wo')
                yield from sq_blocks('w_in', PW)
                yield from sq_blocks('w_mix_out', D)
                yield from sq_blocks('xa_wq', D)
                yield from sq_blocks('xa_wo', D)
                yield from ffn_blocks('ffn2_wi', 'ffn2_wo')
            for it in range(int(os.environ.get('NTI', NTILES)) + (1 if do_sample else 0)):
                for blk, (tag, f) in enumerate(one_pass()):
                    yield (tag, f, it, blk)

        wgen = wblocks()
        wstate = dict(issued=0, consumed=0, pending=[])

        NBLK = 59
        wsc = nc.dram_tensor('wsc', [NBLK, 128, WCAP], BF16, kind='Internal').ap()
        wbsem = [P.dma_sem() for i in range(NWB)]

        def w_used(tag):
            if tag[0].endswith('_wi'):
                return 4096
            if tag[0].endswith('_wo') and tag[0].startswith('ffn'):
                return NJ * 128
            ncols = PW if tag[0] == 'w_in' else D
            return 8 * min(512, ncols - tag[1] * 512)

        def w_issue():
            try:
                item = next(wgen)
            except StopIteration:
                return False
            i = wstate['issued'] % NWB
            if len(item) == 2:
                tag, f = item
                for (dst, src) in f(wbuf[i]):
                    dma('pool', dst, src, [], [('wbuf', i)], wsem[i])
            else:
                tag, f, it, blk = item
                used = w_used(tag)
                if it == 0:
                    for (dst, src) in f(wbuf[i]):
                        dma('pool', dst, src, [], [('wbuf', i)], wsem[i])
                    dma('sp', wsc[blk, :, 0:used], wbuf[i][:, 0:used], [('wbuf', i)], [('wsc', blk)], wbsem[i])
                else:
                    dma('pool', wbuf[i][:, 0:used], wsc[blk, :, 0:used], [('wsc', blk)], [('wbuf', i)], wsem[i])
            wstate['pending'].append((tag, i))
            wstate['issued'] += 1
            return True

        def w_next(tag):
            while wstate['issued'] - wstate['consumed'] < NWB:
                if not w_issue():
                    break
            t, i = wstate['pending'].pop(0)
            assert t == tag, (t, tag)
            wstate['consumed'] += 1
            return wbuf[i], ('wbuf', i)

        def sqview(buf, cn):
            return buf[:, 0:8 * cn].rearrange('p (k c) -> p k c', k=8)

        def load_tokens_fm(src_rows_ap, nrows, dst32, dstb, col0, dkey):
            tokctr[0] += 1
            i = tokctr[0] % 2 if 'A' in os.environ.get('TOG', 'A') else 0
            tk = tok32[i]
            dma('sp', tk[0:nrows, :], src_rows_ap, [], [('tok32', i)], dsem_in[i])
            for half in range(2):
                b = bank()
                for q in range(4):
                    kc = half * 4 + q
                    P.op('pe', lambda e, b=b, q=q, kc=kc: e.transpose(ps[:, b, q * 128:q * 128 + nrows], tk[0:nrows, kc * 128:(kc + 1) * 128], ident[0:nrows, 0:nrows]),
                         reads=[('tok32', i), 'ident'], writes=[('ps', b)], track=('W' not in os.environ.get('TOG', '')))
                src = ps[:, b, :].rearrange('p (q t) -> p q t', q=4)[:, :, 0:nrows]
                if dst32 is not None:
                    cp(dst32[:, half * 4:half * 4 + 4, col0:col0 + nrows], src, [('ps', b), ('tok32', i)], [dkey], eng='act')
                if dstb is not None:
                    cp(dstb[:, half * 4:half * 4 + 4, col0:col0 + nrows], src, [('ps', b)], ['hb' if dkey == 'h32' else 'H3'], eng='dve')

        def store_fm_tokens(src32, skey, col0, nrows, dst_rows_ap, nch=8, feat0=0):
            i = bank_ctr[0] % 2
            tk = tok32[i]
            for g0 in range(0, nch, 4):
                b = bank()
                gn = min(4, nch - g0)
                for q in range(gn):
                    tr(ps[0:nrows, b, q * 128:(q + 1) * 128], src32[:, g0 + q, col0:col0 + nrows], ident[:, :],
                       [skey, 'ident'], [('ps', b)])
                cp(tk[0:nrows, g0 * 128:(g0 + gn) * 128], ps[0:nrows, b, 0:gn * 128], [('ps', b)], [('tok32', i)], eng='act')
            dma('sp', dst_rows_ap, tk[0:nrows, 0:nch * 128], [('tok32', i)], [], dsem_out[i], is_out=True)

        def layernorm(idx, n, eps):
            P.phase = 'layernorm'
            zsq = HB_[0]
            cp(hb[:, :, 0:n], h32[:, :, 0:n], ['h32'], ['hb'], eng='dve')
            act(zsq[:, :, 0:n], h32[:, :, 0:n], AF.Square, ['h32'], ['H0'])
            b1 = bank(); b2 = bank()
            for kc in range(8):
                mm(ps[:, b1, 0:n], onesb[:], hb[:, kc, 0:n], kc == 0, kc == 7, ['onesb', 'hb'], [('ps', b1)])
            for kc in range(8):
                mm(ps[:, b2, 0:n], onesb[:], zsq[:, kc, 0:n], kc == 0, kc == 7, ['onesb', 'H0'], [('ps', b2)])
            mean, msq, var, rstd, nmr = [s[:, 0:n] for s in st1]
            ts(mean, ps[:, b1, 0:n], 1.0 / D, None, ALU.mult, None, [('ps', b1)], ['st0'])
            tt(msq, mean, mean, ALU.mult, ['st0'], ['st1'])
            stt(var, ps[:, b2, 0:n], 1.0 / D, msq, ALU.mult, ALU.subtract, [('ps', b2), 'st1'], ['st2'])
            ts(var, var, 0.0, eps, ALU.max, ALU.add, ['st2'], ['st2'])
            act(var, var, AF.Sqrt, ['st2'], ['st2'])
            recip(rstd, var, ['st2'], ['st3'])
            tt(nmr, mean, rstd, ALU.mult, ['st0', 'st3'], ['st4'])
            fine = [('h32', kc) for kc in range(8)]
            P.op('dve', lambda e: e.engine_nop(), writes=['h32'] + fine)
            tpool = [(st1[1], 'st1'), (st1[2], 'st2'), (st2[0], 'su0'), (st2[1], 'su1'), (st2[2], 'su2'), (st2[3], 'su3'), (st2[4], 'su4'), (sgb[0], ('sg', 0))]
            for kc in range(8):
                T_, tk_ = tpool[kc]
                T_ = T_[:, 0:n]
                tt(T_, h32[:, kc, 0:n], rstd, ALU.mult, [('h32', kc), 'st3'], [tk_])
                tt(T_, T_, nmr, ALU.subtract, [tk_, 'st4'], [tk_])
                act(h32[:, kc, 0:n], T_, AF.Identity, [tk_, 'lng', 'lnb'], [('h32', kc)],
                    bias=lnb[:, idx, kc:kc + 1], scale=lng[:, idx, kc:kc + 1])
                act(hb[:, kc, 0:n], T_, AF.Identity, [tk_, 'lng', 'lnb'], ['hb'],
                    bias=lnb[:, idx, kc:kc + 1], scale=lng[:, idx, kc:kc + 1])
            P.op('dve', lambda e: e.engine_nop(), writes=['h32'] + fine)

        def ffn(wi, wo, ln_idx, n):
            P.phase = 'ffn'
            actb = HX
            sg = [sgb[0][:, 0:n], sgb[1][:, 0:n]]
            for g in range(11):
                wb, wk = w_next((wi, g))
                wv = sqview(wb, 512)
                for jj in range(2):
                    j = 2 * g + jj
                    pg = bank(); pu = bank()
                    for kc in range(8):
                        mm(ps[:, pg, 0:n], wv[:, kc, jj * 128:(jj + 1) * 128], hb[:, kc, 0:n], kc == 0, kc == 7, [wk, 'hb'], [('ps', pg)])
                    for kc in range(8):
                        mm(ps[:, pu, 0:n], wv[:, kc, 256 + jj * 128:256 + (jj + 1) * 128], hb[:, kc, 0:n], kc == 0, kc == 7, [wk, 'hb'], [('ps', pu)])
                    act(sg[jj], ps[:, pg, 0:n], AF.Silu, [('ps', pg)], [('sg', jj)])
                    tt(actb[:, j, 0:n], sg[jj], ps[:, pu, 0:n], ALU.mult, [('sg', jj), ('ps', pu)], ['HX%d' % (j // 8)])
            for m in range(8):
                wb, wk = w_next((wo, m))
                wv = wb[:, 0:NJ * 128].rearrange('p (j c) -> p j c', j=NJ)
                po = bank()
                for j in range(NJ):
                    mm(ps[:, po, 0:n], wv[:, j, :], actb[:, j, 0:n], j == 0, j == NJ - 1, [wk, 'HX%d' % (j // 8)], [('ps', po)])
                stt(h32[:, m, 0:n], ps[:, po, 0:n], 0.5 / ALPHA, h32[:, m, 0:n], ALU.mult, ALU.add, [('ps', po), 'h32'], ['h32'])
            layernorm(ln_idx, n, LN_EPS / (ALPHA * ALPHA))

        def proj(w, ncols, xin, xkey, n, evac):
            nb = (ncols + 511) // 512
            for b in range(nb):
                c0 = b * 512
                cn = min(512, ncols - c0)
                wb, wk = w_next((w, b))
                wv = sqview(wb, cn)
                for q in range(cn // 128):
                    m = c0 // 128 + q
                    pb = bank()
                    for kc in range(8):
                        mm(ps[:, pb, 0:n], wv[:, kc, q * 128:(q + 1) * 128], xin[:, kc, 0:n], kc == 0, kc == 7, [wk, xkey], [('ps', pb)])
                    evac(m, pb)

        def mem_kv():
            P.phase = 'mem_kv'
            for r in range(2):
                load_tokens_fm(di['mem'][r * 128:(r + 1) * 128, :], 128, None, memT, r * 128, 'memT')
            for (w, outname, isk) in [('xa_wk', 'pmk', True), ('xa_wv', 'pmv', False)]:
                for b in range(2):
                    wb, wk = w_next((w, b))
                    wv = sqview(wb, 512)
                    for r in range(2):
                        pb = bank()
                        for kc in range(8):
                            mm(ps[:, pb, :], memT[:, kc, r * 128:(r + 1) * 128], wv[:, kc, :], kc == 0, kc == 7, ['H3', wk], [('ps', pb)])
                        i = bank_ctr[0] % 2
                        cp(tok32[i][:, 0:512], ps[:, pb, :], [('ps', pb)], [('tok32', i)], eng='act')
                        if not isk:
                            cp(mvb[:, r, b * 512:(b + 1) * 512], ps[:, pb, :], [('ps', pb)], ['mvb'], eng='dve')
                        dma('sp', di[outname][r * 128:(r + 1) * 128, b * 512:(b + 1) * 512], tok32[i][:, 0:512], [('tok32', i)], [], dsem_out[i], is_out=True)
                    if isk:
                        for q in range(4):
                            m = b * 4 + q
                            pb = bank()
                            for kc in range(8):
                                mm(ps[:, pb, 0:NMEM], wv[:, kc, q * 128:(q + 1) * 128], memT[:, kc, :], kc == 0, kc == 7, [wk, 'H3'], [('ps', pb)])
                            cp(mkT[:, m, :], ps[:, pb, 0:NMEM], [('ps', pb)], ['mkT'], eng='act')

        def mixer_prompt(ti):
            P.phase = 'mixer_prompt'
            n = NT
            r32, k32, v32, a32, ls32 = FB_
            glb = HB_[1]; geb = HB_[2]; g0b = HB_[3]
            g1b = HX[:, 0:8, :]; xcb = HX[:, 8:16, :]
            first = (ti == 0)
            tmp = st1[0]

            def evac(m, pb):
                psn = ps[:, pb, 0:n]
                SS, KP = stsel(m)
                tmp = SS[0]
                if m < 26:
                    act(tmp[:, 1:n], ps[:, pb, 0:n - 1], AF.Copy, [('ps', pb), 'mu'], [KP + '0'], scale=mu[:, m:m + 1])
                    if first:
                        P.op('dve', lambda e, tmp=tmp: e.memset(tmp[:, 0:1], 0.0), writes=[KP + '0'])
                    else:
                        tt(tmp[:, 0:1], carry_sh[:, m:m + 1], mu[:, m:m + 1], ALU.mult, ['carry_sh', 'mu'], [KP + '0'])
                    cp(carry_sh[:, m:m + 1], ps[:, pb, n - 1:n], [('ps', pb)], ['carry_sh'])
                    if m < 24:
                        dst = [r32, k32, v32][m // 8][:, m % 8, :]
                        dkey = ['F0', 'F1', 'F2'][m // 8]
                        stt(dst, psn, omu[:, m:m + 1], tmp[:, 0:n], ALU.mult, ALU.add, [('ps', pb), 'omu', KP + '0'], [dkey])
                    else:
                        xs_ = SS[1][:, 0:n]
                        stt(xs_, psn, omu[:, m:m + 1], tmp[:, 0:n], ALU.mult, ALU.add, [('ps', pb), 'omu', KP + '0'], [KP + '1'])
                        lb = SS[2][:, 0:n].bitcast(BF16)[:, 0:n]
                        if m == 24:
                            act(lb[0:64, :], xs_[0:64, :], AF.Tanh, [KP + '1'], [KP + '2'])
                            cp(lb[64:128, :], xs_[64:128, :], [KP + '1'], [KP + '2'])
                            for (lo, dstt, dk, bvec) in [(0, ls32, 'F4', 'decay_w0'), (64, a32, 'F3', 'aaa_a0')]:
                                for q in range(8):
                                    p2 = bank()
                                    mm(ps[:, p2, 0:n], w2a2[lo:lo + 64, q * 128:(q + 1) * 128], lb[lo:lo + 64, :], True, True, ['w2a2', KP + '2'], [('ps', p2)])
                                    act(dstt[:, q, :], ps[:, p2, 0:n], AF.Sigmoid, [('ps', p2), 'v_' + bvec], [dk], bias=vec[bvec][:, q:q + 1])
                        else:
                            act(lb, xs_, AF.Sigmoid, [KP + '1'], [KP + '2'])
                            for q in range(8):
                                p2 = bank()
                                mm(ps[:, p2, 0:n], g2[:, q * 128:(q + 1) * 128], lb, True, True, ['g2', KP + '2'], [('ps', p2)])
                                cp(glb[:, q, :], ps[:, p2, 0:n], [('ps', p2)], ['H1'], eng='act')
                elif m < 34:
                    cp(pl[:, m - 26, 3:3 + n], psn, [('ps', pb)], ['pl'], eng='act')
                elif m < 42:
                    act(geb[:, m - 34, :], psn, AF.Gelu, [('ps', pb)], ['H2'])
                elif m < 50:
                    act(g0b[:, m - 42, :], psn, AF.Sigmoid, [('ps', pb)], ['H3'])
                else:
                    act(g1b[:, m - 50, :], psn, AF.Sigmoid, [('ps', pb)], ['HX0'])

            if first:
                P.op('dve', lambda e: e.memset(pl[:, :, 0:3], 0.0), writes=['pl'])
            else:
                cp(pl[:, :, 0:3], osml[:, :, 4:7], ['osml'], ['pl'])
            proj('w_in', PW, hb, 'hb', n, evac)
            cp(osml[:, :, 4:7], pl[:, :, n:n + 3], ['pl'], ['osml'])
            dump('r32', r32[:], 'F0'); dump('k32', k32[:], 'F1'); dump('v32', v32[:], 'F2'); dump('a32', a32[:], 'F3'); dump('ls32', ls32[:], 'F4')
            if ti == int(os.environ.get('NTI', NTILES)) - 1:
                cp(osml[:, :, 0:3], pl[:, :, n:n + 3], ['pl'], ['osml'])
                cp(osh[:], carry_sh[:], ['carry_sh'], ['osh'])
            return dict(r32=r32, k32=k32, v32=v32, a32=a32, ls32=ls32, glb=glb, geb=geb, g0b=g0b, g1b=g1b, xcb=xcb)

        def lru_prompt(ti, B):
            P.phase = 'lru_prompt'
            n = NT
            geb, g1b, xcb = B['geb'], B['g1b'], B['xcb']
            lmb = HX[:, 16:24, :]
            def lru_chunk(c):
                SS, KP = stsel(c)
                xc = SS[0][:, 0:n]; gr = SS[1][:, 0:n]; gi = SS[2][:, 0:n]; t3 = SS[3][:, 0:n]; hs = SS[4][:, 0:n]
                act(xc, pl[:, c, 3:3 + n], AF.Identity, ['pl', 'cw', 'v_conv_b'], [KP + '0'], bias=vec['conv_b'][:, c:c + 1], scale=cw[:, 3, c:c + 1])
                for j in range(3):
                    stt(xc, pl[:, c, j:j + n], cw[:, j, c:c + 1], xc, ALU.mult, ALU.add, ['pl', 'cw', KP + '0'], [KP + '0'])
                yield
                cp(xcb[:, c, :], xc, [KP + '0'], ['HX1'], eng='act')
                p1 = bank(); p2 = bank()
                yield
                mm(ps[:, p1, 0:n], wrbd[:, c, :], xcb[:, c, :], True, True, ['wrbd', 'HX1'], [('ps', p1)])
                yield
                mm(ps[:, p2, 0:n], wibd[:, c, :], xcb[:, c, :], True, True, ['wibd', 'HX1'], [('ps', p2)])
                yield
                act(gr, ps[:, p1, 0:n], AF.Sigmoid, [('ps', p1), 'v_lru_br'], [KP + '1'], bias=vec['lru_br'][:, c:c + 1])
                yield
                act(gi, ps[:, p2, 0:n], AF.Sigmoid, [('ps', p2), 'v_lru_bi'], [KP + '2'], bias=vec['lru_bi'][:, c:c + 1])
                yield
                act(gr, gr, AF.Exp, [KP + '1', 'lsp'], [KP + '1'], scale=lsp[:, c:c + 1])
                yield
                act(t3, gr, AF.Square, [KP + '1'], [KP + '3'])
                yield
                act(t3, t3, AF.Sqrt, [KP + '3'], [KP + '3'], scale=-1.0, bias=1.0)
                yield
                tt(gi, gi, xc, ALU.mult, [KP + '2', KP + '0'], [KP + '2'])
                yield
                tt(gi, gi, t3, ALU.mult, [KP + '2', KP + '3'], [KP + '2'])
                yield
                if ti == 0:
                    P.op('dve', lambda e, hs=hs, gr=gr, gi=gi: e.tensor_tensor_scan(out=hs, data0=gr, data1=gi, initial=0.0, op0=ALU.mult, op1=ALU.add),
                         reads=[KP + '1', KP + '2'], writes=[KP + '4'])
                else:
                    P.op('dve', lambda e, c=c, hs=hs, gr=gr, gi=gi: e.tensor_tensor_scan(out=hs, data0=gr, data1=gi, initial=carry_h[:, c:c + 1], op0=ALU.mult, op1=ALU.add),
                         reads=[KP + '1', KP + '2', ('carry_h', c)], writes=[KP + '4'])
                yield
                cp(carry_h[:, c:c + 1], hs[:, n - 1:n], [KP + '4'], [('carry_h', c)])
                yield
                tt(t3, hs, geb[:, c, :], ALU.mult, [KP + '4', 'H2'], [KP + '3'])
                yield
                tt(lmb[:, c, :], t3, g1b[:, c, :], ALU.mult, [KP + '3', 'HX0'], ['HX2'])
            for _c0 in range(0, 8, 2):
                interleave2(lru_chunk(_c0), lru_chunk(_c0 + 1))
            if ti == int(os.environ.get('NTI', NTILES)) - 1:
                cp(osml[:, :, 3:4], carry_h[:].unsqueeze(2), [('carry_h', c_) for c_ in range(8)], ['osml'])
            return lmb


        plb = pl[:].rearrange('p a b -> p (a b)').bitcast(BF16)
        plA = plb[:, 0:8 * NT].rearrange('p (a b) -> p a b', a=8)
        plB = plb[:, 8 * NT:16 * NT].rearrange('p (a b) -> p a b', a=8)

        def rwkv_core(B, state_in, state_out, skip_inverse=False, prepared=None):
            P.phase = 'rwkv_core'
            n = NT
            r32, k32, v32, a32, ls32 = B['r32'], B['k32'], B['v32'], B['a32'], B['ls32']
            bc8 = lambda v: v[:].unsqueeze(2).to_broadcast([128, 8, n])
            QR = HX[:, 0:16, :].rearrange('p (k two) (c t) -> p k two c t', two=2, t=C)
            KT = HB_[0]; NB = hb
            vb = plB
            if prepared is not None:
                prepared(QR, KT, NB, vb, WCs)
            else:
                kk32 = pl[:, :, 0:n]
                tt(kk32, k32[:], bc8(vec['k_k']), ALU.mult, ['F1', 'v_k_k'], ['pl'])
                act(KT[:], kk32, AF.Square, ['pl'], ['H0'])
                rkb = HB_[2]

                def prep_chunk(kc):
                    SS, KP = stsel(kc)
                    s_ = SS[0][:, 0:n]; u_ = SS[1][:, 0:n]; u2 = SS[2][:, 0:n]
                    pb = bank()
                    mm(ps[:, pb, 0:n], bdb[:], KT[:, kc, :], True, True, ['bdb', 'H0'], [('ps', pb)])
                    act(s_, ps[:, pb, 0:n], AF.Sqrt, [('ps', pb)], [KP + '0'])
                    act(u_, a32[:, kc, :], AF.Identity, ['F3', 'v_k_a', 'oka'], [KP + '1'], bias=oka[:, kc:kc + 1], scale=vec['k_a'][:, kc:kc + 1])
                    yield
                    ts(s_, s_, 1e-12, None, ALU.max, None, [KP + '0'], [KP + '0'])
                    yield
                    recip(s_, s_, [KP + '0'], [KP + '0'])
                    yield
                    tt(kk32[:, kc, :], kk32[:, kc, :], s_, ALU.mult, ['pl', KP + '0'], ['pl'])
                    yield
                    tt(k32[:, kc, :], k32[:, kc, :], u_, ALU.mult, ['F1', KP + '1'], ['F1'])
                    yield
                    tt(a32[:, kc, :], a32[:, kc, :], kk32[:, kc, :], ALU.mult, ['F3', 'pl'], ['F3'])
                    yield
                    tt(u2, r32[:, kc, :], k32[:, kc, :], ALU.mult, ['F0', 'F1'], [KP + '2'])
                    yield
                    act(rkb[:, kc, :], u2, AF.Copy, [KP + '2', 'v_r_k'], ['H2'], scale=vec['r_k'][:, kc:kc + 1])
                for _c0 in range(0, 8, 2):
                    interleave2(prep_chunk(_c0), prep_chunk(_c0 + 1))
                P.phase = 'rw_decay'
                def decay_chunk(kc):
                    SS, KP = stsel(kc)
                    cs = SS[0][:, 0:n]; dd = SS[1][:, 0:n]; Wi = SS[2][:, 0:n]; We = SS[3][:, 0:n]; Wv = SS[4][:, 0:n]
                    P.op('dve', lambda e, kc=kc, cs=cs: e.tensor_tensor_scan(out=cs, data0=reset[:, 0:n], data1=ls32[:, kc, :], initial=0.0, op0=ALU.mult, op1=ALU.add),
                         reads=['reset', 'F4'], writes=[KP + '0'])
                    yield
                    tt(dd, cs, ls32[:, kc, :], ALU.subtract, [KP + '0', 'F4'], [KP + '1'])
                    yield
                    act(Wi, cs, AF.Exp, [KP + '0'], [KP + '2'], scale=-C0)
                    yield
                    act(We, dd, AF.Exp, [KP + '1'], [KP + '3'], scale=-C0)
                    yield
                    act(Wv, cs, AF.Exp, [KP + '0'], [KP + '4'], scale=C0)
                    c4 = lambda a: a.rearrange('p (c t) -> p c t', t=C)
                    yield
                    tt(QR[:, kc, 1, :, :], c4(r32[:, kc, :]), c4(Wi), ALU.mult, ['F0', KP + '2'], ['HX0', 'HX1'])
                    yield
                    tt(QR[:, kc, 0, :, :], c4(kk32[:, kc, :]), c4(We), ALU.mult, ['pl', KP + '3'], ['HX0', 'HX1'])
                    yield
                    cp(WCs[:, kc, :], c4(Wi)[:, :, C - 1], [KP + '2'], ['WCs'])
                    yield
                    tt(Wi, k32[:, kc, :], Wv, ALU.mult, ['F1', KP + '4', KP + '2'], [KP + '2'])
                    yield
                    stt(We, a32[:, kc, :], -1.0, Wv, ALU.mult, ALU.mult, ['F3', KP + '4', KP + '3'], [KP + '3'])
                    yield
                    cp(NB[:, kc, :], We, [KP + '3'], ['hb'], eng='act')
                    yield
                    cp(KT[:, kc, :], Wi, [KP + '2'], ['H0'], eng='act')
                for _c0 in range(0, 8, 2):
                    interleave2(decay_chunk(_c0), decay_chunk(_c0 + 1))
                P.phase = 'rw_tok'
                vb = plB
                cp(vb[:, :, 0:n], v32[:], ['F2'], ['pl'], eng='act')
                for kc in range(8):
                    pb = bank()
                    mm(ps[:, pb, 0:n], bdb[:], rkb[:, kc, :], True, True, ['bdb', 'H2'], [('ps', pb)])
                    tt(v32[:, kc, :], v32[:, kc, :], ps[:, pb, 0:n], ALU.mult, ['F2', ('ps', pb)], ['F2'])
            tmv = lambda a: a.rearrange('p a b -> p (a b)').rearrange('p (c x) -> p c x', c=NCH)
            vT = tmv(HB_[2][:]); kTt = tmv(plA); nbT = tmv(plB)

            def to_tokmajor(src, skey, dst, dkey):
                for c in range(NCH):
                    pb = bank()
                    pbv = ps[:, pb, :].bitcast(BF16)
                    for hp in range(8):
                        for par in range(2):
                            lo = par * 64
                            tr(pbv[lo:lo + 64, hp * 64:(hp + 1) * 64], src[lo:lo + 64, hp, c * C:(c + 1) * C], identb[lo:lo + 64, lo:lo + 64],
                               [skey, 'identb'], [('ps', pb)])
                    cp(dst[:, c, :], pbv[:, 0:512], [('ps', pb)], [dkey], eng=('act' if c % 2 else 'dve'))
            to_tokmajor(vb, 'pl', vT, 'H2')
            to_tokmajor(KT, 'H0', kTt, 'pl')
            to_tokmajor(NB, 'hb', nbT, 'pl')
            vTv = lambda c, hp, lo: vT[lo:lo + 64, c, hp * 64:(hp + 1) * 64]
            kTv = lambda c, hp, lo: kTt[lo:lo + 64, c, hp * 64:(hp + 1) * 64]
            nTv = lambda c, hp, lo: nbT[lo:lo + 64, c, hp * 64:(hp + 1) * 64]
            m_su_ui = mask[:, 0:128].unsqueeze(1).to_broadcast([128, 8, 128])
            m_sl = mask[:, 128:192].unsqueeze(1).to_broadcast([128, 8, 64])
            m_eye = mask[:, 192:256].unsqueeze(1).to_broadcast([128, 8, 64])
            def ph1(c0):
                P.phase = 'rw_ph1'
                ctx = []
                for s in range(2):
                    c = c0 + s
                    b1a = bank(); b1b = bank()
                    for hp in range(8):
                        for par in range(2):
                            lo = par * 64
                            bsel = b1a if hp < 4 else b1b
                            mm(ps[lo:lo + 64, bsel, (hp % 4) * 128:(hp % 4 + 1) * 128], KT[lo:lo + 64, hp, c * C:(c + 1) * C],
                               QR[lo:lo + 64, hp, :, c, :], True, True, ['H0', 'HX0', 'HX1'], [('ps', bsel)])
                    for (bsel, h0) in [(b1a, 0), (b1b, 4)]:
                        tt(LP[:, h0:h0 + 4, c, :], ps[:, bsel, :].rearrange('p (h x) -> p h x', h=4), m_su_ui[:, 0:4, :], ALU.mult,
                           [('ps', bsel), 'mask'], [('LP', c)])
                    b2a = bank(); b2b = bank()
                    for hp in range(8):
                        for par in range(2):
                            lo = par * 64
                            bsel = b2a if hp < 4 else b2b
                            mm(ps[lo:lo + 64, bsel, (hp % 4) * 128:(hp % 4 + 1) * 128], NB[lo:lo + 64, hp, c * C:(c + 1) * C],
                               QR[lo:lo + 64, hp, :, c, :], True, True, ['hb', 'HX0', 'HX1'], [('ps', bsel)])
                    for (bsel, h0) in [(b2a, 0), (b2b, 4)]:
                        tt(PN[:, h0:h0 + 4, c, :], ps[:, bsel, :].rearrange('p (h x) -> p h x', h=4), m_su_ui[:, 0:4, :], ALU.mult,
                           [('ps', bsel), 'mask'], [('PN', c)])
                    b3 = bank()
                    for hp in range(8):
                        for par in range(2):
                            lo = par * 64
                            mm(ps[lo:lo + 64, b3, hp * 64:(hp + 1) * 64], QR[lo:lo + 64, hp, 0, c, :], NB[lo:lo + 64, hp, c * C:(c + 1) * C],
                               True, True, ['HX0', 'HX1', 'hb'], [('ps', b3)])
                    pp = PP[s][0]
                    cp(pp[:, :, 0:64], PN[:, :, c, 0:64], [('PN', c)], [('PP', s, 0)], eng='act')
                    tt(pp[:, :, 64:128], ps[:, b3, :].rearrange('p (h x) -> p h x', h=8), m_sl, ALU.mult, [('ps', b3), 'mask'], [('PP', s, 0)])
                    tt(X32[s][:], PN[:, :, c, 0:64], m_eye, ALU.add, [('PN', c), 'mask'], [('X32', s)])
                    cp(Xb[s][:], X32[s][:], [('X32', s)], [('Xb', s)], eng='act')
                    ctx.append(c)
                if skip_inverse:
                    for s_ in range(2):
                        cp(XT[:, :, ctx[s_], :], X32[s_][:], [('X32', s_)], [('XT', ctx[s_])], eng='act')
                for lvl in ([] if skip_inverse else range(1, 6)):
                    yield
                    P.phase = 'rw_ph1'
                    cur = (lvl - 1) % 2; nxt = lvl % 2
                    banks = []
                    for s in range(2):
                        ba = bank(); bb = bank()
                        src = PP[s][cur]
                        for hp in range(8):
                            for par in range(2):
                                lo = par * 64
                                bsel = ba if hp < 4 else bb
                                o0 = (hp % 4) * 128
                                mm(ps[lo:lo + 64, bsel, o0:o0 + 64], src[lo:lo + 64, hp, 64:128], src[lo:lo + 64, hp, 0:64], True, True,
                                   [('PP', s, cur)], [('ps', bsel)])
                                mm(ps[lo:lo + 64, bsel, o0 + 64:o0 + 128], src[lo:lo + 64, hp, 0:64], src[lo:lo + 64, hp, 64:128], True, True,
                                   [('PP', s, cur)], [('ps', bsel)])
                        banks.append((ba, bb))
                    for s in range(2):
                        ba, bb = banks[s]
                        dst = PP[s][nxt]
                        cp(dst[:, 0:4, :], ps[:, ba, :].rearrange('p (h x) -> p h x', h=4), [('ps', ba)], [('PP', s, nxt)], eng='act')
                        cp(dst[:, 4:8, :], ps[:, bb, :].rearrange('p (h x) -> p h x', h=4), [('ps', bb)], [('PP', s, nxt)], eng='dve')
                    xb_ = []
                    for s in range(2):
                        bx = bank()
                        src = PP[s][nxt]
                        for hp in range(8):
                            for par in range(2):
                                lo = par * 64
                                mm(ps[lo:lo + 64, bx, hp * 64:(hp + 1) * 64], src[lo:lo + 64, hp, 64:128], Xb[s][lo:lo + 64, hp, :], True, True,
                                   [('PP', s, nxt), ('Xb', s)], [('ps', bx)])
                        xb_.append(bx)
                    for s in range(2):
                        bx = xb_[s]
                        tt(X32[s][:], X32[s][:], ps[:, bx, :].rearrange('p (h x) -> p h x', h=8), ALU.add, [('X32', s), ('ps', bx)], [('X32', s)])
                        if lvl < 5:
                            cp(Xb[s][:], X32[s][:], [('X32', s)], [('Xb', s)], eng='act')
                        else:
                            cp(XT[:, :, ctx[s], :], X32[s][:], [('X32', s)], [('XT', ctx[s])], eng='act')
            Y32 = FB_[0]

            def ph2(c):
                P.phase = 'rw_ph2'
                state_in(c)
                cur = c % 2
                a0 = A0b[cur]
                bR = bank()
                for hp in range(8):
                    for par in range(2):
                        lo = par * 64
                        o = ps[lo:lo + 64, bR, hp * 64:(hp + 1) * 64]
                        mm(o, QR[lo:lo + 64, hp, 0, c, :], a0[lo:lo + 64, hp, :], True, False, ['HX0', 'HX1', ('A0b', cur)], [('ps', bR)])
                        mm(o, LP[lo:lo + 64, hp, c, 0:64], vTv(c, hp, lo), False, True, [('LP', c), 'H2'], [('ps', bR)])
                cp(RHSb[:], ps[:, bR, :].rearrange('p (h x) -> p h x', h=8), [('ps', bR)], ['RHSb'], eng='act')
                yield
                P.phase = 'rw_ph2'
                bU = bank()
                for hp in range(8):
                    for par in range(2):
                        lo = par * 64
                        mm(ps[lo:lo + 64, bU, hp * 64:(hp + 1) * 64], XT[lo:lo + 64, hp, c, :], RHSb[lo:lo + 64, hp, :], True, True,
                           [('XT', c), 'RHSb'], [('ps', bU)])
                cp(Ub[:], ps[:, bU, :].rearrange('p (h x) -> p h x', h=8), [('ps', bU)], ['Ub'], eng='act')
                yield
                P.phase = 'rw_ph2'
                bD = bank()
                for hp in range(8):
                    for par in range(2):
                        lo = par * 64
                        o = ps[lo:lo + 64, bD, hp * 64:(hp + 1) * 64]
                        mm(o, kTv(c, hp, lo), vTv(c, hp, lo), True, False, ['pl', 'H2'], [('ps', bD)])
                        mm(o, nTv(c, hp, lo), Ub[lo:lo + 64, hp, :], False, True, ['pl', 'Ub'], [('ps', bD)])
                bY = bank()
                for hp in range(8):
                    for par in range(2):
                        lo = par * 64
                        o = ps[lo:lo + 64, bY, hp * 64:(hp + 1) * 64]
                        mm(o, a0[lo:lo + 64, hp, :], QR[lo:lo + 64, hp, 1, c, :], True, False, [('A0b', cur), 'HX0', 'HX1'], [('ps', bY)])
                        mm(o, vTv(c, hp, lo), LP[lo:lo + 64, hp, c, 64:128], False, False, ['H2', ('LP', c)], [('ps', bY)])
                        mm(o, Ub[lo:lo + 64, hp, :], PN[lo:lo + 64, hp, c, 64:128], False, True, ['Ub', ('PN', c)], [('ps', bY)])
                tt(A32[:], A32[:], ps[:, bD, :].rearrange('p (h x) -> p h x', h=8), ALU.add, ['A32', ('ps', bD)], ['A32'])
                tt(A32[:], A32[:], WCs[:, :, c:c + 1].to_broadcast([128, 8, 64]), ALU.mult, ['A32', 'WCs'], ['A32'])
                cp(A0b[1 - cur][:], A32[:], ['A32'], [('A0b', 1 - cur)], eng='act')
                cp(Y32[:, :, c * C:(c + 1) * C], ps[:, bY, :].rearrange('p (h x) -> p h x', h=8), [('ps', bY)], ['F0'], eng='dve')
                state_out(c)
                yield

            def drain(g):
                for _ in g:
                    pass

            def interleave(ga, gb):
                a_ok = b_ok = True
                while a_ok or b_ok:
                    if a_ok:
                        try:
                            next(ga)
                        except StopIteration:
                            a_ok = False
                    if b_ok:
                        try:
                            next(gb)
                        except StopIteration:
                            b_ok = False

            def chain(*gs):
                for g in gs:
                    yield from g
            drain(ph1(0))
            if NCH == 4:
                interleave(ph1(2), chain(ph2(0), ph2(1)))
                drain(chain(ph2(2), ph2(3)))
            else:
                for c0 in range(2, NCH, 2):
                    drain(ph1(c0))
                for c in range(NCH):
                    drain(ph2(c))
            return Y32, v32

        def rwkv_post(Y, Ykey, bonus, bkey, glb, g0b, lmb, mb, n):
            P.phase = 'rwkv_post'
            Yb = HB_[0]; ysq = hb
            cp(Yb[:, :, 0:n], Y[:, :, 0:n], [Ykey], ['H0'], eng='act')
            act(ysq[:, :, 0:n], Y[:, :, 0:n], AF.Square, [Ykey], ['hb'])
            def post_chunk(kc):
                b1 = bank(); b2 = bank()
                mm(ps[:, b1, 0:n], bdb[:], Yb[:, kc, 0:n], True, True, ['bdb', 'H0'], [('ps', b1)])
                yield
                mm(ps[:, b2, 0:n], bdb[:], ysq[:, kc, 0:n], True, True, ['bdb', 'hb'], [('ps', b2)])
                SS, KP = stsel(kc)
                mean = SS[0][:, 0:n]; var = SS[1][:, 0:n]; t_ = SS[2][:, 0:n]
                yield
                act(mean, ps[:, b1, 0:n], AF.Copy, [('ps', b1)], [KP + '0'], scale=1.0 / 64)
                yield
                act(var, mean, AF.Square, [KP + '0'], [KP + '1'])
                yield
                stt(var, ps[:, b2, 0:n], 1.0 / 64, var, ALU.mult, ALU.subtract, [('ps', b2), KP + '1'], [KP + '1'])
                yield
                ts(var, var, 0.0, GN_EPS, ALU.max, ALU.add, [KP + '1'], [KP + '1'])
                yield
                act(var, var, AF.Sqrt, [KP + '1'], [KP + '1'])
                yield
                recip(var, var, [KP + '1'], [KP + '1'])
                yield
                tt(t_, Y[:, kc, 0:n], mean, ALU.subtract, [Ykey, KP + '0'], [KP + '2'])
                yield
                tt(t_, t_, var, ALU.mult, [KP + '2', KP + '1'], [KP + '2'])
                yield
                act(t_, t_, AF.Identity, [KP + '2', 'v_gn_g', 'v_gn_b'], [KP + '2'], bias=vec['gn_b'][:, kc:kc + 1], scale=vec['gn_g'][:, kc:kc + 1])
                yield
                tt(t_, t_, bonus[:, kc, 0:n], ALU.add, [KP + '2', bkey], [KP + '2'])
                yield
                tt(t_, t_, glb[:, kc, 0:n], ALU.mult, [KP + '2', 'H1'], [KP + '2'])
                yield
                tt(t_, t_, g0b[:, kc, 0:n], ALU.mult, [KP + '2', 'H3'], [KP + '2'])
                yield
                tt(mb[:, kc, 0:n], t_, lmb[:, kc, 0:n], ALU.add, [KP + '2', 'HX2'], ['H2'])
            for _c0 in range(0, 8, 2):
                interleave2(post_chunk(_c0), post_chunk(_c0 + 1))

        def resid_ln(w, xin, xkey, ln_idx, n):
            def evac(m, pb):
                stt(h32[:, m, 0:n], ps[:, pb, 0:n], 1.0 / ALPHA, h32[:, m, 0:n], ALU.mult, ALU.add, [('ps', pb), 'h32'], ['h32'])
            proj(w, D, xin, xkey, n, evac)
            layernorm(ln_idx, n, LN_EPS / (ALPHA * ALPHA))

        def xattn_prompt(n):
            P.phase = 'xattn_prompt'
            qb = HB_[0]; ob = HB_[1]; pT = HX[:, 0:8, :]
            def evq(m, pb):
                cp(qb[:, m, 0:n], ps[:, pb, 0:n], [('ps', pb)], ['H0'], eng='act')
            proj('xa_wq', D, hb, 'hb', n, evq)
            for h in range(4):
                for mc in range(2):
                    pb = bank()
                    for dc in range(2):
                        mm(ps[:, pb, 0:n], mkT[:, 2 * h + dc, mc * 128:(mc + 1) * 128], qb[:, 2 * h + dc, 0:n], dc == 0, dc == 1, ['mkT', 'H0'], [('ps', pb)])
                    act(pT[:, 2 * h + mc, 0:n], ps[:, pb, 0:n], AF.Exp, [('ps', pb)], ['HX0'], scale=1.0 / 16.0)
            rds = []
            for h in range(4):
                pd = bank()
                for mc in range(2):
                    mm(ps[:, pd, 0:n], onesb[:], pT[:, 2 * h + mc, 0:n], mc == 0, mc == 1, ['onesb', 'HX0'], [('ps', pd)])
                SS, KP = stsel(h)
                rd = SS[h // 2][:, 0:n]; rk_ = KP + str(h // 2)
                recip(rd, ps[:, pd, 0:n], [('ps', pd)], [rk_])
                rds.append((rd, rk_))
            for h in range(4):
                rd, rk_ = rds[h]
                for dc in range(2):
                    po = bank()
                    for mc in range(2):
                        mm(ps[:, po, 0:n], mvb[:, mc, (2 * h + dc) * 128:(2 * h + dc + 1) * 128], pT[:, 2 * h + mc, 0:n], mc == 0, mc == 1, ['mvb', 'HX0'], [('ps', po)])
                    tt(ob[:, 2 * h + dc, 0:n], ps[:, po, 0:n], rd, ALU.mult, [('ps', po), rk_], ['H1'])
            resid_ln('xa_wo', ob, 'H1', 2, n)

        if stage >= 1:
            mem_kv()
        if 'm' in PARTS or PARTS == 'abcdefg':
            P.op('dve', lambda e: e.memset(A32[:], 0.0), writes=['A32'])
            P.op('dve', lambda e: e.memset(A0b[0][:], 0.0), writes=[('A0b', 0)])
        for ti in range(int(os.environ.get('NTI', NTILES)) if stage >= 9 else 1):
            TOG = os.environ.get('TOG', '')
            for r in range(1 if '1' in TOG else NT // 128):
                load_tokens_fm(di['xp'][ti * NT + r * 128: ti * NT + (r + 1) * 128, :], 128, h32, None if 'D' in TOG else hb, r * 128, 'h32')
            if stage >= 2:
                ffn('ffn1_wi', 'ffn1_wo', 0, NT)
            if ti == 0:
                dump('h1', h32[:], 'h32')
            if stage < 3:
                break
            B = mixer_prompt(ti)
            if stage < 4:
                break
            lmb = lru_prompt(ti, B)
            if ti == 0:
                dump('lm', lmb, 'HX2')
            if stage < 5:
                break
            Y32, bonus = rwkv_core(B, lambda c: None, lambda c: None)
            if ti == 0:
                dump('Y', Y32[:], 'F0')
            if stage < 6:
                break
            mb = HB_[2]
            rwkv_post(Y32, 'F0', bonus, 'F2', B['glb'], B['g0b'], lmb, mb, NT)
            resid_ln('w_mix_out', mb, 'H2', 1, NT)
            if ti == 0:
                dump('h2', h32[:], 'h32')
            if stage < 7:
                break
            xattn_prompt(NT)
            if ti == 0:
                dump('h3', h32[:], 'h32')
            if stage < 8:
                break
            ffn('ffn2_wi', 'ffn2_wo', 3, NT)
            for r in range(NT // 128):
                store_fm_tokens(h32, 'h32', r * 128, 128, di['yp'][ti * NT + r * 128: ti * NT + (r + 1) * 128, :])
        if stage < 9:
            P.emit()
            return nc, P
        def prompt_outputs():
            pass
            Sout = FB_[1].rearrange('p a b -> p (a b)')[:, 0:1024].rearrange('p (hp par k) -> p hp par k', hp=8, par=2)
            for g in range(2):
                pb = bank()
                for q in range(4):
                    hp = g * 4 + q
                    tr(ps[0:64, pb, q * 128:(q + 1) * 128], A32[:, hp, :], ident[:, :], ['A32', 'ident'], [('ps', pb)])
                cp(Sout[0:64, g * 4:(g + 1) * 4, :, :], ps[0:64, pb, :].rearrange('p (q par k) -> p q par k', q=4, par=2), [('ps', pb)], ['F1'], eng='act')
            dma('sp', di['prw'].rearrange('(hp par v) k -> v hp par k', hp=8, par=2), Sout[0:64, :, :, :], ['F1'], [], dsem_misc, is_out=True)
            osm2 = FB_[2]
            cp(osm2[:, 0:8, 0:3], osml[:, :, 0:3], ['osml'], ['F2'])
            cp(osm2[:, 0:8, 3:4], osml[:, :, 3:4], ['osml'], ['F2'])
            store_fm_tokens(osm2, 'F2', 0, 3, di['pcv'][:, :])
            store_fm_tokens(osm2, 'F2', 3, 1, di['plru'].rearrange('(o d) -> o d', o=1))
            osh3 = FB_[3]
            cp(osh3[:, 0:8, 0:1], osh[:, 0:8].unsqueeze(2), ['osh'], ['F3'])
            cp(osh3[:, 0:8, 1:2], osh[:, 8:16].unsqueeze(2), ['osh'], ['F3'])
            cp(osh3[:, 0:8, 2:3], osh[:, 16:24].unsqueeze(2), ['osh'], ['F3'])
            cp(osh3[:, 0:2, 3:4], osh[:, 24:26].unsqueeze(2), ['osh'], ['F3'])
            pshv = di['psh'].rearrange('(o d) -> o d', o=1)
            for q in range(3):
                store_fm_tokens(osh3, 'F3', q, 1, pshv[:, q * 1024:(q + 1) * 1024])
            store_fm_tokens(osh3, 'F3', 3, 1, pshv[:, 3072:3328], nch=2)


        if os.environ.get('NTI') != '0':
            prompt_outputs()
        def sample_path():
            P.phase = 'sample_path'
            n = NS
            sm = lambda nm, shp, dt=F32: P.sbuf(nm, shp, dt)
            rc = sm('s_rc', [128, 8, n]); kc_ = sm('s_kc', [128, 8, n]); vc = sm('s_vc', [128, 8, n]); ac = sm('s_ac', [128, 8, n]); lsc = sm('s_lsc', [128, 8, n])
            mk32 = mkT[:].rearrange('p a b -> p (a b)').bitcast(F32)
            mv32 = mvb[:].rearrange('p a b -> p (a b)').bitcast(F32)
            prevS = mk32[:, 0:26 * n].rearrange('p (c b) -> p c b', c=26)
            praw = mk32[:, 26 * n:52 * n].rearrange('p (c b) -> p c b', c=26)
            scvT = mv32[:, 0:24 * n].rearrange('p (c b) -> p c b', c=8)
            h0S = mv32[:, 24 * n:32 * n].rearrange('p (c b) -> p c b', c=8)
            yc = mv32[:, 32 * n:40 * n].rearrange('p (c b) -> p c b', c=8)
            bonc = mv32[:, 40 * n:48 * n].rearrange('p (c b) -> p c b', c=8)
            plc = mv32[:, 48 * n:56 * n].rearrange('p (c b) -> p c b', c=8)
            hsc = mv32[:, 56 * n:64 * n].rearrange('p (c b) -> p c b', c=8)
            P.op('dve', lambda e: e.engine_nop(), writes=['mkT', 'mvb', 's_prev', 's_praw', 's_scvT', 's_h0', 's_yc', 's_bonc', 's_plc', 's_hsc'])
            BD = tok32[1][:].rearrange('p (h x) -> p h x', h=8); ones32 = st1[4][:, 0:128]
            glb = HB_[1]; geb = HB_[2]; g0b = HB_[3]; g1b = HX[:, 0:8, :]; lmb = HX[:, 16:24, :]
            dsS = [P.dma_sem() for _ in range(7)]

            def load_fm(src_rows_ap, nrows, nchunks, dst, dkey, tki, sem):
                tk = tok32[tki]
                dma('sp', tk[0:nrows, 0:nchunks * 128], src_rows_ap, [], [('tok32', tki)], sem)
                for g0 in range(0, nchunks, 4):
                    gn = min(4, nchunks - g0)
                    b = bank()
                    for q in range(gn):
                        tr(ps[:, b, q * 128:q * 128 + nrows], tk[0:nrows, (g0 + q) * 128:(g0 + q + 1) * 128], ident[0:nrows, 0:nrows],
                           [('tok32', tki), 'ident'], [('ps', b)])
                    cp(dst[:, g0:g0 + gn, 0:nrows], ps[:, b, :].rearrange('p (q t) -> p q t', q=4)[:, 0:gn, 0:nrows], [('ps', b)], [dkey], eng='act')

            load_tokens_fm(di['xs'], n, h32, hb, 0, 'h32')
            for q in range(4):
                c0 = q * 8; cn = min(8, 26 - c0)
                tmpd = FB_[0] if q % 2 == 0 else FB_[1]
                load_fm(di['ssh'][:, c0 * 128:(c0 + cn) * 128], n, cn, tmpd, 'F%d' % (q % 2), q % 2, dsS[q % 2])
                cp(prevS[:, c0:c0 + cn, :], tmpd[:, 0:cn, 0:n], ['F%d' % (q % 2)], ['s_prev'])
            load_fm(di['slru'], n, 8, h0S, 's_h0', 0, dsS[0])
            load_fm(di['scv'].rearrange('b j d -> (b j) d'), 3 * n, 8, scvT, 's_scvT', 1, dsS[1])
            dma('sp', di['scv_o'][:, 0:2, :], di['scv'][:, 1:3, :], [], [], P.dma_sem(), is_out=True)

            ffn('ffn1_wi', 'ffn1_wo', 0, n)

            tmp = st1[0]

            def evac(m, pb):
                psn = ps[:, pb, 0:n]
                if m < 26:
                    cp(praw[:, m, :], psn, [('ps', pb)], ['s_praw'], eng='act')
                    ts(tmp[:, 0:n], prevS[:, m, :], mu[:, m:m + 1], None, ALU.mult, None, ['s_prev', 'mu'], ['st0'])
                    if m < 24:
                        dst = [rc, kc_, vc][m // 8][:, m % 8, :]
                        dkey = ['s_rc', 's_kc', 's_vc'][m // 8]
                        stt(dst, psn, omu[:, m:m + 1], tmp[:, 0:n], ALU.mult, ALU.add, [('ps', pb), 'omu', 'st0'], [dkey])
                    else:
                        xs_ = st1[1][:, 0:n]
                        stt(xs_, psn, omu[:, m:m + 1], tmp[:, 0:n], ALU.mult, ALU.add, [('ps', pb), 'omu', 'st0'], ['st1'])
                        lb = st1[2][:, 0:NT].bitcast(BF16)[:, 0:n]
                        if m == 24:
                            act(lb[0:64, :], xs_[0:64, :], AF.Tanh, ['st1'], ['st2'])
                            cp(lb[64:128, :], xs_[64:128, :], ['st1'], ['st2'])
                            for (lo, dstt, dk, bvec) in [(0, lsc, 's_lsc', 'decay_w0'), (64, ac, 's_ac', 'aaa_a0')]:
                                for q in range(8):
                                    p2 = bank()
                                    mm(ps[:, p2, 0:n], w2a2[lo:lo + 64, q * 128:(q + 1) * 128], lb[lo:lo + 64, :], True, True, ['w2a2', 'st2'], [('ps', p2)])
                                    act(dstt[:, q, :], ps[:, p2, 0:n], AF.Sigmoid, [('ps', p2), 'v_' + bvec], [dk], bias=vec[bvec][:, q:q + 1])
                        else:
                            act(lb, xs_, AF.Sigmoid, ['st1'], ['st2'])
                            for q in range(8):
                                p2 = bank()
                                mm(ps[:, p2, 0:n], g2[:, q * 128:(q + 1) * 128], lb, True, True, ['g2', 'st2'], [('ps', p2)])
                                cp(glb[:, q, 0:n], ps[:, p2, 0:n], [('ps', p2)], ['H1'], eng='act')
                elif m < 34:
                    cp(plc[:, m - 26, :], psn, [('ps', pb)], ['s_plc'], eng='act')
                elif m < 42:
                    act(geb[:, m - 34, 0:n], psn, AF.Gelu, [('ps', pb)], ['H2'])
                elif m < 50:
                    act(g0b[:, m - 42, 0:n], psn, AF.Sigmoid, [('ps', pb)], ['H3'])
                else:
                    act(g1b[:, m - 50, 0:n], psn, AF.Sigmoid, [('ps', pb)], ['HX0'])
            proj('w_in', PW, hb, 'hb', n, evac)
            for q in range(4):
                c0 = q * 8; cn = min(8, 26 - c0)
                store_fm_tokens(praw[:, c0:c0 + cn, :], 's_praw', 0, n, di['ssh_o'][:, c0 * 128:(c0 + cn) * 128], nch=cn)
            store_fm_tokens(plc, 's_plc', 0, n, di['scv_o'][:, 2, :])

            sc3 = scvT[:].rearrange('p c (b j) -> p c b j', j=3)
            xc = st1[0][:, 0:n]; gr = st1[1][:, 0:n]; gi = st1[2][:, 0:n]; t3 = st1[3][:, 0:n]; hs = st1[4][:, 0:n]
            xcb = HX[:, 8:16, :]
            for c in range(8):
                act(xc, plc[:, c, :], AF.Identity, ['s_plc', 'cw', 'v_conv_b'], ['st0'], bias=vec['conv_b'][:, c:c + 1], scale=cw[:, 3, c:c + 1])
                for j in range(3):
                    stt(xc, sc3[:, c, :, j], cw[:, j, c:c + 1], xc, ALU.mult, ALU.add, ['s_scvT', 'cw', 'st0'], ['st0'])
                cp(xcb[:, c, 0:n], xc, ['st0'], ['HX1'], eng='act')
                p1 = bank(); p2 = bank()
                mm(ps[:, p1, 0:n], wrbd[:, c, :], xcb[:, c, 0:n], True, True, ['wrbd', 'HX1'], [('ps', p1)])
                mm(ps[:, p2, 0:n], wibd[:, c, :], xcb[:, c, 0:n], True, True, ['wibd', 'HX1'], [('ps', p2)])
                act(gr, ps[:, p1, 0:n], AF.Sigmoid, [('ps', p1), 'v_lru_br'], ['st1'], bias=vec['lru_br'][:, c:c + 1])
                act(gi, ps[:, p2, 0:n], AF.Sigmoid, [('ps', p2), 'v_lru_bi'], ['st2'], bias=vec['lru_bi'][:, c:c + 1])
                act(gr, gr, AF.Exp, ['st1', 'lsp'], ['st1'], scale=lsp[:, c:c + 1])
                act(t3, gr, AF.Square, ['st1'], ['st3'])
                ts(t3, t3, -1.0, 1.0, ALU.mult, ALU.add, ['st3'], ['st3'])
                ts(t3, t3, 0.0, None, ALU.max, None, ['st3'], ['st3'])
                act(t3, t3, AF.Sqrt, ['st3'], ['st3'])
                tt(gi, gi, xc, ALU.mult, ['st2', 'st0'], ['st2'])
                tt(gi, gi, t3, ALU.mult, ['st2', 'st3'], ['st2'])
                tt(hs, gr, h0S[:, c, :], ALU.mult, ['st1', 's_h0'], ['st4'])
                tt(hsc[:, c, :], hs, gi, ALU.add, ['st4', 'st2'], ['s_hsc'])
                tt(t3, hsc[:, c, :], geb[:, c, 0:n], ALU.mult, ['s_hsc', 'H2'], ['st3'])
                tt(lmb[:, c, 0:n], t3, g1b[:, c, 0:n], ALU.mult, ['st3', 'HX0'], ['HX2'])
            store_fm_tokens(hsc, 's_hsc', 0, n, di['slru_o'])

            F1flat = FB_[1].rearrange('p a b -> p (a b)')
            Souts = [F1flat[:, 0:1024].rearrange('p (hp par k) -> p hp par k', hp=8, par=2),
                     F1flat[:, 1024:2048].rearrange('p (hp par k) -> p hp par k', hp=8, par=2)]
            Skeys = ['F1', ('F1', 'b')]; Ssems = [dsS[4], P.dma_sem()]
            BDs = [tok32[0][:].rearrange('p (h x) -> p h x', h=8), tok32[1][:].rearrange('p (h x) -> p h x', h=8)]
            Bsems = [P.dma_sem(), dsS[3]]
            for i_ in range(2):
                P.op('dve', lambda e, i_=i_: e.memset(tok32[i_][:], 0.0), writes=[('tok32', i_)])

            def prefetch_state(b_):
                if b_ >= NS:
                    return
                src = di['srw'][b_].rearrange('(hp par v) k -> par v hp k', hp=8, par=2)
                for par in range(2):
                    dma('sp', BDs[b_ % 2][par * 64:(par + 1) * 64, :, par * 64:(par + 1) * 64], src[par], [], [('tok32', b_ % 2)], Bsems[b_ % 2])
            prefetch_state(0)
            bc16 = lambda v: v[:].unsqueeze(2).to_broadcast([128, 8, n])
            v816 = lambda t: t[:, 0:8 * n].rearrange('p (k b) -> p k b', k=8)
            kkn = plc; Wvc = hsc
            sqb = HB_[0][:, :, 0:n]
            tt(kkn[:], kc_[:], bc16(vec['k_k']), ALU.mult, ['s_kc', 'v_k_k', 's_plc'], ['s_plc'])
            act(sqb, kkn[:], AF.Square, ['s_plc'], ['H0'])
            pb = bank()
            for kc in range(8):
                mm(ps[:, pb, kc * n:(kc + 1) * n], bdb[:], sqb[:, kc, :], True, True, ['bdb', 'H0'], [('ps', pb)])
            s16 = v816(st1[0])
            act(s16, ps[:, pb, 0:8 * n].rearrange('p (k b) -> p k b', k=8), AF.Sqrt, [('ps', pb)], ['st0'])
            ts(s16, s16, 1e-12, None, ALU.max, None, ['st0'], ['st0'])
            recip(s16, s16, ['st0'], ['st0'])
            tt(kkn[:], kkn[:], s16, ALU.mult, ['s_plc', 'st0'], ['s_plc'])
            u16 = v816(st1[1])
            tt(u16, ac[:], bc16(vec['k_a']), ALU.mult, ['s_ac', 'v_k_a'], ['st1'])
            tt(u16, u16, bc16(oka), ALU.add, ['st1', 'oka'], ['st1'])
            tt(kc_[:], kc_[:], u16, ALU.mult, ['s_kc', 'st1'], ['s_kc'])
            tt(ac[:], ac[:], kkn[:], ALU.mult, ['s_ac', 's_plc'], ['s_ac'])
            r16 = v816(st1[2])
            tt(r16, rc[:], kc_[:], ALU.mult, ['s_rc', 's_kc'], ['st2'])
            tt(r16, r16, bc16(vec['r_k']), ALU.mult, ['st2', 'v_r_k'], ['st2'])
            cp(sqb, r16, ['st2'], ['H0'], eng='act')
            pb = bank()
            for kc in range(8):
                mm(ps[:, pb, kc * n:(kc + 1) * n], bdb[:], sqb[:, kc, :], True, True, ['bdb', 'H0'], [('ps', pb)])
            tt(bonc[:], vc[:], ps[:, pb, 0:8 * n].rearrange('p (k b) -> p k b', k=8), ALU.mult, ['s_vc', ('ps', pb)], ['s_bonc'])
            act(Wvc[:], lsc[:], AF.Exp, ['s_lsc', 's_hsc'], ['s_hsc'], scale=C0)
            act(lsc[:], lsc[:], AF.Exp, ['s_lsc'], ['s_lsc'], scale=-C0)
            tt(rc[:], rc[:], lsc[:], ALU.mult, ['s_rc', 's_lsc'], ['s_rc'])
            tt(kc_[:], kc_[:], Wvc[:], ALU.mult, ['s_kc', 's_hsc'], ['s_kc'])
            stt(ac[:], ac[:], -1.0, Wvc[:], ALU.mult, ALU.mult, ['s_ac', 's_hsc'], ['s_ac'])

            for g in range(NS // NCH):
                Bp = dict(r32=FB_[0], k32=FB_[1], v32=FB_[2], a32=FB_[3], ls32=FB_[4])
                gs = slice(g * NCH, (g + 1) * NCH)

                def prepared(QR, KT, NB, vb, WCs_, gs=gs):
                    c0v = lambda t: t.rearrange('p k (c t) -> p k c t', t=C)[:, :, :, 0]
                    P.op('dve', lambda e: e.memset(HX[:, 0:16, :], 0.0), writes=['HX0', 'HX1'])
                    P.op('dve', lambda e: e.memset(KT[:], 0.0), writes=['H0'])
                    P.op('dve', lambda e: e.memset(NB[:], 0.0), writes=['hb'])
                    P.op('dve', lambda e: e.memset(vb[:, :, 0:NT], 0.0), writes=['pl'])
                    cp(QR[:, :, 0, :, 0], kkn[:, :, gs], ['s_plc'], ['HX0', 'HX1'])
                    cp(QR[:, :, 1, :, 0], rc[:, :, gs], ['s_rc'], ['HX0', 'HX1'])
                    cp(c0v(KT[:]), kc_[:, :, gs], ['s_kc'], ['H0'])
                    cp(c0v(NB[:]), ac[:, :, gs], ['s_ac'], ['hb'])
                    cp(c0v(vb[:, :, 0:NT]), vc[:, :, gs], ['s_vc'], ['pl'])
                    cp(WCs_[:, :, :], lsc[:, :, gs], ['s_lsc'], ['WCs'])

                def state_in(c, g=g):
                    b_ = g * NCH + c
                    prefetch_state(b_ + 1)
                    BD = BDs[b_ % 2]
                    pb = bank()
                    for hp in range(8):
                        mm(ps[:, pb, hp * 64:(hp + 1) * 64], BD[:, hp, :], mask[:, 192:256], True, True, [('tok32', b_ % 2), 'mask'], [('ps', pb)])
                    v3 = ps[:, pb, :].rearrange('p (h x) -> p h x', h=8)
                    cp(A32[:], v3, [('ps', pb)], ['A32'], eng='dve')
                    cp(A0b[c % 2][:], v3, [('ps', pb)], [('A0b', c % 2)], eng='act')

                def state_out(c, g=g):
                    b_ = g * NCH + c
                    Sout = Souts[b_ % 2]; skey = Skeys[b_ % 2]
                    for gg in range(2):
                        pb = bank()
                        for q in range(4):
                            hp = gg * 4 + q
                            tr(ps[0:64, pb, q * 128:(q + 1) * 128], A32[:, hp, :], ident[:, :], ['A32', 'ident'], [('ps', pb)])
                        cp(Sout[0:64, gg * 4:(gg + 1) * 4, :, :], ps[0:64, pb, :].rearrange('p (q par k) -> p q par k', q=4, par=2), [('ps', pb)], [skey], eng='act')
                    dma('sp', di['srw_o'][b_].rearrange('(hp par v) k -> v hp par k', hp=8, par=2), Sout[0:64, :, :, :], [skey], [], Ssems[b_ % 2], is_out=True)
                Y32, bon = rwkv_core(Bp, state_in, state_out, skip_inverse=True, prepared=prepared)
                cp(yc[:, :, g * NCH:(g + 1) * NCH], Y32[:].rearrange('p k (c t) -> p k c t', t=C)[:, :, :, 0], ['F0'], ['s_yc'])
            mb = HB_[2]
            rwkv_post(yc, 's_yc', bonc, 's_bonc', glb, g0b, lmb, mb, n)
            resid_ln('w_mix_out', mb, 'H2', 1, n)

            qc = FB_[0]; qT = FB_[1].rearrange('p a b -> p (a b)')[:, 0:1024]; sel = FB_[2].rearrange('p a b -> p (a b)')[:, 0:NS * 128].rearrange('p (b m) -> p b m', b=NS)
            Kbs = [FB_[3].rearrange('p a b -> p (a b)').rearrange('p (mc f) -> p mc f', mc=2),
                   pl[:].rearrange('p a b -> p (a b)')[:, 0:2048].rearrange('p (mc f) -> p mc f', mc=2)]
            Kkeys = ['F3', 'pl']; Ksems = [dsS[5], P.dma_sem()]
            prod = FB_[4].rearrange('p a b -> p (a b)')[:, 0:1024]
            Vbs = [HB_[0][:].rearrange('p a b -> p (a b)').rearrange('p (mc f) -> p mc f', mc=2),
                   HB_[2][:].rearrange('p a b -> p (a b)').rearrange('p (mc f) -> p mc f', mc=2)]
            Vkeys = ['H0', 'H2']; Vsems = [dsS[6], P.dma_sem()]
            ob = HB_[1]
            sc = st1[0][:, 0:128]; ex = st1[1][:, 0:128]; den = st1[2][:, 0:64]; pbf = st1[3][:, 0:NT].bitcast(BF16)[:, 0:128]

            def evq(m, pb):
                cp(qc[:, m, 0:n], ps[:, pb, 0:n], [('ps', pb)], ['F0'], eng='act')
            proj('xa_wq', D, hb, 'hb', n, evq)
            for g0 in range(0, 8, 4):
                pb = bank()
                for q in range(4):
                    tr(ps[0:n, pb, q * 128:(q + 1) * 128], qc[:, g0 + q, 0:n], ident[:, :], ['F0', 'ident'], [('ps', pb)])
                cp(qT[0:n, g0 * 128:(g0 + 4) * 128], ps[0:n, pb, :], [('ps', pb)], ['F1'], eng='act')
            cp(sel[0:n, :, :], ident[0:n, 0:n].unsqueeze(2).to_broadcast([n, n, 128]), ['ident'], ['F2'])
            for b_ in range(NS):
                Kb = Kbs[b_ % 2]; kkey = Kkeys[b_ % 2]
                dma('sp', Kb, di['cmk'][b_].rearrange('(mc p) f -> p mc f', p=128), [], [kkey], Ksems[b_ % 2])
                pq = [bank(), bank()]
                for hf in range(2):
                    mm(ps[:, pq[hf], :], sel[0:n, b_, :], qT[0:n, hf * 512:(hf + 1) * 512], True, True, ['F2', 'F1'], [('ps', pq[hf])])
                for mc in range(2):
                    for hf in range(2):
                        tt(prod[:, hf * 512:(hf + 1) * 512], Kb[:, mc, hf * 512:(hf + 1) * 512], ps[:, pq[hf], :], ALU.mult, [kkey, ('ps', pq[hf])], ['F4'])
                    P.op('dve', lambda e, b_=b_, mc=mc: e.tensor_reduce(out=sc[:, (b_ * 2 + mc) * 4:(b_ * 2 + mc) * 4 + 4], in_=prod.rearrange('p (h d) -> p h d', h=4), axis=AX.X, op=ALU.add),
                         reads=['F4'], writes=['st0'])
            act(ex, sc, AF.Exp, ['st0'], ['st1'], scale=1.0 / 16.0)
            dma('sp', ones32, di['c_all'][:, 128:256], [], ['st4'], dsS[2])
            pdn = bank()
            mm(ps[:, pdn, 0:128], ones32, ex, True, True, ['st4', 'st1'], [('ps', pdn)])
            d4 = ps[:, pdn, 0:128].rearrange('p (b mc h) -> p b mc h', mc=2, h=4)
            den3 = den.rearrange('p (b h) -> p b h', h=4)
            cp(den3, d4[:, :, 0, :], [('ps', pdn)], ['st2'])
            tt(den3, den3, d4[:, :, 1, :], ALU.add, ['st2', ('ps', pdn)], ['st2'])
            recip(den, den, ['st2'], ['st2'])
            tt(pbf.rearrange('p (b mc h) -> p b mc h', mc=2, h=4), ex.rearrange('p (b mc h) -> p b mc h', mc=2, h=4),
               den3.unsqueeze(2).to_broadcast([128, NS, 2, 4]), ALU.mult, ['st1', 'st2'], ['st3'])
            po = bank()
            for b_ in range(NS):
                Vb = Vbs[b_ % 2]; vkey = Vkeys[b_ % 2]
                dma('pool', Vb, di['cmv'][b_].rearrange('(mc p) f -> p mc f', p=128), [], [vkey], Vsems[b_ % 2])
                for c in range(8):
                    for mc in range(2):
                        col = (b_ * 2 + mc) * 4 + c // 2
                        mm(ps[:, po, c * NS + b_:c * NS + b_ + 1], Vb[:, mc, c * 128:(c + 1) * 128], pbf[:, col:col + 1], mc == 0, mc == 1, [vkey, 'st3'], [('ps', po)])
            cp(ob[:, :, 0:n], ps[:, po, 0:8 * NS].rearrange('p (c b) -> p c b', c=8), [('ps', po)], ['H1'], eng='act')
            resid_ln('xa_wo', ob, 'H1', 2, n)
            ffn('ffn2_wi', 'ffn2_wo', 3, n)
            store_fm_tokens(h32, 'h32', 0, n, di['ys'])

        if do_sample:
            sample_path()
        P.emit()
    return nc, P


_CACHE = {}


def _consts():
    a = np.arange(128) % 64
    b = np.arange(64)
    su = (a[:, None] < b[None, :]).astype(np.float32)
    ui = (a[:, None] <= b[None, :]).astype(np.float32)
    sl = (a[:, None] > b[None, :]).astype(np.float32)
    ey = (a[:, None] == b[None, :]).astype(np.float32)
    bd = np.zeros((128, 128), np.float32)
    bd[:64, :64] = 1.0
    bd[64:, 64:] = 1.0
    rs = np.ones((128, NT), np.float32)
    rs[:, ::C] = 0.0
    return {'c_all': np.ascontiguousarray(np.concatenate([np.eye(128, dtype=np.float32), np.ones((128, 128), np.float32), bd, su, ui, sl, ey, rs], axis=1))}


def make_in_maps(inputs):
    f = lambda a: np.ascontiguousarray(np.asarray(a, dtype=np.float32))
    shared = {}
    for nm in ['ffn1_wi', 'ffn1_wo', 'ffn2_wi', 'ffn2_wo', 'w_in', 'decay_w2', 'aaa_a2', 'gate_g2',
               'lru_wr', 'lru_wi', 'w_mix_out', 'xa_wq', 'xa_wk', 'xa_wv', 'xa_wo']:
        shared[nm] = np.ascontiguousarray(f(inputs[nm])[0])
    shared['prm'] = np.ascontiguousarray(np.concatenate(
        [f(inputs[nm])[0].reshape(-1, 128) for nm in ['ln_g', 'ln_b', 'shift_mu', 'conv_w'] + VEC_NAMES], axis=0))
    shared.update(_consts())
    maps = []
    for c in range(8):
        m = dict(shared)
        sl = slice(c * NS, (c + 1) * NS)
        m['xp'] = f(inputs['x_prompt'][c])
        m['mem'] = f(inputs['mem_prompt'][c])
        m['xs'] = f(inputs['x_sample'][sl, 0])
        m['cmk'] = f(inputs['cache_mem_k'][0, sl]).reshape(NS, NMEM, D)
        m['cmv'] = f(inputs['cache_mem_v'][0, sl]).reshape(NS, NMEM, D)
        m['srw'] = f(inputs['state_rwkv'][0, sl]).reshape(NS, D, 64)
        m['ssh'] = f(inputs['state_rwkv_shift'][0, sl])
        m['slru'] = f(inputs['state_lru'][0, sl])
        m['scv'] = f(inputs['state_conv'][0, sl])
        maps.append(m)
    return maps


def kernel(**inputs):
    if 'nc' not in _CACHE:
        _CACHE['nc'] = build()[0]
    nc = _CACHE['nc']
    maps = make_in_maps(inputs)
    res = run_bass_kernel_spmd(nc, maps, core_ids=list(range(8)))
    R = res.results
    cat = lambda k: np.stack([np.asarray(r[k], dtype=np.float32) for r in R])
    catc = lambda k: np.concatenate([np.asarray(r[k], dtype=np.float32) for r in R], axis=0)
    yp = cat('yp')
    ys = catc('ys').reshape(8 * NS, 1, D)
    pmk = cat('pmk').reshape(1, 8, NMEM, 4, 256)
    pmv = cat('pmv').reshape(1, 8, NMEM, 4, 256)
    prw = cat('prw').reshape(1, 8, 16, 64, 64)
    psh = cat('psh').reshape(1, 8, RP)
    plru = cat('plru').reshape(1, 8, D)
    pcv = cat('pcv').reshape(1, 8, 3, D)
    srw = catc('srw_o').reshape(1, 8 * NS, 16, 64, 64)
    ssh = catc('ssh_o').reshape(1, 8 * NS, RP)
    slru = catc('slru_o').reshape(1, 8 * NS, D)
    scv = catc('scv_o').reshape(1, 8 * NS, 3, D)
    return (yp, ys, pmk, pmv, prw, psh, plru, pcv, srw, ssh, slru, scv)
```

```python
import math
import os
import numpy as np
from contextlib import ExitStack
import concourse.bass as bass
import concourse.mybir as mybir
from concourse.bass_utils import run_bass_kernel_spmd

F32 = mybir.dt.float32
BF16 = mybir.dt.bfloat16
AF = mybir.ActivationFunctionType
ALU = mybir.AluOpType
AX = mybir.AxisListType

ENGS = ['pe', 'dve', 'act', 'pool', 'sp']

D = 1024
T = 2048
NT = 256
NTILES = T // NT
C = 64
NCH = NT // C
DFF = 2816
NJ = DFF // 128
RP = 3328
PW = 7424
NMEM = 256
NS = 16
ALPHA = 2.0 ** 0.25
LN_EPS = 1e-5
GN_EPS = 64e-5
C0 = math.exp(-0.5)


class DmaSem:
    def __init__(self, sem):
        self.sem = sem
        self.count = 0


class Prog:
    def __init__(self, nc, stack):
        self.nc = nc
        self.stack = stack
        self.ops = {e: [] for e in ENGS}
        self.last_w = {}
        self.readers = {}
        self.seen = {e: {} for e in ENGS}
        self.dsems = []
        self.out_tokens = []

    def dma_sem(self):
        s = DmaSem(self.stack.enter_context(self.nc.semaphore('dsem%d' % len(self.dsems))))
        self.dsems.append(s)
        return s

    def sbuf(self, name, shape, dt):
        return self.stack.enter_context(self.nc.sbuf_tensor(name, list(shape), dt))

    def psum(self, name, shape, dt):
        return self.stack.enter_context(self.nc.psum_tensor(name, list(shape), dt))

    def barrier(self, keys, engines=ENGS):
        if 'B' in os.environ.get('TOG', ''):
            return
        for e in engines:
            self.op(e, None, reads=keys, track=False)

    def op(self, eng, fn, reads=(), writes=(), dsem=None, is_out=False, track=True):
        isps = lambda k: isinstance(k, tuple) and k[0] == 'ps'
        writes = list(writes) + [k for k in reads if isps(k)]
        reads = [k for k in reads if not isps(k)]
        deps = []
        for k in reads:
            t = self.last_w.get(k)
            if t is not None:
                deps.append(t)
        for k in writes:
            t = self.last_w.get(k)
            if t is not None:
                deps.append(t)
            deps.extend(self.readers.get(k, {}).values())
        need = {}
        for t in deps:
            if t[0] == 'eng':
                if t[1] == eng and dsem is None and (eng == 'pe' or os.environ.get('NOSELF')):
                    continue
                key = ('eng', t[1])
            else:
                key = ('dma', id(t[1]))
            if need.get(key, (None, -1))[1] < t[2]:
                need[key] = (t[1], t[2])
        waits = []
        for key, (src, v) in need.items():
            if self.seen[eng].get(key, -1) >= v:
                continue
            self.seen[eng][key] = v
            waits.append((key[0], src, v))
        idx = len(self.ops[eng])
        self.ops[eng].append(dict(fn=fn, waits=waits, dsem=dsem, target=False, phase=getattr(self, 'phase', '')))
        if dsem is not None:
            dsem.count += 16
            tok = ('dma', dsem, dsem.count)
        else:
            tok = ('eng', eng, idx)
        for k in writes:
            self.last_w[k] = tok
            self.readers[k] = {}
        for k in (reads if track else ()):
            r = self.readers.setdefault(k, {})
            rk = (tok[0], tok[1] if tok[0] == 'eng' else id(tok[1]))
            if rk not in r or r[rk][2] < tok[2]:
                r[rk] = tok
        if is_out:
            self.out_tokens.append(tok)
        return tok

    def emit(self):
        nc = self.nc
        fin = {}
        for t in self.out_tokens:
            fin[id(t[1])] = (t[1], max(fin.get(id(t[1]), (None, 0))[1], t[2]))
        self.ops['sp'].append(dict(fn=None, waits=[('dma', s, v) for s, v in fin.values()], dsem=None, target=False))
        for e in ENGS:
            for o in self.ops[e]:
                for kind, src, v in o['waits']:
                    if kind == 'eng':
                        self.ops[src][v]['target'] = True
        semval = {}
        for e in ENGS:
            c = 0
            vals = []
            for o in self.ops[e]:
                if o['target']:
                    c += 1
                vals.append(c)
            semval[e] = vals
        esem = {e: self.stack.enter_context(nc.semaphore('esem_' + e)) for e in ENGS}
        handles = {'pe': 'tensor', 'dve': 'vector', 'act': 'scalar', 'pool': 'gpsimd', 'sp': 'sync'}
        with nc.Block() as block:
            def make(e):
                def body(eng):
                    for o in self.ops[e]:
                        for kind, src, v in o['waits']:
                            if kind == 'eng':
                                eng.wait_ge(esem[src], semval[src][v])
                            else:
                                eng.wait_ge(src.sem, v)
                        if o['fn'] is None:
                            continue
                        inst = o['fn'](eng)
                        if os.environ.get('ANNOT') and o.get('phase'):
                            inst.annotate(o['phase'])
                        if o['dsem'] is not None:
                            inst.then_inc(o['dsem'].sem, 16)
                        elif o['target']:
                            inst.then_inc(esem[e], 1)
                return body
            for e in ENGS:
                getattr(block, handles[e])(make(e))
        self.stats = {e: len(self.ops[e]) for e in ENGS}


VEC_NAMES = ['decay_w0', 'aaa_a0', 'k_k', 'k_a', 'r_k', 'gn_g', 'gn_b', 'conv_b', 'lru_br', 'lru_bi', 'lru_lambda']
W_NAMES = ['ffn1_wi', 'ffn1_wo', 'ffn2_wi', 'ffn2_wo', 'w_in', 'w_mix_out', 'xa_wq', 'xa_wk', 'xa_wv', 'xa_wo']


def build(dbg=None, do_sample=True, stage=99):
    nc = bass.Bass('TRN2', target_bir_lowering=False)
    di = {}

    DECL = os.environ.get('DECL')

    def din(name, shape):
        if DECL and name not in DECL.split(','):
            return None
        di[name] = nc.dram_tensor(name, list(shape), F32, kind='ExternalInput').ap()
        return di[name]

    def dout(name, shape):
        if DECL and name not in DECL.split(','):
            return None
        di[name] = nc.dram_tensor(name, list(shape), F32, kind='ExternalOutput').ap()
        return di[name]

    din('xp', [T, D]); din('mem', [NMEM, D])
    din('xs', [NS, D]); din('cmk', [NS, NMEM, D]); din('cmv', [NS, NMEM, D])
    din('srw', [NS, D, 64]); din('ssh', [NS, RP]); din('slru', [NS, D]); din('scv', [NS, 3, D])
    din('prm', [210, 128])
    din('ffn1_wi', [D, 2 * DFF]); din('ffn1_wo', [DFF, D]); din('ffn2_wi', [D, 2 * DFF]); din('ffn2_wo', [DFF, D])
    din('w_in', [D, PW])
    din('decay_w2', [64, D]); din('aaa_a2', [64, D]); din('gate_g2', [128, D])
    din('lru_wr', [16, 64, 64]); din('lru_wi', [16, 64, 64])
    for w in ['w_mix_out', 'xa_wq', 'xa_wk', 'xa_wv', 'xa_wo']:
        din(w, [D, D])
    din('c_all', [128, 640 + NT])
    dout('yp', [T, D]); dout('ys', [NS, D]); dout('pmk', [NMEM, D]); dout('pmv', [NMEM, D])
    dout('prw', [D, 64]); dout('psh', [RP]); dout('plru', [D]); dout('pcv', [3, D])
    dout('srw_o', [NS, D, 64]); dout('ssh_o', [NS, RP]); dout('slru_o', [NS, D]); dout('scv_o', [NS, 3, D])
    dbg = dbg or {}
    for k, shp in dbg.items():
        dout('dbg_' + k, shp)

    with ExitStack() as st:
        P = Prog(nc, st)
        n = NT
        ident = P.sbuf('ident', [128, 128], F32)
        identb = P.sbuf('identb', [128, 128], BF16)
        onesb = P.sbuf('onesb', [128, 128], BF16)
        bdb = P.sbuf('bdb', [128, 128], BF16)
        mask = P.sbuf('mask', [128, 256], F32)
        reset = P.sbuf('reset', [128, NT], F32)
        lng = P.sbuf('lng', [128, 4, 8], F32); lnb = P.sbuf('lnb', [128, 4, 8], F32)
        mu = P.sbuf('mu', [128, 26], F32); omu = P.sbuf('omu', [128, 26], F32)
        vec = {v: P.sbuf('v_' + v, [128, 8], F32) for v in VEC_NAMES}
        oka = P.sbuf('oka', [128, 8], F32)
        lsp = P.sbuf('lsp', [128, 8], F32)
        cw = P.sbuf('cw', [128, 4, 8], F32)
        w2a2 = P.sbuf('w2a2', [128, D], BF16)
        g2 = P.sbuf('g2', [128, D], BF16)
        wrbd = P.sbuf('wrbd', [128, 8, 128], BF16); wibd = P.sbuf('wibd', [128, 8, 128], BF16)
        WCAP = 4096
        NWB = 4
        wbuf = [P.sbuf('wbuf%d' % i, [128, WCAP], BF16) for i in range(NWB)]
        wsem = [P.dma_sem() for i in range(NWB)]
        h32 = P.sbuf('h32', [128, 8, n], F32)
        hb = P.sbuf('hb', [128, 8, n], BF16)
        FB_ = [P.sbuf('F%d' % i, [128, 8, n], F32) for i in range(5)]
        HX = P.sbuf('HX', [128, 24, n], BF16)
        HB_ = [P.sbuf('H%d' % i, [128, 8, n], BF16) for i in range(4)]
        memT = HB_[3]
        pl = P.sbuf('pl', [128, 8, n + 3], F32)
        st1 = [P.sbuf('st%d' % i, [128, n], F32) for i in range(5)]
        st2 = [P.sbuf('su%d' % i, [128, n], F32) for i in range(5)]
        stsel = lambda i: ((st1, 'st') if i % 2 == 0 else (st2, 'su'))
        carry_sh = P.sbuf('carry_sh', [128, 26], F32)
        carry_h = P.sbuf('carry_h', [128, 8], F32)
        A32 = P.sbuf('A32', [128, 8, 64], F32)
        A0b = [P.sbuf('A0b%d' % i, [128, 8, 64], BF16) for i in range(2)]
        RHSb = P.sbuf('RHSb', [128, 8, 64], BF16)
        Ub = P.sbuf('Ub', [128, 8, 64], BF16)
        WCs = P.sbuf('WCs', [128, 8, NCH], F32)
        X32 = [P.sbuf('X32_%d' % i, [128, 8, 64], F32) for i in range(2)]
        Xb = [P.sbuf('Xb_%d' % i, [128, 8, 64], BF16) for i in range(2)]
        PP = [[P.sbuf('PP_%d_%d' % (i, j), [128, 8, 128], BF16) for j in range(2)] for i in range(2)]
        LP = P.sbuf('LP', [128, 8, NCH, 128], BF16)
        PN = P.sbuf('PN', [128, 8, NCH, 128], BF16)
        XT = P.sbuf('XT', [128, 8, NCH, 64], BF16)
        mkT = P.sbuf('mkT', [128, 8, NMEM], BF16)
        mvb = P.sbuf('mvb', [128, 2, D], BF16)
        tok32 = [P.sbuf('tok32_%d' % i, [128, D], F32) for i in range(2)]
        osml = P.sbuf('osml', [128, 8, 8], F32)
        osh = P.sbuf('osh', [128, 26], F32)
        sgb = [P.sbuf('sgb%d' % i, [128, NT], F32) for i in range(2)]
        ps = P.psum('ps', [128, 8, 512], F32)
        dsem_c = [P.dma_sem() for i in range(4)]
        dsem_in = [P.dma_sem() for i in range(2)]
        dsem_out = [P.dma_sem() for i in range(2)]
        dsem_misc = P.dma_sem()

        bank_ctr = [0]
        tokctr = [0]

        def bank():
            b = bank_ctr[0] % 8
            bank_ctr[0] += 1
            return b

        def mm(out, lhsT, rhs, start, stop, reads, writes):
            P.op('pe', lambda e: e.matmul(out, lhsT=lhsT, rhs=rhs, start=start, stop=stop), reads=reads, writes=writes)

        def tr(out, in_, idn, reads, writes):
            P.op('pe', lambda e: e.transpose(out, in_, idn), reads=reads, writes=writes)

        def act(out, in_, func, reads, writes, bias=None, scale=None):
            kw = {}
            if bias is not None:
                kw['bias'] = bias
            if scale is not None:
                kw['scale'] = scale
            P.op('act', lambda e: e.activation(out=out, in_=in_, func=func, **kw), reads=reads, writes=writes)

        def tt(out, in0, in1, op, reads, writes, eng='dve'):
            P.op(eng, lambda e: e.tensor_tensor(out=out, in0=in0, in1=in1, op=op), reads=reads, writes=writes)

        def ts(out, in0, s1, s2, op0, op1, reads, writes, eng='dve'):
            if s2 is None:
                P.op(eng, lambda e: e.tensor_scalar(out=out, in0=in0, scalar1=s1, scalar2=None, op0=op0), reads=reads, writes=writes)
            else:
                P.op(eng, lambda e: e.tensor_scalar(out=out, in0=in0, scalar1=s1, scalar2=s2, op0=op0, op1=op1), reads=reads, writes=writes)

        def stt(out, in0, scalar, in1, op0, op1, reads, writes):
            P.op('dve', lambda e: e.scalar_tensor_tensor(out=out, in0=in0, scalar=scalar, in1=in1, op0=op0, op1=op1), reads=reads, writes=writes)

        def cp(out, in_, reads, writes, eng='dve'):
            if eng == 'act':
                act(out, in_, AF.Copy, reads, writes)
            else:
                P.op(eng, lambda e: e.tensor_copy(out=out, in_=in_), reads=reads, writes=writes)

        def recip(out, in_, reads, writes):
            P.op('dve', lambda e: e.reciprocal(out=out, in_=in_), reads=reads, writes=writes)

        def dma(eng, out, in_, reads, writes, dsem, is_out=False, **kw):
            P.op(eng, lambda e: e.dma_start(out=out, in_=in_, **kw), reads=reads, writes=writes, dsem=dsem, is_out=is_out)

        def interleave2(ga, gb):
            a_ok = b_ok = True
            while a_ok or b_ok:
                if a_ok:
                    try:
                        next(ga)
                    except StopIteration:
                        a_ok = False
                if b_ok:
                    try:
                        next(gb)
                    except StopIteration:
                        b_ok = False

        def dump(name, src_ap, key):
            if name in dbg:
                dma('sp', di['dbg_' + name], src_ap, [key], [], P.dma_sem(), is_out=True)

        PARTS = os.environ.get('PARTS', 'abcdefg')
        dma('sp', ident[:], di['c_all'][:, 0:128], [], ['ident'], dsem_c[0])
        dma('sp', mask[:], di['c_all'][:, 384:640], [], ['mask'], dsem_c[0])
        dma('sp', reset[:], di['c_all'][:, 640:640 + NT], [], ['reset'], dsem_c[0])
        P.barrier(['ident', 'mask', 'reset'])
        if 'b' in PARTS:
            dma('pool', identb[:], di['c_all'][:, 0:128], [], ['identb'], dsem_c[1])
            dma('pool', onesb[:], di['c_all'][:, 128:256], [], ['onesb'], dsem_c[1])
            dma('pool', bdb[:], di['c_all'][:, 256:384], [], ['bdb'], dsem_c[1])
            dma('pool', w2a2[0:64, :], di['decay_w2'], [], ['w2a2'], dsem_c[1])
            dma('pool', w2a2[64:128, :], di['aaa_a2'], [], ['w2a2'], dsem_c[1])
            dma('pool', g2[:], di['gate_g2'], [], ['g2'], dsem_c[1])
        if 'm' in PARTS or PARTS == 'abcdefg':
            P.op('dve', lambda e: e.memset(wrbd[:], 0.0), writes=['wrbd'])
            P.op('dve', lambda e: e.memset(wibd[:], 0.0), writes=['wibd'])
        for (wt, nm, key) in ([(wrbd, 'lru_wr', 'wrbd'), (wibd, 'lru_wi', 'wibd')] if 'c' in PARTS else []):
            src = di[nm].rearrange('(c two) i o -> two i c o', two=2)
            for par in range(2):
                dma('pool', wt[par * 64:(par + 1) * 64, :, par * 64:(par + 1) * 64], src[par], [], [key], dsem_c[1])
        P.barrier(['identb', 'onesb', 'bdb', 'w2a2', 'g2', 'wrbd', 'wibd'])
        prm = [tok32[0], tok32[1]]
        rows = []
        rows.append((lng[:].rearrange('p l c -> p (l c)'), di['prm'][0:32, :], 'lng'))
        rows.append((lnb[:].rearrange('p l c -> p (l c)'), di['prm'][32:64, :], 'lnb'))
        rows.append((mu[:], di['prm'][64:90, :], 'mu'))
        rows.append((cw[:].rearrange('p l c -> p (l c)'), di['prm'][90:122, :], 'cw'))
        for vi_, v in enumerate(VEC_NAMES):
            rows.append((vec[v][:], di['prm'][122 + 8 * vi_:130 + 8 * vi_, :], 'v_' + v))
        groups = [[]]
        cnt = 0
        for r_ in rows:
            k_ = r_[1].shape[0]
            if cnt + k_ > 128:
                groups.append([]); cnt = 0
            groups[-1].append((cnt, k_) + r_)
            cnt += k_
        for gi_, grp in enumerate(groups if 'd' in PARTS else []):
            tk = prm[gi_ % 2]
            tot = 0
            for (o_, k_, dst, src, key) in grp:
                dma('sp', tk[o_:o_ + k_, 0:128], src, [], [('tok32', gi_ % 2)], dsem_c[2 + gi_ % 2])
                tot = o_ + k_
            pb = bank()
            tr(ps[:, pb, 0:tot], tk[0:tot, 0:128], ident[0:tot, 0:tot], [('tok32', gi_ % 2), 'ident'], [('ps', pb)])
            for (o_, k_, dst, src, key) in grp:
                cp(dst, ps[:, pb, o_:o_ + k_], [('ps', pb)], [key])
        if 'e' in PARTS:
            ts(omu[:], mu[:], -1.0, 1.0, ALU.mult, ALU.add, ['mu'], ['omu'])
            ts(oka[:], vec['k_a'][:], -1.0, 1.0, ALU.mult, ALU.add, ['v_k_a'], ['oka'])
            act(lsp[:], vec['lru_lambda'][:], AF.Exp, ['v_lru_lambda'], ['lsp'], scale=-1.0)
            act(lsp[:], lsp[:], AF.Ln, ['lsp'], ['lsp'], bias=1.0)
            ts(lsp[:], lsp[:], -8.0, None, ALU.mult, None, ['lsp'], ['lsp'])

        def wblocks():
            def ffn_blocks(wi, wo):
                for g in range(11):
                    def f(buf, g=g, wi=wi):
                        v = buf[:, 0:4096].rearrange('p (k c) -> p k c', k=8)
                        return [(v[:, :, 0:256], di[wi][:, g * 256:(g + 1) * 256].rearrange('(k p) c -> p k c', p=128)),
                                (v[:, :, 256:512], di[wi][:, DFF + g * 256:DFF + (g + 1) * 256].rearrange('(k p) c -> p k c', p=128))]
                    yield ((wi, g), f)
                for mp in range(8):
                    def f(buf, mp=mp, wo=wo):
                        v = buf[:, 0:NJ * 128].rearrange('p (j c) -> p j c', j=NJ)
                        return [(v, di[wo][:, mp * 128:(mp + 1) * 128].rearrange('(j p) c -> p j c', p=128))]
                    yield ((wo, mp), f)

            def sq_blocks(w, ncols):
                nb = (ncols + 511) // 512
                for b in range(nb):
                    c0 = b * 512
                    cn = min(512, ncols - c0)
                    def f(buf, c0=c0, cn=cn, w=w):
                        v = buf[:, 0:8 * cn].rearrange('p (k c) -> p k c', k=8)
                        return [(v, di[w][:, c0:c0 + cn].rearrange('(k p) c -> p k c', p=128))]
                    yield ((w, b), f)
            def one_pass():
                yield from ffn_blocks('ffn1_wi', 'ffn1_wo')
                yield from sq_blocks('w_in', PW)
                yield from sq_blocks('w_mix_out', D)
                yield from sq_blocks('xa_wq', D)
                yield from sq_blocks('xa_wo', D)
                yield from ffn_blocks('ffn2_wi', 'ffn2_wo')
            ntr = int(os.environ.get('NTI', NTILES))
            for it in range(ntr + (1 if do_sample else 0)):
                for blk, (tag, f) in enumerate(one_pass()):
                    if it == 0 and ntr > 0 and tag == ('xa_wq', 0):
                        yield from sq_blocks('xa_wk', D)
                        yield from sq_blocks('xa_wv', D)
                    yield (tag, f, it, blk)

        wgen = wblocks()
        wstate = dict(issued=0, consumed=0, pending=[])

        NBLK = 59
        wsc = nc.dram_tensor('wsc', [NBLK, 128, WCAP], BF16, kind='Internal').ap()
        wbsem = [P.dma_sem() for i in range(NWB)]

        def w_used(tag):
            if tag[0].endswith('_wi'):
                return 4096
            if tag[0].endswith('_wo') and tag[0].startswith('ffn'):
                return NJ * 128
            ncols = PW if tag[0] == 'w_in' else D
            return 8 * min(512, ncols - tag[1] * 512)

        def w_issue():
            try:
                item = next(wgen)
            except StopIteration:
                return False
            i = wstate['issued'] % NWB
            if len(item) == 2:
                tag, f = item
                for (dst, src) in f(wbuf[i]):
                    dma('pool', dst, src, [], [('wbuf', i)], wsem[i])
            else:
                tag, f, it, blk = item
                used = w_used(tag)
                if it == 0:
                    for (dst, src) in f(wbuf[i]):
                        dma('pool', dst, src, [], [('wbuf', i)], wsem[i])
                    dma('sp', wsc[blk, :, 0:used], wbuf[i][:, 0:used], [('wbuf', i)], [('wsc', blk)], wbsem[i])
                else:
                    dma('pool', wbuf[i][:, 0:used], wsc[blk, :, 0:used], [('wsc', blk)], [('wbuf', i)], wsem[i])
            wstate['pending'].append((tag, i))
            wstate['issued'] += 1
            return True

        def w_next(tag):
            while wstate['issued'] - wstate['consumed'] < NWB:
                if not w_issue():
                    break
            t, i = wstate['pending'].pop(0)
            assert t == tag, (t, tag)
            wstate['consumed'] += 1
            return wbuf[i], ('wbuf', i)

        def sqview(buf, cn):
            return buf[:, 0:8 * cn].rearrange('p (k c) -> p k c', k=8)

        def load_tokens_fm(src_rows_ap, nrows, dst32, dstb, col0, dkey):
            tokctr[0] += 1
            i = tokctr[0] % 2 if 'A' in os.environ.get('TOG', 'A') else 0
            tk = tok32[i]
            dma('sp', tk[0:nrows, :], src_rows_ap, [], [('tok32', i)], dsem_in[i])
            for half in range(2):
                b = bank()
                for q in range(4):
                    kc = half * 4 + q
                    P.op('pe', lambda e, b=b, q=q, kc=kc: e.transpose(ps[:, b, q * 128:q * 128 + nrows], tk[0:nrows, kc * 128:(kc + 1) * 128], ident[0:nrows, 0:nrows]),
                         reads=[('tok32', i), 'ident'], writes=[('ps', b)], track=('W' not in os.environ.get('TOG', '')))
                src = ps[:, b, :].rearrange('p (q t) -> p q t', q=4)[:, :, 0:nrows]
                if dst32 is not None:
                    cp(dst32[:, half * 4:half * 4 + 4, col0:col0 + nrows], src, [('ps', b), ('tok32', i)], [dkey], eng='act')
                if dstb is not None:
                    cp(dstb[:, half * 4:half * 4 + 4, col0:col0 + nrows], src, [('ps', b)], ['hb' if dkey == 'h32' else 'H3'], eng='dve')

        def store_fm_tokens(src32, skey, col0, nrows, dst_rows_ap, nch=8, feat0=0):
            i = bank_ctr[0] % 2
            tk = tok32[i]
            for g0 in range(0, nch, 4):
                b = bank()
                gn = min(4, nch - g0)
                for q in range(gn):
                    tr(ps[0:nrows, b, q * 128:(q + 1) * 128], src32[:, g0 + q, col0:col0 + nrows], ident[:, :],
                       [skey, 'ident'], [('ps', b)])
                cp(tk[0:nrows, g0 * 128:(g0 + gn) * 128], ps[0:nrows, b, 0:gn * 128], [('ps', b)], [('tok32', i)], eng='act')
            dma('sp', dst_rows_ap, tk[0:nrows, 0:nch * 128], [('tok32', i)], [], dsem_out[i], is_out=True)

        def layernorm(idx, n, eps):
            P.phase = 'layernorm'
            zsq = HB_[0]
            cp(hb[:, :, 0:n], h32[:, :, 0:n], ['h32'], ['hb'], eng='dve')
            act(zsq[:, :, 0:n], h32[:, :, 0:n], AF.Square, ['h32'], ['H0'])
            b1 = bank(); b2 = bank()
            for kc in range(8):
                mm(ps[:, b1, 0:n], onesb[:], hb[:, kc, 0:n], kc == 0, kc == 7, ['onesb', 'hb'], [('ps', b1)])
            for kc in range(8):
                mm(ps[:, b2, 0:n], onesb[:], zsq[:, kc, 0:n], kc == 0, kc == 7, ['onesb', 'H0'], [('ps', b2)])
            mean, msq, var, rstd, nmr = [s[:, 0:n] for s in st1]
            ts(mean, ps[:, b1, 0:n], 1.0 / D, None, ALU.mult, None, [('ps', b1)], ['st0'])
            tt(msq, mean, mean, ALU.mult, ['st0'], ['st1'])
            stt(var, ps[:, b2, 0:n], 1.0 / D, msq, ALU.mult, ALU.subtract, [('ps', b2), 'st1'], ['st2'])
            ts(var, var, 0.0, eps, ALU.max, ALU.add, ['st2'], ['st2'])
            act(var, var, AF.Sqrt, ['st2'], ['st2'])
            recip(rstd, var, ['st2'], ['st3'])
            tt(nmr, mean, rstd, ALU.mult, ['st0', 'st3'], ['st4'])
            fine = [('h32', kc) for kc in range(8)]
            P.op('dve', lambda e: e.engine_nop(), writes=['h32'] + fine)
            tpool = [(st1[1], 'st1'), (st1[2], 'st2'), (st2[0], 'su0'), (st2[1], 'su1'), (st2[2], 'su2'), (st2[3], 'su3'), (st2[4], 'su4'), (sgb[0], ('sg', 0))]
            for kc in range(8):
                T_, tk_ = tpool[kc]
                T_ = T_[:, 0:n]
                tt(T_, h32[:, kc, 0:n], rstd, ALU.mult, [('h32', kc), 'st3'], [tk_])
                tt(T_, T_, nmr, ALU.subtract, [tk_, 'st4'], [tk_])
                act(h32[:, kc, 0:n], T_, AF.Identity, [tk_, 'lng', 'lnb'], [('h32', kc)],
                    bias=lnb[:, idx, kc:kc + 1], scale=lng[:, idx, kc:kc + 1])
                act(hb[:, kc, 0:n], T_, AF.Identity, [tk_, 'lng', 'lnb'], ['hb'],
                    bias=lnb[:, idx, kc:kc + 1], scale=lng[:, idx, kc:kc + 1])
            P.op('dve', lambda e: e.engine_nop(), writes=['h32'] + fine)

        def ffn(wi, wo, ln_idx, n):
            P.phase = 'ffn'
            actb = HX
            sg = [sgb[0][:, 0:n], sgb[1][:, 0:n]]
            for g in range(11):
                wb, wk = w_next((wi, g))
                wv = sqview(wb, 512)
                for jj in range(2):
                    j = 2 * g + jj
                    pg = bank(); pu = bank()
                    for kc in range(8):
                        mm(ps[:, pg, 0:n], wv[:, kc, jj * 128:(jj + 1) * 128], hb[:, kc, 0:n], kc == 0, kc == 7, [wk, 'hb'], [('ps', pg)])
                    for kc in range(8):
                        mm(ps[:, pu, 0:n], wv[:, kc, 256 + jj * 128:256 + (jj + 1) * 128], hb[:, kc, 0:n], kc == 0, kc == 7, [wk, 'hb'], [('ps', pu)])
                    act(sg[jj], ps[:, pg, 0:n], AF.Silu, [('ps', pg)], [('sg', jj)])
                    tt(actb[:, j, 0:n], sg[jj], ps[:, pu, 0:n], ALU.mult, [('sg', jj), ('ps', pu)], ['HX%d' % (j // 8)])
            for m in range(8):
                wb, wk = w_next((wo, m))
                wv = wb[:, 0:NJ * 128].rearrange('p (j c) -> p j c', j=NJ)
                po = bank()
                for j in range(NJ):
                    mm(ps[:, po, 0:n], wv[:, j, :], actb[:, j, 0:n], j == 0, j == NJ - 1, [wk, 'HX%d' % (j // 8)], [('ps', po)])
                stt(h32[:, m, 0:n], ps[:, po, 0:n], 0.5 / ALPHA, h32[:, m, 0:n], ALU.mult, ALU.add, [('ps', po), 'h32'], ['h32'])
            layernorm(ln_idx, n, LN_EPS / (ALPHA * ALPHA))

        def proj(w, ncols, xin, xkey, n, evac):
            nb = (ncols + 511) // 512
            for b in range(nb):
                c0 = b * 512
                cn = min(512, ncols - c0)
                wb, wk = w_next((w, b))
                wv = sqview(wb, cn)
                for q in range(cn // 128):
                    m = c0 // 128 + q
                    pb = bank()
                    for kc in range(8):
                        mm(ps[:, pb, 0:n], wv[:, kc, q * 128:(q + 1) * 128], xin[:, kc, 0:n], kc == 0, kc == 7, [wk, xkey], [('ps', pb)])
                    evac(m, pb)

        def mem_kv():
            P.phase = 'mem_kv'
            for r in range(2):
                load_tokens_fm(di['mem'][r * 128:(r + 1) * 128, :], 128, None, memT, r * 128, 'memT')
            for (w, outname, isk) in [('xa_wk', 'pmk', True), ('xa_wv', 'pmv', False)]:
                for b in range(2):
                    wb, wk = w_next((w, b))
                    wv = sqview(wb, 512)
                    for r in range(2):
                        pb = bank()
                        for kc in range(8):
                            mm(ps[:, pb, :], memT[:, kc, r * 128:(r + 1) * 128], wv[:, kc, :], kc == 0, kc == 7, ['H3', wk], [('ps', pb)])
                        i = bank_ctr[0] % 2
                        cp(tok32[i][:, 0:512], ps[:, pb, :], [('ps', pb)], [('tok32', i)], eng='act')
                        if not isk:
                            cp(mvb[:, r, b * 512:(b + 1) * 512], ps[:, pb, :], [('ps', pb)], ['mvb'], eng='dve')
                        dma('sp', di[outname][r * 128:(r + 1) * 128, b * 512:(b + 1) * 512], tok32[i][:, 0:512], [('tok32', i)], [], dsem_out[i], is_out=True)
                    if isk:
                        for q in range(4):
                            m = b * 4 + q
                            pb = bank()
                            for kc in range(8):
                                mm(ps[:, pb, 0:NMEM], wv[:, kc, q * 128:(q + 1) * 128], memT[:, kc, :], kc == 0, kc == 7, [wk, 'H3'], [('ps', pb)])
                            cp(mkT[:, m, :], ps[:, pb, 0:NMEM], [('ps', pb)], ['mkT'], eng='act')

        def mixer_prompt(ti):
            P.phase = 'mixer_prompt'
            n = NT
            r32, k32, v32, a32, ls32 = FB_
            glb = HB_[1]; geb = HB_[2]; g0b = HB_[3]
            g1b = HX[:, 0:8, :]; xcb = HX[:, 8:16, :]
            first = (ti == 0)
            tmp = st1[0]

            def evac(m, pb):
                psn = ps[:, pb, 0:n]
                SS, KP = stsel(m)
                tmp = SS[0]
                if m < 26:
                    act(tmp[:, 1:n], ps[:, pb, 0:n - 1], AF.Copy, [('ps', pb), 'mu'], [KP + '0'], scale=mu[:, m:m + 1])
                    if first:
                        P.op('dve', lambda e, tmp=tmp: e.memset(tmp[:, 0:1], 0.0), writes=[KP + '0'])
                    else:
                        tt(tmp[:, 0:1], carry_sh[:, m:m + 1], mu[:, m:m + 1], ALU.mult, ['carry_sh', 'mu'], [KP + '0'])
                    cp(carry_sh[:, m:m + 1], ps[:, pb, n - 1:n], [('ps', pb)], ['carry_sh'])
                    if m < 24:
                        dst = [r32, k32, v32][m // 8][:, m % 8, :]
                        dkey = ['F0', 'F1', 'F2'][m // 8]
                        stt(dst, psn, omu[:, m:m + 1], tmp[:, 0:n], ALU.mult, ALU.add, [('ps', pb), 'omu', KP + '0'], [dkey])
                    else:
                        xs_ = SS[1][:, 0:n]
                        stt(xs_, psn, omu[:, m:m + 1], tmp[:, 0:n], ALU.mult, ALU.add, [('ps', pb), 'omu', KP + '0'], [KP + '1'])
                        lb = SS[2][:, 0:n].bitcast(BF16)[:, 0:n]
                        if m == 24:
                            act(lb[0:64, :], xs_[0:64, :], AF.Tanh, [KP + '1'], [KP + '2'])
                            cp(lb[64:128, :], xs_[64:128, :], [KP + '1'], [KP + '2'])
                            for (lo, dstt, dk, bvec) in [(0, ls32, 'F4', 'decay_w0'), (64, a32, 'F3', 'aaa_a0')]:
                                for q in range(8):
                                    p2 = bank()
                                    mm(ps[:, p2, 0:n], w2a2[lo:lo + 64, q * 128:(q + 1) * 128], lb[lo:lo + 64, :], True, True, ['w2a2', KP + '2'], [('ps', p2)])
                                    act(dstt[:, q, :], ps[:, p2, 0:n], AF.Sigmoid, [('ps', p2), 'v_' + bvec], [dk], bias=vec[bvec][:, q:q + 1])
                        else:
                            act(lb, xs_, AF.Sigmoid, [KP + '1'], [KP + '2'])
                            for q in range(8):
                                p2 = bank()
                                mm(ps[:, p2, 0:n], g2[:, q * 128:(q + 1) * 128], lb, True, True, ['g2', KP + '2'], [('ps', p2)])
                                cp(glb[:, q, :], ps[:, p2, 0:n], [('ps', p2)], ['H1'], eng='act')
                elif m < 34:
                    cp(pl[:, m - 26, 3:3 + n], psn, [('ps', pb)], ['pl'], eng='act')
                elif m < 42:
                    act(geb[:, m - 34, :], psn, AF.Gelu, [('ps', pb)], ['H2'])
                elif m < 50:
                    act(g0b[:, m - 42, :], psn, AF.Sigmoid, [('ps', pb)], ['H3'])
                else:
                    act(g1b[:, m - 50, :], psn, AF.Sigmoid, [('ps', pb)], ['HX0'])

            if first:
                P.op('dve', lambda e: e.memset(pl[:, :, 0:3], 0.0), writes=['pl'])
            else:
                cp(pl[:, :, 0:3], osml[:, :, 4:7], ['osml'], ['pl'])
            proj('w_in', PW, hb, 'hb', n, evac)
            cp(osml[:, :, 4:7], pl[:, :, n:n + 3], ['pl'], ['osml'])
            dump('r32', r32[:], 'F0'); dump('k32', k32[:], 'F1'); dump('v32', v32[:], 'F2'); dump('a32', a32[:], 'F3'); dump('ls32', ls32[:], 'F4')
            if ti == int(os.environ.get('NTI', NTILES)) - 1:
                cp(osml[:, :, 0:3], pl[:, :, n:n + 3], ['pl'], ['osml'])
                cp(osh[:], carry_sh[:], ['carry_sh'], ['osh'])
            return dict(r32=r32, k32=k32, v32=v32, a32=a32, ls32=ls32, glb=glb, geb=geb, g0b=g0b, g1b=g1b, xcb=xcb)

        def lru_prompt(ti, B):
            P.phase = 'lru_prompt'
            n = NT
            geb, g1b, xcb = B['geb'], B['g1b'], B['xcb']
            lmb = HX[:, 16:24, :]
            def lru_chunk(c):
                SS, KP = stsel(c)
                xc = SS[0][:, 0:n]; gr = SS[1][:, 0:n]; gi = SS[2][:, 0:n]; t3 = SS[3][:, 0:n]; hs = SS[4][:, 0:n]
                act(xc, pl[:, c, 3:3 + n], AF.Identity, ['pl', 'cw', 'v_conv_b'], [KP + '0'], bias=vec['conv_b'][:, c:c + 1], scale=cw[:, 3, c:c + 1])
                for j in range(3):
                    stt(xc, pl[:, c, j:j + n], cw[:, j, c:c + 1], xc, ALU.mult, ALU.add, ['pl', 'cw', KP + '0'], [KP + '0'])
                yield
                cp(xcb[:, c, :], xc, [KP + '0'], ['HX1'], eng='act')
                p1 = bank(); p2 = bank()
                yield
                mm(ps[:, p1, 0:n], wrbd[:, c, :], xcb[:, c, :], True, True, ['wrbd', 'HX1'], [('ps', p1)])
                yield
                mm(ps[:, p2, 0:n], wibd[:, c, :], xcb[:, c, :], True, True, ['wibd', 'HX1'], [('ps', p2)])
                yield
                act(gr, ps[:, p1, 0:n], AF.Sigmoid, [('ps', p1), 'v_lru_br'], [KP + '1'], bias=vec['lru_br'][:, c:c + 1])
                yield
                act(gi, ps[:, p2, 0:n], AF.Sigmoid, [('ps', p2), 'v_lru_bi'], [KP + '2'], bias=vec['lru_bi'][:, c:c + 1])
                yield
                act(gr, gr, AF.Exp, [KP + '1', 'lsp'], [KP + '1'], scale=lsp[:, c:c + 1])
                yield
                act(t3, gr, AF.Square, [KP + '1'], [KP + '3'])
                yield
                act(t3, t3, AF.Sqrt, [KP + '3'], [KP + '3'], scale=-1.0, bias=1.0)
                yield
                tt(gi, gi, xc, ALU.mult, [KP + '2', KP + '0'], [KP + '2'])
                yield
                tt(gi, gi, t3, ALU.mult, [KP + '2', KP + '3'], [KP + '2'])
                yield
                if ti == 0:
                    P.op('dve', lambda e, hs=hs, gr=gr, gi=gi: e.tensor_tensor_scan(out=hs, data0=gr, data1=gi, initial=0.0, op0=ALU.mult, op1=ALU.add),
                         reads=[KP + '1', KP + '2'], writes=[KP + '4'])
                else:
                    P.op('dve', lambda e, c=c, hs=hs, gr=gr, gi=gi: e.tensor_tensor_scan(out=hs, data0=gr, data1=gi, initial=carry_h[:, c:c + 1], op0=ALU.mult, op1=ALU.add),
                         reads=[KP + '1', KP + '2', ('carry_h', c)], writes=[KP + '4'])
                yield
                cp(carry_h[:, c:c + 1], hs[:, n - 1:n], [KP + '4'], [('carry_h', c)])
                yield
                tt(t3, hs, geb[:, c, :], ALU.mult, [KP + '4', 'H2'], [KP + '3'])
                yield
                tt(lmb[:, c, :], t3, g1b[:, c, :], ALU.mult, [KP + '3', 'HX0'], ['HX2'])
            for _c0 in range(0, 8, 2):
                interleave2(lru_chunk(_c0), lru_chunk(_c0 + 1))
            if ti == int(os.environ.get('NTI', NTILES)) - 1:
                cp(osml[:, :, 3:4], carry_h[:].unsqueeze(2), [('carry_h', c_) for c_ in range(8)], ['osml'])
            return lmb


        plb = pl[:].rearrange('p a b -> p (a b)').bitcast(BF16)
        plA = plb[:, 0:8 * NT].rearrange('p (a b) -> p a b', a=8)
        plB = plb[:, 8 * NT:16 * NT].rearrange('p (a b) -> p a b', a=8)

        def rwkv_core(B, state_in, state_out, skip_inverse=False, prepared=None):
            P.phase = 'rwkv_core'
            n = NT
            r32, k32, v32, a32, ls32 = B['r32'], B['k32'], B['v32'], B['a32'], B['ls32']
            bc8 = lambda v: v[:].unsqueeze(2).to_broadcast([128, 8, n])
            QR = HX[:, 0:16, :].rearrange('p (k two) (c t) -> p k two c t', two=2, t=C)
            KT = HB_[0]; NB = hb
            vb = plB
            if prepared is not None:
                prepared(QR, KT, NB, vb, WCs)
            else:
                kk32 = pl[:, :, 0:n]
                tt(kk32, k32[:], bc8(vec['k_k']), ALU.mult, ['F1', 'v_k_k'], ['pl'])
                act(KT[:], kk32, AF.Square, ['pl'], ['H0'])
                rkb = HB_[2]

                def prep_chunk(kc):
                    SS, KP = stsel(kc)
                    s_ = SS[0][:, 0:n]; u_ = SS[1][:, 0:n]; u2 = SS[2][:, 0:n]
                    pb = bank()
                    mm(ps[:, pb, 0:n], bdb[:], KT[:, kc, :], True, True, ['bdb', 'H0'], [('ps', pb)])
                    act(s_, ps[:, pb, 0:n], AF.Sqrt, [('ps', pb)], [KP + '0'])
                    act(u_, a32[:, kc, :], AF.Identity, ['F3', 'v_k_a', 'oka'], [KP + '1'], bias=oka[:, kc:kc + 1], scale=vec['k_a'][:, kc:kc + 1])
                    yield
                    ts(s_, s_, 1e-12, None, ALU.max, None, [KP + '0'], [KP + '0'])
                    yield
                    recip(s_, s_, [KP + '0'], [KP + '0'])
                    yield
                    tt(kk32[:, kc, :], kk32[:, kc, :], s_, ALU.mult, ['pl', KP + '0'], ['pl'])
                    yield
                    tt(k32[:, kc, :], k32[:, kc, :], u_, ALU.mult, ['F1', KP + '1'], ['F1'])
                    yield
                    tt(a32[:, kc, :], a32[:, kc, :], kk32[:, kc, :], ALU.mult, ['F3', 'pl'], ['F3'])
                    yield
                    tt(u2, r32[:, kc, :], k32[:, kc, :], ALU.mult, ['F0', 'F1'], [KP + '2'])
                    yield
                    act(rkb[:, kc, :], u2, AF.Copy, [KP + '2', 'v_r_k'], ['H2'], scale=vec['r_k'][:, kc:kc + 1])
                for _c0 in range(0, 8, 2):
                    interleave2(prep_chunk(_c0), prep_chunk(_c0 + 1))
                P.phase = 'rw_decay'
                def decay_chunk(kc):
                    SS, KP = stsel(kc)
                    cs = SS[0][:, 0:n]; dd = SS[1][:, 0:n]; Wi = SS[2][:, 0:n]; We = SS[3][:, 0:n]; Wv = SS[4][:, 0:n]
                    P.op('dve', lambda e, kc=kc, cs=cs: e.tensor_tensor_scan(out=cs, data0=reset[:, 0:n], data1=ls32[:, kc, :], initial=0.0, op0=ALU.mult, op1=ALU.add),
                         reads=['reset', 'F4'], writes=[KP + '0'])
                    yield
                    tt(dd, cs, ls32[:, kc, :], ALU.subtract, [KP + '0', 'F4'], [KP + '1'])
                    yield
                    act(Wi, cs, AF.Exp, [KP + '0'], [KP + '2'], scale=-C0)
                    yield
                    act(We, dd, AF.Exp, [KP + '1'], [KP + '3'], scale=-C0)
                    yield
                    act(Wv, cs, AF.Exp, [KP + '0'], [KP + '4'], scale=C0)
                    c4 = lambda a: a.rearrange('p (c t) -> p c t', t=C)
                    yield
                    tt(QR[:, kc, 1, :, :], c4(r32[:, kc, :]), c4(Wi), ALU.mult, ['F0', KP + '2'], ['HX0', 'HX1'])
                    yield
                    tt(QR[:, kc, 0, :, :], c4(kk32[:, kc, :]), c4(We), ALU.mult, ['pl', KP + '3'], ['HX0', 'HX1'])
                    yield
                    cp(WCs[:, kc, :], c4(Wi)[:, :, C - 1], [KP + '2'], ['WCs'])
                    yield
                    tt(Wi, k32[:, kc, :], Wv, ALU.mult, ['F1', KP + '4', KP + '2'], [KP + '2'])
                    yield
                    stt(We, a32[:, kc, :], -1.0, Wv, ALU.mult, ALU.mult, ['F3', KP + '4', KP + '3'], [KP + '3'])
                    yield
                    cp(NB[:, kc, :], We, [KP + '3'], ['hb'], eng='act')
                    yield
                    cp(KT[:, kc, :], Wi, [KP + '2'], ['H0'], eng='act')
                for _c0 in range(0, 8, 2):
                    interleave2(decay_chunk(_c0), decay_chunk(_c0 + 1))
                P.phase = 'rw_tok'
                vb = plB
                cp(vb[:, :, 0:n], v32[:], ['F2'], ['pl'], eng='act')
                for kc in range(8):
                    pb = bank()
                    mm(ps[:, pb, 0:n], bdb[:], rkb[:, kc, :], True, True, ['bdb', 'H2'], [('ps', pb)])
                    tt(v32[:, kc, :], v32[:, kc, :], ps[:, pb, 0:n], ALU.mult, ['F2', ('ps', pb)], ['F2'])
            tmv = lambda a: a.rearrange('p a b -> p (a b)').rearrange('p (c x) -> p c x', c=NCH)
            vT = tmv(HB_[2][:]); kTt = tmv(plA); nbT = tmv(plB)

            def to_tokmajor(src, skey, dst, dkey):
                for c in range(NCH):
                    pb = bank()
                    pbv = ps[:, pb, :].bitcast(BF16)
                    for hp in range(8):
                        for par in range(2):
                            lo = par * 64
                            tr(pbv[lo:lo + 64, hp * 64:(hp + 1) * 64], src[lo:lo + 64, hp, c * C:(c + 1) * C], identb[lo:lo + 64, lo:lo + 64],
                               [skey, 'identb'], [('ps', pb)])
                    cp(dst[:, c, :], pbv[:, 0:512], [('ps', pb)], [dkey], eng=('act' if c % 2 else 'dve'))
            to_tokmajor(vb, 'pl', vT, 'H2')
            to_tokmajor(KT, 'H0', kTt, 'pl')
            to_tokmajor(NB, 'hb', nbT, 'pl')
            vTv = lambda c, hp, lo: vT[lo:lo + 64, c, hp * 64:(hp + 1) * 64]
            kTv = lambda c, hp, lo: kTt[lo:lo + 64, c, hp * 64:(hp + 1) * 64]
            nTv = lambda c, hp, lo: nbT[lo:lo + 64, c, hp * 64:(hp + 1) * 64]
            m_su_ui = mask[:, 0:128].unsqueeze(1).to_broadcast([128, 8, 128])
            m_sl = mask[:, 128:192].unsqueeze(1).to_broadcast([128, 8, 64])
            m_eye = mask[:, 192:256].unsqueeze(1).to_broadcast([128, 8, 64])
            def ph1(c0):
                P.phase = 'rw_ph1'
                ctx = []
                for s in range(2):
                    c = c0 + s
                    b1a = bank(); b1b = bank()
                    for hp in range(8):
                        for par in range(2):
                            lo = par * 64
                            bsel = b1a if hp < 4 else b1b
                            mm(ps[lo:lo + 64, bsel, (hp % 4) * 128:(hp % 4 + 1) * 128], KT[lo:lo + 64, hp, c * C:(c + 1) * C],
                               QR[lo:lo + 64, hp, :, c, :], True, True, ['H0', 'HX0', 'HX1'], [('ps', bsel)])
                    for (bsel, h0) in [(b1a, 0), (b1b, 4)]:
                        tt(LP[:, h0:h0 + 4, c, :], ps[:, bsel, :].rearrange('p (h x) -> p h x', h=4), m_su_ui[:, 0:4, :], ALU.mult,
                           [('ps', bsel), 'mask'], [('LP', c)])
                    b2a = bank(); b2b = bank()
                    for hp in range(8):
                        for par in range(2):
                            lo = par * 64
                            bsel = b2a if hp < 4 else b2b
                            mm(ps[lo:lo + 64, bsel, (hp % 4) * 128:(hp % 4 + 1) * 128], NB[lo:lo + 64, hp, c * C:(c + 1) * C],
                               QR[lo:lo + 64, hp, :, c, :], True, True, ['hb', 'HX0', 'HX1'], [('ps', bsel)])
                    for (bsel, h0) in [(b2a, 0), (b2b, 4)]:
                        tt(PN[:, h0:h0 + 4, c, :], ps[:, bsel, :].rearrange('p (h x) -> p h x', h=4), m_su_ui[:, 0:4, :], ALU.mult,
                           [('ps', bsel), 'mask'], [('PN', c)])
                    b3 = bank()
                    for hp in range(8):
                        for par in range(2):
                            lo = par * 64
                            mm(ps[lo:lo + 64, b3, hp * 64:(hp + 1) * 64], QR[lo:lo + 64, hp, 0, c, :], NB[lo:lo + 64, hp, c * C:(c + 1) * C],
                               True, True, ['HX0', 'HX1', 'hb'], [('ps', b3)])
                    pp = PP[s][0]
                    cp(pp[:, :, 0:64], PN[:, :, c, 0:64], [('PN', c)], [('PP', s, 0)], eng='act')
                    tt(pp[:, :, 64:128], ps[:, b3, :].rearrange('p (h x) -> p h x', h=8), m_sl, ALU.mult, [('ps', b3), 'mask'], [('PP', s, 0)])
                    tt(X32[s][:], PN[:, :, c, 0:64], m_eye, ALU.add, [('PN', c), 'mask'], [('X32', s)])
                    cp(Xb[s][:], X32[s][:], [('X32', s)], [('Xb', s)], eng='act')
                    ctx.append(c)
                if skip_inverse:
                    for s_ in range(2):
                        cp(XT[:, :, ctx[s_], :], X32[s_][:], [('X32', s_)], [('XT', ctx[s_])], eng='act')
                for lvl in ([] if skip_inverse else range(1, 6)):
                    yield
                    P.phase = 'rw_ph1'
                    cur = (lvl - 1) % 2; nxt = lvl % 2
                    banks = []
                    for s in range(2):
                        ba = bank(); bb = bank()
                        src = PP[s][cur]
                        for hp in range(8):
                            for par in range(2):
                                lo = par * 64
                                bsel = ba if hp < 4 else bb
                                o0 = (hp % 4) * 128
                                mm(ps[lo:lo + 64, bsel, o0:o0 + 64], src[lo:lo + 64, hp, 64:128], src[lo:lo + 64, hp, 0:64], True, True,
                                   [('PP', s, cur)], [('ps', bsel)])
                                mm(ps[lo:lo + 64, bsel, o0 + 64:o0 + 128], src[lo:lo + 64, hp, 0:64], src[lo:lo + 64, hp, 64:128], True, True,
                                   [('PP', s, cur)], [('ps', bsel)])
                        banks.append((ba, bb))
                    for s in range(2):
                        ba, bb = banks[s]
                        dst = PP[s][nxt]
                        cp(dst[:, 0:4, :], ps[:, ba, :].rearrange('p (h x) -> p h x', h=4), [('ps', ba)], [('PP', s, nxt)], eng='act')
                        cp(dst[:, 4:8, :], ps[:, bb, :].rearrange('p (h x) -> p h x', h=4), [('ps', bb)], [('PP', s, nxt)], eng='dve')
                    xb_ = []
                    for s in range(2):
                        bx = bank()
                        src = PP[s][nxt]
                        for hp in range(8):
                            for par in range(2):
                                lo = par * 64
                                mm(ps[lo:lo + 64, bx, hp * 64:(hp + 1) * 64], src[lo:lo + 64, hp, 64:128], Xb[s][lo:lo + 64, hp, :], True, True,
                                   [('PP', s, nxt), ('Xb', s)], [('ps', bx)])
                        xb_.append(bx)
                    for s in range(2):
                        bx = xb_[s]
                        tt(X32[s][:], X32[s][:], ps[:, bx, :].rearrange('p (h x) -> p h x', h=8), ALU.add, [('X32', s), ('ps', bx)], [('X32', s)])
                        if lvl < 5:
                            cp(Xb[s][:], X32[s][:], [('X32', s)], [('Xb', s)], eng='act')
                        else:
                            cp(XT[:, :, ctx[s], :], X32[s][:], [('X32', s)], [('XT', ctx[s])], eng='act')
            Y32 = FB_[0]

            def ph2(c):
                P.phase = 'rw_ph2'
                state_in(c)
                cur = c % 2
                a0 = A0b[cur]
                bR = bank()
                for hp in range(8):
                    for par in range(2):
                        lo = par * 64
                        o = ps[lo:lo + 64, bR, hp * 64:(hp + 1) * 64]
                        mm(o, QR[lo:lo + 64, hp, 0, c, :], a0[lo:lo + 64, hp, :], True, False, ['HX0', 'HX1', ('A0b', cur)], [('ps', bR)])
                        mm(o, LP[lo:lo + 64, hp, c, 0:64], vTv(c, hp, lo), False, True, [('LP', c), 'H2'], [('ps', bR)])
                cp(RHSb[:], ps[:, bR, :].rearrange('p (h x) -> p h x', h=8), [('ps', bR)], ['RHSb'], eng='act')
                yield
                P.phase = 'rw_ph2'
                bU = bank()
                for hp in range(8):
                    for par in range(2):
                        lo = par * 64
                        mm(ps[lo:lo + 64, bU, hp * 64:(hp + 1) * 64], XT[lo:lo + 64, hp, c, :], RHSb[lo:lo + 64, hp, :], True, True,
                           [('XT', c), 'RHSb'], [('ps', bU)])
                cp(Ub[:], ps[:, bU, :].rearrange('p (h x) -> p h x', h=8), [('ps', bU)], ['Ub'], eng='act')
                yield
                P.phase = 'rw_ph2'
                bD = bank()
                for hp in range(8):
                    for par in range(2):
                        lo = par * 64
                        o = ps[lo:lo + 64, bD, hp * 64:(hp + 1) * 64]
                        mm(o, kTv(c, hp, lo), vTv(c, hp, lo), True, False, ['pl', 'H2'], [('ps', bD)])
                        mm(o, nTv(c, hp, lo), Ub[lo:lo + 64, hp, :], False, True, ['pl', 'Ub'], [('ps', bD)])
                bY = bank()
                for hp in range(8):
                    for par in range(2):
                        lo = par * 64
                        o = ps[lo:lo + 64, bY, hp * 64:(hp + 1) * 64]
                        mm(o, a0[lo:lo + 64, hp, :], QR[lo:lo + 64, hp, 1, c, :], True, False, [('A0b', cur), 'HX0', 'HX1'], [('ps', bY)])
                        mm(o, vTv(c, hp, lo), LP[lo:lo + 64, hp, c, 64:128], False, False, ['H2', ('LP', c)], [('ps', bY)])
                        mm(o, Ub[lo:lo + 64, hp, :], PN[lo:lo + 64, hp, c, 64:128], False, True, ['Ub', ('PN', c)], [('ps', bY)])
                tt(A32[:], A32[:], ps[:, bD, :].rearrange('p (h x) -> p h x', h=8), ALU.add, ['A32', ('ps', bD)], ['A32'])
                tt(A32[:], A32[:], WCs[:, :, c:c + 1].to_broadcast([128, 8, 64]), ALU.mult, ['A32', 'WCs'], ['A32'])
                cp(A0b[1 - cur][:], A32[:], ['A32'], [('A0b', 1 - cur)], eng='act')
                cp(Y32[:, :, c * C:(c + 1) * C], ps[:, bY, :].rearrange('p (h x) -> p h x', h=8), [('ps', bY)], ['F0'], eng='dve')
                state_out(c)
                yield

            def drain(g):
                for _ in g:
                    pass

            def interleave(ga, gb):
                a_ok = b_ok = True
                while a_ok or b_ok:
                    if a_ok:
                        try:
                            next(ga)
                        except StopIteration:
                            a_ok = False
                    if b_ok:
                        try:
                            next(gb)
                        except StopIteration:
                            b_ok = False

            def chain(*gs):
                for g in gs:
                    yield from g
            drain(ph1(0))
            if NCH == 4:
                interleave(ph1(2), chain(ph2(0), ph2(1)))
                drain(chain(ph2(2), ph2(3)))
            else:
                for c0 in range(2, NCH, 2):
                    drain(ph1(c0))
                for c in range(NCH):
                    drain(ph2(c))
            return Y32, v32

        def rwkv_post(Y, Ykey, bonus, bkey, glb, g0b, lmb, mb, n):
            P.phase = 'rwkv_post'
            Yb = HB_[0]; ysq = hb
            cp(Yb[:, :, 0:n], Y[:, :, 0:n], [Ykey], ['H0'], eng='act')
            act(ysq[:, :, 0:n], Y[:, :, 0:n], AF.Square, [Ykey], ['hb'])
            def post_chunk(kc):
                b1 = bank(); b2 = bank()
                mm(ps[:, b1, 0:n], bdb[:], Yb[:, kc, 0:n], True, True, ['bdb', 'H0'], [('ps', b1)])
                yield
                mm(ps[:, b2, 0:n], bdb[:], ysq[:, kc, 0:n], True, True, ['bdb', 'hb'], [('ps', b2)])
                SS, KP = stsel(kc)
                mean = SS[0][:, 0:n]; var = SS[1][:, 0:n]; t_ = SS[2][:, 0:n]
                yield
                act(mean, ps[:, b1, 0:n], AF.Copy, [('ps', b1)], [KP + '0'], scale=1.0 / 64)
                yield
                act(var, mean, AF.Square, [KP + '0'], [KP + '1'])
                yield
                stt(var, ps[:, b2, 0:n], 1.0 / 64, var, ALU.mult, ALU.subtract, [('ps', b2), KP + '1'], [KP + '1'])
                yield
                ts(var, var, 0.0, GN_EPS, ALU.max, ALU.add, [KP + '1'], [KP + '1'])
                yield
                act(var, var, AF.Sqrt, [KP + '1'], [KP + '1'])
                yield
                recip(var, var, [KP + '1'], [KP + '1'])
                yield
                tt(t_, Y[:, kc, 0:n], mean, ALU.subtract, [Ykey, KP + '0'], [KP + '2'])
                yield
                tt(t_, t_, var, ALU.mult, [KP + '2', KP + '1'], [KP + '2'])
                yield
                act(t_, t_, AF.Identity, [KP + '2', 'v_gn_g', 'v_gn_b'], [KP + '2'], bias=vec['gn_b'][:, kc:kc + 1], scale=vec['gn_g'][:, kc:kc + 1])
                yield
                tt(t_, t_, bonus[:, kc, 0:n], ALU.add, [KP + '2', bkey], [KP + '2'])
                yield
                tt(t_, t_, glb[:, kc, 0:n], ALU.mult, [KP + '2', 'H1'], [KP + '2'])
                yield
                tt(t_, t_, g0b[:, kc, 0:n], ALU.mult, [KP + '2', 'H3'], [KP + '2'])
                yield
                tt(mb[:, kc, 0:n], t_, lmb[:, kc, 0:n], ALU.add, [KP + '2', 'HX2'], ['H2'])
            for _c0 in range(0, 8, 2):
                interleave2(post_chunk(_c0), post_chunk(_c0 + 1))

        def resid_ln(w, xin, xkey, ln_idx, n):
            def evac(m, pb):
                stt(h32[:, m, 0:n], ps[:, pb, 0:n], 1.0 / ALPHA, h32[:, m, 0:n], ALU.mult, ALU.add, [('ps', pb), 'h32'], ['h32'])
            proj(w, D, xin, xkey, n, evac)
            layernorm(ln_idx, n, LN_EPS / (ALPHA * ALPHA))

        def xattn_prompt(n):
            P.phase = 'xattn_prompt'
            qb = HB_[0]; ob = HB_[1]; pT = HX[:, 0:8, :]
            def evq(m, pb):
                cp(qb[:, m, 0:n], ps[:, pb, 0:n], [('ps', pb)], ['H0'], eng='act')
            proj('xa_wq', D, hb, 'hb', n, evq)
            for h in range(4):
                for mc in range(2):
                    pb = bank()
                    for dc in range(2):
                        mm(ps[:, pb, 0:n], mkT[:, 2 * h + dc, mc * 128:(mc + 1) * 128], qb[:, 2 * h + dc, 0:n], dc == 0, dc == 1, ['mkT', 'H0'], [('ps', pb)])
                    act(pT[:, 2 * h + mc, 0:n], ps[:, pb, 0:n], AF.Exp, [('ps', pb)], ['HX0'], scale=1.0 / 16.0)
            rds = []
            for h in range(4):
                pd = bank()
                for mc in range(2):
                    mm(ps[:, pd, 0:n], onesb[:], pT[:, 2 * h + mc, 0:n], mc == 0, mc == 1, ['onesb', 'HX0'], [('ps', pd)])
                SS, KP = stsel(h)
                rd = SS[h // 2][:, 0:n]; rk_ = KP + str(h // 2)
                recip(rd, ps[:, pd, 0:n], [('ps', pd)], [rk_])
                rds.append((rd, rk_))
            for h in range(4):
                rd, rk_ = rds[h]
                for dc in range(2):
                    po = bank()
                    for mc in range(2):
                        mm(ps[:, po, 0:n], mvb[:, mc, (2 * h + dc) * 128:(2 * h + dc + 1) * 128], pT[:, 2 * h + mc, 0:n], mc == 0, mc == 1, ['mvb', 'HX0'], [('ps', po)])
                    tt(ob[:, 2 * h + dc, 0:n], ps[:, po, 0:n], rd, ALU.mult, [('ps', po), rk_], ['H1'])
            resid_ln('xa_wo', ob, 'H1', 2, n)

        if 'm' in PARTS or PARTS == 'abcdefg':
            P.op('dve', lambda e: e.memset(A32[:], 0.0), writes=['A32'])
            P.op('dve', lambda e: e.memset(A0b[0][:], 0.0), writes=[('A0b', 0)])
        for ti in range(int(os.environ.get('NTI', NTILES)) if stage >= 9 else 1):
            TOG = os.environ.get('TOG', '')
            for r in range(1 if '1' in TOG else NT // 128):
                load_tokens_fm(di['xp'][ti * NT + r * 128: ti * NT + (r + 1) * 128, :], 128, h32, None if 'D' in TOG else hb, r * 128, 'h32')
            if stage >= 2:
                ffn('ffn1_wi', 'ffn1_wo', 0, NT)
            if ti == 0:
                dump('h1', h32[:], 'h32')
            if stage < 3:
                break
            B = mixer_prompt(ti)
            if stage < 4:
                break
            lmb = lru_prompt(ti, B)
            if ti == 0:
                dump('lm', lmb, 'HX2')
            if stage < 5:
                break
            Y32, bonus = rwkv_core(B, lambda c: None, lambda c: None)
            if ti == 0:
                dump('Y', Y32[:], 'F0')
            if stage < 6:
                break
            mb = HB_[2]
            rwkv_post(Y32, 'F0', bonus, 'F2', B['glb'], B['g0b'], lmb, mb, NT)
            resid_ln('w_mix_out', mb, 'H2', 1, NT)
            if ti == 0:
                dump('h2', h32[:], 'h32')
            if stage < 7:
                break
            if ti == 0:
                mem_kv()
            xattn_prompt(NT)
            if ti == 0:
                dump('h3', h32[:], 'h32')
            if stage < 8:
                break
            ffn('ffn2_wi', 'ffn2_wo', 3, NT)
            for r in range(NT // 128):
                store_fm_tokens(h32, 'h32', r * 128, 128, di['yp'][ti * NT + r * 128: ti * NT + (r + 1) * 128, :])
        if stage < 9:
            P.emit()
            return nc, P
        def prompt_outputs():
            pass
            Sout = FB_[1].rearrange('p a b -> p (a b)')[:, 0:1024].rearrange('p (hp par k) -> p hp par k', hp=8, par=2)
            for g in range(2):
                pb = bank()
                for q in range(4):
                    hp = g * 4 + q
                    tr(ps[0:64, pb, q * 128:(q + 1) * 128], A32[:, hp, :], ident[:, :], ['A32', 'ident'], [('ps', pb)])
                cp(Sout[0:64, g * 4:(g + 1) * 4, :, :], ps[0:64, pb, :].rearrange('p (q par k) -> p q par k', q=4, par=2), [('ps', pb)], ['F1'], eng='act')
            dma('sp', di['prw'].rearrange('(hp par v) k -> v hp par k', hp=8, par=2), Sout[0:64, :, :, :], ['F1'], [], dsem_misc, is_out=True)
            osm2 = FB_[2]
            cp(osm2[:, 0:8, 0:3], osml[:, :, 0:3], ['osml'], ['F2'])
            cp(osm2[:, 0:8, 3:4], osml[:, :, 3:4], ['osml'], ['F2'])
            store_fm_tokens(osm2, 'F2', 0, 3, di['pcv'][:, :])
            store_fm_tokens(osm2, 'F2', 3, 1, di['plru'].rearrange('(o d) -> o d', o=1))
            osh3 = FB_[3]
            cp(osh3[:, 0:8, 0:1], osh[:, 0:8].unsqueeze(2), ['osh'], ['F3'])
            cp(osh3[:, 0:8, 1:2], osh[:, 8:16].unsqueeze(2), ['osh'], ['F3'])
            cp(osh3[:, 0:8, 2:3], osh[:, 16:24].unsqueeze(2), ['osh'], ['F3'])
            cp(osh3[:, 0:2, 3:4], osh[:, 24:26].unsqueeze(2), ['osh'], ['F3'])
            pshv = di['psh'].rearrange('(o d) -> o d', o=1)
            for q in range(3):
                store_fm_tokens(osh3, 'F3', q, 1, pshv[:, q * 1024:(q + 1) * 1024])
            store_fm_tokens(osh3, 'F3', 3, 1, pshv[:, 3072:3328], nch=2)


        if os.environ.get('NTI') != '0':
            prompt_outputs()
        def sample_path():
            P.phase = 'sample_path'
            n = NS
            sm = lambda nm, shp, dt=F32: P.sbuf(nm, shp, dt)
            rc = sm('s_rc', [128, 8, n]); kc_ = sm('s_kc', [128, 8, n]); vc = sm('s_vc', [128, 8, n]); ac = sm('s_ac', [128, 8, n]); lsc = sm('s_lsc', [128, 8, n])
            mk32 = mkT[:].rearrange('p a b -> p (a b)').bitcast(F32)
            mv32 = mvb[:].rearrange('p a b -> p (a b)').bitcast(F32)
            prevS = mk32[:, 0:26 * n].rearrange('p (c b) -> p c b', c=26)
            praw = mk32[:, 26 * n:52 * n].rearrange('p (c b) -> p c b', c=26)
            scvT = mv32[:, 0:24 * n].rearrange('p (c b) -> p c b', c=8)
            h0S = mv32[:, 24 * n:32 * n].rearrange('p (c b) -> p c b', c=8)
            yc = mv32[:, 32 * n:40 * n].rearrange('p (c b) -> p c b', c=8)
            bonc = mv32[:, 40 * n:48 * n].rearrange('p (c b) -> p c b', c=8)
            plc = mv32[:, 48 * n:56 * n].rearrange('p (c b) -> p c b', c=8)
            hsc = mv32[:, 56 * n:64 * n].rearrange('p (c b) -> p c b', c=8)
            P.op('dve', lambda e: e.engine_nop(), writes=['mkT', 'mvb', 's_prev', 's_praw', 's_scvT', 's_h0', 's_yc', 's_bonc', 's_plc', 's_hsc'])
            BD = tok32[1][:].rearrange('p (h x) -> p h x', h=8); ones32 = st1[4][:, 0:128]
            glb = HB_[1]; geb = HB_[2]; g0b = HB_[3]; g1b = HX[:, 0:8, :]; lmb = HX[:, 16:24, :]
            dsS = [P.dma_sem() for _ in range(7)]

            def load_fm(src_rows_ap, nrows, nchunks, dst, dkey, tki, sem):
                tk = tok32[tki]
                dma('sp', tk[0:nrows, 0:nchunks * 128], src_rows_ap, [], [('tok32', tki)], sem)
                for g0 in range(0, nchunks, 4):
                    gn = min(4, nchunks - g0)
                    b = bank()
                    for q in range(gn):
                        tr(ps[:, b, q * 128:q * 128 + nrows], tk[0:nrows, (g0 + q) * 128:(g0 + q + 1) * 128], ident[0:nrows, 0:nrows],
                           [('tok32', tki), 'ident'], [('ps', b)])
                    cp(dst[:, g0:g0 + gn, 0:nrows], ps[:, b, :].rearrange('p (q t) -> p q t', q=4)[:, 0:gn, 0:nrows], [('ps', b)], [dkey], eng='act')

            load_tokens_fm(di['xs'], n, h32, hb, 0, 'h32')
            for q in range(4):
                c0 = q * 8; cn = min(8, 26 - c0)
                tmpd = FB_[0] if q % 2 == 0 else FB_[1]
                load_fm(di['ssh'][:, c0 * 128:(c0 + cn) * 128], n, cn, tmpd, 'F%d' % (q % 2), q % 2, dsS[q % 2])
                cp(prevS[:, c0:c0 + cn, :], tmpd[:, 0:cn, 0:n], ['F%d' % (q % 2)], ['s_prev'])
            load_fm(di['slru'], n, 8, h0S, 's_h0', 0, dsS[0])
            load_fm(di['scv'].rearrange('b j d -> (b j) d'), 3 * n, 8, scvT, 's_scvT', 1, dsS[1])
            dma('sp', di['scv_o'][:, 0:2, :], di['scv'][:, 1:3, :], [], [], P.dma_sem(), is_out=True)

            ffn('ffn1_wi', 'ffn1_wo', 0, n)

            tmp = st1[0]

            def evac(m, pb):
                psn = ps[:, pb, 0:n]
                if m < 26:
                    cp(praw[:, m, :], psn, [('ps', pb)], ['s_praw'], eng='act')
                    ts(tmp[:, 0:n], prevS[:, m, :], mu[:, m:m + 1], None, ALU.mult, None, ['s_prev', 'mu'], ['st0'])
                    if m < 24:
                        dst = [rc, kc_, vc][m // 8][:, m % 8, :]
                        dkey = ['s_rc', 's_kc', 's_vc'][m // 8]
                        stt(dst, psn, omu[:, m:m + 1], tmp[:, 0:n], ALU.mult, ALU.add, [('ps', pb), 'omu', 'st0'], [dkey])
                    else:
                        xs_ = st1[1][:, 0:n]
                        stt(xs_, psn, omu[:, m:m + 1], tmp[:, 0:n], ALU.mult, ALU.add, [('ps', pb), 'omu', 'st0'], ['st1'])
                        lb = st1[2][:, 0:NT].bitcast(BF16)[:, 0:n]
                        if m == 24:
                            act(lb[0:64, :], xs_[0:64, :], AF.Tanh, ['st1'], ['st2'])
                            cp(lb[64:128, :], xs_[64:128, :], ['st1'], ['st2'])
                            for (lo, dstt, dk, bvec) in [(0, lsc, 's_lsc', 'decay_w0'), (64, ac, 's_ac', 'aaa_a0')]:
                                for q in range(8):
                                    p2 = bank()
                                    mm(ps[:, p2, 0:n], w2a2[lo:lo + 64, q * 128:(q + 1) * 128], lb[lo:lo + 64, :], True, True, ['w2a2', 'st2'], [('ps', p2)])
                                    act(dstt[:, q, :], ps[:, p2, 0:n], AF.Sigmoid, [('ps', p2), 'v_' + bvec], [dk], bias=vec[bvec][:, q:q + 1])
                        else:
                            act(lb, xs_, AF.Sigmoid, ['st1'], ['st2'])
                            for q in range(8):
                                p2 = bank()
                                mm(ps[:, p2, 0:n], g2[:, q * 128:(q + 1) * 128], lb, True, True, ['g2', 'st2'], [('ps', p2)])
                                cp(glb[:, q, 0:n], ps[:, p2, 0:n], [('ps', p2)], ['H1'], eng='act')
                elif m < 34:
                    cp(plc[:, m - 26, :], psn, [('ps', pb)], ['s_plc'], eng='act')
                elif m < 42:
                    act(geb[:, m - 34, 0:n], psn, AF.Gelu, [('ps', pb)], ['H2'])
                elif m < 50:
                    act(g0b[:, m - 42, 0:n], psn, AF.Sigmoid, [('ps', pb)], ['H3'])
                else:
                    act(g1b[:, m - 50, 0:n], psn, AF.Sigmoid, [('ps', pb)], ['HX0'])
            proj('w_in', PW, hb, 'hb', n, evac)
            for q in range(4):
                c0 = q * 8; cn = min(8, 26 - c0)
                store_fm_tokens(praw[:, c0:c0 + cn, :], 's_praw', 0, n, di['ssh_o'][:, c0 * 128:(c0 + cn) * 128], nch=cn)
            store_fm_tokens(plc, 's_plc', 0, n, di['scv_o'][:, 2, :])

            sc3 = scvT[:].rearrange('p c (b j) -> p c b j', j=3)
            xc = st1[0][:, 0:n]; gr = st1[1][:, 0:n]; gi = st1[2][:, 0:n]; t3 = st1[3][:, 0:n]; hs = st1[4][:, 0:n]
            xcb = HX[:, 8:16, :]
            for c in range(8):
                act(xc, plc[:, c, :], AF.Identity, ['s_plc', 'cw', 'v_conv_b'], ['st0'], bias=vec['conv_b'][:, c:c + 1], scale=cw[:, 3, c:c + 1])
                for j in range(3):
                    stt(xc, sc3[:, c, :, j], cw[:, j, c:c + 1], xc, ALU.mult, ALU.add, ['s_scvT', 'cw', 'st0'], ['st0'])
                cp(xcb[:, c, 0:n], xc, ['st0'], ['HX1'], eng='act')
                p1 = bank(); p2 = bank()
                mm(ps[:, p1, 0:n], wrbd[:, c, :], xcb[:, c, 0:n], True, True, ['wrbd', 'HX1'], [('ps', p1)])
                mm(ps[:, p2, 0:n], wibd[:, c, :], xcb[:, c, 0:n], True, True, ['wibd', 'HX1'], [('ps', p2)])
                act(gr, ps[:, p1, 0:n], AF.Sigmoid, [('ps', p1), 'v_lru_br'], ['st1'], bias=vec['lru_br'][:, c:c + 1])
                act(gi, ps[:, p2, 0:n], AF.Sigmoid, [('ps', p2), 'v_lru_bi'], ['st2'], bias=vec['lru_bi'][:, c:c + 1])
                act(gr, gr, AF.Exp, ['st1', 'lsp'], ['st1'], scale=lsp[:, c:c + 1])
                act(t3, gr, AF.Square, ['st1'], ['st3'])
                ts(t3, t3, -1.0, 1.0, ALU.mult, ALU.add, ['st3'], ['st3'])
                ts(t3, t3, 0.0, None, ALU.max, None, ['st3'], ['st3'])
                act(t3, t3, AF.Sqrt, ['st3'], ['st3'])
                tt(gi, gi, xc, ALU.mult, ['st2', 'st0'], ['st2'])
                tt(gi, gi, t3, ALU.mult, ['st2', 'st3'], ['st2'])
                tt(hs, gr, h0S[:, c, :], ALU.mult, ['st1', 's_h0'], ['st4'])
                tt(hsc[:, c, :], hs, gi, ALU.add, ['st4', 'st2'], ['s_hsc'])
                tt(t3, hsc[:, c, :], geb[:, c, 0:n], ALU.mult, ['s_hsc', 'H2'], ['st3'])
                tt(lmb[:, c, 0:n], t3, g1b[:, c, 0:n], ALU.mult, ['st3', 'HX0'], ['HX2'])
            store_fm_tokens(hsc, 's_hsc', 0, n, di['slru_o'])

            F1flat = FB_[1].rearrange('p a b -> p (a b)')
            Souts = [F1flat[:, 0:1024].rearrange('p (hp par k) -> p hp par k', hp=8, par=2),
                     F1flat[:, 1024:2048].rearrange('p (hp par k) -> p hp par k', hp=8, par=2)]
            Skeys = ['F1', ('F1', 'b')]; Ssems = [dsS[4], P.dma_sem()]
            BDs = [tok32[0][:].rearrange('p (h x) -> p h x', h=8), tok32[1][:].rearrange('p (h x) -> p h x', h=8)]
            Bsems = [P.dma_sem(), dsS[3]]
            for i_ in range(2):
                P.op('dve', lambda e, i_=i_: e.memset(tok32[i_][:], 0.0), writes=[('tok32', i_)])

            def prefetch_state(b_):
                if b_ >= NS:
                    return
                src = di['srw'][b_].rearrange('(hp par v) k -> par v hp k', hp=8, par=2)
                for par in range(2):
                    dma('sp', BDs[b_ % 2][par * 64:(par + 1) * 64, :, par * 64:(par + 1) * 64], src[par], [], [('tok32', b_ % 2)], Bsems[b_ % 2])
            prefetch_state(0)
            bc16 = lambda v: v[:].unsqueeze(2).to_broadcast([128, 8, n])
            v816 = lambda t: t[:, 0:8 * n].rearrange('p (k b) -> p k b', k=8)
            kkn = plc; Wvc = hsc
            sqb = HB_[0][:, :, 0:n]
            tt(kkn[:], kc_[:], bc16(vec['k_k']), ALU.mult, ['s_kc', 'v_k_k', 's_plc'], ['s_plc'])
            act(sqb, kkn[:], AF.Square, ['s_plc'], ['H0'])
            pb = bank()
            for kc in range(8):
                mm(ps[:, pb, kc * n:(kc + 1) * n], bdb[:], sqb[:, kc, :], True, True, ['bdb', 'H0'], [('ps', pb)])
            s16 = v816(st1[0])
            act(s16, ps[:, pb, 0:8 * n].rearrange('p (k b) -> p k b', k=8), AF.Sqrt, [('ps', pb)], ['st0'])
            ts(s16, s16, 1e-12, None, ALU.max, None, ['st0'], ['st0'])
            recip(s16, s16, ['st0'], ['st0'])
            tt(kkn[:], kkn[:], s16, ALU.mult, ['s_plc', 'st0'], ['s_plc'])
            u16 = v816(st1[1])
            tt(u16, ac[:], bc16(vec['k_a']), ALU.mult, ['s_ac', 'v_k_a'], ['st1'])
            tt(u16, u16, bc16(oka), ALU.add, ['st1', 'oka'], ['st1'])
            tt(kc_[:], kc_[:], u16, ALU.mult, ['s_kc', 'st1'], ['s_kc'])
            tt(ac[:], ac[:], kkn[:], ALU.mult, ['s_ac', 's_plc'], ['s_ac'])
            r16 = v816(st1[2])
            tt(r16, rc[:], kc_[:], ALU.mult, ['s_rc', 's_kc'], ['st2'])
            tt(r16, r16, bc16(vec['r_k']), ALU.mult, ['st2', 'v_r_k'], ['st2'])
            cp(sqb, r16, ['st2'], ['H0'], eng='act')
            pb = bank()
            for kc in range(8):
                mm(ps[:, pb, kc * n:(kc + 1) * n], bdb[:], sqb[:, kc, :], True, True, ['bdb', 'H0'], [('ps', pb)])
            tt(bonc[:], vc[:], ps[:, pb, 0:8 * n].rearrange('p (k b) -> p k b', k=8), ALU.mult, ['s_vc', ('ps', pb)], ['s_bonc'])
            act(Wvc[:], lsc[:], AF.Exp, ['s_lsc', 's_hsc'], ['s_hsc'], scale=C0)
            act(lsc[:], lsc[:], AF.Exp, ['s_lsc'], ['s_lsc'], scale=-C0)
            tt(rc[:], rc[:], lsc[:], ALU.mult, ['s_rc', 's_lsc'], ['s_rc'])
            tt(kc_[:], kc_[:], Wvc[:], ALU.mult, ['s_kc', 's_hsc'], ['s_kc'])
            stt(ac[:], ac[:], -1.0, Wvc[:], ALU.mult, ALU.mult, ['s_ac', 's_hsc'], ['s_ac'])

            for g in range(NS // NCH):
                Bp = dict(r32=FB_[0], k32=FB_[1], v32=FB_[2], a32=FB_[3], ls32=FB_[4])
                gs = slice(g * NCH, (g + 1) * NCH)

                def prepared(QR, KT, NB, vb, WCs_, gs=gs):
                    c0v = lambda t: t.rearrange('p k (c t) -> p k c t', t=C)[:, :, :, 0]
                    P.op('dve', lambda e: e.memset(HX[:, 0:16, :], 0.0), writes=['HX0', 'HX1'])
                    P.op('dve', lambda e: e.memset(KT[:], 0.0), writes=['H0'])
                    P.op('dve', lambda e: e.memset(NB[:], 0.0), writes=['hb'])
                    P.op('dve', lambda e: e.memset(vb[:, :, 0:NT], 0.0), writes=['pl'])
                    cp(QR[:, :, 0, :, 0], kkn[:, :, gs], ['s_plc'], ['HX0', 'HX1'])
                    cp(QR[:, :, 1, :, 0], rc[:, :, gs], ['s_rc'], ['HX0', 'HX1'])
                    cp(c0v(KT[:]), kc_[:, :, gs], ['s_kc'], ['H0'])
                    cp(c0v(NB[:]), ac[:, :, gs], ['s_ac'], ['hb'])
                    cp(c0v(vb[:, :, 0:NT]), vc[:, :, gs], ['s_vc'], ['pl'])
                    cp(WCs_[:, :, :], lsc[:, :, gs], ['s_lsc'], ['WCs'])

                def state_in(c, g=g):
                    b_ = g * NCH + c
                    prefetch_state(b_ + 1)
                    BD = BDs[b_ % 2]
                    pb = bank()
                    for hp in range(8):
                        mm(ps[:, pb, hp * 64:(hp + 1) * 64], BD[:, hp, :], mask[:, 192:256], True, True, [('tok32', b_ % 2), 'mask'], [('ps', pb)])
                    v3 = ps[:, pb, :].rearrange('p (h x) -> p h x', h=8)
                    cp(A32[:], v3, [('ps', pb)], ['A32'], eng='dve')
                    cp(A0b[c % 2][:], v3, [('ps', pb)], [('A0b', c % 2)], eng='act')

                def state_out(c, g=g):
                    b_ = g * NCH + c
                    Sout = Souts[b_ % 2]; skey = Skeys[b_ % 2]
                    for gg in range(2):
                        pb = bank()
                        for q in range(4):
                            hp = gg * 4 + q
                            tr(ps[0:64, pb, q * 128:(q + 1) * 128], A32[:, hp, :], ident[:, :], ['A32', 'ident'], [('ps', pb)])
                        cp(Sout[0:64, gg * 4:(gg + 1) * 4, :, :], ps[0:64, pb, :].rearrange('p (q par k) -> p q par k', q=4, par=2), [('ps', pb)], [skey], eng='act')
                    dma('sp', di['srw_o'][b_].rearrange('(hp par v) k -> v hp par k', hp=8, par=2), Sout[0:64, :, :, :], [skey], [], Ssems[b_ % 2], is_out=True)
                Y32, bon = rwkv_core(Bp, state_in, state_out, skip_inverse=True, prepared=prepared)
                cp(yc[:, :, g * NCH:(g + 1) * NCH], Y32[:].rearrange('p k (c t) -> p k c t', t=C)[:, :, :, 0], ['F0'], ['s_yc'])
            mb = HB_[2]
            rwkv_post(yc, 's_yc', bonc, 's_bonc', glb, g0b, lmb, mb, n)
            resid_ln('w_mix_out', mb, 'H2', 1, n)

            qc = FB_[0]; qT = FB_[1].rearrange('p a b -> p (a b)')[:, 0:1024]; sel = FB_[2].rearrange('p a b -> p (a b)')[:, 0:NS * 128].rearrange('p (b m) -> p b m', b=NS)
            Kbs = [FB_[3].rearrange('p a b -> p (a b)').rearrange('p (mc f) -> p mc f', mc=2),
                   pl[:].rearrange('p a b -> p (a b)')[:, 0:2048].rearrange('p (mc f) -> p mc f', mc=2)]
            Kkeys = ['F3', 'pl']; Ksems = [dsS[5], P.dma_sem()]
            prod = FB_[4].rearrange('p a b -> p (a b)')[:, 0:1024]
            Vbs = [HB_[0][:].rearrange('p a b -> p (a b)').rearrange('p (mc f) -> p mc f', mc=2),
                   HB_[2][:].rearrange('p a b -> p (a b)').rearrange('p (mc f) -> p mc f', mc=2)]
            Vkeys = ['H0', 'H2']; Vsems = [dsS[6], P.dma_sem()]
            ob = HB_[1]
            sc = st1[0][:, 0:128]; ex = st1[1][:, 0:128]; den = st1[2][:, 0:64]; pbf = st1[3][:, 0:NT].bitcast(BF16)[:, 0:128]

            def evq(m, pb):
                cp(qc[:, m, 0:n], ps[:, pb, 0:n], [('ps', pb)], ['F0'], eng='act')
            proj('xa_wq', D, hb, 'hb', n, evq)
            for g0 in range(0, 8, 4):
                pb = bank()
                for q in range(4):
                    tr(ps[0:n, pb, q * 128:(q + 1) * 128], qc[:, g0 + q, 0:n], ident[:, :], ['F0', 'ident'], [('ps', pb)])
                cp(qT[0:n, g0 * 128:(g0 + 4) * 128], ps[0:n, pb, :], [('ps', pb)], ['F1'], eng='act')
            cp(sel[0:n, :, :], ident[0:n, 0:n].unsqueeze(2).to_broadcast([n, n, 128]), ['ident'], ['F2'])
            for b_ in range(NS):
                Kb = Kbs[b_ % 2]; kkey = Kkeys[b_ % 2]
                dma('sp', Kb, di['cmk'][b_].rearrange('(mc p) f -> p mc f', p=128), [], [kkey], Ksems[b_ % 2])
                pq = [bank(), bank()]
                for hf in range(2):
                    mm(ps[:, pq[hf], :], sel[0:n, b_, :], qT[0:n, hf * 512:(hf + 1) * 512], True, True, ['F2', 'F1'], [('ps', pq[hf])])
                for mc in range(2):
                    for hf in range(2):
                        tt(prod[:, hf * 512:(hf + 1) * 512], Kb[:, mc, hf * 512:(hf + 1) * 512], ps[:, pq[hf], :], ALU.mult, [kkey, ('ps', pq[hf])], ['F4'])
                    P.op('dve', lambda e, b_=b_, mc=mc: e.tensor_reduce(out=sc[:, (b_ * 2 + mc) * 4:(b_ * 2 + mc) * 4 + 4], in_=prod.rearrange('p (h d) -> p h d', h=4), axis=AX.X, op=ALU.add),
                         reads=['F4'], writes=['st0'])
            act(ex, sc, AF.Exp, ['st0'], ['st1'], scale=1.0 / 16.0)
            dma('sp', ones32, di['c_all'][:, 128:256], [], ['st4'], dsS[2])
            pdn = bank()
            mm(ps[:, pdn, 0:128], ones32, ex, True, True, ['st4', 'st1'], [('ps', pdn)])
            d4 = ps[:, pdn, 0:128].rearrange('p (b mc h) -> p b mc h', mc=2, h=4)
            den3 = den.rearrange('p (b h) -> p b h', h=4)
            cp(den3, d4[:, :, 0, :], [('ps', pdn)], ['st2'])
            tt(den3, den3, d4[:, :, 1, :], ALU.add, ['st2', ('ps', pdn)], ['st2'])
            recip(den, den, ['st2'], ['st2'])
            tt(pbf.rearrange('p (b mc h) -> p b mc h', mc=2, h=4), ex.rearrange('p (b mc h) -> p b mc h', mc=2, h=4),
               den3.unsqueeze(2).to_broadcast([128, NS, 2, 4]), ALU.mult, ['st1', 'st2'], ['st3'])
            po = bank()
            for b_ in range(NS):
                Vb = Vbs[b_ % 2]; vkey = Vkeys[b_ % 2]
                dma('pool', Vb, di['cmv'][b_].rearrange('(mc p) f -> p mc f', p=128), [], [vkey], Vsems[b_ % 2])
                for c in range(8):
                    for mc in range(2):
                        col = (b_ * 2 + mc) * 4 + c // 2
                        mm(ps[:, po, c * NS + b_:c * NS + b_ + 1], Vb[:, mc, c * 128:(c + 1) * 128], pbf[:, col:col + 1], mc == 0, mc == 1, [vkey, 'st3'], [('ps', po)])
            cp(ob[:, :, 0:n], ps[:, po, 0:8 * NS].rearrange('p (c b) -> p c b', c=8), [('ps', po)], ['H1'], eng='act')
            resid_ln('xa_wo', ob, 'H1', 2, n)
            ffn('ffn2_wi', 'ffn2_wo', 3, n)
            store_fm_tokens(h32, 'h32', 0, n, di['ys'])

        if do_sample:
            sample_path()
        P.emit()
    return nc, P


_CACHE = {}


def _consts():
    a = np.arange(128) % 64
    b = np.arange(64)
    su = (a[:, None] < b[None, :]).astype(np.float32)
    ui = (a[:, None] <= b[None, :]).astype(np.float32)
    sl = (a[:, None] > b[None, :]).astype(np.float32)
    ey = (a[:, None] == b[None, :]).astype(np.float32)
    bd = np.zeros((128, 128), np.float32)
    bd[:64, :64] = 1.0
    bd[64:, 64:] = 1.0
    rs = np.ones((128, NT), np.float32)
    rs[:, ::C] = 0.0
    return {'c_all': np.ascontiguousarray(np.concatenate([np.eye(128, dtype=np.float32), np.ones((128, 128), np.float32), bd, su, ui, sl, ey, rs], axis=1))}


def make_in_maps(inputs):
    f = lambda a: np.ascontiguousarray(np.asarray(a, dtype=np.float32))
    shared = {}
    for nm in ['ffn1_wi', 'ffn1_wo', 'ffn2_wi', 'ffn2_wo', 'w_in', 'decay_w2', 'aaa_a2', 'gate_g2',
               'lru_wr', 'lru_wi', 'w_mix_out', 'xa_wq', 'xa_wk', 'xa_wv', 'xa_wo']:
        shared[nm] = np.ascontiguousarray(f(inputs[nm])[0])
    shared['prm'] = np.ascontiguousarray(np.concatenate(
        [f(inputs[nm])[0].reshape(-1, 128) for nm in ['ln_g', 'ln_b', 'shift_mu', 'conv_w'] + VEC_NAMES], axis=0))
    shared.update(_consts())
    maps = []
    for c in range(8):
        m = dict(shared)
        sl = slice(c * NS, (c + 1) * NS)
        m['xp'] = f(inputs['x_prompt'][c])
        m['mem'] = f(inputs['mem_prompt'][c])
        m['xs'] = f(inputs['x_sample'][sl, 0])
        m['cmk'] = f(inputs['cache_mem_k'][0, sl]).reshape(NS, NMEM, D)
        m['cmv'] = f(inputs['cache_mem_v'][0, sl]).reshape(NS, NMEM, D)
        m['srw'] = f(inputs['state_rwkv'][0, sl]).reshape(NS, D, 64)
        m['ssh'] = f(inputs['state_rwkv_shift'][0, sl])
        m['slru'] = f(inputs['state_lru'][0, sl])
        m['scv'] = f(inputs['state_conv'][0, sl])
        maps.append(m)
    return maps


def kernel(**inputs):
    if 'nc' not in _CACHE:
        _CACHE['nc'] = build()[0]
    nc = _CACHE['nc']
    maps = make_in_maps(inputs)
    res = run_bass_kernel_spmd(nc, maps, core_ids=list(range(8)))
    R = res.results
    cat = lambda k: np.stack([np.asarray(r[k], dtype=np.float32) for r in R])
    catc = lambda k: np.concatenate([np.asarray(r[k], dtype=np.float32) for r in R], axis=0)
    yp = cat('yp')
    ys = catc('ys').reshape(8 * NS, 1, D)
    pmk = cat('pmk').reshape(1, 8, NMEM, 4, 256)
    pmv = cat('pmv').reshape(1, 8, NMEM, 4, 256)
    prw = cat('prw').reshape(1, 8, 16, 64, 64)
    psh = cat('psh').reshape(1, 8, RP)
    plru = cat('plru').reshape(1, 8, D)
    pcv = cat('pcv').reshape(1, 8, 3, D)
    srw = catc('srw_o').reshape(1, 8 * NS, 16, 64, 64)
    ssh = catc('ssh_o').reshape(1, 8 * NS, RP)
    slru = catc('slru_o').reshape(1, 8 * NS, D)
    scv = catc('scv_o').reshape(1, 8 * NS, 3, D)
    return (yp, ys, pmk, pmv, prw, psh, plru, pcv, srw, ssh, slru, scv)
```
